# Optimizing a Trainium2 kernel written in Bass

```python
import math
import jax, jax.numpy as jnp
from jax import lax
import numpy as np

D_MODEL = 1024
BATCH = 8
SEQ = 2048
DEPTH = 2

CHUNK = 64
NORM_EPS = 1e-6
FFN_HALF = 0.5
D_FF = 2816

A_HEAD_DIM = 64
A_WIDTH = D_MODEL // 2
A_HEADS = A_WIDTH // A_HEAD_DIM
W_LORA = 64
A_LORA = 64
G_LORA = 128
GN_EPS = 64e-5
A_PROJ = 3 * A_WIDTH + W_LORA + A_LORA + G_LORA
A_SPLITS = (A_WIDTH, 2 * A_WIDTH, 3 * A_WIDTH, 3 * A_WIDTH + W_LORA,
            3 * A_WIDTH + W_LORA + A_LORA)

B_WIDTH = D_MODEL // 2
B_BLOCKS = 8
B_BLOCK_DIM = B_WIDTH // B_BLOCKS
CONV_WIDTH = 4
LRU_C = 8.0

IN_PROJ = A_PROJ + 2 * B_WIDTH

C_HEAD_DIM = 64
C_HEADS = D_MODEL // C_HEAD_DIM
C_WIDTH = C_HEADS * C_HEAD_DIM
Q_BLOCK = 128

kernel_name = "hybrid_rwkv7_rglru_stickbreaking_macaron"


def rmsnorm(x, g):
    xf = x.astype(jnp.float32)
    y = xf * lax.rsqrt(jnp.mean(xf * xf, axis=-1, keepdims=True) + NORM_EPS)
    return (y * g.astype(jnp.float32)).astype(x.dtype)


def swiglu(x, w_in, w_out):
    gate, up = jnp.split(x @ w_in, 2, axis=-1)
    return (jax.nn.silu(gate) * up) @ w_out


def shift_right(x):
    return jnp.pad(x, ((0, 0), (1, 0), (0, 0)))[:, :-1]


def rwkv7_step(S, inp):
    r_t, w_t, k_t, v_t, kk_t, a_t = inp
    sa = jnp.einsum("bhvk,bhk->bhv", S, -kk_t)
    S = (S * w_t[:, :, None, :]
         + jnp.einsum("bhv,bhk->bhvk", sa, kk_t * a_t)
         + jnp.einsum("bhv,bhk->bhvk", v_t, k_t))
    y = jnp.einsum("bhvk,bhk->bhv", S, r_t)
    return S, y


def rwkv7_time_mix(pa, mu, w0, w2, a0, a2, g2, k_k, k_a, r_k, lnx_g, lnx_b):
    bsz, seqlen, _ = pa.shape
    dt = pa.dtype
    f32 = jnp.float32
    pa = pa + mu * (shift_right(pa) - pa)
    r, k, v, w_lo, a_lo, g_lo = jnp.split(pa, A_SPLITS, axis=-1)
    w_log = -jax.nn.softplus(-(w0 + jnp.tanh(w_lo) @ w2).astype(f32)) - 0.5
    decay = jnp.exp(-jnp.exp(w_log))
    a = jax.nn.sigmoid((a0 + a_lo @ a2).astype(f32))
    g = jax.nn.sigmoid(g_lo) @ g2

    def heads(t):
        return t.astype(f32).reshape(bsz, seqlen, A_HEADS, A_HEAD_DIM)

    kk = heads(k * k_k)
    kk = kk / jnp.maximum(jnp.sqrt(jnp.sum(kk * kk, axis=-1, keepdims=True)), 1e-12)
    a_h = heads(a)
    k_h = heads(k) * (1.0 + (a_h - 1.0) * k_a.astype(f32).reshape(A_HEADS, A_HEAD_DIM))
    r_h, v_h, w_h = heads(r), heads(v), heads(decay)

    S0 = jnp.zeros((bsz, A_HEADS, A_HEAD_DIM, A_HEAD_DIM), f32)
    xs = tuple(jnp.moveaxis(t, 1, 0) for t in (r_h, w_h, k_h, v_h, kk, a_h))
    _, y = lax.scan(rwkv7_step, S0, xs)
    y = jnp.moveaxis(y, 0, 1)

    mean = jnp.mean(y, axis=-1, keepdims=True)
    var = jnp.mean(jnp.square(y - mean), axis=-1, keepdims=True)
    yn = (y - mean) * lax.rsqrt(var + GN_EPS)
    yn = (yn * lnx_g.astype(f32).reshape(A_HEADS, A_HEAD_DIM)
          + lnx_b.astype(f32).reshape(A_HEADS, A_HEAD_DIM))
    bonus = jnp.sum(r_h * k_h * r_k.astype(f32), axis=-1, keepdims=True) * v_h
    return (yn + bonus).reshape(bsz, seqlen, A_WIDTH).astype(dt) * g


def lru_combine(c1, c2):
    a1, b1 = c1
    a2, b2 = c2
    return a1 * a2, a2 * b1 + b2


def rglru_mix(xb, gb, conv_w, conv_b, gate_a_w, gate_a_b, gate_x_w, gate_x_b, lam):
    bsz, seqlen, _ = xb.shape
    f32 = jnp.float32
    xp = jnp.pad(xb, ((0, 0), (CONV_WIDTH - 1, 0), (0, 0)))
    xc = conv_b + sum(xp[:, i:i + seqlen] * conv_w[i] for i in range(CONV_WIDTH))
    blocks = xc.reshape(bsz, seqlen, B_BLOCKS, B_BLOCK_DIM)
    r = jax.nn.sigmoid((jnp.einsum("btni,nij->btnj", blocks, gate_a_w)
                        .reshape(bsz, seqlen, B_WIDTH) + gate_a_b).astype(f32))
    i_g = jax.nn.sigmoid((jnp.einsum("btni,nij->btnj", blocks, gate_x_w)
                          .reshape(bsz, seqlen, B_WIDTH) + gate_x_b).astype(f32))
    log_a = -LRU_C * r * jax.nn.softplus(-lam.astype(f32))
    a = jnp.exp(log_a)
    mult = jnp.sqrt(-jnp.expm1(2.0 * log_a))
    u = mult * i_g * xc.astype(f32)
    _, h = lax.associative_scan(lru_combine, (a, u), axis=1)
    return h.astype(xb.dtype) * jax.nn.gelu(gb)


def even_mixer(h, w_in, mu, w0, w2, a0, a2, g2, k_k, k_a, r_k, lnx_g, lnx_b,
               conv_w, conv_b, gate_a_w, gate_a_b, gate_x_w, gate_x_b, lam, w_out):
    proj = h @ w_in
    pa, xb, gb = jnp.split(proj, (A_PROJ, A_PROJ + B_WIDTH), axis=-1)
    ya = rwkv7_time_mix(pa, mu, w0, w2, a0, a2, g2, k_k, k_a, r_k, lnx_g, lnx_b)
    yb = rglru_mix(xb, gb, conv_w, conv_b, gate_a_w, gate_a_b, gate_x_w, gate_x_b, lam)
    return jnp.concatenate([ya, yb], axis=-1) @ w_out


def stick_breaking_mixer(h, w_qkv, w_out):
    bsz, seqlen, _ = h.shape
    q, k, v = jnp.split(h @ w_qkv, 3, axis=-1)
    q = q.reshape(bsz, seqlen, C_HEADS, C_HEAD_DIM)
    k = k.reshape(bsz, seqlen, C_HEADS, C_HEAD_DIM)
    v = v.reshape(bsz, seqlen, C_HEADS, C_HEAD_DIM)
    scale = 1.0 / math.sqrt(C_HEAD_DIM)
    outs = []
    for s0 in range(0, seqlen, Q_BLOCK):
        end = s0 + Q_BLOCK
        qb, kb, vb = q[:, s0:end], k[:, :end], v[:, :end]
        z = jnp.einsum("bqhd,bkhd->bhqk", qb, kb).astype(jnp.float32) * scale
        q_pos = s0 + jnp.arange(Q_BLOCK)
        k_pos = jnp.arange(end)
        mask = k_pos[None, :] < q_pos[:, None]
        log_1m = jnp.where(mask, jax.nn.log_sigmoid(-z), 0.0)
        between = lax.cumsum(log_1m, axis=3, reverse=True) - log_1m
        att = jnp.where(mask, jnp.exp(jax.nn.log_sigmoid(z) + between), 0.0)
        outs.append(jnp.einsum("bhqk,bkhd->bqhd", att.astype(vb.dtype), vb))
    o = jnp.concatenate(outs, axis=1).reshape(bsz, seqlen, C_WIDTH)
    return o @ w_out


def macaron_layer(x, mixer, ffn1_pre_g, ffn1_post_g, ffn1_w_in, ffn1_w_out,
                  mix_pre_g, mix_post_g, ffn2_pre_g, ffn2_post_g, ffn2_w_in, ffn2_w_out):
    x = x + FFN_HALF * rmsnorm(swiglu(rmsnorm(x, ffn1_pre_g), ffn1_w_in, ffn1_w_out), ffn1_post_g)
    x = x + rmsnorm(mixer(rmsnorm(x, mix_pre_g)), mix_post_g)
    x = x + FFN_HALF * rmsnorm(swiglu(rmsnorm(x, ffn2_pre_g), ffn2_w_in, ffn2_w_out), ffn2_post_g)
    return x


def setup_inputs(seed: int = 0) -> dict:
    key = jax.random.key(seed)
    keys = iter(jax.random.split(key, 64))
    f32 = jnp.float32

    def nrm(shape, scale):
        return scale * jax.random.normal(next(keys), shape, f32)

    def gain(n):
        return 1.0 + nrm((n,), 0.05)

    def ffn_params(prefix, d):
        d[prefix + "_pre_g"] = gain(D_MODEL)
        d[prefix + "_post_g"] = gain(D_MODEL)
        d[prefix + "_w_in"] = nrm((D_MODEL, 2 * D_FF), D_MODEL ** -0.5)
        d[prefix + "_w_out"] = nrm((D_FF, D_MODEL), D_FF ** -0.5)

    d = {"x": nrm((BATCH, SEQ, D_MODEL), 1.0)}
    ffn_params("l0_ffn1", d)
    d["l0_mix_pre_g"] = gain(D_MODEL)
    d["l0_mix_post_g"] = gain(D_MODEL)
    d["l0_w_in"] = nrm((D_MODEL, IN_PROJ), D_MODEL ** -0.5)
    d["l0_mu"] = jax.random.uniform(next(keys), (A_PROJ,), f32, 0.0, 1.0)
    d["l0_w0"] = jnp.linspace(-6.0, -1.0, A_WIDTH, dtype=f32) + nrm((A_WIDTH,), 0.1)
    d["l0_w2"] = nrm((W_LORA, A_WIDTH), 0.1)
    d["l0_a0"] = nrm((A_WIDTH,), 0.1)
    d["l0_a2"] = nrm((A_LORA, A_WIDTH), A_LORA ** -0.5)
    d["l0_g2"] = nrm((G_LORA, A_WIDTH), G_LORA ** -0.5)
    d["l0_k_k"] = 0.85 + nrm((A_WIDTH,), 0.02)
    d["l0_k_a"] = 1.0 + nrm((A_WIDTH,), 0.02)
    d["l0_r_k"] = nrm((A_HEADS, A_HEAD_DIM), 0.1)
    d["l0_lnx_g"] = gain(A_WIDTH)
    d["l0_lnx_b"] = nrm((A_WIDTH,), 0.02)
    d["l0_conv_w"] = nrm((CONV_WIDTH, B_WIDTH), CONV_WIDTH ** -0.5)
    d["l0_conv_b"] = nrm((B_WIDTH,), 0.02)
    d["l0_gate_a_w"] = nrm((B_BLOCKS, B_BLOCK_DIM, B_BLOCK_DIM), B_BLOCK_DIM ** -0.5)
    d["l0_gate_a_b"] = nrm((B_WIDTH,), 0.02)
    d["l0_gate_x_w"] = nrm((B_BLOCKS, B_BLOCK_DIM, B_BLOCK_DIM), B_BLOCK_DIM ** -0.5)
    d["l0_gate_x_b"] = nrm((B_WIDTH,), 0.02)
    a_pow = jax.random.uniform(next(keys), (B_WIDTH,), f32, 0.9, 0.999) ** (1.0 / LRU_C)
    d["l0_lambda"] = jnp.log(a_pow) - jnp.log1p(-a_pow)
    d["l0_w_out"] = nrm((A_WIDTH + B_WIDTH, D_MODEL), (A_WIDTH + B_WIDTH) ** -0.5)
    ffn_params("l0_ffn2", d)
    ffn_params("l1_ffn1", d)
    d["l1_mix_pre_g"] = gain(D_MODEL)
    d["l1_mix_post_g"] = gain(D_MODEL)
    d["l1_w_qkv"] = nrm((D_MODEL, 3 * C_WIDTH), D_MODEL ** -0.5)
    d["l1_w_out"] = nrm((C_WIDTH, D_MODEL), C_WIDTH ** -0.5)
    ffn_params("l1_ffn2", d)
    return d


def reference(x,
              l0_ffn1_pre_g, l0_ffn1_post_g, l0_ffn1_w_in, l0_ffn1_w_out,
              l0_mix_pre_g, l0_mix_post_g, l0_w_in, l0_mu, l0_w0, l0_w2, l0_a0, l0_a2,
              l0_g2, l0_k_k, l0_k_a, l0_r_k, l0_lnx_g, l0_lnx_b, l0_conv_w, l0_conv_b,
              l0_gate_a_w, l0_gate_a_b, l0_gate_x_w, l0_gate_x_b, l0_lambda, l0_w_out,
              l0_ffn2_pre_g, l0_ffn2_post_g, l0_ffn2_w_in, l0_ffn2_w_out,
              l1_ffn1_pre_g, l1_ffn1_post_g, l1_ffn1_w_in, l1_ffn1_w_out,
              l1_mix_pre_g, l1_mix_post_g, l1_w_qkv, l1_w_out,
              l1_ffn2_pre_g, l1_ffn2_post_g, l1_ffn2_w_in, l1_ffn2_w_out):
    def layer0(t):
        mixer = lambda h: even_mixer(h, l0_w_in, l0_mu, l0_w0, l0_w2, l0_a0, l0_a2, l0_g2,
                                     l0_k_k, l0_k_a, l0_r_k, l0_lnx_g, l0_lnx_b,
                                     l0_conv_w, l0_conv_b, l0_gate_a_w, l0_gate_a_b,
                                     l0_gate_x_w, l0_gate_x_b, l0_lambda, l0_w_out)
        return macaron_layer(t, mixer, l0_ffn1_pre_g, l0_ffn1_post_g, l0_ffn1_w_in, l0_ffn1_w_out,
                             l0_mix_pre_g, l0_mix_post_g,
                             l0_ffn2_pre_g, l0_ffn2_post_g, l0_ffn2_w_in, l0_ffn2_w_out)

    def layer1(t):
        mixer = lambda h: stick_breaking_mixer(h, l1_w_qkv, l1_w_out)
        return macaron_layer(t, mixer, l1_ffn1_pre_g, l1_ffn1_post_g, l1_ffn1_w_in, l1_ffn1_w_out,
                             l1_mix_pre_g, l1_mix_post_g,
                             l1_ffn2_pre_g, l1_ffn2_post_g, l1_ffn2_w_in, l1_ffn2_w_out)

    layers = [layer0, layer1]
    for i in range(DEPTH):
        x = layers[i](x)
    return x
```

```python
import math
from contextlib import ExitStack

import numpy as np
import concourse.bass as bass
import concourse.mybir as mybir
from concourse.bass_utils import run_bass_kernel_spmd

F32 = mybir.dt.float32
BF16 = mybir.dt.bfloat16
AF = mybir.ActivationFunctionType
ALU = mybir.AluOpType

SEQ = 2048
DM = 1024
DFF = 2816
NJ = 22
NORM_EPS = 1e-6
N_CORES = 8


class Buf:
    __slots__ = ("name", "w", "r")

    def __init__(self, name):
        self.name = name
        self.w = None
        self.r = {}


class T:
    def __init__(self, name, t, parts=1):
        self.name = name
        self.t = t
        self.bufs = [Buf(f"{name}.{i}") for i in range(parts)]

    def b(self, *idx):
        return [self.bufs[i] for i in idx]

    def all(self):
        return list(self.bufs)

    def __getitem__(self, key):
        return self.t[key]


class Sched:
    COMPUTE = ("pe", "act", "dve", "pool")

    def __init__(self, nc, es, n_dma_ch=20):
        self.nc = nc
        self.eng = {"pe": nc.tensor, "act": nc.scalar, "dve": nc.vector, "pool": nc.gpsimd, "sp": nc.sync}
        self.sems = {}
        self.cnt = {}
        for e in self.COMPUTE:
            self.sems[e] = es.enter_context(nc.semaphore(f"s_{e}"))
            self.cnt[e] = 0
        self.ch = {}
        self.ch_next = {}
        for q in ("sp", "pool", "act"):
            n = n_dma_ch if q != "act" else 4
            lst = []
            for i in range(n):
                key = f"d_{q}{i}"
                self.sems[key] = es.enter_context(nc.semaphore(key))
                self.cnt[key] = 0
                lst.append(key)
            self.ch[q] = lst
            self.ch_next[q] = 0
        self.seen = {e: {} for e in self.eng}
        self.n_wait = 0
        self.n_ins = 0

    def _wait(self, e, ev):
        key, val = ev
        if val <= 0:
            return
        if self.seen[e].get(key, 0) >= val:
            return
        self.seen[e][key] = val
        self.eng[e].wait_ge(self.sems[key], val)
        self.n_wait += 1

    def _deps(self, e, reads, writes):
        evs = {}

        def need(ev):
            if ev is None:
                return
            k_, v_ = ev
            if e == "pe" and k_ == "pe":
                return
            if evs.get(k_, 0) < v_:
                evs[k_] = v_

        for b in reads:
            need(b.w)
        for b in writes:
            need(b.w)
            for kv in b.r.items():
                need(kv)
        return evs

    def op(self, e, fn, reads=(), writes=()):
        evs = self._deps(e, reads, writes)
        for ev in evs.items():
            self._wait(e, ev)
        ins = fn(self.eng[e])
        self.cnt[e] += 1
        ev = (e, self.cnt[e])
        ins.then_inc(self.sems[e], 1)
        self.seen[e][e] = max(self.seen[e].get(e, 0), 0)
        for b in writes:
            b.w = ev
            b.r = {}
        for b in reads:
            if b.w is not ev:
                b.r[e] = self.cnt[e]
        self.n_ins += 1
        return ins

    def dma(self, q, out, in_, reads=(), writes=()):
        e = q
        evs = self._deps(e, reads, writes)
        key = self.ch[q][self.ch_next[q]]
        self.ch_next[q] = (self.ch_next[q] + 1) % len(self.ch[q])
        if evs.get(key, 0) < self.cnt[key]:
            evs[key] = self.cnt[key]
        for ev in evs.items():
            self._wait(e, ev)
        ins = self.eng[e].dma_start(out=out, in_=in_)
        self.cnt[key] += 16
        ins.then_inc(self.sems[key], 16)
        ev = (key, self.cnt[key])
        for b in writes:
            b.w = ev
            b.r = {}
        for b in reads:
            b.r[key] = self.cnt[key]
        self.n_ins += 1
        return ins

    def barrier(self, engines=None):
        evs = [(k_, v_) for k_, v_ in self.cnt.items() if v_ > 0]
        for e in (engines or self.eng):
            for ev in evs:
                if ev[0] == e:
                    continue
                self._wait(e, ev)


class Phase:
    def __init__(self, k):
        self.k = k
        self.es = ExitStack()

    def __enter__(self):
        self.es.__enter__()
        return self

    def __exit__(self, *a):
        self.k.s.barrier()
        return self.es.__exit__(*a)

    def sb(self, name, shape, dtype, parts=1):
        self.k.uid += 1
        t = self.es.enter_context(self.k.nc.sbuf_tensor(f"ph_{name}_{self.k.uid}", shape, dtype))
        return T(name, t, parts)


class KB:
    def __init__(self):
        self.nc = bass.Bass("TRN2", target_bir_lowering=False)
        self.es = ExitStack()
        self.s = Sched(self.nc, self.es)
        self.uid = 0

    def sb(self, name, shape, dtype, parts=1):
        t = self.es.enter_context(self.nc.sbuf_tensor("sb_" + name, shape, dtype))
        return T(name, t, parts)

    def ps(self, name, shape, dtype=F32, parts=1):
        t = self.es.enter_context(self.nc.psum_tensor("pp_" + name, shape, dtype))
        return T(name, t, parts)

    def dram_in(self, name, shape, dtype=F32):
        return T(name, self.nc.dram_tensor(name, list(shape), dtype, kind="ExternalInput").ap())

    def dram_out(self, name, shape, dtype=F32):
        return T(name, self.nc.dram_tensor(name, list(shape), dtype, kind="ExternalOutput").ap())

    def phase(self):
        return Phase(self)


def rms_rstd(k, C, src, src_bufs, SQ, PST, RSTD, ntok):
    s = k.s
    s.op("act", lambda e: e.activation(out=SQ[:, :, 0:ntok], in_=src, func=AF.Square),
         reads=src_bufs, writes=SQ.all())
    for kc in range(8):
        s.op("pe", lambda e, kc=kc: e.matmul(PST[:, 0:ntok], lhsT=C["ones_m"][:, :], rhs=SQ[:, kc, 0:ntok],
                                             start=(kc == 0), stop=(kc == 7)),
             reads=SQ.all() + C["ones_m"].all(), writes=PST.all())
    s.op("act", lambda e: e.activation(out=RSTD[:, 0:ntok], in_=PST[:, 0:ntok], func=AF.Sqrt, bias=C["eps"][:, 0:1]),
         reads=PST.all() + C["eps"].all(), writes=RSTD.all())
    s.op("dve", lambda e: e.reciprocal(out=RSTD[:, 0:ntok], in_=RSTD[:, 0:ntok]),
         reads=RSTD.all(), writes=RSTD.all())


class Bg:
    def __init__(self):
        self.q = []

    def add(self, gen, period=2):
        self.q.append([gen, period, period])

    def tick(self):
        for item in list(self.q):
            item[2] -= 1
            if item[2] <= 0:
                item[2] = item[1]
                try:
                    next(item[0])
                except StopIteration:
                    self.q.remove(item)

    def drain(self):
        while self.q:
            for item in list(self.q):
                try:
                    next(item[0])
                except StopIteration:
                    self.q.remove(item)


def rms_rstd_gen(k, C, src, src_bufs, SQ, PST, RSTD, ntok, fuse_sq=False):
    s = k.s
    s.op("act", lambda e: e.activation(out=SQ[:, :, 0:ntok], in_=src, func=AF.Square),
         reads=src_bufs, writes=SQ.all())
    if not fuse_sq:
        yield
    for kc in range(8):
        s.op("pe", lambda e, kc=kc: e.matmul(PST[:, 0:ntok], lhsT=C["ones_m"][:, :], rhs=SQ[:, kc, 0:ntok],
                                             start=(kc == 0), stop=(kc == 7)),
             reads=SQ.all() + C["ones_m"].all(), writes=PST.all())
    yield
    s.op("act", lambda e: e.activation(out=RSTD[:, 0:ntok], in_=PST[:, 0:ntok], func=AF.Sqrt, bias=C["eps"][:, 0:1]),
         reads=PST.all() + C["eps"].all(), writes=RSTD.all())
    yield
    s.op("dve", lambda e: e.reciprocal(out=RSTD[:, 0:ntok], in_=RSTD[:, 0:ntok]),
         reads=RSTD.all(), writes=RSTD.all())


def ffn_stage(k, C, XT, PS, ffns, next_gi=None):
    s = k.s
    G_ = C["gains"]
    with k.phase() as ph:
        HTG = C["HTG"]
        HTs = []
        for i in range(2):
            hv = T(f"HTv{i}", HTG.t[:, :, i * 1024:(i + 1) * 1024])
            hv.bufs = HTG.bufs[2 * i:2 * i + 2]
            HTs.append(hv)
        ACTT = ph.sb("ACTT", [128, 11, 1024], BF16, parts=22)
        YT = ph.sb("YT", [128, 8, 1024], F32, parts=16)
        SQ = [ph.sb(f"SQ{i}", [128, 8, 512], BF16) for i in range(2)]
        RSTD = [ph.sb(f"RSTD{i}", [128, 512], F32) for i in range(2)]
        WIN = [ph.sb(f"WIN{i}", [128, 8, 256], BF16) for i in range(3)]
        WOUT = [ph.sb(f"WOUT{i}", [128, 11, 128], BF16) for i in range(3)]
        SG = [ph.sb(f"SG{i}", [128, 512], F32) for i in range(2)]
        PG = [PS[0], PS[1]]
        PU = [PS[2], PS[3]]
        PY = [PS[4], PS[5]]
        PST = [PS[6], PS[7]]
        st = {"win": 0, "wout": 0, "pi": 0, "ni": 0}
        jobs = [(f, B) for f in range(len(ffns)) for B in range(2)]

        bg = Bg()

        def prenorm(ji):
            f, B = jobs[ji]
            HT = HTs[ji % 2]
            gi_pre = ffns[f][2]
            for sb_ in range(2):
                tok = B * 1024 + sb_ * 512
                xb = XT.b(B * 2 + sb_)
                n_ = st["ni"] % 2
                st["ni"] += 1
                yield from rms_rstd_gen(k, C, XT[:, :, tok:tok + 512], xb, SQ[n_], PST[n_], RSTD[n_], 512)
                for kc in range(8):
                    if kc == 4:
                        yield
                    s.op("dve", lambda e, kc=kc, tok=tok, sb_=sb_, n_=n_: e.scalar_tensor_tensor(
                        out=HT[:, kc, sb_ * 512:(sb_ + 1) * 512], in0=XT[:, kc, tok:tok + 512],
                        scalar=G_[:, gi_pre, kc:kc + 1], in1=RSTD[n_][:, :], op0=ALU.mult, op1=ALU.mult),
                        reads=xb + RSTD[n_].all() + G_.all(), writes=HT.b(sb_))

        def postnorm(ji):
            f, B = jobs[ji]
            gi_post = ffns[f][3]
            for sb_ in range(2):
                tok = B * 1024 + sb_ * 512
                rhs_sl = slice(sb_ * 512, (sb_ + 1) * 512)
                ybs = YT.b(*[dc * 2 + sb_ for dc in range(8)])
                xb = XT.b(B * 2 + sb_)
                n_ = st["ni"] % 2
                st["ni"] += 1
                yield from rms_rstd_gen(k, C, YT[:, :, rhs_sl], ybs, SQ[n_], PST[n_], RSTD[n_], 512)
                for dc in range(8):
                    if dc % 2 == 0 and dc > 0:
                        yield
                    s.op("dve", lambda e, dc=dc, rhs_sl=rhs_sl, n_=n_: e.scalar_tensor_tensor(
                        out=YT[:, dc, rhs_sl], in0=YT[:, dc, rhs_sl], scalar=G_[:, gi_post, dc:dc + 1],
                        in1=RSTD[n_][:, :], op0=ALU.mult, op1=ALU.mult),
                        reads=YT.b(dc * 2 + sb_) + RSTD[n_].all() + G_.all(), writes=YT.b(dc * 2 + sb_))
                    s.op("dve", lambda e, dc=dc, rhs_sl=rhs_sl, tok=tok: e.tensor_tensor(
                        out=XT[:, dc, tok:tok + 512], in0=XT[:, dc, tok:tok + 512], in1=YT[:, dc, rhs_sl], op=ALU.add),
                        reads=YT.b(dc * 2 + sb_) + xb, writes=xb)

        def up(ji, G, after_first=None):
            f, B = jobs[ji]
            HT = HTs[ji % 2]
            w_in_d = ffns[f][0]
            for jj in range(11):
                j = G * 11 + jj
                W = WIN[st["win"] % 3]
                st["win"] += 1
                s.dma("pool", W[:, :, :], w_in_d[j].rearrange("p (kc c) -> p kc c", kc=8), writes=W.all())
                for sb_ in range(2):
                    pg, pu, sg = PG[st["pi"] % 2], PU[st["pi"] % 2], SG[st["pi"] % 2]
                    st["pi"] += 1
                    rhs_sl = slice(sb_ * 512, (sb_ + 1) * 512)
                    for kc in range(8):
                        s.op("pe", lambda e, kc=kc, pg=pg, W=W, rhs_sl=rhs_sl: e.matmul(
                            pg[:, :], lhsT=W[:, kc, 0:128], rhs=HT[:, kc, rhs_sl], start=(kc == 0), stop=(kc == 7)),
                            reads=W.all() + HT.b(sb_), writes=pg.all())
                    for kc in range(8):
                        s.op("pe", lambda e, kc=kc, pu=pu, W=W, rhs_sl=rhs_sl: e.matmul(
                            pu[:, :], lhsT=W[:, kc, 128:256], rhs=HT[:, kc, rhs_sl], start=(kc == 0), stop=(kc == 7)),
                            reads=W.all() + HT.b(sb_), writes=pu.all())
                    s.op("act", lambda e, pg=pg, sg=sg: e.activation(out=sg[:, :], in_=pg[:, :], func=AF.Silu),
                         reads=pg.all(), writes=sg.all())
                    s.op("dve", lambda e, pu=pu, sg=sg, jj=jj, rhs_sl=rhs_sl: e.tensor_tensor(
                        out=ACTT[:, jj, rhs_sl], in0=sg[:, :], in1=pu[:, :], op=ALU.mult),
                        reads=sg.all() + pu.all(), writes=ACTT.b(jj * 2 + sb_))
                    bg.tick()
                if jj == 0 and after_first is not None:
                    after_first()

        def down(ji, G):
            f, B = jobs[ji]
            w_out_d = ffns[f][1]
            for dc in range(8):
                W = WOUT[st["wout"] % 3]
                st["wout"] += 1
                s.dma("pool", W[:, :, :], w_out_d[G, dc].rearrange("p (jj c) -> p jj c", jj=11), writes=W.all())
                for sb_ in range(2):
                    py = PY[st["pi"] % 2]
                    st["pi"] += 1
                    rhs_sl = slice(sb_ * 512, (sb_ + 1) * 512)
                    for jj in range(11):
                        s.op("pe", lambda e, jj=jj, py=py, W=W, rhs_sl=rhs_sl: e.matmul(
                            py[:, :], lhsT=W[:, jj, :], rhs=ACTT[:, jj, rhs_sl], start=(jj == 0), stop=(jj == 10)),
                            reads=W.all() + ACTT.b(jj * 2 + sb_), writes=py.all())
                    yb = YT.b(dc * 2 + sb_)
                    if G == 0:
                        s.op("act", lambda e, py=py, dc=dc, rhs_sl=rhs_sl: e.activation(
                            out=YT[:, dc, rhs_sl], in_=py[:, :], func=AF.Copy),
                            reads=py.all(), writes=yb)
                    else:
                        s.op("dve", lambda e, py=py, dc=dc, rhs_sl=rhs_sl: e.tensor_tensor(
                            out=YT[:, dc, rhs_sl], in0=YT[:, dc, rhs_sl], in1=py[:, :], op=ALU.add),
                            reads=py.all() + yb, writes=yb)
                    bg.tick()

        def next_prenorm():
            HT = HTs[0]
            for sb_ in range(2):
                tok = sb_ * 512
                xb = XT.b(sb_)
                n_ = st["ni"] % 2
                st["ni"] += 1
                yield from rms_rstd_gen(k, C, XT[:, :, tok:tok + 512], xb, SQ[n_], PST[n_], RSTD[n_], 512)
                for kc in range(8):
                    if kc == 4:
                        yield
                    s.op("dve", lambda e, kc=kc, tok=tok, sb_=sb_, n_=n_: e.scalar_tensor_tensor(
                        out=HT[:, kc, sb_ * 512:(sb_ + 1) * 512], in0=XT[:, kc, tok:tok + 512],
                        scalar=G_[:, next_gi, kc:kc + 1], in1=RSTD[n_][:, :], op0=ALU.mult, op1=ALU.mult),
                        reads=xb + RSTD[n_].all() + G_.all(), writes=HT.b(sb_))

        n = len(jobs)
        assert n % 2 == 0
        if C["ht_ready"] is not None and C["ht_ready"] == (ffns[0][2], (0, 1)):
            pass
        else:
            bg.add(prenorm(0))
            bg.drain()
        C["ht_ready"] = None
        for ji in range(n):
            up(ji, 0, after_first=(lambda ji=ji: bg.add(postnorm(ji - 1), 1)) if ji > 0 else None)
            bg.drain()
            down(ji, 0)
            if ji + 1 < n:
                bg.add(prenorm(ji + 1), 1)
            elif next_gi is not None:
                bg.add(next_prenorm(), 1)
                C["ht_ready"] = (next_gi, (0, 1))
            up(ji, 1)
            bg.drain()
            down(ji, 1)
        bg.add(postnorm(n - 1))
        bg.drain()


def prenorm_to_HT(k, C, ph, XT, HT, PS, gi_pre, col_off=0):
    s = k.s
    G_ = C["gains"]
    with k.phase() as p2:
        SQ = [p2.sb(f"SQ{i}", [128, 8, 512], BF16) for i in range(2)]
        RSTD = [p2.sb(f"RSTD{i}", [128, 512], F32) for i in range(2)]
        bg = Bg()

        def chain(tb):
            tok = tb * 512
            xb = XT.b(tb)
            yield from rms_rstd_gen(k, C, XT[:, :, tok:tok + 512], xb, SQ[tb % 2], PS[6 + tb % 2], RSTD[tb % 2], 512)
            for kc in range(8):
                if kc == 4:
                    yield
                s.op("dve", lambda e, kc=kc: e.scalar_tensor_tensor(
                    out=HT[:, kc, col_off + tok:col_off + tok + 512], in0=XT[:, kc, tok:tok + 512],
                    scalar=G_[:, gi_pre, kc:kc + 1], in1=RSTD[tb % 2][:, :], op0=ALU.mult, op1=ALU.mult),
                    reads=xb + RSTD[tb % 2].all() + G_.all(), writes=HT.b(tb))

        skip = ()
        if C["ht_ready"] is not None and C["ht_ready"][0] == gi_pre and col_off == 0:
            skip = C["ht_ready"][1]
        C["ht_ready"] = None
        for tb in range(4):
            if tb in skip:
                continue
            bg.add(chain(tb), 1)
            bg.tick()
            bg.tick()
        bg.drain()


def outproj_postnorm(k, C, XT, PS, OT, wo_d, gi_post, next_gi=None, WO=None):
    s = k.s
    G_ = C["gains"]
    with k.phase() as p3:
        if WO is None:
            WO = p3.sb("WO", [128, 8, DM], BF16)
            for kc in range(8):
                s.dma("pool", WO[:, kc, :], wo_d[kc * 128:(kc + 1) * 128, :], writes=WO.all())
        YTs = [p3.sb(f"YT{i}", [128, 8, 512], F32, parts=8) for i in range(2)]
        SQ1 = p3.sb("SQ", [128, 8, 512], BF16)
        RSTD = [p3.sb(f"RSTD{i}", [128, 512], F32) for i in range(2)]
        bg = Bg()

        def chain(tb):
            tok = tb * 512
            YT = YTs[tb % 2]
            yield from rms_rstd_gen(k, C, YT[:, :, :], YT.all(), SQ1, PS[6 + tb % 2], RSTD[tb % 2], 512, fuse_sq=True)
            xb = XT.b(tb)
            for dc in range(8):
                if dc % 2 == 0 and dc > 0:
                    yield
                s.op("dve", lambda e, dc=dc: e.scalar_tensor_tensor(
                    out=YT[:, dc, :], in0=YT[:, dc, :], scalar=G_[:, gi_post, dc:dc + 1],
                    in1=RSTD[tb % 2][:, :], op0=ALU.mult, op1=ALU.mult),
                    reads=YT.b(dc) + RSTD[tb % 2].all() + G_.all(), writes=YT.b(dc))
                s.op("dve", lambda e, dc=dc: e.tensor_tensor(
                    out=XT[:, dc, tok:tok + 512], in0=XT[:, dc, tok:tok + 512], in1=YT[:, dc, :], op=ALU.add),
                    reads=YT.b(dc) + xb, writes=xb)

        if next_gi is not None:
            RSTDn = p3.sb("RSTDn", [128, 512], F32)
        HTG = C["HTG"]

        def next_prenorm():
            for sb_ in range(2):
                tok = sb_ * 512
                xb = XT.b(sb_)
                yield from rms_rstd_gen(k, C, XT[:, :, tok:tok + 512], xb, SQ1, PS[0], RSTDn, 512, fuse_sq=True)
                for kc in range(8):
                    if kc == 4:
                        yield
                    s.op("dve", lambda e, kc=kc, tok=tok: e.scalar_tensor_tensor(
                        out=HTG[:, kc, tok:tok + 512], in0=XT[:, kc, tok:tok + 512],
                        scalar=G_[:, next_gi, kc:kc + 1], in1=RSTDn[:, :], op0=ALU.mult, op1=ALU.mult),
                        reads=xb + RSTDn.all() + G_.all(), writes=HTG.b(sb_))

        pi = 0
        for tb in range(4):
            tok = tb * 512
            YT = YTs[tb % 2]
            if tb == 3 and next_gi is not None:
                bg.add(next_prenorm(), 1)
                C["ht_ready"] = (next_gi, (0, 1))
            for dc in range(8):
                pp = PS[4 + pi % 2]
                pi += 1
                for kc in range(8):
                    s.op("pe", lambda e, kc=kc, dc=dc, pp=pp, tok=tok: e.matmul(
                        pp[:, :], lhsT=WO[:, kc, dc * 128:(dc + 1) * 128], rhs=OT[:, kc, tok:tok + 512],
                        start=(kc == 0), stop=(kc == 7)),
                        reads=WO.all() + OT.all(), writes=pp.all())
                s.op("act", lambda e, dc=dc, pp=pp, YT=YT: e.activation(out=YT[:, dc, :], in_=pp[:, :], func=AF.Copy),
                     reads=pp.all(), writes=YT.b(dc))
                bg.tick()
            bg.drain()
            bg.add(chain(tb), 1)
        bg.drain()


def attn_stage(k, C, XT, PS, wqkv_d, wo_d, gi_pre, gi_post, next_gi=None):
    s = k.s
    with k.phase() as ph:
        OT = ph.sb("OT", [128, 8, SEQ], BF16, parts=1)
        with k.phase() as pab:
            HT = C["HTG"]
            prenorm_to_HT(k, C, pab, XT, HT, PS, gi_pre)
            with k.phase() as pb:
                NEGM = pb.sb("negm", [128, 4, 512], BF16)
                ZB = pb.sb("zb", [128, 512], BF16)
                s.op("dve", lambda e: e.memset(ZB[:, :], 0.0), writes=ZB.all())
                for d in range(4):
                    s.op("pool", lambda e, d=d: e.affine_select(
                        out=NEGM[:, d, :], in_=ZB[:, :], pattern=[[1, 512]], compare_op=ALU.is_gt,
                        fill=-30000.0, base=-128 * d, channel_multiplier=-1),
                        reads=ZB.all(), writes=NEGM.all())
                QT = [pb.sb(f"QT{i}", [128, SEQ], BF16) for i in range(2)]
                KT = [pb.sb(f"KT{i}", [128, SEQ], BF16) for i in range(2)]
                V = [pb.sb(f"V{i}", [128, 16, 128], BF16) for i in range(2)]
                W = [pb.sb(f"WQKV{i}", [128, 8, 384], BF16) for i in range(1)]
                OTOK = [pb.sb(f"OTOK{i}", [128, 16, 128], BF16) for i in range(2)]
                E = [pb.sb(f"E{i}", [128, 512], F32) for i in range(3)]
                SP = [pb.sb(f"SP{i}", [128, 512], BF16) for i in range(5)]
                ATT = [pb.sb(f"ATT{i}", [128, 512], BF16) for i in range(3)]
                OACC = [pb.sb(f"OACC{i}", [128, 4, 64], F32) for i in range(2)]
                CACC = [pb.sb(f"CACC{i}", [128, 4], F32) for i in range(2)]
                FS = [pb.sb(f"FS{i}", [128, 4], F32) for i in range(4)]
                PZ = [PS[0], PS[1], PS[2], PS[3], PS[4]]
                PO = [PS[5], PS[6]]
                PP = [PS[7]]
                st = {"pi": 0}

                bg = Bg()

                def pre_hp(hp):
                    w = W[0]
                    qt, kt_, v = QT[hp % 2], KT[hp % 2], V[hp % 2]
                    s.dma("pool", w[:, :, :], wqkv_d[hp].rearrange("p (kc c) -> p kc c", kc=8), writes=w.all())
                    for which in range(2):
                        for tb in range(4):
                            pp = PP[st["pi"] % len(PP)]
                            st["pi"] += 1
                            for kc in range(8):
                                s.op("pe", lambda e, kc=kc, pp=pp, w=w, which=which, tb=tb: e.matmul(
                                    pp[:, :], lhsT=w[:, kc, which * 128:(which + 1) * 128],
                                    rhs=HT[:, kc, tb * 512:(tb + 1) * 512], start=(kc == 0), stop=(kc == 7)),
                                    reads=w.all() + HT.b(tb), writes=pp.all())
                            if which == 0:
                                s.op("dve", lambda e, pp=pp, qt=qt, tb=tb: e.tensor_scalar(
                                    out=qt[:, tb * 512:(tb + 1) * 512], in0=pp[:, :], scalar1=0.125, scalar2=None,
                                    op0=ALU.mult),
                                    reads=pp.all(), writes=qt.all())
                            else:
                                s.op("dve", lambda e, pp=pp, kt_=kt_, tb=tb: e.tensor_copy(
                                    out=kt_[:, tb * 512:(tb + 1) * 512], in_=pp[:, :]),
                                    reads=pp.all(), writes=kt_.all())
                            yield
                    for tg in range(4):
                        pp = PP[st["pi"] % len(PP)]
                        st["pi"] += 1
                        for tt in range(4):
                            tok = (tg * 4 + tt) * 128
                            for kc in range(8):
                                s.op("pe", lambda e, kc=kc, pp=pp, w=w, tt=tt, tok=tok: e.matmul(
                                    pp[:, tt * 128:(tt + 1) * 128], lhsT=HT[:, kc, tok:tok + 128],
                                    rhs=w[:, kc, 256:384], start=(kc == 0), stop=(kc == 7)),
                                    reads=w.all() + HT.b(tg), writes=pp.all())
                        s.op("dve", lambda e, pp=pp, v=v, tg=tg: e.tensor_copy(
                            out=v[:, tg * 4:(tg + 1) * 4, :], in_=pp[:, :].rearrange("p (a b) -> p a b", a=4)),
                            reads=pp.all(), writes=v.all())
                        yield

                def post_hp(hp):
                    otok = OTOK[hp % 2]
                    for tg in range(4):
                        pp = PP[st["pi"] % len(PP)]
                        st["pi"] += 1
                        for tt in range(4):
                            s.op("pe", lambda e, pp=pp, tt=tt, tg=tg, otok=otok: e.matmul(
                                pp[:, tt * 128:(tt + 1) * 128], lhsT=otok[:, tg * 4 + tt, :], rhs=C["ident"][:, :],
                                start=True, stop=True),
                                reads=otok.all() + C["ident"].all(), writes=pp.all())
                        s.op("dve", lambda e, pp=pp, tg=tg, hp=hp: e.tensor_copy(
                            out=OT[:, hp, tg * 512:(tg + 1) * 512], in_=pp[:, :]),
                            reads=pp.all(), writes=OT.all())

                units = []
                for hp in range(8):
                    for g in range(4):
                        for kt in range(4 * g + 3, -1, -1):
                            for hh in range(2):
                                units.append((hp, hh, g, kt))
                n = len(units)
                NPZ, NSP, NATT, NPO, NE = 5, 5, 3, 2, 3

                def u_(i):
                    hp, hh, g, kt = units[i]
                    d = kt - 4 * g
                    return hp, hh, g, kt, d, slice(hh * 64, (hh + 1) * 64)

                def c0_(i):
                    hp, hh, g, kt = units[i]
                    return max(kt - 4 * g, 0) * 128

                def s0_qk(i):
                    hp, hh, g, kt, d, hs = u_(i)
                    if hh == 0 and g == 0 and kt == 3:
                        if hp == 0:
                            bg.add(pre_hp(0))
                        bg.drain()
                    if hh == 0 and g == 0 and kt == 0 and hp + 1 < 8:
                        bg.add(pre_hp(hp + 1), 5)
                    pz, qt, kt_ = PZ[i % NPZ], QT[hp % 2], KT[hp % 2]
                    q0 = g * 512
                    c0 = c0_(i)
                    s.op("pe", lambda e: e.matmul(pz[:, c0:512], lhsT=kt_[hs, kt * 128:(kt + 1) * 128],
                                                  rhs=qt[hs, q0 + c0:q0 + 512], start=True, stop=(d < 0)),
                         reads=kt_.all() + qt.all(), writes=pz.all())
                    if d >= 0:
                        s.op("pe", lambda e: e.matmul(pz[:, c0:c0 + 128], lhsT=C["ident"][:, :], rhs=NEGM[:, d, c0:c0 + 128],
                                                      start=False, stop=True),
                             reads=C["ident"].all() + NEGM.all(), writes=pz.all())

                def s1_exp(i):
                    pz, e_ = PZ[i % NPZ], E[i % NE]
                    c0 = c0_(i)
                    s.op("act", lambda e: e.activation(out=e_[:, c0:512], in_=pz[:, c0:512], func=AF.Exp),
                         reads=pz.all(), writes=e_.all())

                def s2_ln(i):
                    e_, sp = E[i % NE], SP[i % NSP]
                    c0 = c0_(i)
                    s.op("act", lambda e: e.activation(out=sp[:, c0:512], in_=e_[:, c0:512], func=AF.Ln,
                                                       bias=C["one_f"][:, 0:1]),
                         reads=e_.all() + C["one_f"].all(), writes=sp.all())

                def s3_tri(i):
                    pz, sp = PZ[i % NPZ], SP[i % NSP]
                    c0 = c0_(i)
                    s.op("pe", lambda e: e.matmul(pz[:, c0:512], lhsT=C["ntri"][:, :], rhs=sp[:, c0:512], start=False,
                                                  stop=True, skip_group_check=True),
                         reads=sp.all() + C["ntri"].all(), writes=pz.all())

                def s4_att(i):
                    pz, att = PZ[i % NPZ], ATT[i % NATT]
                    c0 = c0_(i)
                    s.op("act", lambda e: e.activation(out=att[:, c0:512], in_=pz[:, c0:512], func=AF.Exp),
                         reads=pz.all(), writes=att.all())

                def s5_av(i):
                    hp, hh, g, kt, d, hs = u_(i)
                    qlo = max(d, 0)
                    sp, att, po, v = SP[i % NSP], ATT[i % NATT], PO[i % NPO], V[hp % 2]
                    for qi in range(qlo, 4):
                        s.op("pe", lambda e, qi=qi: e.matmul(
                            po[:, qi * 64:(qi + 1) * 64], lhsT=att[:, qi * 128:(qi + 1) * 128],
                            rhs=v[:, kt, hs], start=True, stop=True),
                            reads=att.all() + v.all(), writes=po.all())
                        s.op("pe", lambda e, qi=qi: e.matmul(
                            po[:, 256 + qi:257 + qi], lhsT=sp[:, qi * 128:(qi + 1) * 128],
                            rhs=C["ones_col"][:, 0:1], start=True, stop=True),
                            reads=sp.all() + C["ones_col"].all(), writes=po.all())

                def s6_acc(i):
                    hp, hh, g, kt, d, hs = u_(i)
                    qlo = max(d, 0)
                    span = hh
                    po = PO[i % NPO]
                    oacc, cacc, fs = OACC[span % 2], CACC[span % 2], FS[i % 4]
                    otok = OTOK[hp % 2]
                    if kt != 4 * g + 3:
                        s.op("act", lambda e: e.activation(out=fs[:, :], in_=cacc[:, :], func=AF.Exp, scale=-1.0),
                             reads=cacc.all(), writes=fs.all())
                        s.op("dve", lambda e: e.tensor_tensor(
                            out=cacc[:, qlo:4], in0=cacc[:, qlo:4], in1=po[:, 256 + qlo:260], op=ALU.add),
                            reads=po.all() + cacc.all(), writes=cacc.all())
                        for qi in range(qlo, 4):
                            s.op("dve", lambda e, qi=qi: e.scalar_tensor_tensor(
                                out=oacc[:, qi, :], in0=po[:, qi * 64:(qi + 1) * 64], scalar=fs[:, qi:qi + 1],
                                in1=oacc[:, qi, :], op0=ALU.mult, op1=ALU.add),
                                reads=po.all() + fs.all() + oacc.all(), writes=oacc.all())
                    else:
                        if qlo > 0:
                            s.op("dve", lambda e: e.memset(oacc[:, 0:qlo, :], 0.0), writes=oacc.all())
                            s.op("dve", lambda e: e.memset(cacc[:, 0:qlo], 0.0), writes=cacc.all())
                        s.op("dve", lambda e: e.tensor_copy(
                            out=oacc[:, qlo:4, :], in_=po[:, qlo * 64:256].rearrange("p (a b) -> p a b", b=64)),
                            reads=po.all(), writes=oacc.all())
                        s.op("dve", lambda e: e.tensor_copy(out=cacc[:, qlo:4], in_=po[:, 256 + qlo:260]),
                             reads=po.all(), writes=cacc.all())
                    if kt == 0:
                        s.op("dve", lambda e: e.tensor_copy(out=otok[:, 4 * g:4 * g + 4, hs], in_=oacc[:, :, :]),
                             reads=oacc.all(), writes=otok.all())
                        if hh == 1 and g == 3:
                            post_hp(hp)

                stages = ((0, s0_qk), (1, s1_exp), (2, s2_ln), (3, s3_tri), (4, s4_att), (5, s5_av), (6, s6_acc))
                for i in range(n + 6):
                    for lag, fn in stages:
                        if 0 <= i - lag < n:
                            fn(i - lag)
                    bg.tick()
        if "dbg_OT" in C:
            s.dma("sp", C["dbg_OT"][:, :, :], OT[:, :, :], reads=OT.all(), writes=C["dbg_OT"].all())
        outproj_postnorm(k, C, XT, PS, OT, wo_d, gi_post, next_gi)


AX = mybir.AxisListType


class Ref:
    __slots__ = ("ap", "bufs")

    def __init__(self, ap, bufs):
        self.ap = ap
        self.bufs = bufs


class _RefMaker:
    def __init__(self, t):
        self.t = t

    def __getitem__(self, key):
        return Ref(self.t.t[key], self.t.all())


def rf(t):
    return _RefMaker(t)


def _b(*refs):
    out = []
    for r in refs:
        if isinstance(r, Ref):
            out += r.bufs
    return out


def _a(x):
    return x.ap if isinstance(x, Ref) else x


def e_tt(s, eng, out, a, b, op):
    return s.op(eng, lambda E: E.tensor_tensor(out=out.ap, in0=a.ap, in1=b.ap, op=op), reads=_b(a, b), writes=out.bufs)


def e_ts(s, eng, out, a, s1, s2, op0, op1=None):
    if op1 is None:
        return s.op(eng, lambda E: E.tensor_scalar(out=out.ap, in0=a.ap, scalar1=_a(s1), scalar2=None, op0=op0),
                    reads=_b(a, s1), writes=out.bufs)
    return s.op(eng, lambda E: E.tensor_scalar(out=out.ap, in0=a.ap, scalar1=_a(s1), scalar2=_a(s2), op0=op0, op1=op1),
                reads=_b(a, s1, s2), writes=out.bufs)


def e_stt(s, eng, out, a, sc, b, op0, op1):
    return s.op(eng, lambda E: E.scalar_tensor_tensor(out=out.ap, in0=a.ap, scalar=_a(sc), in1=b.ap, op0=op0, op1=op1),
                reads=_b(a, sc, b), writes=out.bufs)


def e_act(s, out, a, func, bias=None, scale=None):
    kw = {}
    if bias is not None:
        kw["bias"] = _a(bias)
    if scale is not None:
        kw["scale"] = _a(scale)
    return s.op("act", lambda E: E.activation(out=out.ap, in_=a.ap, func=func, **kw), reads=_b(a, bias, scale),
                writes=out.bufs)


def e_mm(s, out, lhsT, rhs, start=True, stop=True):
    return s.op("pe", lambda E: E.matmul(out.ap, lhsT=lhsT.ap, rhs=rhs.ap, start=start, stop=stop),
                reads=_b(lhsT, rhs), writes=out.bufs)


def e_copy(s, eng, out, a):
    if eng == "act":
        return e_act(s, out, a, AF.Copy)
    return s.op(eng, lambda E: E.tensor_copy(out=out.ap, in_=a.ap), reads=_b(a), writes=out.bufs)


def e_memset(s, eng, out, val):
    return s.op(eng, lambda E: E.memset(out.ap, val), writes=out.bufs)


GN_EPS = 64e-5
STAGGER = 3
NEG_EXP_HALF = -0.6065306597126334


def mixer0_stage(k, C, XT, PS, D, gi_pre, gi_post, next_gi=None):
    s = k.s
    with k.phase() as ph:
        OT = ph.sb("OT", [128, 8, SEQ], BF16, parts=1)
        with k.phase() as pab:
            HT = C["HTG"]
            prenorm_to_HT(k, C, pab, XT, HT, PS, gi_pre)
            with k.phase() as pl:
                rglru_part(k, C, pl, HT, OT, PS, D)
            with k.phase() as pr:
                rwkv_part(k, C, pr, HT, OT, PS, D, XT)
        if "dbg_OT" in D:
            s.dma("sp", D["dbg_OT"][:, :, :], OT[:, :, :], reads=OT.all(), writes=D["dbg_OT"].all())
        outproj_postnorm(k, C, XT, PS, OT, D["l0_w_out"], gi_post, next_gi)


def rglru_part(k, C, p, HT, OT, PS, D):
    s = k.s
    PL = p.sb("PL", [128, 4, 8], F32)
    s.dma("sp", PL[:, :, :], D["l0_pl"][:, :].rearrange("p (c n) -> p c n", c=4), writes=PL.all())
    C1 = p.sb("C1", [128, 4], F32)
    e_act(s, rf(C1)[:, :], rf(PL)[:, :, 7], AF.Exp, scale=-1.0)
    e_act(s, rf(C1)[:, :], rf(C1)[:, :], AF.Ln, bias=rf(C["one_f"])[:, 0:1])
    e_ts(s, "dve", rf(C1)[:, :], rf(C1)[:, :], -8.0, None, ALU.mult)
    GAW = p.sb("GAW", [128, 4, 128], BF16)
    GXW = p.sb("GXW", [128, 4, 128], BF16)
    e_memset(s, "dve", rf(GAW)[:, :, :], 0.0)
    e_memset(s, "dve", rf(GXW)[:, :, :], 0.0)
    for n in range(8):
        ps_ = slice((n % 2) * 64, (n % 2) * 64 + 64)
        s.dma("pool", GAW[ps_, n // 2, ps_], D["l0_gate_a_w"][n], writes=GAW.all())
        s.dma("pool", GXW[ps_, n // 2, ps_], D["l0_gate_x_w"][n], writes=GXW.all())
    W = [p.sb(f"WL{i}", [128, 8, 256], BF16) for i in range(2)]
    XBs = [p.sb(f"XB{i}", [128, 515], F32) for i in range(2)]
    HHs = [[p.sb(f"HH{j}_{i}", [128, 512], F32) for i in range(2)] for j in range(2)]
    ts_ = [{n: p.sb(f"{n}{j}", [128, 512], F32) for n in ("GB", "XC", "R", "IG", "A", "U", "T1", "T2")} for j in range(2)]
    XCbs = [p.sb(f"XCb{j}", [128, 512], BF16) for j in range(2)]

    def unit(c, tb, j):
        w = W[j]
        XB, t_, XCb = XBs[j], ts_[j], XCbs[j]
        col = lambda n: rf(PL)[:, c, n:n + 1]
        tok = tb * 512
        px, pg = PS[2 * j], PS[2 * j + 1]
        for kc in range(8):
            e_mm(s, rf(px)[:, :], rf(w)[:, kc, 0:128], Ref(HT[:, kc, tok:tok + 512], HT.b(tb)), kc == 0, kc == 7)
        for kc in range(8):
            e_mm(s, rf(pg)[:, :], rf(w)[:, kc, 128:256], Ref(HT[:, kc, tok:tok + 512], HT.b(tb)), kc == 0, kc == 7)
        if tb == 0:
            e_memset(s, "dve", rf(XB)[:, 0:3], 0.0)
        else:
            e_copy(s, "dve", rf(XB)[:, 0:3], rf(XB)[:, 512:515])
        yield
        e_copy(s, "act", rf(XB)[:, 3:515], rf(px)[:, :])
        e_copy(s, "act", rf(t_["GB"])[:, :], rf(pg)[:, :])
        yield
        XC = t_["XC"]
        e_ts(s, "dve", rf(XC)[:, :], rf(XB)[:, 3:515], col(3), col(4), ALU.mult, ALU.add)
        for i in range(3):
            e_stt(s, "dve", rf(XC)[:, :], rf(XB)[:, i:i + 512], col(i), rf(XC)[:, :], ALU.mult, ALU.add)
        GB, T2 = t_["GB"], t_["T2"]
        e_act(s, rf(T2)[:, :], rf(GB)[:, :], AF.Gelu_apprx_tanh)
        yield
        e_copy(s, "act", rf(XCb)[:, :], rf(XC)[:, :])
        yield
        pr_, pig = PS[4 + 2 * j], PS[5 + 2 * j]
        e_mm(s, rf(pr_)[:, :], rf(GAW)[:, c, :], rf(XCb)[:, :])
        e_mm(s, rf(pig)[:, :], rf(GXW)[:, c, :], rf(XCb)[:, :])
        yield
        e_act(s, rf(t_["R"])[:, :], rf(pr_)[:, :], AF.Sigmoid, bias=col(5))
        e_act(s, rf(t_["IG"])[:, :], rf(pig)[:, :], AF.Sigmoid, bias=col(6))
        yield
        A = t_["A"]
        e_act(s, rf(A)[:, :], rf(t_["R"])[:, :], AF.Exp, scale=rf(C1)[:, c:c + 1])
        T1, U = t_["T1"], t_["U"]
        e_tt(s, "pool", rf(U)[:, :], rf(t_["IG"])[:, :], rf(XC)[:, :], ALU.mult)
        yield
        e_tt(s, "dve", rf(T1)[:, :], rf(A)[:, :], rf(A)[:, :], ALU.mult)
        yield
        e_ts(s, "dve", rf(T1)[:, :], rf(T1)[:, :], -1.0, 1.0, ALU.mult, ALU.add)
        yield
        e_act(s, rf(T1)[:, :], rf(T1)[:, :], AF.Sqrt)
        yield
        e_tt(s, "dve", rf(U)[:, :], rf(U)[:, :], rf(T1)[:, :], ALU.mult)
        yield
        H = HHs[j][tb % 2]
        Hp = HHs[j][(tb + 1) % 2]
        init = 0.0 if tb == 0 else Hp[:, 511:512]
        s.op("dve", lambda E: E.tensor_tensor_scan(
            out=H[:, :], data0=A[:, :], data1=U[:, :], initial=init, op0=ALU.mult, op1=ALU.add),
            reads=A.all() + U.all() + (Hp.all() if tb else []), writes=H.all())
        yield
        s.op("dve", lambda E: E.tensor_tensor(
            out=OT[:, 4 + c, tok:tok + 512], in0=H[:, :], in1=T2[:, :], op=ALU.mult),
            reads=H.all() + T2.all(), writes=OT.all())

    for cp in range(2):
        for j in range(2):
            c = 2 * cp + j
            s.dma("pool", W[j][:, :, :], D["l0_w_lru"][c].rearrange("p (kc n) -> p kc n", kc=8), writes=W[j].all())
        for tb in range(4):
            gens = [unit(2 * cp + j, tb, j) for j in range(2)]
            alive = [True, True]
            while any(alive):
                for j in range(2):
                    if alive[j]:
                        try:
                            next(gens[j])
                        except StopIteration:
                            alive[j] = False


def rwkv_part(k, C, p, HT, OT, PS, D, XT):
    s = k.s
    spill = D["xt_spill"]
    for kc in range(8):
        s.dma("sp", spill[:, kc, :], XT[:, kc, :], reads=XT.all(), writes=spill.all())
    PH = p.sb("PH", [128, 4, 8], F32)
    s.dma("sp", PH[:, :, :], D["l0_ph"][:, :].rearrange("p (h n) -> p h n", h=4), writes=PH.all())
    OM = p.sb("OM", [128, 4, 4], F32)
    e_ts(s, "dve", rf(OM)[:, :, 0:3], rf(PH)[:, :, 0:3], -1.0, 1.0, ALU.mult, ALU.add)
    e_ts(s, "dve", rf(OM)[:, :, 3:4], rf(PH)[:, :, 6:7], -1.0, 1.0, ALU.mult, ALU.add)
    RKb = p.sb("RKb", [128, 4], BF16)
    e_copy(s, "dve", rf(RKb)[:, :], rf(PH)[:, :, 7])
    MUL = p.sb("MUL", [128, 3], F32)
    s.dma("sp", MUL[:, :], D["l0_mul"][:, :], writes=MUL.all())
    OML = p.sb("OML", [128, 3], F32)
    e_ts(s, "dve", rf(OML)[:, :], rf(MUL)[:, :], -1.0, 1.0, ALU.mult, ALU.add)
    LNGBs = [p.sb(f"LNGB{i}", [128, 2, 64], F32) for i in range(2)]
    W2 = p.sb("W2", [64, 512], BF16)
    A2 = p.sb("A2", [64, 512], BF16)
    G2 = p.sb("G2", [128, 512], BF16)
    s.dma("pool", W2[:, :], D["l0_w2"][:, :], writes=W2.all())
    s.dma("pool", A2[:, :], D["l0_a2"][:, :], writes=A2.all())
    s.dma("pool", G2[:, :], D["l0_g2"][:, :], writes=G2.all())
    ob = C["ones_bf"]
    BLK = p.sb("BLK", [128, 128], BF16)
    e_memset(s, "dve", rf(BLK)[:, :], 0.0)
    e_memset(s, "dve", rf(BLK)[0:64, 0:64], 1.0)
    e_memset(s, "dve", rf(BLK)[64:128, 64:128], 1.0)
    M512 = p.sb("M512", [128, 512], BF16)
    MUS = p.sb("MUS", [128, 512], BF16)
    MUI = p.sb("MUI", [128, 512], BF16)
    MLS = p.sb("MLS", [128, 512], BF16)
    ID8 = p.sb("ID8", [128, 512], BF16)
    for hh in range(2):
        hs = slice(hh * 64, hh * 64 + 64)
        for dst, pat, cmp_, cm in ((M512, [[0, 8], [1, 64]], ALU.is_gt, 0), (MUS, [[0, 8], [1, 64]], ALU.is_gt, -1),
                                   (MUI, [[0, 8], [1, 64]], ALU.is_ge, -1), (MLS, [[0, 8], [-1, 64]], ALU.is_gt, 1),
                                   (ID8, [[0, 8], [-1, 64]], ALU.is_equal, 1)):
            s.op("pool", lambda E, dst=dst, pat=pat, cmp_=cmp_, cm=cm, hs=hs: E.affine_select(
                out=dst[hs, :], in_=ob[hs, :], pattern=pat, compare_op=cmp_, fill=0.0, base=0, channel_multiplier=cm),
                reads=ob.all(), writes=dst.all())
    ident = C["ident"]

    TW = p.sb("TW", [64, SEQ], BF16)
    AL = p.sb("AL", [64, SEQ], BF16)
    SGL = p.sb("SGL", [128, SEQ], BF16)
    with k.phase() as p0:
        WLo = p0.sb("WLo", [128, 8, 256], BF16)
        s.dma("pool", WLo[:, :, :], D["l0_w_lora"][:, :].rearrange("p (kc n) -> p kc n", kc=8), writes=WLo.all())
        PAl = [p0.sb(f"PAl{i}", [128, 513], F32) for i in range(3)]
        TMPl = [p0.sb(f"TMPl{i}", [128, 512], F32) for i in range(3)]

        def lora_chain(which, c0, c1, npart, dst):
            PA, tmpl = PAl[which], TMPl[which]
            for tb in range(4):
                tok = tb * 512
                pp = PS[which * 2 + tb % 2]
                for kc in range(8):
                    e_mm(s, rf(pp)[0:npart, :], rf(WLo)[:, kc, c0:c1], Ref(HT[:, kc, tok:tok + 512], HT.b(tb)), kc == 0, kc == 7)
                if tb == 0:
                    e_memset(s, "dve", rf(PA)[0:npart, 0:1], 0.0)
                else:
                    e_copy(s, "dve", rf(PA)[0:npart, 0:1], rf(PA)[0:npart, 512:513])
                yield
                e_copy(s, "act", rf(PA)[0:npart, 1:513], rf(pp)[0:npart, :])
                yield
                e_act(s, rf(tmpl)[0:npart, :], rf(PA)[0:npart, 0:512], AF.Copy, scale=rf(MUL)[0:npart, which:which + 1])
                yield
                e_stt(s, "dve", rf(tmpl)[0:npart, :], rf(PA)[0:npart, 1:513], rf(OML)[0:npart, which:which + 1],
                      rf(tmpl)[0:npart, :], ALU.mult, ALU.add)
                yield
                if which == 0:
                    e_act(s, rf(dst)[:, tok:tok + 512], rf(tmpl)[0:64, :], AF.Tanh)
                elif which == 1:
                    e_copy(s, "act", rf(dst)[:, tok:tok + 512], rf(tmpl)[0:64, :])
                else:
                    e_act(s, rf(dst)[:, tok:tok + 512], rf(tmpl)[:, :], AF.Sigmoid)
                yield

        lbg = Bg()
        for which, (c0, c1, npart, dst) in enumerate(((0, 64, 64, TW), (64, 128, 64, AL), (128, 256, 128, SGL))):
            lbg.add(lora_chain(which, c0, c1, npart, dst), 1)
        lbg.drain()

    s.barrier()
    XTf = XT.t
    XTb = XT.t.bitcast(BF16)
    f32n = ("r", "k", "SIG", "A", "KKN", "KH", "CUM", "EC", "EX", "EN", "TMP")
    b16n = ("Rt", "At", "Bt", "Kt", "Bh", "Kh", "RK", "VT", "KK2")
    t64n = ("V64", "BH64", "KH64", "N", "Q", "N2", "Q2", "XA", "LAK", "ARB", "ARK")
    sets = []
    for S in range(2):
        B = {}
        if S == 0:
            B["WH"] = p.sb("WH", [128, 8, 384], BF16)
            B["PA"] = [p.sb(f"PA{i}", [128, 513], F32) for i in range(3)]
            F = {n: p.sb("f_" + n, [128, 512], F32) for n in f32n}
            Bf = {n: p.sb("b_" + n, [128, 512], BF16) for n in b16n}
            T64 = {n: p.sb("t_" + n, [128, 512], BF16) for n in t64n}
        else:
            B["WH"] = T("WH1", XTb[:, 7, 0:3072].rearrange("p (kc n) -> p kc n", kc=8))
            B["PA"] = [T(f"PA1_{i}", XTf[:, 3, i * 513:(i + 1) * 513]) for i in range(3)]
            F = {n: T("f1_" + n, XTf[:, i // 4, (i % 4) * 512:(i % 4) * 512 + 512]) for i, n in enumerate(f32n)}
            bl = list(b16n) + list(t64n)
            vb = {n: T("b1_" + n, XTb[:, 4 + i // 8, (i % 8) * 512:(i % 8) * 512 + 512]) for i, n in enumerate(bl)}
            Bf = {n: vb[n] for n in b16n}
            T64 = {n: vb[n] for n in t64n}
        F["EH"] = F["TMP"]
        F["BA"] = F["SIG"]
        T64["YA"] = T64["LAK"]
        B["F"], B["Bf"], B["T64"] = F, Bf, T64
        B["R0"], B["YF"], B["YQ"], B["GT"] = F["SIG"], F["EX"], F["EN"], F["KH"]
        B["ST"] = p.sb(f"ST{S}", [128, 8, 4], F32)
        B["RKS"] = p.sb(f"RKS{S}", [128, 8], F32)
        B["Pf"] = p.sb(f"Pf{S}", [128, 64], F32)
        B["Pb"] = p.sb(f"Pb{S}", [128, 64], BF16)
        B["RR"] = p.sb(f"RR{S}", [128, 64], BF16)
        B["UB"] = p.sb(f"UB{S}", [128, 64], BF16)
        B["LNGB"] = LNGBs[S]
        B["PY"] = PS[4 + S]
        B["PT1"] = PS[6 + S]
        sets.append(B)
    b3 = lambda r_: Ref(r_.ap.rearrange("p (a b) -> p a b", a=8), r_.bufs)
    HS = (slice(0, 64), slice(64, 128))
    st = {"sci": 0}

    def newps():
        st["sci"] += 1
        return PS[st["sci"] % 4]

    def mm2(out_t, col0, ncol, lhs_fn, rhs_fn, start=True, stop=True):
        for hs in HS:
            e_mm(s, rf(out_t)[hs, col0:col0 + ncol], lhs_fn(hs), rhs_fn(hs), start, stop)

    def unit(hp, gq, B):
        F, Bf, T64, PA, w = B["F"], B["Bf"], B["T64"], B["PA"], B["WH"]
        R0, YF, YQ, GT, ST, RKS = B["R0"], B["YF"], B["YQ"], B["GT"], B["ST"], B["RKS"]
        Pf, Pb, RR, UB, LNGB = B["Pf"], B["Pb"], B["RR"], B["UB"], B["LNGB"]
        hc = lambda n: rf(PH)[:, hp, n:n + 1]
        tok = gq * 512
        for which, nm in enumerate(("r", "k", "v")):
            pp = newps()
            for kc in range(8):
                e_mm(s, rf(pp)[:, :], rf(w)[:, kc, which * 128:(which + 1) * 128],
                     Ref(HT[:, kc, tok:tok + 512], HT.b(gq)), kc == 0, kc == 7)
            pa = PA[which]
            if gq == 0:
                e_memset(s, "dve", rf(pa)[:, 0:1], 0.0)
            else:
                e_copy(s, "dve", rf(pa)[:, 0:1], rf(pa)[:, 512:513])
            e_copy(s, "act", rf(pa)[:, 1:513], rf(pp)[:, :])
            yield
            tmp_ = rf(F["TMP"])[:, :] if which != 1 else rf(F["CUM"])[:, :]
            e_act(s, tmp_, rf(pa)[:, 0:512], AF.Copy, scale=hc(which))
            dst_ = rf(Bf["VT"])[:, :] if nm == "v" else rf(F[nm])[:, :]
            e_stt(s, "dve", dst_, rf(pa)[:, 1:513], rf(OM)[:, hp, which:which + 1], tmp_, ALU.mult, ALU.add)
        r_, k_ = rf(F["r"])[:, :], rf(F["k"])[:, :]
        pz = newps()
        e_mm(s, rf(pz)[:, :], rf(W2)[:, hp * 128:(hp + 1) * 128], rf(TW)[:, tok:tok + 512])
        e_act(s, rf(F["SIG"])[:, :], rf(pz)[:, :], AF.Sigmoid, bias=hc(3))
        pz2 = newps()
        e_mm(s, rf(pz2)[:, :], rf(A2)[:, hp * 128:(hp + 1) * 128], rf(AL)[:, tok:tok + 512])
        e_act(s, rf(F["A"])[:, :], rf(pz2)[:, :], AF.Sigmoid, bias=hc(4))
        yield
        e_ts(s, "dve", rf(F["KKN"])[:, :], k_, hc(5), None, ALU.mult)
        e_act(s, rf(Bf["KK2"])[:, :], rf(F["KKN"])[:, :], AF.Square)
        yield
        pz = newps()
        e_mm(s, rf(pz)[:, :], rf(BLK)[:, :], rf(Bf["KK2"])[:, :])
        e_act(s, rf(F["TMP"])[:, :], rf(pz)[:, :], AF.Sqrt)
        yield
        e_ts(s, "dve", rf(F["TMP"])[:, :], rf(F["TMP"])[:, :], 1e-12, None, ALU.max)
        s.op("dve", lambda E: E.reciprocal(out=F["TMP"][:, :], in_=F["TMP"][:, :]), reads=F["TMP"].all(),
             writes=F["TMP"].all())
        e_tt(s, "dve", rf(F["KKN"])[:, :], rf(F["KKN"])[:, :], rf(F["TMP"])[:, :], ALU.mult)
        e_act(s, rf(F["KH"])[:, :], rf(F["A"])[:, :], AF.Identity, bias=rf(OM)[:, hp, 3:4], scale=hc(6))
        e_tt(s, "pool", rf(F["KH"])[:, :], rf(F["KH"])[:, :], k_, ALU.mult)
        yield
        s.op("dve", lambda E: E.tensor_tensor_scan(out=F["CUM"][:, :], data0=M512[:, :], data1=F["SIG"][:, :],
                                                   initial=0.0, op0=ALU.mult, op1=ALU.add),
             reads=M512.all() + F["SIG"].all(), writes=F["CUM"].all())
        yield
        cum = rf(F["CUM"])[:, :]
        e_act(s, rf(F["EC"])[:, :], cum, AF.Exp, scale=NEG_EXP_HALF)
        e_tt(s, "dve", rf(F["EX"])[:, :], cum, rf(F["SIG"])[:, :], ALU.subtract)
        e_act(s, rf(F["EN"])[:, :], cum, AF.Exp, scale=-NEG_EXP_HALF)
        cum3 = b3(cum)
        cend = Ref(cum3.ap[:, :, 63:64].to_broadcast([128, 8, 64]), cum.bufs)
        e_tt(s, "dve", b3(rf(F["EH"])[:, :]), cend, cum3, ALU.subtract)
        yield
        e_act(s, rf(F["EX"])[:, :], rf(F["EX"])[:, :], AF.Exp, scale=NEG_EXP_HALF)
        e_act(s, rf(F["EH"])[:, :], rf(F["EH"])[:, :], AF.Exp, scale=NEG_EXP_HALF)
        e_tt(s, "pool", rf(Bf["Rt"])[:, :], r_, rf(F["EC"])[:, :], ALU.mult)
        e_tt(s, "pool", rf(Bf["RK"])[:, :], r_, rf(F["KH"])[:, :], ALU.mult)
        e_tt(s, "dve", rf(F["BA"])[:, :], rf(F["KKN"])[:, :], rf(F["A"])[:, :], ALU.mult)
        yield
        e_stt(s, "dve", rf(Bf["At"])[:, :], rf(F["KKN"])[:, :], -1.0, rf(F["EX"])[:, :], ALU.mult, ALU.mult)
        e_tt(s, "dve", rf(Bf["Bt"])[:, :], rf(F["BA"])[:, :], rf(F["EN"])[:, :], ALU.mult)
        e_tt(s, "pool", rf(Bf["Bh"])[:, :], rf(F["BA"])[:, :], rf(F["EH"])[:, :], ALU.mult)
        e_tt(s, "dve", rf(Bf["Kt"])[:, :], rf(F["KH"])[:, :], rf(F["EN"])[:, :], ALU.mult)
        e_tt(s, "pool", rf(Bf["Kh"])[:, :], rf(F["KH"])[:, :], rf(F["EH"])[:, :], ALU.mult)
        yield
        blk = lambda n, c8, hs: rf(Bf[n])[hs, c8 * 64:(c8 + 1) * 64]
        tb_ = lambda n, c8, hs: rf(T64[n])[hs, c8 * 64:(c8 + 1) * 64]
        idh = lambda hs: rf(ident)[hs, hs]
        for src, dst in (("VT", "V64"), ("Bh", "BH64"), ("Kh", "KH64")):
            pt = newps()
            for c8 in range(8):
                mm2(pt, c8 * 64, 64, lambda hs, c8=c8, src=src: blk(src, c8, hs), idh)
            e_copy(s, "act", rf(T64[dst])[:, :], rf(pt)[:, :])
            yield
        for lh, rh, mask, dst in (("Bt", "At", MUS, "N"), ("At", "Bt", MLS, "Q"), ("Kt", "At", MUS, "LAK"),
                                  ("Bt", "Rt", MUI, "ARB"), ("Kt", "Rt", MUI, "ARK")):
            pt = newps()
            for c8 in range(8):
                mm2(pt, c8 * 64, 64, lambda hs, c8=c8, lh=lh: blk(lh, c8, hs), lambda hs, c8=c8, rh=rh: blk(rh, c8, hs))
            e_tt(s, "dve", rf(T64[dst])[:, :], rf(pt)[:, :], rf(mask)[:, :], ALU.mult)
            yield
        e_tt(s, "pool", rf(T64["XA"])[:, :], rf(T64["N"])[:, :], rf(ID8)[:, :], ALU.add)
        Pn, Qn, Pn2, Qn2 = "N", "Q", "N2", "Q2"
        for lvl in range(1, 6):
            pq = newps()
            for c8 in range(8):
                mm2(pq, c8 * 64, 64, lambda hs, c8=c8, Pn=Pn: tb_(Pn, c8, hs), lambda hs, c8=c8, Qn=Qn: tb_(Qn, c8, hs))
            e_copy(s, "act", rf(T64[Qn2])[:, :], rf(pq)[:, :])
            if lvl < 5:
                pp_ = newps()
                for c8 in range(8):
                    mm2(pp_, c8 * 64, 64, lambda hs, c8=c8, Qn=Qn: tb_(Qn, c8, hs),
                        lambda hs, c8=c8, Pn=Pn: tb_(Pn, c8, hs))
                e_copy(s, "act", rf(T64[Pn2])[:, :], rf(pp_)[:, :])
            yield
            px = newps()
            for c8 in range(8):
                mm2(px, c8 * 64, 64, lambda hs, c8=c8, Qn2=Qn2: tb_(Qn2, c8, hs), lambda hs, c8=c8: tb_("XA", c8, hs))
            e_tt(s, "dve", rf(T64["XA"])[:, :], rf(T64["XA"])[:, :], rf(px)[:, :], ALU.add)
            yield
            Pn, Pn2 = Pn2, Pn
            Qn, Qn2 = Qn2, Qn
        pr0 = newps()
        for c8 in range(8):
            mm2(pr0, c8 * 64, 64, lambda hs, c8=c8: tb_("LAK", c8, hs), lambda hs, c8=c8: tb_("V64", c8, hs))
        e_copy(s, "act", rf(R0)[:, :], rf(pr0)[:, :])
        pg = newps()
        e_mm(s, rf(pg)[:, :], rf(G2)[:, hp * 128:(hp + 1) * 128], rf(SGL)[:, tok:tok + 512])
        e_copy(s, "act", rf(GT)[:, :], rf(pg)[:, :])
        yield
        PY, PT1 = B["PY"], B["PT1"]
        pbh = lambda hs: rf(Pb)[hs, :]
        ubh = lambda hs: rf(UB)[hs, :]
        for c8 in range(8):
            mm2(PT1, 0, 64, lambda hs: blk("At", c8, hs), pbh)
            mm2(PY, c8 * 64, 64, lambda hs: blk("Rt", c8, hs), pbh, True, False)
            e_tt(s, "dve", rf(RR)[:, :], rf(PT1)[:, 0:64], rf(R0)[:, c8 * 64:(c8 + 1) * 64], ALU.add)
            yield
            mm2(PT1, 64, 64, lambda hs: tb_("XA", c8, hs), lambda hs: rf(RR)[hs, :])
            e_copy(s, "act", rf(UB)[:, :], rf(PT1)[:, 64:128])
            yield
            mm2(PT1, 128, 64, lambda hs: tb_("KH64", c8, hs), lambda hs: tb_("V64", c8, hs), True, False)
            mm2(PT1, 128, 64, lambda hs: tb_("BH64", c8, hs), ubh, False, True)
            mm2(PY, c8 * 64, 64, lambda hs: tb_("ARB", c8, hs), ubh, False, False)
            mm2(PY, c8 * 64, 64, lambda hs: tb_("ARK", c8, hs), lambda hs: tb_("V64", c8, hs), False, True)
            e_stt(s, "dve", rf(Pf)[:, :], rf(Pf)[:, :], rf(F["EC"])[:, c8 * 64 + 63:c8 * 64 + 64], rf(PT1)[:, 128:192],
                  ALU.mult, ALU.add)
            e_copy(s, "act", rf(Pb)[:, :], rf(Pf)[:, :])
            yield
        e_copy(s, "act", rf(YF)[:, :], rf(PY)[:, :])
        yf3 = b3(rf(YF)[:, :])
        yq3 = b3(rf(YQ)[:, :])
        prk = newps()
        for c8 in range(8):
            mm2(prk, c8, 1, lambda hs: blk("RK", c8, hs), lambda hs: rf(RKb)[hs, hp:hp + 1])
        e_copy(s, "act", rf(RKS)[:, :], rf(prk)[:, 0:8])
        yield
        s.op("dve", lambda E: E.tensor_reduce(out=ST[:, :, 0], in_=YF[:, :].rearrange("p (a b) -> p a b", a=8),
                                              axis=AX.X, op=ALU.add), reads=YF.all(), writes=ST.all())
        e_act(s, rf(YQ)[:, :], rf(YF)[:, :], AF.Square)
        yield
        s.op("dve", lambda E: E.tensor_reduce(out=ST[:, :, 1], in_=YQ[:, :].rearrange("p (a b) -> p a b", a=8),
                                              axis=AX.X, op=ALU.add), reads=YQ.all(), writes=ST.all())
        e_ts(s, "dve", rf(ST)[:, :, 2], rf(ST)[:, :, 0], 1.0 / 64, None, ALU.mult)
        e_tt(s, "dve", rf(ST)[:, :, 0], rf(ST)[:, :, 2], rf(ST)[:, :, 2], ALU.mult)
        e_stt(s, "dve", rf(ST)[:, :, 1], rf(ST)[:, :, 1], 1.0 / 64, rf(ST)[:, :, 0], ALU.mult, ALU.subtract)
        e_ts(s, "dve", rf(ST)[:, :, 1], rf(ST)[:, :, 1], GN_EPS, None, ALU.add)
        e_act(s, rf(ST)[:, :, 3], rf(ST)[:, :, 1], AF.Sqrt)
        yield
        s.op("dve", lambda E: E.reciprocal(out=ST[:, :, 3], in_=ST[:, :, 3]), reads=ST.all(), writes=ST.all())
        mean_b = Ref(ST[:, :, 2:3].to_broadcast([128, 8, 64]), ST.all())
        rstd_b = Ref(ST[:, :, 3:4].to_broadcast([128, 8, 64]), ST.all())
        e_tt(s, "dve", yf3, yf3, mean_b, ALU.subtract)
        e_tt(s, "dve", yf3, yf3, rstd_b, ALU.mult)
        rks_b = Ref(RKS[:, :].rearrange("p (a b) -> p a b", b=1).to_broadcast([128, 8, 64]), RKS.all())
        e_tt(s, "dve", yq3, b3(rf(T64["V64"])[:, :]), rks_b, ALU.mult)
        lng = Ref(LNGB[:, 0:1, :].to_broadcast([128, 8, 64]), LNGB.all())
        lnb = Ref(LNGB[:, 1:2, :].to_broadcast([128, 8, 64]), LNGB.all())
        yield
        e_tt(s, "pool", yf3, yf3, lng, ALU.mult)
        e_tt(s, "pool", yf3, yf3, lnb, ALU.add)
        yield
        e_tt(s, "dve", rf(T64["YA"])[:, :], rf(YF)[:, :], rf(YQ)[:, :], ALU.add)
        yield
        pt = newps()
        for c8 in range(8):
            mm2(pt, c8 * 64, 64, lambda hs: tb_("YA", c8, hs), idh)
        s.op("dve", lambda E: E.tensor_tensor(
            out=OT[:, hp, tok:tok + 512], in0=pt[:, :], in1=GT[:, :], op=ALU.mult),
            reads=pt.all() + GT.all(), writes=OT.all())
        yield

    def stream(S, hps):
        B = sets[S]
        for hp in hps:
            w = B["WH"]
            s.dma("pool", w[:, :, :], D["l0_w_hp"][hp].rearrange("p (kc n) -> p kc n", kc=8), writes=w.all())
            LNGB = B["LNGB"]
            for hh in range(2):
                h = 2 * hp + hh
                s.dma("sp", LNGB[HS[hh], 0, :], D["l0_lnx_g"][h * 64:(h + 1) * 64].partition_broadcast(64),
                      writes=LNGB.all())
                s.dma("sp", LNGB[HS[hh], 1, :], D["l0_lnx_b"][h * 64:(h + 1) * 64].partition_broadcast(64),
                      writes=LNGB.all())
            e_memset(s, "dve", rf(B["Pf"])[:, :], 0.0)
            e_memset(s, "dve", rf(B["Pb"])[:, :], 0.0)
            yield
            for gq in range(4):
                yield from unit(hp, gq, B)

    gens = [stream(0, (0, 2)), stream(1, (1, 3))]
    alive = [True, True]
    first = True
    while any(alive):
        for S in range(2):
            if alive[S]:
                try:
                    next(gens[S])
                except StopIteration:
                    alive[S] = False
            if first and S == 0:
                for _ in range(STAGGER):
                    next(gens[0])
                first = False
    s.barrier()
    for kc in range(8):
        s.dma("sp", XT[:, kc, :], spill[:, kc, :], reads=spill.all(), writes=XT.all())


GAIN_NAMES = ["l0_ffn1_pre_g", "l0_ffn1_post_g", "l0_mix_pre_g", "l0_mix_post_g", "l0_ffn2_pre_g", "l0_ffn2_post_g",
              "l1_ffn1_pre_g", "l1_ffn1_post_g", "l1_mix_pre_g", "l1_mix_post_g", "l1_ffn2_pre_g", "l1_ffn2_post_g"]
HALF_GAINS = [1, 5, 7, 11]


def build_program(stages=("f01", "m0", "f02f11", "m1", "f12"), dbg=False):
    k = KB()
    nc = k.nc
    s = k.s
    xT_d = k.dram_in("xT", [DM, SEQ])
    gains_d = k.dram_in("gains", [128, 12 * 8])
    ffn_d = {}
    for nm in ("l0_ffn1", "l0_ffn2", "l1_ffn1", "l1_ffn2"):
        ffn_d[nm] = (k.dram_in(nm + "_w_in", [NJ, 128, 2048]), k.dram_in(nm + "_w_out", [2, 8, 128, 11 * 128]))
    D = {}
    for nm, shp in (("l0_pl", [128, 32]), ("l0_ph", [128, 32]), ("l0_mul", [128, 3]), ("l0_lnx_g", [512]),
                    ("l0_lnx_b", [512]), ("l0_w2", [64, 512]), ("l0_a2", [64, 512]), ("l0_g2", [128, 512]),
                    ("l0_gate_a_w", [8, 64, 64]), ("l0_gate_x_w", [8, 64, 64]), ("l0_w_lru", [4, 128, 8 * 256]),
                    ("l0_w_lora", [128, 8 * 256]), ("l0_w_hp", [4, 128, 8 * 384]), ("l0_w_out", [DM, DM])):
        D[nm] = k.dram_in(nm, shp)
    D["xt_spill"] = T("xt_spill", nc.dram_tensor("xt_spill", [128, 8, SEQ], F32, kind="Internal").ap())
    if dbg:
        D["dbg_OT"] = k.dram_out("dbg_OT", [128, 8, SEQ], BF16)
    wqkv_d = k.dram_in("l1_w_qkv", [8, 128, 8 * 384])
    l1_wo_d = k.dram_in("l1_w_out", [DM, DM])
    outT_d = k.dram_out("outT", [DM, SEQ])

    with k.es:
        XT = k.sb("XT", [128, 8, SEQ], F32, parts=4)
        C = {}
        C["gains"] = k.sb("gains", [128, 12, 8], F32)
        C["ones_m"] = k.sb("ones_m", [128, 128], BF16)
        PS = [k.ps(f"ps{i}", [128, 512]) for i in range(8)]
        C["HTG"] = k.sb("HTG", [128, 8, SEQ], BF16, parts=4)
        C["ht_ready"] = None

        s.op("dve", lambda e: e.memset(C["ones_m"][:, :], 1.0 / DM), writes=C["ones_m"].all())
        C["one_f"] = k.sb("one_f", [128, 1], F32)
        C["ones_col"] = k.sb("ones_col", [128, 1], BF16)
        C["ones_bf"] = k.sb("ones_bf", [128, 512], BF16)
        C["ident"] = k.sb("ident", [128, 128], BF16)
        C["ntri"] = k.sb("ntri", [128, 128], BF16)
        s.op("dve", lambda e: e.memset(C["one_f"][:, :], 1.0), writes=C["one_f"].all())
        s.op("dve", lambda e: e.memset(C["ones_col"][:, :], 1.0), writes=C["ones_col"].all())
        s.op("dve", lambda e: e.memset(C["ones_bf"][:, :], 1.0), writes=C["ones_bf"].all())
        s.op("pool", lambda e: e.affine_select(out=C["ident"][:, :], in_=C["ones_bf"][:, 0:128], pattern=[[-1, 128]],
                                               compare_op=ALU.is_equal, fill=0.0, base=0, channel_multiplier=1),
             reads=C["ones_bf"].all(), writes=C["ident"].all())
        s.op("pool", lambda e: e.affine_select(out=C["ntri"][:, :], in_=C["ones_bf"][:, 0:128], pattern=[[-1, 128]],
                                               compare_op=ALU.is_ge, fill=0.0, base=0, channel_multiplier=1),
             reads=C["ones_bf"].all(), writes=C["ntri"].all())
        s.op("dve", lambda e: e.tensor_scalar(out=C["ntri"][:, :], in0=C["ntri"][:, :], scalar1=-1.0, scalar2=None,
                                              op0=ALU.mult),
             reads=C["ntri"].all(), writes=C["ntri"].all())
        C["eps"] = k.sb("eps", [128, 1], F32)
        s.op("dve", lambda e: e.memset(C["eps"][:, :], NORM_EPS), writes=C["eps"].all())
        s.dma("sp", C["gains"][:, :, :], gains_d[:, :].rearrange("p (n c) -> p n c", n=12), writes=C["gains"].all())
        for gi in HALF_GAINS:
            s.op("dve", lambda e, gi=gi: e.tensor_scalar(out=C["gains"][:, gi, :], in0=C["gains"][:, gi, :],
                                                         scalar1=0.5, scalar2=None, op0=ALU.mult),
                 reads=C["gains"].all(), writes=C["gains"].all())
        for tb in range(4):
            for kc in range(8):
                s.dma("sp", XT[:, kc, tb * 512:(tb + 1) * 512], xT_d[kc * 128:(kc + 1) * 128, tb * 512:(tb + 1) * 512],
                      writes=XT.b(tb))

        PRE_GI = {"f01": 0, "m0": 2, "f02": 4, "f02f11": 4, "f11": 6, "m1": 8, "f12": 10}
        for si, st in enumerate(stages):
            nxt = PRE_GI[stages[si + 1]] if si + 1 < len(stages) else None
            if st == "f01":
                ffn_stage(k, C, XT, PS, [(*ffn_d["l0_ffn1"], 0, 1)], nxt)
            elif st == "f02":
                ffn_stage(k, C, XT, PS, [(*ffn_d["l0_ffn2"], 4, 5)], nxt)
            elif st == "f02f11":
                ffn_stage(k, C, XT, PS, [(*ffn_d["l0_ffn2"], 4, 5), (*ffn_d["l1_ffn1"], 6, 7)], nxt)
            elif st == "f11":
                ffn_stage(k, C, XT, PS, [(*ffn_d["l1_ffn1"], 6, 7)], nxt)
            elif st == "m0":
                mixer0_stage(k, C, XT, PS, D, 2, 3, nxt)
            elif st == "m1":
                if dbg:
                    C["dbg_OT"] = D["dbg_OT"]
                attn_stage(k, C, XT, PS, wqkv_d, l1_wo_d, 8, 9, nxt)
            elif st == "f12":
                ffn_stage(k, C, XT, PS, [(*ffn_d["l1_ffn2"], 10, 11)], nxt)

        for kc in range(8):
            for tb in range(4):
                s.dma("sp", outT_d[kc * 128:(kc + 1) * 128, tb * 512:(tb + 1) * 512], XT[:, kc, tb * 512:(tb + 1) * 512],
                      reads=XT.b(tb), writes=outT_d.all())
        s.barrier(engines=["sp"])
    return nc


def _col(v):
    return np.ascontiguousarray(np.asarray(v, np.float32).reshape(8, 128).T)


def prep_shared(inp):
    d = {}
    d["gains"] = np.ascontiguousarray(np.concatenate([_col(inp[n]) for n in GAIN_NAMES], axis=1))
    for nm in ("l0_ffn1", "l0_ffn2", "l1_ffn1", "l1_ffn2"):
        w_in = np.asarray(inp[nm + "_w_in"], np.float32)
        w_out = np.asarray(inp[nm + "_w_out"], np.float32)
        g = w_in[:, :DFF].reshape(8, 128, NJ, 128)
        u = w_in[:, DFF:].reshape(8, 128, NJ, 128)
        gu = np.concatenate([g, u], axis=3)
        d[nm + "_w_in"] = np.ascontiguousarray(gu.transpose(2, 1, 0, 3).reshape(NJ, 128, 2048))
        wo = w_out.reshape(2, 11, 128, 8, 128)
        d[nm + "_w_out"] = np.ascontiguousarray(wo.transpose(0, 3, 2, 1, 4).reshape(2, 8, 128, 11 * 128))
    f = lambda n: np.asarray(inp[n], np.float32)
    cw = f("l0_conv_w")
    pl = np.stack([cw[0], cw[1], cw[2], cw[3], f("l0_conv_b"), f("l0_gate_a_b"), f("l0_gate_x_b"), f("l0_lambda")], axis=1)
    d["l0_pl"] = np.ascontiguousarray(pl.reshape(4, 128, 8).transpose(1, 0, 2).reshape(128, 32))
    mu = f("l0_mu")
    ph = np.stack([mu[0:512], mu[512:1024], mu[1024:1536], f("l0_w0"), f("l0_a0"), f("l0_k_k"), f("l0_k_a"),
                   f("l0_r_k").reshape(512)], axis=1)
    d["l0_ph"] = np.ascontiguousarray(ph.reshape(4, 128, 8).transpose(1, 0, 2).reshape(128, 32))
    mul = np.zeros((128, 3), np.float32)
    mul[0:64, 0] = mu[1536:1600]
    mul[0:64, 1] = mu[1600:1664]
    mul[:, 2] = mu[1664:1792]
    d["l0_mul"] = mul
    for n in ("l0_lnx_g", "l0_lnx_b", "l0_w2", "l0_a2", "l0_g2", "l0_gate_a_w", "l0_gate_x_w", "l0_w_out"):
        d[n] = np.ascontiguousarray(f(n))
    wi = f("l0_w_in").reshape(8, 128, 2816)
    lru = np.concatenate([wi[:, :, 1792:2304].reshape(8, 128, 4, 128), wi[:, :, 2304:2816].reshape(8, 128, 4, 128)], axis=3)
    d["l0_w_lru"] = np.ascontiguousarray(lru.transpose(2, 1, 0, 3).reshape(4, 128, 8 * 256))
    d["l0_w_lora"] = np.ascontiguousarray(wi[:, :, 1536:1792].transpose(1, 0, 2).reshape(128, 8 * 256))
    hd = np.stack([wi[:, :, 0:512].reshape(8, 128, 4, 128), wi[:, :, 512:1024].reshape(8, 128, 4, 128),
                   wi[:, :, 1024:1536].reshape(8, 128, 4, 128)], axis=3)
    d["l0_w_hp"] = np.ascontiguousarray(hd.transpose(2, 1, 0, 3, 4).reshape(4, 128, 8 * 384))
    wq = np.asarray(inp["l1_w_qkv"], np.float32).reshape(8, 128, 3, 8, 128)
    d["l1_w_qkv"] = np.ascontiguousarray(wq.transpose(3, 1, 0, 2, 4).reshape(8, 128, 8 * 384))
    d["l1_w_out"] = np.ascontiguousarray(np.asarray(inp["l1_w_out"], np.float32))
    return d


_CACHE = {}


def kernel(**inputs):
    x = np.asarray(inputs["x"], np.float32)
    shared = prep_shared(inputs)
    if "nc" not in _CACHE:
        _CACHE["nc"] = build_program()
    nc = _CACHE["nc"]
    in_maps = []
    for c in range(N_CORES):
        m = dict(shared)
        m["xT"] = np.ascontiguousarray(x[c].T)
        in_maps.append(m)
    res = run_bass_kernel_spmd(nc, in_maps, core_ids=list(range(N_CORES)))
    out = np.stack([np.ascontiguousarray(res.results[c]["outT"].T) for c in range(N_CORES)], axis=0)
    return out.astype(np.float32)
```

```python
import math
from contextlib import ExitStack

import numpy as np
import concourse.bass as bass
import concourse.mybir as mybir
from concourse.bass_utils import run_bass_kernel_spmd

F32 = mybir.dt.float32
BF16 = mybir.dt.bfloat16
AF = mybir.ActivationFunctionType
ALU = mybir.AluOpType

SEQ = 2048
DM = 1024
DFF = 2816
NJ = 22
NORM_EPS = 1e-6
N_CORES = 8


class Buf:
    __slots__ = ("name", "w", "r")

    def __init__(self, name):
        self.name = name
        self.w = None
        self.r = {}


class T:
    def __init__(self, name, t, parts=1):
        self.name = name
        self.t = t
        self.bufs = [Buf(f"{name}.{i}") for i in range(parts)]

    def b(self, *idx):
        return [self.bufs[i] for i in idx]

    def all(self):
        return list(self.bufs)

    def __getitem__(self, key):
        return self.t[key]


class Sched:
    COMPUTE = ("pe", "act", "dve", "pool")

    def __init__(self, nc, es, n_dma_ch=20):
        self.nc = nc
        self.eng = {"pe": nc.tensor, "act": nc.scalar, "dve": nc.vector, "pool": nc.gpsimd, "sp": nc.sync}
        self.sems = {}
        self.cnt = {}
        for e in self.COMPUTE:
            self.sems[e] = es.enter_context(nc.semaphore(f"s_{e}"))
            self.cnt[e] = 0
        self.ch = {}
        self.ch_next = {}
        for q in ("sp", "pool", "act"):
            n = n_dma_ch if q != "act" else 4
            lst = []
            for i in range(n):
                key = f"d_{q}{i}"
                self.sems[key] = es.enter_context(nc.semaphore(key))
                self.cnt[key] = 0
                lst.append(key)
            self.ch[q] = lst
            self.ch_next[q] = 0
        self.seen = {e: {} for e in self.eng}
        self.n_wait = 0
        self.n_ins = 0

    def _wait(self, e, ev):
        key, val = ev
        if val <= 0:
            return
        if self.seen[e].get(key, 0) >= val:
            return
        self.seen[e][key] = val
        self.eng[e].wait_ge(self.sems[key], val)
        self.n_wait += 1

    def _deps(self, e, reads, writes):
        evs = {}

        def need(ev):
            if ev is None:
                return
            k_, v_ = ev
            if e == "pe" and k_ == "pe":
                return
            if evs.get(k_, 0) < v_:
                evs[k_] = v_

        for b in reads:
            need(b.w)
        for b in writes:
            need(b.w)
            for kv in b.r.items():
                need(kv)
        return evs

    def op(self, e, fn, reads=(), writes=()):
        evs = self._deps(e, reads, writes)
        for ev in evs.items():
            self._wait(e, ev)
        ins = fn(self.eng[e])
        self.cnt[e] += 1
        ev = (e, self.cnt[e])
        ins.then_inc(self.sems[e], 1)
        self.seen[e][e] = max(self.seen[e].get(e, 0), 0)
        for b in writes:
            b.w = ev
            b.r = {}
        for b in reads:
            if b.w is not ev:
                b.r[e] = self.cnt[e]
        self.n_ins += 1
        return ins

    def dma(self, q, out, in_, reads=(), writes=()):
        e = q
        evs = self._deps(e, reads, writes)
        key = self.ch[q][self.ch_next[q]]
        self.ch_next[q] = (self.ch_next[q] + 1) % len(self.ch[q])
        if evs.get(key, 0) < self.cnt[key]:
            evs[key] = self.cnt[key]
        for ev in evs.items():
            self._wait(e, ev)
        ins = self.eng[e].dma_start(out=out, in_=in_)
        self.cnt[key] += 16
        ins.then_inc(self.sems[key], 16)
        ev = (key, self.cnt[key])
        for b in writes:
            b.w = ev
            b.r = {}
        for b in reads:
            b.r[key] = self.cnt[key]
        self.n_ins += 1
        return ins

    def barrier(self, engines=None):
        evs = [(k_, v_) for k_, v_ in self.cnt.items() if v_ > 0]
        for e in (engines or self.eng):
            for ev in evs:
                if ev[0] == e:
                    continue
                self._wait(e, ev)


class Phase:
    def __init__(self, k):
        self.k = k
        self.es = ExitStack()

    def __enter__(self):
        self.es.__enter__()
        return self

    def __exit__(self, *a):
        self.k.s.barrier()
        return self.es.__exit__(*a)

    def sb(self, name, shape, dtype, parts=1):
        self.k.uid += 1
        t = self.es.enter_context(self.k.nc.sbuf_tensor(f"ph_{name}_{self.k.uid}", shape, dtype))
        return T(name, t, parts)


class KB:
    def __init__(self):
        self.nc = bass.Bass("TRN2", target_bir_lowering=False)
        self.es = ExitStack()
        self.s = Sched(self.nc, self.es)
        self.uid = 0

    def sb(self, name, shape, dtype, parts=1):
        t = self.es.enter_context(self.nc.sbuf_tensor("sb_" + name, shape, dtype))
        return T(name, t, parts)

    def ps(self, name, shape, dtype=F32, parts=1):
        t = self.es.enter_context(self.nc.psum_tensor("pp_" + name, shape, dtype))
        return T(name, t, parts)

    def dram_in(self, name, shape, dtype=F32):
        return T(name, self.nc.dram_tensor(name, list(shape), dtype, kind="ExternalInput").ap())

    def dram_out(self, name, shape, dtype=F32):
        return T(name, self.nc.dram_tensor(name, list(shape), dtype, kind="ExternalOutput").ap())

    def phase(self):
        return Phase(self)


def rms_rstd(k, C, src, src_bufs, SQ, PST, RSTD, ntok):
    s = k.s
    s.op("act", lambda e: e.activation(out=SQ[:, :, 0:ntok], in_=src, func=AF.Square),
         reads=src_bufs, writes=SQ.all())
    for kc in range(8):
        s.op("pe", lambda e, kc=kc: e.matmul(PST[:, 0:ntok], lhsT=C["ones_m"][:, :], rhs=SQ[:, kc, 0:ntok],
                                             start=(kc == 0), stop=(kc == 7)),
             reads=SQ.all() + C["ones_m"].all(), writes=PST.all())
    s.op("act", lambda e: e.activation(out=RSTD[:, 0:ntok], in_=PST[:, 0:ntok], func=AF.Sqrt, bias=C["eps"][:, 0:1]),
         reads=PST.all() + C["eps"].all(), writes=RSTD.all())
    s.op("dve", lambda e: e.reciprocal(out=RSTD[:, 0:ntok], in_=RSTD[:, 0:ntok]),
         reads=RSTD.all(), writes=RSTD.all())


class Bg:
    def __init__(self):
        self.q = []

    def add(self, gen, period=2):
        self.q.append([gen, period, period])

    def tick(self):
        for item in list(self.q):
            item[2] -= 1
            if item[2] <= 0:
                item[2] = item[1]
                try:
                    next(item[0])
                except StopIteration:
                    self.q.remove(item)

    def drain(self):
        while self.q:
            for item in list(self.q):
                try:
                    next(item[0])
                except StopIteration:
                    self.q.remove(item)


def rms_rstd_gen(k, C, src, src_bufs, SQ, PST, RSTD, ntok, fuse_sq=False):
    s = k.s
    s.op("act", lambda e: e.activation(out=SQ[:, :, 0:ntok], in_=src, func=AF.Square),
         reads=src_bufs, writes=SQ.all())
    if not fuse_sq:
        yield
    for kc in range(8):
        s.op("pe", lambda e, kc=kc: e.matmul(PST[:, 0:ntok], lhsT=C["ones_m"][:, :], rhs=SQ[:, kc, 0:ntok],
                                             start=(kc == 0), stop=(kc == 7)),
             reads=SQ.all() + C["ones_m"].all(), writes=PST.all())
    yield
    s.op("act", lambda e: e.activation(out=RSTD[:, 0:ntok], in_=PST[:, 0:ntok], func=AF.Sqrt, bias=C["eps"][:, 0:1]),
         reads=PST.all() + C["eps"].all(), writes=RSTD.all())
    yield
    s.op("dve", lambda e: e.reciprocal(out=RSTD[:, 0:ntok], in_=RSTD[:, 0:ntok]),
         reads=RSTD.all(), writes=RSTD.all())


def ffn_stage(k, C, XT, PS, ffns, next_gi=None):
    s = k.s
    G_ = C["gains"]
    with k.phase() as ph:
        HTG = C["HTG"]
        HTs = []
        for i in range(2):
            hv = T(f"HTv{i}", HTG.t[:, :, i * 1024:(i + 1) * 1024])
            hv.bufs = HTG.bufs[2 * i:2 * i + 2]
            HTs.append(hv)
        ACTT = ph.sb("ACTT", [128, 11, 1024], BF16, parts=22)
        YT = ph.sb("YT", [128, 8, 1024], F32, parts=16)
        SQ = [ph.sb(f"SQ{i}", [128, 8, 512], BF16) for i in range(2)]
        RSTD = [ph.sb(f"RSTD{i}", [128, 512], F32) for i in range(2)]
        WIN = [ph.sb(f"WIN{i}", [128, 8, 256], BF16) for i in range(3)]
        WOUT = [ph.sb(f"WOUT{i}", [128, 11, 128], BF16) for i in range(3)]
        SG = [ph.sb(f"SG{i}", [128, 512], F32) for i in range(2)]
        PG = [PS[0], PS[1]]
        PU = [PS[2], PS[3]]
        PY = [PS[4], PS[5]]
        PST = [PS[6], PS[7]]
        st = {"win": 0, "wout": 0, "pi": 0, "ni": 0}
        jobs = [(f, B) for f in range(len(ffns)) for B in range(2)]

        bg = Bg()

        def prenorm(ji):
            f, B = jobs[ji]
            HT = HTs[ji % 2]
            gi_pre = ffns[f][2]
            for sb_ in range(2):
                tok = B * 1024 + sb_ * 512
                xb = XT.b(B * 2 + sb_)
                n_ = st["ni"] % 2
                st["ni"] += 1
                yield from rms_rstd_gen(k, C, XT[:, :, tok:tok + 512], xb, SQ[n_], PST[n_], RSTD[n_], 512)
                for kc in range(8):
                    if kc == 4:
                        yield
                    s.op("dve", lambda e, kc=kc, tok=tok, sb_=sb_, n_=n_: e.scalar_tensor_tensor(
                        out=HT[:, kc, sb_ * 512:(sb_ + 1) * 512], in0=XT[:, kc, tok:tok + 512],
                        scalar=G_[:, gi_pre, kc:kc + 1], in1=RSTD[n_][:, :], op0=ALU.mult, op1=ALU.mult),
                        reads=xb + RSTD[n_].all() + G_.all(), writes=HT.b(sb_))

        def postnorm(ji):
            f, B = jobs[ji]
            gi_post = ffns[f][3]
            for sb_ in range(2):
                tok = B * 1024 + sb_ * 512
                rhs_sl = slice(sb_ * 512, (sb_ + 1) * 512)
                ybs = YT.b(*[dc * 2 + sb_ for dc in range(8)])
                xb = XT.b(B * 2 + sb_)
                n_ = st["ni"] % 2
                st["ni"] += 1
                yield from rms_rstd_gen(k, C, YT[:, :, rhs_sl], ybs, SQ[n_], PST[n_], RSTD[n_], 512)
                for dc in range(8):
                    if dc % 2 == 0 and dc > 0:
                        yield
                    s.op("dve", lambda e, dc=dc, rhs_sl=rhs_sl, n_=n_: e.scalar_tensor_tensor(
                        out=YT[:, dc, rhs_sl], in0=YT[:, dc, rhs_sl], scalar=G_[:, gi_post, dc:dc + 1],
                        in1=RSTD[n_][:, :], op0=ALU.mult, op1=ALU.mult),
                        reads=YT.b(dc * 2 + sb_) + RSTD[n_].all() + G_.all(), writes=YT.b(dc * 2 + sb_))
                    s.op("dve", lambda e, dc=dc, rhs_sl=rhs_sl, tok=tok: e.tensor_tensor(
                        out=XT[:, dc, tok:tok + 512], in0=XT[:, dc, tok:tok + 512], in1=YT[:, dc, rhs_sl], op=ALU.add),
                        reads=YT.b(dc * 2 + sb_) + xb, writes=xb)

        def up(ji, G, after_first=None):
            f, B = jobs[ji]
            HT = HTs[ji % 2]
            w_in_d = ffns[f][0]
            for jj in range(11):
                j = G * 11 + jj
                W = WIN[st["win"] % 3]
                st["win"] += 1
                s.dma("pool", W[:, :, :], w_in_d[j].rearrange("p (kc c) -> p kc c", kc=8), writes=W.all())
                for sb_ in range(2):
                    pg, pu, sg = PG[st["pi"] % 2], PU[st["pi"] % 2], SG[st["pi"] % 2]
                    st["pi"] += 1
                    rhs_sl = slice(sb_ * 512, (sb_ + 1) * 512)
                    for kc in range(8):
                        s.op("pe", lambda e, kc=kc, pg=pg, W=W, rhs_sl=rhs_sl: e.matmul(
                            pg[:, :], lhsT=W[:, kc, 0:128], rhs=HT[:, kc, rhs_sl], start=(kc == 0), stop=(kc == 7)),
                            reads=W.all() + HT.b(sb_), writes=pg.all())
                    for kc in range(8):
                        s.op("pe", lambda e, kc=kc, pu=pu, W=W, rhs_sl=rhs_sl: e.matmul(
                            pu[:, :], lhsT=W[:, kc, 128:256], rhs=HT[:, kc, rhs_sl], start=(kc == 0), stop=(kc == 7)),
                            reads=W.all() + HT.b(sb_), writes=pu.all())
                    s.op("act", lambda e, pg=pg, sg=sg: e.activation(out=sg[:, :], in_=pg[:, :], func=AF.Silu),
                         reads=pg.all(), writes=sg.all())
                    s.op("dve", lambda e, pu=pu, sg=sg, jj=jj, rhs_sl=rhs_sl: e.tensor_tensor(
                        out=ACTT[:, jj, rhs_sl], in0=sg[:, :], in1=pu[:, :], op=ALU.mult),
                        reads=sg.all() + pu.all(), writes=ACTT.b(jj * 2 + sb_))
                    bg.tick()
                if jj == 0 and after_first is not None:
                    after_first()

        def down(ji, G):
            f, B = jobs[ji]
            w_out_d = ffns[f][1]
            for dc in range(8):
                W = WOUT[st["wout"] % 3]
                st["wout"] += 1
                s.dma("pool", W[:, :, :], w_out_d[G, dc].rearrange("p (jj c) -> p jj c", jj=11), writes=W.all())
                for sb_ in range(2):
                    py = PY[st["pi"] % 2]
                    st["pi"] += 1
                    rhs_sl = slice(sb_ * 512, (sb_ + 1) * 512)
                    for jj in range(11):
                        s.op("pe", lambda e, jj=jj, py=py, W=W, rhs_sl=rhs_sl: e.matmul(
                            py[:, :], lhsT=W[:, jj, :], rhs=ACTT[:, jj, rhs_sl], start=(jj == 0), stop=(jj == 10)),
                            reads=W.all() + ACTT.b(jj * 2 + sb_), writes=py.all())
                    yb = YT.b(dc * 2 + sb_)
                    if G == 0:
                        s.op("act", lambda e, py=py, dc=dc, rhs_sl=rhs_sl: e.activation(
                            out=YT[:, dc, rhs_sl], in_=py[:, :], func=AF.Copy),
                            reads=py.all(), writes=yb)
                    else:
                        s.op("dve", lambda e, py=py, dc=dc, rhs_sl=rhs_sl: e.tensor_tensor(
                            out=YT[:, dc, rhs_sl], in0=YT[:, dc, rhs_sl], in1=py[:, :], op=ALU.add),
                            reads=py.all() + yb, writes=yb)
                    bg.tick()

        def next_prenorm():
            HT = HTs[0]
            for sb_ in range(2):
                tok = sb_ * 512
                xb = XT.b(sb_)
                n_ = st["ni"] % 2
                st["ni"] += 1
                yield from rms_rstd_gen(k, C, XT[:, :, tok:tok + 512], xb, SQ[n_], PST[n_], RSTD[n_], 512)
                for kc in range(8):
                    if kc == 4:
                        yield
                    s.op("dve", lambda e, kc=kc, tok=tok, sb_=sb_, n_=n_: e.scalar_tensor_tensor(
                        out=HT[:, kc, sb_ * 512:(sb_ + 1) * 512], in0=XT[:, kc, tok:tok + 512],
                        scalar=G_[:, next_gi, kc:kc + 1], in1=RSTD[n_][:, :], op0=ALU.mult, op1=ALU.mult),
                        reads=xb + RSTD[n_].all() + G_.all(), writes=HT.b(sb_))

        n = len(jobs)
        assert n % 2 == 0
        if C["ht_ready"] is not None and C["ht_ready"] == (ffns[0][2], (0, 1)):
            pass
        else:
            bg.add(prenorm(0))
            bg.drain()
        C["ht_ready"] = None
        for ji in range(n):
            up(ji, 0, after_first=(lambda ji=ji: bg.add(postnorm(ji - 1), 1)) if ji > 0 else None)
            bg.drain()
            down(ji, 0)
            if ji + 1 < n:
                bg.add(prenorm(ji + 1), 1)
            elif next_gi is not None:
                bg.add(next_prenorm(), 1)
                C["ht_ready"] = (next_gi, (0, 1))
            up(ji, 1)
            bg.drain()
            down(ji, 1)
        bg.add(postnorm(n - 1))
        bg.drain()


def prenorm_to_HT(k, C, ph, XT, HT, PS, gi_pre, col_off=0):
    s = k.s
    G_ = C["gains"]
    with k.phase() as p2:
        SQ = [p2.sb(f"SQ{i}", [128, 8, 512], BF16) for i in range(2)]
        RSTD = [p2.sb(f"RSTD{i}", [128, 512], F32) for i in range(2)]
        bg = Bg()

        def chain(tb):
            tok = tb * 512
            xb = XT.b(tb)
            yield from rms_rstd_gen(k, C, XT[:, :, tok:tok + 512], xb, SQ[tb % 2], PS[6 + tb % 2], RSTD[tb % 2], 512)
            for kc in range(8):
                if kc == 4:
                    yield
                s.op("dve", lambda e, kc=kc: e.scalar_tensor_tensor(
                    out=HT[:, kc, col_off + tok:col_off + tok + 512], in0=XT[:, kc, tok:tok + 512],
                    scalar=G_[:, gi_pre, kc:kc + 1], in1=RSTD[tb % 2][:, :], op0=ALU.mult, op1=ALU.mult),
                    reads=xb + RSTD[tb % 2].all() + G_.all(), writes=HT.b(tb))

        skip = ()
        if C["ht_ready"] is not None and C["ht_ready"][0] == gi_pre and col_off == 0:
            skip = C["ht_ready"][1]
        C["ht_ready"] = None
        for tb in range(4):
            if tb in skip:
                continue
            bg.add(chain(tb), 1)
            bg.tick()
            bg.tick()
        bg.drain()


def outproj_postnorm(k, C, XT, PS, OT, wo_d, gi_post, next_gi=None, WO=None):
    s = k.s
    G_ = C["gains"]
    with k.phase() as p3:
        if WO is None:
            WO = p3.sb("WO", [128, 8, DM], BF16)
            for kc in range(8):
                s.dma("pool", WO[:, kc, :], wo_d[kc * 128:(kc + 1) * 128, :], writes=WO.all())
        YTs = [p3.sb(f"YT{i}", [128, 8, 512], F32, parts=8) for i in range(2)]
        SQ1 = p3.sb("SQ", [128, 8, 512], BF16)
        RSTD = [p3.sb(f"RSTD{i}", [128, 512], F32) for i in range(2)]
        bg = Bg()

        def chain(tb):
            tok = tb * 512
            YT = YTs[tb % 2]
            yield from rms_rstd_gen(k, C, YT[:, :, :], YT.all(), SQ1, PS[6 + tb % 2], RSTD[tb % 2], 512, fuse_sq=True)
            xb = XT.b(tb)
            for dc in range(8):
                if dc % 2 == 0 and dc > 0:
                    yield
                s.op("dve", lambda e, dc=dc: e.scalar_tensor_tensor(
                    out=YT[:, dc, :], in0=YT[:, dc, :], scalar=G_[:, gi_post, dc:dc + 1],
                    in1=RSTD[tb % 2][:, :], op0=ALU.mult, op1=ALU.mult),
                    reads=YT.b(dc) + RSTD[tb % 2].all() + G_.all(), writes=YT.b(dc))
                s.op("dve", lambda e, dc=dc: e.tensor_tensor(
                    out=XT[:, dc, tok:tok + 512], in0=XT[:, dc, tok:tok + 512], in1=YT[:, dc, :], op=ALU.add),
                    reads=YT.b(dc) + xb, writes=xb)

        if next_gi is not None:
            RSTDn = p3.sb("RSTDn", [128, 512], F32)
        HTG = C["HTG"]

        def next_prenorm():
            for sb_ in range(2):
                tok = sb_ * 512
                xb = XT.b(sb_)
                yield from rms_rstd_gen(k, C, XT[:, :, tok:tok + 512], xb, SQ1, PS[0], RSTDn, 512, fuse_sq=True)
                for kc in range(8):
                    if kc == 4:
                        yield
                    s.op("dve", lambda e, kc=kc, tok=tok: e.scalar_tensor_tensor(
                        out=HTG[:, kc, tok:tok + 512], in0=XT[:, kc, tok:tok + 512],
                        scalar=G_[:, next_gi, kc:kc + 1], in1=RSTDn[:, :], op0=ALU.mult, op1=ALU.mult),
                        reads=xb + RSTDn.all() + G_.all(), writes=HTG.b(sb_))

        pi = 0
        for tb in range(4):
            tok = tb * 512
            YT = YTs[tb % 2]
            if tb == 3 and next_gi is not None:
                bg.add(next_prenorm(), 1)
                C["ht_ready"] = (next_gi, (0, 1))
            for dc in range(8):
                pp = PS[4 + pi % 2]
                pi += 1
                for kc in range(8):
                    s.op("pe", lambda e, kc=kc, dc=dc, pp=pp, tok=tok: e.matmul(
                        pp[:, :], lhsT=WO[:, kc, dc * 128:(dc + 1) * 128], rhs=OT[:, kc, tok:tok + 512],
                        start=(kc == 0), stop=(kc == 7)),
                        reads=WO.all() + OT.all(), writes=pp.all())
                s.op("act", lambda e, dc=dc, pp=pp, YT=YT: e.activation(out=YT[:, dc, :], in_=pp[:, :], func=AF.Copy),
                     reads=pp.all(), writes=YT.b(dc))
                bg.tick()
            bg.drain()
            bg.add(chain(tb), 1)
        bg.drain()


def attn_stage(k, C, XT, PS, wqkv_d, wo_d, gi_pre, gi_post, next_gi=None):
    s = k.s
    with k.phase() as ph:
        OT = ph.sb("OT", [128, 8, SEQ], BF16, parts=1)
        with k.phase() as pab:
            HT = C["HTG"]
            prenorm_to_HT(k, C, pab, XT, HT, PS, gi_pre)
            with k.phase() as pb:
                NEGM = pb.sb("negm", [128, 4, 512], BF16)
                ZB = pb.sb("zb", [128, 512], BF16)
                s.op("dve", lambda e: e.memset(ZB[:, :], 0.0), writes=ZB.all())
                for d in range(4):
                    s.op("pool", lambda e, d=d: e.affine_select(
                        out=NEGM[:, d, :], in_=ZB[:, :], pattern=[[1, 512]], compare_op=ALU.is_gt,
                        fill=-30000.0, base=-128 * d, channel_multiplier=-1),
                        reads=ZB.all(), writes=NEGM.all())
                QT = [pb.sb(f"QT{i}", [128, SEQ], BF16) for i in range(2)]
                KT = [pb.sb(f"KT{i}", [128, SEQ], BF16) for i in range(2)]
                V = [pb.sb(f"V{i}", [128, 16, 128], BF16) for i in range(2)]
                W = [pb.sb(f"WQKV{i}", [128, 8, 384], BF16) for i in range(1)]
                OTOK = [pb.sb(f"OTOK{i}", [128, 16, 128], BF16) for i in range(2)]
                E = [pb.sb(f"E{i}", [128, 512], F32) for i in range(3)]
                SP = [pb.sb(f"SP{i}", [128, 512], BF16) for i in range(5)]
                ATT = [pb.sb(f"ATT{i}", [128, 512], BF16) for i in range(3)]
                OACC = [pb.sb(f"OACC{i}", [128, 4, 64], F32) for i in range(2)]
                CACC = [pb.sb(f"CACC{i}", [128, 4], F32) for i in range(2)]
                FS = [pb.sb(f"FS{i}", [128, 4], F32) for i in range(4)]
                PZ = [PS[0], PS[1], PS[2], PS[3], PS[4]]
                PO = [PS[5], PS[6]]
                PP = [PS[7]]
                st = {"pi": 0}

                bg = Bg()

                def pre_hp(hp):
                    w = W[0]
                    qt, kt_, v = QT[hp % 2], KT[hp % 2], V[hp % 2]
                    s.dma("pool", w[:, :, :], wqkv_d[hp].rearrange("p (kc c) -> p kc c", kc=8), writes=w.all())
                    for which in range(2):
                        for tb in range(4):
                            pp = PP[st["pi"] % len(PP)]
                            st["pi"] += 1
                            for kc in range(8):
                                s.op("pe", lambda e, kc=kc, pp=pp, w=w, which=which, tb=tb: e.matmul(
                                    pp[:, :], lhsT=w[:, kc, which * 128:(which + 1) * 128],
                                    rhs=HT[:, kc, tb * 512:(tb + 1) * 512], start=(kc == 0), stop=(kc == 7)),
                                    reads=w.all() + HT.b(tb), writes=pp.all())
                            if which == 0:
                                s.op("dve", lambda e, pp=pp, qt=qt, tb=tb: e.tensor_scalar(
                                    out=qt[:, tb * 512:(tb + 1) * 512], in0=pp[:, :], scalar1=0.125, scalar2=None,
                                    op0=ALU.mult),
                                    reads=pp.all(), writes=qt.all())
                            else:
                                s.op("dve", lambda e, pp=pp, kt_=kt_, tb=tb: e.tensor_copy(
                                    out=kt_[:, tb * 512:(tb + 1) * 512], in_=pp[:, :]),
                                    reads=pp.all(), writes=kt_.all())
                            yield
                    for tg in range(4):
                        pp = PP[st["pi"] % len(PP)]
                        st["pi"] += 1
                        for tt in range(4):
                            tok = (tg * 4 + tt) * 128
                            for kc in range(8):
                                s.op("pe", lambda e, kc=kc, pp=pp, w=w, tt=tt, tok=tok: e.matmul(
                                    pp[:, tt * 128:(tt + 1) * 128], lhsT=HT[:, kc, tok:tok + 128],
                                    rhs=w[:, kc, 256:384], start=(kc == 0), stop=(kc == 7)),
                                    reads=w.all() + HT.b(tg), writes=pp.all())
                        s.op("dve", lambda e, pp=pp, v=v, tg=tg: e.tensor_copy(
                            out=v[:, tg * 4:(tg + 1) * 4, :], in_=pp[:, :].rearrange("p (a b) -> p a b", a=4)),
                            reads=pp.all(), writes=v.all())
                        yield

                def post_hp(hp):
                    otok = OTOK[hp % 2]
                    for tg in range(4):
                        pp = PP[st["pi"] % len(PP)]
                        st["pi"] += 1
                        for tt in range(4):
                            s.op("pe", lambda e, pp=pp, tt=tt, tg=tg, otok=otok: e.matmul(
                                pp[:, tt * 128:(tt + 1) * 128], lhsT=otok[:, tg * 4 + tt, :], rhs=C["ident"][:, :],
                                start=True, stop=True),
                                reads=otok.all() + C["ident"].all(), writes=pp.all())
                        s.op("dve", lambda e, pp=pp, tg=tg, hp=hp: e.tensor_copy(
                            out=OT[:, hp, tg * 512:(tg + 1) * 512], in_=pp[:, :]),
                            reads=pp.all(), writes=OT.all())

                units = []
                for hp in range(8):
                    for g in range(4):
                        for kt in range(4 * g + 3, -1, -1):
                            for hh in range(2):
                                units.append((hp, hh, g, kt))
                n = len(units)
                NPZ, NSP, NATT, NPO, NE = 5, 5, 3, 2, 3

                def u_(i):
                    hp, hh, g, kt = units[i]
                    d = kt - 4 * g
                    return hp, hh, g, kt, d, slice(hh * 64, (hh + 1) * 64)

                def c0_(i):
                    hp, hh, g, kt = units[i]
                    return max(kt - 4 * g, 0) * 128

                def s0_qk(i):
                    hp, hh, g, kt, d, hs = u_(i)
                    if hh == 0 and g == 0 and kt == 3:
                        if hp == 0:
                            bg.add(pre_hp(0))
                        bg.drain()
                    if hh == 0 and g == 0 and kt == 0 and hp + 1 < 8:
                        bg.add(pre_hp(hp + 1), 5)
                    pz, qt, kt_ = PZ[i % NPZ], QT[hp % 2], KT[hp % 2]
                    q0 = g * 512
                    c0 = c0_(i)
                    s.op("pe", lambda e: e.matmul(pz[:, c0:512], lhsT=kt_[hs, kt * 128:(kt + 1) * 128],
                                                  rhs=qt[hs, q0 + c0:q0 + 512], start=True, stop=(d < 0)),
                         reads=kt_.all() + qt.all(), writes=pz.all())
                    if d >= 0:
                        s.op("pe", lambda e: e.matmul(pz[:, c0:c0 + 128], lhsT=C["ident"][:, :], rhs=NEGM[:, d, c0:c0 + 128],
                                                      start=False, stop=True),
                             reads=C["ident"].all() + NEGM.all(), writes=pz.all())

                def s1_exp(i):
                    pz, e_ = PZ[i % NPZ], E[i % NE]
                    c0 = c0_(i)
                    s.op("act", lambda e: e.activation(out=e_[:, c0:512], in_=pz[:, c0:512], func=AF.Exp),
                         reads=pz.all(), writes=e_.all())

                def s2_ln(i):
                    e_, sp = E[i % NE], SP[i % NSP]
                    c0 = c0_(i)
                    s.op("act", lambda e: e.activation(out=sp[:, c0:512], in_=e_[:, c0:512], func=AF.Ln,
                                                       bias=C["one_f"][:, 0:1]),
                         reads=e_.all() + C["one_f"].all(), writes=sp.all())

                def s3_tri(i):
                    pz, sp = PZ[i % NPZ], SP[i % NSP]
                    c0 = c0_(i)
                    s.op("pe", lambda e: e.matmul(pz[:, c0:512], lhsT=C["ntri"][:, :], rhs=sp[:, c0:512], start=False,
                                                  stop=True, skip_group_check=True),
                         reads=sp.all() + C["ntri"].all(), writes=pz.all())

                def s4_att(i):
                    pz, att = PZ[i % NPZ], ATT[i % NATT]
                    c0 = c0_(i)
                    s.op("act", lambda e: e.activation(out=att[:, c0:512], in_=pz[:, c0:512], func=AF.Exp),
                         reads=pz.all(), writes=att.all())

                def s5_av(i):
                    hp, hh, g, kt, d, hs = u_(i)
                    qlo = max(d, 0)
                    sp, att, po, v = SP[i % NSP], ATT[i % NATT], PO[i % NPO], V[hp % 2]
                    for qi in range(qlo, 4):
                        s.op("pe", lambda e, qi=qi: e.matmul(
                            po[:, qi * 64:(qi + 1) * 64], lhsT=att[:, qi * 128:(qi + 1) * 128],
                            rhs=v[:, kt, hs], start=True, stop=True),
                            reads=att.all() + v.all(), writes=po.all())
                        s.op("pe", lambda e, qi=qi: e.matmul(
                            po[:, 256 + qi:257 + qi], lhsT=sp[:, qi * 128:(qi + 1) * 128],
                            rhs=C["ones_col"][:, 0:1], start=True, stop=True),
                            reads=sp.all() + C["ones_col"].all(), writes=po.all())

                def s6_acc(i):
                    hp, hh, g, kt, d, hs = u_(i)
                    qlo = max(d, 0)
                    span = hh
                    po = PO[i % NPO]
                    oacc, cacc, fs = OACC[span % 2], CACC[span % 2], FS[i % 4]
                    otok = OTOK[hp % 2]
                    if kt != 4 * g + 3:
                        s.op("act", lambda e: e.activation(out=fs[:, :], in_=cacc[:, :], func=AF.Exp, scale=-1.0),
                             reads=cacc.all(), writes=fs.all())
                        s.op("dve", lambda e: e.tensor_tensor(
                            out=cacc[:, qlo:4], in0=cacc[:, qlo:4], in1=po[:, 256 + qlo:260], op=ALU.add),
                            reads=po.all() + cacc.all(), writes=cacc.all())
                        for qi in range(qlo, 4):
                            s.op("dve", lambda e, qi=qi: e.scalar_tensor_tensor(
                                out=oacc[:, qi, :], in0=po[:, qi * 64:(qi + 1) * 64], scalar=fs[:, qi:qi + 1],
                                in1=oacc[:, qi, :], op0=ALU.mult, op1=ALU.add),
                                reads=po.all() + fs.all() + oacc.all(), writes=oacc.all())
                    else:
                        if qlo > 0:
                            s.op("dve", lambda e: e.memset(oacc[:, 0:qlo, :], 0.0), writes=oacc.all())
                            s.op("dve", lambda e: e.memset(cacc[:, 0:qlo], 0.0), writes=cacc.all())
                        s.op("dve", lambda e: e.tensor_copy(
                            out=oacc[:, qlo:4, :], in_=po[:, qlo * 64:256].rearrange("p (a b) -> p a b", b=64)),
                            reads=po.all(), writes=oacc.all())
                        s.op("dve", lambda e: e.tensor_copy(out=cacc[:, qlo:4], in_=po[:, 256 + qlo:260]),
                             reads=po.all(), writes=cacc.all())
                    if kt == 0:
                        s.op("dve", lambda e: e.tensor_copy(out=otok[:, 4 * g:4 * g + 4, hs], in_=oacc[:, :, :]),
                             reads=oacc.all(), writes=otok.all())
                        if hh == 1 and g == 3:
                            post_hp(hp)

                stages = ((0, s0_qk), (1, s1_exp), (2, s2_ln), (3, s3_tri), (4, s4_att), (5, s5_av), (6, s6_acc))
                for i in range(n + 6):
                    for lag, fn in stages:
                        if 0 <= i - lag < n:
                            fn(i - lag)
                    bg.tick()
        if "dbg_OT" in C:
            s.dma("sp", C["dbg_OT"][:, :, :], OT[:, :, :], reads=OT.all(), writes=C["dbg_OT"].all())
        outproj_postnorm(k, C, XT, PS, OT, wo_d, gi_post, next_gi)


AX = mybir.AxisListType


class Ref:
    __slots__ = ("ap", "bufs")

    def __init__(self, ap, bufs):
        self.ap = ap
        self.bufs = bufs


class _RefMaker:
    def __init__(self, t):
        self.t = t

    def __getitem__(self, key):
        return Ref(self.t.t[key], self.t.all())


def rf(t):
    return _RefMaker(t)


def _b(*refs):
    out = []
    for r in refs:
        if isinstance(r, Ref):
            out += r.bufs
    return out


def _a(x):
    return x.ap if isinstance(x, Ref) else x


def e_tt(s, eng, out, a, b, op):
    return s.op(eng, lambda E: E.tensor_tensor(out=out.ap, in0=a.ap, in1=b.ap, op=op), reads=_b(a, b), writes=out.bufs)


def e_ts(s, eng, out, a, s1, s2, op0, op1=None):
    if op1 is None:
        return s.op(eng, lambda E: E.tensor_scalar(out=out.ap, in0=a.ap, scalar1=_a(s1), scalar2=None, op0=op0),
                    reads=_b(a, s1), writes=out.bufs)
    return s.op(eng, lambda E: E.tensor_scalar(out=out.ap, in0=a.ap, scalar1=_a(s1), scalar2=_a(s2), op0=op0, op1=op1),
                reads=_b(a, s1, s2), writes=out.bufs)


def e_stt(s, eng, out, a, sc, b, op0, op1):
    return s.op(eng, lambda E: E.scalar_tensor_tensor(out=out.ap, in0=a.ap, scalar=_a(sc), in1=b.ap, op0=op0, op1=op1),
                reads=_b(a, sc, b), writes=out.bufs)


def e_act(s, out, a, func, bias=None, scale=None):
    kw = {}
    if bias is not None:
        kw["bias"] = _a(bias)
    if scale is not None:
        kw["scale"] = _a(scale)
    return s.op("act", lambda E: E.activation(out=out.ap, in_=a.ap, func=func, **kw), reads=_b(a, bias, scale),
                writes=out.bufs)


def e_mm(s, out, lhsT, rhs, start=True, stop=True):
    return s.op("pe", lambda E: E.matmul(out.ap, lhsT=lhsT.ap, rhs=rhs.ap, start=start, stop=stop),
                reads=_b(lhsT, rhs), writes=out.bufs)


def e_copy(s, eng, out, a):
    if eng == "act":
        return e_act(s, out, a, AF.Copy)
    return s.op(eng, lambda E: E.tensor_copy(out=out.ap, in_=a.ap), reads=_b(a), writes=out.bufs)


def e_memset(s, eng, out, val):
    return s.op(eng, lambda E: E.memset(out.ap, val), writes=out.bufs)


GN_EPS = 64e-5
STAGGER = 3
NEG_EXP_HALF = -0.6065306597126334


def mixer0_stage(k, C, XT, PS, D, gi_pre, gi_post, next_gi=None):
    s = k.s
    with k.phase() as ph:
        OT = ph.sb("OT", [128, 8, SEQ], BF16, parts=1)
        with k.phase() as pab:
            HT = C["HTG"]
            prenorm_to_HT(k, C, pab, XT, HT, PS, gi_pre)
            with k.phase() as pl:
                rglru_part(k, C, pl, HT, OT, PS, D)
            with k.phase() as pr:
                rwkv_part(k, C, pr, HT, OT, PS, D, XT)
        if "dbg_OT" in D:
            s.dma("sp", D["dbg_OT"][:, :, :], OT[:, :, :], reads=OT.all(), writes=D["dbg_OT"].all())
        outproj_postnorm(k, C, XT, PS, OT, D["l0_w_out"], gi_post, next_gi)


def rglru_part(k, C, p, HT, OT, PS, D):
    s = k.s
    PL = p.sb("PL", [128, 4, 8], F32)
    s.dma("sp", PL[:, :, :], D["l0_pl"][:, :].rearrange("p (c n) -> p c n", c=4), writes=PL.all())
    C1 = p.sb("C1", [128, 4], F32)
    e_act(s, rf(C1)[:, :], rf(PL)[:, :, 7], AF.Exp, scale=-1.0)
    e_act(s, rf(C1)[:, :], rf(C1)[:, :], AF.Ln, bias=rf(C["one_f"])[:, 0:1])
    e_ts(s, "dve", rf(C1)[:, :], rf(C1)[:, :], -8.0, None, ALU.mult)
    GAW = p.sb("GAW", [128, 4, 128], BF16)
    GXW = p.sb("GXW", [128, 4, 128], BF16)
    e_memset(s, "dve", rf(GAW)[:, :, :], 0.0)
    e_memset(s, "dve", rf(GXW)[:, :, :], 0.0)
    for n in range(8):
        ps_ = slice((n % 2) * 64, (n % 2) * 64 + 64)
        s.dma("pool", GAW[ps_, n // 2, ps_], D["l0_gate_a_w"][n], writes=GAW.all())
        s.dma("pool", GXW[ps_, n // 2, ps_], D["l0_gate_x_w"][n], writes=GXW.all())
    W = [p.sb(f"WL{i}", [128, 8, 256], BF16) for i in range(2)]
    XBs = [p.sb(f"XB{i}", [128, 515], F32) for i in range(2)]
    HHs = [[p.sb(f"HH{j}_{i}", [128, 512], F32) for i in range(2)] for j in range(2)]
    ts_ = [{n: p.sb(f"{n}{j}", [128, 512], F32) for n in ("GB", "XC", "R", "IG", "A", "U", "T1", "T2")} for j in range(2)]
    XCbs = [p.sb(f"XCb{j}", [128, 512], BF16) for j in range(2)]

    def unit(c, tb, j):
        w = W[j]
        XB, t_, XCb = XBs[j], ts_[j], XCbs[j]
        col = lambda n: rf(PL)[:, c, n:n + 1]
        tok = tb * 512
        px, pg = PS[2 * j], PS[2 * j + 1]
        for kc in range(8):
            e_mm(s, rf(px)[:, :], rf(w)[:, kc, 0:128], Ref(HT[:, kc, tok:tok + 512], HT.b(tb)), kc == 0, kc == 7)
        for kc in range(8):
            e_mm(s, rf(pg)[:, :], rf(w)[:, kc, 128:256], Ref(HT[:, kc, tok:tok + 512], HT.b(tb)), kc == 0, kc == 7)
        if tb == 0:
            e_memset(s, "dve", rf(XB)[:, 0:3], 0.0)
        else:
            e_copy(s, "dve", rf(XB)[:, 0:3], rf(XB)[:, 512:515])
        yield
        e_copy(s, "act", rf(XB)[:, 3:515], rf(px)[:, :])
        e_copy(s, "act", rf(t_["GB"])[:, :], rf(pg)[:, :])
        yield
        XC = t_["XC"]
        e_ts(s, "dve", rf(XC)[:, :], rf(XB)[:, 3:515], col(3), col(4), ALU.mult, ALU.add)
        for i in range(3):
            e_stt(s, "dve", rf(XC)[:, :], rf(XB)[:, i:i + 512], col(i), rf(XC)[:, :], ALU.mult, ALU.add)
        GB, T2 = t_["GB"], t_["T2"]
        e_act(s, rf(T2)[:, :], rf(GB)[:, :], AF.Gelu_apprx_tanh)
        yield
        e_copy(s, "act", rf(XCb)[:, :], rf(XC)[:, :])
        yield
        pr_, pig = PS[4 + 2 * j], PS[5 + 2 * j]
        e_mm(s, rf(pr_)[:, :], rf(GAW)[:, c, :], rf(XCb)[:, :])
        e_mm(s, rf(pig)[:, :], rf(GXW)[:, c, :], rf(XCb)[:, :])
        yield
        e_act(s, rf(t_["R"])[:, :], rf(pr_)[:, :], AF.Sigmoid, bias=col(5))
        e_act(s, rf(t_["IG"])[:, :], rf(pig)[:, :], AF.Sigmoid, bias=col(6))
        yield
        A = t_["A"]
        e_act(s, rf(A)[:, :], rf(t_["R"])[:, :], AF.Exp, scale=rf(C1)[:, c:c + 1])
        T1, U = t_["T1"], t_["U"]
        e_tt(s, "dve", rf(U)[:, :], rf(t_["IG"])[:, :], rf(XC)[:, :], ALU.mult)
        yield
        e_tt(s, "dve", rf(T1)[:, :], rf(A)[:, :], rf(A)[:, :], ALU.mult)
        yield
        e_ts(s, "dve", rf(T1)[:, :], rf(T1)[:, :], -1.0, 1.0, ALU.mult, ALU.add)
        yield
        e_act(s, rf(T1)[:, :], rf(T1)[:, :], AF.Sqrt)
        yield
        e_tt(s, "dve", rf(U)[:, :], rf(U)[:, :], rf(T1)[:, :], ALU.mult)
        yield
        H = HHs[j][tb % 2]
        Hp = HHs[j][(tb + 1) % 2]
        init = 0.0 if tb == 0 else Hp[:, 511:512]
        s.op("dve", lambda E: E.tensor_tensor_scan(
            out=H[:, :], data0=A[:, :], data1=U[:, :], initial=init, op0=ALU.mult, op1=ALU.add),
            reads=A.all() + U.all() + (Hp.all() if tb else []), writes=H.all())
        yield
        s.op("dve", lambda E: E.tensor_tensor(
            out=OT[:, 4 + c, tok:tok + 512], in0=H[:, :], in1=T2[:, :], op=ALU.mult),
            reads=H.all() + T2.all(), writes=OT.all())

    for cp in range(2):
        for j in range(2):
            c = 2 * cp + j
            s.dma("pool", W[j][:, :, :], D["l0_w_lru"][c].rearrange("p (kc n) -> p kc n", kc=8), writes=W[j].all())
        for tb in range(4):
            gens = [unit(2 * cp + j, tb, j) for j in range(2)]
            alive = [True, True]
            while any(alive):
                for j in range(2):
                    if alive[j]:
                        try:
                            next(gens[j])
                        except StopIteration:
                            alive[j] = False


def rwkv_part(k, C, p, HT, OT, PS, D, XT):
    s = k.s
    spill = D["xt_spill"]
    for kc in range(8):
        s.dma("sp", spill[:, kc, :], XT[:, kc, :], reads=XT.all(), writes=spill.all())
    PH = p.sb("PH", [128, 4, 8], F32)
    s.dma("sp", PH[:, :, :], D["l0_ph"][:, :].rearrange("p (h n) -> p h n", h=4), writes=PH.all())
    OM = p.sb("OM", [128, 4, 4], F32)
    e_ts(s, "dve", rf(OM)[:, :, 0:3], rf(PH)[:, :, 0:3], -1.0, 1.0, ALU.mult, ALU.add)
    e_ts(s, "dve", rf(OM)[:, :, 3:4], rf(PH)[:, :, 6:7], -1.0, 1.0, ALU.mult, ALU.add)
    RKb = p.sb("RKb", [128, 4], BF16)
    e_copy(s, "dve", rf(RKb)[:, :], rf(PH)[:, :, 7])
    MUL = p.sb("MUL", [128, 3], F32)
    s.dma("sp", MUL[:, :], D["l0_mul"][:, :], writes=MUL.all())
    OML = p.sb("OML", [128, 3], F32)
    e_ts(s, "dve", rf(OML)[:, :], rf(MUL)[:, :], -1.0, 1.0, ALU.mult, ALU.add)
    LNGBs = [p.sb(f"LNGB{i}", [128, 2, 64], F32) for i in range(2)]
    W2 = p.sb("W2", [64, 512], BF16)
    A2 = p.sb("A2", [64, 512], BF16)
    G2 = p.sb("G2", [128, 512], BF16)
    s.dma("pool", W2[:, :], D["l0_w2"][:, :], writes=W2.all())
    s.dma("pool", A2[:, :], D["l0_a2"][:, :], writes=A2.all())
    s.dma("pool", G2[:, :], D["l0_g2"][:, :], writes=G2.all())
    ob = C["ones_bf"]
    BLK = p.sb("BLK", [128, 128], BF16)
    e_memset(s, "dve", rf(BLK)[:, :], 0.0)
    e_memset(s, "dve", rf(BLK)[0:64, 0:64], 1.0)
    e_memset(s, "dve", rf(BLK)[64:128, 64:128], 1.0)
    M512 = p.sb("M512", [128, 512], BF16)
    MUS = p.sb("MUS", [128, 512], BF16)
    MUI = p.sb("MUI", [128, 512], BF16)
    MLS = p.sb("MLS", [128, 512], BF16)
    ID8 = p.sb("ID8", [128, 512], BF16)
    for hh in range(2):
        hs = slice(hh * 64, hh * 64 + 64)
        for dst, pat, cmp_, cm in ((M512, [[0, 8], [1, 64]], ALU.is_gt, 0), (MUS, [[0, 8], [1, 64]], ALU.is_gt, -1),
                                   (MUI, [[0, 8], [1, 64]], ALU.is_ge, -1), (MLS, [[0, 8], [-1, 64]], ALU.is_gt, 1),
                                   (ID8, [[0, 8], [-1, 64]], ALU.is_equal, 1)):
            s.op("pool", lambda E, dst=dst, pat=pat, cmp_=cmp_, cm=cm, hs=hs: E.affine_select(
                out=dst[hs, :], in_=ob[hs, :], pattern=pat, compare_op=cmp_, fill=0.0, base=0, channel_multiplier=cm),
                reads=ob.all(), writes=dst.all())
    ident = C["ident"]

    TW = p.sb("TW", [64, SEQ], BF16)
    AL = p.sb("AL", [64, SEQ], BF16)
    SGL = p.sb("SGL", [128, SEQ], BF16)
    with k.phase() as p0:
        WLo = p0.sb("WLo", [128, 8, 256], BF16)
        s.dma("pool", WLo[:, :, :], D["l0_w_lora"][:, :].rearrange("p (kc n) -> p kc n", kc=8), writes=WLo.all())
        PAl = [p0.sb(f"PAl{i}", [128, 513], F32) for i in range(3)]
        TMPl = [p0.sb(f"TMPl{i}", [128, 512], F32) for i in range(3)]

        def lora_chain(which, c0, c1, npart, dst):
            PA, tmpl = PAl[which], TMPl[which]
            for tb in range(4):
                tok = tb * 512
                pp = PS[which * 2 + tb % 2]
                for kc in range(8):
                    e_mm(s, rf(pp)[0:npart, :], rf(WLo)[:, kc, c0:c1], Ref(HT[:, kc, tok:tok + 512], HT.b(tb)), kc == 0, kc == 7)
                if tb == 0:
                    e_memset(s, "dve", rf(PA)[0:npart, 0:1], 0.0)
                else:
                    e_copy(s, "dve", rf(PA)[0:npart, 0:1], rf(PA)[0:npart, 512:513])
                yield
                e_copy(s, "act", rf(PA)[0:npart, 1:513], rf(pp)[0:npart, :])
                yield
                e_act(s, rf(tmpl)[0:npart, :], rf(PA)[0:npart, 0:512], AF.Copy, scale=rf(MUL)[0:npart, which:which + 1])
                yield
                e_stt(s, "dve", rf(tmpl)[0:npart, :], rf(PA)[0:npart, 1:513], rf(OML)[0:npart, which:which + 1],
                      rf(tmpl)[0:npart, :], ALU.mult, ALU.add)
                yield
                if which == 0:
                    e_act(s, rf(dst)[:, tok:tok + 512], rf(tmpl)[0:64, :], AF.Tanh)
                elif which == 1:
                    e_copy(s, "act", rf(dst)[:, tok:tok + 512], rf(tmpl)[0:64, :])
                else:
                    e_act(s, rf(dst)[:, tok:tok + 512], rf(tmpl)[:, :], AF.Sigmoid)
                yield

        lbg = Bg()
        for which, (c0, c1, npart, dst) in enumerate(((0, 64, 64, TW), (64, 128, 64, AL), (128, 256, 128, SGL))):
            lbg.add(lora_chain(which, c0, c1, npart, dst), 1)
        lbg.drain()

    s.barrier()
    XTf = XT.t
    XTb = XT.t.bitcast(BF16)
    f32n = ("r", "k", "SIG", "A", "KKN", "KH", "CUM", "EC", "EX", "EN", "TMP")
    b16n = ("Rt", "At", "Bt", "Kt", "Bh", "Kh", "RK", "VT", "KK2")
    t64n = ("V64", "BH64", "KH64", "N", "Q", "N2", "Q2", "XA", "LAK", "ARB", "ARK")
    sets = []
    for S in range(2):
        B = {}
        if S == 0:
            B["WH"] = p.sb("WH", [128, 8, 384], BF16)
            B["PA"] = [p.sb(f"PA{i}", [128, 513], F32) for i in range(3)]
            F = {n: p.sb("f_" + n, [128, 512], F32) for n in f32n}
            Bf = {n: p.sb("b_" + n, [128, 512], BF16) for n in b16n}
            T64 = {n: p.sb("t_" + n, [128, 512], BF16) for n in t64n}
        else:
            B["WH"] = T("WH1", XTb[:, 7, 0:3072].rearrange("p (kc n) -> p kc n", kc=8))
            B["PA"] = [T(f"PA1_{i}", XTf[:, 3, i * 513:(i + 1) * 513]) for i in range(3)]
            F = {n: T("f1_" + n, XTf[:, i // 4, (i % 4) * 512:(i % 4) * 512 + 512]) for i, n in enumerate(f32n)}
            bl = list(b16n) + list(t64n)
            vb = {n: T("b1_" + n, XTb[:, 4 + i // 8, (i % 8) * 512:(i % 8) * 512 + 512]) for i, n in enumerate(bl)}
            Bf = {n: vb[n] for n in b16n}
            T64 = {n: vb[n] for n in t64n}
        F["EH"] = F["TMP"]
        F["BA"] = F["SIG"]
        T64["YA"] = T64["LAK"]
        B["F"], B["Bf"], B["T64"] = F, Bf, T64
        B["R0"], B["YF"], B["YQ"], B["GT"] = F["SIG"], F["EX"], F["EN"], F["KH"]
        B["ST"] = p.sb(f"ST{S}", [128, 8, 4], F32)
        B["RKS"] = p.sb(f"RKS{S}", [128, 8], F32)
        B["Pf"] = p.sb(f"Pf{S}", [128, 64], F32)
        B["Pb"] = p.sb(f"Pb{S}", [128, 64], BF16)
        B["RR"] = p.sb(f"RR{S}", [128, 64], BF16)
        B["UB"] = p.sb(f"UB{S}", [128, 64], BF16)
        B["LNGB"] = LNGBs[S]
        B["PY"] = PS[4 + S]
        B["PT1"] = PS[6 + S]
        sets.append(B)
    b3 = lambda r_: Ref(r_.ap.rearrange("p (a b) -> p a b", a=8), r_.bufs)
    HS = (slice(0, 64), slice(64, 128))
    st = {"sci": 0}

    def newps():
        st["sci"] += 1
        return PS[st["sci"] % 4]

    def mm2(out_t, col0, ncol, lhs_fn, rhs_fn, start=True, stop=True):
        for hs in HS:
            e_mm(s, rf(out_t)[hs, col0:col0 + ncol], lhs_fn(hs), rhs_fn(hs), start, stop)

    def unit(hp, gq, B):
        F, Bf, T64, PA, w = B["F"], B["Bf"], B["T64"], B["PA"], B["WH"]
        R0, YF, YQ, GT, ST, RKS = B["R0"], B["YF"], B["YQ"], B["GT"], B["ST"], B["RKS"]
        Pf, Pb, RR, UB, LNGB = B["Pf"], B["Pb"], B["RR"], B["UB"], B["LNGB"]
        hc = lambda n: rf(PH)[:, hp, n:n + 1]
        tok = gq * 512
        for which, nm in enumerate(("r", "k", "v")):
            pp = newps()
            for kc in range(8):
                e_mm(s, rf(pp)[:, :], rf(w)[:, kc, which * 128:(which + 1) * 128],
                     Ref(HT[:, kc, tok:tok + 512], HT.b(gq)), kc == 0, kc == 7)
            pa = PA[which]
            if gq == 0:
                e_memset(s, "dve", rf(pa)[:, 0:1], 0.0)
            else:
                e_copy(s, "dve", rf(pa)[:, 0:1], rf(pa)[:, 512:513])
            e_copy(s, "act", rf(pa)[:, 1:513], rf(pp)[:, :])
            yield
            tmp_ = rf(F["TMP"])[:, :] if which != 1 else rf(F["CUM"])[:, :]
            e_act(s, tmp_, rf(pa)[:, 0:512], AF.Copy, scale=hc(which))
            dst_ = rf(Bf["VT"])[:, :] if nm == "v" else rf(F[nm])[:, :]
            e_stt(s, "dve", dst_, rf(pa)[:, 1:513], rf(OM)[:, hp, which:which + 1], tmp_, ALU.mult, ALU.add)
        r_, k_ = rf(F["r"])[:, :], rf(F["k"])[:, :]
        pz = newps()
        e_mm(s, rf(pz)[:, :], rf(W2)[:, hp * 128:(hp + 1) * 128], rf(TW)[:, tok:tok + 512])
        e_act(s, rf(F["SIG"])[:, :], rf(pz)[:, :], AF.Sigmoid, bias=hc(3))
        pz2 = newps()
        e_mm(s, rf(pz2)[:, :], rf(A2)[:, hp * 128:(hp + 1) * 128], rf(AL)[:, tok:tok + 512])
        e_act(s, rf(F["A"])[:, :], rf(pz2)[:, :], AF.Sigmoid, bias=hc(4))
        yield
        e_ts(s, "dve", rf(F["KKN"])[:, :], k_, hc(5), None, ALU.mult)
        e_act(s, rf(Bf["KK2"])[:, :], rf(F["KKN"])[:, :], AF.Square)
        yield
        pz = newps()
        e_mm(s, rf(pz)[:, :], rf(BLK)[:, :], rf(Bf["KK2"])[:, :])
        e_act(s, rf(F["TMP"])[:, :], rf(pz)[:, :], AF.Sqrt)
        yield
        e_ts(s, "dve", rf(F["TMP"])[:, :], rf(F["TMP"])[:, :], 1e-12, None, ALU.max)
        s.op("dve", lambda E: E.reciprocal(out=F["TMP"][:, :], in_=F["TMP"][:, :]), reads=F["TMP"].all(),
             writes=F["TMP"].all())
        e_tt(s, "dve", rf(F["KKN"])[:, :], rf(F["KKN"])[:, :], rf(F["TMP"])[:, :], ALU.mult)
        e_act(s, rf(F["KH"])[:, :], rf(F["A"])[:, :], AF.Identity, bias=rf(OM)[:, hp, 3:4], scale=hc(6))
        e_tt(s, "dve", rf(F["KH"])[:, :], rf(F["KH"])[:, :], k_, ALU.mult)
        yield
        s.op("dve", lambda E: E.tensor_tensor_scan(out=F["CUM"][:, :], data0=M512[:, :], data1=F["SIG"][:, :],
                                                   initial=0.0, op0=ALU.mult, op1=ALU.add),
             reads=M512.all() + F["SIG"].all(), writes=F["CUM"].all())
        yield
        cum = rf(F["CUM"])[:, :]
        e_act(s, rf(F["EC"])[:, :], cum, AF.Exp, scale=NEG_EXP_HALF)
        e_tt(s, "dve", rf(F["EX"])[:, :], cum, rf(F["SIG"])[:, :], ALU.subtract)
        e_act(s, rf(F["EN"])[:, :], cum, AF.Exp, scale=-NEG_EXP_HALF)
        cum3 = b3(cum)
        cend = Ref(cum3.ap[:, :, 63:64].to_broadcast([128, 8, 64]), cum.bufs)
        e_tt(s, "dve", b3(rf(F["EH"])[:, :]), cend, cum3, ALU.subtract)
        yield
        e_act(s, rf(F["EX"])[:, :], rf(F["EX"])[:, :], AF.Exp, scale=NEG_EXP_HALF)
        e_act(s, rf(F["EH"])[:, :], rf(F["EH"])[:, :], AF.Exp, scale=NEG_EXP_HALF)
        e_tt(s, "dve", rf(Bf["Rt"])[:, :], r_, rf(F["EC"])[:, :], ALU.mult)
        e_tt(s, "dve", rf(Bf["RK"])[:, :], r_, rf(F["KH"])[:, :], ALU.mult)
        e_tt(s, "dve", rf(F["BA"])[:, :], rf(F["KKN"])[:, :], rf(F["A"])[:, :], ALU.mult)
        yield
        e_stt(s, "dve", rf(Bf["At"])[:, :], rf(F["KKN"])[:, :], -1.0, rf(F["EX"])[:, :], ALU.mult, ALU.mult)
        e_tt(s, "dve", rf(Bf["Bt"])[:, :], rf(F["BA"])[:, :], rf(F["EN"])[:, :], ALU.mult)
        e_tt(s, "dve", rf(Bf["Bh"])[:, :], rf(F["BA"])[:, :], rf(F["EH"])[:, :], ALU.mult)
        e_tt(s, "dve", rf(Bf["Kt"])[:, :], rf(F["KH"])[:, :], rf(F["EN"])[:, :], ALU.mult)
        e_tt(s, "dve", rf(Bf["Kh"])[:, :], rf(F["KH"])[:, :], rf(F["EH"])[:, :], ALU.mult)
        yield
        blk = lambda n, c8, hs: rf(Bf[n])[hs, c8 * 64:(c8 + 1) * 64]
        tb_ = lambda n, c8, hs: rf(T64[n])[hs, c8 * 64:(c8 + 1) * 64]
        idh = lambda hs: rf(ident)[hs, hs]
        for src, dst in (("VT", "V64"), ("Bh", "BH64"), ("Kh", "KH64")):
            pt = newps()
            for c8 in range(8):
                mm2(pt, c8 * 64, 64, lambda hs, c8=c8, src=src: blk(src, c8, hs), idh)
            e_copy(s, "act", rf(T64[dst])[:, :], rf(pt)[:, :])
            yield
        for lh, rh, mask, dst in (("Bt", "At", MUS, "N"), ("At", "Bt", MLS, "Q"), ("Kt", "At", MUS, "LAK"),
                                  ("Bt", "Rt", MUI, "ARB"), ("Kt", "Rt", MUI, "ARK")):
            pt = newps()
            for c8 in range(8):
                mm2(pt, c8 * 64, 64, lambda hs, c8=c8, lh=lh: blk(lh, c8, hs), lambda hs, c8=c8, rh=rh: blk(rh, c8, hs))
            e_tt(s, "dve", rf(T64[dst])[:, :], rf(pt)[:, :], rf(mask)[:, :], ALU.mult)
            yield
        e_tt(s, "dve", rf(T64["XA"])[:, :], rf(T64["N"])[:, :], rf(ID8)[:, :], ALU.add)
        Pn, Qn, Pn2, Qn2 = "N", "Q", "N2", "Q2"
        for lvl in range(1, 6):
            pq = newps()
            for c8 in range(8):
                mm2(pq, c8 * 64, 64, lambda hs, c8=c8, Pn=Pn: tb_(Pn, c8, hs), lambda hs, c8=c8, Qn=Qn: tb_(Qn, c8, hs))
            e_copy(s, "act", rf(T64[Qn2])[:, :], rf(pq)[:, :])
            if lvl < 5:
                pp_ = newps()
                for c8 in range(8):
                    mm2(pp_, c8 * 64, 64, lambda hs, c8=c8, Qn=Qn: tb_(Qn, c8, hs),
                        lambda hs, c8=c8, Pn=Pn: tb_(Pn, c8, hs))
                e_copy(s, "act", rf(T64[Pn2])[:, :], rf(pp_)[:, :])
            yield
            px = newps()
            for c8 in range(8):
                mm2(px, c8 * 64, 64, lambda hs, c8=c8, Qn2=Qn2: tb_(Qn2, c8, hs), lambda hs, c8=c8: tb_("XA", c8, hs))
            e_tt(s, "dve", rf(T64["XA"])[:, :], rf(T64["XA"])[:, :], rf(px)[:, :], ALU.add)
            yield
            Pn, Pn2 = Pn2, Pn
            Qn, Qn2 = Qn2, Qn
        pr0 = newps()
        for c8 in range(8):
            mm2(pr0, c8 * 64, 64, lambda hs, c8=c8: tb_("LAK", c8, hs), lambda hs, c8=c8: tb_("V64", c8, hs))
        e_copy(s, "act", rf(R0)[:, :], rf(pr0)[:, :])
        pg = newps()
        e_mm(s, rf(pg)[:, :], rf(G2)[:, hp * 128:(hp + 1) * 128], rf(SGL)[:, tok:tok + 512])
        e_copy(s, "act", rf(GT)[:, :], rf(pg)[:, :])
        yield
        PY, PT1 = B["PY"], B["PT1"]
        pbh = lambda hs: rf(Pb)[hs, :]
        ubh = lambda hs: rf(UB)[hs, :]
        for c8 in range(8):
            mm2(PT1, 0, 64, lambda hs: blk("At", c8, hs), pbh)
            mm2(PY, c8 * 64, 64, lambda hs: blk("Rt", c8, hs), pbh, True, False)
            e_tt(s, "dve", rf(RR)[:, :], rf(PT1)[:, 0:64], rf(R0)[:, c8 * 64:(c8 + 1) * 64], ALU.add)
            yield
            mm2(PT1, 64, 64, lambda hs: tb_("XA", c8, hs), lambda hs: rf(RR)[hs, :])
            e_copy(s, "act", rf(UB)[:, :], rf(PT1)[:, 64:128])
            yield
            mm2(PT1, 128, 64, lambda hs: tb_("KH64", c8, hs), lambda hs: tb_("V64", c8, hs), True, False)
            mm2(PT1, 128, 64, lambda hs: tb_("BH64", c8, hs), ubh, False, True)
            mm2(PY, c8 * 64, 64, lambda hs: tb_("ARB", c8, hs), ubh, False, False)
            mm2(PY, c8 * 64, 64, lambda hs: tb_("ARK", c8, hs), lambda hs: tb_("V64", c8, hs), False, True)
            e_stt(s, "dve", rf(Pf)[:, :], rf(Pf)[:, :], rf(F["EC"])[:, c8 * 64 + 63:c8 * 64 + 64], rf(PT1)[:, 128:192],
                  ALU.mult, ALU.add)
            e_copy(s, "act", rf(Pb)[:, :], rf(Pf)[:, :])
            yield
        e_copy(s, "act", rf(YF)[:, :], rf(PY)[:, :])
        yf3 = b3(rf(YF)[:, :])
        yq3 = b3(rf(YQ)[:, :])
        prk = newps()
        for c8 in range(8):
            mm2(prk, c8, 1, lambda hs: blk("RK", c8, hs), lambda hs: rf(RKb)[hs, hp:hp + 1])
        e_copy(s, "act", rf(RKS)[:, :], rf(prk)[:, 0:8])
        yield
        s.op("dve", lambda E: E.tensor_reduce(out=ST[:, :, 0], in_=YF[:, :].rearrange("p (a b) -> p a b", a=8),
                                              axis=AX.X, op=ALU.add), reads=YF.all(), writes=ST.all())
        e_act(s, rf(YQ)[:, :], rf(YF)[:, :], AF.Square)
        yield
        s.op("dve", lambda E: E.tensor_reduce(out=ST[:, :, 1], in_=YQ[:, :].rearrange("p (a b) -> p a b", a=8),
                                              axis=AX.X, op=ALU.add), reads=YQ.all(), writes=ST.all())
        e_ts(s, "dve", rf(ST)[:, :, 2], rf(ST)[:, :, 0], 1.0 / 64, None, ALU.mult)
        e_tt(s, "dve", rf(ST)[:, :, 0], rf(ST)[:, :, 2], rf(ST)[:, :, 2], ALU.mult)
        e_stt(s, "dve", rf(ST)[:, :, 1], rf(ST)[:, :, 1], 1.0 / 64, rf(ST)[:, :, 0], ALU.mult, ALU.subtract)
        e_ts(s, "dve", rf(ST)[:, :, 1], rf(ST)[:, :, 1], GN_EPS, None, ALU.add)
        e_act(s, rf(ST)[:, :, 3], rf(ST)[:, :, 1], AF.Sqrt)
        yield
        s.op("dve", lambda E: E.reciprocal(out=ST[:, :, 3], in_=ST[:, :, 3]), reads=ST.all(), writes=ST.all())
        mean_b = Ref(ST[:, :, 2:3].to_broadcast([128, 8, 64]), ST.all())
        rstd_b = Ref(ST[:, :, 3:4].to_broadcast([128, 8, 64]), ST.all())
        e_tt(s, "dve", yf3, yf3, mean_b, ALU.subtract)
        e_tt(s, "dve", yf3, yf3, rstd_b, ALU.mult)
        rks_b = Ref(RKS[:, :].rearrange("p (a b) -> p a b", b=1).to_broadcast([128, 8, 64]), RKS.all())
        e_tt(s, "dve", yq3, b3(rf(T64["V64"])[:, :]), rks_b, ALU.mult)
        lng = Ref(LNGB[:, 0:1, :].to_broadcast([128, 8, 64]), LNGB.all())
        lnb = Ref(LNGB[:, 1:2, :].to_broadcast([128, 8, 64]), LNGB.all())
        yield
        e_tt(s, "dve", yf3, yf3, lng, ALU.mult)
        e_tt(s, "dve", yf3, yf3, lnb, ALU.add)
        yield
        e_tt(s, "dve", rf(T64["YA"])[:, :], rf(YF)[:, :], rf(YQ)[:, :], ALU.add)
        yield
        pt = newps()
        for c8 in range(8):
            mm2(pt, c8 * 64, 64, lambda hs: tb_("YA", c8, hs), idh)
        s.op("dve", lambda E: E.tensor_tensor(
            out=OT[:, hp, tok:tok + 512], in0=pt[:, :], in1=GT[:, :], op=ALU.mult),
            reads=pt.all() + GT.all(), writes=OT.all())
        yield

    def stream(S, hps):
        B = sets[S]
        for hp in hps:
            w = B["WH"]
            s.dma("pool", w[:, :, :], D["l0_w_hp"][hp].rearrange("p (kc n) -> p kc n", kc=8), writes=w.all())
            LNGB = B["LNGB"]
            for hh in range(2):
                h = 2 * hp + hh
                s.dma("sp", LNGB[HS[hh], 0, :], D["l0_lnx_g"][h * 64:(h + 1) * 64].partition_broadcast(64),
                      writes=LNGB.all())
                s.dma("sp", LNGB[HS[hh], 1, :], D["l0_lnx_b"][h * 64:(h + 1) * 64].partition_broadcast(64),
                      writes=LNGB.all())
            e_memset(s, "dve", rf(B["Pf"])[:, :], 0.0)
            e_memset(s, "dve", rf(B["Pb"])[:, :], 0.0)
            yield
            for gq in range(4):
                yield from unit(hp, gq, B)

    gens = [stream(0, (0, 2)), stream(1, (1, 3))]
    alive = [True, True]
    first = True
    while any(alive):
        for S in range(2):
            if alive[S]:
                try:
                    next(gens[S])
                except StopIteration:
                    alive[S] = False
            if first and S == 0:
                for _ in range(STAGGER):
                    next(gens[0])
                first = False
    s.barrier()
    for kc in range(8):
        s.dma("sp", XT[:, kc, :], spill[:, kc, :], reads=spill.all(), writes=XT.all())


GAIN_NAMES = ["l0_ffn1_pre_g", "l0_ffn1_post_g", "l0_mix_pre_g", "l0_mix_post_g", "l0_ffn2_pre_g", "l0_ffn2_post_g",
              "l1_ffn1_pre_g", "l1_ffn1_post_g", "l1_mix_pre_g", "l1_mix_post_g", "l1_ffn2_pre_g", "l1_ffn2_post_g"]
HALF_GAINS = [1, 5, 7, 11]


def build_program(stages=("f01", "m0", "f02f11", "m1", "f12"), dbg=False):
    k = KB()
    nc = k.nc
    s = k.s
    xT_d = k.dram_in("xT", [DM, SEQ])
    gains_d = k.dram_in("gains", [128, 12 * 8])
    ffn_d = {}
    for nm in ("l0_ffn1", "l0_ffn2", "l1_ffn1", "l1_ffn2"):
        ffn_d[nm] = (k.dram_in(nm + "_w_in", [NJ, 128, 2048]), k.dram_in(nm + "_w_out", [2, 8, 128, 11 * 128]))
    D = {}
    for nm, shp in (("l0_pl", [128, 32]), ("l0_ph", [128, 32]), ("l0_mul", [128, 3]), ("l0_lnx_g", [512]),
                    ("l0_lnx_b", [512]), ("l0_w2", [64, 512]), ("l0_a2", [64, 512]), ("l0_g2", [128, 512]),
                    ("l0_gate_a_w", [8, 64, 64]), ("l0_gate_x_w", [8, 64, 64]), ("l0_w_lru", [4, 128, 8 * 256]),
                    ("l0_w_lora", [128, 8 * 256]), ("l0_w_hp", [4, 128, 8 * 384]), ("l0_w_out", [DM, DM])):
        D[nm] = k.dram_in(nm, shp)
    D["xt_spill"] = T("xt_spill", nc.dram_tensor("xt_spill", [128, 8, SEQ], F32, kind="Internal").ap())
    if dbg:
        D["dbg_OT"] = k.dram_out("dbg_OT", [128, 8, SEQ], BF16)
    wqkv_d = k.dram_in("l1_w_qkv", [8, 128, 8 * 384])
    l1_wo_d = k.dram_in("l1_w_out", [DM, DM])
    outT_d = k.dram_out("outT", [DM, SEQ])

    with k.es:
        XT = k.sb("XT", [128, 8, SEQ], F32, parts=4)
        C = {}
        C["gains"] = k.sb("gains", [128, 12, 8], F32)
        C["ones_m"] = k.sb("ones_m", [128, 128], BF16)
        PS = [k.ps(f"ps{i}", [128, 512]) for i in range(8)]
        C["HTG"] = k.sb("HTG", [128, 8, SEQ], BF16, parts=4)
        C["ht_ready"] = None

        s.op("dve", lambda e: e.memset(C["ones_m"][:, :], 1.0 / DM), writes=C["ones_m"].all())
        C["one_f"] = k.sb("one_f", [128, 1], F32)
        C["ones_col"] = k.sb("ones_col", [128, 1], BF16)
        C["ones_bf"] = k.sb("ones_bf", [128, 512], BF16)
        C["ident"] = k.sb("ident", [128, 128], BF16)
        C["ntri"] = k.sb("ntri", [128, 128], BF16)
        s.op("dve", lambda e: e.memset(C["one_f"][:, :], 1.0), writes=C["one_f"].all())
        s.op("dve", lambda e: e.memset(C["ones_col"][:, :], 1.0), writes=C["ones_col"].all())
        s.op("dve", lambda e: e.memset(C["ones_bf"][:, :], 1.0), writes=C["ones_bf"].all())
        s.op("pool", lambda e: e.affine_select(out=C["ident"][:, :], in_=C["ones_bf"][:, 0:128], pattern=[[-1, 128]],
                                               compare_op=ALU.is_equal, fill=0.0, base=0, channel_multiplier=1),
             reads=C["ones_bf"].all(), writes=C["ident"].all())
        s.op("pool", lambda e: e.affine_select(out=C["ntri"][:, :], in_=C["ones_bf"][:, 0:128], pattern=[[-1, 128]],
                                               compare_op=ALU.is_ge, fill=0.0, base=0, channel_multiplier=1),
             reads=C["ones_bf"].all(), writes=C["ntri"].all())
        s.op("dve", lambda e: e.tensor_scalar(out=C["ntri"][:, :], in0=C["ntri"][:, :], scalar1=-1.0, scalar2=None,
                                              op0=ALU.mult),
             reads=C["ntri"].all(), writes=C["ntri"].all())
        C["eps"] = k.sb("eps", [128, 1], F32)
        s.op("dve", lambda e: e.memset(C["eps"][:, :], NORM_EPS), writes=C["eps"].all())
        s.dma("sp", C["gains"][:, :, :], gains_d[:, :].rearrange("p (n c) -> p n c", n=12), writes=C["gains"].all())
        for gi in HALF_GAINS:
            s.op("dve", lambda e, gi=gi: e.tensor_scalar(out=C["gains"][:, gi, :], in0=C["gains"][:, gi, :],
                                                         scalar1=0.5, scalar2=None, op0=ALU.mult),
                 reads=C["gains"].all(), writes=C["gains"].all())
        for tb in range(4):
            for kc in range(8):
                s.dma("sp", XT[:, kc, tb * 512:(tb + 1) * 512], xT_d[kc * 128:(kc + 1) * 128, tb * 512:(tb + 1) * 512],
                      writes=XT.b(tb))

        PRE_GI = {"f01": 0, "m0": 2, "f02": 4, "f02f11": 4, "f11": 6, "m1": 8, "f12": 10}
        for si, st in enumerate(stages):
            nxt = PRE_GI[stages[si + 1]] if si + 1 < len(stages) else None
            if st == "f01":
                ffn_stage(k, C, XT, PS, [(*ffn_d["l0_ffn1"], 0, 1)], nxt)
            elif st == "f02":
                ffn_stage(k, C, XT, PS, [(*ffn_d["l0_ffn2"], 4, 5)], nxt)
            elif st == "f02f11":
                ffn_stage(k, C, XT, PS, [(*ffn_d["l0_ffn2"], 4, 5), (*ffn_d["l1_ffn1"], 6, 7)], nxt)
            elif st == "f11":
                ffn_stage(k, C, XT, PS, [(*ffn_d["l1_ffn1"], 6, 7)], nxt)
            elif st == "m0":
                mixer0_stage(k, C, XT, PS, D, 2, 3, nxt)
            elif st == "m1":
                if dbg:
                    C["dbg_OT"] = D["dbg_OT"]
                attn_stage(k, C, XT, PS, wqkv_d, l1_wo_d, 8, 9, nxt)
            elif st == "f12":
                ffn_stage(k, C, XT, PS, [(*ffn_d["l1_ffn2"], 10, 11)], nxt)

        for tb in range(4):
            for kc in range(8):
                s.dma("sp", outT_d[kc * 128:(kc + 1) * 128, tb * 512:(tb + 1) * 512], XT[:, kc, tb * 512:(tb + 1) * 512],
                      reads=XT.b(tb), writes=outT_d.all())
        s.barrier(engines=["sp"])
    return nc


def _col(v):
    return np.ascontiguousarray(np.asarray(v, np.float32).reshape(8, 128).T)


def prep_shared(inp):
    d = {}
    d["gains"] = np.ascontiguousarray(np.concatenate([_col(inp[n]) for n in GAIN_NAMES], axis=1))
    for nm in ("l0_ffn1", "l0_ffn2", "l1_ffn1", "l1_ffn2"):
        w_in = np.asarray(inp[nm + "_w_in"], np.float32)
        w_out = np.asarray(inp[nm + "_w_out"], np.float32)
        g = w_in[:, :DFF].reshape(8, 128, NJ, 128)
        u = w_in[:, DFF:].reshape(8, 128, NJ, 128)
        gu = np.concatenate([g, u], axis=3)
        d[nm + "_w_in"] = np.ascontiguousarray(gu.transpose(2, 1, 0, 3).reshape(NJ, 128, 2048))
        wo = w_out.reshape(2, 11, 128, 8, 128)
        d[nm + "_w_out"] = np.ascontiguousarray(wo.transpose(0, 3, 2, 1, 4).reshape(2, 8, 128, 11 * 128))
    f = lambda n: np.asarray(inp[n], np.float32)
    cw = f("l0_conv_w")
    pl = np.stack([cw[0], cw[1], cw[2], cw[3], f("l0_conv_b"), f("l0_gate_a_b"), f("l0_gate_x_b"), f("l0_lambda")], axis=1)
    d["l0_pl"] = np.ascontiguousarray(pl.reshape(4, 128, 8).transpose(1, 0, 2).reshape(128, 32))
    mu = f("l0_mu")
    ph = np.stack([mu[0:512], mu[512:1024], mu[1024:1536], f("l0_w0"), f("l0_a0"), f("l0_k_k"), f("l0_k_a"),
                   f("l0_r_k").reshape(512)], axis=1)
    d["l0_ph"] = np.ascontiguousarray(ph.reshape(4, 128, 8).transpose(1, 0, 2).reshape(128, 32))
    mul = np.zeros((128, 3), np.float32)
    mul[0:64, 0] = mu[1536:1600]
    mul[0:64, 1] = mu[1600:1664]
    mul[:, 2] = mu[1664:1792]
    d["l0_mul"] = mul
    for n in ("l0_lnx_g", "l0_lnx_b", "l0_w2", "l0_a2", "l0_g2", "l0_gate_a_w", "l0_gate_x_w", "l0_w_out"):
        d[n] = np.ascontiguousarray(f(n))
    wi = f("l0_w_in").reshape(8, 128, 2816)
    lru = np.concatenate([wi[:, :, 1792:2304].reshape(8, 128, 4, 128), wi[:, :, 2304:2816].reshape(8, 128, 4, 128)], axis=3)
    d["l0_w_lru"] = np.ascontiguousarray(lru.transpose(2, 1, 0, 3).reshape(4, 128, 8 * 256))
    d["l0_w_lora"] = np.ascontiguousarray(wi[:, :, 1536:1792].transpose(1, 0, 2).reshape(128, 8 * 256))
    hd = np.stack([wi[:, :, 0:512].reshape(8, 128, 4, 128), wi[:, :, 512:1024].reshape(8, 128, 4, 128),
                   wi[:, :, 1024:1536].reshape(8, 128, 4, 128)], axis=3)
    d["l0_w_hp"] = np.ascontiguousarray(hd.transpose(2, 1, 0, 3, 4).reshape(4, 128, 8 * 384))
    wq = np.asarray(inp["l1_w_qkv"], np.float32).reshape(8, 128, 3, 8, 128)
    d["l1_w_qkv"] = np.ascontiguousarray(wq.transpose(3, 1, 0, 2, 4).reshape(8, 128, 8 * 384))
    d["l1_w_out"] = np.ascontiguousarray(np.asarray(inp["l1_w_out"], np.float32))
    return d


_CACHE = {}


def kernel(**inputs):
    x = np.asarray(inputs["x"], np.float32)
    shared = prep_shared(inputs)
    if "nc" not in _CACHE:
        _CACHE["nc"] = build_program()
    nc = _CACHE["nc"]
    in_maps = []
    for c in range(N_CORES):
        m = dict(shared)
        m["xT"] = np.ascontiguousarray(x[c].T)
        in_maps.append(m)
    res = run_bass_kernel_spmd(nc, in_maps, core_ids=list(range(N_CORES)))
    out = np.stack([np.ascontiguousarray(res.results[c]["outT"].T) for c in range(N_CORES)], axis=0)
    return out.astype(np.float32)
```

```python
import math
from contextlib import ExitStack

import numpy as np
import concourse.bass as bass
import concourse.mybir as mybir
from concourse.bass_utils import run_bass_kernel_spmd

F32 = mybir.dt.float32
BF16 = mybir.dt.bfloat16
AF = mybir.ActivationFunctionType
ALU = mybir.AluOpType

SEQ = 2048
DM = 1024
DFF = 2816
NJ = 22
NORM_EPS = 1e-6
N_CORES = 8


class Buf:
    __slots__ = ("name", "w", "r")

    def __init__(self, name):
        self.name = name
        self.w = None
        self.r = {}


class T:
    def __init__(self, name, t, parts=1):
        self.name = name
        self.t = t
        self.bufs = [Buf(f"{name}.{i}") for i in range(parts)]

    def b(self, *idx):
        return [self.bufs[i] for i in idx]

    def all(self):
        return list(self.bufs)

    def __getitem__(self, key):
        return self.t[key]


class Sched:
    COMPUTE = ("pe", "act", "dve", "pool")

    def __init__(self, nc, es, n_dma_ch=20):
        self.nc = nc
        self.eng = {"pe": nc.tensor, "act": nc.scalar, "dve": nc.vector, "pool": nc.gpsimd, "sp": nc.sync}
        self.sems = {}
        self.cnt = {}
        for e in self.COMPUTE:
            self.sems[e] = es.enter_context(nc.semaphore(f"s_{e}"))
            self.cnt[e] = 0
        self.ch = {}
        self.ch_next = {}
        for q in ("sp", "pool", "act"):
            n = n_dma_ch if q != "act" else 4
            lst = []
            for i in range(n):
                key = f"d_{q}{i}"
                self.sems[key] = es.enter_context(nc.semaphore(key))
                self.cnt[key] = 0
                lst.append(key)
            self.ch[q] = lst
            self.ch_next[q] = 0
        self.seen = {e: {} for e in self.eng}
        self.n_wait = 0
        self.n_ins = 0

    def _wait(self, e, ev):
        key, val = ev
        if val <= 0:
            return
        if self.seen[e].get(key, 0) >= val:
            return
        self.seen[e][key] = val
        self.eng[e].wait_ge(self.sems[key], val)
        self.n_wait += 1

    def _deps(self, e, reads, writes):
        evs = {}

        def need(ev):
            if ev is None:
                return
            k_, v_ = ev
            if e == "pe" and k_ == "pe":
                return
            if evs.get(k_, 0) < v_:
                evs[k_] = v_

        for b in reads:
            need(b.w)
        for b in writes:
            need(b.w)
            for kv in b.r.items():
                need(kv)
        return evs

    def op(self, e, fn, reads=(), writes=()):
        evs = self._deps(e, reads, writes)
        for ev in evs.items():
            self._wait(e, ev)
        ins = fn(self.eng[e])
        self.cnt[e] += 1
        ev = (e, self.cnt[e])
        ins.then_inc(self.sems[e], 1)
        self.seen[e][e] = max(self.seen[e].get(e, 0), 0)
        for b in writes:
            b.w = ev
            b.r = {}
        for b in reads:
            if b.w is not ev:
                b.r[e] = self.cnt[e]
        self.n_ins += 1
        return ins

    def dma(self, q, out, in_, reads=(), writes=()):
        e = q
        evs = self._deps(e, reads, writes)
        key = self.ch[q][self.ch_next[q]]
        self.ch_next[q] = (self.ch_next[q] + 1) % len(self.ch[q])
        if evs.get(key, 0) < self.cnt[key]:
            evs[key] = self.cnt[key]
        for ev in evs.items():
            self._wait(e, ev)
        ins = self.eng[e].dma_start(out=out, in_=in_)
        self.cnt[key] += 16
        ins.then_inc(self.sems[key], 16)
        ev = (key, self.cnt[key])
        for b in writes:
            b.w = ev
            b.r = {}
        for b in reads:
            b.r[key] = self.cnt[key]
        self.n_ins += 1
        return ins

    def barrier(self, engines=None):
        evs = [(k_, v_) for k_, v_ in self.cnt.items() if v_ > 0]
        for e in (engines or self.eng):
            for ev in evs:
                if ev[0] == e:
                    continue
                self._wait(e, ev)


class Phase:
    def __init__(self, k):
        self.k = k
        self.es = ExitStack()

    def __enter__(self):
        self.es.__enter__()
        return self

    def __exit__(self, *a):
        self.k.s.barrier()
        return self.es.__exit__(*a)

    def sb(self, name, shape, dtype, parts=1):
        self.k.uid += 1
        t = self.es.enter_context(self.k.nc.sbuf_tensor(f"ph_{name}_{self.k.uid}", shape, dtype))
        return T(name, t, parts)


class KB:
    def __init__(self):
        self.nc = bass.Bass("TRN2", target_bir_lowering=False)
        self.es = ExitStack()
        self.s = Sched(self.nc, self.es)
        self.uid = 0

    def sb(self, name, shape, dtype, parts=1):
        t = self.es.enter_context(self.nc.sbuf_tensor("sb_" + name, shape, dtype))
        return T(name, t, parts)

    def ps(self, name, shape, dtype=F32, parts=1):
        t = self.es.enter_context(self.nc.psum_tensor("pp_" + name, shape, dtype))
        return T(name, t, parts)

    def dram_in(self, name, shape, dtype=F32):
        return T(name, self.nc.dram_tensor(name, list(shape), dtype, kind="ExternalInput").ap())

    def dram_out(self, name, shape, dtype=F32):
        return T(name, self.nc.dram_tensor(name, list(shape), dtype, kind="ExternalOutput").ap())

    def phase(self):
        return Phase(self)


def rms_rstd(k, C, src, src_bufs, SQ, PST, RSTD, ntok):
    s = k.s
    s.op("act", lambda e: e.activation(out=SQ[:, :, 0:ntok], in_=src, func=AF.Square),
         reads=src_bufs, writes=SQ.all())
    for kc in range(8):
        s.op("pe", lambda e, kc=kc: e.matmul(PST[:, 0:ntok], lhsT=C["ones_m"][:, :], rhs=SQ[:, kc, 0:ntok],
                                             start=(kc == 0), stop=(kc == 7)),
             reads=SQ.all() + C["ones_m"].all(), writes=PST.all())
    s.op("act", lambda e: e.activation(out=RSTD[:, 0:ntok], in_=PST[:, 0:ntok], func=AF.Sqrt, bias=C["eps"][:, 0:1]),
         reads=PST.all() + C["eps"].all(), writes=RSTD.all())
    s.op("dve", lambda e: e.reciprocal(out=RSTD[:, 0:ntok], in_=RSTD[:, 0:ntok]),
         reads=RSTD.all(), writes=RSTD.all())


class Bg:
    def __init__(self):
        self.q = []

    def add(self, gen, period=2):
        self.q.append([gen, period, period])

    def tick(self):
        for item in list(self.q):
            item[2] -= 1
            if item[2] <= 0:
                item[2] = item[1]
                try:
                    next(item[0])
                except StopIteration:
                    self.q.remove(item)

    def drain(self):
        while self.q:
            for item in list(self.q):
                try:
                    next(item[0])
                except StopIteration:
                    self.q.remove(item)


def rms_rstd_gen(k, C, src, src_bufs, SQ, PST, RSTD, ntok, fuse_sq=False):
    s = k.s
    s.op("act", lambda e: e.activation(out=SQ[:, :, 0:ntok], in_=src, func=AF.Square),
         reads=src_bufs, writes=SQ.all())
    if not fuse_sq:
        yield
    for kc in range(8):
        s.op("pe", lambda e, kc=kc: e.matmul(PST[:, 0:ntok], lhsT=C["ones_m"][:, :], rhs=SQ[:, kc, 0:ntok],
                                             start=(kc == 0), stop=(kc == 7)),
             reads=SQ.all() + C["ones_m"].all(), writes=PST.all())
    yield
    s.op("act", lambda e: e.activation(out=RSTD[:, 0:ntok], in_=PST[:, 0:ntok], func=AF.Sqrt, bias=C["eps"][:, 0:1]),
         reads=PST.all() + C["eps"].all(), writes=RSTD.all())
    yield
    s.op("dve", lambda e: e.reciprocal(out=RSTD[:, 0:ntok], in_=RSTD[:, 0:ntok]),
         reads=RSTD.all(), writes=RSTD.all())


def ffn_stage(k, C, XT, PS, ffns, next_gi=None):
    s = k.s
    G_ = C["gains"]
    with k.phase() as ph:
        HTG = C["HTG"]
        HTs = []
        for i in range(2):
            hv = T(f"HTv{i}", HTG.t[:, :, i * 1024:(i + 1) * 1024])
            hv.bufs = HTG.bufs[2 * i:2 * i + 2]
            HTs.append(hv)
        ACTT = ph.sb("ACTT", [128, 11, 1024], BF16, parts=22)
        YT = ph.sb("YT", [128, 8, 1024], F32, parts=16)
        SQ = [ph.sb(f"SQ{i}", [128, 8, 512], BF16) for i in range(2)]
        RSTD = [ph.sb(f"RSTD{i}", [128, 512], F32) for i in range(2)]
        WIN = [ph.sb(f"WIN{i}", [128, 8, 256], BF16) for i in range(3)]
        WOUT = [ph.sb(f"WOUT{i}", [128, 11, 128], BF16) for i in range(3)]
        SG = [ph.sb(f"SG{i}", [128, 512], F32) for i in range(2)]
        PG = [PS[0], PS[1]]
        PU = [PS[2], PS[3]]
        PY = [PS[4], PS[5]]
        PST = [PS[6], PS[7]]
        st = {"win": 0, "wout": 0, "pi": 0, "ni": 0}
        jobs = [(f, B) for f in range(len(ffns)) for B in range(2)]

        bg = Bg()

        def prenorm(ji):
            f, B = jobs[ji]
            HT = HTs[ji % 2]
            gi_pre = ffns[f][2]
            for sb_ in range(2):
                tok = B * 1024 + sb_ * 512
                xb = XT.b(B * 2 + sb_)
                n_ = st["ni"] % 2
                st["ni"] += 1
                yield from rms_rstd_gen(k, C, XT[:, :, tok:tok + 512], xb, SQ[n_], PST[n_], RSTD[n_], 512)
                for kc in range(8):
                    if kc == 4:
                        yield
                    s.op("dve", lambda e, kc=kc, tok=tok, sb_=sb_, n_=n_: e.scalar_tensor_tensor(
                        out=HT[:, kc, sb_ * 512:(sb_ + 1) * 512], in0=XT[:, kc, tok:tok + 512],
                        scalar=G_[:, gi_pre, kc:kc + 1], in1=RSTD[n_][:, :], op0=ALU.mult, op1=ALU.mult),
                        reads=xb + RSTD[n_].all() + G_.all(), writes=HT.b(sb_))

        def postnorm(ji):
            f, B = jobs[ji]
            gi_post = ffns[f][3]
            for sb_ in range(2):
                tok = B * 1024 + sb_ * 512
                rhs_sl = slice(sb_ * 512, (sb_ + 1) * 512)
                ybs = YT.b(*[dc * 2 + sb_ for dc in range(8)])
                xb = XT.b(B * 2 + sb_)
                n_ = st["ni"] % 2
                st["ni"] += 1
                yield from rms_rstd_gen(k, C, YT[:, :, rhs_sl], ybs, SQ[n_], PST[n_], RSTD[n_], 512)
                for dc in range(8):
                    if dc % 2 == 0 and dc > 0:
                        yield
                    s.op("dve", lambda e, dc=dc, rhs_sl=rhs_sl, n_=n_: e.scalar_tensor_tensor(
                        out=YT[:, dc, rhs_sl], in0=YT[:, dc, rhs_sl], scalar=G_[:, gi_post, dc:dc + 1],
                        in1=RSTD[n_][:, :], op0=ALU.mult, op1=ALU.mult),
                        reads=YT.b(dc * 2 + sb_) + RSTD[n_].all() + G_.all(), writes=YT.b(dc * 2 + sb_))
                    s.op("dve", lambda e, dc=dc, rhs_sl=rhs_sl, tok=tok: e.tensor_tensor(
                        out=XT[:, dc, tok:tok + 512], in0=XT[:, dc, tok:tok + 512], in1=YT[:, dc, rhs_sl], op=ALU.add),
                        reads=YT.b(dc * 2 + sb_) + xb, writes=xb)

        def up(ji, G, after_first=None):
            f, B = jobs[ji]
            HT = HTs[ji % 2]
            w_in_d = ffns[f][0]
            for jj in range(11):
                j = G * 11 + jj
                W = WIN[st["win"] % 3]
                st["win"] += 1
                s.dma("pool", W[:, :, :], w_in_d[j].rearrange("p (kc c) -> p kc c", kc=8), writes=W.all())
                for sb_ in range(2):
                    pg, pu, sg = PG[st["pi"] % 2], PU[st["pi"] % 2], SG[st["pi"] % 2]
                    st["pi"] += 1
                    rhs_sl = slice(sb_ * 512, (sb_ + 1) * 512)
                    for kc in range(8):
                        s.op("pe", lambda e, kc=kc, pg=pg, W=W, rhs_sl=rhs_sl: e.matmul(
                            pg[:, :], lhsT=W[:, kc, 0:128], rhs=HT[:, kc, rhs_sl], start=(kc == 0), stop=(kc == 7)),
                            reads=W.all() + HT.b(sb_), writes=pg.all())
                    for kc in range(8):
                        s.op("pe", lambda e, kc=kc, pu=pu, W=W, rhs_sl=rhs_sl: e.matmul(
                            pu[:, :], lhsT=W[:, kc, 128:256], rhs=HT[:, kc, rhs_sl], start=(kc == 0), stop=(kc == 7)),
                            reads=W.all() + HT.b(sb_), writes=pu.all())
                    s.op("act", lambda e, pg=pg, sg=sg: e.activation(out=sg[:, :], in_=pg[:, :], func=AF.Silu),
                         reads=pg.all(), writes=sg.all())
                    s.op("dve", lambda e, pu=pu, sg=sg, jj=jj, rhs_sl=rhs_sl: e.tensor_tensor(
                        out=ACTT[:, jj, rhs_sl], in0=sg[:, :], in1=pu[:, :], op=ALU.mult),
                        reads=sg.all() + pu.all(), writes=ACTT.b(jj * 2 + sb_))
                    bg.tick()
                if jj == 0 and after_first is not None:
                    after_first()

        def down(ji, G):
            f, B = jobs[ji]
            w_out_d = ffns[f][1]
            for dc in range(8):
                W = WOUT[st["wout"] % 3]
                st["wout"] += 1
                s.dma("pool", W[:, :, :], w_out_d[G, dc].rearrange("p (jj c) -> p jj c", jj=11), writes=W.all())
                for sb_ in range(2):
                    py = PY[st["pi"] % 2]
                    st["pi"] += 1
                    rhs_sl = slice(sb_ * 512, (sb_ + 1) * 512)
                    for jj in range(11):
                        s.op("pe", lambda e, jj=jj, py=py, W=W, rhs_sl=rhs_sl: e.matmul(
                            py[:, :], lhsT=W[:, jj, :], rhs=ACTT[:, jj, rhs_sl], start=(jj == 0), stop=(jj == 10)),
                            reads=W.all() + ACTT.b(jj * 2 + sb_), writes=py.all())
                    yb = YT.b(dc * 2 + sb_)
                    if G == 0:
                        s.op("act", lambda e, py=py, dc=dc, rhs_sl=rhs_sl: e.activation(
                            out=YT[:, dc, rhs_sl], in_=py[:, :], func=AF.Copy),
                            reads=py.all(), writes=yb)
                    else:
                        s.op("dve", lambda e, py=py, dc=dc, rhs_sl=rhs_sl: e.tensor_tensor(
                            out=YT[:, dc, rhs_sl], in0=YT[:, dc, rhs_sl], in1=py[:, :], op=ALU.add),
                            reads=py.all() + yb, writes=yb)
                    bg.tick()

        def next_prenorm():
            HT = HTs[0]
            for sb_ in range(2):
                tok = sb_ * 512
                xb = XT.b(sb_)
                n_ = st["ni"] % 2
                st["ni"] += 1
                yield from rms_rstd_gen(k, C, XT[:, :, tok:tok + 512], xb, SQ[n_], PST[n_], RSTD[n_], 512)
                for kc in range(8):
                    if kc == 4:
                        yield
                    s.op("dve", lambda e, kc=kc, tok=tok, sb_=sb_, n_=n_: e.scalar_tensor_tensor(
                        out=HT[:, kc, sb_ * 512:(sb_ + 1) * 512], in0=XT[:, kc, tok:tok + 512],
                        scalar=G_[:, next_gi, kc:kc + 1], in1=RSTD[n_][:, :], op0=ALU.mult, op1=ALU.mult),
                        reads=xb + RSTD[n_].all() + G_.all(), writes=HT.b(sb_))

        n = len(jobs)
        assert n % 2 == 0
        if C["ht_ready"] is not None and C["ht_ready"] == (ffns[0][2], (0, 1)):
            pass
        else:
            bg.add(prenorm(0))
            bg.drain()
        C["ht_ready"] = None
        for ji in range(n):
            up(ji, 0, after_first=(lambda ji=ji: bg.add(postnorm(ji - 1), 1)) if ji > 0 else None)
            bg.drain()
            down(ji, 0)
            if ji + 1 < n:
                bg.add(prenorm(ji + 1), 1)
            elif next_gi is not None:
                bg.add(next_prenorm(), 1)
                C["ht_ready"] = (next_gi, (0, 1))
            up(ji, 1)
            bg.drain()
            down(ji, 1)
        bg.add(postnorm(n - 1))
        bg.drain()


def prenorm_to_HT(k, C, ph, XT, HT, PS, gi_pre, col_off=0):
    s = k.s
    G_ = C["gains"]
    with k.phase() as p2:
        SQ = [p2.sb(f"SQ{i}", [128, 8, 512], BF16) for i in range(2)]
        RSTD = [p2.sb(f"RSTD{i}", [128, 512], F32) for i in range(2)]
        bg = Bg()

        def chain(tb):
            tok = tb * 512
            xb = XT.b(tb)
            yield from rms_rstd_gen(k, C, XT[:, :, tok:tok + 512], xb, SQ[tb % 2], PS[6 + tb % 2], RSTD[tb % 2], 512)
            for kc in range(8):
                if kc == 4:
                    yield
                s.op("dve", lambda e, kc=kc: e.scalar_tensor_tensor(
                    out=HT[:, kc, col_off + tok:col_off + tok + 512], in0=XT[:, kc, tok:tok + 512],
                    scalar=G_[:, gi_pre, kc:kc + 1], in1=RSTD[tb % 2][:, :], op0=ALU.mult, op1=ALU.mult),
                    reads=xb + RSTD[tb % 2].all() + G_.all(), writes=HT.b(tb))

        skip = ()
        if C["ht_ready"] is not None and C["ht_ready"][0] == gi_pre and col_off == 0:
            skip = C["ht_ready"][1]
        C["ht_ready"] = None
        for tb in range(4):
            if tb in skip:
                continue
            bg.add(chain(tb), 1)
            bg.tick()
            bg.tick()
        bg.drain()


def outproj_postnorm(k, C, XT, PS, OT, wo_d, gi_post, next_gi=None, WO=None):
    s = k.s
    G_ = C["gains"]
    with k.phase() as p3:
        if WO is None:
            WO = T("WOv", C["HTG"].t[:, :, 1024:2048])
            WO.bufs = C["HTG"].bufs[2:4]
            for kc in range(8):
                s.dma("pool", WO[:, kc, :], wo_d[kc * 128:(kc + 1) * 128, :], writes=WO.all())
        YTs = [p3.sb(f"YT{i}", [128, 8, 512], F32, parts=8) for i in range(2)]
        SQ1 = p3.sb("SQ", [128, 8, 512], BF16)
        RSTD = [p3.sb(f"RSTD{i}", [128, 512], F32) for i in range(2)]
        bg = Bg()

        def chain(tb):
            tok = tb * 512
            YT = YTs[tb % 2]
            yield from rms_rstd_gen(k, C, YT[:, :, :], YT.all(), SQ1, PS[6 + tb % 2], RSTD[tb % 2], 512, fuse_sq=True)
            xb = XT.b(tb)
            for dc in range(8):
                if dc % 2 == 0 and dc > 0:
                    yield
                s.op("dve", lambda e, dc=dc: e.scalar_tensor_tensor(
                    out=YT[:, dc, :], in0=YT[:, dc, :], scalar=G_[:, gi_post, dc:dc + 1],
                    in1=RSTD[tb % 2][:, :], op0=ALU.mult, op1=ALU.mult),
                    reads=YT.b(dc) + RSTD[tb % 2].all() + G_.all(), writes=YT.b(dc))
                s.op("dve", lambda e, dc=dc: e.tensor_tensor(
                    out=XT[:, dc, tok:tok + 512], in0=XT[:, dc, tok:tok + 512], in1=YT[:, dc, :], op=ALU.add),
                    reads=YT.b(dc) + xb, writes=xb)

        if next_gi is not None:
            RSTDn = p3.sb("RSTDn", [128, 512], F32)
        HTG = C["HTG"]

        def next_prenorm():
            for sb_ in range(2):
                tok = sb_ * 512
                xb = XT.b(sb_)
                yield from rms_rstd_gen(k, C, XT[:, :, tok:tok + 512], xb, SQ1, PS[0], RSTDn, 512, fuse_sq=True)
                for kc in range(8):
                    if kc == 4:
                        yield
                    s.op("dve", lambda e, kc=kc, tok=tok: e.scalar_tensor_tensor(
                        out=HTG[:, kc, tok:tok + 512], in0=XT[:, kc, tok:tok + 512],
                        scalar=G_[:, next_gi, kc:kc + 1], in1=RSTDn[:, :], op0=ALU.mult, op1=ALU.mult),
                        reads=xb + RSTDn.all() + G_.all(), writes=HTG.b(sb_))

        pi = 0
        for tb in range(4):
            tok = tb * 512
            YT = YTs[tb % 2]
            if tb == 3 and next_gi is not None:
                bg.add(next_prenorm(), 1)
                C["ht_ready"] = (next_gi, (0, 1))
            for dc in range(8):
                pp = PS[4 + pi % 2]
                pi += 1
                for kc in range(8):
                    s.op("pe", lambda e, kc=kc, dc=dc, pp=pp, tok=tok: e.matmul(
                        pp[:, :], lhsT=WO[:, kc, dc * 128:(dc + 1) * 128], rhs=OT[:, kc, tok:tok + 512],
                        start=(kc == 0), stop=(kc == 7)),
                        reads=WO.all() + OT.all(), writes=pp.all())
                s.op("act", lambda e, dc=dc, pp=pp, YT=YT: e.activation(out=YT[:, dc, :], in_=pp[:, :], func=AF.Copy),
                     reads=pp.all(), writes=YT.b(dc))
                bg.tick()
            bg.drain()
            bg.add(chain(tb), 1)
        bg.drain()


def attn_stage(k, C, XT, PS, wqkv_d, wo_d, gi_pre, gi_post, next_gi=None):
    s = k.s
    with k.phase() as ph:
        OT = ph.sb("OT", [128, 8, SEQ], BF16, parts=1)
        with k.phase() as pab:
            HT = C["HTG"]
            prenorm_to_HT(k, C, pab, XT, HT, PS, gi_pre)
            with k.phase() as pb:
                NEGM = pb.sb("negm", [128, 4, 512], BF16)
                ZB = pb.sb("zb", [128, 512], BF16)
                s.op("dve", lambda e: e.memset(ZB[:, :], 0.0), writes=ZB.all())
                for d in range(4):
                    s.op("pool", lambda e, d=d: e.affine_select(
                        out=NEGM[:, d, :], in_=ZB[:, :], pattern=[[1, 512]], compare_op=ALU.is_gt,
                        fill=-30000.0, base=-128 * d, channel_multiplier=-1),
                        reads=ZB.all(), writes=NEGM.all())
                QT = [pb.sb(f"QT{i}", [128, SEQ], BF16) for i in range(2)]
                KT = [pb.sb(f"KT{i}", [128, SEQ], BF16) for i in range(2)]
                V = [pb.sb(f"V{i}", [128, 16, 128], BF16) for i in range(2)]
                W = [pb.sb(f"WQKV{i}", [128, 8, 384], BF16) for i in range(1)]
                OTOK = [pb.sb(f"OTOK{i}", [128, 16, 128], BF16) for i in range(2)]
                E = [pb.sb(f"E{i}", [128, 512], F32) for i in range(3)]
                SP = [pb.sb(f"SP{i}", [128, 512], BF16) for i in range(5)]
                ATT = [pb.sb(f"ATT{i}", [128, 512], BF16) for i in range(3)]
                OACC = [pb.sb(f"OACC{i}", [128, 4, 64], F32) for i in range(2)]
                CACC = [pb.sb(f"CACC{i}", [128, 4], F32) for i in range(2)]
                FS = [pb.sb(f"FS{i}", [128, 4], F32) for i in range(4)]
                PZ = [PS[0], PS[1], PS[2], PS[3], PS[4]]
                PO = [PS[5], PS[6]]
                PP = [PS[7]]
                st = {"pi": 0}

                bg = Bg()

                def pre_hp(hp):
                    w = W[0]
                    qt, kt_, v = QT[hp % 2], KT[hp % 2], V[hp % 2]
                    s.dma("pool", w[:, :, :], wqkv_d[hp].rearrange("p (kc c) -> p kc c", kc=8), writes=w.all())
                    for which in range(2):
                        for tb in range(4):
                            pp = PP[st["pi"] % len(PP)]
                            st["pi"] += 1
                            for kc in range(8):
                                s.op("pe", lambda e, kc=kc, pp=pp, w=w, which=which, tb=tb: e.matmul(
                                    pp[:, :], lhsT=w[:, kc, which * 128:(which + 1) * 128],
                                    rhs=HT[:, kc, tb * 512:(tb + 1) * 512], start=(kc == 0), stop=(kc == 7)),
                                    reads=w.all() + HT.b(tb), writes=pp.all())
                            if which == 0:
                                s.op("dve", lambda e, pp=pp, qt=qt, tb=tb: e.tensor_scalar(
                                    out=qt[:, tb * 512:(tb + 1) * 512], in0=pp[:, :], scalar1=0.125, scalar2=None,
                                    op0=ALU.mult),
                                    reads=pp.all(), writes=qt.all())
                            else:
                                s.op("dve", lambda e, pp=pp, kt_=kt_, tb=tb: e.tensor_copy(
                                    out=kt_[:, tb * 512:(tb + 1) * 512], in_=pp[:, :]),
                                    reads=pp.all(), writes=kt_.all())
                            yield
                    for tg in range(4):
                        pp = PP[st["pi"] % len(PP)]
                        st["pi"] += 1
                        for tt in range(4):
                            tok = (tg * 4 + tt) * 128
                            for kc in range(8):
                                s.op("pe", lambda e, kc=kc, pp=pp, w=w, tt=tt, tok=tok: e.matmul(
                                    pp[:, tt * 128:(tt + 1) * 128], lhsT=HT[:, kc, tok:tok + 128],
                                    rhs=w[:, kc, 256:384], start=(kc == 0), stop=(kc == 7)),
                                    reads=w.all() + HT.b(tg), writes=pp.all())
                        s.op("dve", lambda e, pp=pp, v=v, tg=tg: e.tensor_copy(
                            out=v[:, tg * 4:(tg + 1) * 4, :], in_=pp[:, :].rearrange("p (a b) -> p a b", a=4)),
                            reads=pp.all(), writes=v.all())
                        yield

                def post_hp(hp):
                    otok = OTOK[hp % 2]
                    for tg in range(4):
                        pp = PP[st["pi"] % len(PP)]
                        st["pi"] += 1
                        for tt in range(4):
                            s.op("pe", lambda e, pp=pp, tt=tt, tg=tg, otok=otok: e.matmul(
                                pp[:, tt * 128:(tt + 1) * 128], lhsT=otok[:, tg * 4 + tt, :], rhs=C["ident"][:, :],
                                start=True, stop=True),
                                reads=otok.all() + C["ident"].all(), writes=pp.all())
                        s.op("dve", lambda e, pp=pp, tg=tg, hp=hp: e.tensor_copy(
                            out=OT[:, hp, tg * 512:(tg + 1) * 512], in_=pp[:, :]),
                            reads=pp.all(), writes=OT.all())

                units = []
                for hp in range(8):
                    for g in range(4):
                        for kt in range(4 * g + 3, -1, -1):
                            for hh in range(2):
                                units.append((hp, hh, g, kt))
                n = len(units)
                NPZ, NSP, NATT, NPO, NE = 5, 5, 3, 2, 3

                def u_(i):
                    hp, hh, g, kt = units[i]
                    d = kt - 4 * g
                    return hp, hh, g, kt, d, slice(hh * 64, (hh + 1) * 64)

                def c0_(i):
                    hp, hh, g, kt = units[i]
                    return max(kt - 4 * g, 0) * 128

                def s0_qk(i):
                    hp, hh, g, kt, d, hs = u_(i)
                    if hh == 0 and g == 0 and kt == 3:
                        if hp == 0:
                            bg.add(pre_hp(0))
                        bg.drain()
                    if hh == 0 and g == 0 and kt == 0 and hp + 1 < 8:
                        bg.add(pre_hp(hp + 1), 5)
                    pz, qt, kt_ = PZ[i % NPZ], QT[hp % 2], KT[hp % 2]
                    q0 = g * 512
                    c0 = c0_(i)
                    s.op("pe", lambda e: e.matmul(pz[:, c0:512], lhsT=kt_[hs, kt * 128:(kt + 1) * 128],
                                                  rhs=qt[hs, q0 + c0:q0 + 512], start=True, stop=(d < 0)),
                         reads=kt_.all() + qt.all(), writes=pz.all())
                    if d >= 0:
                        s.op("pe", lambda e: e.matmul(pz[:, c0:c0 + 128], lhsT=C["ident"][:, :], rhs=NEGM[:, d, c0:c0 + 128],
                                                      start=False, stop=True),
                             reads=C["ident"].all() + NEGM.all(), writes=pz.all())

                def s1_exp(i):
                    pz, e_ = PZ[i % NPZ], E[i % NE]
                    c0 = c0_(i)
                    s.op("act", lambda e: e.activation(out=e_[:, c0:512], in_=pz[:, c0:512], func=AF.Exp),
                         reads=pz.all(), writes=e_.all())

                def s2_ln(i):
                    e_, sp = E[i % NE], SP[i % NSP]
                    c0 = c0_(i)
                    s.op("act", lambda e: e.activation(out=sp[:, c0:512], in_=e_[:, c0:512], func=AF.Ln,
                                                       bias=C["one_f"][:, 0:1]),
                         reads=e_.all() + C["one_f"].all(), writes=sp.all())

                def s3_tri(i):
                    pz, sp = PZ[i % NPZ], SP[i % NSP]
                    c0 = c0_(i)
                    s.op("pe", lambda e: e.matmul(pz[:, c0:512], lhsT=C["ntri"][:, :], rhs=sp[:, c0:512], start=False,
                                                  stop=True, skip_group_check=True),
                         reads=sp.all() + C["ntri"].all(), writes=pz.all())

                def s4_att(i):
                    pz, att = PZ[i % NPZ], ATT[i % NATT]
                    c0 = c0_(i)
                    s.op("act", lambda e: e.activation(out=att[:, c0:512], in_=pz[:, c0:512], func=AF.Exp),
                         reads=pz.all(), writes=att.all())

                def s5_av(i):
                    hp, hh, g, kt, d, hs = u_(i)
                    qlo = max(d, 0)
                    sp, att, po, v = SP[i % NSP], ATT[i % NATT], PO[i % NPO], V[hp % 2]
                    for qi in range(qlo, 4):
                        s.op("pe", lambda e, qi=qi: e.matmul(
                            po[:, qi * 64:(qi + 1) * 64], lhsT=att[:, qi * 128:(qi + 1) * 128],
                            rhs=v[:, kt, hs], start=True, stop=True),
                            reads=att.all() + v.all(), writes=po.all())
                        s.op("pe", lambda e, qi=qi: e.matmul(
                            po[:, 256 + qi:257 + qi], lhsT=sp[:, qi * 128:(qi + 1) * 128],
                            rhs=C["ones_col"][:, 0:1], start=True, stop=True),
                            reads=sp.all() + C["ones_col"].all(), writes=po.all())

                def s6_acc(i):
                    hp, hh, g, kt, d, hs = u_(i)
                    qlo = max(d, 0)
                    span = hh
                    po = PO[i % NPO]
                    oacc, cacc, fs = OACC[span % 2], CACC[span % 2], FS[i % 4]
                    otok = OTOK[hp % 2]
                    if kt != 4 * g + 3:
                        s.op("act", lambda e: e.activation(out=fs[:, :], in_=cacc[:, :], func=AF.Exp, scale=-1.0),
                             reads=cacc.all(), writes=fs.all())
                        s.op("dve", lambda e: e.tensor_tensor(
                            out=cacc[:, qlo:4], in0=cacc[:, qlo:4], in1=po[:, 256 + qlo:260], op=ALU.add),
                            reads=po.all() + cacc.all(), writes=cacc.all())
                        for qi in range(qlo, 4):
                            s.op("dve", lambda e, qi=qi: e.scalar_tensor_tensor(
                                out=oacc[:, qi, :], in0=po[:, qi * 64:(qi + 1) * 64], scalar=fs[:, qi:qi + 1],
                                in1=oacc[:, qi, :], op0=ALU.mult, op1=ALU.add),
                                reads=po.all() + fs.all() + oacc.all(), writes=oacc.all())
                    else:
                        if qlo > 0:
                            s.op("dve", lambda e: e.memset(oacc[:, 0:qlo, :], 0.0), writes=oacc.all())
                            s.op("dve", lambda e: e.memset(cacc[:, 0:qlo], 0.0), writes=cacc.all())
                        s.op("dve", lambda e: e.tensor_copy(
                            out=oacc[:, qlo:4, :], in_=po[:, qlo * 64:256].rearrange("p (a b) -> p a b", b=64)),
                            reads=po.all(), writes=oacc.all())
                        s.op("dve", lambda e: e.tensor_copy(out=cacc[:, qlo:4], in_=po[:, 256 + qlo:260]),
                             reads=po.all(), writes=cacc.all())
                    if kt == 0:
                        s.op("dve", lambda e: e.tensor_copy(out=otok[:, 4 * g:4 * g + 4, hs], in_=oacc[:, :, :]),
                             reads=oacc.all(), writes=otok.all())
                        if hh == 1 and g == 3:
                            post_hp(hp)

                stages = ((0, s0_qk), (1, s1_exp), (2, s2_ln), (3, s3_tri), (4, s4_att), (5, s5_av), (6, s6_acc))
                for i in range(n + 6):
                    for lag, fn in stages:
                        if 0 <= i - lag < n:
                            fn(i - lag)
                    bg.tick()
        if "dbg_OT" in C:
            s.dma("sp", C["dbg_OT"][:, :, :], OT[:, :, :], reads=OT.all(), writes=C["dbg_OT"].all())
        outproj_postnorm(k, C, XT, PS, OT, wo_d, gi_post, next_gi)


AX = mybir.AxisListType


class Ref:
    __slots__ = ("ap", "bufs")

    def __init__(self, ap, bufs):
        self.ap = ap
        self.bufs = bufs


class _RefMaker:
    def __init__(self, t):
        self.t = t

    def __getitem__(self, key):
        return Ref(self.t.t[key], self.t.all())


def rf(t):
    return _RefMaker(t)


def _b(*refs):
    out = []
    for r in refs:
        if isinstance(r, Ref):
            out += r.bufs
    return out


def _a(x):
    return x.ap if isinstance(x, Ref) else x


def e_tt(s, eng, out, a, b, op):
    return s.op(eng, lambda E: E.tensor_tensor(out=out.ap, in0=a.ap, in1=b.ap, op=op), reads=_b(a, b), writes=out.bufs)


def e_ts(s, eng, out, a, s1, s2, op0, op1=None):
    if op1 is None:
        return s.op(eng, lambda E: E.tensor_scalar(out=out.ap, in0=a.ap, scalar1=_a(s1), scalar2=None, op0=op0),
                    reads=_b(a, s1), writes=out.bufs)
    return s.op(eng, lambda E: E.tensor_scalar(out=out.ap, in0=a.ap, scalar1=_a(s1), scalar2=_a(s2), op0=op0, op1=op1),
                reads=_b(a, s1, s2), writes=out.bufs)


def e_stt(s, eng, out, a, sc, b, op0, op1):
    return s.op(eng, lambda E: E.scalar_tensor_tensor(out=out.ap, in0=a.ap, scalar=_a(sc), in1=b.ap, op0=op0, op1=op1),
                reads=_b(a, sc, b), writes=out.bufs)


def e_act(s, out, a, func, bias=None, scale=None):
    kw = {}
    if bias is not None:
        kw["bias"] = _a(bias)
    if scale is not None:
        kw["scale"] = _a(scale)
    return s.op("act", lambda E: E.activation(out=out.ap, in_=a.ap, func=func, **kw), reads=_b(a, bias, scale),
                writes=out.bufs)


def e_mm(s, out, lhsT, rhs, start=True, stop=True):
    return s.op("pe", lambda E: E.matmul(out.ap, lhsT=lhsT.ap, rhs=rhs.ap, start=start, stop=stop),
                reads=_b(lhsT, rhs), writes=out.bufs)


def e_copy(s, eng, out, a):
    if eng == "act":
        return e_act(s, out, a, AF.Copy)
    return s.op(eng, lambda E: E.tensor_copy(out=out.ap, in_=a.ap), reads=_b(a), writes=out.bufs)


def e_memset(s, eng, out, val):
    return s.op(eng, lambda E: E.memset(out.ap, val), writes=out.bufs)


GN_EPS = 64e-5
STAGGER = 3
NEG_EXP_HALF = -0.6065306597126334


def mixer0_stage(k, C, XT, PS, D, gi_pre, gi_post, next_gi=None):
    s = k.s
    with k.phase() as ph:
        OT = ph.sb("OT", [128, 8, SEQ], BF16, parts=1)
        with k.phase() as pab:
            HT = C["HTG"]
            prenorm_to_HT(k, C, pab, XT, HT, PS, gi_pre)
            with k.phase() as pl:
                rglru_part(k, C, pl, HT, OT, PS, D)
            with k.phase() as pr:
                rwkv_part(k, C, pr, HT, OT, PS, D, XT)
        if "dbg_OT" in D:
            s.dma("sp", D["dbg_OT"][:, :, :], OT[:, :, :], reads=OT.all(), writes=D["dbg_OT"].all())
        outproj_postnorm(k, C, XT, PS, OT, D["l0_w_out"], gi_post, next_gi)


def rglru_part(k, C, p, HT, OT, PS, D):
    s = k.s
    PL = p.sb("PL", [128, 4, 8], F32)
    s.dma("sp", PL[:, :, :], D["l0_pl"][:, :].rearrange("p (c n) -> p c n", c=4), writes=PL.all())
    C1 = p.sb("C1", [128, 4], F32)
    e_act(s, rf(C1)[:, :], rf(PL)[:, :, 7], AF.Exp, scale=-1.0)
    e_act(s, rf(C1)[:, :], rf(C1)[:, :], AF.Ln, bias=rf(C["one_f"])[:, 0:1])
    e_ts(s, "dve", rf(C1)[:, :], rf(C1)[:, :], -8.0, None, ALU.mult)
    GAW = p.sb("GAW", [128, 4, 128], BF16)
    GXW = p.sb("GXW", [128, 4, 128], BF16)
    e_memset(s, "dve", rf(GAW)[:, :, :], 0.0)
    e_memset(s, "dve", rf(GXW)[:, :, :], 0.0)
    for n in range(8):
        ps_ = slice((n % 2) * 64, (n % 2) * 64 + 64)
        s.dma("pool", GAW[ps_, n // 2, ps_], D["l0_gate_a_w"][n], writes=GAW.all())
        s.dma("pool", GXW[ps_, n // 2, ps_], D["l0_gate_x_w"][n], writes=GXW.all())
    W = [p.sb(f"WL{i}", [128, 8, 256], BF16) for i in range(2)]
    XBs = [p.sb(f"XB{i}", [128, 515], F32) for i in range(2)]
    HHs = [[p.sb(f"HH{j}_{i}", [128, 512], F32) for i in range(2)] for j in range(2)]
    ts_ = [{n: p.sb(f"{n}{j}", [128, 512], F32) for n in ("GB", "XC", "R", "IG", "A", "U", "T1", "T2")} for j in range(2)]
    XCbs = [p.sb(f"XCb{j}", [128, 512], BF16) for j in range(2)]

    def unit(c, tb, j):
        w = W[j]
        XB, t_, XCb = XBs[j], ts_[j], XCbs[j]
        col = lambda n: rf(PL)[:, c, n:n + 1]
        tok = tb * 512
        px, pg = PS[2 * j], PS[2 * j + 1]
        for kc in range(8):
            e_mm(s, rf(px)[:, :], rf(w)[:, kc, 0:128], Ref(HT[:, kc, tok:tok + 512], HT.b(tb)), kc == 0, kc == 7)
        for kc in range(8):
            e_mm(s, rf(pg)[:, :], rf(w)[:, kc, 128:256], Ref(HT[:, kc, tok:tok + 512], HT.b(tb)), kc == 0, kc == 7)
        if tb == 0:
            e_memset(s, "dve", rf(XB)[:, 0:3], 0.0)
        else:
            e_copy(s, "dve", rf(XB)[:, 0:3], rf(XB)[:, 512:515])
        yield
        e_copy(s, "act", rf(XB)[:, 3:515], rf(px)[:, :])
        e_copy(s, "act", rf(t_["GB"])[:, :], rf(pg)[:, :])
        yield
        XC = t_["XC"]
        e_ts(s, "dve", rf(XC)[:, :], rf(XB)[:, 3:515], col(3), col(4), ALU.mult, ALU.add)
        for i in range(3):
            e_stt(s, "dve", rf(XC)[:, :], rf(XB)[:, i:i + 512], col(i), rf(XC)[:, :], ALU.mult, ALU.add)
        GB, T2 = t_["GB"], t_["T2"]
        e_act(s, rf(T2)[:, :], rf(GB)[:, :], AF.Gelu_apprx_tanh)
        yield
        e_copy(s, "act", rf(XCb)[:, :], rf(XC)[:, :])
        yield
        pr_, pig = PS[4 + 2 * j], PS[5 + 2 * j]
        e_mm(s, rf(pr_)[:, :], rf(GAW)[:, c, :], rf(XCb)[:, :])
        e_mm(s, rf(pig)[:, :], rf(GXW)[:, c, :], rf(XCb)[:, :])
        yield
        e_act(s, rf(t_["R"])[:, :], rf(pr_)[:, :], AF.Sigmoid, bias=col(5))
        e_act(s, rf(t_["IG"])[:, :], rf(pig)[:, :], AF.Sigmoid, bias=col(6))
        yield
        A = t_["A"]
        e_act(s, rf(A)[:, :], rf(t_["R"])[:, :], AF.Exp, scale=rf(C1)[:, c:c + 1])
        T1, U = t_["T1"], t_["U"]
        e_tt(s, "dve", rf(U)[:, :], rf(t_["IG"])[:, :], rf(XC)[:, :], ALU.mult)
        yield
        e_tt(s, "dve", rf(T1)[:, :], rf(A)[:, :], rf(A)[:, :], ALU.mult)
        yield
        e_ts(s, "dve", rf(T1)[:, :], rf(T1)[:, :], -1.0, 1.0, ALU.mult, ALU.add)
        yield
        e_act(s, rf(T1)[:, :], rf(T1)[:, :], AF.Sqrt)
        yield
        e_tt(s, "dve", rf(U)[:, :], rf(U)[:, :], rf(T1)[:, :], ALU.mult)
        yield
        H = HHs[j][tb % 2]
        Hp = HHs[j][(tb + 1) % 2]
        init = 0.0 if tb == 0 else Hp[:, 511:512]
        s.op("dve", lambda E: E.tensor_tensor_scan(
            out=H[:, :], data0=A[:, :], data1=U[:, :], initial=init, op0=ALU.mult, op1=ALU.add),
            reads=A.all() + U.all() + (Hp.all() if tb else []), writes=H.all())
        yield
        s.op("dve", lambda E: E.tensor_tensor(
            out=OT[:, 4 + c, tok:tok + 512], in0=H[:, :], in1=T2[:, :], op=ALU.mult),
            reads=H.all() + T2.all(), writes=OT.all())

    for cp in range(2):
        for j in range(2):
            c = 2 * cp + j
            s.dma("pool", W[j][:, :, :], D["l0_w_lru"][c].rearrange("p (kc n) -> p kc n", kc=8), writes=W[j].all())
        for tb in range(4):
            gens = [unit(2 * cp + j, tb, j) for j in range(2)]
            alive = [True, True]
            while any(alive):
                for j in range(2):
                    if alive[j]:
                        try:
                            next(gens[j])
                        except StopIteration:
                            alive[j] = False


def rwkv_part(k, C, p, HT, OT, PS, D, XT):
    s = k.s
    spill = D["xt_spill"]
    for kc in range(8):
        s.dma("sp", spill[:, kc, :], XT[:, kc, :], reads=XT.all(), writes=spill.all())
    PH = p.sb("PH", [128, 4, 8], F32)
    s.dma("sp", PH[:, :, :], D["l0_ph"][:, :].rearrange("p (h n) -> p h n", h=4), writes=PH.all())
    OM = p.sb("OM", [128, 4, 4], F32)
    e_ts(s, "dve", rf(OM)[:, :, 0:3], rf(PH)[:, :, 0:3], -1.0, 1.0, ALU.mult, ALU.add)
    e_ts(s, "dve", rf(OM)[:, :, 3:4], rf(PH)[:, :, 6:7], -1.0, 1.0, ALU.mult, ALU.add)
    RKb = p.sb("RKb", [128, 4], BF16)
    e_copy(s, "dve", rf(RKb)[:, :], rf(PH)[:, :, 7])
    MUL = p.sb("MUL", [128, 3], F32)
    s.dma("sp", MUL[:, :], D["l0_mul"][:, :], writes=MUL.all())
    OML = p.sb("OML", [128, 3], F32)
    e_ts(s, "dve", rf(OML)[:, :], rf(MUL)[:, :], -1.0, 1.0, ALU.mult, ALU.add)
    LNGBs = [p.sb(f"LNGB{i}", [128, 2, 64], F32) for i in range(2)]
    W2 = p.sb("W2", [64, 512], BF16)
    A2 = p.sb("A2", [64, 512], BF16)
    G2 = p.sb("G2", [128, 512], BF16)
    s.dma("pool", W2[:, :], D["l0_w2"][:, :], writes=W2.all())
    s.dma("pool", A2[:, :], D["l0_a2"][:, :], writes=A2.all())
    s.dma("pool", G2[:, :], D["l0_g2"][:, :], writes=G2.all())
    ob = C["ones_bf"]
    BLK = p.sb("BLK", [128, 128], BF16)
    e_memset(s, "dve", rf(BLK)[:, :], 0.0)
    e_memset(s, "dve", rf(BLK)[0:64, 0:64], 1.0)
    e_memset(s, "dve", rf(BLK)[64:128, 64:128], 1.0)
    M512 = p.sb("M512", [128, 512], BF16)
    MUS = p.sb("MUS", [128, 512], BF16)
    MUI = p.sb("MUI", [128, 512], BF16)
    MLS = p.sb("MLS", [128, 512], BF16)
    ID8 = p.sb("ID8", [128, 512], BF16)
    for hh in range(2):
        hs = slice(hh * 64, hh * 64 + 64)
        for dst, pat, cmp_, cm in ((M512, [[0, 8], [1, 64]], ALU.is_gt, 0), (MUS, [[0, 8], [1, 64]], ALU.is_gt, -1),
                                   (MUI, [[0, 8], [1, 64]], ALU.is_ge, -1), (MLS, [[0, 8], [-1, 64]], ALU.is_gt, 1),
                                   (ID8, [[0, 8], [-1, 64]], ALU.is_equal, 1)):
            s.op("pool", lambda E, dst=dst, pat=pat, cmp_=cmp_, cm=cm, hs=hs: E.affine_select(
                out=dst[hs, :], in_=ob[hs, :], pattern=pat, compare_op=cmp_, fill=0.0, base=0, channel_multiplier=cm),
                reads=ob.all(), writes=dst.all())
    ident = C["ident"]

    TW = p.sb("TW", [64, SEQ], BF16)
    AL = p.sb("AL", [64, SEQ], BF16)
    SGL = p.sb("SGL", [128, SEQ], BF16)
    with k.phase() as p0:
        WLo = p0.sb("WLo", [128, 8, 256], BF16)
        s.dma("pool", WLo[:, :, :], D["l0_w_lora"][:, :].rearrange("p (kc n) -> p kc n", kc=8), writes=WLo.all())
        PAl = [p0.sb(f"PAl{i}", [128, 513], F32) for i in range(3)]
        TMPl = [p0.sb(f"TMPl{i}", [128, 512], F32) for i in range(3)]

        def lora_chain(which, c0, c1, npart, dst):
            PA, tmpl = PAl[which], TMPl[which]
            for tb in range(4):
                tok = tb * 512
                pp = PS[which * 2 + tb % 2]
                for kc in range(8):
                    e_mm(s, rf(pp)[0:npart, :], rf(WLo)[:, kc, c0:c1], Ref(HT[:, kc, tok:tok + 512], HT.b(tb)), kc == 0, kc == 7)
                if tb == 0:
                    e_memset(s, "dve", rf(PA)[0:npart, 0:1], 0.0)
                else:
                    e_copy(s, "dve", rf(PA)[0:npart, 0:1], rf(PA)[0:npart, 512:513])
                yield
                e_copy(s, "act", rf(PA)[0:npart, 1:513], rf(pp)[0:npart, :])
                yield
                e_act(s, rf(tmpl)[0:npart, :], rf(PA)[0:npart, 0:512], AF.Copy, scale=rf(MUL)[0:npart, which:which + 1])
                yield
                e_stt(s, "dve", rf(tmpl)[0:npart, :], rf(PA)[0:npart, 1:513], rf(OML)[0:npart, which:which + 1],
                      rf(tmpl)[0:npart, :], ALU.mult, ALU.add)
                yield
                if which == 0:
                    e_act(s, rf(dst)[:, tok:tok + 512], rf(tmpl)[0:64, :], AF.Tanh)
                elif which == 1:
                    e_copy(s, "act", rf(dst)[:, tok:tok + 512], rf(tmpl)[0:64, :])
                else:
                    e_act(s, rf(dst)[:, tok:tok + 512], rf(tmpl)[:, :], AF.Sigmoid)
                yield

        lbg = Bg()
        for which, (c0, c1, npart, dst) in enumerate(((0, 64, 64, TW), (64, 128, 64, AL), (128, 256, 128, SGL))):
            lbg.add(lora_chain(which, c0, c1, npart, dst), 1)
        lbg.drain()

    s.barrier()
    XTf = XT.t
    XTb = XT.t.bitcast(BF16)
    f32n = ("r", "k", "SIG", "A", "KKN", "KH", "CUM", "EC", "EX", "EN", "TMP")
    b16n = ("Rt", "At", "Bt", "Kt", "Bh", "Kh", "RK", "VT", "KK2")
    t64n = ("V64", "BH64", "KH64", "N", "Q", "N2", "Q2", "XA", "LAK", "ARB", "ARK")
    sets = []
    for S in range(2):
        B = {}
        if S == 0:
            B["WH"] = p.sb("WH", [128, 8, 384], BF16)
            B["PA"] = [p.sb(f"PA{i}", [128, 513], F32) for i in range(3)]
            F = {n: p.sb("f_" + n, [128, 512], F32) for n in f32n}
            Bf = {n: p.sb("b_" + n, [128, 512], BF16) for n in b16n}
            T64 = {n: p.sb("t_" + n, [128, 512], BF16) for n in t64n}
        else:
            B["WH"] = T("WH1", XTb[:, 7, 0:3072].rearrange("p (kc n) -> p kc n", kc=8))
            B["PA"] = [T(f"PA1_{i}", XTf[:, 3, i * 513:(i + 1) * 513]) for i in range(3)]
            F = {n: T("f1_" + n, XTf[:, i // 4, (i % 4) * 512:(i % 4) * 512 + 512]) for i, n in enumerate(f32n)}
            bl = list(b16n) + list(t64n)
            vb = {n: T("b1_" + n, XTb[:, 4 + i // 8, (i % 8) * 512:(i % 8) * 512 + 512]) for i, n in enumerate(bl)}
            Bf = {n: vb[n] for n in b16n}
            T64 = {n: vb[n] for n in t64n}
        F["EH"] = F["TMP"]
        F["BA"] = F["SIG"]
        T64["YA"] = T64["LAK"]
        B["F"], B["Bf"], B["T64"] = F, Bf, T64
        B["R0"], B["YF"], B["YQ"], B["GT"] = F["SIG"], F["EX"], F["EN"], F["KH"]
        B["ST"] = p.sb(f"ST{S}", [128, 8, 4], F32)
        B["RKS"] = p.sb(f"RKS{S}", [128, 8], F32)
        B["Pf"] = p.sb(f"Pf{S}", [128, 64], F32)
        B["Pb"] = p.sb(f"Pb{S}", [128, 64], BF16)
        B["RR"] = p.sb(f"RR{S}", [128, 64], BF16)
        B["UB"] = p.sb(f"UB{S}", [128, 64], BF16)
        B["LNGB"] = LNGBs[S]
        B["PY"] = PS[4 + S]
        B["PT1"] = PS[6 + S]
        sets.append(B)
    b3 = lambda r_: Ref(r_.ap.rearrange("p (a b) -> p a b", a=8), r_.bufs)
    HS = (slice(0, 64), slice(64, 128))
    st = {"sci": 0}

    def newps():
        st["sci"] += 1
        return PS[st["sci"] % 4]

    def mm2(out_t, col0, ncol, lhs_fn, rhs_fn, start=True, stop=True):
        for hs in HS:
            e_mm(s, rf(out_t)[hs, col0:col0 + ncol], lhs_fn(hs), rhs_fn(hs), start, stop)

    def unit(hp, gq, B):
        F, Bf, T64, PA, w = B["F"], B["Bf"], B["T64"], B["PA"], B["WH"]
        R0, YF, YQ, GT, ST, RKS = B["R0"], B["YF"], B["YQ"], B["GT"], B["ST"], B["RKS"]
        Pf, Pb, RR, UB, LNGB = B["Pf"], B["Pb"], B["RR"], B["UB"], B["LNGB"]
        hc = lambda n: rf(PH)[:, hp, n:n + 1]
        tok = gq * 512
        for which, nm in enumerate(("r", "k", "v")):
            pp = newps()
            for kc in range(8):
                e_mm(s, rf(pp)[:, :], rf(w)[:, kc, which * 128:(which + 1) * 128],
                     Ref(HT[:, kc, tok:tok + 512], HT.b(gq)), kc == 0, kc == 7)
            pa = PA[which]
            if gq == 0:
                e_memset(s, "dve", rf(pa)[:, 0:1], 0.0)
            else:
                e_copy(s, "dve", rf(pa)[:, 0:1], rf(pa)[:, 512:513])
            e_copy(s, "act", rf(pa)[:, 1:513], rf(pp)[:, :])
            yield
            tmp_ = rf(F["TMP"])[:, :] if which != 1 else rf(F["CUM"])[:, :]
            e_act(s, tmp_, rf(pa)[:, 0:512], AF.Copy, scale=hc(which))
            dst_ = rf(Bf["VT"])[:, :] if nm == "v" else rf(F[nm])[:, :]
            e_stt(s, "dve", dst_, rf(pa)[:, 1:513], rf(OM)[:, hp, which:which + 1], tmp_, ALU.mult, ALU.add)
        r_, k_ = rf(F["r"])[:, :], rf(F["k"])[:, :]
        pz = newps()
        e_mm(s, rf(pz)[:, :], rf(W2)[:, hp * 128:(hp + 1) * 128], rf(TW)[:, tok:tok + 512])
        e_act(s, rf(F["SIG"])[:, :], rf(pz)[:, :], AF.Sigmoid, bias=hc(3))
        pz2 = newps()
        e_mm(s, rf(pz2)[:, :], rf(A2)[:, hp * 128:(hp + 1) * 128], rf(AL)[:, tok:tok + 512])
        e_act(s, rf(F["A"])[:, :], rf(pz2)[:, :], AF.Sigmoid, bias=hc(4))
        yield
        e_ts(s, "dve", rf(F["KKN"])[:, :], k_, hc(5), None, ALU.mult)
        e_act(s, rf(Bf["KK2"])[:, :], rf(F["KKN"])[:, :], AF.Square)
        yield
        pz = newps()
        e_mm(s, rf(pz)[:, :], rf(BLK)[:, :], rf(Bf["KK2"])[:, :])
        e_act(s, rf(F["TMP"])[:, :], rf(pz)[:, :], AF.Sqrt)
        yield
        e_ts(s, "dve", rf(F["TMP"])[:, :], rf(F["TMP"])[:, :], 1e-12, None, ALU.max)
        s.op("dve", lambda E: E.reciprocal(out=F["TMP"][:, :], in_=F["TMP"][:, :]), reads=F["TMP"].all(),
             writes=F["TMP"].all())
        e_tt(s, "dve", rf(F["KKN"])[:, :], rf(F["KKN"])[:, :], rf(F["TMP"])[:, :], ALU.mult)
        e_act(s, rf(F["KH"])[:, :], rf(F["A"])[:, :], AF.Identity, bias=rf(OM)[:, hp, 3:4], scale=hc(6))
        e_tt(s, "dve", rf(F["KH"])[:, :], rf(F["KH"])[:, :], k_, ALU.mult)
        yield
        s.op("dve", lambda E: E.tensor_tensor_scan(out=F["CUM"][:, :], data0=M512[:, :], data1=F["SIG"][:, :],
                                                   initial=0.0, op0=ALU.mult, op1=ALU.add),
             reads=M512.all() + F["SIG"].all(), writes=F["CUM"].all())
        yield
        cum = rf(F["CUM"])[:, :]
        e_act(s, rf(F["EC"])[:, :], cum, AF.Exp, scale=NEG_EXP_HALF)
        e_tt(s, "dve", rf(F["EX"])[:, :], cum, rf(F["SIG"])[:, :], ALU.subtract)
        e_act(s, rf(F["EN"])[:, :], cum, AF.Exp, scale=-NEG_EXP_HALF)
        cum3 = b3(cum)
        cend = Ref(cum3.ap[:, :, 63:64].to_broadcast([128, 8, 64]), cum.bufs)
        e_tt(s, "dve", b3(rf(F["EH"])[:, :]), cend, cum3, ALU.subtract)
        yield
        e_act(s, rf(F["EX"])[:, :], rf(F["EX"])[:, :], AF.Exp, scale=NEG_EXP_HALF)
        e_act(s, rf(F["EH"])[:, :], rf(F["EH"])[:, :], AF.Exp, scale=NEG_EXP_HALF)
        e_tt(s, "dve", rf(Bf["Rt"])[:, :], r_, rf(F["EC"])[:, :], ALU.mult)
        e_tt(s, "dve", rf(Bf["RK"])[:, :], r_, rf(F["KH"])[:, :], ALU.mult)
        e_tt(s, "dve", rf(F["BA"])[:, :], rf(F["KKN"])[:, :], rf(F["A"])[:, :], ALU.mult)
        yield
        e_stt(s, "dve", rf(Bf["At"])[:, :], rf(F["KKN"])[:, :], -1.0, rf(F["EX"])[:, :], ALU.mult, ALU.mult)
        e_tt(s, "dve", rf(Bf["Bt"])[:, :], rf(F["BA"])[:, :], rf(F["EN"])[:, :], ALU.mult)
        e_tt(s, "dve", rf(Bf["Bh"])[:, :], rf(F["BA"])[:, :], rf(F["EH"])[:, :], ALU.mult)
        e_tt(s, "dve", rf(Bf["Kt"])[:, :], rf(F["KH"])[:, :], rf(F["EN"])[:, :], ALU.mult)
        e_tt(s, "dve", rf(Bf["Kh"])[:, :], rf(F["KH"])[:, :], rf(F["EH"])[:, :], ALU.mult)
        yield
        blk = lambda n, c8, hs: rf(Bf[n])[hs, c8 * 64:(c8 + 1) * 64]
        tb_ = lambda n, c8, hs: rf(T64[n])[hs, c8 * 64:(c8 + 1) * 64]
        idh = lambda hs: rf(ident)[hs, hs]
        for src, dst in (("VT", "V64"), ("Bh", "BH64"), ("Kh", "KH64")):
            pt = newps()
            for c8 in range(8):
                mm2(pt, c8 * 64, 64, lambda hs, c8=c8, src=src: blk(src, c8, hs), idh)
            e_copy(s, "act", rf(T64[dst])[:, :], rf(pt)[:, :])
            yield
        for lh, rh, mask, dst in (("Bt", "At", MUS, "N"), ("At", "Bt", MLS, "Q"), ("Kt", "At", MUS, "LAK"),
                                  ("Bt", "Rt", MUI, "ARB"), ("Kt", "Rt", MUI, "ARK")):
            pt = newps()
            for c8 in range(8):
                mm2(pt, c8 * 64, 64, lambda hs, c8=c8, lh=lh: blk(lh, c8, hs), lambda hs, c8=c8, rh=rh: blk(rh, c8, hs))
            e_tt(s, "dve", rf(T64[dst])[:, :], rf(pt)[:, :], rf(mask)[:, :], ALU.mult)
            yield
        e_tt(s, "dve", rf(T64["XA"])[:, :], rf(T64["N"])[:, :], rf(ID8)[:, :], ALU.add)
        Pn, Qn, Pn2, Qn2 = "N", "Q", "N2", "Q2"
        for lvl in range(1, 6):
            pq = newps()
            for c8 in range(8):
                mm2(pq, c8 * 64, 64, lambda hs, c8=c8, Pn=Pn: tb_(Pn, c8, hs), lambda hs, c8=c8, Qn=Qn: tb_(Qn, c8, hs))
            e_copy(s, "act", rf(T64[Qn2])[:, :], rf(pq)[:, :])
            if lvl < 5:
                pp_ = newps()
                for c8 in range(8):
                    mm2(pp_, c8 * 64, 64, lambda hs, c8=c8, Qn=Qn: tb_(Qn, c8, hs),
                        lambda hs, c8=c8, Pn=Pn: tb_(Pn, c8, hs))
                e_copy(s, "act", rf(T64[Pn2])[:, :], rf(pp_)[:, :])
            yield
            px = newps()
            for c8 in range(8):
                mm2(px, c8 * 64, 64, lambda hs, c8=c8, Qn2=Qn2: tb_(Qn2, c8, hs), lambda hs, c8=c8: tb_("XA", c8, hs))
            e_tt(s, "dve", rf(T64["XA"])[:, :], rf(T64["XA"])[:, :], rf(px)[:, :], ALU.add)
            yield
            Pn, Pn2 = Pn2, Pn
            Qn, Qn2 = Qn2, Qn
        pr0 = newps()
        for c8 in range(8):
            mm2(pr0, c8 * 64, 64, lambda hs, c8=c8: tb_("LAK", c8, hs), lambda hs, c8=c8: tb_("V64", c8, hs))
        e_copy(s, "act", rf(R0)[:, :], rf(pr0)[:, :])
        pg = newps()
        e_mm(s, rf(pg)[:, :], rf(G2)[:, hp * 128:(hp + 1) * 128], rf(SGL)[:, tok:tok + 512])
        e_copy(s, "act", rf(GT)[:, :], rf(pg)[:, :])
        yield
        PY, PT1 = B["PY"], B["PT1"]
        pbh = lambda hs: rf(Pb)[hs, :]
        ubh = lambda hs: rf(UB)[hs, :]
        for c8 in range(8):
            mm2(PT1, 0, 64, lambda hs: blk("At", c8, hs), pbh)
            mm2(PY, c8 * 64, 64, lambda hs: blk("Rt", c8, hs), pbh, True, False)
            e_tt(s, "dve", rf(RR)[:, :], rf(PT1)[:, 0:64], rf(R0)[:, c8 * 64:(c8 + 1) * 64], ALU.add)
            yield
            mm2(PT1, 64, 64, lambda hs: tb_("XA", c8, hs), lambda hs: rf(RR)[hs, :])
            e_copy(s, "act", rf(UB)[:, :], rf(PT1)[:, 64:128])
            yield
            mm2(PT1, 128, 64, lambda hs: tb_("KH64", c8, hs), lambda hs: tb_("V64", c8, hs), True, False)
            mm2(PT1, 128, 64, lambda hs: tb_("BH64", c8, hs), ubh, False, True)
            mm2(PY, c8 * 64, 64, lambda hs: tb_("ARB", c8, hs), ubh, False, False)
            mm2(PY, c8 * 64, 64, lambda hs: tb_("ARK", c8, hs), lambda hs: tb_("V64", c8, hs), False, True)
            e_stt(s, "dve", rf(Pf)[:, :], rf(Pf)[:, :], rf(F["EC"])[:, c8 * 64 + 63:c8 * 64 + 64], rf(PT1)[:, 128:192],
                  ALU.mult, ALU.add)
            e_copy(s, "act", rf(Pb)[:, :], rf(Pf)[:, :])
            yield
        e_copy(s, "act", rf(YF)[:, :], rf(PY)[:, :])
        yf3 = b3(rf(YF)[:, :])
        yq3 = b3(rf(YQ)[:, :])
        prk = newps()
        for c8 in range(8):
            mm2(prk, c8, 1, lambda hs: blk("RK", c8, hs), lambda hs: rf(RKb)[hs, hp:hp + 1])
        e_copy(s, "act", rf(RKS)[:, :], rf(prk)[:, 0:8])
        yield
        s.op("dve", lambda E: E.tensor_reduce(out=ST[:, :, 0], in_=YF[:, :].rearrange("p (a b) -> p a b", a=8),
                                              axis=AX.X, op=ALU.add), reads=YF.all(), writes=ST.all())
        e_act(s, rf(YQ)[:, :], rf(YF)[:, :], AF.Square)
        yield
        s.op("dve", lambda E: E.tensor_reduce(out=ST[:, :, 1], in_=YQ[:, :].rearrange("p (a b) -> p a b", a=8),
                                              axis=AX.X, op=ALU.add), reads=YQ.all(), writes=ST.all())
        e_ts(s, "dve", rf(ST)[:, :, 2], rf(ST)[:, :, 0], 1.0 / 64, None, ALU.mult)
        e_tt(s, "dve", rf(ST)[:, :, 0], rf(ST)[:, :, 2], rf(ST)[:, :, 2], ALU.mult)
        e_stt(s, "dve", rf(ST)[:, :, 1], rf(ST)[:, :, 1], 1.0 / 64, rf(ST)[:, :, 0], ALU.mult, ALU.subtract)
        e_ts(s, "dve", rf(ST)[:, :, 1], rf(ST)[:, :, 1], GN_EPS, None, ALU.add)
        e_act(s, rf(ST)[:, :, 3], rf(ST)[:, :, 1], AF.Sqrt)
        yield
        s.op("dve", lambda E: E.reciprocal(out=ST[:, :, 3], in_=ST[:, :, 3]), reads=ST.all(), writes=ST.all())
        mean_b = Ref(ST[:, :, 2:3].to_broadcast([128, 8, 64]), ST.all())
        rstd_b = Ref(ST[:, :, 3:4].to_broadcast([128, 8, 64]), ST.all())
        e_tt(s, "dve", yf3, yf3, mean_b, ALU.subtract)
        e_tt(s, "dve", yf3, yf3, rstd_b, ALU.mult)
        rks_b = Ref(RKS[:, :].rearrange("p (a b) -> p a b", b=1).to_broadcast([128, 8, 64]), RKS.all())
        e_tt(s, "dve", yq3, b3(rf(T64["V64"])[:, :]), rks_b, ALU.mult)
        lng = Ref(LNGB[:, 0:1, :].to_broadcast([128, 8, 64]), LNGB.all())
        lnb = Ref(LNGB[:, 1:2, :].to_broadcast([128, 8, 64]), LNGB.all())
        yield
        e_tt(s, "dve", yf3, yf3, lng, ALU.mult)
        e_tt(s, "dve", yf3, yf3, lnb, ALU.add)
        yield
        e_tt(s, "dve", rf(T64["YA"])[:, :], rf(YF)[:, :], rf(YQ)[:, :], ALU.add)
        yield
        pt = newps()
        for c8 in range(8):
            mm2(pt, c8 * 64, 64, lambda hs: tb_("YA", c8, hs), idh)
        s.op("dve", lambda E: E.tensor_tensor(
            out=OT[:, hp, tok:tok + 512], in0=pt[:, :], in1=GT[:, :], op=ALU.mult),
            reads=pt.all() + GT.all(), writes=OT.all())
        yield

    def stream(S, hps):
        B = sets[S]
        for hp in hps:
            w = B["WH"]
            s.dma("pool", w[:, :, :], D["l0_w_hp"][hp].rearrange("p (kc n) -> p kc n", kc=8), writes=w.all())
            LNGB = B["LNGB"]
            for hh in range(2):
                h = 2 * hp + hh
                s.dma("sp", LNGB[HS[hh], 0, :], D["l0_lnx_g"][h * 64:(h + 1) * 64].partition_broadcast(64),
                      writes=LNGB.all())
                s.dma("sp", LNGB[HS[hh], 1, :], D["l0_lnx_b"][h * 64:(h + 1) * 64].partition_broadcast(64),
                      writes=LNGB.all())
            e_memset(s, "dve", rf(B["Pf"])[:, :], 0.0)
            e_memset(s, "dve", rf(B["Pb"])[:, :], 0.0)
            yield
            for gq in range(4):
                yield from unit(hp, gq, B)

    gens = [stream(0, (0, 2)), stream(1, (1, 3))]
    alive = [True, True]
    first = True
    while any(alive):
        for S in range(2):
            if alive[S]:
                try:
                    next(gens[S])
                except StopIteration:
                    alive[S] = False
            if first and S == 0:
                for _ in range(STAGGER):
                    next(gens[0])
                first = False
    s.barrier()
    for kc in range(8):
        s.dma("sp", XT[:, kc, :], spill[:, kc, :], reads=spill.all(), writes=XT.all())


GAIN_NAMES = ["l0_ffn1_pre_g", "l0_ffn1_post_g", "l0_mix_pre_g", "l0_mix_post_g", "l0_ffn2_pre_g", "l0_ffn2_post_g",
              "l1_ffn1_pre_g", "l1_ffn1_post_g", "l1_mix_pre_g", "l1_mix_post_g", "l1_ffn2_pre_g", "l1_ffn2_post_g"]
HALF_GAINS = [1, 5, 7, 11]


def build_program(stages=("f01", "m0", "f02f11", "m1", "f12"), dbg=False):
    k = KB()
    nc = k.nc
    s = k.s
    xT_d = k.dram_in("xT", [DM, SEQ])
    gains_d = k.dram_in("gains", [128, 12 * 8])
    ffn_d = {}
    for nm in ("l0_ffn1", "l0_ffn2", "l1_ffn1", "l1_ffn2"):
        ffn_d[nm] = (k.dram_in(nm + "_w_in", [NJ, 128, 2048]), k.dram_in(nm + "_w_out", [2, 8, 128, 11 * 128]))
    D = {}
    for nm, shp in (("l0_pl", [128, 32]), ("l0_ph", [128, 32]), ("l0_mul", [128, 3]), ("l0_lnx_g", [512]),
                    ("l0_lnx_b", [512]), ("l0_w2", [64, 512]), ("l0_a2", [64, 512]), ("l0_g2", [128, 512]),
                    ("l0_gate_a_w", [8, 64, 64]), ("l0_gate_x_w", [8, 64, 64]), ("l0_w_lru", [4, 128, 8 * 256]),
                    ("l0_w_lora", [128, 8 * 256]), ("l0_w_hp", [4, 128, 8 * 384]), ("l0_w_out", [DM, DM])):
        D[nm] = k.dram_in(nm, shp)
    D["xt_spill"] = T("xt_spill", nc.dram_tensor("xt_spill", [128, 8, SEQ], F32, kind="Internal").ap())
    if dbg:
        D["dbg_OT"] = k.dram_out("dbg_OT", [128, 8, SEQ], BF16)
    wqkv_d = k.dram_in("l1_w_qkv", [8, 128, 8 * 384])
    l1_wo_d = k.dram_in("l1_w_out", [DM, DM])
    outT_d = k.dram_out("outT", [DM, SEQ])

    with k.es:
        XT = k.sb("XT", [128, 8, SEQ], F32, parts=4)
        C = {}
        C["gains"] = k.sb("gains", [128, 12, 8], F32)
        C["ones_m"] = k.sb("ones_m", [128, 128], BF16)
        PS = [k.ps(f"ps{i}", [128, 512]) for i in range(8)]
        C["HTG"] = k.sb("HTG", [128, 8, SEQ], BF16, parts=4)
        C["ht_ready"] = None

        s.op("dve", lambda e: e.memset(C["ones_m"][:, :], 1.0 / DM), writes=C["ones_m"].all())
        C["one_f"] = k.sb("one_f", [128, 1], F32)
        C["ones_col"] = k.sb("ones_col", [128, 1], BF16)
        C["ones_bf"] = k.sb("ones_bf", [128, 512], BF16)
        C["ident"] = k.sb("ident", [128, 128], BF16)
        C["ntri"] = k.sb("ntri", [128, 128], BF16)
        s.op("dve", lambda e: e.memset(C["one_f"][:, :], 1.0), writes=C["one_f"].all())
        s.op("dve", lambda e: e.memset(C["ones_col"][:, :], 1.0), writes=C["ones_col"].all())
        s.op("dve", lambda e: e.memset(C["ones_bf"][:, :], 1.0), writes=C["ones_bf"].all())
        s.op("pool", lambda e: e.affine_select(out=C["ident"][:, :], in_=C["ones_bf"][:, 0:128], pattern=[[-1, 128]],
                                               compare_op=ALU.is_equal, fill=0.0, base=0, channel_multiplier=1),
             reads=C["ones_bf"].all(), writes=C["ident"].all())
        s.op("pool", lambda e: e.affine_select(out=C["ntri"][:, :], in_=C["ones_bf"][:, 0:128], pattern=[[-1, 128]],
                                               compare_op=ALU.is_ge, fill=0.0, base=0, channel_multiplier=1),
             reads=C["ones_bf"].all(), writes=C["ntri"].all())
        s.op("dve", lambda e: e.tensor_scalar(out=C["ntri"][:, :], in0=C["ntri"][:, :], scalar1=-1.0, scalar2=None,
                                              op0=ALU.mult),
             reads=C["ntri"].all(), writes=C["ntri"].all())
        C["eps"] = k.sb("eps", [128, 1], F32)
        s.op("dve", lambda e: e.memset(C["eps"][:, :], NORM_EPS), writes=C["eps"].all())
        s.dma("sp", C["gains"][:, :, :], gains_d[:, :].rearrange("p (n c) -> p n c", n=12), writes=C["gains"].all())
        for gi in HALF_GAINS:
            s.op("dve", lambda e, gi=gi: e.tensor_scalar(out=C["gains"][:, gi, :], in0=C["gains"][:, gi, :],
                                                         scalar1=0.5, scalar2=None, op0=ALU.mult),
                 reads=C["gains"].all(), writes=C["gains"].all())
        for tb in range(4):
            for kc in range(8):
                s.dma("sp", XT[:, kc, tb * 512:(tb + 1) * 512], xT_d[kc * 128:(kc + 1) * 128, tb * 512:(tb + 1) * 512],
                      writes=XT.b(tb))

        PRE_GI = {"f01": 0, "m0": 2, "f02": 4, "f02f11": 4, "f11": 6, "m1": 8, "f12": 10}
        for si, st in enumerate(stages):
            nxt = PRE_GI[stages[si + 1]] if si + 1 < len(stages) else None
            if st == "f01":
                ffn_stage(k, C, XT, PS, [(*ffn_d["l0_ffn1"], 0, 1)], nxt)
            elif st == "f02":
                ffn_stage(k, C, XT, PS, [(*ffn_d["l0_ffn2"], 4, 5)], nxt)
            elif st == "f02f11":
                ffn_stage(k, C, XT, PS, [(*ffn_d["l0_ffn2"], 4, 5), (*ffn_d["l1_ffn1"], 6, 7)], nxt)
            elif st == "f11":
                ffn_stage(k, C, XT, PS, [(*ffn_d["l1_ffn1"], 6, 7)], nxt)
            elif st == "m0":
                mixer0_stage(k, C, XT, PS, D, 2, 3, nxt)
            elif st == "m1":
                if dbg:
                    C["dbg_OT"] = D["dbg_OT"]
                attn_stage(k, C, XT, PS, wqkv_d, l1_wo_d, 8, 9, nxt)
            elif st == "f12":
                ffn_stage(k, C, XT, PS, [(*ffn_d["l1_ffn2"], 10, 11)], nxt)

        for tb in range(4):
            for kc in range(8):
                s.dma("sp", outT_d[kc * 128:(kc + 1) * 128, tb * 512:(tb + 1) * 512], XT[:, kc, tb * 512:(tb + 1) * 512],
                      reads=XT.b(tb), writes=outT_d.all())
        s.barrier(engines=["sp"])
    return nc


def _col(v):
    return np.ascontiguousarray(np.asarray(v, np.float32).reshape(8, 128).T)


def prep_shared(inp):
    d = {}
    d["gains"] = np.ascontiguousarray(np.concatenate([_col(inp[n]) for n in GAIN_NAMES], axis=1))
    for nm in ("l0_ffn1", "l0_ffn2", "l1_ffn1", "l1_ffn2"):
        w_in = np.asarray(inp[nm + "_w_in"], np.float32)
        w_out = np.asarray(inp[nm + "_w_out"], np.float32)
        g = w_in[:, :DFF].reshape(8, 128, NJ, 128)
        u = w_in[:, DFF:].reshape(8, 128, NJ, 128)
        gu = np.concatenate([g, u], axis=3)
        d[nm + "_w_in"] = np.ascontiguousarray(gu.transpose(2, 1, 0, 3).reshape(NJ, 128, 2048))
        wo = w_out.reshape(2, 11, 128, 8, 128)
        d[nm + "_w_out"] = np.ascontiguousarray(wo.transpose(0, 3, 2, 1, 4).reshape(2, 8, 128, 11 * 128))
    f = lambda n: np.asarray(inp[n], np.float32)
    cw = f("l0_conv_w")
    pl = np.stack([cw[0], cw[1], cw[2], cw[3], f("l0_conv_b"), f("l0_gate_a_b"), f("l0_gate_x_b"), f("l0_lambda")], axis=1)
    d["l0_pl"] = np.ascontiguousarray(pl.reshape(4, 128, 8).transpose(1, 0, 2).reshape(128, 32))
    mu = f("l0_mu")
    ph = np.stack([mu[0:512], mu[512:1024], mu[1024:1536], f("l0_w0"), f("l0_a0"), f("l0_k_k"), f("l0_k_a"),
                   f("l0_r_k").reshape(512)], axis=1)
    d["l0_ph"] = np.ascontiguousarray(ph.reshape(4, 128, 8).transpose(1, 0, 2).reshape(128, 32))
    mul = np.zeros((128, 3), np.float32)
    mul[0:64, 0] = mu[1536:1600]
    mul[0:64, 1] = mu[1600:1664]
    mul[:, 2] = mu[1664:1792]
    d["l0_mul"] = mul
    for n in ("l0_lnx_g", "l0_lnx_b", "l0_w2", "l0_a2", "l0_g2", "l0_gate_a_w", "l0_gate_x_w", "l0_w_out"):
        d[n] = np.ascontiguousarray(f(n))
    wi = f("l0_w_in").reshape(8, 128, 2816)
    lru = np.concatenate([wi[:, :, 1792:2304].reshape(8, 128, 4, 128), wi[:, :, 2304:2816].reshape(8, 128, 4, 128)], axis=3)
    d["l0_w_lru"] = np.ascontiguousarray(lru.transpose(2, 1, 0, 3).reshape(4, 128, 8 * 256))
    d["l0_w_lora"] = np.ascontiguousarray(wi[:, :, 1536:1792].transpose(1, 0, 2).reshape(128, 8 * 256))
    hd = np.stack([wi[:, :, 0:512].reshape(8, 128, 4, 128), wi[:, :, 512:1024].reshape(8, 128, 4, 128),
                   wi[:, :, 1024:1536].reshape(8, 128, 4, 128)], axis=3)
    d["l0_w_hp"] = np.ascontiguousarray(hd.transpose(2, 1, 0, 3, 4).reshape(4, 128, 8 * 384))
    wq = np.asarray(inp["l1_w_qkv"], np.float32).reshape(8, 128, 3, 8, 128)
    d["l1_w_qkv"] = np.ascontiguousarray(wq.transpose(3, 1, 0, 2, 4).reshape(8, 128, 8 * 384))
    d["l1_w_out"] = np.ascontiguousarray(np.asarray(inp["l1_w_out"], np.float32))
    return d


_CACHE = {}


def kernel(**inputs):
    x = np.asarray(inputs["x"], np.float32)
    shared = prep_shared(inputs)
    if "nc" not in _CACHE:
        _CACHE["nc"] = build_program()
    nc = _CACHE["nc"]
    in_maps = []
    for c in range(N_CORES):
        m = dict(shared)
        m["xT"] = np.ascontiguousarray(x[c].T)
        in_maps.append(m)
    res = run_bass_kernel_spmd(nc, in_maps, core_ids=list(range(N_CORES)))
    out = np.stack([np.ascontiguousarray(res.results[c]["outT"].T) for c in range(N_CORES)], axis=0)
    return out.astype(np.float32)
```

```python
import math
from contextlib import ExitStack

import numpy as np
import concourse.bass as bass
import concourse.mybir as mybir
from concourse.bass_utils import run_bass_kernel_spmd

F32 = mybir.dt.float32
BF16 = mybir.dt.bfloat16
AF = mybir.ActivationFunctionType
ALU = mybir.AluOpType

SEQ = 2048
DM = 1024
DFF = 2816
NJ = 22
NORM_EPS = 1e-6
N_CORES = 8


class Buf:
    __slots__ = ("name", "w", "r")

    def __init__(self, name):
        self.name = name
        self.w = None
        self.r = {}


class T:
    def __init__(self, name, t, parts=1):
        self.name = name
        self.t = t
        self.bufs = [Buf(f"{name}.{i}") for i in range(parts)]

    def b(self, *idx):
        return [self.bufs[i] for i in idx]

    def all(self):
        return list(self.bufs)

    def __getitem__(self, key):
        return self.t[key]


class Sched:
    COMPUTE = ("pe", "act", "dve", "pool")

    def __init__(self, nc, es, n_dma_ch=20):
        self.nc = nc
        self.eng = {"pe": nc.tensor, "act": nc.scalar, "dve": nc.vector, "pool": nc.gpsimd, "sp": nc.sync}
        self.sems = {}
        self.cnt = {}
        for e in self.COMPUTE:
            self.sems[e] = es.enter_context(nc.semaphore(f"s_{e}"))
            self.cnt[e] = 0
        self.ch = {}
        self.ch_next = {}
        for q in ("sp", "pool", "act"):
            n = n_dma_ch if q != "act" else 4
            lst = []
            for i in range(n):
                key = f"d_{q}{i}"
                self.sems[key] = es.enter_context(nc.semaphore(key))
                self.cnt[key] = 0
                lst.append(key)
            self.ch[q] = lst
            self.ch_next[q] = 0
        self.seen = {e: {} for e in self.eng}
        self.n_wait = 0
        self.n_ins = 0

    def _wait(self, e, ev):
        key, val = ev
        if val <= 0:
            return
        if self.seen[e].get(key, 0) >= val:
            return
        self.seen[e][key] = val
        self.eng[e].wait_ge(self.sems[key], val)
        self.n_wait += 1

    def _deps(self, e, reads, writes):
        evs = {}

        def need(ev):
            if ev is None:
                return
            k_, v_ = ev
            if e == "pe" and k_ == "pe":
                return
            if evs.get(k_, 0) < v_:
                evs[k_] = v_

        for b in reads:
            need(b.w)
        for b in writes:
            need(b.w)
            for kv in b.r.items():
                need(kv)
        return evs

    def op(self, e, fn, reads=(), writes=()):
        evs = self._deps(e, reads, writes)
        for ev in evs.items():
            self._wait(e, ev)
        ins = fn(self.eng[e])
        self.cnt[e] += 1
        ev = (e, self.cnt[e])
        ins.then_inc(self.sems[e], 1)
        self.seen[e][e] = max(self.seen[e].get(e, 0), 0)
        for b in writes:
            b.w = ev
            b.r = {}
        for b in reads:
            if b.w is not ev:
                b.r[e] = self.cnt[e]
        self.n_ins += 1
        return ins

    def dma(self, q, out, in_, reads=(), writes=()):
        e = q
        evs = self._deps(e, reads, writes)
        key = self.ch[q][self.ch_next[q]]
        self.ch_next[q] = (self.ch_next[q] + 1) % len(self.ch[q])
        if evs.get(key, 0) < self.cnt[key]:
            evs[key] = self.cnt[key]
        for ev in evs.items():
            self._wait(e, ev)
        ins = self.eng[e].dma_start(out=out, in_=in_)
        self.cnt[key] += 16
        ins.then_inc(self.sems[key], 16)
        ev = (key, self.cnt[key])
        for b in writes:
            b.w = ev
            b.r = {}
        for b in reads:
            b.r[key] = self.cnt[key]
        self.n_ins += 1
        return ins

    def barrier(self, engines=None):
        evs = [(k_, v_) for k_, v_ in self.cnt.items() if v_ > 0]
        for e in (engines or self.eng):
            for ev in evs:
                if ev[0] == e:
                    continue
                self._wait(e, ev)


class Phase:
    def __init__(self, k):
        self.k = k
        self.es = ExitStack()

    def __enter__(self):
        self.es.__enter__()
        return self

    def __exit__(self, *a):
        self.k.s.barrier()
        return self.es.__exit__(*a)

    def sb(self, name, shape, dtype, parts=1):
        self.k.uid += 1
        t = self.es.enter_context(self.k.nc.sbuf_tensor(f"ph_{name}_{self.k.uid}", shape, dtype))
        return T(name, t, parts)


class KB:
    def __init__(self):
        self.nc = bass.Bass("TRN2", target_bir_lowering=False)
        self.es = ExitStack()
        self.s = Sched(self.nc, self.es)
        self.uid = 0

    def sb(self, name, shape, dtype, parts=1):
        t = self.es.enter_context(self.nc.sbuf_tensor("sb_" + name, shape, dtype))
        return T(name, t, parts)

    def ps(self, name, shape, dtype=F32, parts=1):
        t = self.es.enter_context(self.nc.psum_tensor("pp_" + name, shape, dtype))
        return T(name, t, parts)

    def dram_in(self, name, shape, dtype=F32):
        return T(name, self.nc.dram_tensor(name, list(shape), dtype, kind="ExternalInput").ap())

    def dram_out(self, name, shape, dtype=F32):
        return T(name, self.nc.dram_tensor(name, list(shape), dtype, kind="ExternalOutput").ap())

    def phase(self):
        return Phase(self)


class Bg:
    def __init__(self):
        self.q = []

    def add(self, gen, period=2):
        self.q.append([gen, period, period])

    def tick(self):
        for item in list(self.q):
            item[2] -= 1
            if item[2] <= 0:
                item[2] = item[1]
                try:
                    next(item[0])
                except StopIteration:
                    self.q.remove(item)

    def drain(self):
        while self.q:
            for item in list(self.q):
                try:
                    next(item[0])
                except StopIteration:
                    self.q.remove(item)


def rms_rstd_gen(k, C, src, src_bufs, SQ, PST, RSTD, ntok, fuse_sq=False):
    s = k.s
    s.op("act", lambda e: e.activation(out=SQ[:, :, 0:ntok], in_=src, func=AF.Square),
         reads=src_bufs, writes=SQ.all())
    if not fuse_sq:
        yield
    for kc in range(8):
        s.op("pe", lambda e, kc=kc: e.matmul(PST[:, 0:ntok], lhsT=C["ones_m"][:, :], rhs=SQ[:, kc, 0:ntok],
                                             start=(kc == 0), stop=(kc == 7)),
             reads=SQ.all() + C["ones_m"].all(), writes=PST.all())
    yield
    s.op("act", lambda e: e.activation(out=RSTD[:, 0:ntok], in_=PST[:, 0:ntok], func=AF.Sqrt, bias=C["eps"][:, 0:1]),
         reads=PST.all() + C["eps"].all(), writes=RSTD.all())
    yield
    s.op("dve", lambda e: e.reciprocal(out=RSTD[:, 0:ntok], in_=RSTD[:, 0:ntok]),
         reads=RSTD.all(), writes=RSTD.all())


def ffn_stage(k, C, XT, PS, ffns, next_gi=None):
    s = k.s
    G_ = C["gains"]
    with k.phase() as ph:
        HTG = C["HTG"]
        HTs = []
        for i in range(2):
            hv = T(f"HTv{i}", HTG.t[:, :, i * 1024:(i + 1) * 1024])
            hv.bufs = HTG.bufs[2 * i:2 * i + 2]
            HTs.append(hv)
        ACTT = ph.sb("ACTT", [128, 11, 1024], BF16, parts=22)
        YT = ph.sb("YT", [128, 8, 1024], F32, parts=16)
        SQ = [ph.sb(f"SQ{i}", [128, 8, 512], BF16) for i in range(2)]
        RSTD = [ph.sb(f"RSTD{i}", [128, 512], F32) for i in range(2)]
        WIN = [ph.sb(f"WIN{i}", [128, 8, 256], BF16) for i in range(3)]
        WOUT = [ph.sb(f"WOUT{i}", [128, 11, 128], BF16) for i in range(3)]
        SG = [ph.sb(f"SG{i}", [128, 512], F32) for i in range(2)]
        PG = [PS[0], PS[1]]
        PU = [PS[2], PS[3]]
        PY = [PS[4], PS[5]]
        PST = [PS[6], PS[7]]
        st = {"win": 0, "wout": 0, "pi": 0, "ni": 0}
        jobs = [(f, B) for f in range(len(ffns)) for B in range(2)]

        bg = Bg()

        def prenorm(ji):
            f, B = jobs[ji]
            HT = HTs[ji % 2]
            gi_pre = ffns[f][2]
            for sb_ in range(2):
                tok = B * 1024 + sb_ * 512
                xb = XT.b(B * 2 + sb_)
                n_ = st["ni"] % 2
                st["ni"] += 1
                yield from rms_rstd_gen(k, C, XT[:, :, tok:tok + 512], xb, SQ[n_], PST[n_], RSTD[n_], 512)
                for kc in range(8):
                    if kc == 4:
                        yield
                    s.op("dve", lambda e, kc=kc, tok=tok, sb_=sb_, n_=n_: e.scalar_tensor_tensor(
                        out=HT[:, kc, sb_ * 512:(sb_ + 1) * 512], in0=XT[:, kc, tok:tok + 512],
                        scalar=G_[:, gi_pre, kc:kc + 1], in1=RSTD[n_][:, :], op0=ALU.mult, op1=ALU.mult),
                        reads=xb + RSTD[n_].all() + G_.all(), writes=HT.b(sb_))

        def postnorm(ji):
            f, B = jobs[ji]
            gi_post = ffns[f][3]
            for sb_ in range(2):
                tok = B * 1024 + sb_ * 512
                rhs_sl = slice(sb_ * 512, (sb_ + 1) * 512)
                ybs = YT.b(*[dc * 2 + sb_ for dc in range(8)])
                xb = XT.b(B * 2 + sb_)
                n_ = st["ni"] % 2
                st["ni"] += 1
                yield from rms_rstd_gen(k, C, YT[:, :, rhs_sl], ybs, SQ[n_], PST[n_], RSTD[n_], 512)
                for dc in range(8):
                    if dc % 2 == 0 and dc > 0:
                        yield
                    s.op("dve", lambda e, dc=dc, rhs_sl=rhs_sl, n_=n_: e.scalar_tensor_tensor(
                        out=YT[:, dc, rhs_sl], in0=YT[:, dc, rhs_sl], scalar=G_[:, gi_post, dc:dc + 1],
                        in1=RSTD[n_][:, :], op0=ALU.mult, op1=ALU.mult),
                        reads=YT.b(dc * 2 + sb_) + RSTD[n_].all() + G_.all(), writes=YT.b(dc * 2 + sb_))
                    s.op("dve", lambda e, dc=dc, rhs_sl=rhs_sl, tok=tok: e.tensor_tensor(
                        out=XT[:, dc, tok:tok + 512], in0=XT[:, dc, tok:tok + 512], in1=YT[:, dc, rhs_sl], op=ALU.add),
                        reads=YT.b(dc * 2 + sb_) + xb, writes=xb)

        def up(ji, G, after_first=None):
            f, B = jobs[ji]
            HT = HTs[ji % 2]
            w_in_d = ffns[f][0]
            for jj in range(11):
                j = G * 11 + jj
                W = WIN[st["win"] % 3]
                st["win"] += 1
                s.dma("pool", W[:, :, :], w_in_d[j].rearrange("p (kc c) -> p kc c", kc=8), writes=W.all())
                for sb_ in range(2):
                    pg, pu, sg = PG[st["pi"] % 2], PU[st["pi"] % 2], SG[st["pi"] % 2]
                    st["pi"] += 1
                    rhs_sl = slice(sb_ * 512, (sb_ + 1) * 512)
                    for kc in range(8):
                        s.op("pe", lambda e, kc=kc, pg=pg, W=W, rhs_sl=rhs_sl: e.matmul(
                            pg[:, :], lhsT=W[:, kc, 0:128], rhs=HT[:, kc, rhs_sl], start=(kc == 0), stop=(kc == 7)),
                            reads=W.all() + HT.b(sb_), writes=pg.all())
                    for kc in range(8):
                        s.op("pe", lambda e, kc=kc, pu=pu, W=W, rhs_sl=rhs_sl: e.matmul(
                            pu[:, :], lhsT=W[:, kc, 128:256], rhs=HT[:, kc, rhs_sl], start=(kc == 0), stop=(kc == 7)),
                            reads=W.all() + HT.b(sb_), writes=pu.all())
                    s.op("act", lambda e, pg=pg, sg=sg: e.activation(out=sg[:, :], in_=pg[:, :], func=AF.Silu),
                         reads=pg.all(), writes=sg.all())
                    s.op("dve", lambda e, pu=pu, sg=sg, jj=jj, rhs_sl=rhs_sl: e.tensor_tensor(
                        out=ACTT[:, jj, rhs_sl], in0=sg[:, :], in1=pu[:, :], op=ALU.mult),
                        reads=sg.all() + pu.all(), writes=ACTT.b(jj * 2 + sb_))
                    bg.tick()
                if jj == 0 and after_first is not None:
                    after_first()

        def down(ji, G):
            f, B = jobs[ji]
            w_out_d = ffns[f][1]
            for dc in range(8):
                W = WOUT[st["wout"] % 3]
                st["wout"] += 1
                s.dma("pool", W[:, :, :], w_out_d[G, dc].rearrange("p (jj c) -> p jj c", jj=11), writes=W.all())
                for sb_ in range(2):
                    py = PY[st["pi"] % 2]
                    st["pi"] += 1
                    rhs_sl = slice(sb_ * 512, (sb_ + 1) * 512)
                    for jj in range(11):
                        s.op("pe", lambda e, jj=jj, py=py, W=W, rhs_sl=rhs_sl: e.matmul(
                            py[:, :], lhsT=W[:, jj, :], rhs=ACTT[:, jj, rhs_sl], start=(jj == 0), stop=(jj == 10)),
                            reads=W.all() + ACTT.b(jj * 2 + sb_), writes=py.all())
                    yb = YT.b(dc * 2 + sb_)
                    if G == 0:
                        s.op("act", lambda e, py=py, dc=dc, rhs_sl=rhs_sl: e.activation(
                            out=YT[:, dc, rhs_sl], in_=py[:, :], func=AF.Copy),
                            reads=py.all(), writes=yb)
                    else:
                        s.op("dve", lambda e, py=py, dc=dc, rhs_sl=rhs_sl: e.tensor_tensor(
                            out=YT[:, dc, rhs_sl], in0=YT[:, dc, rhs_sl], in1=py[:, :], op=ALU.add),
                            reads=py.all() + yb, writes=yb)
                    bg.tick()

        def next_prenorm():
            HT = HTs[0]
            for sb_ in range(2):
                tok = sb_ * 512
                xb = XT.b(sb_)
                n_ = st["ni"] % 2
                st["ni"] += 1
                yield from rms_rstd_gen(k, C, XT[:, :, tok:tok + 512], xb, SQ[n_], PST[n_], RSTD[n_], 512)
                for kc in range(8):
                    if kc == 4:
                        yield
                    s.op("dve", lambda e, kc=kc, tok=tok, sb_=sb_, n_=n_: e.scalar_tensor_tensor(
                        out=HT[:, kc, sb_ * 512:(sb_ + 1) * 512], in0=XT[:, kc, tok:tok + 512],
                        scalar=G_[:, next_gi, kc:kc + 1], in1=RSTD[n_][:, :], op0=ALU.mult, op1=ALU.mult),
                        reads=xb + RSTD[n_].all() + G_.all(), writes=HT.b(sb_))

        n = len(jobs)
        assert n % 2 == 0
        if C["ht_ready"] is not None and C["ht_ready"] == (ffns[0][2], (0, 1)):
            pass
        else:
            bg.add(prenorm(0))
            bg.drain()
        C["ht_ready"] = None
        for ji in range(n):
            up(ji, 0, after_first=(lambda ji=ji: bg.add(postnorm(ji - 1), 1)) if ji > 0 else None)
            bg.drain()
            down(ji, 0)
            if ji + 1 < n:
                bg.add(prenorm(ji + 1), 1)
            elif next_gi is not None:
                bg.add(next_prenorm(), 1)
                C["ht_ready"] = (next_gi, (0, 1))
            up(ji, 1)
            bg.drain()
            down(ji, 1)
        bg.add(postnorm(n - 1))
        bg.drain()


def prenorm_to_HT(k, C, ph, XT, HT, PS, gi_pre, col_off=0):
    s = k.s
    G_ = C["gains"]
    with k.phase() as p2:
        SQ = [p2.sb(f"SQ{i}", [128, 8, 512], BF16) for i in range(2)]
        RSTD = [p2.sb(f"RSTD{i}", [128, 512], F32) for i in range(2)]
        bg = Bg()

        def chain(tb):
            tok = tb * 512
            xb = XT.b(tb)
            yield from rms_rstd_gen(k, C, XT[:, :, tok:tok + 512], xb, SQ[tb % 2], PS[6 + tb % 2], RSTD[tb % 2], 512)
            for kc in range(8):
                if kc == 4:
                    yield
                s.op("dve", lambda e, kc=kc: e.scalar_tensor_tensor(
                    out=HT[:, kc, col_off + tok:col_off + tok + 512], in0=XT[:, kc, tok:tok + 512],
                    scalar=G_[:, gi_pre, kc:kc + 1], in1=RSTD[tb % 2][:, :], op0=ALU.mult, op1=ALU.mult),
                    reads=xb + RSTD[tb % 2].all() + G_.all(), writes=HT.b(tb))

        skip = ()
        if C["ht_ready"] is not None and C["ht_ready"][0] == gi_pre and col_off == 0:
            skip = C["ht_ready"][1]
        C["ht_ready"] = None
        for tb in range(4):
            if tb in skip:
                continue
            bg.add(chain(tb), 1)
            bg.tick()
            bg.tick()
        bg.drain()


def outproj_postnorm(k, C, XT, PS, OT, wo_d, gi_post, next_gi=None, WO=None):
    s = k.s
    G_ = C["gains"]
    with k.phase() as p3:
        if WO is None:
            WO = T("WOv", C["HTG"].t[:, :, 1024:2048])
            WO.bufs = C["HTG"].bufs[2:4]
            for kc in range(8):
                s.dma("pool", WO[:, kc, :], wo_d[kc * 128:(kc + 1) * 128, :], writes=WO.all())
        YTs = [p3.sb(f"YT{i}", [128, 8, 512], F32, parts=8) for i in range(2)]
        SQ1 = p3.sb("SQ", [128, 8, 512], BF16)
        RSTD = [p3.sb(f"RSTD{i}", [128, 512], F32) for i in range(2)]
        bg = Bg()

        def chain(tb):
            tok = tb * 512
            YT = YTs[tb % 2]
            yield from rms_rstd_gen(k, C, YT[:, :, :], YT.all(), SQ1, PS[6 + tb % 2], RSTD[tb % 2], 512, fuse_sq=True)
            xb = XT.b(tb)
            for dc in range(8):
                if dc % 2 == 0 and dc > 0:
                    yield
                s.op("dve", lambda e, dc=dc: e.scalar_tensor_tensor(
                    out=YT[:, dc, :], in0=YT[:, dc, :], scalar=G_[:, gi_post, dc:dc + 1],
                    in1=RSTD[tb % 2][:, :], op0=ALU.mult, op1=ALU.mult),
                    reads=YT.b(dc) + RSTD[tb % 2].all() + G_.all(), writes=YT.b(dc))
                s.op("dve", lambda e, dc=dc: e.tensor_tensor(
                    out=XT[:, dc, tok:tok + 512], in0=XT[:, dc, tok:tok + 512], in1=YT[:, dc, :], op=ALU.add),
                    reads=YT.b(dc) + xb, writes=xb)

        if next_gi is not None:
            RSTDn = p3.sb("RSTDn", [128, 512], F32)
        HTG = C["HTG"]

        def next_prenorm():
            for sb_ in range(2):
                tok = sb_ * 512
                xb = XT.b(sb_)
                yield from rms_rstd_gen(k, C, XT[:, :, tok:tok + 512], xb, SQ1, PS[0], RSTDn, 512, fuse_sq=True)
                for kc in range(8):
                    if kc == 4:
                        yield
                    s.op("dve", lambda e, kc=kc, tok=tok: e.scalar_tensor_tensor(
                        out=HTG[:, kc, tok:tok + 512], in0=XT[:, kc, tok:tok + 512],
                        scalar=G_[:, next_gi, kc:kc + 1], in1=RSTDn[:, :], op0=ALU.mult, op1=ALU.mult),
                        reads=xb + RSTDn.all() + G_.all(), writes=HTG.b(sb_))

        pi = 0
        for tb in range(4):
            tok = tb * 512
            YT = YTs[tb % 2]
            if tb == 3 and next_gi is not None:
                bg.add(next_prenorm(), 1)
                C["ht_ready"] = (next_gi, (0, 1))
            for dc in range(8):
                pp = PS[4 + pi % 2]
                pi += 1
                for kc in range(8):
                    s.op("pe", lambda e, kc=kc, dc=dc, pp=pp, tok=tok: e.matmul(
                        pp[:, :], lhsT=WO[:, kc, dc * 128:(dc + 1) * 128], rhs=OT[:, kc, tok:tok + 512],
                        start=(kc == 0), stop=(kc == 7)),
                        reads=WO.all() + OT.all(), writes=pp.all())
                s.op("act", lambda e, dc=dc, pp=pp, YT=YT: e.activation(out=YT[:, dc, :], in_=pp[:, :], func=AF.Copy),
                     reads=pp.all(), writes=YT.b(dc))
                bg.tick()
            bg.drain()
            bg.add(chain(tb), 1)
        bg.drain()


def attn_stage(k, C, XT, PS, wqkv_d, wo_d, gi_pre, gi_post, next_gi=None):
    s = k.s
    with k.phase() as ph:
        OT = ph.sb("OT", [128, 8, SEQ], BF16, parts=1)
        with k.phase() as pab:
            HT = C["HTG"]
            prenorm_to_HT(k, C, pab, XT, HT, PS, gi_pre)
            with k.phase() as pb:
                NEGM = pb.sb("negm", [128, 4, 512], BF16)
                ZB = pb.sb("zb", [128, 512], BF16)
                s.op("dve", lambda e: e.memset(ZB[:, :], 0.0), writes=ZB.all())
                for d in range(4):
                    s.op("pool", lambda e, d=d: e.affine_select(
                        out=NEGM[:, d, :], in_=ZB[:, :], pattern=[[1, 512]], compare_op=ALU.is_gt,
                        fill=-30000.0, base=-128 * d, channel_multiplier=-1),
                        reads=ZB.all(), writes=NEGM.all())
                QT = [pb.sb(f"QT{i}", [128, SEQ], BF16) for i in range(2)]
                KT = [pb.sb(f"KT{i}", [128, SEQ], BF16) for i in range(2)]
                V = [pb.sb(f"V{i}", [128, 16, 128], BF16) for i in range(2)]
                W = [pb.sb(f"WQKV{i}", [128, 8, 384], BF16) for i in range(1)]
                OTOK = [pb.sb(f"OTOK{i}", [128, 16, 128], BF16) for i in range(2)]
                E = [pb.sb(f"E{i}", [128, 512], F32) for i in range(3)]
                SP = [pb.sb(f"SP{i}", [128, 512], BF16) for i in range(5)]
                ATT = [pb.sb(f"ATT{i}", [128, 512], BF16) for i in range(3)]
                OACC = [pb.sb(f"OACC{i}", [128, 4, 64], F32) for i in range(2)]
                CACC = [pb.sb(f"CACC{i}", [128, 4], F32) for i in range(2)]
                FS = [pb.sb(f"FS{i}", [128, 4], F32) for i in range(4)]
                PZ = [PS[0], PS[1], PS[2], PS[3], PS[4]]
                PO = [PS[5], PS[6]]
                PP = [PS[7]]
                st = {"pi": 0}

                bg = Bg()

                def pre_hp(hp):
                    w = W[0]
                    qt, kt_, v = QT[hp % 2], KT[hp % 2], V[hp % 2]
                    s.dma("pool", w[:, :, :], wqkv_d[hp].rearrange("p (kc c) -> p kc c", kc=8), writes=w.all())
                    for which in range(2):
                        for tb in range(4):
                            pp = PP[st["pi"] % len(PP)]
                            st["pi"] += 1
                            for kc in range(8):
                                s.op("pe", lambda e, kc=kc, pp=pp, w=w, which=which, tb=tb: e.matmul(
                                    pp[:, :], lhsT=w[:, kc, which * 128:(which + 1) * 128],
                                    rhs=HT[:, kc, tb * 512:(tb + 1) * 512], start=(kc == 0), stop=(kc == 7)),
                                    reads=w.all() + HT.b(tb), writes=pp.all())
                            if which == 0:
                                s.op("dve", lambda e, pp=pp, qt=qt, tb=tb: e.tensor_scalar(
                                    out=qt[:, tb * 512:(tb + 1) * 512], in0=pp[:, :], scalar1=0.125, scalar2=None,
                                    op0=ALU.mult),
                                    reads=pp.all(), writes=qt.all())
                            else:
                                s.op("dve", lambda e, pp=pp, kt_=kt_, tb=tb: e.tensor_copy(
                                    out=kt_[:, tb * 512:(tb + 1) * 512], in_=pp[:, :]),
                                    reads=pp.all(), writes=kt_.all())
                            yield
                    for tg in range(4):
                        pp = PP[st["pi"] % len(PP)]
                        st["pi"] += 1
                        for tt in range(4):
                            tok = (tg * 4 + tt) * 128
                            for kc in range(8):
                                s.op("pe", lambda e, kc=kc, pp=pp, w=w, tt=tt, tok=tok: e.matmul(
                                    pp[:, tt * 128:(tt + 1) * 128], lhsT=HT[:, kc, tok:tok + 128],
                                    rhs=w[:, kc, 256:384], start=(kc == 0), stop=(kc == 7)),
                                    reads=w.all() + HT.b(tg), writes=pp.all())
                        s.op("dve", lambda e, pp=pp, v=v, tg=tg: e.tensor_copy(
                            out=v[:, tg * 4:(tg + 1) * 4, :], in_=pp[:, :].rearrange("p (a b) -> p a b", a=4)),
                            reads=pp.all(), writes=v.all())
                        yield

                def post_hp(hp):
                    otok = OTOK[hp % 2]
                    for tg in range(4):
                        pp = PP[st["pi"] % len(PP)]
                        st["pi"] += 1
                        for tt in range(4):
                            s.op("pe", lambda e, pp=pp, tt=tt, tg=tg, otok=otok: e.matmul(
                                pp[:, tt * 128:(tt + 1) * 128], lhsT=otok[:, tg * 4 + tt, :], rhs=C["ident"][:, :],
                                start=True, stop=True),
                                reads=otok.all() + C["ident"].all(), writes=pp.all())
                        s.op("dve", lambda e, pp=pp, tg=tg, hp=hp: e.tensor_copy(
                            out=OT[:, hp, tg * 512:(tg + 1) * 512], in_=pp[:, :]),
                            reads=pp.all(), writes=OT.all())

                units = []
                for hp in range(8):
                    for g in range(4):
                        for kt in range(4 * g + 3, -1, -1):
                            for hh in range(2):
                                units.append((hp, hh, g, kt))
                n = len(units)
                NPZ, NSP, NATT, NPO, NE = 5, 5, 3, 2, 3

                def u_(i):
                    hp, hh, g, kt = units[i]
                    d = kt - 4 * g
                    return hp, hh, g, kt, d, slice(hh * 64, (hh + 1) * 64)

                def c0_(i):
                    hp, hh, g, kt = units[i]
                    return max(kt - 4 * g, 0) * 128

                def s0_qk(i):
                    hp, hh, g, kt, d, hs = u_(i)
                    if hh == 0 and g == 0 and kt == 3:
                        if hp == 0:
                            bg.add(pre_hp(0))
                        bg.drain()
                    if hh == 0 and g == 0 and kt == 0 and hp + 1 < 8:
                        bg.add(pre_hp(hp + 1), 5)
                    pz, qt, kt_ = PZ[i % NPZ], QT[hp % 2], KT[hp % 2]
                    q0 = g * 512
                    c0 = c0_(i)
                    s.op("pe", lambda e: e.matmul(pz[:, c0:512], lhsT=kt_[hs, kt * 128:(kt + 1) * 128],
                                                  rhs=qt[hs, q0 + c0:q0 + 512], start=True, stop=(d < 0)),
                         reads=kt_.all() + qt.all(), writes=pz.all())
                    if d >= 0:
                        s.op("pe", lambda e: e.matmul(pz[:, c0:c0 + 128], lhsT=C["ident"][:, :], rhs=NEGM[:, d, c0:c0 + 128],
                                                      start=False, stop=True),
                             reads=C["ident"].all() + NEGM.all(), writes=pz.all())

                def s1_exp(i):
                    pz, e_ = PZ[i % NPZ], E[i % NE]
                    c0 = c0_(i)
                    s.op("act", lambda e: e.activation(out=e_[:, c0:512], in_=pz[:, c0:512], func=AF.Exp),
                         reads=pz.all(), writes=e_.all())

                def s2_ln(i):
                    e_, sp = E[i % NE], SP[i % NSP]
                    c0 = c0_(i)
                    s.op("act", lambda e: e.activation(out=sp[:, c0:512], in_=e_[:, c0:512], func=AF.Ln,
                                                       bias=C["one_f"][:, 0:1]),
                         reads=e_.all() + C["one_f"].all(), writes=sp.all())

                def s3_tri(i):
                    pz, sp = PZ[i % NPZ], SP[i % NSP]
                    c0 = c0_(i)
                    s.op("pe", lambda e: e.matmul(pz[:, c0:512], lhsT=C["ntri"][:, :], rhs=sp[:, c0:512], start=False,
                                                  stop=True, skip_group_check=True),
                         reads=sp.all() + C["ntri"].all(), writes=pz.all())

                def s4_att(i):
                    pz, att = PZ[i % NPZ], ATT[i % NATT]
                    c0 = c0_(i)
                    s.op("act", lambda e: e.activation(out=att[:, c0:512], in_=pz[:, c0:512], func=AF.Exp),
                         reads=pz.all(), writes=att.all())

                def s5_av(i):
                    hp, hh, g, kt, d, hs = u_(i)
                    qlo = max(d, 0)
                    sp, att, po, v = SP[i % NSP], ATT[i % NATT], PO[i % NPO], V[hp % 2]
                    for qi in range(qlo, 4):
                        s.op("pe", lambda e, qi=qi: e.matmul(
                            po[:, qi * 64:(qi + 1) * 64], lhsT=att[:, qi * 128:(qi + 1) * 128],
                            rhs=v[:, kt, hs], start=True, stop=True),
                            reads=att.all() + v.all(), writes=po.all())
                        s.op("pe", lambda e, qi=qi: e.matmul(
                            po[:, 256 + qi:257 + qi], lhsT=sp[:, qi * 128:(qi + 1) * 128],
                            rhs=C["ones_col"][:, 0:1], start=True, stop=True),
                            reads=sp.all() + C["ones_col"].all(), writes=po.all())

                def s6_acc(i):
                    hp, hh, g, kt, d, hs = u_(i)
                    qlo = max(d, 0)
                    span = hh
                    po = PO[i % NPO]
                    oacc, cacc, fs = OACC[span % 2], CACC[span % 2], FS[i % 4]
                    otok = OTOK[hp % 2]
                    if kt != 4 * g + 3:
                        s.op("act", lambda e: e.activation(out=fs[:, :], in_=cacc[:, :], func=AF.Exp, scale=-1.0),
                             reads=cacc.all(), writes=fs.all())
                        s.op("dve", lambda e: e.tensor_tensor(
                            out=cacc[:, qlo:4], in0=cacc[:, qlo:4], in1=po[:, 256 + qlo:260], op=ALU.add),
                            reads=po.all() + cacc.all(), writes=cacc.all())
                        for qi in range(qlo, 4):
                            s.op("dve", lambda e, qi=qi: e.scalar_tensor_tensor(
                                out=oacc[:, qi, :], in0=po[:, qi * 64:(qi + 1) * 64], scalar=fs[:, qi:qi + 1],
                                in1=oacc[:, qi, :], op0=ALU.mult, op1=ALU.add),
                                reads=po.all() + fs.all() + oacc.all(), writes=oacc.all())
                    else:
                        if qlo > 0:
                            s.op("dve", lambda e: e.memset(oacc[:, 0:qlo, :], 0.0), writes=oacc.all())
                            s.op("dve", lambda e: e.memset(cacc[:, 0:qlo], 0.0), writes=cacc.all())
                        s.op("dve", lambda e: e.tensor_copy(
                            out=oacc[:, qlo:4, :], in_=po[:, qlo * 64:256].rearrange("p (a b) -> p a b", b=64)),
                            reads=po.all(), writes=oacc.all())
                        s.op("dve", lambda e: e.tensor_copy(out=cacc[:, qlo:4], in_=po[:, 256 + qlo:260]),
                             reads=po.all(), writes=cacc.all())
                    if kt == 0:
                        s.op("dve", lambda e: e.tensor_copy(out=otok[:, 4 * g:4 * g + 4, hs], in_=oacc[:, :, :]),
                             reads=oacc.all(), writes=otok.all())
                        if hh == 1 and g == 3:
                            post_hp(hp)

                stages = ((0, s0_qk), (1, s1_exp), (2, s2_ln), (3, s3_tri), (4, s4_att), (5, s5_av), (6, s6_acc))
                for i in range(n + 6):
                    for lag, fn in stages:
                        if 0 <= i - lag < n:
                            fn(i - lag)
                    bg.tick()
        if "dbg_OT" in C:
            s.dma("sp", C["dbg_OT"][:, :, :], OT[:, :, :], reads=OT.all(), writes=C["dbg_OT"].all())
        outproj_postnorm(k, C, XT, PS, OT, wo_d, gi_post, next_gi)


AX = mybir.AxisListType


class Ref:
    __slots__ = ("ap", "bufs")

    def __init__(self, ap, bufs):
        self.ap = ap
        self.bufs = bufs


class _RefMaker:
    def __init__(self, t):
        self.t = t

    def __getitem__(self, key):
        return Ref(self.t.t[key], self.t.all())


def rf(t):
    return _RefMaker(t)


def _b(*refs):
    out = []
    for r in refs:
        if isinstance(r, Ref):
            out += r.bufs
    return out


def _a(x):
    return x.ap if isinstance(x, Ref) else x


def e_tt(s, eng, out, a, b, op):
    return s.op(eng, lambda E: E.tensor_tensor(out=out.ap, in0=a.ap, in1=b.ap, op=op), reads=_b(a, b), writes=out.bufs)


def e_ts(s, eng, out, a, s1, s2, op0, op1=None):
    if op1 is None:
        return s.op(eng, lambda E: E.tensor_scalar(out=out.ap, in0=a.ap, scalar1=_a(s1), scalar2=None, op0=op0),
                    reads=_b(a, s1), writes=out.bufs)
    return s.op(eng, lambda E: E.tensor_scalar(out=out.ap, in0=a.ap, scalar1=_a(s1), scalar2=_a(s2), op0=op0, op1=op1),
                reads=_b(a, s1, s2), writes=out.bufs)


def e_stt(s, eng, out, a, sc, b, op0, op1):
    return s.op(eng, lambda E: E.scalar_tensor_tensor(out=out.ap, in0=a.ap, scalar=_a(sc), in1=b.ap, op0=op0, op1=op1),
                reads=_b(a, sc, b), writes=out.bufs)


def e_act(s, out, a, func, bias=None, scale=None):
    kw = {}
    if bias is not None:
        kw["bias"] = _a(bias)
    if scale is not None:
        kw["scale"] = _a(scale)
    return s.op("act", lambda E: E.activation(out=out.ap, in_=a.ap, func=func, **kw), reads=_b(a, bias, scale),
                writes=out.bufs)


def e_mm(s, out, lhsT, rhs, start=True, stop=True):
    return s.op("pe", lambda E: E.matmul(out.ap, lhsT=lhsT.ap, rhs=rhs.ap, start=start, stop=stop),
                reads=_b(lhsT, rhs), writes=out.bufs)


def e_copy(s, eng, out, a):
    if eng == "act":
        return e_act(s, out, a, AF.Copy)
    return s.op(eng, lambda E: E.tensor_copy(out=out.ap, in_=a.ap), reads=_b(a), writes=out.bufs)


def e_memset(s, eng, out, val):
    return s.op(eng, lambda E: E.memset(out.ap, val), writes=out.bufs)


GN_EPS = 64e-5
STAGGER = 3
NEG_EXP_HALF = -0.6065306597126334


def mixer0_stage(k, C, XT, PS, D, gi_pre, gi_post, next_gi=None):
    s = k.s
    with k.phase() as ph:
        OT = ph.sb("OT", [128, 8, SEQ], BF16, parts=1)
        with k.phase() as pab:
            HT = C["HTG"]
            prenorm_to_HT(k, C, pab, XT, HT, PS, gi_pre)
            with k.phase() as pl:
                rglru_part(k, C, pl, HT, OT, PS, D)
            with k.phase() as pr:
                rwkv_part(k, C, pr, HT, OT, PS, D, XT)
        if "dbg_OT" in D:
            s.dma("sp", D["dbg_OT"][:, :, :], OT[:, :, :], reads=OT.all(), writes=D["dbg_OT"].all())
        outproj_postnorm(k, C, XT, PS, OT, D["l0_w_out"], gi_post, next_gi)


def rglru_part(k, C, p, HT, OT, PS, D):
    s = k.s
    PL = p.sb("PL", [128, 4, 8], F32)
    s.dma("sp", PL[:, :, :], D["l0_pl"][:, :].rearrange("p (c n) -> p c n", c=4), writes=PL.all())
    C1 = p.sb("C1", [128, 4], F32)
    e_act(s, rf(C1)[:, :], rf(PL)[:, :, 7], AF.Exp, scale=-1.0)
    e_act(s, rf(C1)[:, :], rf(C1)[:, :], AF.Ln, bias=rf(C["one_f"])[:, 0:1])
    e_ts(s, "dve", rf(C1)[:, :], rf(C1)[:, :], -8.0, None, ALU.mult)
    GAW = p.sb("GAW", [128, 4, 128], BF16)
    GXW = p.sb("GXW", [128, 4, 128], BF16)
    e_memset(s, "dve", rf(GAW)[:, :, :], 0.0)
    e_memset(s, "dve", rf(GXW)[:, :, :], 0.0)
    for n in range(8):
        ps_ = slice((n % 2) * 64, (n % 2) * 64 + 64)
        s.dma("pool", GAW[ps_, n // 2, ps_], D["l0_gate_a_w"][n], writes=GAW.all())
        s.dma("pool", GXW[ps_, n // 2, ps_], D["l0_gate_x_w"][n], writes=GXW.all())
    W = [p.sb(f"WL{i}", [128, 8, 256], BF16) for i in range(2)]
    XBs = [p.sb(f"XB{i}", [128, 515], F32) for i in range(2)]
    HHs = [[p.sb(f"HH{j}_{i}", [128, 512], F32) for i in range(2)] for j in range(2)]
    ts_ = [{n: p.sb(f"{n}{j}", [128, 512], F32) for n in ("GB", "XC", "R", "IG", "A", "U", "T1", "T2")} for j in range(2)]
    XCbs = [p.sb(f"XCb{j}", [128, 512], BF16) for j in range(2)]

    def unit(c, tb, j):
        w = W[j]
        XB, t_, XCb = XBs[j], ts_[j], XCbs[j]
        col = lambda n: rf(PL)[:, c, n:n + 1]
        tok = tb * 512
        px, pg = PS[2 * j], PS[2 * j + 1]
        for kc in range(8):
            e_mm(s, rf(px)[:, :], rf(w)[:, kc, 0:128], Ref(HT[:, kc, tok:tok + 512], HT.b(tb)), kc == 0, kc == 7)
        for kc in range(8):
            e_mm(s, rf(pg)[:, :], rf(w)[:, kc, 128:256], Ref(HT[:, kc, tok:tok + 512], HT.b(tb)), kc == 0, kc == 7)
        if tb == 0:
            e_memset(s, "dve", rf(XB)[:, 0:3], 0.0)
        else:
            e_copy(s, "dve", rf(XB)[:, 0:3], rf(XB)[:, 512:515])
        yield
        e_copy(s, "act", rf(XB)[:, 3:515], rf(px)[:, :])
        e_copy(s, "act", rf(t_["GB"])[:, :], rf(pg)[:, :])
        yield
        XC = t_["XC"]
        e_ts(s, "dve", rf(XC)[:, :], rf(XB)[:, 3:515], col(3), col(4), ALU.mult, ALU.add)
        for i in range(3):
            e_stt(s, "dve", rf(XC)[:, :], rf(XB)[:, i:i + 512], col(i), rf(XC)[:, :], ALU.mult, ALU.add)
        GB, T2 = t_["GB"], t_["T2"]
        e_act(s, rf(T2)[:, :], rf(GB)[:, :], AF.Gelu_apprx_tanh)
        yield
        e_copy(s, "act", rf(XCb)[:, :], rf(XC)[:, :])
        yield
        pr_, pig = PS[4 + 2 * j], PS[5 + 2 * j]
        e_mm(s, rf(pr_)[:, :], rf(GAW)[:, c, :], rf(XCb)[:, :])
        e_mm(s, rf(pig)[:, :], rf(GXW)[:, c, :], rf(XCb)[:, :])
        yield
        e_act(s, rf(t_["R"])[:, :], rf(pr_)[:, :], AF.Sigmoid, bias=col(5))
        e_act(s, rf(t_["IG"])[:, :], rf(pig)[:, :], AF.Sigmoid, bias=col(6))
        yield
        A = t_["A"]
        e_act(s, rf(A)[:, :], rf(t_["R"])[:, :], AF.Exp, scale=rf(C1)[:, c:c + 1])
        T1, U = t_["T1"], t_["U"]
        e_tt(s, "dve", rf(U)[:, :], rf(t_["IG"])[:, :], rf(XC)[:, :], ALU.mult)
        yield
        e_tt(s, "dve", rf(T1)[:, :], rf(A)[:, :], rf(A)[:, :], ALU.mult)
        yield
        e_ts(s, "dve", rf(T1)[:, :], rf(T1)[:, :], -1.0, 1.0, ALU.mult, ALU.add)
        yield
        e_act(s, rf(T1)[:, :], rf(T1)[:, :], AF.Sqrt)
        yield
        e_tt(s, "dve", rf(U)[:, :], rf(U)[:, :], rf(T1)[:, :], ALU.mult)
        yield
        H = HHs[j][tb % 2]
        Hp = HHs[j][(tb + 1) % 2]
        init = 0.0 if tb == 0 else Hp[:, 511:512]
        s.op("dve", lambda E: E.tensor_tensor_scan(
            out=H[:, :], data0=A[:, :], data1=U[:, :], initial=init, op0=ALU.mult, op1=ALU.add),
            reads=A.all() + U.all() + (Hp.all() if tb else []), writes=H.all())
        yield
        s.op("dve", lambda E: E.tensor_tensor(
            out=OT[:, 4 + c, tok:tok + 512], in0=H[:, :], in1=T2[:, :], op=ALU.mult),
            reads=H.all() + T2.all(), writes=OT.all())

    for cp in range(2):
        for j in range(2):
            c = 2 * cp + j
            s.dma("pool", W[j][:, :, :], D["l0_w_lru"][c].rearrange("p (kc n) -> p kc n", kc=8), writes=W[j].all())
        for tb in range(4):
            gens = [unit(2 * cp + j, tb, j) for j in range(2)]
            alive = [True, True]
            while any(alive):
                for j in range(2):
                    if alive[j]:
                        try:
                            next(gens[j])
                        except StopIteration:
                            alive[j] = False


def rwkv_part(k, C, p, HT, OT, PS, D, XT):
    s = k.s
    spill = D["xt_spill"]
    for kc in range(8):
        s.dma("sp", spill[:, kc, :], XT[:, kc, :], reads=XT.all(), writes=spill.all())
    PH = p.sb("PH", [128, 4, 8], F32)
    s.dma("sp", PH[:, :, :], D["l0_ph"][:, :].rearrange("p (h n) -> p h n", h=4), writes=PH.all())
    OM = p.sb("OM", [128, 4, 4], F32)
    e_ts(s, "dve", rf(OM)[:, :, 0:3], rf(PH)[:, :, 0:3], -1.0, 1.0, ALU.mult, ALU.add)
    e_ts(s, "dve", rf(OM)[:, :, 3:4], rf(PH)[:, :, 6:7], -1.0, 1.0, ALU.mult, ALU.add)
    RKb = p.sb("RKb", [128, 4], BF16)
    e_copy(s, "dve", rf(RKb)[:, :], rf(PH)[:, :, 7])
    MUL = p.sb("MUL", [128, 3], F32)
    s.dma("sp", MUL[:, :], D["l0_mul"][:, :], writes=MUL.all())
    OML = p.sb("OML", [128, 3], F32)
    e_ts(s, "dve", rf(OML)[:, :], rf(MUL)[:, :], -1.0, 1.0, ALU.mult, ALU.add)
    LNGBs = [p.sb(f"LNGB{i}", [128, 2, 64], F32) for i in range(2)]
    W2 = p.sb("W2", [64, 512], BF16)
    A2 = p.sb("A2", [64, 512], BF16)
    G2 = p.sb("G2", [128, 512], BF16)
    s.dma("pool", W2[:, :], D["l0_w2"][:, :], writes=W2.all())
    s.dma("pool", A2[:, :], D["l0_a2"][:, :], writes=A2.all())
    s.dma("pool", G2[:, :], D["l0_g2"][:, :], writes=G2.all())
    ob = C["ones_bf"]
    BLK = p.sb("BLK", [128, 128], BF16)
    e_memset(s, "dve", rf(BLK)[:, :], 0.0)
    e_memset(s, "dve", rf(BLK)[0:64, 0:64], 1.0)
    e_memset(s, "dve", rf(BLK)[64:128, 64:128], 1.0)
    M512 = p.sb("M512", [128, 512], BF16)
    MUS = p.sb("MUS", [128, 512], BF16)
    MUI = p.sb("MUI", [128, 512], BF16)
    MLS = p.sb("MLS", [128, 512], BF16)
    ID8 = p.sb("ID8", [128, 512], BF16)
    for hh in range(2):
        hs = slice(hh * 64, hh * 64 + 64)
        for dst, pat, cmp_, cm in ((M512, [[0, 8], [1, 64]], ALU.is_gt, 0), (MUS, [[0, 8], [1, 64]], ALU.is_gt, -1),
                                   (MUI, [[0, 8], [1, 64]], ALU.is_ge, -1), (MLS, [[0, 8], [-1, 64]], ALU.is_gt, 1),
                                   (ID8, [[0, 8], [-1, 64]], ALU.is_equal, 1)):
            s.op("pool", lambda E, dst=dst, pat=pat, cmp_=cmp_, cm=cm, hs=hs: E.affine_select(
                out=dst[hs, :], in_=ob[hs, :], pattern=pat, compare_op=cmp_, fill=0.0, base=0, channel_multiplier=cm),
                reads=ob.all(), writes=dst.all())
    ident = C["ident"]

    TW = p.sb("TW", [64, SEQ], BF16)
    AL = p.sb("AL", [64, SEQ], BF16)
    SGL = p.sb("SGL", [128, SEQ], BF16)
    with k.phase() as p0:
        WLo = p0.sb("WLo", [128, 8, 256], BF16)
        s.dma("pool", WLo[:, :, :], D["l0_w_lora"][:, :].rearrange("p (kc n) -> p kc n", kc=8), writes=WLo.all())
        PAl = [p0.sb(f"PAl{i}", [128, 513], F32) for i in range(3)]
        TMPl = [p0.sb(f"TMPl{i}", [128, 512], F32) for i in range(3)]

        def lora_chain(which, c0, c1, npart, dst):
            PA, tmpl = PAl[which], TMPl[which]
            for tb in range(4):
                tok = tb * 512
                pp = PS[which * 2 + tb % 2]
                for kc in range(8):
                    e_mm(s, rf(pp)[0:npart, :], rf(WLo)[:, kc, c0:c1], Ref(HT[:, kc, tok:tok + 512], HT.b(tb)), kc == 0, kc == 7)
                if tb == 0:
                    e_memset(s, "dve", rf(PA)[0:npart, 0:1], 0.0)
                else:
                    e_copy(s, "dve", rf(PA)[0:npart, 0:1], rf(PA)[0:npart, 512:513])
                yield
                e_copy(s, "act", rf(PA)[0:npart, 1:513], rf(pp)[0:npart, :])
                yield
                e_act(s, rf(tmpl)[0:npart, :], rf(PA)[0:npart, 0:512], AF.Copy, scale=rf(MUL)[0:npart, which:which + 1])
                yield
                e_stt(s, "dve", rf(tmpl)[0:npart, :], rf(PA)[0:npart, 1:513], rf(OML)[0:npart, which:which + 1],
                      rf(tmpl)[0:npart, :], ALU.mult, ALU.add)
                yield
                if which == 0:
                    e_act(s, rf(dst)[:, tok:tok + 512], rf(tmpl)[0:64, :], AF.Tanh)
                elif which == 1:
                    e_copy(s, "act", rf(dst)[:, tok:tok + 512], rf(tmpl)[0:64, :])
                else:
                    e_act(s, rf(dst)[:, tok:tok + 512], rf(tmpl)[:, :], AF.Sigmoid)
                yield

        lbg = Bg()
        for which, (c0, c1, npart, dst) in enumerate(((0, 64, 64, TW), (64, 128, 64, AL), (128, 256, 128, SGL))):
            lbg.add(lora_chain(which, c0, c1, npart, dst), 1)
        lbg.drain()

    s.barrier()
    XTf = XT.t
    XTb = XT.t.bitcast(BF16)
    f32n = ("r", "k", "SIG", "A", "KKN", "KH", "CUM", "EC", "EX", "EN", "TMP")
    b16n = ("Rt", "At", "Bt", "Kt", "Bh", "Kh", "RK", "VT", "KK2")
    t64n = ("V64", "BH64", "KH64", "N", "Q", "N2", "Q2", "XA", "LAK", "ARB", "ARK")
    sets = []
    for S in range(2):
        B = {}
        if S == 0:
            B["WH"] = p.sb("WH", [128, 8, 384], BF16)
            B["PA"] = [p.sb(f"PA{i}", [128, 513], F32) for i in range(3)]
            F = {n: p.sb("f_" + n, [128, 512], F32) for n in f32n}
            Bf = {n: p.sb("b_" + n, [128, 512], BF16) for n in b16n}
            T64 = {n: p.sb("t_" + n, [128, 512], BF16) for n in t64n}
        else:
            B["WH"] = T("WH1", XTb[:, 7, 0:3072].rearrange("p (kc n) -> p kc n", kc=8))
            B["PA"] = [T(f"PA1_{i}", XTf[:, 3, i * 513:(i + 1) * 513]) for i in range(3)]
            F = {n: T("f1_" + n, XTf[:, i // 4, (i % 4) * 512:(i % 4) * 512 + 512]) for i, n in enumerate(f32n)}
            bl = list(b16n) + list(t64n)
            vb = {n: T("b1_" + n, XTb[:, 4 + i // 8, (i % 8) * 512:(i % 8) * 512 + 512]) for i, n in enumerate(bl)}
            Bf = {n: vb[n] for n in b16n}
            T64 = {n: vb[n] for n in t64n}
        F["EH"] = F["TMP"]
        F["BA"] = F["SIG"]
        T64["YA"] = T64["LAK"]
        B["F"], B["Bf"], B["T64"] = F, Bf, T64
        B["R0"], B["YF"], B["YQ"], B["GT"] = F["SIG"], F["EX"], F["EN"], F["KH"]
        B["ST"] = p.sb(f"ST{S}", [128, 8, 4], F32)
        B["RKS"] = p.sb(f"RKS{S}", [128, 8], F32)
        B["Pf"] = p.sb(f"Pf{S}", [128, 64], F32)
        B["Pb"] = p.sb(f"Pb{S}", [128, 64], BF16)
        B["RR"] = p.sb(f"RR{S}", [128, 64], BF16)
        B["UB"] = p.sb(f"UB{S}", [128, 64], BF16)
        B["LNGB"] = LNGBs[S]
        B["PY"] = PS[4 + S]
        B["PT1"] = PS[6 + S]
        sets.append(B)
    b3 = lambda r_: Ref(r_.ap.rearrange("p (a b) -> p a b", a=8), r_.bufs)
    HS = (slice(0, 64), slice(64, 128))
    st = {"sci": 0}

    def newps():
        st["sci"] += 1
        return PS[st["sci"] % 4]

    def mm2(out_t, col0, ncol, lhs_fn, rhs_fn, start=True, stop=True):
        for hs in HS:
            e_mm(s, rf(out_t)[hs, col0:col0 + ncol], lhs_fn(hs), rhs_fn(hs), start, stop)

    def unit(hp, gq, B):
        F, Bf, T64, PA, w = B["F"], B["Bf"], B["T64"], B["PA"], B["WH"]
        R0, YF, YQ, GT, ST, RKS = B["R0"], B["YF"], B["YQ"], B["GT"], B["ST"], B["RKS"]
        Pf, Pb, RR, UB, LNGB = B["Pf"], B["Pb"], B["RR"], B["UB"], B["LNGB"]
        hc = lambda n: rf(PH)[:, hp, n:n + 1]
        tok = gq * 512
        for which, nm in enumerate(("r", "k", "v")):
            pp = newps()
            for kc in range(8):
                e_mm(s, rf(pp)[:, :], rf(w)[:, kc, which * 128:(which + 1) * 128],
                     Ref(HT[:, kc, tok:tok + 512], HT.b(gq)), kc == 0, kc == 7)
            pa = PA[which]
            if gq == 0:
                e_memset(s, "dve", rf(pa)[:, 0:1], 0.0)
            else:
                e_copy(s, "dve", rf(pa)[:, 0:1], rf(pa)[:, 512:513])
            e_copy(s, "act", rf(pa)[:, 1:513], rf(pp)[:, :])
            yield
            tmp_ = rf(F["TMP"])[:, :] if which != 1 else rf(F["CUM"])[:, :]
            e_act(s, tmp_, rf(pa)[:, 0:512], AF.Copy, scale=hc(which))
            dst_ = rf(Bf["VT"])[:, :] if nm == "v" else rf(F[nm])[:, :]
            e_stt(s, "dve", dst_, rf(pa)[:, 1:513], rf(OM)[:, hp, which:which + 1], tmp_, ALU.mult, ALU.add)
        r_, k_ = rf(F["r"])[:, :], rf(F["k"])[:, :]
        pz = newps()
        e_mm(s, rf(pz)[:, :], rf(W2)[:, hp * 128:(hp + 1) * 128], rf(TW)[:, tok:tok + 512])
        e_act(s, rf(F["SIG"])[:, :], rf(pz)[:, :], AF.Sigmoid, bias=hc(3))
        pz2 = newps()
        e_mm(s, rf(pz2)[:, :], rf(A2)[:, hp * 128:(hp + 1) * 128], rf(AL)[:, tok:tok + 512])
        e_act(s, rf(F["A"])[:, :], rf(pz2)[:, :], AF.Sigmoid, bias=hc(4))
        yield
        e_ts(s, "dve", rf(F["KKN"])[:, :], k_, hc(5), None, ALU.mult)
        e_act(s, rf(Bf["KK2"])[:, :], rf(F["KKN"])[:, :], AF.Square)
        yield
        pz = newps()
        e_mm(s, rf(pz)[:, :], rf(BLK)[:, :], rf(Bf["KK2"])[:, :])
        e_act(s, rf(F["TMP"])[:, :], rf(pz)[:, :], AF.Sqrt)
        yield
        e_ts(s, "dve", rf(F["TMP"])[:, :], rf(F["TMP"])[:, :], 1e-12, None, ALU.max)
        s.op("dve", lambda E: E.reciprocal(out=F["TMP"][:, :], in_=F["TMP"][:, :]), reads=F["TMP"].all(),
             writes=F["TMP"].all())
        e_tt(s, "dve", rf(F["KKN"])[:, :], rf(F["KKN"])[:, :], rf(F["TMP"])[:, :], ALU.mult)
        e_act(s, rf(F["KH"])[:, :], rf(F["A"])[:, :], AF.Identity, bias=rf(OM)[:, hp, 3:4], scale=hc(6))
        e_tt(s, "dve", rf(F["KH"])[:, :], rf(F["KH"])[:, :], k_, ALU.mult)
        yield
        s.op("dve", lambda E: E.tensor_tensor_scan(out=F["CUM"][:, :], data0=M512[:, :], data1=F["SIG"][:, :],
                                                   initial=0.0, op0=ALU.mult, op1=ALU.add),
             reads=M512.all() + F["SIG"].all(), writes=F["CUM"].all())
        yield
        cum = rf(F["CUM"])[:, :]
        e_act(s, rf(F["EC"])[:, :], cum, AF.Exp, scale=NEG_EXP_HALF)
        e_tt(s, "dve", rf(F["EX"])[:, :], cum, rf(F["SIG"])[:, :], ALU.subtract)
        e_act(s, rf(F["EN"])[:, :], cum, AF.Exp, scale=-NEG_EXP_HALF)
        cum3 = b3(cum)
        cend = Ref(cum3.ap[:, :, 63:64].to_broadcast([128, 8, 64]), cum.bufs)
        e_tt(s, "dve", b3(rf(F["EH"])[:, :]), cend, cum3, ALU.subtract)
        yield
        e_act(s, rf(F["EX"])[:, :], rf(F["EX"])[:, :], AF.Exp, scale=NEG_EXP_HALF)
        e_act(s, rf(F["EH"])[:, :], rf(F["EH"])[:, :], AF.Exp, scale=NEG_EXP_HALF)
        e_tt(s, "dve", rf(Bf["Rt"])[:, :], r_, rf(F["EC"])[:, :], ALU.mult)
        e_tt(s, "dve", rf(Bf["RK"])[:, :], r_, rf(F["KH"])[:, :], ALU.mult)
        e_tt(s, "dve", rf(F["BA"])[:, :], rf(F["KKN"])[:, :], rf(F["A"])[:, :], ALU.mult)
        yield
        e_stt(s, "dve", rf(Bf["At"])[:, :], rf(F["KKN"])[:, :], -1.0, rf(F["EX"])[:, :], ALU.mult, ALU.mult)
        e_tt(s, "dve", rf(Bf["Bt"])[:, :], rf(F["BA"])[:, :], rf(F["EN"])[:, :], ALU.mult)
        e_tt(s, "dve", rf(Bf["Bh"])[:, :], rf(F["BA"])[:, :], rf(F["EH"])[:, :], ALU.mult)
        e_tt(s, "dve", rf(Bf["Kt"])[:, :], rf(F["KH"])[:, :], rf(F["EN"])[:, :], ALU.mult)
        e_tt(s, "dve", rf(Bf["Kh"])[:, :], rf(F["KH"])[:, :], rf(F["EH"])[:, :], ALU.mult)
        yield
        blk = lambda n, c8, hs: rf(Bf[n])[hs, c8 * 64:(c8 + 1) * 64]
        tb_ = lambda n, c8, hs: rf(T64[n])[hs, c8 * 64:(c8 + 1) * 64]
        idh = lambda hs: rf(ident)[hs, hs]
        for src, dst in (("VT", "V64"), ("Bh", "BH64"), ("Kh", "KH64")):
            pt = newps()
            for c8 in range(8):
                mm2(pt, c8 * 64, 64, lambda hs, c8=c8, src=src: blk(src, c8, hs), idh)
            e_copy(s, "act", rf(T64[dst])[:, :], rf(pt)[:, :])
            yield
        for lh, rh, mask, dst in (("Bt", "At", MUS, "N"), ("At", "Bt", MLS, "Q"), ("Kt", "At", MUS, "LAK"),
                                  ("Bt", "Rt", MUI, "ARB"), ("Kt", "Rt", MUI, "ARK")):
            pt = newps()
            for c8 in range(8):
                mm2(pt, c8 * 64, 64, lambda hs, c8=c8, lh=lh: blk(lh, c8, hs), lambda hs, c8=c8, rh=rh: blk(rh, c8, hs))
            e_tt(s, "dve", rf(T64[dst])[:, :], rf(pt)[:, :], rf(mask)[:, :], ALU.mult)
            yield
        e_tt(s, "dve", rf(T64["XA"])[:, :], rf(T64["N"])[:, :], rf(ID8)[:, :], ALU.add)
        Pn, Qn, Pn2, Qn2 = "N", "Q", "N2", "Q2"
        for lvl in range(1, 6):
            pq = newps()
            for c8 in range(8):
                mm2(pq, c8 * 64, 64, lambda hs, c8=c8, Pn=Pn: tb_(Pn, c8, hs), lambda hs, c8=c8, Qn=Qn: tb_(Qn, c8, hs))
            e_copy(s, "act", rf(T64[Qn2])[:, :], rf(pq)[:, :])
            if lvl < 5:
                pp_ = newps()
                for c8 in range(8):
                    mm2(pp_, c8 * 64, 64, lambda hs, c8=c8, Qn=Qn: tb_(Qn, c8, hs),
                        lambda hs, c8=c8, Pn=Pn: tb_(Pn, c8, hs))
                e_copy(s, "act", rf(T64[Pn2])[:, :], rf(pp_)[:, :])
            yield
            px = newps()
            for c8 in range(8):
                mm2(px, c8 * 64, 64, lambda hs, c8=c8, Qn2=Qn2: tb_(Qn2, c8, hs), lambda hs, c8=c8: tb_("XA", c8, hs))
            e_tt(s, "dve", rf(T64["XA"])[:, :], rf(T64["XA"])[:, :], rf(px)[:, :], ALU.add)
            yield
            Pn, Pn2 = Pn2, Pn
            Qn, Qn2 = Qn2, Qn
        pr0 = newps()
        for c8 in range(8):
            mm2(pr0, c8 * 64, 64, lambda hs, c8=c8: tb_("LAK", c8, hs), lambda hs, c8=c8: tb_("V64", c8, hs))
        e_copy(s, "act", rf(R0)[:, :], rf(pr0)[:, :])
        pg = newps()
        e_mm(s, rf(pg)[:, :], rf(G2)[:, hp * 128:(hp + 1) * 128], rf(SGL)[:, tok:tok + 512])
        e_copy(s, "act", rf(GT)[:, :], rf(pg)[:, :])
        yield
        PY, PT1 = B["PY"], B["PT1"]
        pbh = lambda hs: rf(Pb)[hs, :]
        ubh = lambda hs: rf(UB)[hs, :]
        for c8 in range(8):
            mm2(PT1, 0, 64, lambda hs: blk("At", c8, hs), pbh)
            mm2(PY, c8 * 64, 64, lambda hs: blk("Rt", c8, hs), pbh, True, False)
            e_tt(s, "dve", rf(RR)[:, :], rf(PT1)[:, 0:64], rf(R0)[:, c8 * 64:(c8 + 1) * 64], ALU.add)
            yield
            mm2(PT1, 64, 64, lambda hs: tb_("XA", c8, hs), lambda hs: rf(RR)[hs, :])
            e_copy(s, "act", rf(UB)[:, :], rf(PT1)[:, 64:128])
            yield
            mm2(PT1, 128, 64, lambda hs: tb_("KH64", c8, hs), lambda hs: tb_("V64", c8, hs), True, False)
            mm2(PT1, 128, 64, lambda hs: tb_("BH64", c8, hs), ubh, False, True)
            mm2(PY, c8 * 64, 64, lambda hs: tb_("ARB", c8, hs), ubh, False, False)
            mm2(PY, c8 * 64, 64, lambda hs: tb_("ARK", c8, hs), lambda hs: tb_("V64", c8, hs), False, True)
            e_stt(s, "dve", rf(Pf)[:, :], rf(Pf)[:, :], rf(F["EC"])[:, c8 * 64 + 63:c8 * 64 + 64], rf(PT1)[:, 128:192],
                  ALU.mult, ALU.add)
            e_copy(s, "act", rf(Pb)[:, :], rf(Pf)[:, :])
            yield
        e_copy(s, "act", rf(YF)[:, :], rf(PY)[:, :])
        yf3 = b3(rf(YF)[:, :])
        yq3 = b3(rf(YQ)[:, :])
        prk = newps()
        for c8 in range(8):
            mm2(prk, c8, 1, lambda hs: blk("RK", c8, hs), lambda hs: rf(RKb)[hs, hp:hp + 1])
        e_copy(s, "act", rf(RKS)[:, :], rf(prk)[:, 0:8])
        yield
        s.op("dve", lambda E: E.tensor_reduce(out=ST[:, :, 0], in_=YF[:, :].rearrange("p (a b) -> p a b", a=8),
                                              axis=AX.X, op=ALU.add), reads=YF.all(), writes=ST.all())
        e_act(s, rf(YQ)[:, :], rf(YF)[:, :], AF.Square)
        yield
        s.op("dve", lambda E: E.tensor_reduce(out=ST[:, :, 1], in_=YQ[:, :].rearrange("p (a b) -> p a b", a=8),
                                              axis=AX.X, op=ALU.add), reads=YQ.all(), writes=ST.all())
        e_ts(s, "dve", rf(ST)[:, :, 2], rf(ST)[:, :, 0], 1.0 / 64, None, ALU.mult)
        e_tt(s, "dve", rf(ST)[:, :, 0], rf(ST)[:, :, 2], rf(ST)[:, :, 2], ALU.mult)
        e_stt(s, "dve", rf(ST)[:, :, 1], rf(ST)[:, :, 1], 1.0 / 64, rf(ST)[:, :, 0], ALU.mult, ALU.subtract)
        e_ts(s, "dve", rf(ST)[:, :, 1], rf(ST)[:, :, 1], GN_EPS, None, ALU.add)
        e_act(s, rf(ST)[:, :, 3], rf(ST)[:, :, 1], AF.Sqrt)
        yield
        s.op("dve", lambda E: E.reciprocal(out=ST[:, :, 3], in_=ST[:, :, 3]), reads=ST.all(), writes=ST.all())
        mean_b = Ref(ST[:, :, 2:3].to_broadcast([128, 8, 64]), ST.all())
        rstd_b = Ref(ST[:, :, 3:4].to_broadcast([128, 8, 64]), ST.all())
        e_tt(s, "dve", yf3, yf3, mean_b, ALU.subtract)
        e_tt(s, "dve", yf3, yf3, rstd_b, ALU.mult)
        rks_b = Ref(RKS[:, :].rearrange("p (a b) -> p a b", b=1).to_broadcast([128, 8, 64]), RKS.all())
        e_tt(s, "dve", yq3, b3(rf(T64["V64"])[:, :]), rks_b, ALU.mult)
        lng = Ref(LNGB[:, 0:1, :].to_broadcast([128, 8, 64]), LNGB.all())
        lnb = Ref(LNGB[:, 1:2, :].to_broadcast([128, 8, 64]), LNGB.all())
        yield
        e_tt(s, "dve", yf3, yf3, lng, ALU.mult)
        e_tt(s, "dve", yf3, yf3, lnb, ALU.add)
        yield
        e_tt(s, "dve", rf(T64["YA"])[:, :], rf(YF)[:, :], rf(YQ)[:, :], ALU.add)
        yield
        pt = newps()
        for c8 in range(8):
            mm2(pt, c8 * 64, 64, lambda hs: tb_("YA", c8, hs), idh)
        s.op("dve", lambda E: E.tensor_tensor(
            out=OT[:, hp, tok:tok + 512], in0=pt[:, :], in1=GT[:, :], op=ALU.mult),
            reads=pt.all() + GT.all(), writes=OT.all())
        yield

    def stream(S, hps):
        B = sets[S]
        for hp in hps:
            w = B["WH"]
            s.dma("pool", w[:, :, :], D["l0_w_hp"][hp].rearrange("p (kc n) -> p kc n", kc=8), writes=w.all())
            LNGB = B["LNGB"]
            for hh in range(2):
                h = 2 * hp + hh
                s.dma("sp", LNGB[HS[hh], 0, :], D["l0_lnx_g"][h * 64:(h + 1) * 64].partition_broadcast(64),
                      writes=LNGB.all())
                s.dma("sp", LNGB[HS[hh], 1, :], D["l0_lnx_b"][h * 64:(h + 1) * 64].partition_broadcast(64),
                      writes=LNGB.all())
            e_memset(s, "dve", rf(B["Pf"])[:, :], 0.0)
            e_memset(s, "dve", rf(B["Pb"])[:, :], 0.0)
            yield
            for gq in range(4):
                yield from unit(hp, gq, B)

    gens = [stream(0, (0, 2)), stream(1, (1, 3))]
    alive = [True, True]
    first = True
    while any(alive):
        for S in range(2):
            if alive[S]:
                try:
                    next(gens[S])
                except StopIteration:
                    alive[S] = False
            if first and S == 0:
                for _ in range(STAGGER):
                    next(gens[0])
                first = False
    s.barrier()
    for kc in range(8):
        s.dma("sp", XT[:, kc, :], spill[:, kc, :], reads=spill.all(), writes=XT.all())


GAIN_NAMES = ["l0_ffn1_pre_g", "l0_ffn1_post_g", "l0_mix_pre_g", "l0_mix_post_g", "l0_ffn2_pre_g", "l0_ffn2_post_g",
              "l1_ffn1_pre_g", "l1_ffn1_post_g", "l1_mix_pre_g", "l1_mix_post_g", "l1_ffn2_pre_g", "l1_ffn2_post_g"]
HALF_GAINS = [1, 5, 7, 11]


def build_program(stages=("f01", "m0", "f02f11", "m1", "f12"), dbg=False):
    k = KB()
    nc = k.nc
    s = k.s
    xT_d = k.dram_in("xT", [DM, SEQ])
    gains_d = k.dram_in("gains", [128, 12 * 8])
    ffn_d = {}
    for nm in ("l0_ffn1", "l0_ffn2", "l1_ffn1", "l1_ffn2"):
        ffn_d[nm] = (k.dram_in(nm + "_w_in", [NJ, 128, 2048]), k.dram_in(nm + "_w_out", [2, 8, 128, 11 * 128]))
    D = {}
    for nm, shp in (("l0_pl", [128, 32]), ("l0_ph", [128, 32]), ("l0_mul", [128, 3]), ("l0_lnx_g", [512]),
                    ("l0_lnx_b", [512]), ("l0_w2", [64, 512]), ("l0_a2", [64, 512]), ("l0_g2", [128, 512]),
                    ("l0_gate_a_w", [8, 64, 64]), ("l0_gate_x_w", [8, 64, 64]), ("l0_w_lru", [4, 128, 8 * 256]),
                    ("l0_w_lora", [128, 8 * 256]), ("l0_w_hp", [4, 128, 8 * 384]), ("l0_w_out", [DM, DM])):
        D[nm] = k.dram_in(nm, shp)
    D["xt_spill"] = T("xt_spill", nc.dram_tensor("xt_spill", [128, 8, SEQ], F32, kind="Internal").ap())
    if dbg:
        D["dbg_OT"] = k.dram_out("dbg_OT", [128, 8, SEQ], BF16)
    wqkv_d = k.dram_in("l1_w_qkv", [8, 128, 8 * 384])
    l1_wo_d = k.dram_in("l1_w_out", [DM, DM])
    outT_d = k.dram_out("outT", [DM, SEQ])

    with k.es:
        XT = k.sb("XT", [128, 8, SEQ], F32, parts=4)
        C = {}
        C["gains"] = k.sb("gains", [128, 12, 8], F32)
        C["ones_m"] = k.sb("ones_m", [128, 128], BF16)
        PS = [k.ps(f"ps{i}", [128, 512]) for i in range(8)]
        C["HTG"] = k.sb("HTG", [128, 8, SEQ], BF16, parts=4)
        C["ht_ready"] = None

        s.op("dve", lambda e: e.memset(C["ones_m"][:, :], 1.0 / DM), writes=C["ones_m"].all())
        C["one_f"] = k.sb("one_f", [128, 1], F32)
        C["ones_col"] = k.sb("ones_col", [128, 1], BF16)
        C["ones_bf"] = k.sb("ones_bf", [128, 512], BF16)
        C["ident"] = k.sb("ident", [128, 128], BF16)
        C["ntri"] = k.sb("ntri", [128, 128], BF16)
        s.op("dve", lambda e: e.memset(C["one_f"][:, :], 1.0), writes=C["one_f"].all())
        s.op("dve", lambda e: e.memset(C["ones_col"][:, :], 1.0), writes=C["ones_col"].all())
        s.op("dve", lambda e: e.memset(C["ones_bf"][:, :], 1.0), writes=C["ones_bf"].all())
        s.op("pool", lambda e: e.affine_select(out=C["ident"][:, :], in_=C["ones_bf"][:, 0:128], pattern=[[-1, 128]],
                                               compare_op=ALU.is_equal, fill=0.0, base=0, channel_multiplier=1),
             reads=C["ones_bf"].all(), writes=C["ident"].all())
        s.op("pool", lambda e: e.affine_select(out=C["ntri"][:, :], in_=C["ones_bf"][:, 0:128], pattern=[[-1, 128]],
                                               compare_op=ALU.is_ge, fill=0.0, base=0, channel_multiplier=1),
             reads=C["ones_bf"].all(), writes=C["ntri"].all())
        s.op("dve", lambda e: e.tensor_scalar(out=C["ntri"][:, :], in0=C["ntri"][:, :], scalar1=-1.0, scalar2=None,
                                              op0=ALU.mult),
             reads=C["ntri"].all(), writes=C["ntri"].all())
        C["eps"] = k.sb("eps", [128, 1], F32)
        s.op("dve", lambda e: e.memset(C["eps"][:, :], NORM_EPS), writes=C["eps"].all())
        s.dma("sp", C["gains"][:, :, :], gains_d[:, :].rearrange("p (n c) -> p n c", n=12), writes=C["gains"].all())
        for gi in HALF_GAINS:
            s.op("dve", lambda e, gi=gi: e.tensor_scalar(out=C["gains"][:, gi, :], in0=C["gains"][:, gi, :],
                                                         scalar1=0.5, scalar2=None, op0=ALU.mult),
                 reads=C["gains"].all(), writes=C["gains"].all())
        for tb in range(4):
            for kc in range(8):
                s.dma("sp", XT[:, kc, tb * 512:(tb + 1) * 512], xT_d[kc * 128:(kc + 1) * 128, tb * 512:(tb + 1) * 512],
                      writes=XT.b(tb))

        PRE_GI = {"f01": 0, "m0": 2, "f02": 4, "f02f11": 4, "f11": 6, "m1": 8, "f12": 10}
        for si, st in enumerate(stages):
            nxt = PRE_GI[stages[si + 1]] if si + 1 < len(stages) else None
            if st == "f01":
                ffn_stage(k, C, XT, PS, [(*ffn_d["l0_ffn1"], 0, 1)], nxt)
            elif st == "f02":
                ffn_stage(k, C, XT, PS, [(*ffn_d["l0_ffn2"], 4, 5)], nxt)
            elif st == "f02f11":
                ffn_stage(k, C, XT, PS, [(*ffn_d["l0_ffn2"], 4, 5), (*ffn_d["l1_ffn1"], 6, 7)], nxt)
            elif st == "f11":
                ffn_stage(k, C, XT, PS, [(*ffn_d["l1_ffn1"], 6, 7)], nxt)
            elif st == "m0":
                mixer0_stage(k, C, XT, PS, D, 2, 3, nxt)
            elif st == "m1":
                if dbg:
                    C["dbg_OT"] = D["dbg_OT"]
                attn_stage(k, C, XT, PS, wqkv_d, l1_wo_d, 8, 9, nxt)
            elif st == "f12":
                ffn_stage(k, C, XT, PS, [(*ffn_d["l1_ffn2"], 10, 11)], nxt)

        for tb in range(4):
            for kc in range(8):
                s.dma("sp", outT_d[kc * 128:(kc + 1) * 128, tb * 512:(tb + 1) * 512], XT[:, kc, tb * 512:(tb + 1) * 512],
                      reads=XT.b(tb), writes=outT_d.all())
        s.barrier(engines=["sp"])
    return nc


def _col(v):
    return np.ascontiguousarray(np.asarray(v, np.float32).reshape(8, 128).T)


def prep_shared(inp):
    d = {}
    d["gains"] = np.ascontiguousarray(np.concatenate([_col(inp[n]) for n in GAIN_NAMES], axis=1))
    for nm in ("l0_ffn1", "l0_ffn2", "l1_ffn1", "l1_ffn2"):
        w_in = np.asarray(inp[nm + "_w_in"], np.float32)
        w_out = np.asarray(inp[nm + "_w_out"], np.float32)
        g = w_in[:, :DFF].reshape(8, 128, NJ, 128)
        u = w_in[:, DFF:].reshape(8, 128, NJ, 128)
        gu = np.concatenate([g, u], axis=3)
        d[nm + "_w_in"] = np.ascontiguousarray(gu.transpose(2, 1, 0, 3).reshape(NJ, 128, 2048))
        wo = w_out.reshape(2, 11, 128, 8, 128)
        d[nm + "_w_out"] = np.ascontiguousarray(wo.transpose(0, 3, 2, 1, 4).reshape(2, 8, 128, 11 * 128))
    f = lambda n: np.asarray(inp[n], np.float32)
    cw = f("l0_conv_w")
    pl = np.stack([cw[0], cw[1], cw[2], cw[3], f("l0_conv_b"), f("l0_gate_a_b"), f("l0_gate_x_b"), f("l0_lambda")], axis=1)
    d["l0_pl"] = np.ascontiguousarray(pl.reshape(4, 128, 8).transpose(1, 0, 2).reshape(128, 32))
    mu = f("l0_mu")
    ph = np.stack([mu[0:512], mu[512:1024], mu[1024:1536], f("l0_w0"), f("l0_a0"), f("l0_k_k"), f("l0_k_a"),
                   f("l0_r_k").reshape(512)], axis=1)
    d["l0_ph"] = np.ascontiguousarray(ph.reshape(4, 128, 8).transpose(1, 0, 2).reshape(128, 32))
    mul = np.zeros((128, 3), np.float32)
    mul[0:64, 0] = mu[1536:1600]
    mul[0:64, 1] = mu[1600:1664]
    mul[:, 2] = mu[1664:1792]
    d["l0_mul"] = mul
    for n in ("l0_lnx_g", "l0_lnx_b", "l0_w2", "l0_a2", "l0_g2", "l0_gate_a_w", "l0_gate_x_w", "l0_w_out"):
        d[n] = np.ascontiguousarray(f(n))
    wi = f("l0_w_in").reshape(8, 128, 2816)
    lru = np.concatenate([wi[:, :, 1792:2304].reshape(8, 128, 4, 128), wi[:, :, 2304:2816].reshape(8, 128, 4, 128)], axis=3)
    d["l0_w_lru"] = np.ascontiguousarray(lru.transpose(2, 1, 0, 3).reshape(4, 128, 8 * 256))
    d["l0_w_lora"] = np.ascontiguousarray(wi[:, :, 1536:1792].transpose(1, 0, 2).reshape(128, 8 * 256))
    hd = np.stack([wi[:, :, 0:512].reshape(8, 128, 4, 128), wi[:, :, 512:1024].reshape(8, 128, 4, 128),
                   wi[:, :, 1024:1536].reshape(8, 128, 4, 128)], axis=3)
    d["l0_w_hp"] = np.ascontiguousarray(hd.transpose(2, 1, 0, 3, 4).reshape(4, 128, 8 * 384))
    wq = np.asarray(inp["l1_w_qkv"], np.float32).reshape(8, 128, 3, 8, 128)
    d["l1_w_qkv"] = np.ascontiguousarray(wq.transpose(3, 1, 0, 2, 4).reshape(8, 128, 8 * 384))
    d["l1_w_out"] = np.ascontiguousarray(np.asarray(inp["l1_w_out"], np.float32))
    return d


_CACHE = {}


def kernel(**inputs):
    x = np.asarray(inputs["x"], np.float32)
    shared = prep_shared(inputs)
    if "nc" not in _CACHE:
        _CACHE["nc"] = build_program()
    nc = _CACHE["nc"]
    in_maps = []
    for c in range(N_CORES):
        m = dict(shared)
        m["xT"] = np.ascontiguousarray(x[c].T)
        in_maps.append(m)
    res = run_bass_kernel_spmd(nc, in_maps, core_ids=list(range(N_CORES)))
    out = np.stack([np.ascontiguousarray(res.results[c]["outT"].T) for c in range(N_CORES)], axis=0)
    return out.astype(np.float32)
```

```python
import math
from contextlib import ExitStack

import numpy as np
import concourse.bass as bass
import concourse.mybir as mybir
from concourse.bass_utils import run_bass_kernel_spmd

F32 = mybir.dt.float32
BF16 = mybir.dt.bfloat16
AF = mybir.ActivationFunctionType
ALU = mybir.AluOpType

SEQ = 2048
DM = 1024
DFF = 2816
NJ = 22
NORM_EPS = 1e-6
N_CORES = 8


class Buf:
    __slots__ = ("name", "w", "r")

    def __init__(self, name):
        self.name = name
        self.w = None
        self.r = {}


class T:
    def __init__(self, name, t, parts=1):
        self.name = name
        self.t = t
        self.bufs = [Buf(f"{name}.{i}") for i in range(parts)]

    def b(self, *idx):
        return [self.bufs[i] for i in idx]

    def all(self):
        return list(self.bufs)

    def __getitem__(self, key):
        return self.t[key]


class Sched:
    COMPUTE = ("pe", "act", "dve", "pool")

    def __init__(self, nc, es, n_dma_ch=20):
        self.nc = nc
        self.eng = {"pe": nc.tensor, "act": nc.scalar, "dve": nc.vector, "pool": nc.gpsimd, "sp": nc.sync}
        self.sems = {}
        self.cnt = {}
        for e in self.COMPUTE:
            self.sems[e] = es.enter_context(nc.semaphore(f"s_{e}"))
            self.cnt[e] = 0
        self.ch = {}
        self.ch_next = {}
        for q in ("sp", "pool", "act"):
            n = n_dma_ch if q != "act" else 4
            lst = []
            for i in range(n):
                key = f"d_{q}{i}"
                self.sems[key] = es.enter_context(nc.semaphore(key))
                self.cnt[key] = 0
                lst.append(key)
            self.ch[q] = lst
            self.ch_next[q] = 0
        self.seen = {e: {} for e in self.eng}
        self.n_wait = 0
        self.n_ins = 0

    def _wait(self, e, ev):
        key, val = ev
        if val <= 0:
            return
        if self.seen[e].get(key, 0) >= val:
            return
        self.seen[e][key] = val
        self.eng[e].wait_ge(self.sems[key], val)
        self.n_wait += 1

    def _deps(self, e, reads, writes):
        evs = {}

        def need(ev):
            if ev is None:
                return
            k_, v_ = ev
            if e == "pe" and k_ == "pe":
                return
            if evs.get(k_, 0) < v_:
                evs[k_] = v_

        for b in reads:
            need(b.w)
        for b in writes:
            need(b.w)
            for kv in b.r.items():
                need(kv)
        return evs

    def op(self, e, fn, reads=(), writes=()):
        evs = self._deps(e, reads, writes)
        for ev in evs.items():
            self._wait(e, ev)
        ins = fn(self.eng[e])
        self.cnt[e] += 1
        ev = (e, self.cnt[e])
        ins.then_inc(self.sems[e], 1)
        self.seen[e][e] = max(self.seen[e].get(e, 0), 0)
        for b in writes:
            b.w = ev
            b.r = {}
        for b in reads:
            if b.w is not ev:
                b.r[e] = self.cnt[e]
        self.n_ins += 1
        return ins

    def dma(self, q, out, in_, reads=(), writes=()):
        e = q
        evs = self._deps(e, reads, writes)
        key = self.ch[q][self.ch_next[q]]
        self.ch_next[q] = (self.ch_next[q] + 1) % len(self.ch[q])
        if evs.get(key, 0) < self.cnt[key]:
            evs[key] = self.cnt[key]
        for ev in evs.items():
            self._wait(e, ev)
        ins = self.eng[e].dma_start(out=out, in_=in_)
        self.cnt[key] += 16
        ins.then_inc(self.sems[key], 16)
        ev = (key, self.cnt[key])
        for b in writes:
            b.w = ev
            b.r = {}
        for b in reads:
            b.r[key] = self.cnt[key]
        self.n_ins += 1
        return ins

    def barrier(self, engines=None):
        evs = [(k_, v_) for k_, v_ in self.cnt.items() if v_ > 0]
        for e in (engines or self.eng):
            for ev in evs:
                if ev[0] == e:
                    continue
                self._wait(e, ev)


class Phase:
    def __init__(self, k):
        self.k = k
        self.es = ExitStack()

    def __enter__(self):
        self.es.__enter__()
        return self

    def __exit__(self, *a):
        self.k.s.barrier()
        return self.es.__exit__(*a)

    def sb(self, name, shape, dtype, parts=1):
        self.k.uid += 1
        t = self.es.enter_context(self.k.nc.sbuf_tensor(f"ph_{name}_{self.k.uid}", shape, dtype))
        return T(name, t, parts)


class KB:
    def __init__(self):
        self.nc = bass.Bass("TRN2", target_bir_lowering=False)
        self.es = ExitStack()
        self.s = Sched(self.nc, self.es)
        self.uid = 0

    def sb(self, name, shape, dtype, parts=1):
        t = self.es.enter_context(self.nc.sbuf_tensor("sb_" + name, shape, dtype))
        return T(name, t, parts)

    def ps(self, name, shape, dtype=F32, parts=1):
        t = self.es.enter_context(self.nc.psum_tensor("pp_" + name, shape, dtype))
        return T(name, t, parts)

    def dram_in(self, name, shape, dtype=F32):
        return T(name, self.nc.dram_tensor(name, list(shape), dtype, kind="ExternalInput").ap())

    def dram_out(self, name, shape, dtype=F32):
        return T(name, self.nc.dram_tensor(name, list(shape), dtype, kind="ExternalOutput").ap())

    def phase(self):
        return Phase(self)


class Bg:
    def __init__(self):
        self.q = []

    def add(self, gen, period=2):
        self.q.append([gen, period, period])

    def tick(self):
        for item in list(self.q):
            item[2] -= 1
            if item[2] <= 0:
                item[2] = item[1]
                try:
                    next(item[0])
                except StopIteration:
                    self.q.remove(item)

    def drain(self):
        while self.q:
            for item in list(self.q):
                try:
                    next(item[0])
                except StopIteration:
                    self.q.remove(item)


def rms_rstd_gen(k, C, src, src_bufs, SQ, PST, RSTD, ntok, fuse_sq=False):
    s = k.s
    s.op("act", lambda e: e.activation(out=SQ[:, :, 0:ntok], in_=src, func=AF.Square),
         reads=src_bufs, writes=SQ.all())
    if not fuse_sq:
        yield
    for kc in range(8):
        s.op("pe", lambda e, kc=kc: e.matmul(PST[:, 0:ntok], lhsT=C["ones_m"][:, :], rhs=SQ[:, kc, 0:ntok],
                                             start=(kc == 0), stop=(kc == 7)),
             reads=SQ.all() + C["ones_m"].all(), writes=PST.all())
    yield
    s.op("act", lambda e: e.activation(out=RSTD[:, 0:ntok], in_=PST[:, 0:ntok], func=AF.Ln, bias=C["eps"][:, 0:1]),
         reads=PST.all() + C["eps"].all(), writes=RSTD.all())
    yield
    s.op("act", lambda e: e.activation(out=RSTD[:, 0:ntok], in_=RSTD[:, 0:ntok], func=AF.Exp, scale=-0.5),
         reads=RSTD.all(), writes=RSTD.all())


def ffn_stage(k, C, XT, PS, ffns, next_gi=None):
    s = k.s
    G_ = C["gains"]
    with k.phase() as ph:
        HTG = C["HTG"]
        HTs = []
        for i in range(2):
            hv = T(f"HTv{i}", HTG.t[:, :, i * 1024:(i + 1) * 1024])
            hv.bufs = HTG.bufs[2 * i:2 * i + 2]
            HTs.append(hv)
        ACTT = ph.sb("ACTT", [128, 11, 1024], BF16, parts=22)
        YT = ph.sb("YT", [128, 8, 1024], F32, parts=16)
        SQ = [ph.sb(f"SQ{i}", [128, 8, 512], BF16) for i in range(2)]
        RSTD = [ph.sb(f"RSTD{i}", [128, 512], F32) for i in range(2)]
        WIN = [ph.sb(f"WIN{i}", [128, 8, 256], BF16) for i in range(3)]
        WOUT = [ph.sb(f"WOUT{i}", [128, 11, 128], BF16) for i in range(3)]
        SG = [ph.sb(f"SG{i}", [128, 512], F32) for i in range(2)]
        PG = [PS[0], PS[1]]
        PU = [PS[2], PS[3]]
        PY = [PS[4], PS[5]]
        PST = [PS[6], PS[7]]
        st = {"win": 0, "wout": 0, "pi": 0, "ni": 0}
        jobs = [(f, B) for f in range(len(ffns)) for B in range(2)]

        bg = Bg()

        def prenorm(ji):
            f, B = jobs[ji]
            HT = HTs[ji % 2]
            gi_pre = ffns[f][2]
            for sb_ in range(2):
                tok = B * 1024 + sb_ * 512
                xb = XT.b(B * 2 + sb_)
                n_ = st["ni"] % 2
                st["ni"] += 1
                yield from rms_rstd_gen(k, C, XT[:, :, tok:tok + 512], xb, SQ[n_], PST[n_], RSTD[n_], 512)
                for kc in range(8):
                    if kc == 4:
                        yield
                    s.op("dve", lambda e, kc=kc, tok=tok, sb_=sb_, n_=n_: e.scalar_tensor_tensor(
                        out=HT[:, kc, sb_ * 512:(sb_ + 1) * 512], in0=XT[:, kc, tok:tok + 512],
                        scalar=G_[:, gi_pre, kc:kc + 1], in1=RSTD[n_][:, :], op0=ALU.mult, op1=ALU.mult),
                        reads=xb + RSTD[n_].all() + G_.all(), writes=HT.b(sb_))

        def postnorm(ji):
            f, B = jobs[ji]
            gi_post = ffns[f][3]
            for sb_ in range(2):
                tok = B * 1024 + sb_ * 512
                rhs_sl = slice(sb_ * 512, (sb_ + 1) * 512)
                ybs = YT.b(*[dc * 2 + sb_ for dc in range(8)])
                xb = XT.b(B * 2 + sb_)
                n_ = st["ni"] % 2
                st["ni"] += 1
                yield from rms_rstd_gen(k, C, YT[:, :, rhs_sl], ybs, SQ[n_], PST[n_], RSTD[n_], 512)
                for dc in range(8):
                    if dc % 2 == 0 and dc > 0:
                        yield
                    s.op("dve", lambda e, dc=dc, rhs_sl=rhs_sl, n_=n_: e.scalar_tensor_tensor(
                        out=YT[:, dc, rhs_sl], in0=YT[:, dc, rhs_sl], scalar=G_[:, gi_post, dc:dc + 1],
                        in1=RSTD[n_][:, :], op0=ALU.mult, op1=ALU.mult),
                        reads=YT.b(dc * 2 + sb_) + RSTD[n_].all() + G_.all(), writes=YT.b(dc * 2 + sb_))
                    s.op("dve", lambda e, dc=dc, rhs_sl=rhs_sl, tok=tok: e.tensor_tensor(
                        out=XT[:, dc, tok:tok + 512], in0=XT[:, dc, tok:tok + 512], in1=YT[:, dc, rhs_sl], op=ALU.add),
                        reads=YT.b(dc * 2 + sb_) + xb, writes=xb)

        def up(ji, G, after_first=None):
            f, B = jobs[ji]
            HT = HTs[ji % 2]
            w_in_d = ffns[f][0]
            for jj in range(11):
                j = G * 11 + jj
                W = WIN[st["win"] % 3]
                st["win"] += 1
                s.dma("pool", W[:, :, :], w_in_d[j].rearrange("p (kc c) -> p kc c", kc=8), writes=W.all())
                for sb_ in range(2):
                    pg, pu, sg = PG[st["pi"] % 2], PU[st["pi"] % 2], SG[st["pi"] % 2]
                    st["pi"] += 1
                    rhs_sl = slice(sb_ * 512, (sb_ + 1) * 512)
                    for kc in range(8):
                        s.op("pe", lambda e, kc=kc, pg=pg, W=W, rhs_sl=rhs_sl: e.matmul(
                            pg[:, :], lhsT=W[:, kc, 0:128], rhs=HT[:, kc, rhs_sl], start=(kc == 0), stop=(kc == 7)),
                            reads=W.all() + HT.b(sb_), writes=pg.all())
                    for kc in range(8):
                        s.op("pe", lambda e, kc=kc, pu=pu, W=W, rhs_sl=rhs_sl: e.matmul(
                            pu[:, :], lhsT=W[:, kc, 128:256], rhs=HT[:, kc, rhs_sl], start=(kc == 0), stop=(kc == 7)),
                            reads=W.all() + HT.b(sb_), writes=pu.all())
                    s.op("act", lambda e, pg=pg, sg=sg: e.activation(out=sg[:, :], in_=pg[:, :], func=AF.Silu),
                         reads=pg.all(), writes=sg.all())
                    s.op("dve", lambda e, pu=pu, sg=sg, jj=jj, rhs_sl=rhs_sl: e.tensor_tensor(
                        out=ACTT[:, jj, rhs_sl], in0=sg[:, :], in1=pu[:, :], op=ALU.mult),
                        reads=sg.all() + pu.all(), writes=ACTT.b(jj * 2 + sb_))
                    bg.tick()
                if jj == 0 and after_first is not None:
                    after_first()

        def down(ji, G):
            f, B = jobs[ji]
            w_out_d = ffns[f][1]
            for dc in range(8):
                W = WOUT[st["wout"] % 3]
                st["wout"] += 1
                s.dma("pool", W[:, :, :], w_out_d[G, dc].rearrange("p (jj c) -> p jj c", jj=11), writes=W.all())
                for sb_ in range(2):
                    py = PY[st["pi"] % 2]
                    st["pi"] += 1
                    rhs_sl = slice(sb_ * 512, (sb_ + 1) * 512)
                    for jj in range(11):
                        s.op("pe", lambda e, jj=jj, py=py, W=W, rhs_sl=rhs_sl: e.matmul(
                            py[:, :], lhsT=W[:, jj, :], rhs=ACTT[:, jj, rhs_sl], start=(jj == 0), stop=(jj == 10)),
                            reads=W.all() + ACTT.b(jj * 2 + sb_), writes=py.all())
                    yb = YT.b(dc * 2 + sb_)
                    if G == 0:
                        s.op("act", lambda e, py=py, dc=dc, rhs_sl=rhs_sl: e.activation(
                            out=YT[:, dc, rhs_sl], in_=py[:, :], func=AF.Copy),
                            reads=py.all(), writes=yb)
                    else:
                        s.op("dve", lambda e, py=py, dc=dc, rhs_sl=rhs_sl: e.tensor_tensor(
                            out=YT[:, dc, rhs_sl], in0=YT[:, dc, rhs_sl], in1=py[:, :], op=ALU.add),
                            reads=py.all() + yb, writes=yb)
                    bg.tick()

        def next_prenorm():
            HT = HTs[0]
            for sb_ in range(2):
                tok = sb_ * 512
                xb = XT.b(sb_)
                n_ = st["ni"] % 2
                st["ni"] += 1
                yield from rms_rstd_gen(k, C, XT[:, :, tok:tok + 512], xb, SQ[n_], PST[n_], RSTD[n_], 512)
                for kc in range(8):
                    if kc == 4:
                        yield
                    s.op("dve", lambda e, kc=kc, tok=tok, sb_=sb_, n_=n_: e.scalar_tensor_tensor(
                        out=HT[:, kc, sb_ * 512:(sb_ + 1) * 512], in0=XT[:, kc, tok:tok + 512],
                        scalar=G_[:, next_gi, kc:kc + 1], in1=RSTD[n_][:, :], op0=ALU.mult, op1=ALU.mult),
                        reads=xb + RSTD[n_].all() + G_.all(), writes=HT.b(sb_))

        n = len(jobs)
        assert n % 2 == 0
        if C["ht_ready"] is not None and C["ht_ready"] == (ffns[0][2], (0, 1)):
            pass
        else:
            bg.add(prenorm(0))
            bg.drain()
        C["ht_ready"] = None
        for ji in range(n):
            up(ji, 0, after_first=(lambda ji=ji: bg.add(postnorm(ji - 1), 1)) if ji > 0 else None)
            bg.drain()
            down(ji, 0)
            if ji + 1 < n:
                bg.add(prenorm(ji + 1), 1)
            elif next_gi is not None:
                bg.add(next_prenorm(), 1)
                C["ht_ready"] = (next_gi, (0, 1))
            up(ji, 1)
            bg.drain()
            down(ji, 1)
        bg.add(postnorm(n - 1))
        bg.drain()


def prenorm_to_HT(k, C, ph, XT, HT, PS, gi_pre, col_off=0):
    s = k.s
    G_ = C["gains"]
    with k.phase() as p2:
        SQ = [p2.sb(f"SQ{i}", [128, 8, 512], BF16) for i in range(2)]
        RSTD = [p2.sb(f"RSTD{i}", [128, 512], F32) for i in range(2)]
        bg = Bg()

        def chain(tb):
            tok = tb * 512
            xb = XT.b(tb)
            yield from rms_rstd_gen(k, C, XT[:, :, tok:tok + 512], xb, SQ[tb % 2], PS[6 + tb % 2], RSTD[tb % 2], 512)
            for kc in range(8):
                if kc == 4:
                    yield
                s.op("dve", lambda e, kc=kc: e.scalar_tensor_tensor(
                    out=HT[:, kc, col_off + tok:col_off + tok + 512], in0=XT[:, kc, tok:tok + 512],
                    scalar=G_[:, gi_pre, kc:kc + 1], in1=RSTD[tb % 2][:, :], op0=ALU.mult, op1=ALU.mult),
                    reads=xb + RSTD[tb % 2].all() + G_.all(), writes=HT.b(tb))

        skip = ()
        if C["ht_ready"] is not None and C["ht_ready"][0] == gi_pre and col_off == 0:
            skip = C["ht_ready"][1]
        C["ht_ready"] = None
        for tb in range(4):
            if tb in skip:
                continue
            bg.add(chain(tb), 1)
            bg.tick()
            bg.tick()
        bg.drain()


def outproj_postnorm(k, C, XT, PS, OT, wo_d, gi_post, next_gi=None, WO=None):
    s = k.s
    G_ = C["gains"]
    with k.phase() as p3:
        if WO is None:
            WO = T("WOv", C["HTG"].t[:, :, 1024:2048])
            WO.bufs = C["HTG"].bufs[2:4]
            for kc in range(8):
                s.dma("pool", WO[:, kc, :], wo_d[kc * 128:(kc + 1) * 128, :], writes=WO.all())
        YTs = [p3.sb(f"YT{i}", [128, 8, 512], F32, parts=8) for i in range(2)]
        SQ1 = p3.sb("SQ", [128, 8, 512], BF16)
        RSTD = [p3.sb(f"RSTD{i}", [128, 512], F32) for i in range(2)]
        bg = Bg()

        def chain(tb):
            tok = tb * 512
            YT = YTs[tb % 2]
            yield from rms_rstd_gen(k, C, YT[:, :, :], YT.all(), SQ1, PS[6 + tb % 2], RSTD[tb % 2], 512, fuse_sq=True)
            xb = XT.b(tb)
            for dc in range(8):
                if dc % 2 == 0 and dc > 0:
                    yield
                s.op("dve", lambda e, dc=dc: e.scalar_tensor_tensor(
                    out=YT[:, dc, :], in0=YT[:, dc, :], scalar=G_[:, gi_post, dc:dc + 1],
                    in1=RSTD[tb % 2][:, :], op0=ALU.mult, op1=ALU.mult),
                    reads=YT.b(dc) + RSTD[tb % 2].all() + G_.all(), writes=YT.b(dc))
                s.op("dve", lambda e, dc=dc: e.tensor_tensor(
                    out=XT[:, dc, tok:tok + 512], in0=XT[:, dc, tok:tok + 512], in1=YT[:, dc, :], op=ALU.add),
                    reads=YT.b(dc) + xb, writes=xb)

        if next_gi is not None:
            RSTDn = p3.sb("RSTDn", [128, 512], F32)
        HTG = C["HTG"]

        def next_prenorm():
            for sb_ in range(2):
                tok = sb_ * 512
                xb = XT.b(sb_)
                yield from rms_rstd_gen(k, C, XT[:, :, tok:tok + 512], xb, SQ1, PS[0], RSTDn, 512, fuse_sq=True)
                for kc in range(8):
                    if kc == 4:
                        yield
                    s.op("dve", lambda e, kc=kc, tok=tok: e.scalar_tensor_tensor(
                        out=HTG[:, kc, tok:tok + 512], in0=XT[:, kc, tok:tok + 512],
                        scalar=G_[:, next_gi, kc:kc + 1], in1=RSTDn[:, :], op0=ALU.mult, op1=ALU.mult),
                        reads=xb + RSTDn.all() + G_.all(), writes=HTG.b(sb_))

        pi = 0
        for tb in range(4):
            tok = tb * 512
            YT = YTs[tb % 2]
            if tb == 3 and next_gi is not None:
                bg.add(next_prenorm(), 1)
                C["ht_ready"] = (next_gi, (0, 1))
            for dc in range(8):
                pp = PS[4 + pi % 2]
                pi += 1
                for kc in range(8):
                    s.op("pe", lambda e, kc=kc, dc=dc, pp=pp, tok=tok: e.matmul(
                        pp[:, :], lhsT=WO[:, kc, dc * 128:(dc + 1) * 128], rhs=OT[:, kc, tok:tok + 512],
                        start=(kc == 0), stop=(kc == 7)),
                        reads=WO.all() + OT.all(), writes=pp.all())
                s.op("act", lambda e, dc=dc, pp=pp, YT=YT: e.activation(out=YT[:, dc, :], in_=pp[:, :], func=AF.Copy),
                     reads=pp.all(), writes=YT.b(dc))
                bg.tick()
            bg.drain()
            bg.add(chain(tb), 1)
        bg.drain()


def attn_stage(k, C, XT, PS, wqkv_d, wo_d, gi_pre, gi_post, next_gi=None):
    s = k.s
    with k.phase() as ph:
        OT = ph.sb("OT", [128, 8, SEQ], BF16, parts=1)
        with k.phase() as pab:
            HT = C["HTG"]
            prenorm_to_HT(k, C, pab, XT, HT, PS, gi_pre)
            with k.phase() as pb:
                NEGM = pb.sb("negm", [128, 4, 512], BF16)
                ZB = pb.sb("zb", [128, 512], BF16)
                s.op("dve", lambda e: e.memset(ZB[:, :], 0.0), writes=ZB.all())
                for d in range(4):
                    s.op("pool", lambda e, d=d: e.affine_select(
                        out=NEGM[:, d, :], in_=ZB[:, :], pattern=[[1, 512]], compare_op=ALU.is_gt,
                        fill=-30000.0, base=-128 * d, channel_multiplier=-1),
                        reads=ZB.all(), writes=NEGM.all())
                QT = [pb.sb(f"QT{i}", [128, SEQ], BF16) for i in range(2)]
                KT = [pb.sb(f"KT{i}", [128, SEQ], BF16) for i in range(2)]
                V = [pb.sb(f"V{i}", [128, 16, 128], BF16) for i in range(2)]
                W = [pb.sb(f"WQKV{i}", [128, 8, 384], BF16) for i in range(1)]
                OTOK = [pb.sb(f"OTOK{i}", [128, 16, 128], BF16) for i in range(2)]
                E = [pb.sb(f"E{i}", [128, 512], F32) for i in range(3)]
                SP = [pb.sb(f"SP{i}", [128, 512], BF16) for i in range(5)]
                ATT = [pb.sb(f"ATT{i}", [128, 512], BF16) for i in range(3)]
                OACC = [pb.sb(f"OACC{i}", [128, 4, 64], F32) for i in range(2)]
                CACC = [pb.sb(f"CACC{i}", [128, 4], F32) for i in range(2)]
                FS = [pb.sb(f"FS{i}", [128, 4], F32) for i in range(4)]
                PZ = [PS[0], PS[1], PS[2], PS[3], PS[4]]
                PO = [PS[5], PS[6]]
                PP = [PS[7]]
                st = {"pi": 0}

                bg = Bg()

                def pre_hp(hp):
                    w = W[0]
                    qt, kt_, v = QT[hp % 2], KT[hp % 2], V[hp % 2]
                    s.dma("pool", w[:, :, :], wqkv_d[hp].rearrange("p (kc c) -> p kc c", kc=8), writes=w.all())
                    for which in range(2):
                        for tb in range(4):
                            pp = PP[st["pi"] % len(PP)]
                            st["pi"] += 1
                            for kc in range(8):
                                s.op("pe", lambda e, kc=kc, pp=pp, w=w, which=which, tb=tb: e.matmul(
                                    pp[:, :], lhsT=w[:, kc, which * 128:(which + 1) * 128],
                                    rhs=HT[:, kc, tb * 512:(tb + 1) * 512], start=(kc == 0), stop=(kc == 7)),
                                    reads=w.all() + HT.b(tb), writes=pp.all())
                            if which == 0:
                                s.op("dve", lambda e, pp=pp, qt=qt, tb=tb: e.tensor_scalar(
                                    out=qt[:, tb * 512:(tb + 1) * 512], in0=pp[:, :], scalar1=0.125, scalar2=None,
                                    op0=ALU.mult),
                                    reads=pp.all(), writes=qt.all())
                            else:
                                s.op("dve", lambda e, pp=pp, kt_=kt_, tb=tb: e.tensor_copy(
                                    out=kt_[:, tb * 512:(tb + 1) * 512], in_=pp[:, :]),
                                    reads=pp.all(), writes=kt_.all())
                            yield
                    for tg in range(4):
                        pp = PP[st["pi"] % len(PP)]
                        st["pi"] += 1
                        for tt in range(4):
                            tok = (tg * 4 + tt) * 128
                            for kc in range(8):
                                s.op("pe", lambda e, kc=kc, pp=pp, w=w, tt=tt, tok=tok: e.matmul(
                                    pp[:, tt * 128:(tt + 1) * 128], lhsT=HT[:, kc, tok:tok + 128],
                                    rhs=w[:, kc, 256:384], start=(kc == 0), stop=(kc == 7)),
                                    reads=w.all() + HT.b(tg), writes=pp.all())
                        s.op("dve", lambda e, pp=pp, v=v, tg=tg: e.tensor_copy(
                            out=v[:, tg * 4:(tg + 1) * 4, :], in_=pp[:, :].rearrange("p (a b) -> p a b", a=4)),
                            reads=pp.all(), writes=v.all())
                        yield

                def post_hp(hp):
                    otok = OTOK[hp % 2]
                    for tg in range(4):
                        pp = PP[st["pi"] % len(PP)]
                        st["pi"] += 1
                        for tt in range(4):
                            s.op("pe", lambda e, pp=pp, tt=tt, tg=tg, otok=otok: e.matmul(
                                pp[:, tt * 128:(tt + 1) * 128], lhsT=otok[:, tg * 4 + tt, :], rhs=C["ident"][:, :],
                                start=True, stop=True),
                                reads=otok.all() + C["ident"].all(), writes=pp.all())
                        s.op("dve", lambda e, pp=pp, tg=tg, hp=hp: e.tensor_copy(
                            out=OT[:, hp, tg * 512:(tg + 1) * 512], in_=pp[:, :]),
                            reads=pp.all(), writes=OT.all())

                units = []
                for hp in range(8):
                    for g in range(4):
                        for kt in range(4 * g + 3, -1, -1):
                            for hh in range(2):
                                units.append((hp, hh, g, kt))
                n = len(units)
                NPZ, NSP, NATT, NPO, NE = 5, 5, 3, 2, 3

                def u_(i):
                    hp, hh, g, kt = units[i]
                    d = kt - 4 * g
                    return hp, hh, g, kt, d, slice(hh * 64, (hh + 1) * 64)

                def c0_(i):
                    hp, hh, g, kt = units[i]
                    return max(kt - 4 * g, 0) * 128

                def s0_qk(i):
                    hp, hh, g, kt, d, hs = u_(i)
                    if hh == 0 and g == 0 and kt == 3:
                        if hp == 0:
                            bg.add(pre_hp(0))
                        bg.drain()
                    if hh == 0 and g == 0 and kt == 0 and hp + 1 < 8:
                        bg.add(pre_hp(hp + 1), 5)
                    pz, qt, kt_ = PZ[i % NPZ], QT[hp % 2], KT[hp % 2]
                    q0 = g * 512
                    c0 = c0_(i)
                    s.op("pe", lambda e: e.matmul(pz[:, c0:512], lhsT=kt_[hs, kt * 128:(kt + 1) * 128],
                                                  rhs=qt[hs, q0 + c0:q0 + 512], start=True, stop=(d < 0)),
                         reads=kt_.all() + qt.all(), writes=pz.all())
                    if d >= 0:
                        s.op("pe", lambda e: e.matmul(pz[:, c0:c0 + 128], lhsT=C["ident"][:, :], rhs=NEGM[:, d, c0:c0 + 128],
                                                      start=False, stop=True),
                             reads=C["ident"].all() + NEGM.all(), writes=pz.all())

                def s1_exp(i):
                    pz, e_ = PZ[i % NPZ], E[i % NE]
                    c0 = c0_(i)
                    s.op("act", lambda e: e.activation(out=e_[:, c0:512], in_=pz[:, c0:512], func=AF.Exp),
                         reads=pz.all(), writes=e_.all())

                def s2_ln(i):
                    e_, sp = E[i % NE], SP[i % NSP]
                    c0 = c0_(i)
                    s.op("act", lambda e: e.activation(out=sp[:, c0:512], in_=e_[:, c0:512], func=AF.Ln,
                                                       bias=C["one_f"][:, 0:1]),
                         reads=e_.all() + C["one_f"].all(), writes=sp.all())

                def s3_tri(i):
                    pz, sp = PZ[i % NPZ], SP[i % NSP]
                    c0 = c0_(i)
                    s.op("pe", lambda e: e.matmul(pz[:, c0:512], lhsT=C["ntri"][:, :], rhs=sp[:, c0:512], start=False,
                                                  stop=True, skip_group_check=True),
                         reads=sp.all() + C["ntri"].all(), writes=pz.all())

                def s4_att(i):
                    pz, att = PZ[i % NPZ], ATT[i % NATT]
                    c0 = c0_(i)
                    s.op("act", lambda e: e.activation(out=att[:, c0:512], in_=pz[:, c0:512], func=AF.Exp),
                         reads=pz.all(), writes=att.all())

                def s5_av(i):
                    hp, hh, g, kt, d, hs = u_(i)
                    qlo = max(d, 0)
                    sp, att, po, v = SP[i % NSP], ATT[i % NATT], PO[i % NPO], V[hp % 2]
                    for qi in range(qlo, 4):
                        s.op("pe", lambda e, qi=qi: e.matmul(
                            po[:, qi * 64:(qi + 1) * 64], lhsT=att[:, qi * 128:(qi + 1) * 128],
                            rhs=v[:, kt, hs], start=True, stop=True),
                            reads=att.all() + v.all(), writes=po.all())
                        s.op("pe", lambda e, qi=qi: e.matmul(
                            po[:, 256 + qi:257 + qi], lhsT=sp[:, qi * 128:(qi + 1) * 128],
                            rhs=C["ones_col"][:, 0:1], start=True, stop=True),
                            reads=sp.all() + C["ones_col"].all(), writes=po.all())

                def s6_acc(i):
                    hp, hh, g, kt, d, hs = u_(i)
                    qlo = max(d, 0)
                    span = hh
                    po = PO[i % NPO]
                    oacc, cacc, fs = OACC[span % 2], CACC[span % 2], FS[i % 4]
                    otok = OTOK[hp % 2]
                    if kt != 4 * g + 3:
                        s.op("act", lambda e: e.activation(out=fs[:, :], in_=cacc[:, :], func=AF.Exp, scale=-1.0),
                             reads=cacc.all(), writes=fs.all())
                        s.op("dve", lambda e: e.tensor_tensor(
                            out=cacc[:, qlo:4], in0=cacc[:, qlo:4], in1=po[:, 256 + qlo:260], op=ALU.add),
                            reads=po.all() + cacc.all(), writes=cacc.all())
                        for qi in range(qlo, 4):
                            s.op("dve", lambda e, qi=qi: e.scalar_tensor_tensor(
                                out=oacc[:, qi, :], in0=po[:, qi * 64:(qi + 1) * 64], scalar=fs[:, qi:qi + 1],
                                in1=oacc[:, qi, :], op0=ALU.mult, op1=ALU.add),
                                reads=po.all() + fs.all() + oacc.all(), writes=oacc.all())
                    else:
                        if qlo > 0:
                            s.op("dve", lambda e: e.memset(oacc[:, 0:qlo, :], 0.0), writes=oacc.all())
                            s.op("dve", lambda e: e.memset(cacc[:, 0:qlo], 0.0), writes=cacc.all())
                        s.op("dve", lambda e: e.tensor_copy(
                            out=oacc[:, qlo:4, :], in_=po[:, qlo * 64:256].rearrange("p (a b) -> p a b", b=64)),
                            reads=po.all(), writes=oacc.all())
                        s.op("dve", lambda e: e.tensor_copy(out=cacc[:, qlo:4], in_=po[:, 256 + qlo:260]),
                             reads=po.all(), writes=cacc.all())
                    if kt == 0:
                        s.op("dve", lambda e: e.tensor_copy(out=otok[:, 4 * g:4 * g + 4, hs], in_=oacc[:, :, :]),
                             reads=oacc.all(), writes=otok.all())
                        if hh == 1 and g == 3:
                            post_hp(hp)

                stages = ((0, s0_qk), (1, s1_exp), (2, s2_ln), (3, s3_tri), (4, s4_att), (5, s5_av), (6, s6_acc))
                for i in range(n + 6):
                    for lag, fn in stages:
                        if 0 <= i - lag < n:
                            fn(i - lag)
                    bg.tick()
        if "dbg_OT" in C:
            s.dma("sp", C["dbg_OT"][:, :, :], OT[:, :, :], reads=OT.all(), writes=C["dbg_OT"].all())
        outproj_postnorm(k, C, XT, PS, OT, wo_d, gi_post, next_gi)


AX = mybir.AxisListType


class Ref:
    __slots__ = ("ap", "bufs")

    def __init__(self, ap, bufs):
        self.ap = ap
        self.bufs = bufs


class _RefMaker:
    def __init__(self, t):
        self.t = t

    def __getitem__(self, key):
        return Ref(self.t.t[key], self.t.all())


def rf(t):
    return _RefMaker(t)


def _b(*refs):
    out = []
    for r in refs:
        if isinstance(r, Ref):
            out += r.bufs
    return out


def _a(x):
    return x.ap if isinstance(x, Ref) else x


def e_tt(s, eng, out, a, b, op):
    return s.op(eng, lambda E: E.tensor_tensor(out=out.ap, in0=a.ap, in1=b.ap, op=op), reads=_b(a, b), writes=out.bufs)


def e_ts(s, eng, out, a, s1, s2, op0, op1=None):
    if op1 is None:
        return s.op(eng, lambda E: E.tensor_scalar(out=out.ap, in0=a.ap, scalar1=_a(s1), scalar2=None, op0=op0),
                    reads=_b(a, s1), writes=out.bufs)
    return s.op(eng, lambda E: E.tensor_scalar(out=out.ap, in0=a.ap, scalar1=_a(s1), scalar2=_a(s2), op0=op0, op1=op1),
                reads=_b(a, s1, s2), writes=out.bufs)


def e_stt(s, eng, out, a, sc, b, op0, op1):
    return s.op(eng, lambda E: E.scalar_tensor_tensor(out=out.ap, in0=a.ap, scalar=_a(sc), in1=b.ap, op0=op0, op1=op1),
                reads=_b(a, sc, b), writes=out.bufs)


def e_act(s, out, a, func, bias=None, scale=None):
    kw = {}
    if bias is not None:
        kw["bias"] = _a(bias)
    if scale is not None:
        kw["scale"] = _a(scale)
    return s.op("act", lambda E: E.activation(out=out.ap, in_=a.ap, func=func, **kw), reads=_b(a, bias, scale),
                writes=out.bufs)


def e_mm(s, out, lhsT, rhs, start=True, stop=True):
    return s.op("pe", lambda E: E.matmul(out.ap, lhsT=lhsT.ap, rhs=rhs.ap, start=start, stop=stop),
                reads=_b(lhsT, rhs), writes=out.bufs)


def e_copy(s, eng, out, a):
    if eng == "act":
        return e_act(s, out, a, AF.Copy)
    return s.op(eng, lambda E: E.tensor_copy(out=out.ap, in_=a.ap), reads=_b(a), writes=out.bufs)


def e_memset(s, eng, out, val):
    return s.op(eng, lambda E: E.memset(out.ap, val), writes=out.bufs)


GN_EPS = 64e-5
STAGGER = 3
NEG_EXP_HALF = -0.6065306597126334


def mixer0_stage(k, C, XT, PS, D, gi_pre, gi_post, next_gi=None):
    s = k.s
    with k.phase() as ph:
        OT = ph.sb("OT", [128, 8, SEQ], BF16, parts=1)
        with k.phase() as pab:
            HT = C["HTG"]
            prenorm_to_HT(k, C, pab, XT, HT, PS, gi_pre)
            with k.phase() as pl:
                rglru_part(k, C, pl, HT, OT, PS, D)
            with k.phase() as pr:
                rwkv_part(k, C, pr, HT, OT, PS, D, XT)
        if "dbg_OT" in D:
            s.dma("sp", D["dbg_OT"][:, :, :], OT[:, :, :], reads=OT.all(), writes=D["dbg_OT"].all())
        outproj_postnorm(k, C, XT, PS, OT, D["l0_w_out"], gi_post, next_gi)


def rglru_part(k, C, p, HT, OT, PS, D):
    s = k.s
    PL = p.sb("PL", [128, 4, 8], F32)
    s.dma("sp", PL[:, :, :], D["l0_pl"][:, :].rearrange("p (c n) -> p c n", c=4), writes=PL.all())
    C1 = p.sb("C1", [128, 4], F32)
    e_act(s, rf(C1)[:, :], rf(PL)[:, :, 7], AF.Exp, scale=-1.0)
    e_act(s, rf(C1)[:, :], rf(C1)[:, :], AF.Ln, bias=rf(C["one_f"])[:, 0:1])
    e_ts(s, "dve", rf(C1)[:, :], rf(C1)[:, :], -8.0, None, ALU.mult)
    GAW = p.sb("GAW", [128, 4, 128], BF16)
    GXW = p.sb("GXW", [128, 4, 128], BF16)
    e_memset(s, "dve", rf(GAW)[:, :, :], 0.0)
    e_memset(s, "dve", rf(GXW)[:, :, :], 0.0)
    for n in range(8):
        ps_ = slice((n % 2) * 64, (n % 2) * 64 + 64)
        s.dma("pool", GAW[ps_, n // 2, ps_], D["l0_gate_a_w"][n], writes=GAW.all())
        s.dma("pool", GXW[ps_, n // 2, ps_], D["l0_gate_x_w"][n], writes=GXW.all())
    W = [p.sb(f"WL{i}", [128, 8, 256], BF16) for i in range(2)]
    XBs = [p.sb(f"XB{i}", [128, 515], F32) for i in range(2)]
    HHs = [[p.sb(f"HH{j}_{i}", [128, 512], F32) for i in range(2)] for j in range(2)]
    ts_ = [{n: p.sb(f"{n}{j}", [128, 512], F32) for n in ("GB", "XC", "R", "IG", "A", "U", "T1", "T2")} for j in range(2)]
    XCbs = [p.sb(f"XCb{j}", [128, 512], BF16) for j in range(2)]

    def unit(c, tb, j):
        w = W[j]
        XB, t_, XCb = XBs[j], ts_[j], XCbs[j]
        col = lambda n: rf(PL)[:, c, n:n + 1]
        tok = tb * 512
        px, pg = PS[2 * j], PS[2 * j + 1]
        for kc in range(8):
            e_mm(s, rf(px)[:, :], rf(w)[:, kc, 0:128], Ref(HT[:, kc, tok:tok + 512], HT.b(tb)), kc == 0, kc == 7)
        for kc in range(8):
            e_mm(s, rf(pg)[:, :], rf(w)[:, kc, 128:256], Ref(HT[:, kc, tok:tok + 512], HT.b(tb)), kc == 0, kc == 7)
        if tb == 0:
            e_memset(s, "dve", rf(XB)[:, 0:3], 0.0)
        else:
            e_copy(s, "dve", rf(XB)[:, 0:3], rf(XB)[:, 512:515])
        yield
        e_copy(s, "act", rf(XB)[:, 3:515], rf(px)[:, :])
        e_copy(s, "act", rf(t_["GB"])[:, :], rf(pg)[:, :])
        yield
        XC = t_["XC"]
        e_ts(s, "dve", rf(XC)[:, :], rf(XB)[:, 3:515], col(3), col(4), ALU.mult, ALU.add)
        for i in range(3):
            e_stt(s, "dve", rf(XC)[:, :], rf(XB)[:, i:i + 512], col(i), rf(XC)[:, :], ALU.mult, ALU.add)
        GB, T2 = t_["GB"], t_["T2"]
        e_act(s, rf(T2)[:, :], rf(GB)[:, :], AF.Gelu_apprx_tanh)
        yield
        e_copy(s, "act", rf(XCb)[:, :], rf(XC)[:, :])
        yield
        pr_, pig = PS[4 + 2 * j], PS[5 + 2 * j]
        e_mm(s, rf(pr_)[:, :], rf(GAW)[:, c, :], rf(XCb)[:, :])
        e_mm(s, rf(pig)[:, :], rf(GXW)[:, c, :], rf(XCb)[:, :])
        yield
        e_act(s, rf(t_["R"])[:, :], rf(pr_)[:, :], AF.Sigmoid, bias=col(5))
        e_act(s, rf(t_["IG"])[:, :], rf(pig)[:, :], AF.Sigmoid, bias=col(6))
        yield
        A = t_["A"]
        e_act(s, rf(A)[:, :], rf(t_["R"])[:, :], AF.Exp, scale=rf(C1)[:, c:c + 1])
        T1, U = t_["T1"], t_["U"]
        e_tt(s, "dve", rf(U)[:, :], rf(t_["IG"])[:, :], rf(XC)[:, :], ALU.mult)
        yield
        e_tt(s, "dve", rf(T1)[:, :], rf(A)[:, :], rf(A)[:, :], ALU.mult)
        yield
        e_ts(s, "dve", rf(T1)[:, :], rf(T1)[:, :], -1.0, 1.0, ALU.mult, ALU.add)
        yield
        e_act(s, rf(T1)[:, :], rf(T1)[:, :], AF.Sqrt)
        yield
        e_tt(s, "dve", rf(U)[:, :], rf(U)[:, :], rf(T1)[:, :], ALU.mult)
        yield
        H = HHs[j][tb % 2]
        Hp = HHs[j][(tb + 1) % 2]
        init = 0.0 if tb == 0 else Hp[:, 511:512]
        s.op("dve", lambda E: E.tensor_tensor_scan(
            out=H[:, :], data0=A[:, :], data1=U[:, :], initial=init, op0=ALU.mult, op1=ALU.add),
            reads=A.all() + U.all() + (Hp.all() if tb else []), writes=H.all())
        yield
        s.op("dve", lambda E: E.tensor_tensor(
            out=OT[:, 4 + c, tok:tok + 512], in0=H[:, :], in1=T2[:, :], op=ALU.mult),
            reads=H.all() + T2.all(), writes=OT.all())

    for cp in range(2):
        for j in range(2):
            c = 2 * cp + j
            s.dma("pool", W[j][:, :, :], D["l0_w_lru"][c].rearrange("p (kc n) -> p kc n", kc=8), writes=W[j].all())
        for tb in range(4):
            gens = [unit(2 * cp + j, tb, j) for j in range(2)]
            alive = [True, True]
            while any(alive):
                for j in range(2):
                    if alive[j]:
                        try:
                            next(gens[j])
                        except StopIteration:
                            alive[j] = False


def rwkv_part(k, C, p, HT, OT, PS, D, XT):
    s = k.s
    spill = D["xt_spill"]
    for kc in range(8):
        s.dma("sp", spill[:, kc, :], XT[:, kc, :], reads=XT.all(), writes=spill.all())
    PH = p.sb("PH", [128, 4, 8], F32)
    s.dma("sp", PH[:, :, :], D["l0_ph"][:, :].rearrange("p (h n) -> p h n", h=4), writes=PH.all())
    OM = p.sb("OM", [128, 4, 4], F32)
    e_ts(s, "dve", rf(OM)[:, :, 0:3], rf(PH)[:, :, 0:3], -1.0, 1.0, ALU.mult, ALU.add)
    e_ts(s, "dve", rf(OM)[:, :, 3:4], rf(PH)[:, :, 6:7], -1.0, 1.0, ALU.mult, ALU.add)
    RKb = p.sb("RKb", [128, 4], BF16)
    e_copy(s, "dve", rf(RKb)[:, :], rf(PH)[:, :, 7])
    MUL = p.sb("MUL", [128, 3], F32)
    s.dma("sp", MUL[:, :], D["l0_mul"][:, :], writes=MUL.all())
    OML = p.sb("OML", [128, 3], F32)
    e_ts(s, "dve", rf(OML)[:, :], rf(MUL)[:, :], -1.0, 1.0, ALU.mult, ALU.add)
    LNGBs = [p.sb(f"LNGB{i}", [128, 2, 64], F32) for i in range(2)]
    W2 = p.sb("W2", [64, 512], BF16)
    A2 = p.sb("A2", [64, 512], BF16)
    G2 = p.sb("G2", [128, 512], BF16)
    s.dma("pool", W2[:, :], D["l0_w2"][:, :], writes=W2.all())
    s.dma("pool", A2[:, :], D["l0_a2"][:, :], writes=A2.all())
    s.dma("pool", G2[:, :], D["l0_g2"][:, :], writes=G2.all())
    ob = C["ones_bf"]
    BLK = p.sb("BLK", [128, 128], BF16)
    e_memset(s, "dve", rf(BLK)[:, :], 0.0)
    e_memset(s, "dve", rf(BLK)[0:64, 0:64], 1.0)
    e_memset(s, "dve", rf(BLK)[64:128, 64:128], 1.0)
    M512 = p.sb("M512", [128, 512], BF16)
    MUS = p.sb("MUS", [128, 512], BF16)
    MUI = p.sb("MUI", [128, 512], BF16)
    MLS = p.sb("MLS", [128, 512], BF16)
    ID8 = p.sb("ID8", [128, 512], BF16)
    for hh in range(2):
        hs = slice(hh * 64, hh * 64 + 64)
        for dst, pat, cmp_, cm in ((M512, [[0, 8], [1, 64]], ALU.is_gt, 0), (MUS, [[0, 8], [1, 64]], ALU.is_gt, -1),
                                   (MUI, [[0, 8], [1, 64]], ALU.is_ge, -1), (MLS, [[0, 8], [-1, 64]], ALU.is_gt, 1),
                                   (ID8, [[0, 8], [-1, 64]], ALU.is_equal, 1)):
            s.op("pool", lambda E, dst=dst, pat=pat, cmp_=cmp_, cm=cm, hs=hs: E.affine_select(
                out=dst[hs, :], in_=ob[hs, :], pattern=pat, compare_op=cmp_, fill=0.0, base=0, channel_multiplier=cm),
                reads=ob.all(), writes=dst.all())
    ident = C["ident"]

    TW = p.sb("TW", [64, SEQ], BF16)
    AL = p.sb("AL", [64, SEQ], BF16)
    SGL = p.sb("SGL", [128, SEQ], BF16)
    with k.phase() as p0:
        WLo = p0.sb("WLo", [128, 8, 256], BF16)
        s.dma("pool", WLo[:, :, :], D["l0_w_lora"][:, :].rearrange("p (kc n) -> p kc n", kc=8), writes=WLo.all())
        PAl = [p0.sb(f"PAl{i}", [128, 513], F32) for i in range(3)]
        TMPl = [p0.sb(f"TMPl{i}", [128, 512], F32) for i in range(3)]

        def lora_chain(which, c0, c1, npart, dst):
            PA, tmpl = PAl[which], TMPl[which]
            for tb in range(4):
                tok = tb * 512
                pp = PS[which * 2 + tb % 2]
                for kc in range(8):
                    e_mm(s, rf(pp)[0:npart, :], rf(WLo)[:, kc, c0:c1], Ref(HT[:, kc, tok:tok + 512], HT.b(tb)), kc == 0, kc == 7)
                if tb == 0:
                    e_memset(s, "dve", rf(PA)[0:npart, 0:1], 0.0)
                else:
                    e_copy(s, "dve", rf(PA)[0:npart, 0:1], rf(PA)[0:npart, 512:513])
                yield
                e_copy(s, "act", rf(PA)[0:npart, 1:513], rf(pp)[0:npart, :])
                yield
                e_act(s, rf(tmpl)[0:npart, :], rf(PA)[0:npart, 0:512], AF.Copy, scale=rf(MUL)[0:npart, which:which + 1])
                yield
                e_stt(s, "dve", rf(tmpl)[0:npart, :], rf(PA)[0:npart, 1:513], rf(OML)[0:npart, which:which + 1],
                      rf(tmpl)[0:npart, :], ALU.mult, ALU.add)
                yield
                if which == 0:
                    e_act(s, rf(dst)[:, tok:tok + 512], rf(tmpl)[0:64, :], AF.Tanh)
                elif which == 1:
                    e_copy(s, "act", rf(dst)[:, tok:tok + 512], rf(tmpl)[0:64, :])
                else:
                    e_act(s, rf(dst)[:, tok:tok + 512], rf(tmpl)[:, :], AF.Sigmoid)
                yield

        lbg = Bg()
        for which, (c0, c1, npart, dst) in enumerate(((0, 64, 64, TW), (64, 128, 64, AL), (128, 256, 128, SGL))):
            lbg.add(lora_chain(which, c0, c1, npart, dst), 1)
        lbg.drain()

    s.barrier()
    XTf = XT.t
    XTb = XT.t.bitcast(BF16)
    f32n = ("r", "k", "SIG", "A", "KKN", "KH", "CUM", "EC", "EX", "EN", "TMP")
    b16n = ("Rt", "At", "Bt", "Kt", "Bh", "Kh", "RK", "VT", "KK2")
    t64n = ("V64", "BH64", "KH64", "N", "Q", "N2", "Q2", "XA", "LAK", "ARB", "ARK")
    sets = []
    for S in range(2):
        B = {}
        if S == 0:
            B["WH"] = p.sb("WH", [128, 8, 384], BF16)
            B["PA"] = [p.sb(f"PA{i}", [128, 513], F32) for i in range(3)]
            F = {n: p.sb("f_" + n, [128, 512], F32) for n in f32n}
            Bf = {n: p.sb("b_" + n, [128, 512], BF16) for n in b16n}
            T64 = {n: p.sb("t_" + n, [128, 512], BF16) for n in t64n}
        else:
            B["WH"] = T("WH1", XTb[:, 7, 0:3072].rearrange("p (kc n) -> p kc n", kc=8))
            B["PA"] = [T(f"PA1_{i}", XTf[:, 3, i * 513:(i + 1) * 513]) for i in range(3)]
            F = {n: T("f1_" + n, XTf[:, i // 4, (i % 4) * 512:(i % 4) * 512 + 512]) for i, n in enumerate(f32n)}
            bl = list(b16n) + list(t64n)
            vb = {n: T("b1_" + n, XTb[:, 4 + i // 8, (i % 8) * 512:(i % 8) * 512 + 512]) for i, n in enumerate(bl)}
            Bf = {n: vb[n] for n in b16n}
            T64 = {n: vb[n] for n in t64n}
        F["EH"] = F["TMP"]
        F["BA"] = F["SIG"]
        T64["YA"] = T64["LAK"]
        B["F"], B["Bf"], B["T64"] = F, Bf, T64
        B["R0"], B["YF"], B["YQ"], B["GT"] = F["SIG"], F["EX"], F["EN"], F["KH"]
        B["ST"] = p.sb(f"ST{S}", [128, 8, 4], F32)
        B["RKS"] = p.sb(f"RKS{S}", [128, 8], F32)
        B["Pf"] = p.sb(f"Pf{S}", [128, 64], F32)
        B["Pb"] = p.sb(f"Pb{S}", [128, 64], BF16)
        B["RR"] = p.sb(f"RR{S}", [128, 64], BF16)
        B["UB"] = p.sb(f"UB{S}", [128, 64], BF16)
        B["LNGB"] = LNGBs[S]
        B["PY"] = PS[4 + S]
        B["PT1"] = PS[6 + S]
        sets.append(B)
    b3 = lambda r_: Ref(r_.ap.rearrange("p (a b) -> p a b", a=8), r_.bufs)
    HS = (slice(0, 64), slice(64, 128))
    st = {"sci": 0}

    def newps():
        st["sci"] += 1
        return PS[st["sci"] % 4]

    def mm2(out_t, col0, ncol, lhs_fn, rhs_fn, start=True, stop=True):
        for hs in HS:
            e_mm(s, rf(out_t)[hs, col0:col0 + ncol], lhs_fn(hs), rhs_fn(hs), start, stop)

    def unit(hp, gq, B):
        F, Bf, T64, PA, w = B["F"], B["Bf"], B["T64"], B["PA"], B["WH"]
        R0, YF, YQ, GT, ST, RKS = B["R0"], B["YF"], B["YQ"], B["GT"], B["ST"], B["RKS"]
        Pf, Pb, RR, UB, LNGB = B["Pf"], B["Pb"], B["RR"], B["UB"], B["LNGB"]
        hc = lambda n: rf(PH)[:, hp, n:n + 1]
        tok = gq * 512
        for which, nm in enumerate(("r", "k", "v")):
            pp = newps()
            for kc in range(8):
                e_mm(s, rf(pp)[:, :], rf(w)[:, kc, which * 128:(which + 1) * 128],
                     Ref(HT[:, kc, tok:tok + 512], HT.b(gq)), kc == 0, kc == 7)
            pa = PA[which]
            if gq == 0:
                e_memset(s, "dve", rf(pa)[:, 0:1], 0.0)
            else:
                e_copy(s, "dve", rf(pa)[:, 0:1], rf(pa)[:, 512:513])
            e_copy(s, "act", rf(pa)[:, 1:513], rf(pp)[:, :])
            yield
            tmp_ = rf(F["TMP"])[:, :] if which != 1 else rf(F["CUM"])[:, :]
            e_act(s, tmp_, rf(pa)[:, 0:512], AF.Copy, scale=hc(which))
            dst_ = rf(Bf["VT"])[:, :] if nm == "v" else rf(F[nm])[:, :]
            e_stt(s, "dve", dst_, rf(pa)[:, 1:513], rf(OM)[:, hp, which:which + 1], tmp_, ALU.mult, ALU.add)
        r_, k_ = rf(F["r"])[:, :], rf(F["k"])[:, :]
        pz = newps()
        e_mm(s, rf(pz)[:, :], rf(W2)[:, hp * 128:(hp + 1) * 128], rf(TW)[:, tok:tok + 512])
        e_act(s, rf(F["SIG"])[:, :], rf(pz)[:, :], AF.Sigmoid, bias=hc(3))
        pz2 = newps()
        e_mm(s, rf(pz2)[:, :], rf(A2)[:, hp * 128:(hp + 1) * 128], rf(AL)[:, tok:tok + 512])
        e_act(s, rf(F["A"])[:, :], rf(pz2)[:, :], AF.Sigmoid, bias=hc(4))
        yield
        e_ts(s, "dve", rf(F["KKN"])[:, :], k_, hc(5), None, ALU.mult)
        e_act(s, rf(Bf["KK2"])[:, :], rf(F["KKN"])[:, :], AF.Square)
        yield
        pz = newps()
        e_mm(s, rf(pz)[:, :], rf(BLK)[:, :], rf(Bf["KK2"])[:, :])
        e_act(s, rf(F["TMP"])[:, :], rf(pz)[:, :], AF.Sqrt)
        yield
        e_ts(s, "dve", rf(F["TMP"])[:, :], rf(F["TMP"])[:, :], 1e-12, None, ALU.max)
        s.op("dve", lambda E: E.reciprocal(out=F["TMP"][:, :], in_=F["TMP"][:, :]), reads=F["TMP"].all(),
             writes=F["TMP"].all())
        e_tt(s, "dve", rf(F["KKN"])[:, :], rf(F["KKN"])[:, :], rf(F["TMP"])[:, :], ALU.mult)
        e_act(s, rf(F["KH"])[:, :], rf(F["A"])[:, :], AF.Identity, bias=rf(OM)[:, hp, 3:4], scale=hc(6))
        e_tt(s, "dve", rf(F["KH"])[:, :], rf(F["KH"])[:, :], k_, ALU.mult)
        yield
        s.op("dve", lambda E: E.tensor_tensor_scan(out=F["CUM"][:, :], data0=M512[:, :], data1=F["SIG"][:, :],
                                                   initial=0.0, op0=ALU.mult, op1=ALU.add),
             reads=M512.all() + F["SIG"].all(), writes=F["CUM"].all())
        yield
        cum = rf(F["CUM"])[:, :]
        e_act(s, rf(F["EC"])[:, :], cum, AF.Exp, scale=NEG_EXP_HALF)
        e_tt(s, "dve", rf(F["EX"])[:, :], cum, rf(F["SIG"])[:, :], ALU.subtract)
        e_act(s, rf(F["EN"])[:, :], cum, AF.Exp, scale=-NEG_EXP_HALF)
        cum3 = b3(cum)
        cend = Ref(cum3.ap[:, :, 63:64].to_broadcast([128, 8, 64]), cum.bufs)
        e_tt(s, "dve", b3(rf(F["EH"])[:, :]), cend, cum3, ALU.subtract)
        yield
        e_act(s, rf(F["EX"])[:, :], rf(F["EX"])[:, :], AF.Exp, scale=NEG_EXP_HALF)
        e_act(s, rf(F["EH"])[:, :], rf(F["EH"])[:, :], AF.Exp, scale=NEG_EXP_HALF)
        e_tt(s, "dve", rf(Bf["Rt"])[:, :], r_, rf(F["EC"])[:, :], ALU.mult)
        e_tt(s, "dve", rf(Bf["RK"])[:, :], r_, rf(F["KH"])[:, :], ALU.mult)
        e_tt(s, "dve", rf(F["BA"])[:, :], rf(F["KKN"])[:, :], rf(F["A"])[:, :], ALU.mult)
        yield
        e_stt(s, "dve", rf(Bf["At"])[:, :], rf(F["KKN"])[:, :], -1.0, rf(F["EX"])[:, :], ALU.mult, ALU.mult)
        e_tt(s, "dve", rf(Bf["Bt"])[:, :], rf(F["BA"])[:, :], rf(F["EN"])[:, :], ALU.mult)
        e_tt(s, "dve", rf(Bf["Bh"])[:, :], rf(F["BA"])[:, :], rf(F["EH"])[:, :], ALU.mult)
        e_tt(s, "dve", rf(Bf["Kt"])[:, :], rf(F["KH"])[:, :], rf(F["EN"])[:, :], ALU.mult)
        e_tt(s, "dve", rf(Bf["Kh"])[:, :], rf(F["KH"])[:, :], rf(F["EH"])[:, :], ALU.mult)
        yield
        blk = lambda n, c8, hs: rf(Bf[n])[hs, c8 * 64:(c8 + 1) * 64]
        tb_ = lambda n, c8, hs: rf(T64[n])[hs, c8 * 64:(c8 + 1) * 64]
        idh = lambda hs: rf(ident)[hs, hs]
        for src, dst in (("VT", "V64"), ("Bh", "BH64"), ("Kh", "KH64")):
            pt = newps()
            for c8 in range(8):
                mm2(pt, c8 * 64, 64, lambda hs, c8=c8, src=src: blk(src, c8, hs), idh)
            e_copy(s, "act", rf(T64[dst])[:, :], rf(pt)[:, :])
            yield
        for lh, rh, mask, dst in (("Bt", "At", MUS, "N"), ("At", "Bt", MLS, "Q"), ("Kt", "At", MUS, "LAK"),
                                  ("Bt", "Rt", MUI, "ARB"), ("Kt", "Rt", MUI, "ARK")):
            pt = newps()
            for c8 in range(8):
                mm2(pt, c8 * 64, 64, lambda hs, c8=c8, lh=lh: blk(lh, c8, hs), lambda hs, c8=c8, rh=rh: blk(rh, c8, hs))
            e_tt(s, "dve", rf(T64[dst])[:, :], rf(pt)[:, :], rf(mask)[:, :], ALU.mult)
            yield
        e_tt(s, "dve", rf(T64["XA"])[:, :], rf(T64["N"])[:, :], rf(ID8)[:, :], ALU.add)
        Pn, Qn, Pn2, Qn2 = "N", "Q", "N2", "Q2"
        for lvl in range(1, 6):
            pq = newps()
            for c8 in range(8):
                mm2(pq, c8 * 64, 64, lambda hs, c8=c8, Pn=Pn: tb_(Pn, c8, hs), lambda hs, c8=c8, Qn=Qn: tb_(Qn, c8, hs))
            e_copy(s, "act", rf(T64[Qn2])[:, :], rf(pq)[:, :])
            if lvl < 5:
                pp_ = newps()
                for c8 in range(8):
                    mm2(pp_, c8 * 64, 64, lambda hs, c8=c8, Qn=Qn: tb_(Qn, c8, hs),
                        lambda hs, c8=c8, Pn=Pn: tb_(Pn, c8, hs))
                e_copy(s, "act", rf(T64[Pn2])[:, :], rf(pp_)[:, :])
            yield
            px = newps()
            for c8 in range(8):
                mm2(px, c8 * 64, 64, lambda hs, c8=c8, Qn2=Qn2: tb_(Qn2, c8, hs), lambda hs, c8=c8: tb_("XA", c8, hs))
            e_tt(s, "dve", rf(T64["XA"])[:, :], rf(T64["XA"])[:, :], rf(px)[:, :], ALU.add)
            yield
            Pn, Pn2 = Pn2, Pn
            Qn, Qn2 = Qn2, Qn
        pr0 = newps()
        for c8 in range(8):
            mm2(pr0, c8 * 64, 64, lambda hs, c8=c8: tb_("LAK", c8, hs), lambda hs, c8=c8: tb_("V64", c8, hs))
        e_copy(s, "act", rf(R0)[:, :], rf(pr0)[:, :])
        pg = newps()
        e_mm(s, rf(pg)[:, :], rf(G2)[:, hp * 128:(hp + 1) * 128], rf(SGL)[:, tok:tok + 512])
        e_copy(s, "act", rf(GT)[:, :], rf(pg)[:, :])
        yield
        PY, PT1 = B["PY"], B["PT1"]
        pbh = lambda hs: rf(Pb)[hs, :]
        ubh = lambda hs: rf(UB)[hs, :]
        for c8 in range(8):
            mm2(PT1, 0, 64, lambda hs: blk("At", c8, hs), pbh)
            mm2(PY, c8 * 64, 64, lambda hs: blk("Rt", c8, hs), pbh, True, False)
            e_tt(s, "dve", rf(RR)[:, :], rf(PT1)[:, 0:64], rf(R0)[:, c8 * 64:(c8 + 1) * 64], ALU.add)
            yield
            mm2(PT1, 64, 64, lambda hs: tb_("XA", c8, hs), lambda hs: rf(RR)[hs, :])
            e_copy(s, "act", rf(UB)[:, :], rf(PT1)[:, 64:128])
            yield
            mm2(PT1, 128, 64, lambda hs: tb_("KH64", c8, hs), lambda hs: tb_("V64", c8, hs), True, False)
            mm2(PT1, 128, 64, lambda hs: tb_("BH64", c8, hs), ubh, False, True)
            mm2(PY, c8 * 64, 64, lambda hs: tb_("ARB", c8, hs), ubh, False, False)
            mm2(PY, c8 * 64, 64, lambda hs: tb_("ARK", c8, hs), lambda hs: tb_("V64", c8, hs), False, True)
            e_stt(s, "dve", rf(Pf)[:, :], rf(Pf)[:, :], rf(F["EC"])[:, c8 * 64 + 63:c8 * 64 + 64], rf(PT1)[:, 128:192],
                  ALU.mult, ALU.add)
            e_copy(s, "act", rf(Pb)[:, :], rf(Pf)[:, :])
            yield
        e_copy(s, "act", rf(YF)[:, :], rf(PY)[:, :])
        yf3 = b3(rf(YF)[:, :])
        yq3 = b3(rf(YQ)[:, :])
        prk = newps()
        for c8 in range(8):
            mm2(prk, c8, 1, lambda hs: blk("RK", c8, hs), lambda hs: rf(RKb)[hs, hp:hp + 1])
        e_copy(s, "act", rf(RKS)[:, :], rf(prk)[:, 0:8])
        yield
        s.op("dve", lambda E: E.tensor_reduce(out=ST[:, :, 0], in_=YF[:, :].rearrange("p (a b) -> p a b", a=8),
                                              axis=AX.X, op=ALU.add), reads=YF.all(), writes=ST.all())
        e_act(s, rf(YQ)[:, :], rf(YF)[:, :], AF.Square)
        yield
        s.op("dve", lambda E: E.tensor_reduce(out=ST[:, :, 1], in_=YQ[:, :].rearrange("p (a b) -> p a b", a=8),
                                              axis=AX.X, op=ALU.add), reads=YQ.all(), writes=ST.all())
        e_ts(s, "dve", rf(ST)[:, :, 2], rf(ST)[:, :, 0], 1.0 / 64, None, ALU.mult)
        e_tt(s, "dve", rf(ST)[:, :, 0], rf(ST)[:, :, 2], rf(ST)[:, :, 2], ALU.mult)
        e_stt(s, "dve", rf(ST)[:, :, 1], rf(ST)[:, :, 1], 1.0 / 64, rf(ST)[:, :, 0], ALU.mult, ALU.subtract)
        e_ts(s, "dve", rf(ST)[:, :, 1], rf(ST)[:, :, 1], GN_EPS, None, ALU.add)
        e_act(s, rf(ST)[:, :, 3], rf(ST)[:, :, 1], AF.Sqrt)
        yield
        s.op("dve", lambda E: E.reciprocal(out=ST[:, :, 3], in_=ST[:, :, 3]), reads=ST.all(), writes=ST.all())
        mean_b = Ref(ST[:, :, 2:3].to_broadcast([128, 8, 64]), ST.all())
        rstd_b = Ref(ST[:, :, 3:4].to_broadcast([128, 8, 64]), ST.all())
        e_tt(s, "dve", yf3, yf3, mean_b, ALU.subtract)
        e_tt(s, "dve", yf3, yf3, rstd_b, ALU.mult)
        rks_b = Ref(RKS[:, :].rearrange("p (a b) -> p a b", b=1).to_broadcast([128, 8, 64]), RKS.all())
        e_tt(s, "dve", yq3, b3(rf(T64["V64"])[:, :]), rks_b, ALU.mult)
        lng = Ref(LNGB[:, 0:1, :].to_broadcast([128, 8, 64]), LNGB.all())
        lnb = Ref(LNGB[:, 1:2, :].to_broadcast([128, 8, 64]), LNGB.all())
        yield
        e_tt(s, "dve", yf3, yf3, lng, ALU.mult)
        e_tt(s, "dve", yf3, yf3, lnb, ALU.add)
        yield
        e_tt(s, "dve", rf(T64["YA"])[:, :], rf(YF)[:, :], rf(YQ)[:, :], ALU.add)
        yield
        pt = newps()
        for c8 in range(8):
            mm2(pt, c8 * 64, 64, lambda hs: tb_("YA", c8, hs), idh)
        s.op("dve", lambda E: E.tensor_tensor(
            out=OT[:, hp, tok:tok + 512], in0=pt[:, :], in1=GT[:, :], op=ALU.mult),
            reads=pt.all() + GT.all(), writes=OT.all())
        yield

    def stream(S, hps):
        B = sets[S]
        for hp in hps:
            w = B["WH"]
            s.dma("pool", w[:, :, :], D["l0_w_hp"][hp].rearrange("p (kc n) -> p kc n", kc=8), writes=w.all())
            LNGB = B["LNGB"]
            for hh in range(2):
                h = 2 * hp + hh
                s.dma("sp", LNGB[HS[hh], 0, :], D["l0_lnx_g"][h * 64:(h + 1) * 64].partition_broadcast(64),
                      writes=LNGB.all())
                s.dma("sp", LNGB[HS[hh], 1, :], D["l0_lnx_b"][h * 64:(h + 1) * 64].partition_broadcast(64),
                      writes=LNGB.all())
            e_memset(s, "dve", rf(B["Pf"])[:, :], 0.0)
            e_memset(s, "dve", rf(B["Pb"])[:, :], 0.0)
            yield
            for gq in range(4):
                yield from unit(hp, gq, B)

    gens = [stream(0, (0, 2)), stream(1, (1, 3))]
    alive = [True, True]
    first = True
    while any(alive):
        for S in range(2):
            if alive[S]:
                try:
                    next(gens[S])
                except StopIteration:
                    alive[S] = False
            if first and S == 0:
                for _ in range(STAGGER):
                    next(gens[0])
                first = False
    s.barrier()
    for kc in range(8):
        s.dma("sp", XT[:, kc, :], spill[:, kc, :], reads=spill.all(), writes=XT.all())


GAIN_NAMES = ["l0_ffn1_pre_g", "l0_ffn1_post_g", "l0_mix_pre_g", "l0_mix_post_g", "l0_ffn2_pre_g", "l0_ffn2_post_g",
              "l1_ffn1_pre_g", "l1_ffn1_post_g", "l1_mix_pre_g", "l1_mix_post_g", "l1_ffn2_pre_g", "l1_ffn2_post_g"]
HALF_GAINS = [1, 5, 7, 11]


def build_program(stages=("f01", "m0", "f02f11", "m1", "f12"), dbg=False):
    k = KB()
    nc = k.nc
    s = k.s
    xT_d = k.dram_in("xT", [DM, SEQ])
    gains_d = k.dram_in("gains", [128, 12 * 8])
    ffn_d = {}
    for nm in ("l0_ffn1", "l0_ffn2", "l1_ffn1", "l1_ffn2"):
        ffn_d[nm] = (k.dram_in(nm + "_w_in", [NJ, 128, 2048]), k.dram_in(nm + "_w_out", [2, 8, 128, 11 * 128]))
    D = {}
    for nm, shp in (("l0_pl", [128, 32]), ("l0_ph", [128, 32]), ("l0_mul", [128, 3]), ("l0_lnx_g", [512]),
                    ("l0_lnx_b", [512]), ("l0_w2", [64, 512]), ("l0_a2", [64, 512]), ("l0_g2", [128, 512]),
                    ("l0_gate_a_w", [8, 64, 64]), ("l0_gate_x_w", [8, 64, 64]), ("l0_w_lru", [4, 128, 8 * 256]),
                    ("l0_w_lora", [128, 8 * 256]), ("l0_w_hp", [4, 128, 8 * 384]), ("l0_w_out", [DM, DM])):
        D[nm] = k.dram_in(nm, shp)
    D["xt_spill"] = T("xt_spill", nc.dram_tensor("xt_spill", [128, 8, SEQ], F32, kind="Internal").ap())
    if dbg:
        D["dbg_OT"] = k.dram_out("dbg_OT", [128, 8, SEQ], BF16)
    wqkv_d = k.dram_in("l1_w_qkv", [8, 128, 8 * 384])
    l1_wo_d = k.dram_in("l1_w_out", [DM, DM])
    outT_d = k.dram_out("outT", [DM, SEQ])

    with k.es:
        XT = k.sb("XT", [128, 8, SEQ], F32, parts=4)
        C = {}
        C["gains"] = k.sb("gains", [128, 12, 8], F32)
        C["ones_m"] = k.sb("ones_m", [128, 128], BF16)
        PS = [k.ps(f"ps{i}", [128, 512]) for i in range(8)]
        C["HTG"] = k.sb("HTG", [128, 8, SEQ], BF16, parts=4)
        C["ht_ready"] = None

        s.op("dve", lambda e: e.memset(C["ones_m"][:, :], 1.0 / DM), writes=C["ones_m"].all())
        C["one_f"] = k.sb("one_f", [128, 1], F32)
        C["ones_col"] = k.sb("ones_col", [128, 1], BF16)
        C["ones_bf"] = k.sb("ones_bf", [128, 512], BF16)
        C["ident"] = k.sb("ident", [128, 128], BF16)
        C["ntri"] = k.sb("ntri", [128, 128], BF16)
        s.op("dve", lambda e: e.memset(C["one_f"][:, :], 1.0), writes=C["one_f"].all())
        s.op("dve", lambda e: e.memset(C["ones_col"][:, :], 1.0), writes=C["ones_col"].all())
        s.op("dve", lambda e: e.memset(C["ones_bf"][:, :], 1.0), writes=C["ones_bf"].all())
        s.op("pool", lambda e: e.affine_select(out=C["ident"][:, :], in_=C["ones_bf"][:, 0:128], pattern=[[-1, 128]],
                                               compare_op=ALU.is_equal, fill=0.0, base=0, channel_multiplier=1),
             reads=C["ones_bf"].all(), writes=C["ident"].all())
        s.op("pool", lambda e: e.affine_select(out=C["ntri"][:, :], in_=C["ones_bf"][:, 0:128], pattern=[[-1, 128]],
                                               compare_op=ALU.is_ge, fill=0.0, base=0, channel_multiplier=1),
             reads=C["ones_bf"].all(), writes=C["ntri"].all())
        s.op("dve", lambda e: e.tensor_scalar(out=C["ntri"][:, :], in0=C["ntri"][:, :], scalar1=-1.0, scalar2=None,
                                              op0=ALU.mult),
             reads=C["ntri"].all(), writes=C["ntri"].all())
        C["eps"] = k.sb("eps", [128, 1], F32)
        s.op("dve", lambda e: e.memset(C["eps"][:, :], NORM_EPS), writes=C["eps"].all())
        s.dma("sp", C["gains"][:, :, :], gains_d[:, :].rearrange("p (n c) -> p n c", n=12), writes=C["gains"].all())
        for gi in HALF_GAINS:
            s.op("dve", lambda e, gi=gi: e.tensor_scalar(out=C["gains"][:, gi, :], in0=C["gains"][:, gi, :],
                                                         scalar1=0.5, scalar2=None, op0=ALU.mult),
                 reads=C["gains"].all(), writes=C["gains"].all())
        for tb in range(4):
            for kc in range(8):
                s.dma("sp", XT[:, kc, tb * 512:(tb + 1) * 512], xT_d[kc * 128:(kc + 1) * 128, tb * 512:(tb + 1) * 512],
                      writes=XT.b(tb))

        PRE_GI = {"f01": 0, "m0": 2, "f02": 4, "f02f11": 4, "f11": 6, "m1": 8, "f12": 10}
        for si, st in enumerate(stages):
            nxt = PRE_GI[stages[si + 1]] if si + 1 < len(stages) else None
            if st == "f01":
                ffn_stage(k, C, XT, PS, [(*ffn_d["l0_ffn1"], 0, 1)], nxt)
            elif st == "f02":
                ffn_stage(k, C, XT, PS, [(*ffn_d["l0_ffn2"], 4, 5)], nxt)
            elif st == "f02f11":
                ffn_stage(k, C, XT, PS, [(*ffn_d["l0_ffn2"], 4, 5), (*ffn_d["l1_ffn1"], 6, 7)], nxt)
            elif st == "f11":
                ffn_stage(k, C, XT, PS, [(*ffn_d["l1_ffn1"], 6, 7)], nxt)
            elif st == "m0":
                mixer0_stage(k, C, XT, PS, D, 2, 3, nxt)
            elif st == "m1":
                if dbg:
                    C["dbg_OT"] = D["dbg_OT"]
                attn_stage(k, C, XT, PS, wqkv_d, l1_wo_d, 8, 9, nxt)
            elif st == "f12":
                ffn_stage(k, C, XT, PS, [(*ffn_d["l1_ffn2"], 10, 11)], nxt)

        for tb in range(4):
            for kc in range(8):
                s.dma("sp", outT_d[kc * 128:(kc + 1) * 128, tb * 512:(tb + 1) * 512], XT[:, kc, tb * 512:(tb + 1) * 512],
                      reads=XT.b(tb), writes=outT_d.all())
        s.barrier(engines=["sp"])
    return nc


def _col(v):
    return np.ascontiguousarray(np.asarray(v, np.float32).reshape(8, 128).T)


def prep_shared(inp):
    d = {}
    d["gains"] = np.ascontiguousarray(np.concatenate([_col(inp[n]) for n in GAIN_NAMES], axis=1))
    for nm in ("l0_ffn1", "l0_ffn2", "l1_ffn1", "l1_ffn2"):
        w_in = np.asarray(inp[nm + "_w_in"], np.float32)
        w_out = np.asarray(inp[nm + "_w_out"], np.float32)
        g = w_in[:, :DFF].reshape(8, 128, NJ, 128)
        u = w_in[:, DFF:].reshape(8, 128, NJ, 128)
        gu = np.concatenate([g, u], axis=3)
        d[nm + "_w_in"] = np.ascontiguousarray(gu.transpose(2, 1, 0, 3).reshape(NJ, 128, 2048))
        wo = w_out.reshape(2, 11, 128, 8, 128)
        d[nm + "_w_out"] = np.ascontiguousarray(wo.transpose(0, 3, 2, 1, 4).reshape(2, 8, 128, 11 * 128))
    f = lambda n: np.asarray(inp[n], np.float32)
    cw = f("l0_conv_w")
    pl = np.stack([cw[0], cw[1], cw[2], cw[3], f("l0_conv_b"), f("l0_gate_a_b"), f("l0_gate_x_b"), f("l0_lambda")], axis=1)
    d["l0_pl"] = np.ascontiguousarray(pl.reshape(4, 128, 8).transpose(1, 0, 2).reshape(128, 32))
    mu = f("l0_mu")
    ph = np.stack([mu[0:512], mu[512:1024], mu[1024:1536], f("l0_w0"), f("l0_a0"), f("l0_k_k"), f("l0_k_a"),
                   f("l0_r_k").reshape(512)], axis=1)
    d["l0_ph"] = np.ascontiguousarray(ph.reshape(4, 128, 8).transpose(1, 0, 2).reshape(128, 32))
    mul = np.zeros((128, 3), np.float32)
    mul[0:64, 0] = mu[1536:1600]
    mul[0:64, 1] = mu[1600:1664]
    mul[:, 2] = mu[1664:1792]
    d["l0_mul"] = mul
    for n in ("l0_lnx_g", "l0_lnx_b", "l0_w2", "l0_a2", "l0_g2", "l0_gate_a_w", "l0_gate_x_w", "l0_w_out"):
        d[n] = np.ascontiguousarray(f(n))
    wi = f("l0_w_in").reshape(8, 128, 2816)
    lru = np.concatenate([wi[:, :, 1792:2304].reshape(8, 128, 4, 128), wi[:, :, 2304:2816].reshape(8, 128, 4, 128)], axis=3)
    d["l0_w_lru"] = np.ascontiguousarray(lru.transpose(2, 1, 0, 3).reshape(4, 128, 8 * 256))
    d["l0_w_lora"] = np.ascontiguousarray(wi[:, :, 1536:1792].transpose(1, 0, 2).reshape(128, 8 * 256))
    hd = np.stack([wi[:, :, 0:512].reshape(8, 128, 4, 128), wi[:, :, 512:1024].reshape(8, 128, 4, 128),
                   wi[:, :, 1024:1536].reshape(8, 128, 4, 128)], axis=3)
    d["l0_w_hp"] = np.ascontiguousarray(hd.transpose(2, 1, 0, 3, 4).reshape(4, 128, 8 * 384))
    wq = np.asarray(inp["l1_w_qkv"], np.float32).reshape(8, 128, 3, 8, 128)
    d["l1_w_qkv"] = np.ascontiguousarray(wq.transpose(3, 1, 0, 2, 4).reshape(8, 128, 8 * 384))
    d["l1_w_out"] = np.ascontiguousarray(np.asarray(inp["l1_w_out"], np.float32))
    return d


_CACHE = {}


def kernel(**inputs):
    x = np.asarray(inputs["x"], np.float32)
    shared = prep_shared(inputs)
    if "nc" not in _CACHE:
        _CACHE["nc"] = build_program()
    nc = _CACHE["nc"]
    in_maps = []
    for c in range(N_CORES):
        m = dict(shared)
        m["xT"] = np.ascontiguousarray(x[c].T)
        in_maps.append(m)
    res = run_bass_kernel_spmd(nc, in_maps, core_ids=list(range(N_CORES)))
    out = np.stack([np.ascontiguousarray(res.results[c]["outT"].T) for c in range(N_CORES)], axis=0)
    return out.astype(np.float32)
```

```python
import math
from contextlib import ExitStack

import numpy as np
import concourse.bass as bass
import concourse.mybir as mybir
from concourse.bass_utils import run_bass_kernel_spmd

F32 = mybir.dt.float32
BF16 = mybir.dt.bfloat16
AF = mybir.ActivationFunctionType
ALU = mybir.AluOpType

SEQ = 2048
DM = 1024
DFF = 2816
NJ = 22
NORM_EPS = 1e-6
N_CORES = 8


class Buf:
    __slots__ = ("name", "w", "r")

    def __init__(self, name):
        self.name = name
        self.w = None
        self.r = {}


class T:
    def __init__(self, name, t, parts=1):
        self.name = name
        self.t = t
        self.bufs = [Buf(f"{name}.{i}") for i in range(parts)]

    def b(self, *idx):
        return [self.bufs[i] for i in idx]

    def all(self):
        return list(self.bufs)

    def __getitem__(self, key):
        return self.t[key]


class Sched:
    COMPUTE = ("pe", "act", "dve", "pool")

    def __init__(self, nc, es, n_dma_ch=20):
        self.nc = nc
        self.eng = {"pe": nc.tensor, "act": nc.scalar, "dve": nc.vector, "pool": nc.gpsimd, "sp": nc.sync}
        self.sems = {}
        self.cnt = {}
        for e in self.COMPUTE:
            self.sems[e] = es.enter_context(nc.semaphore(f"s_{e}"))
            self.cnt[e] = 0
        self.ch = {}
        self.ch_next = {}
        for q in ("sp", "pool", "act"):
            n = n_dma_ch if q != "act" else 4
            lst = []
            for i in range(n):
                key = f"d_{q}{i}"
                self.sems[key] = es.enter_context(nc.semaphore(key))
                self.cnt[key] = 0
                lst.append(key)
            self.ch[q] = lst
            self.ch_next[q] = 0
        self.seen = {e: {} for e in self.eng}
        self.n_wait = 0
        self.n_ins = 0

    def _wait(self, e, ev):
        key, val = ev
        if val <= 0:
            return
        if self.seen[e].get(key, 0) >= val:
            return
        self.seen[e][key] = val
        self.eng[e].wait_ge(self.sems[key], val)
        self.n_wait += 1

    def _deps(self, e, reads, writes):
        evs = {}

        def need(ev):
            if ev is None:
                return
            k_, v_ = ev
            if e == "pe" and k_ == "pe":
                return
            if evs.get(k_, 0) < v_:
                evs[k_] = v_

        for b in reads:
            need(b.w)
        for b in writes:
            need(b.w)
            for kv in b.r.items():
                need(kv)
        return evs

    def op(self, e, fn, reads=(), writes=()):
        evs = self._deps(e, reads, writes)
        for ev in evs.items():
            self._wait(e, ev)
        ins = fn(self.eng[e])
        self.cnt[e] += 1
        ev = (e, self.cnt[e])
        ins.then_inc(self.sems[e], 1)
        self.seen[e][e] = max(self.seen[e].get(e, 0), 0)
        for b in writes:
            b.w = ev
            b.r = {}
        for b in reads:
            if b.w is not ev:
                b.r[e] = self.cnt[e]
        self.n_ins += 1
        return ins

    def dma(self, q, out, in_, reads=(), writes=()):
        e = q
        evs = self._deps(e, reads, writes)
        key = self.ch[q][self.ch_next[q]]
        self.ch_next[q] = (self.ch_next[q] + 1) % len(self.ch[q])
        if evs.get(key, 0) < self.cnt[key]:
            evs[key] = self.cnt[key]
        for ev in evs.items():
            self._wait(e, ev)
        ins = self.eng[e].dma_start(out=out, in_=in_)
        self.cnt[key] += 16
        ins.then_inc(self.sems[key], 16)
        ev = (key, self.cnt[key])
        for b in writes:
            b.w = ev
            b.r = {}
        for b in reads:
            b.r[key] = self.cnt[key]
        self.n_ins += 1
        return ins

    def barrier(self, engines=None):
        evs = [(k_, v_) for k_, v_ in self.cnt.items() if v_ > 0]
        for e in (engines or self.eng):
            for ev in evs:
                if ev[0] == e:
                    continue
                self._wait(e, ev)


class Phase:
    def __init__(self, k):
        self.k = k
        self.es = ExitStack()

    def __enter__(self):
        self.es.__enter__()
        return self

    def __exit__(self, *a):
        self.k.s.barrier()
        return self.es.__exit__(*a)

    def sb(self, name, shape, dtype, parts=1):
        self.k.uid += 1
        t = self.es.enter_context(self.k.nc.sbuf_tensor(f"ph_{name}_{self.k.uid}", shape, dtype))
        return T(name, t, parts)


class KB:
    def __init__(self):
        self.nc = bass.Bass("TRN2", target_bir_lowering=False)
        self.es = ExitStack()
        self.s = Sched(self.nc, self.es)
        self.uid = 0

    def sb(self, name, shape, dtype, parts=1):
        t = self.es.enter_context(self.nc.sbuf_tensor("sb_" + name, shape, dtype))
        return T(name, t, parts)

    def ps(self, name, shape, dtype=F32, parts=1):
        t = self.es.enter_context(self.nc.psum_tensor("pp_" + name, shape, dtype))
        return T(name, t, parts)

    def dram_in(self, name, shape, dtype=F32):
        return T(name, self.nc.dram_tensor(name, list(shape), dtype, kind="ExternalInput").ap())

    def dram_out(self, name, shape, dtype=F32):
        return T(name, self.nc.dram_tensor(name, list(shape), dtype, kind="ExternalOutput").ap())

    def phase(self):
        return Phase(self)


class Bg:
    def __init__(self):
        self.q = []

    def add(self, gen, period=2):
        self.q.append([gen, period, period])

    def tick(self):
        for item in list(self.q):
            item[2] -= 1
            if item[2] <= 0:
                item[2] = item[1]
                try:
                    next(item[0])
                except StopIteration:
                    self.q.remove(item)

    def drain(self):
        while self.q:
            for item in list(self.q):
                try:
                    next(item[0])
                except StopIteration:
                    self.q.remove(item)


def rms_rstd_gen(k, C, src, src_bufs, SQ, PST, RSTD, ntok, fuse_sq=False):
    s = k.s
    s.op("act", lambda e: e.activation(out=SQ[:, :, 0:ntok], in_=src, func=AF.Square),
         reads=src_bufs, writes=SQ.all())
    if not fuse_sq:
        yield
    for kc in range(8):
        s.op("pe", lambda e, kc=kc: e.matmul(PST[:, 0:ntok], lhsT=C["ones_m"][:, :], rhs=SQ[:, kc, 0:ntok],
                                             start=(kc == 0), stop=(kc == 7)),
             reads=SQ.all() + C["ones_m"].all(), writes=PST.all())
    yield
    s.op("act", lambda e: e.activation(out=RSTD[:, 0:ntok], in_=PST[:, 0:ntok], func=AF.Ln, bias=C["eps"][:, 0:1]),
         reads=PST.all() + C["eps"].all(), writes=RSTD.all())
    yield
    s.op("act", lambda e: e.activation(out=RSTD[:, 0:ntok], in_=RSTD[:, 0:ntok], func=AF.Exp, scale=-0.5),
         reads=RSTD.all(), writes=RSTD.all())


def ffn_stage(k, C, XT, PS, ffns, next_gi=None):
    s = k.s
    G_ = C["gains"]
    with k.phase() as ph:
        HTG = C["HTG"]
        HTs = []
        for i in range(2):
            hv = T(f"HTv{i}", HTG.t[:, :, i * 1024:(i + 1) * 1024])
            hv.bufs = HTG.bufs[2 * i:2 * i + 2]
            HTs.append(hv)
        ACTT = ph.sb("ACTT", [128, 11, 1024], BF16, parts=22)
        YT = ph.sb("YT", [128, 8, 1024], F32, parts=16)
        SQ = [ph.sb(f"SQ{i}", [128, 8, 512], BF16) for i in range(2)]
        RSTD = [ph.sb(f"RSTD{i}", [128, 512], F32) for i in range(2)]
        WIN = [ph.sb(f"WIN{i}", [128, 8, 256], BF16) for i in range(3)]
        WOUT = [ph.sb(f"WOUT{i}", [128, 11, 128], BF16) for i in range(3)]
        SG = [ph.sb(f"SG{i}", [128, 512], F32) for i in range(2)]
        PG = [PS[0], PS[1]]
        PU = [PS[2], PS[3]]
        PY = [PS[4], PS[5]]
        PST = [PS[6], PS[7]]
        st = {"win": 0, "wout": 0, "pi": 0, "ni": 0}
        jobs = [(f, B) for f in range(len(ffns)) for B in range(2)]

        bg = Bg()

        def prenorm(ji):
            f, B = jobs[ji]
            HT = HTs[ji % 2]
            gi_pre = ffns[f][2]
            for sb_ in range(2):
                tok = B * 1024 + sb_ * 512
                xb = XT.b(B * 2 + sb_)
                n_ = st["ni"] % 2
                st["ni"] += 1
                yield from rms_rstd_gen(k, C, XT[:, :, tok:tok + 512], xb, SQ[n_], PST[n_], RSTD[n_], 512)
                for kc in range(8):
                    if kc == 4:
                        yield
                    s.op("dve", lambda e, kc=kc, tok=tok, sb_=sb_, n_=n_: e.scalar_tensor_tensor(
                        out=HT[:, kc, sb_ * 512:(sb_ + 1) * 512], in0=XT[:, kc, tok:tok + 512],
                        scalar=G_[:, gi_pre, kc:kc + 1], in1=RSTD[n_][:, :], op0=ALU.mult, op1=ALU.mult),
                        reads=xb + RSTD[n_].all() + G_.all(), writes=HT.b(sb_))

        def postnorm(ji):
            f, B = jobs[ji]
            gi_post = ffns[f][3]
            for sb_ in range(2):
                tok = B * 1024 + sb_ * 512
                rhs_sl = slice(sb_ * 512, (sb_ + 1) * 512)
                ybs = YT.b(*[dc * 2 + sb_ for dc in range(8)])
                xb = XT.b(B * 2 + sb_)
                n_ = st["ni"] % 2
                st["ni"] += 1
                yield from rms_rstd_gen(k, C, YT[:, :, rhs_sl], ybs, SQ[n_], PST[n_], RSTD[n_], 512)
                for dc in range(8):
                    if dc % 2 == 0 and dc > 0:
                        yield
                    s.op("dve", lambda e, dc=dc, rhs_sl=rhs_sl, n_=n_: e.scalar_tensor_tensor(
                        out=YT[:, dc, rhs_sl], in0=YT[:, dc, rhs_sl], scalar=G_[:, gi_post, dc:dc + 1],
                        in1=RSTD[n_][:, :], op0=ALU.mult, op1=ALU.mult),
                        reads=YT.b(dc * 2 + sb_) + RSTD[n_].all() + G_.all(), writes=YT.b(dc * 2 + sb_))
                    s.op("dve", lambda e, dc=dc, rhs_sl=rhs_sl, tok=tok: e.tensor_tensor(
                        out=XT[:, dc, tok:tok + 512], in0=XT[:, dc, tok:tok + 512], in1=YT[:, dc, rhs_sl], op=ALU.add),
                        reads=YT.b(dc * 2 + sb_) + xb, writes=xb)

        def up(ji, G, after_first=None):
            f, B = jobs[ji]
            HT = HTs[ji % 2]
            w_in_d = ffns[f][0]
            for jj in range(11):
                j = G * 11 + jj
                W = WIN[st["win"] % 3]
                st["win"] += 1
                s.dma("pool", W[:, :, :], w_in_d[j].rearrange("p (kc c) -> p kc c", kc=8), writes=W.all())
                for sb_ in range(2):
                    pg, pu, sg = PG[st["pi"] % 2], PU[st["pi"] % 2], SG[st["pi"] % 2]
                    st["pi"] += 1
                    rhs_sl = slice(sb_ * 512, (sb_ + 1) * 512)
                    for kc in range(8):
                        s.op("pe", lambda e, kc=kc, pg=pg, W=W, rhs_sl=rhs_sl: e.matmul(
                            pg[:, :], lhsT=W[:, kc, 0:128], rhs=HT[:, kc, rhs_sl], start=(kc == 0), stop=(kc == 7)),
                            reads=W.all() + HT.b(sb_), writes=pg.all())
                    for kc in range(8):
                        s.op("pe", lambda e, kc=kc, pu=pu, W=W, rhs_sl=rhs_sl: e.matmul(
                            pu[:, :], lhsT=W[:, kc, 128:256], rhs=HT[:, kc, rhs_sl], start=(kc == 0), stop=(kc == 7)),
                            reads=W.all() + HT.b(sb_), writes=pu.all())
                    s.op("act", lambda e, pg=pg, sg=sg: e.activation(out=sg[:, :], in_=pg[:, :], func=AF.Silu),
                         reads=pg.all(), writes=sg.all())
                    s.op("dve", lambda e, pu=pu, sg=sg, jj=jj, rhs_sl=rhs_sl: e.tensor_tensor(
                        out=ACTT[:, jj, rhs_sl], in0=sg[:, :], in1=pu[:, :], op=ALU.mult),
                        reads=sg.all() + pu.all(), writes=ACTT.b(jj * 2 + sb_))
                    bg.tick()
                if jj == 0 and after_first is not None:
                    after_first()

        def down(ji, G):
            f, B = jobs[ji]
            w_out_d = ffns[f][1]
            for dc in range(8):
                W = WOUT[st["wout"] % 3]
                st["wout"] += 1
                s.dma("pool", W[:, :, :], w_out_d[G, dc].rearrange("p (jj c) -> p jj c", jj=11), writes=W.all())
                for sb_ in range(2):
                    py = PY[st["pi"] % 2]
                    st["pi"] += 1
                    rhs_sl = slice(sb_ * 512, (sb_ + 1) * 512)
                    for jj in range(11):
                        s.op("pe", lambda e, jj=jj, py=py, W=W, rhs_sl=rhs_sl: e.matmul(
                            py[:, :], lhsT=W[:, jj, :], rhs=ACTT[:, jj, rhs_sl], start=(jj == 0), stop=(jj == 10)),
                            reads=W.all() + ACTT.b(jj * 2 + sb_), writes=py.all())
                    yb = YT.b(dc * 2 + sb_)
                    if G == 0:
                        s.op("act", lambda e, py=py, dc=dc, rhs_sl=rhs_sl: e.activation(
                            out=YT[:, dc, rhs_sl], in_=py[:, :], func=AF.Copy),
                            reads=py.all(), writes=yb)
                    else:
                        s.op("dve", lambda e, py=py, dc=dc, rhs_sl=rhs_sl: e.tensor_tensor(
                            out=YT[:, dc, rhs_sl], in0=YT[:, dc, rhs_sl], in1=py[:, :], op=ALU.add),
                            reads=py.all() + yb, writes=yb)
                    bg.tick()

        def next_prenorm():
            HT = HTs[0]
            for sb_ in range(2):
                tok = sb_ * 512
                xb = XT.b(sb_)
                n_ = st["ni"] % 2
                st["ni"] += 1
                yield from rms_rstd_gen(k, C, XT[:, :, tok:tok + 512], xb, SQ[n_], PST[n_], RSTD[n_], 512)
                for kc in range(8):
                    if kc == 4:
                        yield
                    s.op("dve", lambda e, kc=kc, tok=tok, sb_=sb_, n_=n_: e.scalar_tensor_tensor(
                        out=HT[:, kc, sb_ * 512:(sb_ + 1) * 512], in0=XT[:, kc, tok:tok + 512],
                        scalar=G_[:, next_gi, kc:kc + 1], in1=RSTD[n_][:, :], op0=ALU.mult, op1=ALU.mult),
                        reads=xb + RSTD[n_].all() + G_.all(), writes=HT.b(sb_))

        n = len(jobs)
        assert n % 2 == 0
        if C["ht_ready"] is not None and C["ht_ready"] == (ffns[0][2], (0, 1)):
            pass
        else:
            bg.add(prenorm(0))
            bg.drain()
        C["ht_ready"] = None
        for ji in range(n):
            up(ji, 0, after_first=(lambda ji=ji: bg.add(postnorm(ji - 1), 1)) if ji > 0 else None)
            bg.drain()
            down(ji, 0)
            if ji + 1 < n:
                bg.add(prenorm(ji + 1), 1)
            elif next_gi is not None:
                bg.add(next_prenorm(), 1)
                C["ht_ready"] = (next_gi, (0, 1))
            up(ji, 1)
            bg.drain()
            down(ji, 1)
        bg.add(postnorm(n - 1))
        bg.drain()


def prenorm_to_HT(k, C, ph, XT, HT, PS, gi_pre, col_off=0):
    s = k.s
    G_ = C["gains"]
    with k.phase() as p2:
        SQ = [p2.sb(f"SQ{i}", [128, 8, 512], BF16) for i in range(2)]
        RSTD = [p2.sb(f"RSTD{i}", [128, 512], F32) for i in range(2)]
        bg = Bg()

        def chain(tb):
            tok = tb * 512
            xb = XT.b(tb)
            yield from rms_rstd_gen(k, C, XT[:, :, tok:tok + 512], xb, SQ[tb % 2], PS[6 + tb % 2], RSTD[tb % 2], 512)
            for kc in range(8):
                if kc == 4:
                    yield
                s.op("dve", lambda e, kc=kc: e.scalar_tensor_tensor(
                    out=HT[:, kc, col_off + tok:col_off + tok + 512], in0=XT[:, kc, tok:tok + 512],
                    scalar=G_[:, gi_pre, kc:kc + 1], in1=RSTD[tb % 2][:, :], op0=ALU.mult, op1=ALU.mult),
                    reads=xb + RSTD[tb % 2].all() + G_.all(), writes=HT.b(tb))

        skip = ()
        if C["ht_ready"] is not None and C["ht_ready"][0] == gi_pre and col_off == 0:
            skip = C["ht_ready"][1]
        C["ht_ready"] = None
        for tb in range(4):
            if tb in skip:
                continue
            bg.add(chain(tb), 1)
            bg.tick()
            bg.tick()
        bg.drain()


def outproj_postnorm(k, C, XT, PS, OT, wo_d, gi_post, next_gi=None, WO=None):
    s = k.s
    G_ = C["gains"]
    with k.phase() as p3:
        if WO is None:
            WO = T("WOv", C["HTG"].t[:, :, 1024:2048])
            WO.bufs = C["HTG"].bufs[2:4]
            for kc in range(8):
                s.dma("pool", WO[:, kc, :], wo_d[kc * 128:(kc + 1) * 128, :], writes=WO.all())
        YTs = [p3.sb(f"YT{i}", [128, 8, 512], F32, parts=8) for i in range(2)]
        SQ1 = p3.sb("SQ", [128, 8, 512], BF16)
        RSTD = [p3.sb(f"RSTD{i}", [128, 512], F32) for i in range(2)]
        bg = Bg()

        def chain(tb):
            tok = tb * 512
            YT = YTs[tb % 2]
            yield from rms_rstd_gen(k, C, YT[:, :, :], YT.all(), SQ1, PS[6 + tb % 2], RSTD[tb % 2], 512, fuse_sq=True)
            xb = XT.b(tb)
            for dc in range(8):
                if dc % 2 == 0 and dc > 0:
                    yield
                s.op("dve", lambda e, dc=dc: e.scalar_tensor_tensor(
                    out=YT[:, dc, :], in0=YT[:, dc, :], scalar=G_[:, gi_post, dc:dc + 1],
                    in1=RSTD[tb % 2][:, :], op0=ALU.mult, op1=ALU.mult),
                    reads=YT.b(dc) + RSTD[tb % 2].all() + G_.all(), writes=YT.b(dc))
                s.op("dve", lambda e, dc=dc: e.tensor_tensor(
                    out=XT[:, dc, tok:tok + 512], in0=XT[:, dc, tok:tok + 512], in1=YT[:, dc, :], op=ALU.add),
                    reads=YT.b(dc) + xb, writes=xb)

        if next_gi is not None:
            RSTDn = p3.sb("RSTDn", [128, 512], F32)
        HTG = C["HTG"]

        def next_prenorm():
            for sb_ in range(2):
                tok = sb_ * 512
                xb = XT.b(sb_)
                yield from rms_rstd_gen(k, C, XT[:, :, tok:tok + 512], xb, SQ1, PS[0], RSTDn, 512, fuse_sq=True)
                for kc in range(8):
                    if kc == 4:
                        yield
                    s.op("dve", lambda e, kc=kc, tok=tok: e.scalar_tensor_tensor(
                        out=HTG[:, kc, tok:tok + 512], in0=XT[:, kc, tok:tok + 512],
                        scalar=G_[:, next_gi, kc:kc + 1], in1=RSTDn[:, :], op0=ALU.mult, op1=ALU.mult),
                        reads=xb + RSTDn.all() + G_.all(), writes=HTG.b(sb_))

        pi = 0
        for tb in range(4):
            tok = tb * 512
            YT = YTs[tb % 2]
            if tb == 3 and next_gi is not None:
                bg.add(next_prenorm(), 1)
                C["ht_ready"] = (next_gi, (0, 1))
            for dc in range(8):
                pp = PS[4 + pi % 2]
                pi += 1
                for kc in range(8):
                    s.op("pe", lambda e, kc=kc, dc=dc, pp=pp, tok=tok: e.matmul(
                        pp[:, :], lhsT=WO[:, kc, dc * 128:(dc + 1) * 128], rhs=OT[:, kc, tok:tok + 512],
                        start=(kc == 0), stop=(kc == 7)),
                        reads=WO.all() + OT.all(), writes=pp.all())
                s.op("act", lambda e, dc=dc, pp=pp, YT=YT: e.activation(out=YT[:, dc, :], in_=pp[:, :], func=AF.Copy),
                     reads=pp.all(), writes=YT.b(dc))
                bg.tick()
            bg.drain()
            bg.add(chain(tb), 1)
        bg.drain()


def attn_stage(k, C, XT, PS, wqkv_d, wo_d, gi_pre, gi_post, next_gi=None):
    s = k.s
    with k.phase() as ph:
        OT = ph.sb("OT", [128, 8, SEQ], BF16, parts=1)
        with k.phase() as pab:
            HT = C["HTG"]
            prenorm_to_HT(k, C, pab, XT, HT, PS, gi_pre)
            with k.phase() as pb:
                NEGM = pb.sb("negm", [128, 4, 512], BF16)
                ZB = pb.sb("zb", [128, 512], BF16)
                s.op("dve", lambda e: e.memset(ZB[:, :], 0.0), writes=ZB.all())
                for d in range(4):
                    s.op("pool", lambda e, d=d: e.affine_select(
                        out=NEGM[:, d, :], in_=ZB[:, :], pattern=[[1, 512]], compare_op=ALU.is_gt,
                        fill=-30000.0, base=-128 * d, channel_multiplier=-1),
                        reads=ZB.all(), writes=NEGM.all())
                QT = [pb.sb(f"QT{i}", [128, SEQ], BF16) for i in range(2)]
                KT = [pb.sb(f"KT{i}", [128, SEQ], BF16) for i in range(2)]
                V = [pb.sb(f"V{i}", [128, 16, 128], BF16) for i in range(2)]
                W = [pb.sb(f"WQKV{i}", [128, 8, 384], BF16) for i in range(1)]
                OTOK = [pb.sb(f"OTOK{i}", [128, 16, 128], BF16) for i in range(2)]
                E = [pb.sb(f"E{i}", [128, 512], F32) for i in range(3)]
                SP = [pb.sb(f"SP{i}", [128, 512], BF16) for i in range(5)]
                ATT = [pb.sb(f"ATT{i}", [128, 512], BF16) for i in range(3)]
                OACC = [pb.sb(f"OACC{i}", [128, 4, 64], F32) for i in range(2)]
                CACC2 = pb.sb("CACC2", [128, 2, 4], F32, parts=2)
                FS2 = [pb.sb(f"FS2_{i}", [128, 2, 4], F32) for i in range(4)]
                PZ = [PS[0], PS[1], PS[2], PS[3], PS[4]]
                PO = [PS[5], PS[6]]
                PP = [PS[7]]
                st = {"pi": 0}

                bg = Bg()

                def pre_hp(hp):
                    w = W[0]
                    qt, kt_, v = QT[hp % 2], KT[hp % 2], V[hp % 2]
                    s.dma("pool", w[:, :, :], wqkv_d[hp].rearrange("p (kc c) -> p kc c", kc=8), writes=w.all())
                    for which in range(2):
                        for tb in range(4):
                            pp = PP[st["pi"] % len(PP)]
                            st["pi"] += 1
                            for kc in range(8):
                                s.op("pe", lambda e, kc=kc, pp=pp, w=w, which=which, tb=tb: e.matmul(
                                    pp[:, :], lhsT=w[:, kc, which * 128:(which + 1) * 128],
                                    rhs=HT[:, kc, tb * 512:(tb + 1) * 512], start=(kc == 0), stop=(kc == 7)),
                                    reads=w.all() + HT.b(tb), writes=pp.all())
                            if which == 0:
                                s.op("dve", lambda e, pp=pp, qt=qt, tb=tb: e.tensor_scalar(
                                    out=qt[:, tb * 512:(tb + 1) * 512], in0=pp[:, :], scalar1=0.125, scalar2=None,
                                    op0=ALU.mult),
                                    reads=pp.all(), writes=qt.all())
                            else:
                                s.op("dve", lambda e, pp=pp, kt_=kt_, tb=tb: e.tensor_copy(
                                    out=kt_[:, tb * 512:(tb + 1) * 512], in_=pp[:, :]),
                                    reads=pp.all(), writes=kt_.all())
                            yield
                    for tg in range(4):
                        pp = PP[st["pi"] % len(PP)]
                        st["pi"] += 1
                        for tt in range(4):
                            tok = (tg * 4 + tt) * 128
                            for kc in range(8):
                                s.op("pe", lambda e, kc=kc, pp=pp, w=w, tt=tt, tok=tok: e.matmul(
                                    pp[:, tt * 128:(tt + 1) * 128], lhsT=HT[:, kc, tok:tok + 128],
                                    rhs=w[:, kc, 256:384], start=(kc == 0), stop=(kc == 7)),
                                    reads=w.all() + HT.b(tg), writes=pp.all())
                        s.op("dve", lambda e, pp=pp, v=v, tg=tg: e.tensor_copy(
                            out=v[:, tg * 4:(tg + 1) * 4, :], in_=pp[:, :].rearrange("p (a b) -> p a b", a=4)),
                            reads=pp.all(), writes=v.all())
                        yield

                def post_hp(hp):
                    otok = OTOK[hp % 2]
                    for tg in range(4):
                        pp = PP[st["pi"] % len(PP)]
                        st["pi"] += 1
                        for tt in range(4):
                            s.op("pe", lambda e, pp=pp, tt=tt, tg=tg, otok=otok: e.matmul(
                                pp[:, tt * 128:(tt + 1) * 128], lhsT=otok[:, tg * 4 + tt, :], rhs=C["ident"][:, :],
                                start=True, stop=True),
                                reads=otok.all() + C["ident"].all(), writes=pp.all())
                        s.op("dve", lambda e, pp=pp, tg=tg, hp=hp: e.tensor_copy(
                            out=OT[:, hp, tg * 512:(tg + 1) * 512], in_=pp[:, :]),
                            reads=pp.all(), writes=OT.all())

                units = []
                for hp in range(8):
                    for g in range(4):
                        for kt in range(4 * g + 3, -1, -1):
                            for hh in range(2):
                                units.append((hp, hh, g, kt))
                n = len(units)
                NPZ, NSP, NATT, NPO, NE = 5, 5, 3, 2, 3

                def u_(i):
                    hp, hh, g, kt = units[i]
                    d = kt - 4 * g
                    return hp, hh, g, kt, d, slice(hh * 64, (hh + 1) * 64)

                def c0_(i):
                    hp, hh, g, kt = units[i]
                    return max(kt - 4 * g, 0) * 128

                def s0_qk(i):
                    hp, hh, g, kt, d, hs = u_(i)
                    if hh == 0 and g == 0 and kt == 3:
                        if hp == 0:
                            bg.add(pre_hp(0))
                        bg.drain()
                    if hh == 0 and g == 0 and kt == 0 and hp + 1 < 8:
                        bg.add(pre_hp(hp + 1), 5)
                    pz, qt, kt_ = PZ[i % NPZ], QT[hp % 2], KT[hp % 2]
                    q0 = g * 512
                    c0 = c0_(i)
                    s.op("pe", lambda e: e.matmul(pz[:, c0:512], lhsT=kt_[hs, kt * 128:(kt + 1) * 128],
                                                  rhs=qt[hs, q0 + c0:q0 + 512], start=True, stop=(d < 0)),
                         reads=kt_.all() + qt.all(), writes=pz.all())
                    if d >= 0:
                        s.op("pe", lambda e: e.matmul(pz[:, c0:c0 + 128], lhsT=C["ident"][:, :], rhs=NEGM[:, d, c0:c0 + 128],
                                                      start=False, stop=True),
                             reads=C["ident"].all() + NEGM.all(), writes=pz.all())

                def s1_exp(i):
                    pz, e_ = PZ[i % NPZ], E[i % NE]
                    c0 = c0_(i)
                    s.op("act", lambda e: e.activation(out=e_[:, c0:512], in_=pz[:, c0:512], func=AF.Exp),
                         reads=pz.all(), writes=e_.all())

                def s2_ln(i):
                    e_, sp = E[i % NE], SP[i % NSP]
                    c0 = c0_(i)
                    s.op("act", lambda e: e.activation(out=sp[:, c0:512], in_=e_[:, c0:512], func=AF.Ln,
                                                       bias=C["one_f"][:, 0:1]),
                         reads=e_.all() + C["one_f"].all(), writes=sp.all())

                def s3_tri(i):
                    pz, sp = PZ[i % NPZ], SP[i % NSP]
                    c0 = c0_(i)
                    s.op("pe", lambda e: e.matmul(pz[:, c0:512], lhsT=C["ntri"][:, :], rhs=sp[:, c0:512], start=False,
                                                  stop=True, skip_group_check=True),
                         reads=sp.all() + C["ntri"].all(), writes=pz.all())

                def s4_att(i):
                    pz, att = PZ[i % NPZ], ATT[i % NATT]
                    c0 = c0_(i)
                    s.op("act", lambda e: e.activation(out=att[:, c0:512], in_=pz[:, c0:512], func=AF.Exp),
                         reads=pz.all(), writes=att.all())

                def s5_av(i):
                    hp, hh, g, kt, d, hs = u_(i)
                    qlo = max(d, 0)
                    sp, att, po, v = SP[i % NSP], ATT[i % NATT], PO[i % NPO], V[hp % 2]
                    for qi in range(qlo, 4):
                        s.op("pe", lambda e, qi=qi: e.matmul(
                            po[:, qi * 64:(qi + 1) * 64], lhsT=att[:, qi * 128:(qi + 1) * 128],
                            rhs=v[:, kt, hs], start=True, stop=True),
                            reads=att.all() + v.all(), writes=po.all())
                        s.op("pe", lambda e, qi=qi: e.matmul(
                            po[:, 256 + qi:257 + qi], lhsT=sp[:, qi * 128:(qi + 1) * 128],
                            rhs=C["ones_col"][:, 0:1], start=True, stop=True),
                            reads=sp.all() + C["ones_col"].all(), writes=po.all())

                def s6_acc(i):
                    hp, hh, g, kt, d, hs = u_(i)
                    qlo = max(d, 0)
                    span = hh
                    po = PO[i % NPO]
                    oacc, fs2 = OACC[span % 2], FS2[(i // 2) % 4]
                    cb = CACC2.b(hh)
                    otok = OTOK[hp % 2]
                    if kt != 4 * g + 3:
                        if hh == 0:
                            s.op("act", lambda e: e.activation(out=fs2[:, :, :], in_=CACC2[:, :, :], func=AF.Exp, scale=-1.0),
                                 reads=CACC2.all(), writes=fs2.all())
                        s.op("dve", lambda e: e.tensor_tensor(
                            out=CACC2[:, hh, qlo:4], in0=CACC2[:, hh, qlo:4], in1=po[:, 256 + qlo:260], op=ALU.add),
                            reads=po.all() + cb, writes=cb)
                        for qi in range(qlo, 4):
                            s.op("dve", lambda e, qi=qi: e.scalar_tensor_tensor(
                                out=oacc[:, qi, :], in0=po[:, qi * 64:(qi + 1) * 64], scalar=fs2[:, hh, qi:qi + 1],
                                in1=oacc[:, qi, :], op0=ALU.mult, op1=ALU.add),
                                reads=po.all() + fs2.all() + oacc.all(), writes=oacc.all())
                    else:
                        if qlo > 0:
                            s.op("dve", lambda e: e.memset(oacc[:, 0:qlo, :], 0.0), writes=oacc.all())
                            s.op("dve", lambda e: e.memset(CACC2[:, hh, 0:qlo], 0.0), writes=cb)
                        s.op("dve", lambda e: e.tensor_copy(
                            out=oacc[:, qlo:4, :], in_=po[:, qlo * 64:256].rearrange("p (a b) -> p a b", b=64)),
                            reads=po.all(), writes=oacc.all())
                        s.op("dve", lambda e: e.tensor_copy(out=CACC2[:, hh, qlo:4], in_=po[:, 256 + qlo:260]),
                             reads=po.all(), writes=cb)
                    if kt == 0:
                        s.op("dve", lambda e: e.tensor_copy(out=otok[:, 4 * g:4 * g + 4, hs], in_=oacc[:, :, :]),
                             reads=oacc.all(), writes=otok.all())
                        if hh == 1 and g == 3:
                            post_hp(hp)

                stages = ((0, s0_qk), (1, s1_exp), (2, s2_ln), (3, s3_tri), (4, s4_att), (5, s5_av), (6, s6_acc))
                for i in range(n + 6):
                    for lag, fn in stages:
                        if 0 <= i - lag < n:
                            fn(i - lag)
                    bg.tick()
        if "dbg_OT" in C:
            s.dma("sp", C["dbg_OT"][:, :, :], OT[:, :, :], reads=OT.all(), writes=C["dbg_OT"].all())
        outproj_postnorm(k, C, XT, PS, OT, wo_d, gi_post, next_gi)


AX = mybir.AxisListType


class Ref:
    __slots__ = ("ap", "bufs")

    def __init__(self, ap, bufs):
        self.ap = ap
        self.bufs = bufs


class _RefMaker:
    def __init__(self, t):
        self.t = t

    def __getitem__(self, key):
        return Ref(self.t.t[key], self.t.all())


def rf(t):
    return _RefMaker(t)


def _b(*refs):
    out = []
    for r in refs:
        if isinstance(r, Ref):
            out += r.bufs
    return out


def _a(x):
    return x.ap if isinstance(x, Ref) else x


def e_tt(s, eng, out, a, b, op):
    return s.op(eng, lambda E: E.tensor_tensor(out=out.ap, in0=a.ap, in1=b.ap, op=op), reads=_b(a, b), writes=out.bufs)


def e_ts(s, eng, out, a, s1, s2, op0, op1=None):
    if op1 is None:
        return s.op(eng, lambda E: E.tensor_scalar(out=out.ap, in0=a.ap, scalar1=_a(s1), scalar2=None, op0=op0),
                    reads=_b(a, s1), writes=out.bufs)
    return s.op(eng, lambda E: E.tensor_scalar(out=out.ap, in0=a.ap, scalar1=_a(s1), scalar2=_a(s2), op0=op0, op1=op1),
                reads=_b(a, s1, s2), writes=out.bufs)


def e_stt(s, eng, out, a, sc, b, op0, op1):
    return s.op(eng, lambda E: E.scalar_tensor_tensor(out=out.ap, in0=a.ap, scalar=_a(sc), in1=b.ap, op0=op0, op1=op1),
                reads=_b(a, sc, b), writes=out.bufs)


def e_act(s, out, a, func, bias=None, scale=None):
    kw = {}
    if bias is not None:
        kw["bias"] = _a(bias)
    if scale is not None:
        kw["scale"] = _a(scale)
    return s.op("act", lambda E: E.activation(out=out.ap, in_=a.ap, func=func, **kw), reads=_b(a, bias, scale),
                writes=out.bufs)


def e_mm(s, out, lhsT, rhs, start=True, stop=True):
    return s.op("pe", lambda E: E.matmul(out.ap, lhsT=lhsT.ap, rhs=rhs.ap, start=start, stop=stop),
                reads=_b(lhsT, rhs), writes=out.bufs)


def e_copy(s, eng, out, a):
    if eng == "act":
        return e_act(s, out, a, AF.Copy)
    return s.op(eng, lambda E: E.tensor_copy(out=out.ap, in_=a.ap), reads=_b(a), writes=out.bufs)


def e_memset(s, eng, out, val):
    return s.op(eng, lambda E: E.memset(out.ap, val), writes=out.bufs)


GN_EPS = 64e-5
STAGGER = 3
NEG_EXP_HALF = -0.6065306597126334


def mixer0_stage(k, C, XT, PS, D, gi_pre, gi_post, next_gi=None):
    s = k.s
    with k.phase() as ph:
        OT = ph.sb("OT", [128, 8, SEQ], BF16, parts=1)
        with k.phase() as pab:
            HT = C["HTG"]
            prenorm_to_HT(k, C, pab, XT, HT, PS, gi_pre)
            with k.phase() as pl:
                rglru_part(k, C, pl, HT, OT, PS, D)
            with k.phase() as pr:
                rwkv_part(k, C, pr, HT, OT, PS, D, XT)
        if "dbg_OT" in D:
            s.dma("sp", D["dbg_OT"][:, :, :], OT[:, :, :], reads=OT.all(), writes=D["dbg_OT"].all())
        outproj_postnorm(k, C, XT, PS, OT, D["l0_w_out"], gi_post, next_gi)


def rglru_part(k, C, p, HT, OT, PS, D):
    s = k.s
    PL = p.sb("PL", [128, 4, 8], F32)
    s.dma("sp", PL[:, :, :], D["l0_pl"][:, :].rearrange("p (c n) -> p c n", c=4), writes=PL.all())
    C1 = p.sb("C1", [128, 4], F32)
    e_act(s, rf(C1)[:, :], rf(PL)[:, :, 7], AF.Exp, scale=-1.0)
    e_act(s, rf(C1)[:, :], rf(C1)[:, :], AF.Ln, bias=rf(C["one_f"])[:, 0:1])
    e_ts(s, "dve", rf(C1)[:, :], rf(C1)[:, :], -8.0, None, ALU.mult)
    GAW = p.sb("GAW", [128, 4, 128], BF16)
    GXW = p.sb("GXW", [128, 4, 128], BF16)
    e_memset(s, "dve", rf(GAW)[:, :, :], 0.0)
    e_memset(s, "dve", rf(GXW)[:, :, :], 0.0)
    for n in range(8):
        ps_ = slice((n % 2) * 64, (n % 2) * 64 + 64)
        s.dma("pool", GAW[ps_, n // 2, ps_], D["l0_gate_a_w"][n], writes=GAW.all())
        s.dma("pool", GXW[ps_, n // 2, ps_], D["l0_gate_x_w"][n], writes=GXW.all())
    W = [p.sb(f"WL{i}", [128, 8, 256], BF16) for i in range(2)]
    XBs = [p.sb(f"XB{i}", [128, 515], F32) for i in range(2)]
    HHs = [[p.sb(f"HH{j}_{i}", [128, 512], F32) for i in range(2)] for j in range(2)]
    ts_ = [{n: p.sb(f"{n}{j}", [128, 512], F32) for n in ("GB", "XC", "R", "IG", "A", "U", "T1", "T2")} for j in range(2)]
    XCbs = [p.sb(f"XCb{j}", [128, 512], BF16) for j in range(2)]

    def unit(c, tb, j):
        w = W[j]
        XB, t_, XCb = XBs[j], ts_[j], XCbs[j]
        col = lambda n: rf(PL)[:, c, n:n + 1]
        tok = tb * 512
        px, pg = PS[2 * j], PS[2 * j + 1]
        for kc in range(8):
            e_mm(s, rf(px)[:, :], rf(w)[:, kc, 0:128], Ref(HT[:, kc, tok:tok + 512], HT.b(tb)), kc == 0, kc == 7)
        for kc in range(8):
            e_mm(s, rf(pg)[:, :], rf(w)[:, kc, 128:256], Ref(HT[:, kc, tok:tok + 512], HT.b(tb)), kc == 0, kc == 7)
        if tb == 0:
            e_memset(s, "dve", rf(XB)[:, 0:3], 0.0)
        else:
            e_copy(s, "dve", rf(XB)[:, 0:3], rf(XB)[:, 512:515])
        yield
        e_copy(s, "act", rf(XB)[:, 3:515], rf(px)[:, :])
        e_copy(s, "act", rf(t_["GB"])[:, :], rf(pg)[:, :])
        yield
        XC = t_["XC"]
        e_ts(s, "dve", rf(XC)[:, :], rf(XB)[:, 3:515], col(3), col(4), ALU.mult, ALU.add)
        for i in range(3):
            e_stt(s, "dve", rf(XC)[:, :], rf(XB)[:, i:i + 512], col(i), rf(XC)[:, :], ALU.mult, ALU.add)
        GB, T2 = t_["GB"], t_["T2"]
        e_act(s, rf(T2)[:, :], rf(GB)[:, :], AF.Gelu_apprx_tanh)
        yield
        e_copy(s, "act", rf(XCb)[:, :], rf(XC)[:, :])
        yield
        pr_, pig = PS[4 + 2 * j], PS[5 + 2 * j]
        e_mm(s, rf(pr_)[:, :], rf(GAW)[:, c, :], rf(XCb)[:, :])
        e_mm(s, rf(pig)[:, :], rf(GXW)[:, c, :], rf(XCb)[:, :])
        yield
        e_act(s, rf(t_["R"])[:, :], rf(pr_)[:, :], AF.Sigmoid, bias=col(5))
        e_act(s, rf(t_["IG"])[:, :], rf(pig)[:, :], AF.Sigmoid, bias=col(6))
        yield
        A = t_["A"]
        e_act(s, rf(A)[:, :], rf(t_["R"])[:, :], AF.Exp, scale=rf(C1)[:, c:c + 1])
        T1, U = t_["T1"], t_["U"]
        e_tt(s, "dve", rf(U)[:, :], rf(t_["IG"])[:, :], rf(XC)[:, :], ALU.mult)
        yield
        e_tt(s, "dve", rf(T1)[:, :], rf(A)[:, :], rf(A)[:, :], ALU.mult)
        yield
        e_ts(s, "dve", rf(T1)[:, :], rf(T1)[:, :], -1.0, 1.0, ALU.mult, ALU.add)
        yield
        e_act(s, rf(T1)[:, :], rf(T1)[:, :], AF.Sqrt)
        yield
        e_tt(s, "dve", rf(U)[:, :], rf(U)[:, :], rf(T1)[:, :], ALU.mult)
        yield
        H = HHs[j][tb % 2]
        Hp = HHs[j][(tb + 1) % 2]
        init = 0.0 if tb == 0 else Hp[:, 511:512]
        s.op("dve", lambda E: E.tensor_tensor_scan(
            out=H[:, :], data0=A[:, :], data1=U[:, :], initial=init, op0=ALU.mult, op1=ALU.add),
            reads=A.all() + U.all() + (Hp.all() if tb else []), writes=H.all())
        yield
        s.op("dve", lambda E: E.tensor_tensor(
            out=OT[:, 4 + c, tok:tok + 512], in0=H[:, :], in1=T2[:, :], op=ALU.mult),
            reads=H.all() + T2.all(), writes=OT.all())

    for cp in range(2):
        for j in range(2):
            c = 2 * cp + j
            s.dma("pool", W[j][:, :, :], D["l0_w_lru"][c].rearrange("p (kc n) -> p kc n", kc=8), writes=W[j].all())
        for tb in range(4):
            gens = [unit(2 * cp + j, tb, j) for j in range(2)]
            alive = [True, True]
            while any(alive):
                for j in range(2):
                    if alive[j]:
                        try:
                            next(gens[j])
                        except StopIteration:
                            alive[j] = False


def rwkv_part(k, C, p, HT, OT, PS, D, XT):
    s = k.s
    spill = D["xt_spill"]
    for kc in range(8):
        s.dma("sp", spill[:, kc, :], XT[:, kc, :], reads=XT.all(), writes=spill.all())
    PH = p.sb("PH", [128, 4, 8], F32)
    s.dma("sp", PH[:, :, :], D["l0_ph"][:, :].rearrange("p (h n) -> p h n", h=4), writes=PH.all())
    OM = p.sb("OM", [128, 4, 4], F32)
    e_ts(s, "dve", rf(OM)[:, :, 0:3], rf(PH)[:, :, 0:3], -1.0, 1.0, ALU.mult, ALU.add)
    e_ts(s, "dve", rf(OM)[:, :, 3:4], rf(PH)[:, :, 6:7], -1.0, 1.0, ALU.mult, ALU.add)
    RKb = p.sb("RKb", [128, 4], BF16)
    e_copy(s, "dve", rf(RKb)[:, :], rf(PH)[:, :, 7])
    MUL = p.sb("MUL", [128, 3], F32)
    s.dma("sp", MUL[:, :], D["l0_mul"][:, :], writes=MUL.all())
    OML = p.sb("OML", [128, 3], F32)
    e_ts(s, "dve", rf(OML)[:, :], rf(MUL)[:, :], -1.0, 1.0, ALU.mult, ALU.add)
    LNGBs = [p.sb(f"LNGB{i}", [128, 2, 64], F32) for i in range(2)]
    W2 = p.sb("W2", [64, 512], BF16)
    A2 = p.sb("A2", [64, 512], BF16)
    G2 = p.sb("G2", [128, 512], BF16)
    s.dma("pool", W2[:, :], D["l0_w2"][:, :], writes=W2.all())
    s.dma("pool", A2[:, :], D["l0_a2"][:, :], writes=A2.all())
    s.dma("pool", G2[:, :], D["l0_g2"][:, :], writes=G2.all())
    ob = C["ones_bf"]
    BLK = p.sb("BLK", [128, 128], BF16)
    e_memset(s, "dve", rf(BLK)[:, :], 0.0)
    e_memset(s, "dve", rf(BLK)[0:64, 0:64], 1.0)
    e_memset(s, "dve", rf(BLK)[64:128, 64:128], 1.0)
    M512 = p.sb("M512", [128, 512], BF16)
    MUS = p.sb("MUS", [128, 512], BF16)
    MUI = p.sb("MUI", [128, 512], BF16)
    MLS = p.sb("MLS", [128, 512], BF16)
    ID8 = p.sb("ID8", [128, 512], BF16)
    for hh in range(2):
        hs = slice(hh * 64, hh * 64 + 64)
        for dst, pat, cmp_, cm in ((M512, [[0, 8], [1, 64]], ALU.is_gt, 0), (MUS, [[0, 8], [1, 64]], ALU.is_gt, -1),
                                   (MUI, [[0, 8], [1, 64]], ALU.is_ge, -1), (MLS, [[0, 8], [-1, 64]], ALU.is_gt, 1),
                                   (ID8, [[0, 8], [-1, 64]], ALU.is_equal, 1)):
            s.op("pool", lambda E, dst=dst, pat=pat, cmp_=cmp_, cm=cm, hs=hs: E.affine_select(
                out=dst[hs, :], in_=ob[hs, :], pattern=pat, compare_op=cmp_, fill=0.0, base=0, channel_multiplier=cm),
                reads=ob.all(), writes=dst.all())
    ident = C["ident"]

    TW = p.sb("TW", [64, SEQ], BF16)
    AL = p.sb("AL", [64, SEQ], BF16)
    SGL = p.sb("SGL", [128, SEQ], BF16)
    with k.phase() as p0:
        WLo = p0.sb("WLo", [128, 8, 256], BF16)
        s.dma("pool", WLo[:, :, :], D["l0_w_lora"][:, :].rearrange("p (kc n) -> p kc n", kc=8), writes=WLo.all())
        PAl = [p0.sb(f"PAl{i}", [128, 513], F32) for i in range(3)]
        TMPl = [p0.sb(f"TMPl{i}", [128, 512], F32) for i in range(3)]

        def lora_chain(which, c0, c1, npart, dst):
            PA, tmpl = PAl[which], TMPl[which]
            for tb in range(4):
                tok = tb * 512
                pp = PS[which * 2 + tb % 2]
                for kc in range(8):
                    e_mm(s, rf(pp)[0:npart, :], rf(WLo)[:, kc, c0:c1], Ref(HT[:, kc, tok:tok + 512], HT.b(tb)), kc == 0, kc == 7)
                if tb == 0:
                    e_memset(s, "dve", rf(PA)[0:npart, 0:1], 0.0)
                else:
                    e_copy(s, "dve", rf(PA)[0:npart, 0:1], rf(PA)[0:npart, 512:513])
                yield
                e_copy(s, "act", rf(PA)[0:npart, 1:513], rf(pp)[0:npart, :])
                yield
                e_act(s, rf(tmpl)[0:npart, :], rf(PA)[0:npart, 0:512], AF.Copy, scale=rf(MUL)[0:npart, which:which + 1])
                yield
                e_stt(s, "dve", rf(tmpl)[0:npart, :], rf(PA)[0:npart, 1:513], rf(OML)[0:npart, which:which + 1],
                      rf(tmpl)[0:npart, :], ALU.mult, ALU.add)
                yield
                if which == 0:
                    e_act(s, rf(dst)[:, tok:tok + 512], rf(tmpl)[0:64, :], AF.Tanh)
                elif which == 1:
                    e_copy(s, "act", rf(dst)[:, tok:tok + 512], rf(tmpl)[0:64, :])
                else:
                    e_act(s, rf(dst)[:, tok:tok + 512], rf(tmpl)[:, :], AF.Sigmoid)
                yield

        lbg = Bg()
        for which, (c0, c1, npart, dst) in enumerate(((0, 64, 64, TW), (64, 128, 64, AL), (128, 256, 128, SGL))):
            lbg.add(lora_chain(which, c0, c1, npart, dst), 1)
        lbg.drain()

    s.barrier()
    XTf = XT.t
    XTb = XT.t.bitcast(BF16)
    f32n = ("r", "k", "SIG", "A", "KKN", "KH", "CUM", "EC", "EX", "EN", "TMP")
    b16n = ("Rt", "At", "Bt", "Kt", "Bh", "Kh", "RK", "VT", "KK2")
    t64n = ("V64", "BH64", "KH64", "N", "Q", "N2", "Q2", "XA", "LAK", "ARB", "ARK")
    sets = []
    for S in range(2):
        B = {}
        if S == 0:
            B["WH"] = p.sb("WH", [128, 8, 384], BF16)
            B["PA"] = [p.sb(f"PA{i}", [128, 513], F32) for i in range(3)]
            F = {n: p.sb("f_" + n, [128, 512], F32) for n in f32n}
            Bf = {n: p.sb("b_" + n, [128, 512], BF16) for n in b16n}
            T64 = {n: p.sb("t_" + n, [128, 512], BF16) for n in t64n}
        else:
            B["WH"] = T("WH1", XTb[:, 7, 0:3072].rearrange("p (kc n) -> p kc n", kc=8))
            B["PA"] = [T(f"PA1_{i}", XTf[:, 3, i * 513:(i + 1) * 513]) for i in range(3)]
            F = {n: T("f1_" + n, XTf[:, i // 4, (i % 4) * 512:(i % 4) * 512 + 512]) for i, n in enumerate(f32n)}
            bl = list(b16n) + list(t64n)
            vb = {n: T("b1_" + n, XTb[:, 4 + i // 8, (i % 8) * 512:(i % 8) * 512 + 512]) for i, n in enumerate(bl)}
            Bf = {n: vb[n] for n in b16n}
            T64 = {n: vb[n] for n in t64n}
        F["EH"] = F["TMP"]
        F["BA"] = F["SIG"]
        T64["YA"] = T64["LAK"]
        B["F"], B["Bf"], B["T64"] = F, Bf, T64
        B["R0"], B["YF"], B["YQ"], B["GT"] = F["SIG"], F["EX"], F["EN"], F["KH"]
        B["ST"] = p.sb(f"ST{S}", [128, 8, 4], F32)
        B["RKS"] = p.sb(f"RKS{S}", [128, 8], F32)
        B["Pf"] = p.sb(f"Pf{S}", [128, 64], F32)
        B["Pb"] = p.sb(f"Pb{S}", [128, 64], BF16)
        B["RR"] = p.sb(f"RR{S}", [128, 64], BF16)
        B["UB"] = p.sb(f"UB{S}", [128, 64], BF16)
        B["LNGB"] = LNGBs[S]
        B["PY"] = PS[4 + S]
        B["PT1"] = PS[6 + S]
        sets.append(B)
    b3 = lambda r_: Ref(r_.ap.rearrange("p (a b) -> p a b", a=8), r_.bufs)
    HS = (slice(0, 64), slice(64, 128))
    st = {"sci": 0}

    def newps():
        st["sci"] += 1
        return PS[st["sci"] % 4]

    def mm2(out_t, col0, ncol, lhs_fn, rhs_fn, start=True, stop=True):
        for hs in HS:
            e_mm(s, rf(out_t)[hs, col0:col0 + ncol], lhs_fn(hs), rhs_fn(hs), start, stop)

    def unit(hp, gq, B):
        F, Bf, T64, PA, w = B["F"], B["Bf"], B["T64"], B["PA"], B["WH"]
        R0, YF, YQ, GT, ST, RKS = B["R0"], B["YF"], B["YQ"], B["GT"], B["ST"], B["RKS"]
        Pf, Pb, RR, UB, LNGB = B["Pf"], B["Pb"], B["RR"], B["UB"], B["LNGB"]
        hc = lambda n: rf(PH)[:, hp, n:n + 1]
        tok = gq * 512
        for which, nm in enumerate(("r", "k", "v")):
            pp = newps()
            for kc in range(8):
                e_mm(s, rf(pp)[:, :], rf(w)[:, kc, which * 128:(which + 1) * 128],
                     Ref(HT[:, kc, tok:tok + 512], HT.b(gq)), kc == 0, kc == 7)
            pa = PA[which]
            if gq == 0:
                e_memset(s, "dve", rf(pa)[:, 0:1], 0.0)
            else:
                e_copy(s, "dve", rf(pa)[:, 0:1], rf(pa)[:, 512:513])
            e_copy(s, "act", rf(pa)[:, 1:513], rf(pp)[:, :])
            yield
            tmp_ = rf(F["TMP"])[:, :] if which != 1 else rf(F["CUM"])[:, :]
            e_act(s, tmp_, rf(pa)[:, 0:512], AF.Copy, scale=hc(which))
            dst_ = rf(Bf["VT"])[:, :] if nm == "v" else rf(F[nm])[:, :]
            e_stt(s, "dve", dst_, rf(pa)[:, 1:513], rf(OM)[:, hp, which:which + 1], tmp_, ALU.mult, ALU.add)
        r_, k_ = rf(F["r"])[:, :], rf(F["k"])[:, :]
        pz = newps()
        e_mm(s, rf(pz)[:, :], rf(W2)[:, hp * 128:(hp + 1) * 128], rf(TW)[:, tok:tok + 512])
        e_act(s, rf(F["SIG"])[:, :], rf(pz)[:, :], AF.Sigmoid, bias=hc(3))
        pz2 = newps()
        e_mm(s, rf(pz2)[:, :], rf(A2)[:, hp * 128:(hp + 1) * 128], rf(AL)[:, tok:tok + 512])
        e_act(s, rf(F["A"])[:, :], rf(pz2)[:, :], AF.Sigmoid, bias=hc(4))
        yield
        e_ts(s, "dve", rf(F["KKN"])[:, :], k_, hc(5), None, ALU.mult)
        e_act(s, rf(Bf["KK2"])[:, :], rf(F["KKN"])[:, :], AF.Square)
        yield
        pz = newps()
        e_mm(s, rf(pz)[:, :], rf(BLK)[:, :], rf(Bf["KK2"])[:, :])
        e_act(s, rf(F["TMP"])[:, :], rf(pz)[:, :], AF.Sqrt)
        yield
        e_ts(s, "dve", rf(F["TMP"])[:, :], rf(F["TMP"])[:, :], 1e-12, None, ALU.max)
        s.op("dve", lambda E: E.reciprocal(out=F["TMP"][:, :], in_=F["TMP"][:, :]), reads=F["TMP"].all(),
             writes=F["TMP"].all())
        e_tt(s, "dve", rf(F["KKN"])[:, :], rf(F["KKN"])[:, :], rf(F["TMP"])[:, :], ALU.mult)
        e_act(s, rf(F["KH"])[:, :], rf(F["A"])[:, :], AF.Identity, bias=rf(OM)[:, hp, 3:4], scale=hc(6))
        e_tt(s, "dve", rf(F["KH"])[:, :], rf(F["KH"])[:, :], k_, ALU.mult)
        yield
        s.op("dve", lambda E: E.tensor_tensor_scan(out=F["CUM"][:, :], data0=M512[:, :], data1=F["SIG"][:, :],
                                                   initial=0.0, op0=ALU.mult, op1=ALU.add),
             reads=M512.all() + F["SIG"].all(), writes=F["CUM"].all())
        yield
        cum = rf(F["CUM"])[:, :]
        e_act(s, rf(F["EC"])[:, :], cum, AF.Exp, scale=NEG_EXP_HALF)
        e_tt(s, "dve", rf(F["EX"])[:, :], cum, rf(F["SIG"])[:, :], ALU.subtract)
        e_act(s, rf(F["EN"])[:, :], cum, AF.Exp, scale=-NEG_EXP_HALF)
        cum3 = b3(cum)
        cend = Ref(cum3.ap[:, :, 63:64].to_broadcast([128, 8, 64]), cum.bufs)
        e_tt(s, "dve", b3(rf(F["EH"])[:, :]), cend, cum3, ALU.subtract)
        yield
        e_act(s, rf(F["EX"])[:, :], rf(F["EX"])[:, :], AF.Exp, scale=NEG_EXP_HALF)
        e_act(s, rf(F["EH"])[:, :], rf(F["EH"])[:, :], AF.Exp, scale=NEG_EXP_HALF)
        e_tt(s, "dve", rf(Bf["Rt"])[:, :], r_, rf(F["EC"])[:, :], ALU.mult)
        e_tt(s, "dve", rf(Bf["RK"])[:, :], r_, rf(F["KH"])[:, :], ALU.mult)
        e_tt(s, "dve", rf(F["BA"])[:, :], rf(F["KKN"])[:, :], rf(F["A"])[:, :], ALU.mult)
        yield
        e_stt(s, "dve", rf(Bf["At"])[:, :], rf(F["KKN"])[:, :], -1.0, rf(F["EX"])[:, :], ALU.mult, ALU.mult)
        e_tt(s, "dve", rf(Bf["Bt"])[:, :], rf(F["BA"])[:, :], rf(F["EN"])[:, :], ALU.mult)
        e_tt(s, "dve", rf(Bf["Bh"])[:, :], rf(F["BA"])[:, :], rf(F["EH"])[:, :], ALU.mult)
        e_tt(s, "dve", rf(Bf["Kt"])[:, :], rf(F["KH"])[:, :], rf(F["EN"])[:, :], ALU.mult)
        e_tt(s, "dve", rf(Bf["Kh"])[:, :], rf(F["KH"])[:, :], rf(F["EH"])[:, :], ALU.mult)
        yield
        blk = lambda n, c8, hs: rf(Bf[n])[hs, c8 * 64:(c8 + 1) * 64]
        tb_ = lambda n, c8, hs: rf(T64[n])[hs, c8 * 64:(c8 + 1) * 64]
        idh = lambda hs: rf(ident)[hs, hs]
        for src, dst in (("VT", "V64"), ("Bh", "BH64"), ("Kh", "KH64")):
            pt = newps()
            for c8 in range(8):
                mm2(pt, c8 * 64, 64, lambda hs, c8=c8, src=src: blk(src, c8, hs), idh)
            e_copy(s, "act", rf(T64[dst])[:, :], rf(pt)[:, :])
            yield
        for lh, rh, mask, dst in (("Bt", "At", MUS, "N"), ("At", "Bt", MLS, "Q"), ("Kt", "At", MUS, "LAK"),
                                  ("Bt", "Rt", MUI, "ARB"), ("Kt", "Rt", MUI, "ARK")):
            pt = newps()
            for c8 in range(8):
                mm2(pt, c8 * 64, 64, lambda hs, c8=c8, lh=lh: blk(lh, c8, hs), lambda hs, c8=c8, rh=rh: blk(rh, c8, hs))
            e_tt(s, "dve", rf(T64[dst])[:, :], rf(pt)[:, :], rf(mask)[:, :], ALU.mult)
            yield
        e_tt(s, "dve", rf(T64["XA"])[:, :], rf(T64["N"])[:, :], rf(ID8)[:, :], ALU.add)
        Pn, Qn, Pn2, Qn2 = "N", "Q", "N2", "Q2"
        for lvl in range(1, 6):
            pq = newps()
            for c8 in range(8):
                mm2(pq, c8 * 64, 64, lambda hs, c8=c8, Pn=Pn: tb_(Pn, c8, hs), lambda hs, c8=c8, Qn=Qn: tb_(Qn, c8, hs))
            e_copy(s, "act", rf(T64[Qn2])[:, :], rf(pq)[:, :])
            if lvl < 5:
                pp_ = newps()
                for c8 in range(8):
                    mm2(pp_, c8 * 64, 64, lambda hs, c8=c8, Qn=Qn: tb_(Qn, c8, hs),
                        lambda hs, c8=c8, Pn=Pn: tb_(Pn, c8, hs))
                e_copy(s, "act", rf(T64[Pn2])[:, :], rf(pp_)[:, :])
            yield
            px = newps()
            for c8 in range(8):
                mm2(px, c8 * 64, 64, lambda hs, c8=c8, Qn2=Qn2: tb_(Qn2, c8, hs), lambda hs, c8=c8: tb_("XA", c8, hs))
            e_tt(s, "dve", rf(T64["XA"])[:, :], rf(T64["XA"])[:, :], rf(px)[:, :], ALU.add)
            yield
            Pn, Pn2 = Pn2, Pn
            Qn, Qn2 = Qn2, Qn
        pr0 = newps()
        for c8 in range(8):
            mm2(pr0, c8 * 64, 64, lambda hs, c8=c8: tb_("LAK", c8, hs), lambda hs, c8=c8: tb_("V64", c8, hs))
        e_copy(s, "act", rf(R0)[:, :], rf(pr0)[:, :])
        pg = newps()
        e_mm(s, rf(pg)[:, :], rf(G2)[:, hp * 128:(hp + 1) * 128], rf(SGL)[:, tok:tok + 512])
        e_copy(s, "act", rf(GT)[:, :], rf(pg)[:, :])
        yield
        PY, PT1 = B["PY"], B["PT1"]
        pbh = lambda hs: rf(Pb)[hs, :]
        ubh = lambda hs: rf(UB)[hs, :]
        for c8 in range(8):
            mm2(PT1, 0, 64, lambda hs: blk("At", c8, hs), pbh)
            mm2(PY, c8 * 64, 64, lambda hs: blk("Rt", c8, hs), pbh, True, False)
            e_tt(s, "dve", rf(RR)[:, :], rf(PT1)[:, 0:64], rf(R0)[:, c8 * 64:(c8 + 1) * 64], ALU.add)
            yield
            mm2(PT1, 64, 64, lambda hs: tb_("XA", c8, hs), lambda hs: rf(RR)[hs, :])
            e_copy(s, "act", rf(UB)[:, :], rf(PT1)[:, 64:128])
            yield
            mm2(PT1, 128, 64, lambda hs: tb_("KH64", c8, hs), lambda hs: tb_("V64", c8, hs), True, False)
            mm2(PT1, 128, 64, lambda hs: tb_("BH64", c8, hs), ubh, False, True)
            mm2(PY, c8 * 64, 64, lambda hs: tb_("ARB", c8, hs), ubh, False, False)
            mm2(PY, c8 * 64, 64, lambda hs: tb_("ARK", c8, hs), lambda hs: tb_("V64", c8, hs), False, True)
            e_stt(s, "dve", rf(Pf)[:, :], rf(Pf)[:, :], rf(F["EC"])[:, c8 * 64 + 63:c8 * 64 + 64], rf(PT1)[:, 128:192],
                  ALU.mult, ALU.add)
            e_copy(s, "act", rf(Pb)[:, :], rf(Pf)[:, :])
            yield
        e_copy(s, "act", rf(YF)[:, :], rf(PY)[:, :])
        yf3 = b3(rf(YF)[:, :])
        yq3 = b3(rf(YQ)[:, :])
        prk = newps()
        for c8 in range(8):
            mm2(prk, c8, 1, lambda hs: blk("RK", c8, hs), lambda hs: rf(RKb)[hs, hp:hp + 1])
        e_copy(s, "act", rf(RKS)[:, :], rf(prk)[:, 0:8])
        yield
        s.op("dve", lambda E: E.tensor_reduce(out=ST[:, :, 0], in_=YF[:, :].rearrange("p (a b) -> p a b", a=8),
                                              axis=AX.X, op=ALU.add), reads=YF.all(), writes=ST.all())
        e_act(s, rf(YQ)[:, :], rf(YF)[:, :], AF.Square)
        yield
        s.op("dve", lambda E: E.tensor_reduce(out=ST[:, :, 1], in_=YQ[:, :].rearrange("p (a b) -> p a b", a=8),
                                              axis=AX.X, op=ALU.add), reads=YQ.all(), writes=ST.all())
        e_ts(s, "dve", rf(ST)[:, :, 2], rf(ST)[:, :, 0], 1.0 / 64, None, ALU.mult)
        e_tt(s, "dve", rf(ST)[:, :, 0], rf(ST)[:, :, 2], rf(ST)[:, :, 2], ALU.mult)
        e_stt(s, "dve", rf(ST)[:, :, 1], rf(ST)[:, :, 1], 1.0 / 64, rf(ST)[:, :, 0], ALU.mult, ALU.subtract)
        e_ts(s, "dve", rf(ST)[:, :, 1], rf(ST)[:, :, 1], GN_EPS, None, ALU.add)
        e_act(s, rf(ST)[:, :, 3], rf(ST)[:, :, 1], AF.Sqrt)
        yield
        s.op("dve", lambda E: E.reciprocal(out=ST[:, :, 3], in_=ST[:, :, 3]), reads=ST.all(), writes=ST.all())
        mean_b = Ref(ST[:, :, 2:3].to_broadcast([128, 8, 64]), ST.all())
        rstd_b = Ref(ST[:, :, 3:4].to_broadcast([128, 8, 64]), ST.all())
        e_tt(s, "dve", yf3, yf3, mean_b, ALU.subtract)
        e_tt(s, "dve", yf3, yf3, rstd_b, ALU.mult)
        rks_b = Ref(RKS[:, :].rearrange("p (a b) -> p a b", b=1).to_broadcast([128, 8, 64]), RKS.all())
        e_tt(s, "dve", yq3, b3(rf(T64["V64"])[:, :]), rks_b, ALU.mult)
        lng = Ref(LNGB[:, 0:1, :].to_broadcast([128, 8, 64]), LNGB.all())
        lnb = Ref(LNGB[:, 1:2, :].to_broadcast([128, 8, 64]), LNGB.all())
        yield
        e_tt(s, "dve", yf3, yf3, lng, ALU.mult)
        e_tt(s, "dve", yf3, yf3, lnb, ALU.add)
        yield
        e_tt(s, "dve", rf(T64["YA"])[:, :], rf(YF)[:, :], rf(YQ)[:, :], ALU.add)
        yield
        pt = newps()
        for c8 in range(8):
            mm2(pt, c8 * 64, 64, lambda hs: tb_("YA", c8, hs), idh)
        s.op("dve", lambda E: E.tensor_tensor(
            out=OT[:, hp, tok:tok + 512], in0=pt[:, :], in1=GT[:, :], op=ALU.mult),
            reads=pt.all() + GT.all(), writes=OT.all())
        yield

    def stream(S, hps):
        B = sets[S]
        for hp in hps:
            w = B["WH"]
            s.dma("pool", w[:, :, :], D["l0_w_hp"][hp].rearrange("p (kc n) -> p kc n", kc=8), writes=w.all())
            LNGB = B["LNGB"]
            for hh in range(2):
                h = 2 * hp + hh
                s.dma("sp", LNGB[HS[hh], 0, :], D["l0_lnx_g"][h * 64:(h + 1) * 64].partition_broadcast(64),
                      writes=LNGB.all())
                s.dma("sp", LNGB[HS[hh], 1, :], D["l0_lnx_b"][h * 64:(h + 1) * 64].partition_broadcast(64),
                      writes=LNGB.all())
            e_memset(s, "dve", rf(B["Pf"])[:, :], 0.0)
            e_memset(s, "dve", rf(B["Pb"])[:, :], 0.0)
            yield
            for gq in range(4):
                yield from unit(hp, gq, B)

    gens = [stream(0, (0, 2)), stream(1, (1, 3))]
    alive = [True, True]
    first = True
    while any(alive):
        for S in range(2):
            if alive[S]:
                try:
                    next(gens[S])
                except StopIteration:
                    alive[S] = False
            if first and S == 0:
                for _ in range(STAGGER):
                    next(gens[0])
                first = False
    s.barrier()
    for kc in range(8):
        s.dma("sp", XT[:, kc, :], spill[:, kc, :], reads=spill.all(), writes=XT.all())


GAIN_NAMES = ["l0_ffn1_pre_g", "l0_ffn1_post_g", "l0_mix_pre_g", "l0_mix_post_g", "l0_ffn2_pre_g", "l0_ffn2_post_g",
              "l1_ffn1_pre_g", "l1_ffn1_post_g", "l1_mix_pre_g", "l1_mix_post_g", "l1_ffn2_pre_g", "l1_ffn2_post_g"]
HALF_GAINS = [1, 5, 7, 11]


def build_program(stages=("f01", "m0", "f02f11", "m1", "f12"), dbg=False):
    k = KB()
    nc = k.nc
    s = k.s
    xT_d = k.dram_in("xT", [DM, SEQ])
    gains_d = k.dram_in("gains", [128, 12 * 8])
    ffn_d = {}
    for nm in ("l0_ffn1", "l0_ffn2", "l1_ffn1", "l1_ffn2"):
        ffn_d[nm] = (k.dram_in(nm + "_w_in", [NJ, 128, 2048]), k.dram_in(nm + "_w_out", [2, 8, 128, 11 * 128]))
    D = {}
    for nm, shp in (("l0_pl", [128, 32]), ("l0_ph", [128, 32]), ("l0_mul", [128, 3]), ("l0_lnx_g", [512]),
                    ("l0_lnx_b", [512]), ("l0_w2", [64, 512]), ("l0_a2", [64, 512]), ("l0_g2", [128, 512]),
                    ("l0_gate_a_w", [8, 64, 64]), ("l0_gate_x_w", [8, 64, 64]), ("l0_w_lru", [4, 128, 8 * 256]),
                    ("l0_w_lora", [128, 8 * 256]), ("l0_w_hp", [4, 128, 8 * 384]), ("l0_w_out", [DM, DM])):
        D[nm] = k.dram_in(nm, shp)
    D["xt_spill"] = T("xt_spill", nc.dram_tensor("xt_spill", [128, 8, SEQ], F32, kind="Internal").ap())
    if dbg:
        D["dbg_OT"] = k.dram_out("dbg_OT", [128, 8, SEQ], BF16)
    wqkv_d = k.dram_in("l1_w_qkv", [8, 128, 8 * 384])
    l1_wo_d = k.dram_in("l1_w_out", [DM, DM])
    outT_d = k.dram_out("outT", [DM, SEQ])

    with k.es:
        XT = k.sb("XT", [128, 8, SEQ], F32, parts=4)
        C = {}
        C["gains"] = k.sb("gains", [128, 12, 8], F32)
        C["ones_m"] = k.sb("ones_m", [128, 128], BF16)
        PS = [k.ps(f"ps{i}", [128, 512]) for i in range(8)]
        C["HTG"] = k.sb("HTG", [128, 8, SEQ], BF16, parts=4)
        C["ht_ready"] = None

        s.op("dve", lambda e: e.memset(C["ones_m"][:, :], 1.0 / DM), writes=C["ones_m"].all())
        C["one_f"] = k.sb("one_f", [128, 1], F32)
        C["ones_col"] = k.sb("ones_col", [128, 1], BF16)
        C["ones_bf"] = k.sb("ones_bf", [128, 512], BF16)
        C["ident"] = k.sb("ident", [128, 128], BF16)
        C["ntri"] = k.sb("ntri", [128, 128], BF16)
        s.op("dve", lambda e: e.memset(C["one_f"][:, :], 1.0), writes=C["one_f"].all())
        s.op("dve", lambda e: e.memset(C["ones_col"][:, :], 1.0), writes=C["ones_col"].all())
        s.op("dve", lambda e: e.memset(C["ones_bf"][:, :], 1.0), writes=C["ones_bf"].all())
        s.op("pool", lambda e: e.affine_select(out=C["ident"][:, :], in_=C["ones_bf"][:, 0:128], pattern=[[-1, 128]],
                                               compare_op=ALU.is_equal, fill=0.0, base=0, channel_multiplier=1),
             reads=C["ones_bf"].all(), writes=C["ident"].all())
        s.op("pool", lambda e: e.affine_select(out=C["ntri"][:, :], in_=C["ones_bf"][:, 0:128], pattern=[[-1, 128]],
                                               compare_op=ALU.is_ge, fill=0.0, base=0, channel_multiplier=1),
             reads=C["ones_bf"].all(), writes=C["ntri"].all())
        s.op("dve", lambda e: e.tensor_scalar(out=C["ntri"][:, :], in0=C["ntri"][:, :], scalar1=-1.0, scalar2=None,
                                              op0=ALU.mult),
             reads=C["ntri"].all(), writes=C["ntri"].all())
        C["eps"] = k.sb("eps", [128, 1], F32)
        s.op("dve", lambda e: e.memset(C["eps"][:, :], NORM_EPS), writes=C["eps"].all())
        s.dma("sp", C["gains"][:, :, :], gains_d[:, :].rearrange("p (n c) -> p n c", n=12), writes=C["gains"].all())
        for gi in HALF_GAINS:
            s.op("dve", lambda e, gi=gi: e.tensor_scalar(out=C["gains"][:, gi, :], in0=C["gains"][:, gi, :],
                                                         scalar1=0.5, scalar2=None, op0=ALU.mult),
                 reads=C["gains"].all(), writes=C["gains"].all())
        for tb in range(4):
            for kc in range(8):
                s.dma("sp", XT[:, kc, tb * 512:(tb + 1) * 512], xT_d[kc * 128:(kc + 1) * 128, tb * 512:(tb + 1) * 512],
                      writes=XT.b(tb))

        PRE_GI = {"f01": 0, "m0": 2, "f02": 4, "f02f11": 4, "f11": 6, "m1": 8, "f12": 10}
        for si, st in enumerate(stages):
            nxt = PRE_GI[stages[si + 1]] if si + 1 < len(stages) else None
            if st == "f01":
                ffn_stage(k, C, XT, PS, [(*ffn_d["l0_ffn1"], 0, 1)], nxt)
            elif st == "f02":
                ffn_stage(k, C, XT, PS, [(*ffn_d["l0_ffn2"], 4, 5)], nxt)
            elif st == "f02f11":
                ffn_stage(k, C, XT, PS, [(*ffn_d["l0_ffn2"], 4, 5), (*ffn_d["l1_ffn1"], 6, 7)], nxt)
            elif st == "f11":
                ffn_stage(k, C, XT, PS, [(*ffn_d["l1_ffn1"], 6, 7)], nxt)
            elif st == "m0":
                mixer0_stage(k, C, XT, PS, D, 2, 3, nxt)
            elif st == "m1":
                if dbg:
                    C["dbg_OT"] = D["dbg_OT"]
                attn_stage(k, C, XT, PS, wqkv_d, l1_wo_d, 8, 9, nxt)
            elif st == "f12":
                ffn_stage(k, C, XT, PS, [(*ffn_d["l1_ffn2"], 10, 11)], nxt)

        for tb in range(4):
            for kc in range(8):
                s.dma("sp", outT_d[kc * 128:(kc + 1) * 128, tb * 512:(tb + 1) * 512], XT[:, kc, tb * 512:(tb + 1) * 512],
                      reads=XT.b(tb), writes=outT_d.all())
        s.barrier(engines=["sp"])
    return nc


def _col(v):
    return np.ascontiguousarray(np.asarray(v, np.float32).reshape(8, 128).T)


def prep_shared(inp):
    d = {}
    d["gains"] = np.ascontiguousarray(np.concatenate([_col(inp[n]) for n in GAIN_NAMES], axis=1))
    for nm in ("l0_ffn1", "l0_ffn2", "l1_ffn1", "l1_ffn2"):
        w_in = np.asarray(inp[nm + "_w_in"], np.float32)
        w_out = np.asarray(inp[nm + "_w_out"], np.float32)
        g = w_in[:, :DFF].reshape(8, 128, NJ, 128)
        u = w_in[:, DFF:].reshape(8, 128, NJ, 128)
        gu = np.concatenate([g, u], axis=3)
        d[nm + "_w_in"] = np.ascontiguousarray(gu.transpose(2, 1, 0, 3).reshape(NJ, 128, 2048))
        wo = w_out.reshape(2, 11, 128, 8, 128)
        d[nm + "_w_out"] = np.ascontiguousarray(wo.transpose(0, 3, 2, 1, 4).reshape(2, 8, 128, 11 * 128))
    f = lambda n: np.asarray(inp[n], np.float32)
    cw = f("l0_conv_w")
    pl = np.stack([cw[0], cw[1], cw[2], cw[3], f("l0_conv_b"), f("l0_gate_a_b"), f("l0_gate_x_b"), f("l0_lambda")], axis=1)
    d["l0_pl"] = np.ascontiguousarray(pl.reshape(4, 128, 8).transpose(1, 0, 2).reshape(128, 32))
    mu = f("l0_mu")
    ph = np.stack([mu[0:512], mu[512:1024], mu[1024:1536], f("l0_w0"), f("l0_a0"), f("l0_k_k"), f("l0_k_a"),
                   f("l0_r_k").reshape(512)], axis=1)
    d["l0_ph"] = np.ascontiguousarray(ph.reshape(4, 128, 8).transpose(1, 0, 2).reshape(128, 32))
    mul = np.zeros((128, 3), np.float32)
    mul[0:64, 0] = mu[1536:1600]
    mul[0:64, 1] = mu[1600:1664]
    mul[:, 2] = mu[1664:1792]
    d["l0_mul"] = mul
    for n in ("l0_lnx_g", "l0_lnx_b", "l0_w2", "l0_a2", "l0_g2", "l0_gate_a_w", "l0_gate_x_w", "l0_w_out"):
        d[n] = np.ascontiguousarray(f(n))
    wi = f("l0_w_in").reshape(8, 128, 2816)
    lru = np.concatenate([wi[:, :, 1792:2304].reshape(8, 128, 4, 128), wi[:, :, 2304:2816].reshape(8, 128, 4, 128)], axis=3)
    d["l0_w_lru"] = np.ascontiguousarray(lru.transpose(2, 1, 0, 3).reshape(4, 128, 8 * 256))
    d["l0_w_lora"] = np.ascontiguousarray(wi[:, :, 1536:1792].transpose(1, 0, 2).reshape(128, 8 * 256))
    hd = np.stack([wi[:, :, 0:512].reshape(8, 128, 4, 128), wi[:, :, 512:1024].reshape(8, 128, 4, 128),
                   wi[:, :, 1024:1536].reshape(8, 128, 4, 128)], axis=3)
    d["l0_w_hp"] = np.ascontiguousarray(hd.transpose(2, 1, 0, 3, 4).reshape(4, 128, 8 * 384))
    wq = np.asarray(inp["l1_w_qkv"], np.float32).reshape(8, 128, 3, 8, 128)
    d["l1_w_qkv"] = np.ascontiguousarray(wq.transpose(3, 1, 0, 2, 4).reshape(8, 128, 8 * 384))
    d["l1_w_out"] = np.ascontiguousarray(np.asarray(inp["l1_w_out"], np.float32))
    return d


_CACHE = {}


def kernel(**inputs):
    x = np.asarray(inputs["x"], np.float32)
    shared = prep_shared(inputs)
    if "nc" not in _CACHE:
        _CACHE["nc"] = build_program()
    nc = _CACHE["nc"]
    in_maps = []
    for c in range(N_CORES):
        m = dict(shared)
        m["xT"] = np.ascontiguousarray(x[c].T)
        in_maps.append(m)
    res = run_bass_kernel_spmd(nc, in_maps, core_ids=list(range(N_CORES)))
    out = np.stack([np.ascontiguousarray(res.results[c]["outT"].T) for c in range(N_CORES)], axis=0)
    return out.astype(np.float32)
```

```python
import math
from contextlib import ExitStack

import numpy as np
import concourse.bass as bass
import concourse.mybir as mybir
from concourse.bass_utils import run_bass_kernel_spmd

F32 = mybir.dt.float32
BF16 = mybir.dt.bfloat16
AF = mybir.ActivationFunctionType
ALU = mybir.AluOpType

SEQ = 2048
DM = 1024
DFF = 2816
NJ = 22
NORM_EPS = 1e-6
N_CORES = 8


class Buf:
    __slots__ = ("name", "w", "r")

    def __init__(self, name):
        self.name = name
        self.w = None
        self.r = {}


class T:
    def __init__(self, name, t, parts=1):
        self.name = name
        self.t = t
        self.bufs = [Buf(f"{name}.{i}") for i in range(parts)]

    def b(self, *idx):
        return [self.bufs[i] for i in idx]

    def all(self):
        return list(self.bufs)

    def __getitem__(self, key):
        return self.t[key]


class Sched:
    COMPUTE = ("pe", "act", "dve", "pool")

    def __init__(self, nc, es, n_dma_ch=20):
        self.nc = nc
        self.eng = {"pe": nc.tensor, "act": nc.scalar, "dve": nc.vector, "pool": nc.gpsimd, "sp": nc.sync}
        self.sems = {}
        self.cnt = {}
        for e in self.COMPUTE:
            self.sems[e] = es.enter_context(nc.semaphore(f"s_{e}"))
            self.cnt[e] = 0
        self.ch = {}
        self.ch_next = {}
        for q in ("sp", "pool", "act"):
            n = n_dma_ch if q != "act" else 4
            lst = []
            for i in range(n):
                key = f"d_{q}{i}"
                self.sems[key] = es.enter_context(nc.semaphore(key))
                self.cnt[key] = 0
                lst.append(key)
            self.ch[q] = lst
            self.ch_next[q] = 0
        self.seen = {e: {} for e in self.eng}
        self.n_wait = 0
        self.n_ins = 0

    def _wait(self, e, ev):
        key, val = ev
        if val <= 0:
            return
        if self.seen[e].get(key, 0) >= val:
            return
        self.seen[e][key] = val
        self.eng[e].wait_ge(self.sems[key], val)
        self.n_wait += 1

    def _deps(self, e, reads, writes):
        evs = {}

        def need(ev):
            if ev is None:
                return
            k_, v_ = ev
            if e == "pe" and k_ == "pe":
                return
            if evs.get(k_, 0) < v_:
                evs[k_] = v_

        for b in reads:
            need(b.w)
        for b in writes:
            need(b.w)
            for kv in b.r.items():
                need(kv)
        return evs

    def op(self, e, fn, reads=(), writes=()):
        evs = self._deps(e, reads, writes)
        for ev in evs.items():
            self._wait(e, ev)
        ins = fn(self.eng[e])
        self.cnt[e] += 1
        ev = (e, self.cnt[e])
        ins.then_inc(self.sems[e], 1)
        self.seen[e][e] = max(self.seen[e].get(e, 0), 0)
        for b in writes:
            b.w = ev
            b.r = {}
        for b in reads:
            if b.w is not ev:
                b.r[e] = self.cnt[e]
        self.n_ins += 1
        return ins

    def dma(self, q, out, in_, reads=(), writes=()):
        e = q
        evs = self._deps(e, reads, writes)
        key = self.ch[q][self.ch_next[q]]
        self.ch_next[q] = (self.ch_next[q] + 1) % len(self.ch[q])
        if evs.get(key, 0) < self.cnt[key]:
            evs[key] = self.cnt[key]
        for ev in evs.items():
            self._wait(e, ev)
        ins = self.eng[e].dma_start(out=out, in_=in_)
        self.cnt[key] += 16
        ins.then_inc(self.sems[key], 16)
        ev = (key, self.cnt[key])
        for b in writes:
            b.w = ev
            b.r = {}
        for b in reads:
            b.r[key] = self.cnt[key]
        self.n_ins += 1
        return ins

    def barrier(self, engines=None):
        evs = [(k_, v_) for k_, v_ in self.cnt.items() if v_ > 0]
        for e in (engines or self.eng):
            for ev in evs:
                if ev[0] == e:
                    continue
                self._wait(e, ev)


class Phase:
    def __init__(self, k):
        self.k = k
        self.es = ExitStack()

    def __enter__(self):
        self.es.__enter__()
        return self

    def __exit__(self, *a):
        self.k.s.barrier()
        return self.es.__exit__(*a)

    def sb(self, name, shape, dtype, parts=1):
        self.k.uid += 1
        t = self.es.enter_context(self.k.nc.sbuf_tensor(f"ph_{name}_{self.k.uid}", shape, dtype))
        return T(name, t, parts)


class KB:
    def __init__(self):
        self.nc = bass.Bass("TRN2", target_bir_lowering=False)
        self.es = ExitStack()
        self.s = Sched(self.nc, self.es)
        self.uid = 0

    def sb(self, name, shape, dtype, parts=1):
        t = self.es.enter_context(self.nc.sbuf_tensor("sb_" + name, shape, dtype))
        return T(name, t, parts)

    def ps(self, name, shape, dtype=F32, parts=1):
        t = self.es.enter_context(self.nc.psum_tensor("pp_" + name, shape, dtype))
        return T(name, t, parts)

    def dram_in(self, name, shape, dtype=F32):
        return T(name, self.nc.dram_tensor(name, list(shape), dtype, kind="ExternalInput").ap())

    def dram_out(self, name, shape, dtype=F32):
        return T(name, self.nc.dram_tensor(name, list(shape), dtype, kind="ExternalOutput").ap())

    def phase(self):
        return Phase(self)


class Bg:
    def __init__(self):
        self.q = []

    def add(self, gen, period=2):
        self.q.append([gen, period, period])

    def tick(self):
        for item in list(self.q):
            item[2] -= 1
            if item[2] <= 0:
                item[2] = item[1]
                try:
                    next(item[0])
                except StopIteration:
                    self.q.remove(item)

    def drain(self):
        while self.q:
            for item in list(self.q):
                try:
                    next(item[0])
                except StopIteration:
                    self.q.remove(item)


def rms_rstd_gen(k, C, src, src_bufs, SQ, PST, RSTD, ntok, fuse_sq=False):
    s = k.s
    s.op("act", lambda e: e.activation(out=SQ[:, :, 0:ntok], in_=src, func=AF.Square),
         reads=src_bufs, writes=SQ.all())
    if not fuse_sq:
        yield
    for kc in range(8):
        s.op("pe", lambda e, kc=kc: e.matmul(PST[:, 0:ntok], lhsT=C["ones_m"][:, :], rhs=SQ[:, kc, 0:ntok],
                                             start=(kc == 0), stop=(kc == 7)),
             reads=SQ.all() + C["ones_m"].all(), writes=PST.all())
    yield
    s.op("act", lambda e: e.activation(out=RSTD[:, 0:ntok], in_=PST[:, 0:ntok], func=AF.Ln, bias=C["eps"][:, 0:1]),
         reads=PST.all() + C["eps"].all(), writes=RSTD.all())
    yield
    s.op("act", lambda e: e.activation(out=RSTD[:, 0:ntok], in_=RSTD[:, 0:ntok], func=AF.Exp, scale=-0.5),
         reads=RSTD.all(), writes=RSTD.all())


def ffn_stage(k, C, XT, PS, ffns, next_gi=None):
    s = k.s
    G_ = C["gains"]
    with k.phase() as ph:
        HTG = C["HTG"]
        HTs = []
        for i in range(2):
            hv = T(f"HTv{i}", HTG.t[:, :, i * 1024:(i + 1) * 1024])
            hv.bufs = HTG.bufs[2 * i:2 * i + 2]
            HTs.append(hv)
        ACTT = ph.sb("ACTT", [128, 11, 1024], BF16, parts=22)
        YT = ph.sb("YT", [128, 8, 1024], F32, parts=16)
        SQ = [ph.sb(f"SQ{i}", [128, 8, 512], BF16) for i in range(2)]
        RSTD = [ph.sb(f"RSTD{i}", [128, 512], F32) for i in range(2)]
        WIN = [ph.sb(f"WIN{i}", [128, 8, 256], BF16) for i in range(3)]
        WOUT = [ph.sb(f"WOUT{i}", [128, 11, 128], BF16) for i in range(3)]
        SG = [ph.sb(f"SG{i}", [128, 512], F32) for i in range(2)]
        PG = [PS[0], PS[1]]
        PU = [PS[2], PS[3]]
        PY = [PS[4], PS[5]]
        PST = [PS[6], PS[7]]
        st = {"win": 0, "wout": 0, "pi": 0, "ni": 0}
        jobs = [(f, B) for f in range(len(ffns)) for B in range(2)]

        bg = Bg()

        def prenorm(ji):
            f, B = jobs[ji]
            HT = HTs[ji % 2]
            gi_pre = ffns[f][2]
            for sb_ in range(2):
                tok = B * 1024 + sb_ * 512
                xb = XT.b(B * 2 + sb_)
                n_ = st["ni"] % 2
                st["ni"] += 1
                yield from rms_rstd_gen(k, C, XT[:, :, tok:tok + 512], xb, SQ[n_], PST[n_], RSTD[n_], 512)
                for kc in range(8):
                    if kc == 4:
                        yield
                    s.op("dve", lambda e, kc=kc, tok=tok, sb_=sb_, n_=n_: e.scalar_tensor_tensor(
                        out=HT[:, kc, sb_ * 512:(sb_ + 1) * 512], in0=XT[:, kc, tok:tok + 512],
                        scalar=G_[:, gi_pre, kc:kc + 1], in1=RSTD[n_][:, :], op0=ALU.mult, op1=ALU.mult),
                        reads=xb + RSTD[n_].all() + G_.all(), writes=HT.b(sb_))

        def postnorm(ji):
            f, B = jobs[ji]
            gi_post = ffns[f][3]
            for sb_ in range(2):
                tok = B * 1024 + sb_ * 512
                rhs_sl = slice(sb_ * 512, (sb_ + 1) * 512)
                ybs = YT.b(*[dc * 2 + sb_ for dc in range(8)])
                xb = XT.b(B * 2 + sb_)
                n_ = st["ni"] % 2
                st["ni"] += 1
                yield from rms_rstd_gen(k, C, YT[:, :, rhs_sl], ybs, SQ[n_], PST[n_], RSTD[n_], 512)
                for dc in range(8):
                    if dc % 2 == 0 and dc > 0:
                        yield
                    s.op("dve", lambda e, dc=dc, rhs_sl=rhs_sl, n_=n_: e.scalar_tensor_tensor(
                        out=YT[:, dc, rhs_sl], in0=YT[:, dc, rhs_sl], scalar=G_[:, gi_post, dc:dc + 1],
                        in1=RSTD[n_][:, :], op0=ALU.mult, op1=ALU.mult),
                        reads=YT.b(dc * 2 + sb_) + RSTD[n_].all() + G_.all(), writes=YT.b(dc * 2 + sb_))
                    s.op("dve", lambda e, dc=dc, rhs_sl=rhs_sl, tok=tok: e.tensor_tensor(
                        out=XT[:, dc, tok:tok + 512], in0=XT[:, dc, tok:tok + 512], in1=YT[:, dc, rhs_sl], op=ALU.add),
                        reads=YT.b(dc * 2 + sb_) + xb, writes=xb)

        def up(ji, G, after_first=None):
            f, B = jobs[ji]
            HT = HTs[ji % 2]
            w_in_d = ffns[f][0]
            for jj in range(11):
                j = G * 11 + jj
                W = WIN[st["win"] % 3]
                st["win"] += 1
                s.dma("pool", W[:, :, :], w_in_d[j].rearrange("p (kc c) -> p kc c", kc=8), writes=W.all())
                for sb_ in range(2):
                    pg, pu, sg = PG[st["pi"] % 2], PU[st["pi"] % 2], SG[st["pi"] % 2]
                    st["pi"] += 1
                    rhs_sl = slice(sb_ * 512, (sb_ + 1) * 512)
                    for kc in range(8):
                        s.op("pe", lambda e, kc=kc, pg=pg, W=W, rhs_sl=rhs_sl: e.matmul(
                            pg[:, :], lhsT=W[:, kc, 0:128], rhs=HT[:, kc, rhs_sl], start=(kc == 0), stop=(kc == 7)),
                            reads=W.all() + HT.b(sb_), writes=pg.all())
                    for kc in range(8):
                        s.op("pe", lambda e, kc=kc, pu=pu, W=W, rhs_sl=rhs_sl: e.matmul(
                            pu[:, :], lhsT=W[:, kc, 128:256], rhs=HT[:, kc, rhs_sl], start=(kc == 0), stop=(kc == 7)),
                            reads=W.all() + HT.b(sb_), writes=pu.all())
                    s.op("act", lambda e, pg=pg, sg=sg: e.activation(out=sg[:, :], in_=pg[:, :], func=AF.Silu),
                         reads=pg.all(), writes=sg.all())
                    s.op("dve", lambda e, pu=pu, sg=sg, jj=jj, rhs_sl=rhs_sl: e.tensor_tensor(
                        out=ACTT[:, jj, rhs_sl], in0=sg[:, :], in1=pu[:, :], op=ALU.mult),
                        reads=sg.all() + pu.all(), writes=ACTT.b(jj * 2 + sb_))
                    bg.tick()
                if jj == 0 and after_first is not None:
                    after_first()

        def down(ji, G):
            f, B = jobs[ji]
            w_out_d = ffns[f][1]
            for dc in range(8):
                W = WOUT[st["wout"] % 3]
                st["wout"] += 1
                s.dma("pool", W[:, :, :], w_out_d[G, dc].rearrange("p (jj c) -> p jj c", jj=11), writes=W.all())
                for sb_ in range(2):
                    py = PY[st["pi"] % 2]
                    st["pi"] += 1
                    rhs_sl = slice(sb_ * 512, (sb_ + 1) * 512)
                    for jj in range(11):
                        s.op("pe", lambda e, jj=jj, py=py, W=W, rhs_sl=rhs_sl: e.matmul(
                            py[:, :], lhsT=W[:, jj, :], rhs=ACTT[:, jj, rhs_sl], start=(jj == 0), stop=(jj == 10)),
                            reads=W.all() + ACTT.b(jj * 2 + sb_), writes=py.all())
                    yb = YT.b(dc * 2 + sb_)
                    if G == 0:
                        s.op("act", lambda e, py=py, dc=dc, rhs_sl=rhs_sl: e.activation(
                            out=YT[:, dc, rhs_sl], in_=py[:, :], func=AF.Copy),
                            reads=py.all(), writes=yb)
                    else:
                        s.op("dve", lambda e, py=py, dc=dc, rhs_sl=rhs_sl: e.tensor_tensor(
                            out=YT[:, dc, rhs_sl], in0=YT[:, dc, rhs_sl], in1=py[:, :], op=ALU.add),
                            reads=py.all() + yb, writes=yb)
                    bg.tick()

        def next_prenorm():
            HT = HTs[0]
            for sb_ in range(2):
                tok = sb_ * 512
                xb = XT.b(sb_)
                n_ = st["ni"] % 2
                st["ni"] += 1
                yield from rms_rstd_gen(k, C, XT[:, :, tok:tok + 512], xb, SQ[n_], PST[n_], RSTD[n_], 512)
                for kc in range(8):
                    if kc == 4:
                        yield
                    s.op("dve", lambda e, kc=kc, tok=tok, sb_=sb_, n_=n_: e.scalar_tensor_tensor(
                        out=HT[:, kc, sb_ * 512:(sb_ + 1) * 512], in0=XT[:, kc, tok:tok + 512],
                        scalar=G_[:, next_gi, kc:kc + 1], in1=RSTD[n_][:, :], op0=ALU.mult, op1=ALU.mult),
                        reads=xb + RSTD[n_].all() + G_.all(), writes=HT.b(sb_))

        n = len(jobs)
        assert n % 2 == 0
        if C["ht_ready"] is not None and C["ht_ready"] == (ffns[0][2], (0, 1)):
            pass
        else:
            bg.add(prenorm(0))
            bg.drain()
        C["ht_ready"] = None
        for ji in range(n):
            up(ji, 0, after_first=(lambda ji=ji: bg.add(postnorm(ji - 1), 1)) if ji > 0 else None)
            bg.drain()
            down(ji, 0)
            if ji + 1 < n:
                bg.add(prenorm(ji + 1), 1)
            elif next_gi is not None:
                bg.add(next_prenorm(), 1)
                C["ht_ready"] = (next_gi, (0, 1))
            up(ji, 1)
            bg.drain()
            down(ji, 1)
        bg.add(postnorm(n - 1))
        bg.drain()


def prenorm_to_HT(k, C, ph, XT, HT, PS, gi_pre, col_off=0):
    s = k.s
    G_ = C["gains"]
    with k.phase() as p2:
        SQ = [p2.sb(f"SQ{i}", [128, 8, 512], BF16) for i in range(2)]
        RSTD = [p2.sb(f"RSTD{i}", [128, 512], F32) for i in range(2)]
        bg = Bg()

        def chain(tb):
            tok = tb * 512
            xb = XT.b(tb)
            yield from rms_rstd_gen(k, C, XT[:, :, tok:tok + 512], xb, SQ[tb % 2], PS[6 + tb % 2], RSTD[tb % 2], 512)
            for kc in range(8):
                if kc == 4:
                    yield
                s.op("dve", lambda e, kc=kc: e.scalar_tensor_tensor(
                    out=HT[:, kc, col_off + tok:col_off + tok + 512], in0=XT[:, kc, tok:tok + 512],
                    scalar=G_[:, gi_pre, kc:kc + 1], in1=RSTD[tb % 2][:, :], op0=ALU.mult, op1=ALU.mult),
                    reads=xb + RSTD[tb % 2].all() + G_.all(), writes=HT.b(tb))

        skip = ()
        if C["ht_ready"] is not None and C["ht_ready"][0] == gi_pre and col_off == 0:
            skip = C["ht_ready"][1]
        C["ht_ready"] = None
        for tb in range(4):
            if tb in skip:
                continue
            bg.add(chain(tb), 1)
            bg.tick()
            bg.tick()
        bg.drain()


def outproj_postnorm(k, C, XT, PS, OT, wo_d, gi_post, next_gi=None, WO=None):
    s = k.s
    G_ = C["gains"]
    with k.phase() as p3:
        if WO is None:
            WO = T("WOv", C["HTG"].t[:, :, 1024:2048])
            WO.bufs = C["HTG"].bufs[2:4]
            for kc in range(8):
                s.dma("pool", WO[:, kc, :], wo_d[kc * 128:(kc + 1) * 128, :], writes=WO.all())
        YTs = [p3.sb(f"YT{i}", [128, 8, 512], F32, parts=8) for i in range(2)]
        SQ1 = p3.sb("SQ", [128, 8, 512], BF16)
        RSTD = [p3.sb(f"RSTD{i}", [128, 512], F32) for i in range(2)]
        bg = Bg()

        def chain(tb):
            tok = tb * 512
            YT = YTs[tb % 2]
            yield from rms_rstd_gen(k, C, YT[:, :, :], YT.all(), SQ1, PS[6 + tb % 2], RSTD[tb % 2], 512, fuse_sq=True)
            xb = XT.b(tb)
            for dc in range(8):
                if dc % 2 == 0 and dc > 0:
                    yield
                s.op("dve", lambda e, dc=dc: e.scalar_tensor_tensor(
                    out=YT[:, dc, :], in0=YT[:, dc, :], scalar=G_[:, gi_post, dc:dc + 1],
                    in1=RSTD[tb % 2][:, :], op0=ALU.mult, op1=ALU.mult),
                    reads=YT.b(dc) + RSTD[tb % 2].all() + G_.all(), writes=YT.b(dc))
                s.op("dve", lambda e, dc=dc: e.tensor_tensor(
                    out=XT[:, dc, tok:tok + 512], in0=XT[:, dc, tok:tok + 512], in1=YT[:, dc, :], op=ALU.add),
                    reads=YT.b(dc) + xb, writes=xb)

        if next_gi is not None:
            RSTDn = p3.sb("RSTDn", [128, 512], F32)
        HTG = C["HTG"]

        def next_prenorm():
            for sb_ in range(2):
                tok = sb_ * 512
                xb = XT.b(sb_)
                yield from rms_rstd_gen(k, C, XT[:, :, tok:tok + 512], xb, SQ1, PS[0], RSTDn, 512, fuse_sq=True)
                for kc in range(8):
                    if kc == 4:
                        yield
                    s.op("dve", lambda e, kc=kc, tok=tok: e.scalar_tensor_tensor(
                        out=HTG[:, kc, tok:tok + 512], in0=XT[:, kc, tok:tok + 512],
                        scalar=G_[:, next_gi, kc:kc + 1], in1=RSTDn[:, :], op0=ALU.mult, op1=ALU.mult),
                        reads=xb + RSTDn.all() + G_.all(), writes=HTG.b(sb_))

        pi = 0
        for tb in range(4):
            tok = tb * 512
            YT = YTs[tb % 2]
            if tb == 3 and next_gi is not None:
                bg.add(next_prenorm(), 1)
                C["ht_ready"] = (next_gi, (0, 1))
            for dc in range(8):
                pp = PS[4 + pi % 2]
                pi += 1
                for kc in range(8):
                    s.op("pe", lambda e, kc=kc, dc=dc, pp=pp, tok=tok: e.matmul(
                        pp[:, :], lhsT=WO[:, kc, dc * 128:(dc + 1) * 128], rhs=OT[:, kc, tok:tok + 512],
                        start=(kc == 0), stop=(kc == 7)),
                        reads=WO.all() + OT.all(), writes=pp.all())
                s.op("act", lambda e, dc=dc, pp=pp, YT=YT: e.activation(out=YT[:, dc, :], in_=pp[:, :], func=AF.Copy),
                     reads=pp.all(), writes=YT.b(dc))
                bg.tick()
            bg.drain()
            bg.add(chain(tb), 1)
        bg.drain()


def attn_stage(k, C, XT, PS, wqkv_d, wo_d, gi_pre, gi_post, next_gi=None):
    s = k.s
    with k.phase() as ph:
        OT = ph.sb("OT", [128, 8, SEQ], BF16, parts=1)
        with k.phase() as pab:
            HT = C["HTG"]
            prenorm_to_HT(k, C, pab, XT, HT, PS, gi_pre)
            with k.phase() as pb:
                NEGM = pb.sb("negm", [128, 4, 512], BF16)
                ZB = pb.sb("zb", [128, 512], BF16)
                s.op("dve", lambda e: e.memset(ZB[:, :], 0.0), writes=ZB.all())
                for d in range(4):
                    s.op("pool", lambda e, d=d: e.affine_select(
                        out=NEGM[:, d, :], in_=ZB[:, :], pattern=[[1, 512]], compare_op=ALU.is_gt,
                        fill=-30000.0, base=-128 * d, channel_multiplier=-1),
                        reads=ZB.all(), writes=NEGM.all())
                QT = [pb.sb(f"QT{i}", [128, SEQ], BF16) for i in range(2)]
                KT = [pb.sb(f"KT{i}", [128, SEQ], BF16) for i in range(2)]
                V = [pb.sb(f"V{i}", [128, 16, 128], BF16) for i in range(2)]
                W = [pb.sb(f"WQKV{i}", [128, 8, 384], BF16) for i in range(1)]
                OTOK = [pb.sb(f"OTOK{i}", [128, 16, 128], BF16) for i in range(2)]
                E = [pb.sb(f"E{i}", [128, 512], F32) for i in range(3)]
                SP = [pb.sb(f"SP{i}", [128, 512], BF16) for i in range(5)]
                ATT = [pb.sb(f"ATT{i}", [128, 512], BF16) for i in range(3)]
                OACC = [pb.sb(f"OACC{i}", [128, 4, 64], F32) for i in range(2)]
                CACC2 = pb.sb("CACC2", [128, 2, 4], F32, parts=2)
                FS2 = [pb.sb(f"FS2_{i}", [128, 2, 4], F32) for i in range(4)]
                PZ = [PS[0], PS[1], PS[2], PS[3], PS[4]]
                PO = [PS[5], PS[6]]
                PP = [PS[7]]
                st = {"pi": 0}

                bg = Bg()

                def pre_hp(hp):
                    w = W[0]
                    qt, kt_, v = QT[hp % 2], KT[hp % 2], V[hp % 2]
                    s.dma("pool", w[:, :, :], wqkv_d[hp].rearrange("p (kc c) -> p kc c", kc=8), writes=w.all())
                    for which in range(2):
                        for tb in range(4):
                            pp = PP[st["pi"] % len(PP)]
                            st["pi"] += 1
                            for kc in range(8):
                                s.op("pe", lambda e, kc=kc, pp=pp, w=w, which=which, tb=tb: e.matmul(
                                    pp[:, :], lhsT=w[:, kc, which * 128:(which + 1) * 128],
                                    rhs=HT[:, kc, tb * 512:(tb + 1) * 512], start=(kc == 0), stop=(kc == 7)),
                                    reads=w.all() + HT.b(tb), writes=pp.all())
                            if which == 0:
                                s.op("dve", lambda e, pp=pp, qt=qt, tb=tb: e.tensor_scalar(
                                    out=qt[:, tb * 512:(tb + 1) * 512], in0=pp[:, :], scalar1=0.125, scalar2=None,
                                    op0=ALU.mult),
                                    reads=pp.all(), writes=qt.all())
                            else:
                                s.op("dve", lambda e, pp=pp, kt_=kt_, tb=tb: e.tensor_copy(
                                    out=kt_[:, tb * 512:(tb + 1) * 512], in_=pp[:, :]),
                                    reads=pp.all(), writes=kt_.all())
                            yield
                    for tg in range(4):
                        pp = PP[st["pi"] % len(PP)]
                        st["pi"] += 1
                        for tt in range(4):
                            tok = (tg * 4 + tt) * 128
                            for kc in range(8):
                                s.op("pe", lambda e, kc=kc, pp=pp, w=w, tt=tt, tok=tok: e.matmul(
                                    pp[:, tt * 128:(tt + 1) * 128], lhsT=HT[:, kc, tok:tok + 128],
                                    rhs=w[:, kc, 256:384], start=(kc == 0), stop=(kc == 7)),
                                    reads=w.all() + HT.b(tg), writes=pp.all())
                        s.op("dve", lambda e, pp=pp, v=v, tg=tg: e.tensor_copy(
                            out=v[:, tg * 4:(tg + 1) * 4, :], in_=pp[:, :].rearrange("p (a b) -> p a b", a=4)),
                            reads=pp.all(), writes=v.all())
                        yield

                def post_hp(hp):
                    otok = OTOK[hp % 2]
                    for tg in range(4):
                        pp = PP[st["pi"] % len(PP)]
                        st["pi"] += 1
                        for tt in range(4):
                            s.op("pe", lambda e, pp=pp, tt=tt, tg=tg, otok=otok: e.matmul(
                                pp[:, tt * 128:(tt + 1) * 128], lhsT=otok[:, tg * 4 + tt, :], rhs=C["ident"][:, :],
                                start=True, stop=True),
                                reads=otok.all() + C["ident"].all(), writes=pp.all())
                        s.op("dve", lambda e, pp=pp, tg=tg, hp=hp: e.tensor_copy(
                            out=OT[:, hp, tg * 512:(tg + 1) * 512], in_=pp[:, :]),
                            reads=pp.all(), writes=OT.all())
                        yield

                units = []
                for hp in range(8):
                    for g in range(4):
                        for kt in range(4 * g + 3, -1, -1):
                            for hh in range(2):
                                units.append((hp, hh, g, kt))
                n = len(units)
                NPZ, NSP, NATT, NPO, NE = 5, 5, 3, 2, 3

                def u_(i):
                    hp, hh, g, kt = units[i]
                    d = kt - 4 * g
                    return hp, hh, g, kt, d, slice(hh * 64, (hh + 1) * 64)

                def c0_(i):
                    hp, hh, g, kt = units[i]
                    return max(kt - 4 * g, 0) * 128

                def s0_qk(i):
                    hp, hh, g, kt, d, hs = u_(i)
                    if hh == 0 and g == 0 and kt == 3:
                        if hp == 0:
                            bg.add(pre_hp(0))
                        bg.drain()
                    if hh == 0 and g == 0 and kt == 0 and hp + 1 < 8:
                        bg.add(pre_hp(hp + 1), 5)
                    pz, qt, kt_ = PZ[i % NPZ], QT[hp % 2], KT[hp % 2]
                    q0 = g * 512
                    c0 = c0_(i)
                    s.op("pe", lambda e: e.matmul(pz[:, c0:512], lhsT=kt_[hs, kt * 128:(kt + 1) * 128],
                                                  rhs=qt[hs, q0 + c0:q0 + 512], start=True, stop=(d < 0)),
                         reads=kt_.all() + qt.all(), writes=pz.all())
                    if d >= 0:
                        s.op("pe", lambda e: e.matmul(pz[:, c0:c0 + 128], lhsT=C["ident"][:, :], rhs=NEGM[:, d, c0:c0 + 128],
                                                      start=False, stop=True),
                             reads=C["ident"].all() + NEGM.all(), writes=pz.all())

                def s1_exp(i):
                    pz, e_ = PZ[i % NPZ], E[i % NE]
                    c0 = c0_(i)
                    s.op("act", lambda e: e.activation(out=e_[:, c0:512], in_=pz[:, c0:512], func=AF.Exp),
                         reads=pz.all(), writes=e_.all())

                def s2_ln(i):
                    e_, sp = E[i % NE], SP[i % NSP]
                    c0 = c0_(i)
                    s.op("act", lambda e: e.activation(out=sp[:, c0:512], in_=e_[:, c0:512], func=AF.Ln,
                                                       bias=C["one_f"][:, 0:1]),
                         reads=e_.all() + C["one_f"].all(), writes=sp.all())

                def s3_tri(i):
                    pz, sp = PZ[i % NPZ], SP[i % NSP]
                    c0 = c0_(i)
                    s.op("pe", lambda e: e.matmul(pz[:, c0:512], lhsT=C["ntri"][:, :], rhs=sp[:, c0:512], start=False,
                                                  stop=True, skip_group_check=True),
                         reads=sp.all() + C["ntri"].all(), writes=pz.all())

                def s4_att(i):
                    pz, att = PZ[i % NPZ], ATT[i % NATT]
                    c0 = c0_(i)
                    s.op("act", lambda e: e.activation(out=att[:, c0:512], in_=pz[:, c0:512], func=AF.Exp),
                         reads=pz.all(), writes=att.all())

                def s5_av(i):
                    hp, hh, g, kt, d, hs = u_(i)
                    qlo = max(d, 0)
                    sp, att, po, v = SP[i % NSP], ATT[i % NATT], PO[i % NPO], V[hp % 2]
                    for qi in range(qlo, 4):
                        s.op("pe", lambda e, qi=qi: e.matmul(
                            po[:, qi * 64:(qi + 1) * 64], lhsT=att[:, qi * 128:(qi + 1) * 128],
                            rhs=v[:, kt, hs], start=True, stop=True),
                            reads=att.all() + v.all(), writes=po.all())
                        s.op("pe", lambda e, qi=qi: e.matmul(
                            po[:, 256 + qi:257 + qi], lhsT=sp[:, qi * 128:(qi + 1) * 128],
                            rhs=C["ones_col"][:, 0:1], start=True, stop=True),
                            reads=sp.all() + C["ones_col"].all(), writes=po.all())

                def s6_acc(i):
                    hp, hh, g, kt, d, hs = u_(i)
                    qlo = max(d, 0)
                    span = hh
                    po = PO[i % NPO]
                    oacc, fs2 = OACC[span % 2], FS2[(i // 2) % 4]
                    cb = CACC2.b(hh)
                    otok = OTOK[hp % 2]
                    if kt != 4 * g + 3:
                        if hh == 0:
                            s.op("act", lambda e: e.activation(out=fs2[:, :, :], in_=CACC2[:, :, :], func=AF.Exp, scale=-1.0),
                                 reads=CACC2.all(), writes=fs2.all())
                        s.op("dve", lambda e: e.tensor_tensor(
                            out=CACC2[:, hh, qlo:4], in0=CACC2[:, hh, qlo:4], in1=po[:, 256 + qlo:260], op=ALU.add),
                            reads=po.all() + cb, writes=cb)
                        for qi in range(qlo, 4):
                            s.op("dve", lambda e, qi=qi: e.scalar_tensor_tensor(
                                out=oacc[:, qi, :], in0=po[:, qi * 64:(qi + 1) * 64], scalar=fs2[:, hh, qi:qi + 1],
                                in1=oacc[:, qi, :], op0=ALU.mult, op1=ALU.add),
                                reads=po.all() + fs2.all() + oacc.all(), writes=oacc.all())
                    else:
                        if qlo > 0:
                            s.op("dve", lambda e: e.memset(oacc[:, 0:qlo, :], 0.0), writes=oacc.all())
                            s.op("dve", lambda e: e.memset(CACC2[:, hh, 0:qlo], 0.0), writes=cb)
                        s.op("dve", lambda e: e.tensor_copy(
                            out=oacc[:, qlo:4, :], in_=po[:, qlo * 64:256].rearrange("p (a b) -> p a b", b=64)),
                            reads=po.all(), writes=oacc.all())
                        s.op("dve", lambda e: e.tensor_copy(out=CACC2[:, hh, qlo:4], in_=po[:, 256 + qlo:260]),
                             reads=po.all(), writes=cb)
                    if kt == 0:
                        s.op("dve", lambda e: e.tensor_copy(out=otok[:, 4 * g:4 * g + 4, hs], in_=oacc[:, :, :]),
                             reads=oacc.all(), writes=otok.all())
                        if hh == 1 and g == 3:
                            bg.add(post_hp(hp), 1)

                stages = ((0, s0_qk), (1, s1_exp), (2, s2_ln), (3, s3_tri), (4, s4_att), (5, s5_av), (6, s6_acc))
                for i in range(n + 6):
                    for lag, fn in stages:
                        if 0 <= i - lag < n:
                            fn(i - lag)
                    bg.tick()
                bg.drain()
        if "dbg_OT" in C:
            s.dma("sp", C["dbg_OT"][:, :, :], OT[:, :, :], reads=OT.all(), writes=C["dbg_OT"].all())
        outproj_postnorm(k, C, XT, PS, OT, wo_d, gi_post, next_gi)


AX = mybir.AxisListType


class Ref:
    __slots__ = ("ap", "bufs")

    def __init__(self, ap, bufs):
        self.ap = ap
        self.bufs = bufs


class _RefMaker:
    def __init__(self, t):
        self.t = t

    def __getitem__(self, key):
        return Ref(self.t.t[key], self.t.all())


def rf(t):
    return _RefMaker(t)


def _b(*refs):
    out = []
    for r in refs:
        if isinstance(r, Ref):
            out += r.bufs
    return out


def _a(x):
    return x.ap if isinstance(x, Ref) else x


def e_tt(s, eng, out, a, b, op):
    return s.op(eng, lambda E: E.tensor_tensor(out=out.ap, in0=a.ap, in1=b.ap, op=op), reads=_b(a, b), writes=out.bufs)


def e_ts(s, eng, out, a, s1, s2, op0, op1=None):
    if op1 is None:
        return s.op(eng, lambda E: E.tensor_scalar(out=out.ap, in0=a.ap, scalar1=_a(s1), scalar2=None, op0=op0),
                    reads=_b(a, s1), writes=out.bufs)
    return s.op(eng, lambda E: E.tensor_scalar(out=out.ap, in0=a.ap, scalar1=_a(s1), scalar2=_a(s2), op0=op0, op1=op1),
                reads=_b(a, s1, s2), writes=out.bufs)


def e_stt(s, eng, out, a, sc, b, op0, op1):
    return s.op(eng, lambda E: E.scalar_tensor_tensor(out=out.ap, in0=a.ap, scalar=_a(sc), in1=b.ap, op0=op0, op1=op1),
                reads=_b(a, sc, b), writes=out.bufs)


def e_act(s, out, a, func, bias=None, scale=None):
    kw = {}
    if bias is not None:
        kw["bias"] = _a(bias)
    if scale is not None:
        kw["scale"] = _a(scale)
    return s.op("act", lambda E: E.activation(out=out.ap, in_=a.ap, func=func, **kw), reads=_b(a, bias, scale),
                writes=out.bufs)


def e_mm(s, out, lhsT, rhs, start=True, stop=True):
    return s.op("pe", lambda E: E.matmul(out.ap, lhsT=lhsT.ap, rhs=rhs.ap, start=start, stop=stop),
                reads=_b(lhsT, rhs), writes=out.bufs)


def e_copy(s, eng, out, a):
    if eng == "act":
        return e_act(s, out, a, AF.Copy)
    return s.op(eng, lambda E: E.tensor_copy(out=out.ap, in_=a.ap), reads=_b(a), writes=out.bufs)


def e_memset(s, eng, out, val):
    return s.op(eng, lambda E: E.memset(out.ap, val), writes=out.bufs)


GN_EPS = 64e-5
STAGGER = 3
NEG_EXP_HALF = -0.6065306597126334


def mixer0_stage(k, C, XT, PS, D, gi_pre, gi_post, next_gi=None):
    s = k.s
    with k.phase() as ph:
        OT = ph.sb("OT", [128, 8, SEQ], BF16, parts=1)
        with k.phase() as pab:
            HT = C["HTG"]
            prenorm_to_HT(k, C, pab, XT, HT, PS, gi_pre)
            with k.phase() as pl:
                rglru_part(k, C, pl, HT, OT, PS, D)
            with k.phase() as pr:
                rwkv_part(k, C, pr, HT, OT, PS, D, XT)
        if "dbg_OT" in D:
            s.dma("sp", D["dbg_OT"][:, :, :], OT[:, :, :], reads=OT.all(), writes=D["dbg_OT"].all())
        outproj_postnorm(k, C, XT, PS, OT, D["l0_w_out"], gi_post, next_gi)


def rglru_part(k, C, p, HT, OT, PS, D):
    s = k.s
    PL = p.sb("PL", [128, 4, 8], F32)
    s.dma("sp", PL[:, :, :], D["l0_pl"][:, :].rearrange("p (c n) -> p c n", c=4), writes=PL.all())
    C1 = p.sb("C1", [128, 4], F32)
    e_act(s, rf(C1)[:, :], rf(PL)[:, :, 7], AF.Exp, scale=-1.0)
    e_act(s, rf(C1)[:, :], rf(C1)[:, :], AF.Ln, bias=rf(C["one_f"])[:, 0:1])
    e_ts(s, "dve", rf(C1)[:, :], rf(C1)[:, :], -8.0, None, ALU.mult)
    GAW = p.sb("GAW", [128, 4, 128], BF16)
    GXW = p.sb("GXW", [128, 4, 128], BF16)
    e_memset(s, "dve", rf(GAW)[:, :, :], 0.0)
    e_memset(s, "dve", rf(GXW)[:, :, :], 0.0)
    for n in range(8):
        ps_ = slice((n % 2) * 64, (n % 2) * 64 + 64)
        s.dma("pool", GAW[ps_, n // 2, ps_], D["l0_gate_a_w"][n], writes=GAW.all())
        s.dma("pool", GXW[ps_, n // 2, ps_], D["l0_gate_x_w"][n], writes=GXW.all())
    W = [p.sb(f"WL{i}", [128, 8, 256], BF16) for i in range(2)]
    XBs = [p.sb(f"XB{i}", [128, 515], F32) for i in range(2)]
    HHs = [[p.sb(f"HH{j}_{i}", [128, 512], F32) for i in range(2)] for j in range(2)]
    ts_ = [{n: p.sb(f"{n}{j}", [128, 512], F32) for n in ("GB", "XC", "R", "IG", "A", "U", "T1", "T2")} for j in range(2)]
    XCbs = [p.sb(f"XCb{j}", [128, 512], BF16) for j in range(2)]

    def unit(c, tb, j):
        w = W[j]
        XB, t_, XCb = XBs[j], ts_[j], XCbs[j]
        col = lambda n: rf(PL)[:, c, n:n + 1]
        tok = tb * 512
        px, pg = PS[2 * j], PS[2 * j + 1]
        for kc in range(8):
            e_mm(s, rf(px)[:, :], rf(w)[:, kc, 0:128], Ref(HT[:, kc, tok:tok + 512], HT.b(tb)), kc == 0, kc == 7)
        for kc in range(8):
            e_mm(s, rf(pg)[:, :], rf(w)[:, kc, 128:256], Ref(HT[:, kc, tok:tok + 512], HT.b(tb)), kc == 0, kc == 7)
        if tb == 0:
            e_memset(s, "dve", rf(XB)[:, 0:3], 0.0)
        else:
            e_copy(s, "dve", rf(XB)[:, 0:3], rf(XB)[:, 512:515])
        yield
        e_copy(s, "act", rf(XB)[:, 3:515], rf(px)[:, :])
        e_copy(s, "act", rf(t_["GB"])[:, :], rf(pg)[:, :])
        yield
        XC = t_["XC"]
        e_ts(s, "dve", rf(XC)[:, :], rf(XB)[:, 3:515], col(3), col(4), ALU.mult, ALU.add)
        for i in range(3):
            e_stt(s, "dve", rf(XC)[:, :], rf(XB)[:, i:i + 512], col(i), rf(XC)[:, :], ALU.mult, ALU.add)
        GB, T2 = t_["GB"], t_["T2"]
        e_act(s, rf(T2)[:, :], rf(GB)[:, :], AF.Gelu_apprx_tanh)
        yield
        e_copy(s, "act", rf(XCb)[:, :], rf(XC)[:, :])
        yield
        pr_, pig = PS[4 + 2 * j], PS[5 + 2 * j]
        e_mm(s, rf(pr_)[:, :], rf(GAW)[:, c, :], rf(XCb)[:, :])
        e_mm(s, rf(pig)[:, :], rf(GXW)[:, c, :], rf(XCb)[:, :])
        yield
        e_act(s, rf(t_["R"])[:, :], rf(pr_)[:, :], AF.Sigmoid, bias=col(5))
        e_act(s, rf(t_["IG"])[:, :], rf(pig)[:, :], AF.Sigmoid, bias=col(6))
        yield
        A = t_["A"]
        e_act(s, rf(A)[:, :], rf(t_["R"])[:, :], AF.Exp, scale=rf(C1)[:, c:c + 1])
        T1, U = t_["T1"], t_["U"]
        e_tt(s, "dve", rf(U)[:, :], rf(t_["IG"])[:, :], rf(XC)[:, :], ALU.mult)
        yield
        e_tt(s, "dve", rf(T1)[:, :], rf(A)[:, :], rf(A)[:, :], ALU.mult)
        yield
        e_ts(s, "dve", rf(T1)[:, :], rf(T1)[:, :], -1.0, 1.0, ALU.mult, ALU.add)
        yield
        e_act(s, rf(T1)[:, :], rf(T1)[:, :], AF.Sqrt)
        yield
        e_tt(s, "dve", rf(U)[:, :], rf(U)[:, :], rf(T1)[:, :], ALU.mult)
        yield
        H = HHs[j][tb % 2]
        Hp = HHs[j][(tb + 1) % 2]
        init = 0.0 if tb == 0 else Hp[:, 511:512]
        s.op("dve", lambda E: E.tensor_tensor_scan(
            out=H[:, :], data0=A[:, :], data1=U[:, :], initial=init, op0=ALU.mult, op1=ALU.add),
            reads=A.all() + U.all() + (Hp.all() if tb else []), writes=H.all())
        yield
        s.op("dve", lambda E: E.tensor_tensor(
            out=OT[:, 4 + c, tok:tok + 512], in0=H[:, :], in1=T2[:, :], op=ALU.mult),
            reads=H.all() + T2.all(), writes=OT.all())

    for cp in range(2):
        for j in range(2):
            c = 2 * cp + j
            s.dma("pool", W[j][:, :, :], D["l0_w_lru"][c].rearrange("p (kc n) -> p kc n", kc=8), writes=W[j].all())
        for tb in range(4):
            gens = [unit(2 * cp + j, tb, j) for j in range(2)]
            alive = [True, True]
            while any(alive):
                for j in range(2):
                    if alive[j]:
                        try:
                            next(gens[j])
                        except StopIteration:
                            alive[j] = False


def rwkv_part(k, C, p, HT, OT, PS, D, XT):
    s = k.s
    spill = D["xt_spill"]
    for kc in range(8):
        s.dma("sp", spill[:, kc, :], XT[:, kc, :], reads=XT.all(), writes=spill.all())
    PH = p.sb("PH", [128, 4, 8], F32)
    s.dma("sp", PH[:, :, :], D["l0_ph"][:, :].rearrange("p (h n) -> p h n", h=4), writes=PH.all())
    OM = p.sb("OM", [128, 4, 4], F32)
    e_ts(s, "dve", rf(OM)[:, :, 0:3], rf(PH)[:, :, 0:3], -1.0, 1.0, ALU.mult, ALU.add)
    e_ts(s, "dve", rf(OM)[:, :, 3:4], rf(PH)[:, :, 6:7], -1.0, 1.0, ALU.mult, ALU.add)
    RKb = p.sb("RKb", [128, 4], BF16)
    e_copy(s, "dve", rf(RKb)[:, :], rf(PH)[:, :, 7])
    MUL = p.sb("MUL", [128, 3], F32)
    s.dma("sp", MUL[:, :], D["l0_mul"][:, :], writes=MUL.all())
    OML = p.sb("OML", [128, 3], F32)
    e_ts(s, "dve", rf(OML)[:, :], rf(MUL)[:, :], -1.0, 1.0, ALU.mult, ALU.add)
    LNGBs = [p.sb(f"LNGB{i}", [128, 2, 64], F32) for i in range(2)]
    W2 = p.sb("W2", [64, 512], BF16)
    A2 = p.sb("A2", [64, 512], BF16)
    G2 = p.sb("G2", [128, 512], BF16)
    s.dma("pool", W2[:, :], D["l0_w2"][:, :], writes=W2.all())
    s.dma("pool", A2[:, :], D["l0_a2"][:, :], writes=A2.all())
    s.dma("pool", G2[:, :], D["l0_g2"][:, :], writes=G2.all())
    ob = C["ones_bf"]
    BLK = p.sb("BLK", [128, 128], BF16)
    e_memset(s, "dve", rf(BLK)[:, :], 0.0)
    e_memset(s, "dve", rf(BLK)[0:64, 0:64], 1.0)
    e_memset(s, "dve", rf(BLK)[64:128, 64:128], 1.0)
    M512 = p.sb("M512", [128, 512], BF16)
    MUS = p.sb("MUS", [128, 512], BF16)
    MUI = p.sb("MUI", [128, 512], BF16)
    MLS = p.sb("MLS", [128, 512], BF16)
    ID8 = p.sb("ID8", [128, 512], BF16)
    for hh in range(2):
        hs = slice(hh * 64, hh * 64 + 64)
        for dst, pat, cmp_, cm in ((M512, [[0, 8], [1, 64]], ALU.is_gt, 0), (MUS, [[0, 8], [1, 64]], ALU.is_gt, -1),
                                   (MUI, [[0, 8], [1, 64]], ALU.is_ge, -1), (MLS, [[0, 8], [-1, 64]], ALU.is_gt, 1),
                                   (ID8, [[0, 8], [-1, 64]], ALU.is_equal, 1)):
            s.op("pool", lambda E, dst=dst, pat=pat, cmp_=cmp_, cm=cm, hs=hs: E.affine_select(
                out=dst[hs, :], in_=ob[hs, :], pattern=pat, compare_op=cmp_, fill=0.0, base=0, channel_multiplier=cm),
                reads=ob.all(), writes=dst.all())
    ident = C["ident"]

    TW = p.sb("TW", [64, SEQ], BF16)
    AL = p.sb("AL", [64, SEQ], BF16)
    SGL = p.sb("SGL", [128, SEQ], BF16)
    with k.phase() as p0:
        WLo = p0.sb("WLo", [128, 8, 256], BF16)
        s.dma("pool", WLo[:, :, :], D["l0_w_lora"][:, :].rearrange("p (kc n) -> p kc n", kc=8), writes=WLo.all())
        PAl = [p0.sb(f"PAl{i}", [128, 513], F32) for i in range(3)]
        TMPl = [p0.sb(f"TMPl{i}", [128, 512], F32) for i in range(3)]

        def lora_chain(which, c0, c1, npart, dst):
            PA, tmpl = PAl[which], TMPl[which]
            for tb in range(4):
                tok = tb * 512
                pp = PS[which * 2 + tb % 2]
                for kc in range(8):
                    e_mm(s, rf(pp)[0:npart, :], rf(WLo)[:, kc, c0:c1], Ref(HT[:, kc, tok:tok + 512], HT.b(tb)), kc == 0, kc == 7)
                if tb == 0:
                    e_memset(s, "dve", rf(PA)[0:npart, 0:1], 0.0)
                else:
                    e_copy(s, "dve", rf(PA)[0:npart, 0:1], rf(PA)[0:npart, 512:513])
                yield
                e_copy(s, "act", rf(PA)[0:npart, 1:513], rf(pp)[0:npart, :])
                yield
                e_act(s, rf(tmpl)[0:npart, :], rf(PA)[0:npart, 0:512], AF.Copy, scale=rf(MUL)[0:npart, which:which + 1])
                yield
                e_stt(s, "dve", rf(tmpl)[0:npart, :], rf(PA)[0:npart, 1:513], rf(OML)[0:npart, which:which + 1],
                      rf(tmpl)[0:npart, :], ALU.mult, ALU.add)
                yield
                if which == 0:
                    e_act(s, rf(dst)[:, tok:tok + 512], rf(tmpl)[0:64, :], AF.Tanh)
                elif which == 1:
                    e_copy(s, "act", rf(dst)[:, tok:tok + 512], rf(tmpl)[0:64, :])
                else:
                    e_act(s, rf(dst)[:, tok:tok + 512], rf(tmpl)[:, :], AF.Sigmoid)
                yield

        lbg = Bg()
        for which, (c0, c1, npart, dst) in enumerate(((0, 64, 64, TW), (64, 128, 64, AL), (128, 256, 128, SGL))):
            lbg.add(lora_chain(which, c0, c1, npart, dst), 1)
        lbg.drain()

    s.barrier()
    XTf = XT.t
    XTb = XT.t.bitcast(BF16)
    f32n = ("r", "k", "SIG", "A", "KKN", "KH", "CUM", "EC", "EX", "EN", "TMP")
    b16n = ("Rt", "At", "Bt", "Kt", "Bh", "Kh", "RK", "VT", "KK2")
    t64n = ("V64", "BH64", "KH64", "N", "Q", "N2", "Q2", "XA", "LAK", "ARB", "ARK")
    sets = []
    for S in range(2):
        B = {}
        if S == 0:
            B["WH"] = p.sb("WH", [128, 8, 384], BF16)
            B["PA"] = [p.sb(f"PA{i}", [128, 513], F32) for i in range(3)]
            F = {n: p.sb("f_" + n, [128, 512], F32) for n in f32n}
            Bf = {n: p.sb("b_" + n, [128, 512], BF16) for n in b16n}
            T64 = {n: p.sb("t_" + n, [128, 512], BF16) for n in t64n}
        else:
            B["WH"] = T("WH1", XTb[:, 7, 0:3072].rearrange("p (kc n) -> p kc n", kc=8))
            B["PA"] = [T(f"PA1_{i}", XTf[:, 3, i * 513:(i + 1) * 513]) for i in range(3)]
            F = {n: T("f1_" + n, XTf[:, i // 4, (i % 4) * 512:(i % 4) * 512 + 512]) for i, n in enumerate(f32n)}
            bl = list(b16n) + list(t64n)
            vb = {n: T("b1_" + n, XTb[:, 4 + i // 8, (i % 8) * 512:(i % 8) * 512 + 512]) for i, n in enumerate(bl)}
            Bf = {n: vb[n] for n in b16n}
            T64 = {n: vb[n] for n in t64n}
        F["EH"] = F["TMP"]
        F["BA"] = F["SIG"]
        T64["YA"] = T64["LAK"]
        B["F"], B["Bf"], B["T64"] = F, Bf, T64
        B["R0"], B["YF"], B["YQ"], B["GT"] = F["SIG"], F["EX"], F["EN"], F["KH"]
        B["ST"] = p.sb(f"ST{S}", [128, 8, 4], F32)
        B["RKS"] = p.sb(f"RKS{S}", [128, 8], F32)
        B["Pf"] = p.sb(f"Pf{S}", [128, 64], F32)
        B["Pb"] = p.sb(f"Pb{S}", [128, 64], BF16)
        B["RR"] = p.sb(f"RR{S}", [128, 64], BF16)
        B["UB"] = p.sb(f"UB{S}", [128, 64], BF16)
        B["LNGB"] = LNGBs[S]
        B["PY"] = PS[4 + S]
        B["PT1"] = PS[6 + S]
        sets.append(B)
    b3 = lambda r_: Ref(r_.ap.rearrange("p (a b) -> p a b", a=8), r_.bufs)
    HS = (slice(0, 64), slice(64, 128))
    st = {"sci": 0}

    def newps():
        st["sci"] += 1
        return PS[st["sci"] % 4]

    def mm2(out_t, col0, ncol, lhs_fn, rhs_fn, start=True, stop=True):
        for hs in HS:
            e_mm(s, rf(out_t)[hs, col0:col0 + ncol], lhs_fn(hs), rhs_fn(hs), start, stop)

    def unit(hp, gq, B):
        F, Bf, T64, PA, w = B["F"], B["Bf"], B["T64"], B["PA"], B["WH"]
        R0, YF, YQ, GT, ST, RKS = B["R0"], B["YF"], B["YQ"], B["GT"], B["ST"], B["RKS"]
        Pf, Pb, RR, UB, LNGB = B["Pf"], B["Pb"], B["RR"], B["UB"], B["LNGB"]
        hc = lambda n: rf(PH)[:, hp, n:n + 1]
        tok = gq * 512
        for which, nm in enumerate(("r", "k", "v")):
            pp = newps()
            for kc in range(8):
                e_mm(s, rf(pp)[:, :], rf(w)[:, kc, which * 128:(which + 1) * 128],
                     Ref(HT[:, kc, tok:tok + 512], HT.b(gq)), kc == 0, kc == 7)
            pa = PA[which]
            if gq == 0:
                e_memset(s, "dve", rf(pa)[:, 0:1], 0.0)
            else:
                e_copy(s, "dve", rf(pa)[:, 0:1], rf(pa)[:, 512:513])
            e_copy(s, "act", rf(pa)[:, 1:513], rf(pp)[:, :])
            yield
            tmp_ = rf(F["TMP"])[:, :] if which != 1 else rf(F["CUM"])[:, :]
            e_act(s, tmp_, rf(pa)[:, 0:512], AF.Copy, scale=hc(which))
            dst_ = rf(Bf["VT"])[:, :] if nm == "v" else rf(F[nm])[:, :]
            e_stt(s, "dve", dst_, rf(pa)[:, 1:513], rf(OM)[:, hp, which:which + 1], tmp_, ALU.mult, ALU.add)
        r_, k_ = rf(F["r"])[:, :], rf(F["k"])[:, :]
        pz = newps()
        e_mm(s, rf(pz)[:, :], rf(W2)[:, hp * 128:(hp + 1) * 128], rf(TW)[:, tok:tok + 512])
        e_act(s, rf(F["SIG"])[:, :], rf(pz)[:, :], AF.Sigmoid, bias=hc(3))
        pz2 = newps()
        e_mm(s, rf(pz2)[:, :], rf(A2)[:, hp * 128:(hp + 1) * 128], rf(AL)[:, tok:tok + 512])
        e_act(s, rf(F["A"])[:, :], rf(pz2)[:, :], AF.Sigmoid, bias=hc(4))
        yield
        e_ts(s, "dve", rf(F["KKN"])[:, :], k_, hc(5), None, ALU.mult)
        e_act(s, rf(Bf["KK2"])[:, :], rf(F["KKN"])[:, :], AF.Square)
        yield
        pz = newps()
        e_mm(s, rf(pz)[:, :], rf(BLK)[:, :], rf(Bf["KK2"])[:, :])
        e_act(s, rf(F["TMP"])[:, :], rf(pz)[:, :], AF.Sqrt)
        yield
        e_ts(s, "dve", rf(F["TMP"])[:, :], rf(F["TMP"])[:, :], 1e-12, None, ALU.max)
        s.op("dve", lambda E: E.reciprocal(out=F["TMP"][:, :], in_=F["TMP"][:, :]), reads=F["TMP"].all(),
             writes=F["TMP"].all())
        e_tt(s, "dve", rf(F["KKN"])[:, :], rf(F["KKN"])[:, :], rf(F["TMP"])[:, :], ALU.mult)
        e_act(s, rf(F["KH"])[:, :], rf(F["A"])[:, :], AF.Identity, bias=rf(OM)[:, hp, 3:4], scale=hc(6))
        e_tt(s, "dve", rf(F["KH"])[:, :], rf(F["KH"])[:, :], k_, ALU.mult)
        yield
        s.op("dve", lambda E: E.tensor_tensor_scan(out=F["CUM"][:, :], data0=M512[:, :], data1=F["SIG"][:, :],
                                                   initial=0.0, op0=ALU.mult, op1=ALU.add),
             reads=M512.all() + F["SIG"].all(), writes=F["CUM"].all())
        yield
        cum = rf(F["CUM"])[:, :]
        e_act(s, rf(F["EC"])[:, :], cum, AF.Exp, scale=NEG_EXP_HALF)
        e_tt(s, "dve", rf(F["EX"])[:, :], cum, rf(F["SIG"])[:, :], ALU.subtract)
        e_act(s, rf(F["EN"])[:, :], cum, AF.Exp, scale=-NEG_EXP_HALF)
        cum3 = b3(cum)
        cend = Ref(cum3.ap[:, :, 63:64].to_broadcast([128, 8, 64]), cum.bufs)
        e_tt(s, "dve", b3(rf(F["EH"])[:, :]), cend, cum3, ALU.subtract)
        yield
        e_act(s, rf(F["EX"])[:, :], rf(F["EX"])[:, :], AF.Exp, scale=NEG_EXP_HALF)
        e_act(s, rf(F["EH"])[:, :], rf(F["EH"])[:, :], AF.Exp, scale=NEG_EXP_HALF)
        e_tt(s, "dve", rf(Bf["Rt"])[:, :], r_, rf(F["EC"])[:, :], ALU.mult)
        e_tt(s, "dve", rf(Bf["RK"])[:, :], r_, rf(F["KH"])[:, :], ALU.mult)
        e_tt(s, "dve", rf(F["BA"])[:, :], rf(F["KKN"])[:, :], rf(F["A"])[:, :], ALU.mult)
        yield
        e_stt(s, "dve", rf(Bf["At"])[:, :], rf(F["KKN"])[:, :], -1.0, rf(F["EX"])[:, :], ALU.mult, ALU.mult)
        e_tt(s, "dve", rf(Bf["Bt"])[:, :], rf(F["BA"])[:, :], rf(F["EN"])[:, :], ALU.mult)
        e_tt(s, "dve", rf(Bf["Bh"])[:, :], rf(F["BA"])[:, :], rf(F["EH"])[:, :], ALU.mult)
        e_tt(s, "dve", rf(Bf["Kt"])[:, :], rf(F["KH"])[:, :], rf(F["EN"])[:, :], ALU.mult)
        e_tt(s, "dve", rf(Bf["Kh"])[:, :], rf(F["KH"])[:, :], rf(F["EH"])[:, :], ALU.mult)
        yield
        blk = lambda n, c8, hs: rf(Bf[n])[hs, c8 * 64:(c8 + 1) * 64]
        tb_ = lambda n, c8, hs: rf(T64[n])[hs, c8 * 64:(c8 + 1) * 64]
        idh = lambda hs: rf(ident)[hs, hs]
        for src, dst in (("VT", "V64"), ("Bh", "BH64"), ("Kh", "KH64")):
            pt = newps()
            for c8 in range(8):
                mm2(pt, c8 * 64, 64, lambda hs, c8=c8, src=src: blk(src, c8, hs), idh)
            e_copy(s, "act", rf(T64[dst])[:, :], rf(pt)[:, :])
            yield
        for lh, rh, mask, dst in (("Bt", "At", MUS, "N"), ("At", "Bt", MLS, "Q"), ("Kt", "At", MUS, "LAK"),
                                  ("Bt", "Rt", MUI, "ARB"), ("Kt", "Rt", MUI, "ARK")):
            pt = newps()
            for c8 in range(8):
                mm2(pt, c8 * 64, 64, lambda hs, c8=c8, lh=lh: blk(lh, c8, hs), lambda hs, c8=c8, rh=rh: blk(rh, c8, hs))
            e_tt(s, "dve", rf(T64[dst])[:, :], rf(pt)[:, :], rf(mask)[:, :], ALU.mult)
            yield
        e_tt(s, "dve", rf(T64["XA"])[:, :], rf(T64["N"])[:, :], rf(ID8)[:, :], ALU.add)
        Pn, Qn, Pn2, Qn2 = "N", "Q", "N2", "Q2"
        for lvl in range(1, 6):
            pq = newps()
            for c8 in range(8):
                mm2(pq, c8 * 64, 64, lambda hs, c8=c8, Pn=Pn: tb_(Pn, c8, hs), lambda hs, c8=c8, Qn=Qn: tb_(Qn, c8, hs))
            e_copy(s, "act", rf(T64[Qn2])[:, :], rf(pq)[:, :])
            if lvl < 5:
                pp_ = newps()
                for c8 in range(8):
                    mm2(pp_, c8 * 64, 64, lambda hs, c8=c8, Qn=Qn: tb_(Qn, c8, hs),
                        lambda hs, c8=c8, Pn=Pn: tb_(Pn, c8, hs))
                e_copy(s, "act", rf(T64[Pn2])[:, :], rf(pp_)[:, :])
            yield
            px = newps()
            for c8 in range(8):
                mm2(px, c8 * 64, 64, lambda hs, c8=c8, Qn2=Qn2: tb_(Qn2, c8, hs), lambda hs, c8=c8: tb_("XA", c8, hs))
            e_tt(s, "dve", rf(T64["XA"])[:, :], rf(T64["XA"])[:, :], rf(px)[:, :], ALU.add)
            yield
            Pn, Pn2 = Pn2, Pn
            Qn, Qn2 = Qn2, Qn
        pr0 = newps()
        for c8 in range(8):
            mm2(pr0, c8 * 64, 64, lambda hs, c8=c8: tb_("LAK", c8, hs), lambda hs, c8=c8: tb_("V64", c8, hs))
        e_copy(s, "act", rf(R0)[:, :], rf(pr0)[:, :])
        pg = newps()
        e_mm(s, rf(pg)[:, :], rf(G2)[:, hp * 128:(hp + 1) * 128], rf(SGL)[:, tok:tok + 512])
        e_copy(s, "act", rf(GT)[:, :], rf(pg)[:, :])
        yield
        PY, PT1 = B["PY"], B["PT1"]
        pbh = lambda hs: rf(Pb)[hs, :]
        ubh = lambda hs: rf(UB)[hs, :]
        for c8 in range(8):
            mm2(PT1, 0, 64, lambda hs: blk("At", c8, hs), pbh)
            mm2(PY, c8 * 64, 64, lambda hs: blk("Rt", c8, hs), pbh, True, False)
            e_tt(s, "dve", rf(RR)[:, :], rf(PT1)[:, 0:64], rf(R0)[:, c8 * 64:(c8 + 1) * 64], ALU.add)
            yield
            mm2(PT1, 64, 64, lambda hs: tb_("XA", c8, hs), lambda hs: rf(RR)[hs, :])
            e_copy(s, "act", rf(UB)[:, :], rf(PT1)[:, 64:128])
            yield
            mm2(PT1, 128, 64, lambda hs: tb_("KH64", c8, hs), lambda hs: tb_("V64", c8, hs), True, False)
            mm2(PT1, 128, 64, lambda hs: tb_("BH64", c8, hs), ubh, False, True)
            mm2(PY, c8 * 64, 64, lambda hs: tb_("ARB", c8, hs), ubh, False, False)
            mm2(PY, c8 * 64, 64, lambda hs: tb_("ARK", c8, hs), lambda hs: tb_("V64", c8, hs), False, True)
            e_stt(s, "dve", rf(Pf)[:, :], rf(Pf)[:, :], rf(F["EC"])[:, c8 * 64 + 63:c8 * 64 + 64], rf(PT1)[:, 128:192],
                  ALU.mult, ALU.add)
            e_copy(s, "act", rf(Pb)[:, :], rf(Pf)[:, :])
            yield
        e_copy(s, "act", rf(YF)[:, :], rf(PY)[:, :])
        yf3 = b3(rf(YF)[:, :])
        yq3 = b3(rf(YQ)[:, :])
        prk = newps()
        for c8 in range(8):
            mm2(prk, c8, 1, lambda hs: blk("RK", c8, hs), lambda hs: rf(RKb)[hs, hp:hp + 1])
        e_copy(s, "act", rf(RKS)[:, :], rf(prk)[:, 0:8])
        yield
        s.op("dve", lambda E: E.tensor_reduce(out=ST[:, :, 0], in_=YF[:, :].rearrange("p (a b) -> p a b", a=8),
                                              axis=AX.X, op=ALU.add), reads=YF.all(), writes=ST.all())
        e_act(s, rf(YQ)[:, :], rf(YF)[:, :], AF.Square)
        yield
        s.op("dve", lambda E: E.tensor_reduce(out=ST[:, :, 1], in_=YQ[:, :].rearrange("p (a b) -> p a b", a=8),
                                              axis=AX.X, op=ALU.add), reads=YQ.all(), writes=ST.all())
        e_ts(s, "dve", rf(ST)[:, :, 2], rf(ST)[:, :, 0], 1.0 / 64, None, ALU.mult)
        e_tt(s, "dve", rf(ST)[:, :, 0], rf(ST)[:, :, 2], rf(ST)[:, :, 2], ALU.mult)
        e_stt(s, "dve", rf(ST)[:, :, 1], rf(ST)[:, :, 1], 1.0 / 64, rf(ST)[:, :, 0], ALU.mult, ALU.subtract)
        e_ts(s, "dve", rf(ST)[:, :, 1], rf(ST)[:, :, 1], GN_EPS, None, ALU.add)
        e_act(s, rf(ST)[:, :, 3], rf(ST)[:, :, 1], AF.Sqrt)
        yield
        s.op("dve", lambda E: E.reciprocal(out=ST[:, :, 3], in_=ST[:, :, 3]), reads=ST.all(), writes=ST.all())
        mean_b = Ref(ST[:, :, 2:3].to_broadcast([128, 8, 64]), ST.all())
        rstd_b = Ref(ST[:, :, 3:4].to_broadcast([128, 8, 64]), ST.all())
        e_tt(s, "dve", yf3, yf3, mean_b, ALU.subtract)
        e_tt(s, "dve", yf3, yf3, rstd_b, ALU.mult)
        rks_b = Ref(RKS[:, :].rearrange("p (a b) -> p a b", b=1).to_broadcast([128, 8, 64]), RKS.all())
        e_tt(s, "dve", yq3, b3(rf(T64["V64"])[:, :]), rks_b, ALU.mult)
        lng = Ref(LNGB[:, 0:1, :].to_broadcast([128, 8, 64]), LNGB.all())
        lnb = Ref(LNGB[:, 1:2, :].to_broadcast([128, 8, 64]), LNGB.all())
        yield
        e_tt(s, "dve", yf3, yf3, lng, ALU.mult)
        e_tt(s, "dve", yf3, yf3, lnb, ALU.add)
        yield
        e_tt(s, "dve", rf(T64["YA"])[:, :], rf(YF)[:, :], rf(YQ)[:, :], ALU.add)
        yield
        pt = newps()
        for c8 in range(8):
            mm2(pt, c8 * 64, 64, lambda hs: tb_("YA", c8, hs), idh)
        s.op("dve", lambda E: E.tensor_tensor(
            out=OT[:, hp, tok:tok + 512], in0=pt[:, :], in1=GT[:, :], op=ALU.mult),
            reads=pt.all() + GT.all(), writes=OT.all())
        yield

    def stream(S, hps):
        B = sets[S]
        for hp in hps:
            w = B["WH"]
            s.dma("pool", w[:, :, :], D["l0_w_hp"][hp].rearrange("p (kc n) -> p kc n", kc=8), writes=w.all())
            LNGB = B["LNGB"]
            for hh in range(2):
                h = 2 * hp + hh
                s.dma("sp", LNGB[HS[hh], 0, :], D["l0_lnx_g"][h * 64:(h + 1) * 64].partition_broadcast(64),
                      writes=LNGB.all())
                s.dma("sp", LNGB[HS[hh], 1, :], D["l0_lnx_b"][h * 64:(h + 1) * 64].partition_broadcast(64),
                      writes=LNGB.all())
            e_memset(s, "dve", rf(B["Pf"])[:, :], 0.0)
            e_memset(s, "dve", rf(B["Pb"])[:, :], 0.0)
            yield
            for gq in range(4):
                yield from unit(hp, gq, B)

    gens = [stream(0, (0, 2)), stream(1, (1, 3))]
    alive = [True, True]
    first = True
    while any(alive):
        for S in range(2):
            if alive[S]:
                try:
                    next(gens[S])
                except StopIteration:
                    alive[S] = False
            if first and S == 0:
                for _ in range(STAGGER):
                    next(gens[0])
                first = False
    s.barrier()
    for kc in range(8):
        s.dma("sp", XT[:, kc, :], spill[:, kc, :], reads=spill.all(), writes=XT.all())


GAIN_NAMES = ["l0_ffn1_pre_g", "l0_ffn1_post_g", "l0_mix_pre_g", "l0_mix_post_g", "l0_ffn2_pre_g", "l0_ffn2_post_g",
              "l1_ffn1_pre_g", "l1_ffn1_post_g", "l1_mix_pre_g", "l1_mix_post_g", "l1_ffn2_pre_g", "l1_ffn2_post_g"]
HALF_GAINS = [1, 5, 7, 11]


def build_program(stages=("f01", "m0", "f02f11", "m1", "f12"), dbg=False):
    k = KB()
    nc = k.nc
    s = k.s
    xT_d = k.dram_in("xT", [DM, SEQ])
    gains_d = k.dram_in("gains", [128, 12 * 8])
    ffn_d = {}
    for nm in ("l0_ffn1", "l0_ffn2", "l1_ffn1", "l1_ffn2"):
        ffn_d[nm] = (k.dram_in(nm + "_w_in", [NJ, 128, 2048]), k.dram_in(nm + "_w_out", [2, 8, 128, 11 * 128]))
    D = {}
    for nm, shp in (("l0_pl", [128, 32]), ("l0_ph", [128, 32]), ("l0_mul", [128, 3]), ("l0_lnx_g", [512]),
                    ("l0_lnx_b", [512]), ("l0_w2", [64, 512]), ("l0_a2", [64, 512]), ("l0_g2", [128, 512]),
                    ("l0_gate_a_w", [8, 64, 64]), ("l0_gate_x_w", [8, 64, 64]), ("l0_w_lru", [4, 128, 8 * 256]),
                    ("l0_w_lora", [128, 8 * 256]), ("l0_w_hp", [4, 128, 8 * 384]), ("l0_w_out", [DM, DM])):
        D[nm] = k.dram_in(nm, shp)
    D["xt_spill"] = T("xt_spill", nc.dram_tensor("xt_spill", [128, 8, SEQ], F32, kind="Internal").ap())
    if dbg:
        D["dbg_OT"] = k.dram_out("dbg_OT", [128, 8, SEQ], BF16)
    wqkv_d = k.dram_in("l1_w_qkv", [8, 128, 8 * 384])
    l1_wo_d = k.dram_in("l1_w_out", [DM, DM])
    outT_d = k.dram_out("outT", [DM, SEQ])

    with k.es:
        XT = k.sb("XT", [128, 8, SEQ], F32, parts=4)
        C = {}
        C["gains"] = k.sb("gains", [128, 12, 8], F32)
        C["ones_m"] = k.sb("ones_m", [128, 128], BF16)
        PS = [k.ps(f"ps{i}", [128, 512]) for i in range(8)]
        C["HTG"] = k.sb("HTG", [128, 8, SEQ], BF16, parts=4)
        C["ht_ready"] = None

        s.op("dve", lambda e: e.memset(C["ones_m"][:, :], 1.0 / DM), writes=C["ones_m"].all())
        C["one_f"] = k.sb("one_f", [128, 1], F32)
        C["ones_col"] = k.sb("ones_col", [128, 1], BF16)
        C["ones_bf"] = k.sb("ones_bf", [128, 512], BF16)
        C["ident"] = k.sb("ident", [128, 128], BF16)
        C["ntri"] = k.sb("ntri", [128, 128], BF16)
        s.op("dve", lambda e: e.memset(C["one_f"][:, :], 1.0), writes=C["one_f"].all())
        s.op("dve", lambda e: e.memset(C["ones_col"][:, :], 1.0), writes=C["ones_col"].all())
        s.op("dve", lambda e: e.memset(C["ones_bf"][:, :], 1.0), writes=C["ones_bf"].all())
        s.op("pool", lambda e: e.affine_select(out=C["ident"][:, :], in_=C["ones_bf"][:, 0:128], pattern=[[-1, 128]],
                                               compare_op=ALU.is_equal, fill=0.0, base=0, channel_multiplier=1),
             reads=C["ones_bf"].all(), writes=C["ident"].all())
        s.op("pool", lambda e: e.affine_select(out=C["ntri"][:, :], in_=C["ones_bf"][:, 0:128], pattern=[[-1, 128]],
                                               compare_op=ALU.is_ge, fill=0.0, base=0, channel_multiplier=1),
             reads=C["ones_bf"].all(), writes=C["ntri"].all())
        s.op("dve", lambda e: e.tensor_scalar(out=C["ntri"][:, :], in0=C["ntri"][:, :], scalar1=-1.0, scalar2=None,
                                              op0=ALU.mult),
             reads=C["ntri"].all(), writes=C["ntri"].all())
        C["eps"] = k.sb("eps", [128, 1], F32)
        s.op("dve", lambda e: e.memset(C["eps"][:, :], NORM_EPS), writes=C["eps"].all())
        s.dma("sp", C["gains"][:, :, :], gains_d[:, :].rearrange("p (n c) -> p n c", n=12), writes=C["gains"].all())
        for gi in HALF_GAINS:
            s.op("dve", lambda e, gi=gi: e.tensor_scalar(out=C["gains"][:, gi, :], in0=C["gains"][:, gi, :],
                                                         scalar1=0.5, scalar2=None, op0=ALU.mult),
                 reads=C["gains"].all(), writes=C["gains"].all())
        for tb in range(4):
            for kc in range(8):
                s.dma("sp", XT[:, kc, tb * 512:(tb + 1) * 512], xT_d[kc * 128:(kc + 1) * 128, tb * 512:(tb + 1) * 512],
                      writes=XT.b(tb))

        PRE_GI = {"f01": 0, "m0": 2, "f02": 4, "f02f11": 4, "f11": 6, "m1": 8, "f12": 10}
        for si, st in enumerate(stages):
            nxt = PRE_GI[stages[si + 1]] if si + 1 < len(stages) else None
            if st == "f01":
                ffn_stage(k, C, XT, PS, [(*ffn_d["l0_ffn1"], 0, 1)], nxt)
            elif st == "f02":
                ffn_stage(k, C, XT, PS, [(*ffn_d["l0_ffn2"], 4, 5)], nxt)
            elif st == "f02f11":
                ffn_stage(k, C, XT, PS, [(*ffn_d["l0_ffn2"], 4, 5), (*ffn_d["l1_ffn1"], 6, 7)], nxt)
            elif st == "f11":
                ffn_stage(k, C, XT, PS, [(*ffn_d["l1_ffn1"], 6, 7)], nxt)
            elif st == "m0":
                mixer0_stage(k, C, XT, PS, D, 2, 3, nxt)
            elif st == "m1":
                if dbg:
                    C["dbg_OT"] = D["dbg_OT"]
                attn_stage(k, C, XT, PS, wqkv_d, l1_wo_d, 8, 9, nxt)
            elif st == "f12":
                ffn_stage(k, C, XT, PS, [(*ffn_d["l1_ffn2"], 10, 11)], nxt)

        for tb in range(4):
            for kc in range(8):
                s.dma("sp", outT_d[kc * 128:(kc + 1) * 128, tb * 512:(tb + 1) * 512], XT[:, kc, tb * 512:(tb + 1) * 512],
                      reads=XT.b(tb), writes=outT_d.all())
        s.barrier(engines=["sp"])
    return nc


def _col(v):
    return np.ascontiguousarray(np.asarray(v, np.float32).reshape(8, 128).T)


def prep_shared(inp):
    d = {}
    d["gains"] = np.ascontiguousarray(np.concatenate([_col(inp[n]) for n in GAIN_NAMES], axis=1))
    for nm in ("l0_ffn1", "l0_ffn2", "l1_ffn1", "l1_ffn2"):
        w_in = np.asarray(inp[nm + "_w_in"], np.float32)
        w_out = np.asarray(inp[nm + "_w_out"], np.float32)
        g = w_in[:, :DFF].reshape(8, 128, NJ, 128)
        u = w_in[:, DFF:].reshape(8, 128, NJ, 128)
        gu = np.concatenate([g, u], axis=3)
        d[nm + "_w_in"] = np.ascontiguousarray(gu.transpose(2, 1, 0, 3).reshape(NJ, 128, 2048))
        wo = w_out.reshape(2, 11, 128, 8, 128)
        d[nm + "_w_out"] = np.ascontiguousarray(wo.transpose(0, 3, 2, 1, 4).reshape(2, 8, 128, 11 * 128))
    f = lambda n: np.asarray(inp[n], np.float32)
    cw = f("l0_conv_w")
    pl = np.stack([cw[0], cw[1], cw[2], cw[3], f("l0_conv_b"), f("l0_gate_a_b"), f("l0_gate_x_b"), f("l0_lambda")], axis=1)
    d["l0_pl"] = np.ascontiguousarray(pl.reshape(4, 128, 8).transpose(1, 0, 2).reshape(128, 32))
    mu = f("l0_mu")
    ph = np.stack([mu[0:512], mu[512:1024], mu[1024:1536], f("l0_w0"), f("l0_a0"), f("l0_k_k"), f("l0_k_a"),
                   f("l0_r_k").reshape(512)], axis=1)
    d["l0_ph"] = np.ascontiguousarray(ph.reshape(4, 128, 8).transpose(1, 0, 2).reshape(128, 32))
    mul = np.zeros((128, 3), np.float32)
    mul[0:64, 0] = mu[1536:1600]
    mul[0:64, 1] = mu[1600:1664]
    mul[:, 2] = mu[1664:1792]
    d["l0_mul"] = mul
    for n in ("l0_lnx_g", "l0_lnx_b", "l0_w2", "l0_a2", "l0_g2", "l0_gate_a_w", "l0_gate_x_w", "l0_w_out"):
        d[n] = np.ascontiguousarray(f(n))
    wi = f("l0_w_in").reshape(8, 128, 2816)
    lru = np.concatenate([wi[:, :, 1792:2304].reshape(8, 128, 4, 128), wi[:, :, 2304:2816].reshape(8, 128, 4, 128)], axis=3)
    d["l0_w_lru"] = np.ascontiguousarray(lru.transpose(2, 1, 0, 3).reshape(4, 128, 8 * 256))
    d["l0_w_lora"] = np.ascontiguousarray(wi[:, :, 1536:1792].transpose(1, 0, 2).reshape(128, 8 * 256))
    hd = np.stack([wi[:, :, 0:512].reshape(8, 128, 4, 128), wi[:, :, 512:1024].reshape(8, 128, 4, 128),
                   wi[:, :, 1024:1536].reshape(8, 128, 4, 128)], axis=3)
    d["l0_w_hp"] = np.ascontiguousarray(hd.transpose(2, 1, 0, 3, 4).reshape(4, 128, 8 * 384))
    wq = np.asarray(inp["l1_w_qkv"], np.float32).reshape(8, 128, 3, 8, 128)
    d["l1_w_qkv"] = np.ascontiguousarray(wq.transpose(3, 1, 0, 2, 4).reshape(8, 128, 8 * 384))
    d["l1_w_out"] = np.ascontiguousarray(np.asarray(inp["l1_w_out"], np.float32))
    return d


_CACHE = {}


def kernel(**inputs):
    x = np.asarray(inputs["x"], np.float32)
    shared = prep_shared(inputs)
    if "nc" not in _CACHE:
        _CACHE["nc"] = build_program()
    nc = _CACHE["nc"]
    in_maps = []
    for c in range(N_CORES):
        m = dict(shared)
        m["xT"] = np.ascontiguousarray(x[c].T)
        in_maps.append(m)
    res = run_bass_kernel_spmd(nc, in_maps, core_ids=list(range(N_CORES)))
    out = np.stack([np.ascontiguousarray(res.results[c]["outT"].T) for c in range(N_CORES)], axis=0)
    return out.astype(np.float32)
```

```python
import math
from contextlib import ExitStack

import numpy as np
import concourse.bass as bass
import concourse.mybir as mybir
from concourse.bass_utils import run_bass_kernel_spmd

F32 = mybir.dt.float32
BF16 = mybir.dt.bfloat16
AF = mybir.ActivationFunctionType
ALU = mybir.AluOpType

SEQ = 2048
DM = 1024
DFF = 2816
NJ = 22
NORM_EPS = 1e-6
N_CORES = 8


class Buf:
    __slots__ = ("name", "w", "r")

    def __init__(self, name):
        self.name = name
        self.w = None
        self.r = {}


class T:
    def __init__(self, name, t, parts=1):
        self.name = name
        self.t = t
        self.bufs = [Buf(f"{name}.{i}") for i in range(parts)]

    def b(self, *idx):
        return [self.bufs[i] for i in idx]

    def all(self):
        return list(self.bufs)

    def __getitem__(self, key):
        return self.t[key]


class Sched:
    COMPUTE = ("pe", "act", "dve", "pool")

    def __init__(self, nc, es, n_dma_ch=20):
        self.nc = nc
        self.eng = {"pe": nc.tensor, "act": nc.scalar, "dve": nc.vector, "pool": nc.gpsimd, "sp": nc.sync}
        self.sems = {}
        self.cnt = {}
        for e in self.COMPUTE:
            self.sems[e] = es.enter_context(nc.semaphore(f"s_{e}"))
            self.cnt[e] = 0
        self.ch = {}
        self.ch_next = {}
        for q in ("sp", "pool", "act"):
            n = n_dma_ch if q != "act" else 4
            lst = []
            for i in range(n):
                key = f"d_{q}{i}"
                self.sems[key] = es.enter_context(nc.semaphore(key))
                self.cnt[key] = 0
                lst.append(key)
            self.ch[q] = lst
            self.ch_next[q] = 0
        self.seen = {e: {} for e in self.eng}
        self.n_wait = 0
        self.n_ins = 0

    def _wait(self, e, ev):
        key, val = ev
        if val <= 0:
            return
        if self.seen[e].get(key, 0) >= val:
            return
        self.seen[e][key] = val
        self.eng[e].wait_ge(self.sems[key], val)
        self.n_wait += 1

    def _deps(self, e, reads, writes):
        evs = {}

        def need(ev):
            if ev is None:
                return
            k_, v_ = ev
            if e == "pe" and k_ == "pe":
                return
            if evs.get(k_, 0) < v_:
                evs[k_] = v_

        for b in reads:
            need(b.w)
        for b in writes:
            need(b.w)
            for kv in b.r.items():
                need(kv)
        return evs

    def op(self, e, fn, reads=(), writes=()):
        evs = self._deps(e, reads, writes)
        for ev in evs.items():
            self._wait(e, ev)
        ins = fn(self.eng[e])
        self.cnt[e] += 1
        ev = (e, self.cnt[e])
        ins.then_inc(self.sems[e], 1)
        self.seen[e][e] = max(self.seen[e].get(e, 0), 0)
        for b in writes:
            b.w = ev
            b.r = {}
        for b in reads:
            if b.w is not ev:
                b.r[e] = self.cnt[e]
        self.n_ins += 1
        return ins

    def dma(self, q, out, in_, reads=(), writes=()):
        e = q
        evs = self._deps(e, reads, writes)
        key = self.ch[q][self.ch_next[q]]
        self.ch_next[q] = (self.ch_next[q] + 1) % len(self.ch[q])
        if evs.get(key, 0) < self.cnt[key]:
            evs[key] = self.cnt[key]
        for ev in evs.items():
            self._wait(e, ev)
        ins = self.eng[e].dma_start(out=out, in_=in_)
        self.cnt[key] += 16
        ins.then_inc(self.sems[key], 16)
        ev = (key, self.cnt[key])
        for b in writes:
            b.w = ev
            b.r = {}
        for b in reads:
            b.r[key] = self.cnt[key]
        self.n_ins += 1
        return ins

    def barrier(self, engines=None):
        evs = [(k_, v_) for k_, v_ in self.cnt.items() if v_ > 0]
        for e in (engines or self.eng):
            for ev in evs:
                if ev[0] == e:
                    continue
                self._wait(e, ev)


class Phase:
    def __init__(self, k):
        self.k = k
        self.es = ExitStack()

    def __enter__(self):
        self.es.__enter__()
        return self

    def __exit__(self, *a):
        self.k.s.barrier()
        return self.es.__exit__(*a)

    def sb(self, name, shape, dtype, parts=1):
        self.k.uid += 1
        t = self.es.enter_context(self.k.nc.sbuf_tensor(f"ph_{name}_{self.k.uid}", shape, dtype))
        return T(name, t, parts)


class KB:
    def __init__(self):
        self.nc = bass.Bass("TRN2", target_bir_lowering=False)
        self.es = ExitStack()
        self.s = Sched(self.nc, self.es)
        self.uid = 0

    def sb(self, name, shape, dtype, parts=1):
        t = self.es.enter_context(self.nc.sbuf_tensor("sb_" + name, shape, dtype))
        return T(name, t, parts)

    def ps(self, name, shape, dtype=F32, parts=1):
        t = self.es.enter_context(self.nc.psum_tensor("pp_" + name, shape, dtype))
        return T(name, t, parts)

    def dram_in(self, name, shape, dtype=F32):
        return T(name, self.nc.dram_tensor(name, list(shape), dtype, kind="ExternalInput").ap())

    def dram_out(self, name, shape, dtype=F32):
        return T(name, self.nc.dram_tensor(name, list(shape), dtype, kind="ExternalOutput").ap())

    def phase(self):
        return Phase(self)


class Bg:
    def __init__(self):
        self.q = []

    def add(self, gen, period=2):
        self.q.append([gen, period, period])

    def tick(self):
        for item in list(self.q):
            item[2] -= 1
            if item[2] <= 0:
                item[2] = item[1]
                try:
                    next(item[0])
                except StopIteration:
                    self.q.remove(item)

    def drain(self):
        while self.q:
            for item in list(self.q):
                try:
                    next(item[0])
                except StopIteration:
                    self.q.remove(item)


def rms_rstd_gen(k, C, src, src_bufs, SQ, PST, RSTD, ntok, fuse_sq=False):
    s = k.s
    s.op("act", lambda e: e.activation(out=SQ[:, :, 0:ntok], in_=src, func=AF.Square),
         reads=src_bufs, writes=SQ.all())
    if not fuse_sq:
        yield
    for kc in range(8):
        s.op("pe", lambda e, kc=kc: e.matmul(PST[:, 0:ntok], lhsT=C["ones_m"][:, :], rhs=SQ[:, kc, 0:ntok],
                                             start=(kc == 0), stop=(kc == 7)),
             reads=SQ.all() + C["ones_m"].all(), writes=PST.all())
    yield
    s.op("act", lambda e: e.activation(out=RSTD[:, 0:ntok], in_=PST[:, 0:ntok], func=AF.Ln, bias=C["eps"][:, 0:1]),
         reads=PST.all() + C["eps"].all(), writes=RSTD.all())
    yield
    s.op("act", lambda e: e.activation(out=RSTD[:, 0:ntok], in_=RSTD[:, 0:ntok], func=AF.Exp, scale=-0.5),
         reads=RSTD.all(), writes=RSTD.all())


def ffn_stage(k, C, XT, PS, ffns, next_gi=None):
    s = k.s
    G_ = C["gains"]
    with k.phase() as ph:
        HTG = C["HTG"]
        HTs = []
        for i in range(2):
            hv = T(f"HTv{i}", HTG.t[:, :, i * 1024:(i + 1) * 1024])
            hv.bufs = HTG.bufs[2 * i:2 * i + 2]
            HTs.append(hv)
        ACTT = ph.sb("ACTT", [128, 11, 1024], BF16, parts=22)
        YT = ph.sb("YT", [128, 8, 1024], F32, parts=16)
        SQ = [ph.sb(f"SQ{i}", [128, 8, 512], BF16) for i in range(2)]
        RSTD = [ph.sb(f"RSTD{i}", [128, 512], F32) for i in range(2)]
        WIN = [ph.sb(f"WIN{i}", [128, 8, 256], BF16) for i in range(3)]
        WOUT = [ph.sb(f"WOUT{i}", [128, 11, 128], BF16) for i in range(3)]
        SG = [ph.sb(f"SG{i}", [128, 512], F32) for i in range(2)]
        PG = [PS[0], PS[1]]
        PU = [PS[2], PS[3]]
        PY = [PS[4], PS[5]]
        PST = [PS[6], PS[7]]
        st = {"win": 0, "wout": 0, "pi": 0, "ni": 0}
        jobs = [(f, B) for f in range(len(ffns)) for B in range(2)]

        bg = Bg()

        def prenorm(ji):
            f, B = jobs[ji]
            HT = HTs[ji % 2]
            gi_pre = ffns[f][2]
            for sb_ in range(2):
                tok = B * 1024 + sb_ * 512
                xb = XT.b(B * 2 + sb_)
                n_ = st["ni"] % 2
                st["ni"] += 1
                yield from rms_rstd_gen(k, C, XT[:, :, tok:tok + 512], xb, SQ[n_], PST[n_], RSTD[n_], 512)
                for kc in range(8):
                    if kc == 4:
                        yield
                    s.op("dve", lambda e, kc=kc, tok=tok, sb_=sb_, n_=n_: e.scalar_tensor_tensor(
                        out=HT[:, kc, sb_ * 512:(sb_ + 1) * 512], in0=XT[:, kc, tok:tok + 512],
                        scalar=G_[:, gi_pre, kc:kc + 1], in1=RSTD[n_][:, :], op0=ALU.mult, op1=ALU.mult),
                        reads=xb + RSTD[n_].all() + G_.all(), writes=HT.b(sb_))

        def postnorm(ji):
            f, B = jobs[ji]
            gi_post = ffns[f][3]
            for sb_ in range(2):
                tok = B * 1024 + sb_ * 512
                rhs_sl = slice(sb_ * 512, (sb_ + 1) * 512)
                ybs = YT.b(*[dc * 2 + sb_ for dc in range(8)])
                xb = XT.b(B * 2 + sb_)
                n_ = st["ni"] % 2
                st["ni"] += 1
                yield from rms_rstd_gen(k, C, YT[:, :, rhs_sl], ybs, SQ[n_], PST[n_], RSTD[n_], 512)
                for dc in range(8):
                    if dc % 2 == 0 and dc > 0:
                        yield
                    s.op("dve", lambda e, dc=dc, rhs_sl=rhs_sl, n_=n_: e.scalar_tensor_tensor(
                        out=YT[:, dc, rhs_sl], in0=YT[:, dc, rhs_sl], scalar=G_[:, gi_post, dc:dc + 1],
                        in1=RSTD[n_][:, :], op0=ALU.mult, op1=ALU.mult),
                        reads=YT.b(dc * 2 + sb_) + RSTD[n_].all() + G_.all(), writes=YT.b(dc * 2 + sb_))
                    s.op("dve", lambda e, dc=dc, rhs_sl=rhs_sl, tok=tok: e.tensor_tensor(
                        out=XT[:, dc, tok:tok + 512], in0=XT[:, dc, tok:tok + 512], in1=YT[:, dc, rhs_sl], op=ALU.add),
                        reads=YT.b(dc * 2 + sb_) + xb, writes=xb)

        def up(ji, G, after_first=None):
            f, B = jobs[ji]
            HT = HTs[ji % 2]
            w_in_d = ffns[f][0]
            for jj in range(11):
                j = G * 11 + jj
                W = WIN[st["win"] % 3]
                st["win"] += 1
                s.dma("pool", W[:, :, :], w_in_d[j].rearrange("p (kc c) -> p kc c", kc=8), writes=W.all())
                for sb_ in range(2):
                    pg, pu, sg = PG[st["pi"] % 2], PU[st["pi"] % 2], SG[st["pi"] % 2]
                    st["pi"] += 1
                    rhs_sl = slice(sb_ * 512, (sb_ + 1) * 512)
                    for kc in range(8):
                        s.op("pe", lambda e, kc=kc, pg=pg, W=W, rhs_sl=rhs_sl: e.matmul(
                            pg[:, :], lhsT=W[:, kc, 0:128], rhs=HT[:, kc, rhs_sl], start=(kc == 0), stop=(kc == 7)),
                            reads=W.all() + HT.b(sb_), writes=pg.all())
                    for kc in range(8):
                        s.op("pe", lambda e, kc=kc, pu=pu, W=W, rhs_sl=rhs_sl: e.matmul(
                            pu[:, :], lhsT=W[:, kc, 128:256], rhs=HT[:, kc, rhs_sl], start=(kc == 0), stop=(kc == 7)),
                            reads=W.all() + HT.b(sb_), writes=pu.all())
                    s.op("act", lambda e, pg=pg, sg=sg: e.activation(out=sg[:, :], in_=pg[:, :], func=AF.Silu),
                         reads=pg.all(), writes=sg.all())
                    s.op("dve", lambda e, pu=pu, sg=sg, jj=jj, rhs_sl=rhs_sl: e.tensor_tensor(
                        out=ACTT[:, jj, rhs_sl], in0=sg[:, :], in1=pu[:, :], op=ALU.mult),
                        reads=sg.all() + pu.all(), writes=ACTT.b(jj * 2 + sb_))
                    bg.tick()
                if jj == 0 and after_first is not None:
                    after_first()

        def down(ji, G):
            f, B = jobs[ji]
            w_out_d = ffns[f][1]
            for dc in range(8):
                W = WOUT[st["wout"] % 3]
                st["wout"] += 1
                s.dma("pool", W[:, :, :], w_out_d[G, dc].rearrange("p (jj c) -> p jj c", jj=11), writes=W.all())
                for sb_ in range(2):
                    py = PY[st["pi"] % 2]
                    st["pi"] += 1
                    rhs_sl = slice(sb_ * 512, (sb_ + 1) * 512)
                    for jj in range(11):
                        s.op("pe", lambda e, jj=jj, py=py, W=W, rhs_sl=rhs_sl: e.matmul(
                            py[:, :], lhsT=W[:, jj, :], rhs=ACTT[:, jj, rhs_sl], start=(jj == 0), stop=(jj == 10)),
                            reads=W.all() + ACTT.b(jj * 2 + sb_), writes=py.all())
                    yb = YT.b(dc * 2 + sb_)
                    if G == 0:
                        s.op("act", lambda e, py=py, dc=dc, rhs_sl=rhs_sl: e.activation(
                            out=YT[:, dc, rhs_sl], in_=py[:, :], func=AF.Copy),
                            reads=py.all(), writes=yb)
                    else:
                        s.op("dve", lambda e, py=py, dc=dc, rhs_sl=rhs_sl: e.tensor_tensor(
                            out=YT[:, dc, rhs_sl], in0=YT[:, dc, rhs_sl], in1=py[:, :], op=ALU.add),
                            reads=py.all() + yb, writes=yb)
                    bg.tick()

        def next_prenorm():
            HT = HTs[0]
            for sb_ in range(2):
                tok = sb_ * 512
                xb = XT.b(sb_)
                n_ = st["ni"] % 2
                st["ni"] += 1
                yield from rms_rstd_gen(k, C, XT[:, :, tok:tok + 512], xb, SQ[n_], PST[n_], RSTD[n_], 512)
                for kc in range(8):
                    if kc == 4:
                        yield
                    s.op("dve", lambda e, kc=kc, tok=tok, sb_=sb_, n_=n_: e.scalar_tensor_tensor(
                        out=HT[:, kc, sb_ * 512:(sb_ + 1) * 512], in0=XT[:, kc, tok:tok + 512],
                        scalar=G_[:, next_gi, kc:kc + 1], in1=RSTD[n_][:, :], op0=ALU.mult, op1=ALU.mult),
                        reads=xb + RSTD[n_].all() + G_.all(), writes=HT.b(sb_))

        n = len(jobs)
        assert n % 2 == 0
        if C["ht_ready"] is not None and C["ht_ready"] == (ffns[0][2], (0, 1)):
            pass
        else:
            bg.add(prenorm(0))
            bg.drain()
        C["ht_ready"] = None
        for ji in range(n):
            up(ji, 0, after_first=(lambda ji=ji: bg.add(postnorm(ji - 1), 1)) if ji > 0 else None)
            bg.drain()
            down(ji, 0)
            if ji + 1 < n:
                bg.add(prenorm(ji + 1), 1)
            elif next_gi is not None:
                bg.add(next_prenorm(), 1)
                C["ht_ready"] = (next_gi, (0, 1))
            up(ji, 1)
            bg.drain()
            down(ji, 1)
        bg.add(postnorm(n - 1))
        bg.drain()


def prenorm_to_HT(k, C, ph, XT, HT, PS, gi_pre, col_off=0):
    s = k.s
    G_ = C["gains"]
    with k.phase() as p2:
        SQ = [p2.sb(f"SQ{i}", [128, 8, 512], BF16) for i in range(2)]
        RSTD = [p2.sb(f"RSTD{i}", [128, 512], F32) for i in range(2)]
        bg = Bg()

        def chain(tb):
            tok = tb * 512
            xb = XT.b(tb)
            yield from rms_rstd_gen(k, C, XT[:, :, tok:tok + 512], xb, SQ[tb % 2], PS[6 + tb % 2], RSTD[tb % 2], 512)
            for kc in range(8):
                if kc == 4:
                    yield
                s.op("dve", lambda e, kc=kc: e.scalar_tensor_tensor(
                    out=HT[:, kc, col_off + tok:col_off + tok + 512], in0=XT[:, kc, tok:tok + 512],
                    scalar=G_[:, gi_pre, kc:kc + 1], in1=RSTD[tb % 2][:, :], op0=ALU.mult, op1=ALU.mult),
                    reads=xb + RSTD[tb % 2].all() + G_.all(), writes=HT.b(tb))

        skip = ()
        if C["ht_ready"] is not None and C["ht_ready"][0] == gi_pre and col_off == 0:
            skip = C["ht_ready"][1]
        C["ht_ready"] = None
        for tb in range(4):
            if tb in skip:
                continue
            bg.add(chain(tb), 1)
            bg.tick()
            bg.tick()
        bg.drain()


def outproj_postnorm(k, C, XT, PS, OT, wo_d, gi_post, next_gi=None, WO=None):
    s = k.s
    G_ = C["gains"]
    with k.phase() as p3:
        if WO is None:
            WO = T("WOv", C["HTG"].t[:, :, 1024:2048])
            WO.bufs = C["HTG"].bufs[2:4]
            for kc in range(8):
                s.dma("pool", WO[:, kc, :], wo_d[kc * 128:(kc + 1) * 128, :], writes=WO.all())
        YTs = [p3.sb(f"YT{i}", [128, 8, 512], F32, parts=8) for i in range(2)]
        SQ1 = p3.sb("SQ", [128, 8, 512], BF16)
        RSTD = [p3.sb(f"RSTD{i}", [128, 512], F32) for i in range(2)]
        bg = Bg()

        def chain(tb):
            tok = tb * 512
            YT = YTs[tb % 2]
            yield from rms_rstd_gen(k, C, YT[:, :, :], YT.all(), SQ1, PS[6 + tb % 2], RSTD[tb % 2], 512, fuse_sq=True)
            xb = XT.b(tb)
            for dc in range(8):
                if dc % 2 == 0 and dc > 0:
                    yield
                s.op("dve", lambda e, dc=dc: e.scalar_tensor_tensor(
                    out=YT[:, dc, :], in0=YT[:, dc, :], scalar=G_[:, gi_post, dc:dc + 1],
                    in1=RSTD[tb % 2][:, :], op0=ALU.mult, op1=ALU.mult),
                    reads=YT.b(dc) + RSTD[tb % 2].all() + G_.all(), writes=YT.b(dc))
                s.op("dve", lambda e, dc=dc: e.tensor_tensor(
                    out=XT[:, dc, tok:tok + 512], in0=XT[:, dc, tok:tok + 512], in1=YT[:, dc, :], op=ALU.add),
                    reads=YT.b(dc) + xb, writes=xb)

        if next_gi is not None:
            RSTDn = p3.sb("RSTDn", [128, 512], F32)
        HTG = C["HTG"]

        def next_prenorm():
            for sb_ in range(2):
                tok = sb_ * 512
                xb = XT.b(sb_)
                yield from rms_rstd_gen(k, C, XT[:, :, tok:tok + 512], xb, SQ1, PS[0], RSTDn, 512, fuse_sq=True)
                for kc in range(8):
                    if kc == 4:
                        yield
                    s.op("dve", lambda e, kc=kc, tok=tok: e.scalar_tensor_tensor(
                        out=HTG[:, kc, tok:tok + 512], in0=XT[:, kc, tok:tok + 512],
                        scalar=G_[:, next_gi, kc:kc + 1], in1=RSTDn[:, :], op0=ALU.mult, op1=ALU.mult),
                        reads=xb + RSTDn.all() + G_.all(), writes=HTG.b(sb_))

        pi = 0
        for tb in range(4):
            tok = tb * 512
            YT = YTs[tb % 2]
            if tb == 3 and next_gi is not None:
                bg.add(next_prenorm(), 1)
                C["ht_ready"] = (next_gi, (0, 1))
            for dc in range(8):
                pp = PS[4 + pi % 2]
                pi += 1
                for kc in range(8):
                    s.op("pe", lambda e, kc=kc, dc=dc, pp=pp, tok=tok: e.matmul(
                        pp[:, :], lhsT=WO[:, kc, dc * 128:(dc + 1) * 128], rhs=OT[:, kc, tok:tok + 512],
                        start=(kc == 0), stop=(kc == 7)),
                        reads=WO.all() + OT.all(), writes=pp.all())
                s.op("act", lambda e, dc=dc, pp=pp, YT=YT: e.activation(out=YT[:, dc, :], in_=pp[:, :], func=AF.Copy),
                     reads=pp.all(), writes=YT.b(dc))
                bg.tick()
            bg.drain()
            bg.add(chain(tb), 1)
        bg.drain()


def attn_stage(k, C, XT, PS, wqkv_d, wo_d, gi_pre, gi_post, next_gi=None):
    s = k.s
    with k.phase() as ph:
        OT = ph.sb("OT", [128, 8, SEQ], BF16, parts=1)
        with k.phase() as pab:
            HT = C["HTG"]
            prenorm_to_HT(k, C, pab, XT, HT, PS, gi_pre)
            with k.phase() as pb:
                NEGM = pb.sb("negm", [128, 4, 512], BF16)
                ZB = pb.sb("zb", [128, 512], BF16)
                s.op("dve", lambda e: e.memset(ZB[:, :], 0.0), writes=ZB.all())
                for d in range(4):
                    s.op("pool", lambda e, d=d: e.affine_select(
                        out=NEGM[:, d, :], in_=ZB[:, :], pattern=[[1, 512]], compare_op=ALU.is_gt,
                        fill=-30000.0, base=-128 * d, channel_multiplier=-1),
                        reads=ZB.all(), writes=NEGM.all())
                QT = [pb.sb(f"QT{i}", [128, SEQ], BF16) for i in range(2)]
                KT = [pb.sb(f"KT{i}", [128, SEQ], BF16) for i in range(2)]
                V = [pb.sb(f"V{i}", [128, 16, 128], BF16) for i in range(2)]
                W = [pb.sb(f"WQKV{i}", [128, 8, 384], BF16) for i in range(1)]
                OTOK = [pb.sb(f"OTOK{i}", [128, 16, 128], BF16) for i in range(2)]
                E = [pb.sb(f"E{i}", [128, 512], F32) for i in range(3)]
                SP = [pb.sb(f"SP{i}", [128, 512], BF16) for i in range(5)]
                ATT = [pb.sb(f"ATT{i}", [128, 512], BF16) for i in range(3)]
                OACC = [pb.sb(f"OACC{i}", [128, 4, 64], F32) for i in range(2)]
                CACC2 = pb.sb("CACC2", [128, 2, 4], F32, parts=2)
                FS2 = [pb.sb(f"FS2_{i}", [128, 2, 4], F32) for i in range(4)]
                PZ = [PS[0], PS[1], PS[2], PS[3], PS[4]]
                PO = [PS[5], PS[6]]
                PP = [PS[7]]
                st = {"pi": 0}

                bg = Bg()

                def pre_hp(hp):
                    w = W[0]
                    qt, kt_, v = QT[hp % 2], KT[hp % 2], V[hp % 2]
                    s.dma("pool", w[:, :, :], wqkv_d[hp].rearrange("p (kc c) -> p kc c", kc=8), writes=w.all())
                    for which in range(2):
                        for tb in range(4):
                            pp = PP[st["pi"] % len(PP)]
                            st["pi"] += 1
                            for kc in range(8):
                                if kc == 4:
                                    yield
                                s.op("pe", lambda e, kc=kc, pp=pp, w=w, which=which, tb=tb: e.matmul(
                                    pp[:, :], lhsT=w[:, kc, which * 128:(which + 1) * 128],
                                    rhs=HT[:, kc, tb * 512:(tb + 1) * 512], start=(kc == 0), stop=(kc == 7)),
                                    reads=w.all() + HT.b(tb), writes=pp.all())
                            if which == 0:
                                s.op("dve", lambda e, pp=pp, qt=qt, tb=tb: e.tensor_scalar(
                                    out=qt[:, tb * 512:(tb + 1) * 512], in0=pp[:, :], scalar1=0.125, scalar2=None,
                                    op0=ALU.mult),
                                    reads=pp.all(), writes=qt.all())
                            else:
                                s.op("dve", lambda e, pp=pp, kt_=kt_, tb=tb: e.tensor_copy(
                                    out=kt_[:, tb * 512:(tb + 1) * 512], in_=pp[:, :]),
                                    reads=pp.all(), writes=kt_.all())
                            yield
                    for tg in range(4):
                        pp = PP[st["pi"] % len(PP)]
                        st["pi"] += 1
                        for tt in range(4):
                            tok = (tg * 4 + tt) * 128
                            if tt > 0:
                                yield
                            for kc in range(8):
                                s.op("pe", lambda e, kc=kc, pp=pp, w=w, tt=tt, tok=tok: e.matmul(
                                    pp[:, tt * 128:(tt + 1) * 128], lhsT=HT[:, kc, tok:tok + 128],
                                    rhs=w[:, kc, 256:384], start=(kc == 0), stop=(kc == 7)),
                                    reads=w.all() + HT.b(tg), writes=pp.all())
                        s.op("dve", lambda e, pp=pp, v=v, tg=tg: e.tensor_copy(
                            out=v[:, tg * 4:(tg + 1) * 4, :], in_=pp[:, :].rearrange("p (a b) -> p a b", a=4)),
                            reads=pp.all(), writes=v.all())
                        yield

                def post_hp(hp):
                    otok = OTOK[hp % 2]
                    for tg in range(4):
                        pp = PP[st["pi"] % len(PP)]
                        st["pi"] += 1
                        for tt in range(4):
                            s.op("pe", lambda e, pp=pp, tt=tt, tg=tg, otok=otok: e.matmul(
                                pp[:, tt * 128:(tt + 1) * 128], lhsT=otok[:, tg * 4 + tt, :], rhs=C["ident"][:, :],
                                start=True, stop=True),
                                reads=otok.all() + C["ident"].all(), writes=pp.all())
                        s.op("dve", lambda e, pp=pp, tg=tg, hp=hp: e.tensor_copy(
                            out=OT[:, hp, tg * 512:(tg + 1) * 512], in_=pp[:, :]),
                            reads=pp.all(), writes=OT.all())
                        yield

                units = []
                for hp in range(8):
                    for g in range(4):
                        for kt in range(4 * g + 3, -1, -1):
                            for hh in range(2):
                                units.append((hp, hh, g, kt))
                n = len(units)
                NPZ, NSP, NATT, NPO, NE = 5, 5, 3, 2, 3

                def u_(i):
                    hp, hh, g, kt = units[i]
                    d = kt - 4 * g
                    return hp, hh, g, kt, d, slice(hh * 64, (hh + 1) * 64)

                def c0_(i):
                    hp, hh, g, kt = units[i]
                    return max(kt - 4 * g, 0) * 128

                def s0_qk(i):
                    hp, hh, g, kt, d, hs = u_(i)
                    if hh == 0 and g == 0 and kt == 3:
                        if hp == 0:
                            bg.add(pre_hp(0))
                        bg.drain()
                    if hh == 0 and g == 1 and kt == 5 and hp + 1 < 8:
                        bg.add(pre_hp(hp + 1), 2)
                    pz, qt, kt_ = PZ[i % NPZ], QT[hp % 2], KT[hp % 2]
                    q0 = g * 512
                    c0 = c0_(i)
                    s.op("pe", lambda e: e.matmul(pz[:, c0:512], lhsT=kt_[hs, kt * 128:(kt + 1) * 128],
                                                  rhs=qt[hs, q0 + c0:q0 + 512], start=True, stop=(d < 0)),
                         reads=kt_.all() + qt.all(), writes=pz.all())
                    if d >= 0:
                        s.op("pe", lambda e: e.matmul(pz[:, c0:c0 + 128], lhsT=C["ident"][:, :], rhs=NEGM[:, d, c0:c0 + 128],
                                                      start=False, stop=True),
                             reads=C["ident"].all() + NEGM.all(), writes=pz.all())

                def s1_exp(i):
                    pz, e_ = PZ[i % NPZ], E[i % NE]
                    c0 = c0_(i)
                    s.op("act", lambda e: e.activation(out=e_[:, c0:512], in_=pz[:, c0:512], func=AF.Exp),
                         reads=pz.all(), writes=e_.all())

                def s2_ln(i):
                    e_, sp = E[i % NE], SP[i % NSP]
                    c0 = c0_(i)
                    s.op("act", lambda e: e.activation(out=sp[:, c0:512], in_=e_[:, c0:512], func=AF.Ln,
                                                       bias=C["one_f"][:, 0:1]),
                         reads=e_.all() + C["one_f"].all(), writes=sp.all())

                def s3_tri(i):
                    pz, sp = PZ[i % NPZ], SP[i % NSP]
                    c0 = c0_(i)
                    s.op("pe", lambda e: e.matmul(pz[:, c0:512], lhsT=C["ntri"][:, :], rhs=sp[:, c0:512], start=False,
                                                  stop=True, skip_group_check=True),
                         reads=sp.all() + C["ntri"].all(), writes=pz.all())

                def s4_att(i):
                    pz, att = PZ[i % NPZ], ATT[i % NATT]
                    c0 = c0_(i)
                    s.op("act", lambda e: e.activation(out=att[:, c0:512], in_=pz[:, c0:512], func=AF.Exp),
                         reads=pz.all(), writes=att.all())

                def s5_av(i):
                    hp, hh, g, kt, d, hs = u_(i)
                    qlo = max(d, 0)
                    sp, att, po, v = SP[i % NSP], ATT[i % NATT], PO[i % NPO], V[hp % 2]
                    for qi in range(qlo, 4):
                        s.op("pe", lambda e, qi=qi: e.matmul(
                            po[:, qi * 64:(qi + 1) * 64], lhsT=att[:, qi * 128:(qi + 1) * 128],
                            rhs=v[:, kt, hs], start=True, stop=True),
                            reads=att.all() + v.all(), writes=po.all())
                        s.op("pe", lambda e, qi=qi: e.matmul(
                            po[:, 256 + qi:257 + qi], lhsT=sp[:, qi * 128:(qi + 1) * 128],
                            rhs=C["ones_col"][:, 0:1], start=True, stop=True),
                            reads=sp.all() + C["ones_col"].all(), writes=po.all())

                def s6_acc(i):
                    hp, hh, g, kt, d, hs = u_(i)
                    qlo = max(d, 0)
                    span = hh
                    po = PO[i % NPO]
                    oacc, fs2 = OACC[span % 2], FS2[(i // 2) % 4]
                    cb = CACC2.b(hh)
                    otok = OTOK[hp % 2]
                    if kt != 4 * g + 3:
                        if hh == 0:
                            s.op("act", lambda e: e.activation(out=fs2[:, :, :], in_=CACC2[:, :, :], func=AF.Exp, scale=-1.0),
                                 reads=CACC2.all(), writes=fs2.all())
                        s.op("dve", lambda e: e.tensor_tensor(
                            out=CACC2[:, hh, qlo:4], in0=CACC2[:, hh, qlo:4], in1=po[:, 256 + qlo:260], op=ALU.add),
                            reads=po.all() + cb, writes=cb)
                        for qi in range(qlo, 4):
                            s.op("dve", lambda e, qi=qi: e.scalar_tensor_tensor(
                                out=oacc[:, qi, :], in0=po[:, qi * 64:(qi + 1) * 64], scalar=fs2[:, hh, qi:qi + 1],
                                in1=oacc[:, qi, :], op0=ALU.mult, op1=ALU.add),
                                reads=po.all() + fs2.all() + oacc.all(), writes=oacc.all())
                    else:
                        if qlo > 0:
                            s.op("dve", lambda e: e.memset(oacc[:, 0:qlo, :], 0.0), writes=oacc.all())
                            s.op("dve", lambda e: e.memset(CACC2[:, hh, 0:qlo], 0.0), writes=cb)
                        s.op("dve", lambda e: e.tensor_copy(
                            out=oacc[:, qlo:4, :], in_=po[:, qlo * 64:256].rearrange("p (a b) -> p a b", b=64)),
                            reads=po.all(), writes=oacc.all())
                        s.op("dve", lambda e: e.tensor_copy(out=CACC2[:, hh, qlo:4], in_=po[:, 256 + qlo:260]),
                             reads=po.all(), writes=cb)
                    if kt == 0:
                        s.op("dve", lambda e: e.tensor_copy(out=otok[:, 4 * g:4 * g + 4, hs], in_=oacc[:, :, :]),
                             reads=oacc.all(), writes=otok.all())
                        if hh == 1 and g == 3:
                            bg.add(post_hp(hp), 1)

                stages = ((0, s0_qk), (1, s1_exp), (2, s2_ln), (3, s3_tri), (4, s4_att), (5, s5_av), (6, s6_acc))
                for i in range(n + 6):
                    for lag, fn in stages:
                        if 0 <= i - lag < n:
                            fn(i - lag)
                    bg.tick()
                bg.drain()
        if "dbg_OT" in C:
            s.dma("sp", C["dbg_OT"][:, :, :], OT[:, :, :], reads=OT.all(), writes=C["dbg_OT"].all())
        outproj_postnorm(k, C, XT, PS, OT, wo_d, gi_post, next_gi)


AX = mybir.AxisListType


class Ref:
    __slots__ = ("ap", "bufs")

    def __init__(self, ap, bufs):
        self.ap = ap
        self.bufs = bufs


class _RefMaker:
    def __init__(self, t):
        self.t = t

    def __getitem__(self, key):
        return Ref(self.t.t[key], self.t.all())


def rf(t):
    return _RefMaker(t)


def _b(*refs):
    out = []
    for r in refs:
        if isinstance(r, Ref):
            out += r.bufs
    return out


def _a(x):
    return x.ap if isinstance(x, Ref) else x


def e_tt(s, eng, out, a, b, op):
    return s.op(eng, lambda E: E.tensor_tensor(out=out.ap, in0=a.ap, in1=b.ap, op=op), reads=_b(a, b), writes=out.bufs)


def e_ts(s, eng, out, a, s1, s2, op0, op1=None):
    if op1 is None:
        return s.op(eng, lambda E: E.tensor_scalar(out=out.ap, in0=a.ap, scalar1=_a(s1), scalar2=None, op0=op0),
                    reads=_b(a, s1), writes=out.bufs)
    return s.op(eng, lambda E: E.tensor_scalar(out=out.ap, in0=a.ap, scalar1=_a(s1), scalar2=_a(s2), op0=op0, op1=op1),
                reads=_b(a, s1, s2), writes=out.bufs)


def e_stt(s, eng, out, a, sc, b, op0, op1):
    return s.op(eng, lambda E: E.scalar_tensor_tensor(out=out.ap, in0=a.ap, scalar=_a(sc), in1=b.ap, op0=op0, op1=op1),
                reads=_b(a, sc, b), writes=out.bufs)


def e_act(s, out, a, func, bias=None, scale=None):
    kw = {}
    if bias is not None:
        kw["bias"] = _a(bias)
    if scale is not None:
        kw["scale"] = _a(scale)
    return s.op("act", lambda E: E.activation(out=out.ap, in_=a.ap, func=func, **kw), reads=_b(a, bias, scale),
                writes=out.bufs)


def e_mm(s, out, lhsT, rhs, start=True, stop=True):
    return s.op("pe", lambda E: E.matmul(out.ap, lhsT=lhsT.ap, rhs=rhs.ap, start=start, stop=stop),
                reads=_b(lhsT, rhs), writes=out.bufs)


def e_copy(s, eng, out, a):
    if eng == "act":
        return e_act(s, out, a, AF.Copy)
    return s.op(eng, lambda E: E.tensor_copy(out=out.ap, in_=a.ap), reads=_b(a), writes=out.bufs)


def e_memset(s, eng, out, val):
    return s.op(eng, lambda E: E.memset(out.ap, val), writes=out.bufs)


GN_EPS = 64e-5
STAGGER = 3
NEG_EXP_HALF = -0.6065306597126334


def mixer0_stage(k, C, XT, PS, D, gi_pre, gi_post, next_gi=None):
    s = k.s
    with k.phase() as ph:
        OT = ph.sb("OT", [128, 8, SEQ], BF16, parts=1)
        with k.phase() as pab:
            HT = C["HTG"]
            prenorm_to_HT(k, C, pab, XT, HT, PS, gi_pre)
            with k.phase() as pl:
                rglru_part(k, C, pl, HT, OT, PS, D)
            with k.phase() as pr:
                rwkv_part(k, C, pr, HT, OT, PS, D, XT)
        if "dbg_OT" in D:
            s.dma("sp", D["dbg_OT"][:, :, :], OT[:, :, :], reads=OT.all(), writes=D["dbg_OT"].all())
        outproj_postnorm(k, C, XT, PS, OT, D["l0_w_out"], gi_post, next_gi)


def rglru_part(k, C, p, HT, OT, PS, D):
    s = k.s
    PL = p.sb("PL", [128, 4, 8], F32)
    s.dma("sp", PL[:, :, :], D["l0_pl"][:, :].rearrange("p (c n) -> p c n", c=4), writes=PL.all())
    C1 = p.sb("C1", [128, 4], F32)
    e_act(s, rf(C1)[:, :], rf(PL)[:, :, 7], AF.Exp, scale=-1.0)
    e_act(s, rf(C1)[:, :], rf(C1)[:, :], AF.Ln, bias=rf(C["one_f"])[:, 0:1])
    e_ts(s, "dve", rf(C1)[:, :], rf(C1)[:, :], -8.0, None, ALU.mult)
    GAW = p.sb("GAW", [128, 4, 128], BF16)
    GXW = p.sb("GXW", [128, 4, 128], BF16)
    e_memset(s, "dve", rf(GAW)[:, :, :], 0.0)
    e_memset(s, "dve", rf(GXW)[:, :, :], 0.0)
    for n in range(8):
        ps_ = slice((n % 2) * 64, (n % 2) * 64 + 64)
        s.dma("pool", GAW[ps_, n // 2, ps_], D["l0_gate_a_w"][n], writes=GAW.all())
        s.dma("pool", GXW[ps_, n // 2, ps_], D["l0_gate_x_w"][n], writes=GXW.all())
    W = [p.sb(f"WL{i}", [128, 8, 256], BF16) for i in range(2)]
    XBs = [p.sb(f"XB{i}", [128, 515], F32) for i in range(2)]
    HHs = [[p.sb(f"HH{j}_{i}", [128, 512], F32) for i in range(2)] for j in range(2)]
    ts_ = [{n: p.sb(f"{n}{j}", [128, 512], F32) for n in ("GB", "XC", "R", "IG", "A", "U", "T1", "T2")} for j in range(2)]
    XCbs = [p.sb(f"XCb{j}", [128, 512], BF16) for j in range(2)]

    def unit(c, tb, j):
        w = W[j]
        XB, t_, XCb = XBs[j], ts_[j], XCbs[j]
        col = lambda n: rf(PL)[:, c, n:n + 1]
        tok = tb * 512
        px, pg = PS[2 * j], PS[2 * j + 1]
        for kc in range(8):
            e_mm(s, rf(px)[:, :], rf(w)[:, kc, 0:128], Ref(HT[:, kc, tok:tok + 512], HT.b(tb)), kc == 0, kc == 7)
        for kc in range(8):
            e_mm(s, rf(pg)[:, :], rf(w)[:, kc, 128:256], Ref(HT[:, kc, tok:tok + 512], HT.b(tb)), kc == 0, kc == 7)
        if tb == 0:
            e_memset(s, "dve", rf(XB)[:, 0:3], 0.0)
        else:
            e_copy(s, "dve", rf(XB)[:, 0:3], rf(XB)[:, 512:515])
        yield
        e_copy(s, "act", rf(XB)[:, 3:515], rf(px)[:, :])
        e_copy(s, "act", rf(t_["GB"])[:, :], rf(pg)[:, :])
        yield
        XC = t_["XC"]
        e_ts(s, "dve", rf(XC)[:, :], rf(XB)[:, 3:515], col(3), col(4), ALU.mult, ALU.add)
        for i in range(3):
            e_stt(s, "dve", rf(XC)[:, :], rf(XB)[:, i:i + 512], col(i), rf(XC)[:, :], ALU.mult, ALU.add)
        GB, T2 = t_["GB"], t_["T2"]
        e_act(s, rf(T2)[:, :], rf(GB)[:, :], AF.Gelu_apprx_tanh)
        yield
        e_copy(s, "act", rf(XCb)[:, :], rf(XC)[:, :])
        yield
        pr_, pig = PS[4 + 2 * j], PS[5 + 2 * j]
        e_mm(s, rf(pr_)[:, :], rf(GAW)[:, c, :], rf(XCb)[:, :])
        e_mm(s, rf(pig)[:, :], rf(GXW)[:, c, :], rf(XCb)[:, :])
        yield
        e_act(s, rf(t_["R"])[:, :], rf(pr_)[:, :], AF.Sigmoid, bias=col(5))
        e_act(s, rf(t_["IG"])[:, :], rf(pig)[:, :], AF.Sigmoid, bias=col(6))
        yield
        A = t_["A"]
        e_act(s, rf(A)[:, :], rf(t_["R"])[:, :], AF.Exp, scale=rf(C1)[:, c:c + 1])
        T1, U = t_["T1"], t_["U"]
        e_tt(s, "dve", rf(U)[:, :], rf(t_["IG"])[:, :], rf(XC)[:, :], ALU.mult)
        yield
        e_tt(s, "dve", rf(T1)[:, :], rf(A)[:, :], rf(A)[:, :], ALU.mult)
        yield
        e_ts(s, "dve", rf(T1)[:, :], rf(T1)[:, :], -1.0, 1.0, ALU.mult, ALU.add)
        yield
        e_act(s, rf(T1)[:, :], rf(T1)[:, :], AF.Sqrt)
        yield
        e_tt(s, "dve", rf(U)[:, :], rf(U)[:, :], rf(T1)[:, :], ALU.mult)
        yield
        H = HHs[j][tb % 2]
        Hp = HHs[j][(tb + 1) % 2]
        init = 0.0 if tb == 0 else Hp[:, 511:512]
        s.op("dve", lambda E: E.tensor_tensor_scan(
            out=H[:, :], data0=A[:, :], data1=U[:, :], initial=init, op0=ALU.mult, op1=ALU.add),
            reads=A.all() + U.all() + (Hp.all() if tb else []), writes=H.all())
        yield
        s.op("dve", lambda E: E.tensor_tensor(
            out=OT[:, 4 + c, tok:tok + 512], in0=H[:, :], in1=T2[:, :], op=ALU.mult),
            reads=H.all() + T2.all(), writes=OT.all())

    for cp in range(2):
        for j in range(2):
            c = 2 * cp + j
            s.dma("pool", W[j][:, :, :], D["l0_w_lru"][c].rearrange("p (kc n) -> p kc n", kc=8), writes=W[j].all())
        for tb in range(4):
            gens = [unit(2 * cp + j, tb, j) for j in range(2)]
            alive = [True, True]
            while any(alive):
                for j in range(2):
                    if alive[j]:
                        try:
                            next(gens[j])
                        except StopIteration:
                            alive[j] = False


def rwkv_part(k, C, p, HT, OT, PS, D, XT):
    s = k.s
    spill = D["xt_spill"]
    for kc in range(8):
        s.dma("sp", spill[:, kc, :], XT[:, kc, :], reads=XT.all(), writes=spill.all())
    PH = p.sb("PH", [128, 4, 8], F32)
    s.dma("sp", PH[:, :, :], D["l0_ph"][:, :].rearrange("p (h n) -> p h n", h=4), writes=PH.all())
    OM = p.sb("OM", [128, 4, 4], F32)
    e_ts(s, "dve", rf(OM)[:, :, 0:3], rf(PH)[:, :, 0:3], -1.0, 1.0, ALU.mult, ALU.add)
    e_ts(s, "dve", rf(OM)[:, :, 3:4], rf(PH)[:, :, 6:7], -1.0, 1.0, ALU.mult, ALU.add)
    RKb = p.sb("RKb", [128, 4], BF16)
    e_copy(s, "dve", rf(RKb)[:, :], rf(PH)[:, :, 7])
    MUL = p.sb("MUL", [128, 3], F32)
    s.dma("sp", MUL[:, :], D["l0_mul"][:, :], writes=MUL.all())
    OML = p.sb("OML", [128, 3], F32)
    e_ts(s, "dve", rf(OML)[:, :], rf(MUL)[:, :], -1.0, 1.0, ALU.mult, ALU.add)
    LNGBs = [p.sb(f"LNGB{i}", [128, 2, 64], F32) for i in range(2)]
    W2 = p.sb("W2", [64, 512], BF16)
    A2 = p.sb("A2", [64, 512], BF16)
    G2 = p.sb("G2", [128, 512], BF16)
    s.dma("pool", W2[:, :], D["l0_w2"][:, :], writes=W2.all())
    s.dma("pool", A2[:, :], D["l0_a2"][:, :], writes=A2.all())
    s.dma("pool", G2[:, :], D["l0_g2"][:, :], writes=G2.all())
    ob = C["ones_bf"]
    BLK = p.sb("BLK", [128, 128], BF16)
    e_memset(s, "dve", rf(BLK)[:, :], 0.0)
    e_memset(s, "dve", rf(BLK)[0:64, 0:64], 1.0)
    e_memset(s, "dve", rf(BLK)[64:128, 64:128], 1.0)
    M512 = p.sb("M512", [128, 512], BF16)
    MUS = p.sb("MUS", [128, 512], BF16)
    MUI = p.sb("MUI", [128, 512], BF16)
    MLS = p.sb("MLS", [128, 512], BF16)
    ID8 = p.sb("ID8", [128, 512], BF16)
    for hh in range(2):
        hs = slice(hh * 64, hh * 64 + 64)
        for dst, pat, cmp_, cm in ((M512, [[0, 8], [1, 64]], ALU.is_gt, 0), (MUS, [[0, 8], [1, 64]], ALU.is_gt, -1),
                                   (MUI, [[0, 8], [1, 64]], ALU.is_ge, -1), (MLS, [[0, 8], [-1, 64]], ALU.is_gt, 1),
                                   (ID8, [[0, 8], [-1, 64]], ALU.is_equal, 1)):
            s.op("pool", lambda E, dst=dst, pat=pat, cmp_=cmp_, cm=cm, hs=hs: E.affine_select(
                out=dst[hs, :], in_=ob[hs, :], pattern=pat, compare_op=cmp_, fill=0.0, base=0, channel_multiplier=cm),
                reads=ob.all(), writes=dst.all())
    ident = C["ident"]

    TW = p.sb("TW", [64, SEQ], BF16)
    AL = p.sb("AL", [64, SEQ], BF16)
    SGL = p.sb("SGL", [128, SEQ], BF16)
    with k.phase() as p0:
        WLo = p0.sb("WLo", [128, 8, 256], BF16)
        s.dma("pool", WLo[:, :, :], D["l0_w_lora"][:, :].rearrange("p (kc n) -> p kc n", kc=8), writes=WLo.all())
        PAl = [p0.sb(f"PAl{i}", [128, 513], F32) for i in range(3)]
        TMPl = [p0.sb(f"TMPl{i}", [128, 512], F32) for i in range(3)]

        def lora_chain(which, c0, c1, npart, dst):
            PA, tmpl = PAl[which], TMPl[which]
            for tb in range(4):
                tok = tb * 512
                pp = PS[which * 2 + tb % 2]
                for kc in range(8):
                    e_mm(s, rf(pp)[0:npart, :], rf(WLo)[:, kc, c0:c1], Ref(HT[:, kc, tok:tok + 512], HT.b(tb)), kc == 0, kc == 7)
                if tb == 0:
                    e_memset(s, "dve", rf(PA)[0:npart, 0:1], 0.0)
                else:
                    e_copy(s, "dve", rf(PA)[0:npart, 0:1], rf(PA)[0:npart, 512:513])
                yield
                e_copy(s, "act", rf(PA)[0:npart, 1:513], rf(pp)[0:npart, :])
                yield
                e_act(s, rf(tmpl)[0:npart, :], rf(PA)[0:npart, 0:512], AF.Copy, scale=rf(MUL)[0:npart, which:which + 1])
                yield
                e_stt(s, "dve", rf(tmpl)[0:npart, :], rf(PA)[0:npart, 1:513], rf(OML)[0:npart, which:which + 1],
                      rf(tmpl)[0:npart, :], ALU.mult, ALU.add)
                yield
                if which == 0:
                    e_act(s, rf(dst)[:, tok:tok + 512], rf(tmpl)[0:64, :], AF.Tanh)
                elif which == 1:
                    e_copy(s, "act", rf(dst)[:, tok:tok + 512], rf(tmpl)[0:64, :])
                else:
                    e_act(s, rf(dst)[:, tok:tok + 512], rf(tmpl)[:, :], AF.Sigmoid)
                yield

        lbg = Bg()
        for which, (c0, c1, npart, dst) in enumerate(((0, 64, 64, TW), (64, 128, 64, AL), (128, 256, 128, SGL))):
            lbg.add(lora_chain(which, c0, c1, npart, dst), 1)
        lbg.drain()

    s.barrier()
    XTf = XT.t
    XTb = XT.t.bitcast(BF16)
    f32n = ("r", "k", "SIG", "A", "KKN", "KH", "CUM", "EC", "EX", "EN", "TMP")
    b16n = ("Rt", "At", "Bt", "Kt", "Bh", "Kh", "RK", "VT", "KK2")
    t64n = ("V64", "BH64", "KH64", "N", "Q", "N2", "Q2", "XA", "LAK", "ARB", "ARK")
    sets = []
    for S in range(2):
        B = {}
        if S == 0:
            B["WH"] = p.sb("WH", [128, 8, 384], BF16)
            B["PA"] = [p.sb(f"PA{i}", [128, 513], F32) for i in range(3)]
            F = {n: p.sb("f_" + n, [128, 512], F32) for n in f32n}
            Bf = {n: p.sb("b_" + n, [128, 512], BF16) for n in b16n}
            T64 = {n: p.sb("t_" + n, [128, 512], BF16) for n in t64n}
        else:
            B["WH"] = T("WH1", XTb[:, 7, 0:3072].rearrange("p (kc n) -> p kc n", kc=8))
            B["PA"] = [T(f"PA1_{i}", XTf[:, 3, i * 513:(i + 1) * 513]) for i in range(3)]
            F = {n: T("f1_" + n, XTf[:, i // 4, (i % 4) * 512:(i % 4) * 512 + 512]) for i, n in enumerate(f32n)}
            bl = list(b16n) + list(t64n)
            vb = {n: T("b1_" + n, XTb[:, 4 + i // 8, (i % 8) * 512:(i % 8) * 512 + 512]) for i, n in enumerate(bl)}
            Bf = {n: vb[n] for n in b16n}
            T64 = {n: vb[n] for n in t64n}
        F["EH"] = F["TMP"]
        F["BA"] = F["SIG"]
        T64["YA"] = T64["LAK"]
        B["F"], B["Bf"], B["T64"] = F, Bf, T64
        B["R0"], B["YF"], B["YQ"], B["GT"] = F["SIG"], F["EX"], F["EN"], F["KH"]
        B["ST"] = p.sb(f"ST{S}", [128, 8, 4], F32)
        B["RKS"] = p.sb(f"RKS{S}", [128, 8], F32)
        B["Pf"] = p.sb(f"Pf{S}", [128, 64], F32)
        B["Pb"] = p.sb(f"Pb{S}", [128, 64], BF16)
        B["RR"] = p.sb(f"RR{S}", [128, 64], BF16)
        B["UB"] = p.sb(f"UB{S}", [128, 64], BF16)
        B["LNGB"] = LNGBs[S]
        B["PY"] = PS[4 + S]
        B["PT1"] = PS[6 + S]
        sets.append(B)
    b3 = lambda r_: Ref(r_.ap.rearrange("p (a b) -> p a b", a=8), r_.bufs)
    HS = (slice(0, 64), slice(64, 128))
    st = {"sci": 0}

    def newps():
        st["sci"] += 1
        return PS[st["sci"] % 4]

    def mm2(out_t, col0, ncol, lhs_fn, rhs_fn, start=True, stop=True):
        for hs in HS:
            e_mm(s, rf(out_t)[hs, col0:col0 + ncol], lhs_fn(hs), rhs_fn(hs), start, stop)

    def unit(hp, gq, B):
        F, Bf, T64, PA, w = B["F"], B["Bf"], B["T64"], B["PA"], B["WH"]
        R0, YF, YQ, GT, ST, RKS = B["R0"], B["YF"], B["YQ"], B["GT"], B["ST"], B["RKS"]
        Pf, Pb, RR, UB, LNGB = B["Pf"], B["Pb"], B["RR"], B["UB"], B["LNGB"]
        hc = lambda n: rf(PH)[:, hp, n:n + 1]
        tok = gq * 512
        for which, nm in enumerate(("r", "k", "v")):
            pp = newps()
            for kc in range(8):
                e_mm(s, rf(pp)[:, :], rf(w)[:, kc, which * 128:(which + 1) * 128],
                     Ref(HT[:, kc, tok:tok + 512], HT.b(gq)), kc == 0, kc == 7)
            pa = PA[which]
            if gq == 0:
                e_memset(s, "dve", rf(pa)[:, 0:1], 0.0)
            else:
                e_copy(s, "dve", rf(pa)[:, 0:1], rf(pa)[:, 512:513])
            e_copy(s, "act", rf(pa)[:, 1:513], rf(pp)[:, :])
            yield
            tmp_ = rf(F["TMP"])[:, :] if which != 1 else rf(F["CUM"])[:, :]
            e_act(s, tmp_, rf(pa)[:, 0:512], AF.Copy, scale=hc(which))
            dst_ = rf(Bf["VT"])[:, :] if nm == "v" else rf(F[nm])[:, :]
            e_stt(s, "dve", dst_, rf(pa)[:, 1:513], rf(OM)[:, hp, which:which + 1], tmp_, ALU.mult, ALU.add)
        r_, k_ = rf(F["r"])[:, :], rf(F["k"])[:, :]
        pz = newps()
        e_mm(s, rf(pz)[:, :], rf(W2)[:, hp * 128:(hp + 1) * 128], rf(TW)[:, tok:tok + 512])
        e_act(s, rf(F["SIG"])[:, :], rf(pz)[:, :], AF.Sigmoid, bias=hc(3))
        pz2 = newps()
        e_mm(s, rf(pz2)[:, :], rf(A2)[:, hp * 128:(hp + 1) * 128], rf(AL)[:, tok:tok + 512])
        e_act(s, rf(F["A"])[:, :], rf(pz2)[:, :], AF.Sigmoid, bias=hc(4))
        yield
        e_ts(s, "dve", rf(F["KKN"])[:, :], k_, hc(5), None, ALU.mult)
        e_act(s, rf(Bf["KK2"])[:, :], rf(F["KKN"])[:, :], AF.Square)
        yield
        pz = newps()
        e_mm(s, rf(pz)[:, :], rf(BLK)[:, :], rf(Bf["KK2"])[:, :])
        e_act(s, rf(F["TMP"])[:, :], rf(pz)[:, :], AF.Sqrt)
        yield
        e_ts(s, "dve", rf(F["TMP"])[:, :], rf(F["TMP"])[:, :], 1e-12, None, ALU.max)
        s.op("dve", lambda E: E.reciprocal(out=F["TMP"][:, :], in_=F["TMP"][:, :]), reads=F["TMP"].all(),
             writes=F["TMP"].all())
        e_tt(s, "dve", rf(F["KKN"])[:, :], rf(F["KKN"])[:, :], rf(F["TMP"])[:, :], ALU.mult)
        e_act(s, rf(F["KH"])[:, :], rf(F["A"])[:, :], AF.Identity, bias=rf(OM)[:, hp, 3:4], scale=hc(6))
        e_tt(s, "dve", rf(F["KH"])[:, :], rf(F["KH"])[:, :], k_, ALU.mult)
        yield
        s.op("dve", lambda E: E.tensor_tensor_scan(out=F["CUM"][:, :], data0=M512[:, :], data1=F["SIG"][:, :],
                                                   initial=0.0, op0=ALU.mult, op1=ALU.add),
             reads=M512.all() + F["SIG"].all(), writes=F["CUM"].all())
        yield
        cum = rf(F["CUM"])[:, :]
        e_act(s, rf(F["EC"])[:, :], cum, AF.Exp, scale=NEG_EXP_HALF)
        e_tt(s, "dve", rf(F["EX"])[:, :], cum, rf(F["SIG"])[:, :], ALU.subtract)
        e_act(s, rf(F["EN"])[:, :], cum, AF.Exp, scale=-NEG_EXP_HALF)
        cum3 = b3(cum)
        cend = Ref(cum3.ap[:, :, 63:64].to_broadcast([128, 8, 64]), cum.bufs)
        e_tt(s, "dve", b3(rf(F["EH"])[:, :]), cend, cum3, ALU.subtract)
        yield
        e_act(s, rf(F["EX"])[:, :], rf(F["EX"])[:, :], AF.Exp, scale=NEG_EXP_HALF)
        e_act(s, rf(F["EH"])[:, :], rf(F["EH"])[:, :], AF.Exp, scale=NEG_EXP_HALF)
        e_tt(s, "dve", rf(Bf["Rt"])[:, :], r_, rf(F["EC"])[:, :], ALU.mult)
        e_tt(s, "dve", rf(Bf["RK"])[:, :], r_, rf(F["KH"])[:, :], ALU.mult)
        e_tt(s, "dve", rf(F["BA"])[:, :], rf(F["KKN"])[:, :], rf(F["A"])[:, :], ALU.mult)
        yield
        e_stt(s, "dve", rf(Bf["At"])[:, :], rf(F["KKN"])[:, :], -1.0, rf(F["EX"])[:, :], ALU.mult, ALU.mult)
        e_tt(s, "dve", rf(Bf["Bt"])[:, :], rf(F["BA"])[:, :], rf(F["EN"])[:, :], ALU.mult)
        e_tt(s, "dve", rf(Bf["Bh"])[:, :], rf(F["BA"])[:, :], rf(F["EH"])[:, :], ALU.mult)
        e_tt(s, "dve", rf(Bf["Kt"])[:, :], rf(F["KH"])[:, :], rf(F["EN"])[:, :], ALU.mult)
        e_tt(s, "dve", rf(Bf["Kh"])[:, :], rf(F["KH"])[:, :], rf(F["EH"])[:, :], ALU.mult)
        yield
        blk = lambda n, c8, hs: rf(Bf[n])[hs, c8 * 64:(c8 + 1) * 64]
        tb_ = lambda n, c8, hs: rf(T64[n])[hs, c8 * 64:(c8 + 1) * 64]
        idh = lambda hs: rf(ident)[hs, hs]
        for src, dst in (("VT", "V64"), ("Bh", "BH64"), ("Kh", "KH64")):
            pt = newps()
            for c8 in range(8):
                mm2(pt, c8 * 64, 64, lambda hs, c8=c8, src=src: blk(src, c8, hs), idh)
            e_copy(s, "act", rf(T64[dst])[:, :], rf(pt)[:, :])
            yield
        for lh, rh, mask, dst in (("Bt", "At", MUS, "N"), ("At", "Bt", MLS, "Q"), ("Kt", "At", MUS, "LAK"),
                                  ("Bt", "Rt", MUI, "ARB"), ("Kt", "Rt", MUI, "ARK")):
            pt = newps()
            for c8 in range(8):
                mm2(pt, c8 * 64, 64, lambda hs, c8=c8, lh=lh: blk(lh, c8, hs), lambda hs, c8=c8, rh=rh: blk(rh, c8, hs))
            e_tt(s, "dve", rf(T64[dst])[:, :], rf(pt)[:, :], rf(mask)[:, :], ALU.mult)
            yield
        e_tt(s, "dve", rf(T64["XA"])[:, :], rf(T64["N"])[:, :], rf(ID8)[:, :], ALU.add)
        Pn, Qn, Pn2, Qn2 = "N", "Q", "N2", "Q2"
        for lvl in range(1, 6):
            pq = newps()
            for c8 in range(8):
                mm2(pq, c8 * 64, 64, lambda hs, c8=c8, Pn=Pn: tb_(Pn, c8, hs), lambda hs, c8=c8, Qn=Qn: tb_(Qn, c8, hs))
            e_copy(s, "act", rf(T64[Qn2])[:, :], rf(pq)[:, :])
            if lvl < 5:
                pp_ = newps()
                for c8 in range(8):
                    mm2(pp_, c8 * 64, 64, lambda hs, c8=c8, Qn=Qn: tb_(Qn, c8, hs),
                        lambda hs, c8=c8, Pn=Pn: tb_(Pn, c8, hs))
                e_copy(s, "act", rf(T64[Pn2])[:, :], rf(pp_)[:, :])
            yield
            px = newps()
            for c8 in range(8):
                mm2(px, c8 * 64, 64, lambda hs, c8=c8, Qn2=Qn2: tb_(Qn2, c8, hs), lambda hs, c8=c8: tb_("XA", c8, hs))
            e_tt(s, "dve", rf(T64["XA"])[:, :], rf(T64["XA"])[:, :], rf(px)[:, :], ALU.add)
            yield
            Pn, Pn2 = Pn2, Pn
            Qn, Qn2 = Qn2, Qn
        pr0 = newps()
        for c8 in range(8):
            mm2(pr0, c8 * 64, 64, lambda hs, c8=c8: tb_("LAK", c8, hs), lambda hs, c8=c8: tb_("V64", c8, hs))
        e_copy(s, "act", rf(R0)[:, :], rf(pr0)[:, :])
        pg = newps()
        e_mm(s, rf(pg)[:, :], rf(G2)[:, hp * 128:(hp + 1) * 128], rf(SGL)[:, tok:tok + 512])
        e_copy(s, "act", rf(GT)[:, :], rf(pg)[:, :])
        yield
        PY, PT1 = B["PY"], B["PT1"]
        pbh = lambda hs: rf(Pb)[hs, :]
        ubh = lambda hs: rf(UB)[hs, :]
        for c8 in range(8):
            mm2(PT1, 0, 64, lambda hs: blk("At", c8, hs), pbh)
            mm2(PY, c8 * 64, 64, lambda hs: blk("Rt", c8, hs), pbh, True, False)
            e_tt(s, "dve", rf(RR)[:, :], rf(PT1)[:, 0:64], rf(R0)[:, c8 * 64:(c8 + 1) * 64], ALU.add)
            yield
            mm2(PT1, 64, 64, lambda hs: tb_("XA", c8, hs), lambda hs: rf(RR)[hs, :])
            e_copy(s, "act", rf(UB)[:, :], rf(PT1)[:, 64:128])
            yield
            mm2(PT1, 128, 64, lambda hs: tb_("KH64", c8, hs), lambda hs: tb_("V64", c8, hs), True, False)
            mm2(PT1, 128, 64, lambda hs: tb_("BH64", c8, hs), ubh, False, True)
            mm2(PY, c8 * 64, 64, lambda hs: tb_("ARB", c8, hs), ubh, False, False)
            mm2(PY, c8 * 64, 64, lambda hs: tb_("ARK", c8, hs), lambda hs: tb_("V64", c8, hs), False, True)
            e_stt(s, "dve", rf(Pf)[:, :], rf(Pf)[:, :], rf(F["EC"])[:, c8 * 64 + 63:c8 * 64 + 64], rf(PT1)[:, 128:192],
                  ALU.mult, ALU.add)
            e_copy(s, "act", rf(Pb)[:, :], rf(Pf)[:, :])
            yield
        e_copy(s, "act", rf(YF)[:, :], rf(PY)[:, :])
        yf3 = b3(rf(YF)[:, :])
        yq3 = b3(rf(YQ)[:, :])
        prk = newps()
        for c8 in range(8):
            mm2(prk, c8, 1, lambda hs: blk("RK", c8, hs), lambda hs: rf(RKb)[hs, hp:hp + 1])
        e_copy(s, "act", rf(RKS)[:, :], rf(prk)[:, 0:8])
        yield
        s.op("dve", lambda E: E.tensor_reduce(out=ST[:, :, 0], in_=YF[:, :].rearrange("p (a b) -> p a b", a=8),
                                              axis=AX.X, op=ALU.add), reads=YF.all(), writes=ST.all())
        e_act(s, rf(YQ)[:, :], rf(YF)[:, :], AF.Square)
        yield
        s.op("dve", lambda E: E.tensor_reduce(out=ST[:, :, 1], in_=YQ[:, :].rearrange("p (a b) -> p a b", a=8),
                                              axis=AX.X, op=ALU.add), reads=YQ.all(), writes=ST.all())
        e_ts(s, "dve", rf(ST)[:, :, 2], rf(ST)[:, :, 0], 1.0 / 64, None, ALU.mult)
        e_tt(s, "dve", rf(ST)[:, :, 0], rf(ST)[:, :, 2], rf(ST)[:, :, 2], ALU.mult)
        e_stt(s, "dve", rf(ST)[:, :, 1], rf(ST)[:, :, 1], 1.0 / 64, rf(ST)[:, :, 0], ALU.mult, ALU.subtract)
        e_ts(s, "dve", rf(ST)[:, :, 1], rf(ST)[:, :, 1], GN_EPS, None, ALU.add)
        e_act(s, rf(ST)[:, :, 3], rf(ST)[:, :, 1], AF.Sqrt)
        yield
        s.op("dve", lambda E: E.reciprocal(out=ST[:, :, 3], in_=ST[:, :, 3]), reads=ST.all(), writes=ST.all())
        mean_b = Ref(ST[:, :, 2:3].to_broadcast([128, 8, 64]), ST.all())
        rstd_b = Ref(ST[:, :, 3:4].to_broadcast([128, 8, 64]), ST.all())
        e_tt(s, "dve", yf3, yf3, mean_b, ALU.subtract)
        e_tt(s, "dve", yf3, yf3, rstd_b, ALU.mult)
        rks_b = Ref(RKS[:, :].rearrange("p (a b) -> p a b", b=1).to_broadcast([128, 8, 64]), RKS.all())
        e_tt(s, "dve", yq3, b3(rf(T64["V64"])[:, :]), rks_b, ALU.mult)
        lng = Ref(LNGB[:, 0:1, :].to_broadcast([128, 8, 64]), LNGB.all())
        lnb = Ref(LNGB[:, 1:2, :].to_broadcast([128, 8, 64]), LNGB.all())
        yield
        e_tt(s, "dve", yf3, yf3, lng, ALU.mult)
        e_tt(s, "dve", yf3, yf3, lnb, ALU.add)
        yield
        e_tt(s, "dve", rf(T64["YA"])[:, :], rf(YF)[:, :], rf(YQ)[:, :], ALU.add)
        yield
        pt = newps()
        for c8 in range(8):
            mm2(pt, c8 * 64, 64, lambda hs: tb_("YA", c8, hs), idh)
        s.op("dve", lambda E: E.tensor_tensor(
            out=OT[:, hp, tok:tok + 512], in0=pt[:, :], in1=GT[:, :], op=ALU.mult),
            reads=pt.all() + GT.all(), writes=OT.all())
        yield

    def stream(S, hps):
        B = sets[S]
        for hp in hps:
            w = B["WH"]
            s.dma("pool", w[:, :, :], D["l0_w_hp"][hp].rearrange("p (kc n) -> p kc n", kc=8), writes=w.all())
            LNGB = B["LNGB"]
            for hh in range(2):
                h = 2 * hp + hh
                s.dma("sp", LNGB[HS[hh], 0, :], D["l0_lnx_g"][h * 64:(h + 1) * 64].partition_broadcast(64),
                      writes=LNGB.all())
                s.dma("sp", LNGB[HS[hh], 1, :], D["l0_lnx_b"][h * 64:(h + 1) * 64].partition_broadcast(64),
                      writes=LNGB.all())
            e_memset(s, "dve", rf(B["Pf"])[:, :], 0.0)
            e_memset(s, "dve", rf(B["Pb"])[:, :], 0.0)
            yield
            for gq in range(4):
                yield from unit(hp, gq, B)

    gens = [stream(0, (0, 2)), stream(1, (1, 3))]
    alive = [True, True]
    first = True
    while any(alive):
        for S in range(2):
            if alive[S]:
                try:
                    next(gens[S])
                except StopIteration:
                    alive[S] = False
            if first and S == 0:
                for _ in range(STAGGER):
                    next(gens[0])
                first = False
    s.barrier()
    for kc in range(8):
        s.dma("sp", XT[:, kc, :], spill[:, kc, :], reads=spill.all(), writes=XT.all())


GAIN_NAMES = ["l0_ffn1_pre_g", "l0_ffn1_post_g", "l0_mix_pre_g", "l0_mix_post_g", "l0_ffn2_pre_g", "l0_ffn2_post_g",
              "l1_ffn1_pre_g", "l1_ffn1_post_g", "l1_mix_pre_g", "l1_mix_post_g", "l1_ffn2_pre_g", "l1_ffn2_post_g"]
HALF_GAINS = [1, 5, 7, 11]


def build_program(stages=("f01", "m0", "f02f11", "m1", "f12"), dbg=False):
    k = KB()
    nc = k.nc
    s = k.s
    xT_d = k.dram_in("xT", [DM, SEQ])
    gains_d = k.dram_in("gains", [128, 12 * 8])
    ffn_d = {}
    for nm in ("l0_ffn1", "l0_ffn2", "l1_ffn1", "l1_ffn2"):
        ffn_d[nm] = (k.dram_in(nm + "_w_in", [NJ, 128, 2048]), k.dram_in(nm + "_w_out", [2, 8, 128, 11 * 128]))
    D = {}
    for nm, shp in (("l0_pl", [128, 32]), ("l0_ph", [128, 32]), ("l0_mul", [128, 3]), ("l0_lnx_g", [512]),
                    ("l0_lnx_b", [512]), ("l0_w2", [64, 512]), ("l0_a2", [64, 512]), ("l0_g2", [128, 512]),
                    ("l0_gate_a_w", [8, 64, 64]), ("l0_gate_x_w", [8, 64, 64]), ("l0_w_lru", [4, 128, 8 * 256]),
                    ("l0_w_lora", [128, 8 * 256]), ("l0_w_hp", [4, 128, 8 * 384]), ("l0_w_out", [DM, DM])):
        D[nm] = k.dram_in(nm, shp)
    D["xt_spill"] = T("xt_spill", nc.dram_tensor("xt_spill", [128, 8, SEQ], F32, kind="Internal").ap())
    if dbg:
        D["dbg_OT"] = k.dram_out("dbg_OT", [128, 8, SEQ], BF16)
    wqkv_d = k.dram_in("l1_w_qkv", [8, 128, 8 * 384])
    l1_wo_d = k.dram_in("l1_w_out", [DM, DM])
    outT_d = k.dram_out("outT", [DM, SEQ])

    with k.es:
        XT = k.sb("XT", [128, 8, SEQ], F32, parts=4)
        C = {}
        C["gains"] = k.sb("gains", [128, 12, 8], F32)
        C["ones_m"] = k.sb("ones_m", [128, 128], BF16)
        PS = [k.ps(f"ps{i}", [128, 512]) for i in range(8)]
        C["HTG"] = k.sb("HTG", [128, 8, SEQ], BF16, parts=4)
        C["ht_ready"] = None

        s.op("dve", lambda e: e.memset(C["ones_m"][:, :], 1.0 / DM), writes=C["ones_m"].all())
        C["one_f"] = k.sb("one_f", [128, 1], F32)
        C["ones_col"] = k.sb("ones_col", [128, 1], BF16)
        C["ones_bf"] = k.sb("ones_bf", [128, 512], BF16)
        C["ident"] = k.sb("ident", [128, 128], BF16)
        C["ntri"] = k.sb("ntri", [128, 128], BF16)
        s.op("dve", lambda e: e.memset(C["one_f"][:, :], 1.0), writes=C["one_f"].all())
        s.op("dve", lambda e: e.memset(C["ones_col"][:, :], 1.0), writes=C["ones_col"].all())
        s.op("dve", lambda e: e.memset(C["ones_bf"][:, :], 1.0), writes=C["ones_bf"].all())
        s.op("pool", lambda e: e.affine_select(out=C["ident"][:, :], in_=C["ones_bf"][:, 0:128], pattern=[[-1, 128]],
                                               compare_op=ALU.is_equal, fill=0.0, base=0, channel_multiplier=1),
             reads=C["ones_bf"].all(), writes=C["ident"].all())
        s.op("pool", lambda e: e.affine_select(out=C["ntri"][:, :], in_=C["ones_bf"][:, 0:128], pattern=[[-1, 128]],
                                               compare_op=ALU.is_ge, fill=0.0, base=0, channel_multiplier=1),
             reads=C["ones_bf"].all(), writes=C["ntri"].all())
        s.op("dve", lambda e: e.tensor_scalar(out=C["ntri"][:, :], in0=C["ntri"][:, :], scalar1=-1.0, scalar2=None,
                                              op0=ALU.mult),
             reads=C["ntri"].all(), writes=C["ntri"].all())
        C["eps"] = k.sb("eps", [128, 1], F32)
        s.op("dve", lambda e: e.memset(C["eps"][:, :], NORM_EPS), writes=C["eps"].all())
        s.dma("sp", C["gains"][:, :, :], gains_d[:, :].rearrange("p (n c) -> p n c", n=12), writes=C["gains"].all())
        for gi in HALF_GAINS:
            s.op("dve", lambda e, gi=gi: e.tensor_scalar(out=C["gains"][:, gi, :], in0=C["gains"][:, gi, :],
                                                         scalar1=0.5, scalar2=None, op0=ALU.mult),
                 reads=C["gains"].all(), writes=C["gains"].all())
        for tb in range(4):
            for kc in range(8):
                s.dma("sp", XT[:, kc, tb * 512:(tb + 1) * 512], xT_d[kc * 128:(kc + 1) * 128, tb * 512:(tb + 1) * 512],
                      writes=XT.b(tb))

        PRE_GI = {"f01": 0, "m0": 2, "f02": 4, "f02f11": 4, "f11": 6, "m1": 8, "f12": 10}
        for si, st in enumerate(stages):
            nxt = PRE_GI[stages[si + 1]] if si + 1 < len(stages) else None
            if st == "f01":
                ffn_stage(k, C, XT, PS, [(*ffn_d["l0_ffn1"], 0, 1)], nxt)
            elif st == "f02":
                ffn_stage(k, C, XT, PS, [(*ffn_d["l0_ffn2"], 4, 5)], nxt)
            elif st == "f02f11":
                ffn_stage(k, C, XT, PS, [(*ffn_d["l0_ffn2"], 4, 5), (*ffn_d["l1_ffn1"], 6, 7)], nxt)
            elif st == "f11":
                ffn_stage(k, C, XT, PS, [(*ffn_d["l1_ffn1"], 6, 7)], nxt)
            elif st == "m0":
                mixer0_stage(k, C, XT, PS, D, 2, 3, nxt)
            elif st == "m1":
                if dbg:
                    C["dbg_OT"] = D["dbg_OT"]
                attn_stage(k, C, XT, PS, wqkv_d, l1_wo_d, 8, 9, nxt)
            elif st == "f12":
                ffn_stage(k, C, XT, PS, [(*ffn_d["l1_ffn2"], 10, 11)], nxt)

        for tb in range(4):
            for kc in range(8):
                s.dma("sp", outT_d[kc * 128:(kc + 1) * 128, tb * 512:(tb + 1) * 512], XT[:, kc, tb * 512:(tb + 1) * 512],
                      reads=XT.b(tb), writes=outT_d.all())
        s.barrier(engines=["sp"])
    return nc


def _col(v):
    return np.ascontiguousarray(np.asarray(v, np.float32).reshape(8, 128).T)


def prep_shared(inp):
    d = {}
    d["gains"] = np.ascontiguousarray(np.concatenate([_col(inp[n]) for n in GAIN_NAMES], axis=1))
    for nm in ("l0_ffn1", "l0_ffn2", "l1_ffn1", "l1_ffn2"):
        w_in = np.asarray(inp[nm + "_w_in"], np.float32)
        w_out = np.asarray(inp[nm + "_w_out"], np.float32)
        g = w_in[:, :DFF].reshape(8, 128, NJ, 128)
        u = w_in[:, DFF:].reshape(8, 128, NJ, 128)
        gu = np.concatenate([g, u], axis=3)
        d[nm + "_w_in"] = np.ascontiguousarray(gu.transpose(2, 1, 0, 3).reshape(NJ, 128, 2048))
        wo = w_out.reshape(2, 11, 128, 8, 128)
        d[nm + "_w_out"] = np.ascontiguousarray(wo.transpose(0, 3, 2, 1, 4).reshape(2, 8, 128, 11 * 128))
    f = lambda n: np.asarray(inp[n], np.float32)
    cw = f("l0_conv_w")
    pl = np.stack([cw[0], cw[1], cw[2], cw[3], f("l0_conv_b"), f("l0_gate_a_b"), f("l0_gate_x_b"), f("l0_lambda")], axis=1)
    d["l0_pl"] = np.ascontiguousarray(pl.reshape(4, 128, 8).transpose(1, 0, 2).reshape(128, 32))
    mu = f("l0_mu")
    ph = np.stack([mu[0:512], mu[512:1024], mu[1024:1536], f("l0_w0"), f("l0_a0"), f("l0_k_k"), f("l0_k_a"),
                   f("l0_r_k").reshape(512)], axis=1)
    d["l0_ph"] = np.ascontiguousarray(ph.reshape(4, 128, 8).transpose(1, 0, 2).reshape(128, 32))
    mul = np.zeros((128, 3), np.float32)
    mul[0:64, 0] = mu[1536:1600]
    mul[0:64, 1] = mu[1600:1664]
    mul[:, 2] = mu[1664:1792]
    d["l0_mul"] = mul
    for n in ("l0_lnx_g", "l0_lnx_b", "l0_w2", "l0_a2", "l0_g2", "l0_gate_a_w", "l0_gate_x_w", "l0_w_out"):
        d[n] = np.ascontiguousarray(f(n))
    wi = f("l0_w_in").reshape(8, 128, 2816)
    lru = np.concatenate([wi[:, :, 1792:2304].reshape(8, 128, 4, 128), wi[:, :, 2304:2816].reshape(8, 128, 4, 128)], axis=3)
    d["l0_w_lru"] = np.ascontiguousarray(lru.transpose(2, 1, 0, 3).reshape(4, 128, 8 * 256))
    d["l0_w_lora"] = np.ascontiguousarray(wi[:, :, 1536:1792].transpose(1, 0, 2).reshape(128, 8 * 256))
    hd = np.stack([wi[:, :, 0:512].reshape(8, 128, 4, 128), wi[:, :, 512:1024].reshape(8, 128, 4, 128),
                   wi[:, :, 1024:1536].reshape(8, 128, 4, 128)], axis=3)
    d["l0_w_hp"] = np.ascontiguousarray(hd.transpose(2, 1, 0, 3, 4).reshape(4, 128, 8 * 384))
    wq = np.asarray(inp["l1_w_qkv"], np.float32).reshape(8, 128, 3, 8, 128)
    d["l1_w_qkv"] = np.ascontiguousarray(wq.transpose(3, 1, 0, 2, 4).reshape(8, 128, 8 * 384))
    d["l1_w_out"] = np.ascontiguousarray(np.asarray(inp["l1_w_out"], np.float32))
    return d


_CACHE = {}


def kernel(**inputs):
    x = np.asarray(inputs["x"], np.float32)
    shared = prep_shared(inputs)
    if "nc" not in _CACHE:
        _CACHE["nc"] = build_program()
    nc = _CACHE["nc"]
    in_maps = []
    for c in range(N_CORES):
        m = dict(shared)
        m["xT"] = np.ascontiguousarray(x[c].T)
        in_maps.append(m)
    res = run_bass_kernel_spmd(nc, in_maps, core_ids=list(range(N_CORES)))
    out = np.stack([np.ascontiguousarray(res.results[c]["outT"].T) for c in range(N_CORES)], axis=0)
    return out.astype(np.float32)
```

```python
import math
from contextlib import ExitStack

import numpy as np
import concourse.bass as bass
import concourse.mybir as mybir
from concourse.bass_utils import run_bass_kernel_spmd

F32 = mybir.dt.float32
BF16 = mybir.dt.bfloat16
AF = mybir.ActivationFunctionType
ALU = mybir.AluOpType

SEQ = 2048
DM = 1024
DFF = 2816
NJ = 22
NORM_EPS = 1e-6
N_CORES = 8


class Buf:
    __slots__ = ("name", "w", "r")

    def __init__(self, name):
        self.name = name
        self.w = None
        self.r = {}


class T:
    def __init__(self, name, t, parts=1):
        self.name = name
        self.t = t
        self.bufs = [Buf(f"{name}.{i}") for i in range(parts)]

    def b(self, *idx):
        return [self.bufs[i] for i in idx]

    def all(self):
        return list(self.bufs)

    def __getitem__(self, key):
        return self.t[key]


class Sched:
    COMPUTE = ("pe", "act", "dve", "pool")

    def __init__(self, nc, es, n_dma_ch=20):
        self.nc = nc
        self.eng = {"pe": nc.tensor, "act": nc.scalar, "dve": nc.vector, "pool": nc.gpsimd, "sp": nc.sync}
        self.sems = {}
        self.cnt = {}
        for e in self.COMPUTE:
            self.sems[e] = es.enter_context(nc.semaphore(f"s_{e}"))
            self.cnt[e] = 0
        self.ch = {}
        self.ch_next = {}
        for q in ("sp", "pool", "act"):
            n = n_dma_ch if q != "act" else 4
            lst = []
            for i in range(n):
                key = f"d_{q}{i}"
                self.sems[key] = es.enter_context(nc.semaphore(key))
                self.cnt[key] = 0
                lst.append(key)
            self.ch[q] = lst
            self.ch_next[q] = 0
        self.seen = {e: {} for e in self.eng}
        self.n_wait = 0
        self.n_ins = 0

    def _wait(self, e, ev):
        key, val = ev
        if val <= 0:
            return
        if self.seen[e].get(key, 0) >= val:
            return
        self.seen[e][key] = val
        self.eng[e].wait_ge(self.sems[key], val)
        self.n_wait += 1

    def _deps(self, e, reads, writes):
        evs = {}

        def need(ev):
            if ev is None:
                return
            k_, v_ = ev
            if e == "pe" and k_ == "pe":
                return
            if evs.get(k_, 0) < v_:
                evs[k_] = v_

        for b in reads:
            need(b.w)
        for b in writes:
            need(b.w)
            for kv in b.r.items():
                need(kv)
        return evs

    def op(self, e, fn, reads=(), writes=()):
        evs = self._deps(e, reads, writes)
        for ev in evs.items():
            self._wait(e, ev)
        ins = fn(self.eng[e])
        self.cnt[e] += 1
        ev = (e, self.cnt[e])
        ins.then_inc(self.sems[e], 1)
        self.seen[e][e] = max(self.seen[e].get(e, 0), 0)
        for b in writes:
            b.w = ev
            b.r = {}
        for b in reads:
            if b.w is not ev:
                b.r[e] = self.cnt[e]
        self.n_ins += 1
        return ins

    def dma(self, q, out, in_, reads=(), writes=()):
        e = q
        evs = self._deps(e, reads, writes)
        key = self.ch[q][self.ch_next[q]]
        self.ch_next[q] = (self.ch_next[q] + 1) % len(self.ch[q])
        if evs.get(key, 0) < self.cnt[key]:
            evs[key] = self.cnt[key]
        for ev in evs.items():
            self._wait(e, ev)
        ins = self.eng[e].dma_start(out=out, in_=in_)
        self.cnt[key] += 16
        ins.then_inc(self.sems[key], 16)
        ev = (key, self.cnt[key])
        for b in writes:
            b.w = ev
            b.r = {}
        for b in reads:
            b.r[key] = self.cnt[key]
        self.n_ins += 1
        return ins

    def barrier(self, engines=None):
        evs = [(k_, v_) for k_, v_ in self.cnt.items() if v_ > 0]
        for e in (engines or self.eng):
            for ev in evs:
                if ev[0] == e:
                    continue
                self._wait(e, ev)


class Phase:
    def __init__(self, k):
        self.k = k
        self.es = ExitStack()

    def __enter__(self):
        self.es.__enter__()
        return self

    def __exit__(self, *a):
        self.k.s.barrier()
        return self.es.__exit__(*a)

    def sb(self, name, shape, dtype, parts=1):
        self.k.uid += 1
        t = self.es.enter_context(self.k.nc.sbuf_tensor(f"ph_{name}_{self.k.uid}", shape, dtype))
        return T(name, t, parts)


class KB:
    def __init__(self):
        self.nc = bass.Bass("TRN2", target_bir_lowering=False)
        self.es = ExitStack()
        self.s = Sched(self.nc, self.es)
        self.uid = 0

    def sb(self, name, shape, dtype, parts=1):
        t = self.es.enter_context(self.nc.sbuf_tensor("sb_" + name, shape, dtype))
        return T(name, t, parts)

    def ps(self, name, shape, dtype=F32, parts=1):
        t = self.es.enter_context(self.nc.psum_tensor("pp_" + name, shape, dtype))
        return T(name, t, parts)

    def dram_in(self, name, shape, dtype=F32):
        return T(name, self.nc.dram_tensor(name, list(shape), dtype, kind="ExternalInput").ap())

    def dram_out(self, name, shape, dtype=F32):
        return T(name, self.nc.dram_tensor(name, list(shape), dtype, kind="ExternalOutput").ap())

    def phase(self):
        return Phase(self)


class Bg:
    def __init__(self):
        self.q = []

    def add(self, gen, period=2):
        self.q.append([gen, period, period])

    def tick(self):
        for item in list(self.q):
            item[2] -= 1
            if item[2] <= 0:
                item[2] = item[1]
                try:
                    next(item[0])
                except StopIteration:
                    self.q.remove(item)

    def drain(self):
        while self.q:
            for item in list(self.q):
                try:
                    next(item[0])
                except StopIteration:
                    self.q.remove(item)


def rms_rstd_gen(k, C, src, src_bufs, SQ, PST, RSTD, ntok, fuse_sq=False):
    s = k.s
    s.op("act", lambda e: e.activation(out=SQ[:, :, 0:ntok], in_=src, func=AF.Square),
         reads=src_bufs, writes=SQ.all())
    if not fuse_sq:
        yield
    for kc in range(8):
        s.op("pe", lambda e, kc=kc: e.matmul(PST[:, 0:ntok], lhsT=C["ones_m"][:, :], rhs=SQ[:, kc, 0:ntok],
                                             start=(kc == 0), stop=(kc == 7)),
             reads=SQ.all() + C["ones_m"].all(), writes=PST.all())
    yield
    s.op("act", lambda e: e.activation(out=RSTD[:, 0:ntok], in_=PST[:, 0:ntok], func=AF.Ln, bias=C["eps"][:, 0:1]),
         reads=PST.all() + C["eps"].all(), writes=RSTD.all())
    yield
    s.op("act", lambda e: e.activation(out=RSTD[:, 0:ntok], in_=RSTD[:, 0:ntok], func=AF.Exp, scale=-0.5),
         reads=RSTD.all(), writes=RSTD.all())


def ffn_stage(k, C, XT, PS, ffns, next_gi=None):
    s = k.s
    G_ = C["gains"]
    with k.phase() as ph:
        HTG = C["HTG"]
        HTs = []
        for i in range(2):
            hv = T(f"HTv{i}", HTG.t[:, :, i * 1024:(i + 1) * 1024])
            hv.bufs = HTG.bufs[2 * i:2 * i + 2]
            HTs.append(hv)
        ACTT = ph.sb("ACTT", [128, 11, 1024], BF16, parts=22)
        YT = ph.sb("YT", [128, 8, 1024], F32, parts=16)
        SQ = [ph.sb(f"SQ{i}", [128, 8, 512], BF16) for i in range(2)]
        RSTD = [ph.sb(f"RSTD{i}", [128, 512], F32) for i in range(2)]
        WIN = [ph.sb(f"WIN{i}", [128, 8, 256], BF16) for i in range(3)]
        WOUT = [ph.sb(f"WOUT{i}", [128, 11, 128], BF16) for i in range(3)]
        SG = [ph.sb(f"SG{i}", [128, 512], F32) for i in range(2)]
        PG = [PS[0], PS[1]]
        PU = [PS[2], PS[3]]
        PY = [PS[4], PS[5]]
        PST = [PS[6], PS[7]]
        st = {"win": 0, "wout": 0, "pi": 0, "ni": 0}
        jobs = [(f, B) for f in range(len(ffns)) for B in range(2)]

        bg = Bg()

        def prenorm(ji):
            f, B = jobs[ji]
            HT = HTs[ji % 2]
            gi_pre = ffns[f][2]
            for sb_ in range(2):
                tok = B * 1024 + sb_ * 512
                xb = XT.b(B * 2 + sb_)
                n_ = st["ni"] % 2
                st["ni"] += 1
                yield from rms_rstd_gen(k, C, XT[:, :, tok:tok + 512], xb, SQ[n_], PST[n_], RSTD[n_], 512)
                for kc in range(8):
                    if kc == 4:
                        yield
                    s.op("dve", lambda e, kc=kc, tok=tok, sb_=sb_, n_=n_: e.scalar_tensor_tensor(
                        out=HT[:, kc, sb_ * 512:(sb_ + 1) * 512], in0=XT[:, kc, tok:tok + 512],
                        scalar=G_[:, gi_pre, kc:kc + 1], in1=RSTD[n_][:, :], op0=ALU.mult, op1=ALU.mult),
                        reads=xb + RSTD[n_].all() + G_.all(), writes=HT.b(sb_))

        def postnorm(ji):
            f, B = jobs[ji]
            gi_post = ffns[f][3]
            for sb_ in range(2):
                tok = B * 1024 + sb_ * 512
                rhs_sl = slice(sb_ * 512, (sb_ + 1) * 512)
                ybs = YT.b(*[dc * 2 + sb_ for dc in range(8)])
                xb = XT.b(B * 2 + sb_)
                n_ = st["ni"] % 2
                st["ni"] += 1
                yield from rms_rstd_gen(k, C, YT[:, :, rhs_sl], ybs, SQ[n_], PST[n_], RSTD[n_], 512)
                for dc in range(8):
                    if dc % 2 == 0 and dc > 0:
                        yield
                    s.op("dve", lambda e, dc=dc, rhs_sl=rhs_sl, n_=n_: e.scalar_tensor_tensor(
                        out=YT[:, dc, rhs_sl], in0=YT[:, dc, rhs_sl], scalar=G_[:, gi_post, dc:dc + 1],
                        in1=RSTD[n_][:, :], op0=ALU.mult, op1=ALU.mult),
                        reads=YT.b(dc * 2 + sb_) + RSTD[n_].all() + G_.all(), writes=YT.b(dc * 2 + sb_))
                    s.op("dve", lambda e, dc=dc, rhs_sl=rhs_sl, tok=tok: e.tensor_tensor(
                        out=XT[:, dc, tok:tok + 512], in0=XT[:, dc, tok:tok + 512], in1=YT[:, dc, rhs_sl], op=ALU.add),
                        reads=YT.b(dc * 2 + sb_) + xb, writes=xb)

        def up(ji, G, after_first=None):
            f, B = jobs[ji]
            HT = HTs[ji % 2]
            w_in_d = ffns[f][0]
            for jj in range(11):
                j = G * 11 + jj
                W = WIN[st["win"] % 3]
                st["win"] += 1
                s.dma("pool", W[:, :, :], w_in_d[j].rearrange("p (kc c) -> p kc c", kc=8), writes=W.all())
                for sb_ in range(2):
                    pg, pu, sg = PG[st["pi"] % 2], PU[st["pi"] % 2], SG[st["pi"] % 2]
                    st["pi"] += 1
                    rhs_sl = slice(sb_ * 512, (sb_ + 1) * 512)
                    for kc in range(8):
                        s.op("pe", lambda e, kc=kc, pg=pg, W=W, rhs_sl=rhs_sl: e.matmul(
                            pg[:, :], lhsT=W[:, kc, 0:128], rhs=HT[:, kc, rhs_sl], start=(kc == 0), stop=(kc == 7)),
                            reads=W.all() + HT.b(sb_), writes=pg.all())
                    for kc in range(8):
                        s.op("pe", lambda e, kc=kc, pu=pu, W=W, rhs_sl=rhs_sl: e.matmul(
                            pu[:, :], lhsT=W[:, kc, 128:256], rhs=HT[:, kc, rhs_sl], start=(kc == 0), stop=(kc == 7)),
                            reads=W.all() + HT.b(sb_), writes=pu.all())
                    s.op("act", lambda e, pg=pg, sg=sg: e.activation(out=sg[:, :], in_=pg[:, :], func=AF.Silu),
                         reads=pg.all(), writes=sg.all())
                    s.op("dve", lambda e, pu=pu, sg=sg, jj=jj, rhs_sl=rhs_sl: e.tensor_tensor(
                        out=ACTT[:, jj, rhs_sl], in0=sg[:, :], in1=pu[:, :], op=ALU.mult),
                        reads=sg.all() + pu.all(), writes=ACTT.b(jj * 2 + sb_))
                    bg.tick()
                if jj == 0 and after_first is not None:
                    after_first()

        def down(ji, G):
            f, B = jobs[ji]
            w_out_d = ffns[f][1]
            for dc in range(8):
                W = WOUT[st["wout"] % 3]
                st["wout"] += 1
                s.dma("pool", W[:, :, :], w_out_d[G, dc].rearrange("p (jj c) -> p jj c", jj=11), writes=W.all())
                for sb_ in range(2):
                    py = PY[st["pi"] % 2]
                    st["pi"] += 1
                    rhs_sl = slice(sb_ * 512, (sb_ + 1) * 512)
                    for jj in range(11):
                        s.op("pe", lambda e, jj=jj, py=py, W=W, rhs_sl=rhs_sl: e.matmul(
                            py[:, :], lhsT=W[:, jj, :], rhs=ACTT[:, jj, rhs_sl], start=(jj == 0), stop=(jj == 10)),
                            reads=W.all() + ACTT.b(jj * 2 + sb_), writes=py.all())
                    yb = YT.b(dc * 2 + sb_)
                    if G == 0:
                        s.op("act", lambda e, py=py, dc=dc, rhs_sl=rhs_sl: e.activation(
                            out=YT[:, dc, rhs_sl], in_=py[:, :], func=AF.Copy),
                            reads=py.all(), writes=yb)
                    else:
                        s.op("dve", lambda e, py=py, dc=dc, rhs_sl=rhs_sl: e.tensor_tensor(
                            out=YT[:, dc, rhs_sl], in0=YT[:, dc, rhs_sl], in1=py[:, :], op=ALU.add),
                            reads=py.all() + yb, writes=yb)
                    bg.tick()

        def next_prenorm():
            HT = HTs[0]
            for sb_ in range(2):
                tok = sb_ * 512
                xb = XT.b(sb_)
                n_ = st["ni"] % 2
                st["ni"] += 1
                yield from rms_rstd_gen(k, C, XT[:, :, tok:tok + 512], xb, SQ[n_], PST[n_], RSTD[n_], 512)
                for kc in range(8):
                    if kc == 4:
                        yield
                    s.op("dve", lambda e, kc=kc, tok=tok, sb_=sb_, n_=n_: e.scalar_tensor_tensor(
                        out=HT[:, kc, sb_ * 512:(sb_ + 1) * 512], in0=XT[:, kc, tok:tok + 512],
                        scalar=G_[:, next_gi, kc:kc + 1], in1=RSTD[n_][:, :], op0=ALU.mult, op1=ALU.mult),
                        reads=xb + RSTD[n_].all() + G_.all(), writes=HT.b(sb_))

        n = len(jobs)
        assert n % 2 == 0
        if C["ht_ready"] is not None and C["ht_ready"] == (ffns[0][2], (0, 1)):
            pass
        else:
            bg.add(prenorm(0))
            bg.drain()
        C["ht_ready"] = None
        for ji in range(n):
            up(ji, 0, after_first=(lambda ji=ji: bg.add(postnorm(ji - 1), 1)) if ji > 0 else None)
            bg.drain()
            down(ji, 0)
            if ji + 1 < n:
                bg.add(prenorm(ji + 1), 1)
            elif next_gi is not None:
                bg.add(next_prenorm(), 1)
                C["ht_ready"] = (next_gi, (0, 1))
            up(ji, 1)
            bg.drain()
            down(ji, 1)
        bg.add(postnorm(n - 1))
        bg.drain()


def prenorm_to_HT(k, C, ph, XT, HT, PS, gi_pre, col_off=0):
    s = k.s
    G_ = C["gains"]
    with k.phase() as p2:
        SQ = [p2.sb(f"SQ{i}", [128, 8, 512], BF16) for i in range(2)]
        RSTD = [p2.sb(f"RSTD{i}", [128, 512], F32) for i in range(2)]
        bg = Bg()

        def chain(tb):
            tok = tb * 512
            xb = XT.b(tb)
            yield from rms_rstd_gen(k, C, XT[:, :, tok:tok + 512], xb, SQ[tb % 2], PS[6 + tb % 2], RSTD[tb % 2], 512)
            for kc in range(8):
                if kc == 4:
                    yield
                s.op("dve", lambda e, kc=kc: e.scalar_tensor_tensor(
                    out=HT[:, kc, col_off + tok:col_off + tok + 512], in0=XT[:, kc, tok:tok + 512],
                    scalar=G_[:, gi_pre, kc:kc + 1], in1=RSTD[tb % 2][:, :], op0=ALU.mult, op1=ALU.mult),
                    reads=xb + RSTD[tb % 2].all() + G_.all(), writes=HT.b(tb))

        skip = ()
        if C["ht_ready"] is not None and C["ht_ready"][0] == gi_pre and col_off == 0:
            skip = C["ht_ready"][1]
        C["ht_ready"] = None
        for tb in range(4):
            if tb in skip:
                continue
            bg.add(chain(tb), 1)
            bg.tick()
            bg.tick()
        bg.drain()


def outproj_postnorm(k, C, XT, PS, OT, wo_d, gi_post, next_gi=None, WO=None):
    s = k.s
    G_ = C["gains"]
    with k.phase() as p3:
        if WO is None:
            WO = T("WOv", C["HTG"].t[:, :, 1024:2048])
            WO.bufs = C["HTG"].bufs[2:4]
            for kc in range(8):
                s.dma("pool", WO[:, kc, :], wo_d[kc * 128:(kc + 1) * 128, :], writes=WO.all())
        YTs = [p3.sb(f"YT{i}", [128, 8, 512], F32, parts=8) for i in range(2)]
        SQ1 = p3.sb("SQ", [128, 8, 512], BF16)
        RSTD = [p3.sb(f"RSTD{i}", [128, 512], F32) for i in range(2)]
        bg = Bg()

        def chain(tb):
            tok = tb * 512
            YT = YTs[tb % 2]
            yield from rms_rstd_gen(k, C, YT[:, :, :], YT.all(), SQ1, PS[6 + tb % 2], RSTD[tb % 2], 512, fuse_sq=True)
            xb = XT.b(tb)
            for dc in range(8):
                if dc % 2 == 0 and dc > 0:
                    yield
                s.op("dve", lambda e, dc=dc: e.scalar_tensor_tensor(
                    out=YT[:, dc, :], in0=YT[:, dc, :], scalar=G_[:, gi_post, dc:dc + 1],
                    in1=RSTD[tb % 2][:, :], op0=ALU.mult, op1=ALU.mult),
                    reads=YT.b(dc) + RSTD[tb % 2].all() + G_.all(), writes=YT.b(dc))
                s.op("dve", lambda e, dc=dc: e.tensor_tensor(
                    out=XT[:, dc, tok:tok + 512], in0=XT[:, dc, tok:tok + 512], in1=YT[:, dc, :], op=ALU.add),
                    reads=YT.b(dc) + xb, writes=xb)

        if next_gi is not None:
            RSTDn = p3.sb("RSTDn", [128, 512], F32)
        HTG = C["HTG"]

        def next_prenorm():
            for sb_ in range(2):
                tok = sb_ * 512
                xb = XT.b(sb_)
                yield from rms_rstd_gen(k, C, XT[:, :, tok:tok + 512], xb, SQ1, PS[0], RSTDn, 512, fuse_sq=True)
                for kc in range(8):
                    if kc == 4:
                        yield
                    s.op("dve", lambda e, kc=kc, tok=tok: e.scalar_tensor_tensor(
                        out=HTG[:, kc, tok:tok + 512], in0=XT[:, kc, tok:tok + 512],
                        scalar=G_[:, next_gi, kc:kc + 1], in1=RSTDn[:, :], op0=ALU.mult, op1=ALU.mult),
                        reads=xb + RSTDn.all() + G_.all(), writes=HTG.b(sb_))

        pi = 0
        for tb in range(4):
            tok = tb * 512
            YT = YTs[tb % 2]
            if tb == 3 and next_gi is not None:
                bg.add(next_prenorm(), 1)
                C["ht_ready"] = (next_gi, (0, 1))
            for dc in range(8):
                pp = PS[4 + pi % 2]
                pi += 1
                for kc in range(8):
                    s.op("pe", lambda e, kc=kc, dc=dc, pp=pp, tok=tok: e.matmul(
                        pp[:, :], lhsT=WO[:, kc, dc * 128:(dc + 1) * 128], rhs=OT[:, kc, tok:tok + 512],
                        start=(kc == 0), stop=(kc == 7)),
                        reads=WO.all() + OT.all(), writes=pp.all())
                s.op("act", lambda e, dc=dc, pp=pp, YT=YT: e.activation(out=YT[:, dc, :], in_=pp[:, :], func=AF.Copy),
                     reads=pp.all(), writes=YT.b(dc))
                bg.tick()
            bg.drain()
            bg.add(chain(tb), 1)
        bg.drain()


def attn_stage(k, C, XT, PS, wqkv_d, wo_d, gi_pre, gi_post, next_gi=None):
    s = k.s
    with k.phase() as ph:
        OT = ph.sb("OT", [128, 8, SEQ], BF16, parts=1)
        with k.phase() as pab:
            HT = C["HTG"]
            prenorm_to_HT(k, C, pab, XT, HT, PS, gi_pre)
            with k.phase() as pb:
                NEGM = pb.sb("negm", [128, 4, 512], BF16)
                ZB = pb.sb("zb", [128, 512], BF16)
                s.op("dve", lambda e: e.memset(ZB[:, :], 0.0), writes=ZB.all())
                for d in range(4):
                    s.op("pool", lambda e, d=d: e.affine_select(
                        out=NEGM[:, d, :], in_=ZB[:, :], pattern=[[1, 512]], compare_op=ALU.is_gt,
                        fill=-30000.0, base=-128 * d, channel_multiplier=-1),
                        reads=ZB.all(), writes=NEGM.all())
                QT = [pb.sb(f"QT{i}", [128, SEQ], BF16) for i in range(2)]
                KT = [pb.sb(f"KT{i}", [128, SEQ], BF16) for i in range(2)]
                V = [pb.sb(f"V{i}", [128, 16, 128], BF16) for i in range(2)]
                W = [pb.sb(f"WQKV{i}", [128, 8, 384], BF16) for i in range(1)]
                OTOK = [pb.sb(f"OTOK{i}", [128, 16, 128], BF16) for i in range(2)]
                E = [pb.sb(f"E{i}", [128, 512], F32) for i in range(3)]
                SP = [pb.sb(f"SP{i}", [128, 512], BF16) for i in range(5)]
                ATT = [pb.sb(f"ATT{i}", [128, 512], BF16) for i in range(3)]
                OACC = [pb.sb(f"OACC{i}", [128, 4, 64], F32) for i in range(2)]
                CACC2 = pb.sb("CACC2", [128, 2, 4], F32, parts=2)
                FS2 = [pb.sb(f"FS2_{i}", [128, 2, 4], F32) for i in range(4)]
                PZ = [PS[0], PS[1], PS[2], PS[3], PS[4]]
                PO = [PS[5], PS[6]]
                PP = [PS[7]]
                st = {"pi": 0}

                bg = Bg()

                def pre_hp(hp):
                    w = W[0]
                    qt, kt_, v = QT[hp % 2], KT[hp % 2], V[hp % 2]
                    s.dma("pool", w[:, :, :], wqkv_d[hp].rearrange("p (kc c) -> p kc c", kc=8), writes=w.all())
                    for which in range(2):
                        for tb in range(4):
                            pp = PP[st["pi"] % len(PP)]
                            st["pi"] += 1
                            for kc in range(8):
                                if kc in (2, 4, 6):
                                    yield
                                s.op("pe", lambda e, kc=kc, pp=pp, w=w, which=which, tb=tb: e.matmul(
                                    pp[:, :], lhsT=w[:, kc, which * 128:(which + 1) * 128],
                                    rhs=HT[:, kc, tb * 512:(tb + 1) * 512], start=(kc == 0), stop=(kc == 7)),
                                    reads=w.all() + HT.b(tb), writes=pp.all())
                            if which == 0:
                                s.op("dve", lambda e, pp=pp, qt=qt, tb=tb: e.tensor_scalar(
                                    out=qt[:, tb * 512:(tb + 1) * 512], in0=pp[:, :], scalar1=0.125, scalar2=None,
                                    op0=ALU.mult),
                                    reads=pp.all(), writes=qt.all())
                            else:
                                s.op("dve", lambda e, pp=pp, kt_=kt_, tb=tb: e.tensor_copy(
                                    out=kt_[:, tb * 512:(tb + 1) * 512], in_=pp[:, :]),
                                    reads=pp.all(), writes=kt_.all())
                            yield
                    for tg in range(4):
                        pp = PP[st["pi"] % len(PP)]
                        st["pi"] += 1
                        for tt in range(4):
                            tok = (tg * 4 + tt) * 128
                            if tt > 0:
                                yield
                            for kc in range(8):
                                s.op("pe", lambda e, kc=kc, pp=pp, w=w, tt=tt, tok=tok: e.matmul(
                                    pp[:, tt * 128:(tt + 1) * 128], lhsT=HT[:, kc, tok:tok + 128],
                                    rhs=w[:, kc, 256:384], start=(kc == 0), stop=(kc == 7)),
                                    reads=w.all() + HT.b(tg), writes=pp.all())
                        s.op("dve", lambda e, pp=pp, v=v, tg=tg: e.tensor_copy(
                            out=v[:, tg * 4:(tg + 1) * 4, :], in_=pp[:, :].rearrange("p (a b) -> p a b", a=4)),
                            reads=pp.all(), writes=v.all())
                        yield

                def post_hp(hp):
                    otok = OTOK[hp % 2]
                    for tg in range(4):
                        pp = PP[st["pi"] % len(PP)]
                        st["pi"] += 1
                        for tt in range(4):
                            s.op("pe", lambda e, pp=pp, tt=tt, tg=tg, otok=otok: e.matmul(
                                pp[:, tt * 128:(tt + 1) * 128], lhsT=otok[:, tg * 4 + tt, :], rhs=C["ident"][:, :],
                                start=True, stop=True),
                                reads=otok.all() + C["ident"].all(), writes=pp.all())
                        s.op("dve", lambda e, pp=pp, tg=tg, hp=hp: e.tensor_copy(
                            out=OT[:, hp, tg * 512:(tg + 1) * 512], in_=pp[:, :]),
                            reads=pp.all(), writes=OT.all())
                        yield

                units = []
                for hp in range(8):
                    for g in range(4):
                        for kt in range(4 * g + 3, -1, -1):
                            for hh in range(2):
                                units.append((hp, hh, g, kt))
                n = len(units)
                NPZ, NSP, NATT, NPO, NE = 5, 5, 3, 2, 3

                def u_(i):
                    hp, hh, g, kt = units[i]
                    d = kt - 4 * g
                    return hp, hh, g, kt, d, slice(hh * 64, (hh + 1) * 64)

                def c0_(i):
                    hp, hh, g, kt = units[i]
                    return max(kt - 4 * g, 0) * 128

                def s0_qk(i):
                    hp, hh, g, kt, d, hs = u_(i)
                    if hh == 0 and g == 0 and kt == 3:
                        if hp == 0:
                            bg.add(pre_hp(0))
                        bg.drain()
                    if hh == 0 and g == 1 and kt == 5 and hp + 1 < 8:
                        bg.add(pre_hp(hp + 1), 1)
                    pz, qt, kt_ = PZ[i % NPZ], QT[hp % 2], KT[hp % 2]
                    q0 = g * 512
                    c0 = c0_(i)
                    s.op("pe", lambda e: e.matmul(pz[:, c0:512], lhsT=kt_[hs, kt * 128:(kt + 1) * 128],
                                                  rhs=qt[hs, q0 + c0:q0 + 512], start=True, stop=(d < 0)),
                         reads=kt_.all() + qt.all(), writes=pz.all())
                    if d >= 0:
                        s.op("pe", lambda e: e.matmul(pz[:, c0:c0 + 128], lhsT=C["ident"][:, :], rhs=NEGM[:, d, c0:c0 + 128],
                                                      start=False, stop=True),
                             reads=C["ident"].all() + NEGM.all(), writes=pz.all())

                def s1_exp(i):
                    pz, e_ = PZ[i % NPZ], E[i % NE]
                    c0 = c0_(i)
                    s.op("act", lambda e: e.activation(out=e_[:, c0:512], in_=pz[:, c0:512], func=AF.Exp),
                         reads=pz.all(), writes=e_.all())

                def s2_ln(i):
                    e_, sp = E[i % NE], SP[i % NSP]
                    c0 = c0_(i)
                    s.op("act", lambda e: e.activation(out=sp[:, c0:512], in_=e_[:, c0:512], func=AF.Ln,
                                                       bias=C["one_f"][:, 0:1]),
                         reads=e_.all() + C["one_f"].all(), writes=sp.all())

                def s3_tri(i):
                    pz, sp = PZ[i % NPZ], SP[i % NSP]
                    c0 = c0_(i)
                    s.op("pe", lambda e: e.matmul(pz[:, c0:512], lhsT=C["ntri"][:, :], rhs=sp[:, c0:512], start=False,
                                                  stop=True, skip_group_check=True),
                         reads=sp.all() + C["ntri"].all(), writes=pz.all())

                def s4_att(i):
                    pz, att = PZ[i % NPZ], ATT[i % NATT]
                    c0 = c0_(i)
                    s.op("act", lambda e: e.activation(out=att[:, c0:512], in_=pz[:, c0:512], func=AF.Exp),
                         reads=pz.all(), writes=att.all())

                def s5_av(i):
                    hp, hh, g, kt, d, hs = u_(i)
                    qlo = max(d, 0)
                    sp, att, po, v = SP[i % NSP], ATT[i % NATT], PO[i % NPO], V[hp % 2]
                    for qi in range(qlo, 4):
                        s.op("pe", lambda e, qi=qi: e.matmul(
                            po[:, qi * 64:(qi + 1) * 64], lhsT=att[:, qi * 128:(qi + 1) * 128],
                            rhs=v[:, kt, hs], start=True, stop=True),
                            reads=att.all() + v.all(), writes=po.all())
                        s.op("pe", lambda e, qi=qi: e.matmul(
                            po[:, 256 + qi:257 + qi], lhsT=sp[:, qi * 128:(qi + 1) * 128],
                            rhs=C["ones_col"][:, 0:1], start=True, stop=True),
                            reads=sp.all() + C["ones_col"].all(), writes=po.all())

                def s6_acc(i):
                    hp, hh, g, kt, d, hs = u_(i)
                    qlo = max(d, 0)
                    span = hh
                    po = PO[i % NPO]
                    oacc, fs2 = OACC[span % 2], FS2[(i // 2) % 4]
                    cb = CACC2.b(hh)
                    otok = OTOK[hp % 2]
                    if kt != 4 * g + 3:
                        if hh == 0:
                            s.op("act", lambda e: e.activation(out=fs2[:, :, :], in_=CACC2[:, :, :], func=AF.Exp, scale=-1.0),
                                 reads=CACC2.all(), writes=fs2.all())
                        s.op("dve", lambda e: e.tensor_tensor(
                            out=CACC2[:, hh, qlo:4], in0=CACC2[:, hh, qlo:4], in1=po[:, 256 + qlo:260], op=ALU.add),
                            reads=po.all() + cb, writes=cb)
                        for qi in range(qlo, 4):
                            s.op("dve", lambda e, qi=qi: e.scalar_tensor_tensor(
                                out=oacc[:, qi, :], in0=po[:, qi * 64:(qi + 1) * 64], scalar=fs2[:, hh, qi:qi + 1],
                                in1=oacc[:, qi, :], op0=ALU.mult, op1=ALU.add),
                                reads=po.all() + fs2.all() + oacc.all(), writes=oacc.all())
                    else:
                        if qlo > 0:
                            s.op("dve", lambda e: e.memset(oacc[:, 0:qlo, :], 0.0), writes=oacc.all())
                            s.op("dve", lambda e: e.memset(CACC2[:, hh, 0:qlo], 0.0), writes=cb)
                        s.op("dve", lambda e: e.tensor_copy(
                            out=oacc[:, qlo:4, :], in_=po[:, qlo * 64:256].rearrange("p (a b) -> p a b", b=64)),
                            reads=po.all(), writes=oacc.all())
                        s.op("dve", lambda e: e.tensor_copy(out=CACC2[:, hh, qlo:4], in_=po[:, 256 + qlo:260]),
                             reads=po.all(), writes=cb)
                    if kt == 0:
                        s.op("dve", lambda e: e.tensor_copy(out=otok[:, 4 * g:4 * g + 4, hs], in_=oacc[:, :, :]),
                             reads=oacc.all(), writes=otok.all())
                        if hh == 1 and g == 3:
                            bg.add(post_hp(hp), 1)

                stages = ((0, s0_qk), (1, s1_exp), (2, s2_ln), (3, s3_tri), (4, s4_att), (5, s5_av), (6, s6_acc))
                for i in range(n + 6):
                    for lag, fn in stages:
                        if 0 <= i - lag < n:
                            fn(i - lag)
                    bg.tick()
                bg.drain()
        if "dbg_OT" in C:
            s.dma("sp", C["dbg_OT"][:, :, :], OT[:, :, :], reads=OT.all(), writes=C["dbg_OT"].all())
        outproj_postnorm(k, C, XT, PS, OT, wo_d, gi_post, next_gi)


AX = mybir.AxisListType


class Ref:
    __slots__ = ("ap", "bufs")

    def __init__(self, ap, bufs):
        self.ap = ap
        self.bufs = bufs


class _RefMaker:
    def __init__(self, t):
        self.t = t

    def __getitem__(self, key):
        return Ref(self.t.t[key], self.t.all())


def rf(t):
    return _RefMaker(t)


def _b(*refs):
    out = []
    for r in refs:
        if isinstance(r, Ref):
            out += r.bufs
    return out


def _a(x):
    return x.ap if isinstance(x, Ref) else x


def e_tt(s, eng, out, a, b, op):
    return s.op(eng, lambda E: E.tensor_tensor(out=out.ap, in0=a.ap, in1=b.ap, op=op), reads=_b(a, b), writes=out.bufs)


def e_ts(s, eng, out, a, s1, s2, op0, op1=None):
    if op1 is None:
        return s.op(eng, lambda E: E.tensor_scalar(out=out.ap, in0=a.ap, scalar1=_a(s1), scalar2=None, op0=op0),
                    reads=_b(a, s1), writes=out.bufs)
    return s.op(eng, lambda E: E.tensor_scalar(out=out.ap, in0=a.ap, scalar1=_a(s1), scalar2=_a(s2), op0=op0, op1=op1),
                reads=_b(a, s1, s2), writes=out.bufs)


def e_stt(s, eng, out, a, sc, b, op0, op1):
    return s.op(eng, lambda E: E.scalar_tensor_tensor(out=out.ap, in0=a.ap, scalar=_a(sc), in1=b.ap, op0=op0, op1=op1),
                reads=_b(a, sc, b), writes=out.bufs)


def e_act(s, out, a, func, bias=None, scale=None):
    kw = {}
    if bias is not None:
        kw["bias"] = _a(bias)
    if scale is not None:
        kw["scale"] = _a(scale)
    return s.op("act", lambda E: E.activation(out=out.ap, in_=a.ap, func=func, **kw), reads=_b(a, bias, scale),
                writes=out.bufs)


def e_mm(s, out, lhsT, rhs, start=True, stop=True):
    return s.op("pe", lambda E: E.matmul(out.ap, lhsT=lhsT.ap, rhs=rhs.ap, start=start, stop=stop),
                reads=_b(lhsT, rhs), writes=out.bufs)


def e_copy(s, eng, out, a):
    if eng == "act":
        return e_act(s, out, a, AF.Copy)
    return s.op(eng, lambda E: E.tensor_copy(out=out.ap, in_=a.ap), reads=_b(a), writes=out.bufs)


def e_memset(s, eng, out, val):
    return s.op(eng, lambda E: E.memset(out.ap, val), writes=out.bufs)


GN_EPS = 64e-5
STAGGER = 3
NEG_EXP_HALF = -0.6065306597126334


def mixer0_stage(k, C, XT, PS, D, gi_pre, gi_post, next_gi=None):
    s = k.s
    with k.phase() as ph:
        OT = ph.sb("OT", [128, 8, SEQ], BF16, parts=1)
        with k.phase() as pab:
            HT = C["HTG"]
            prenorm_to_HT(k, C, pab, XT, HT, PS, gi_pre)
            with k.phase() as pl:
                rglru_part(k, C, pl, HT, OT, PS, D)
            with k.phase() as pr:
                rwkv_part(k, C, pr, HT, OT, PS, D, XT)
        if "dbg_OT" in D:
            s.dma("sp", D["dbg_OT"][:, :, :], OT[:, :, :], reads=OT.all(), writes=D["dbg_OT"].all())
        outproj_postnorm(k, C, XT, PS, OT, D["l0_w_out"], gi_post, next_gi)


def rglru_part(k, C, p, HT, OT, PS, D):
    s = k.s
    PL = p.sb("PL", [128, 4, 8], F32)
    s.dma("sp", PL[:, :, :], D["l0_pl"][:, :].rearrange("p (c n) -> p c n", c=4), writes=PL.all())
    C1 = p.sb("C1", [128, 4], F32)
    e_act(s, rf(C1)[:, :], rf(PL)[:, :, 7], AF.Exp, scale=-1.0)
    e_act(s, rf(C1)[:, :], rf(C1)[:, :], AF.Ln, bias=rf(C["one_f"])[:, 0:1])
    e_ts(s, "dve", rf(C1)[:, :], rf(C1)[:, :], -8.0, None, ALU.mult)
    GAW = p.sb("GAW", [128, 4, 128], BF16)
    GXW = p.sb("GXW", [128, 4, 128], BF16)
    e_memset(s, "dve", rf(GAW)[:, :, :], 0.0)
    e_memset(s, "dve", rf(GXW)[:, :, :], 0.0)
    for n in range(8):
        ps_ = slice((n % 2) * 64, (n % 2) * 64 + 64)
        s.dma("pool", GAW[ps_, n // 2, ps_], D["l0_gate_a_w"][n], writes=GAW.all())
        s.dma("pool", GXW[ps_, n // 2, ps_], D["l0_gate_x_w"][n], writes=GXW.all())
    W = [p.sb(f"WL{i}", [128, 8, 256], BF16) for i in range(2)]
    XBs = [p.sb(f"XB{i}", [128, 515], F32) for i in range(2)]
    HHs = [[p.sb(f"HH{j}_{i}", [128, 512], F32) for i in range(2)] for j in range(2)]
    ts_ = [{n: p.sb(f"{n}{j}", [128, 512], F32) for n in ("GB", "XC", "R", "IG", "A", "U", "T1", "T2")} for j in range(2)]
    XCbs = [p.sb(f"XCb{j}", [128, 512], BF16) for j in range(2)]

    def unit(c, tb, j):
        w = W[j]
        XB, t_, XCb = XBs[j], ts_[j], XCbs[j]
        col = lambda n: rf(PL)[:, c, n:n + 1]
        tok = tb * 512
        px, pg = PS[2 * j], PS[2 * j + 1]
        for kc in range(8):
            e_mm(s, rf(px)[:, :], rf(w)[:, kc, 0:128], Ref(HT[:, kc, tok:tok + 512], HT.b(tb)), kc == 0, kc == 7)
        for kc in range(8):
            e_mm(s, rf(pg)[:, :], rf(w)[:, kc, 128:256], Ref(HT[:, kc, tok:tok + 512], HT.b(tb)), kc == 0, kc == 7)
        if tb == 0:
            e_memset(s, "dve", rf(XB)[:, 0:3], 0.0)
        else:
            e_copy(s, "dve", rf(XB)[:, 0:3], rf(XB)[:, 512:515])
        yield
        e_copy(s, "act", rf(XB)[:, 3:515], rf(px)[:, :])
        e_copy(s, "act", rf(t_["GB"])[:, :], rf(pg)[:, :])
        yield
        XC = t_["XC"]
        e_ts(s, "dve", rf(XC)[:, :], rf(XB)[:, 3:515], col(3), col(4), ALU.mult, ALU.add)
        for i in range(3):
            e_stt(s, "dve", rf(XC)[:, :], rf(XB)[:, i:i + 512], col(i), rf(XC)[:, :], ALU.mult, ALU.add)
        GB, T2 = t_["GB"], t_["T2"]
        e_act(s, rf(T2)[:, :], rf(GB)[:, :], AF.Gelu_apprx_tanh)
        yield
        e_copy(s, "act", rf(XCb)[:, :], rf(XC)[:, :])
        yield
        pr_, pig = PS[4 + 2 * j], PS[5 + 2 * j]
        e_mm(s, rf(pr_)[:, :], rf(GAW)[:, c, :], rf(XCb)[:, :])
        e_mm(s, rf(pig)[:, :], rf(GXW)[:, c, :], rf(XCb)[:, :])
        yield
        e_act(s, rf(t_["R"])[:, :], rf(pr_)[:, :], AF.Sigmoid, bias=col(5))
        e_act(s, rf(t_["IG"])[:, :], rf(pig)[:, :], AF.Sigmoid, bias=col(6))
        yield
        A = t_["A"]
        e_act(s, rf(A)[:, :], rf(t_["R"])[:, :], AF.Exp, scale=rf(C1)[:, c:c + 1])
        T1, U = t_["T1"], t_["U"]
        e_tt(s, "dve", rf(U)[:, :], rf(t_["IG"])[:, :], rf(XC)[:, :], ALU.mult)
        yield
        e_tt(s, "dve", rf(T1)[:, :], rf(A)[:, :], rf(A)[:, :], ALU.mult)
        yield
        e_ts(s, "dve", rf(T1)[:, :], rf(T1)[:, :], -1.0, 1.0, ALU.mult, ALU.add)
        yield
        e_act(s, rf(T1)[:, :], rf(T1)[:, :], AF.Sqrt)
        yield
        e_tt(s, "dve", rf(U)[:, :], rf(U)[:, :], rf(T1)[:, :], ALU.mult)
        yield
        H = HHs[j][tb % 2]
        Hp = HHs[j][(tb + 1) % 2]
        init = 0.0 if tb == 0 else Hp[:, 511:512]
        s.op("dve", lambda E: E.tensor_tensor_scan(
            out=H[:, :], data0=A[:, :], data1=U[:, :], initial=init, op0=ALU.mult, op1=ALU.add),
            reads=A.all() + U.all() + (Hp.all() if tb else []), writes=H.all())
        yield
        s.op("dve", lambda E: E.tensor_tensor(
            out=OT[:, 4 + c, tok:tok + 512], in0=H[:, :], in1=T2[:, :], op=ALU.mult),
            reads=H.all() + T2.all(), writes=OT.all())

    for cp in range(2):
        for j in range(2):
            c = 2 * cp + j
            s.dma("pool", W[j][:, :, :], D["l0_w_lru"][c].rearrange("p (kc n) -> p kc n", kc=8), writes=W[j].all())
        for tb in range(4):
            gens = [unit(2 * cp + j, tb, j) for j in range(2)]
            alive = [True, True]
            while any(alive):
                for j in range(2):
                    if alive[j]:
                        try:
                            next(gens[j])
                        except StopIteration:
                            alive[j] = False


def rwkv_part(k, C, p, HT, OT, PS, D, XT):
    s = k.s
    spill = D["xt_spill"]
    for kc in range(8):
        s.dma("sp", spill[:, kc, :], XT[:, kc, :], reads=XT.all(), writes=spill.all())
    PH = p.sb("PH", [128, 4, 8], F32)
    s.dma("sp", PH[:, :, :], D["l0_ph"][:, :].rearrange("p (h n) -> p h n", h=4), writes=PH.all())
    OM = p.sb("OM", [128, 4, 4], F32)
    e_ts(s, "dve", rf(OM)[:, :, 0:3], rf(PH)[:, :, 0:3], -1.0, 1.0, ALU.mult, ALU.add)
    e_ts(s, "dve", rf(OM)[:, :, 3:4], rf(PH)[:, :, 6:7], -1.0, 1.0, ALU.mult, ALU.add)
    RKb = p.sb("RKb", [128, 4], BF16)
    e_copy(s, "dve", rf(RKb)[:, :], rf(PH)[:, :, 7])
    MUL = p.sb("MUL", [128, 3], F32)
    s.dma("sp", MUL[:, :], D["l0_mul"][:, :], writes=MUL.all())
    OML = p.sb("OML", [128, 3], F32)
    e_ts(s, "dve", rf(OML)[:, :], rf(MUL)[:, :], -1.0, 1.0, ALU.mult, ALU.add)
    LNGBs = [p.sb(f"LNGB{i}", [128, 2, 64], F32) for i in range(2)]
    W2 = p.sb("W2", [64, 512], BF16)
    A2 = p.sb("A2", [64, 512], BF16)
    G2 = p.sb("G2", [128, 512], BF16)
    s.dma("pool", W2[:, :], D["l0_w2"][:, :], writes=W2.all())
    s.dma("pool", A2[:, :], D["l0_a2"][:, :], writes=A2.all())
    s.dma("pool", G2[:, :], D["l0_g2"][:, :], writes=G2.all())
    ob = C["ones_bf"]
    BLK = p.sb("BLK", [128, 128], BF16)
    e_memset(s, "dve", rf(BLK)[:, :], 0.0)
    e_memset(s, "dve", rf(BLK)[0:64, 0:64], 1.0)
    e_memset(s, "dve", rf(BLK)[64:128, 64:128], 1.0)
    M512 = p.sb("M512", [128, 512], BF16)
    MUS = p.sb("MUS", [128, 512], BF16)
    MUI = p.sb("MUI", [128, 512], BF16)
    MLS = p.sb("MLS", [128, 512], BF16)
    ID8 = p.sb("ID8", [128, 512], BF16)
    for hh in range(2):
        hs = slice(hh * 64, hh * 64 + 64)
        for dst, pat, cmp_, cm in ((M512, [[0, 8], [1, 64]], ALU.is_gt, 0), (MUS, [[0, 8], [1, 64]], ALU.is_gt, -1),
                                   (MUI, [[0, 8], [1, 64]], ALU.is_ge, -1), (MLS, [[0, 8], [-1, 64]], ALU.is_gt, 1),
                                   (ID8, [[0, 8], [-1, 64]], ALU.is_equal, 1)):
            s.op("pool", lambda E, dst=dst, pat=pat, cmp_=cmp_, cm=cm, hs=hs: E.affine_select(
                out=dst[hs, :], in_=ob[hs, :], pattern=pat, compare_op=cmp_, fill=0.0, base=0, channel_multiplier=cm),
                reads=ob.all(), writes=dst.all())
    ident = C["ident"]

    TW = p.sb("TW", [64, SEQ], BF16)
    AL = p.sb("AL", [64, SEQ], BF16)
    SGL = p.sb("SGL", [128, SEQ], BF16)
    with k.phase() as p0:
        WLo = p0.sb("WLo", [128, 8, 256], BF16)
        s.dma("pool", WLo[:, :, :], D["l0_w_lora"][:, :].rearrange("p (kc n) -> p kc n", kc=8), writes=WLo.all())
        PAl = [p0.sb(f"PAl{i}", [128, 513], F32) for i in range(3)]
        TMPl = [p0.sb(f"TMPl{i}", [128, 512], F32) for i in range(3)]

        def lora_chain(which, c0, c1, npart, dst):
            PA, tmpl = PAl[which], TMPl[which]
            for tb in range(4):
                tok = tb * 512
                pp = PS[which * 2 + tb % 2]
                for kc in range(8):
                    e_mm(s, rf(pp)[0:npart, :], rf(WLo)[:, kc, c0:c1], Ref(HT[:, kc, tok:tok + 512], HT.b(tb)), kc == 0, kc == 7)
                if tb == 0:
                    e_memset(s, "dve", rf(PA)[0:npart, 0:1], 0.0)
                else:
                    e_copy(s, "dve", rf(PA)[0:npart, 0:1], rf(PA)[0:npart, 512:513])
                yield
                e_copy(s, "act", rf(PA)[0:npart, 1:513], rf(pp)[0:npart, :])
                yield
                e_act(s, rf(tmpl)[0:npart, :], rf(PA)[0:npart, 0:512], AF.Copy, scale=rf(MUL)[0:npart, which:which + 1])
                yield
                e_stt(s, "dve", rf(tmpl)[0:npart, :], rf(PA)[0:npart, 1:513], rf(OML)[0:npart, which:which + 1],
                      rf(tmpl)[0:npart, :], ALU.mult, ALU.add)
                yield
                if which == 0:
                    e_act(s, rf(dst)[:, tok:tok + 512], rf(tmpl)[0:64, :], AF.Tanh)
                elif which == 1:
                    e_copy(s, "act", rf(dst)[:, tok:tok + 512], rf(tmpl)[0:64, :])
                else:
                    e_act(s, rf(dst)[:, tok:tok + 512], rf(tmpl)[:, :], AF.Sigmoid)
                yield

        lbg = Bg()
        for which, (c0, c1, npart, dst) in enumerate(((0, 64, 64, TW), (64, 128, 64, AL), (128, 256, 128, SGL))):
            lbg.add(lora_chain(which, c0, c1, npart, dst), 1)
        lbg.drain()

    s.barrier()
    XTf = XT.t
    XTb = XT.t.bitcast(BF16)
    f32n = ("r", "k", "SIG", "A", "KKN", "KH", "CUM", "EC", "EX", "EN", "TMP")
    b16n = ("Rt", "At", "Bt", "Kt", "Bh", "Kh", "RK", "VT", "KK2")
    t64n = ("V64", "BH64", "KH64", "N", "Q", "N2", "Q2", "XA", "LAK", "ARB", "ARK")
    sets = []
    for S in range(2):
        B = {}
        if S == 0:
            B["WH"] = p.sb("WH", [128, 8, 384], BF16)
            B["PA"] = [p.sb(f"PA{i}", [128, 513], F32) for i in range(3)]
            F = {n: p.sb("f_" + n, [128, 512], F32) for n in f32n}
            Bf = {n: p.sb("b_" + n, [128, 512], BF16) for n in b16n}
            T64 = {n: p.sb("t_" + n, [128, 512], BF16) for n in t64n}
        else:
            B["WH"] = T("WH1", XTb[:, 7, 0:3072].rearrange("p (kc n) -> p kc n", kc=8))
            B["PA"] = [T(f"PA1_{i}", XTf[:, 3, i * 513:(i + 1) * 513]) for i in range(3)]
            F = {n: T("f1_" + n, XTf[:, i // 4, (i % 4) * 512:(i % 4) * 512 + 512]) for i, n in enumerate(f32n)}
            bl = list(b16n) + list(t64n)
            vb = {n: T("b1_" + n, XTb[:, 4 + i // 8, (i % 8) * 512:(i % 8) * 512 + 512]) for i, n in enumerate(bl)}
            Bf = {n: vb[n] for n in b16n}
            T64 = {n: vb[n] for n in t64n}
        F["EH"] = F["TMP"]
        F["BA"] = F["SIG"]
        T64["YA"] = T64["LAK"]
        B["F"], B["Bf"], B["T64"] = F, Bf, T64
        B["R0"], B["YF"], B["YQ"], B["GT"] = F["SIG"], F["EX"], F["EN"], F["KH"]
        B["ST"] = p.sb(f"ST{S}", [128, 8, 4], F32)
        B["RKS"] = p.sb(f"RKS{S}", [128, 8], F32)
        B["Pf"] = p.sb(f"Pf{S}", [128, 64], F32)
        B["Pb"] = p.sb(f"Pb{S}", [128, 64], BF16)
        B["RR"] = p.sb(f"RR{S}", [128, 64], BF16)
        B["UB"] = p.sb(f"UB{S}", [128, 64], BF16)
        B["LNGB"] = LNGBs[S]
        B["PY"] = PS[4 + S]
        B["PT1"] = PS[6 + S]
        sets.append(B)
    b3 = lambda r_: Ref(r_.ap.rearrange("p (a b) -> p a b", a=8), r_.bufs)
    HS = (slice(0, 64), slice(64, 128))
    st = {"sci": 0}

    def newps():
        st["sci"] += 1
        return PS[st["sci"] % 4]

    def mm2(out_t, col0, ncol, lhs_fn, rhs_fn, start=True, stop=True):
        for hs in HS:
            e_mm(s, rf(out_t)[hs, col0:col0 + ncol], lhs_fn(hs), rhs_fn(hs), start, stop)

    def unit(hp, gq, B):
        F, Bf, T64, PA, w = B["F"], B["Bf"], B["T64"], B["PA"], B["WH"]
        R0, YF, YQ, GT, ST, RKS = B["R0"], B["YF"], B["YQ"], B["GT"], B["ST"], B["RKS"]
        Pf, Pb, RR, UB, LNGB = B["Pf"], B["Pb"], B["RR"], B["UB"], B["LNGB"]
        hc = lambda n: rf(PH)[:, hp, n:n + 1]
        tok = gq * 512
        for which, nm in enumerate(("r", "k", "v")):
            pp = newps()
            for kc in range(8):
                e_mm(s, rf(pp)[:, :], rf(w)[:, kc, which * 128:(which + 1) * 128],
                     Ref(HT[:, kc, tok:tok + 512], HT.b(gq)), kc == 0, kc == 7)
            pa = PA[which]
            if gq == 0:
                e_memset(s, "dve", rf(pa)[:, 0:1], 0.0)
            else:
                e_copy(s, "dve", rf(pa)[:, 0:1], rf(pa)[:, 512:513])
            e_copy(s, "act", rf(pa)[:, 1:513], rf(pp)[:, :])
            yield
            tmp_ = rf(F["TMP"])[:, :] if which != 1 else rf(F["CUM"])[:, :]
            e_act(s, tmp_, rf(pa)[:, 0:512], AF.Copy, scale=hc(which))
            dst_ = rf(Bf["VT"])[:, :] if nm == "v" else rf(F[nm])[:, :]
            e_stt(s, "dve", dst_, rf(pa)[:, 1:513], rf(OM)[:, hp, which:which + 1], tmp_, ALU.mult, ALU.add)
        r_, k_ = rf(F["r"])[:, :], rf(F["k"])[:, :]
        pz = newps()
        e_mm(s, rf(pz)[:, :], rf(W2)[:, hp * 128:(hp + 1) * 128], rf(TW)[:, tok:tok + 512])
        e_act(s, rf(F["SIG"])[:, :], rf(pz)[:, :], AF.Sigmoid, bias=hc(3))
        pz2 = newps()
        e_mm(s, rf(pz2)[:, :], rf(A2)[:, hp * 128:(hp + 1) * 128], rf(AL)[:, tok:tok + 512])
        e_act(s, rf(F["A"])[:, :], rf(pz2)[:, :], AF.Sigmoid, bias=hc(4))
        yield
        e_ts(s, "dve", rf(F["KKN"])[:, :], k_, hc(5), None, ALU.mult)
        e_act(s, rf(Bf["KK2"])[:, :], rf(F["KKN"])[:, :], AF.Square)
        yield
        pz = newps()
        e_mm(s, rf(pz)[:, :], rf(BLK)[:, :], rf(Bf["KK2"])[:, :])
        e_act(s, rf(F["TMP"])[:, :], rf(pz)[:, :], AF.Sqrt)
        yield
        e_ts(s, "dve", rf(F["TMP"])[:, :], rf(F["TMP"])[:, :], 1e-12, None, ALU.max)
        s.op("dve", lambda E: E.reciprocal(out=F["TMP"][:, :], in_=F["TMP"][:, :]), reads=F["TMP"].all(),
             writes=F["TMP"].all())
        e_tt(s, "dve", rf(F["KKN"])[:, :], rf(F["KKN"])[:, :], rf(F["TMP"])[:, :], ALU.mult)
        e_act(s, rf(F["KH"])[:, :], rf(F["A"])[:, :], AF.Identity, bias=rf(OM)[:, hp, 3:4], scale=hc(6))
        e_tt(s, "dve", rf(F["KH"])[:, :], rf(F["KH"])[:, :], k_, ALU.mult)
        yield
        s.op("dve", lambda E: E.tensor_tensor_scan(out=F["CUM"][:, :], data0=M512[:, :], data1=F["SIG"][:, :],
                                                   initial=0.0, op0=ALU.mult, op1=ALU.add),
             reads=M512.all() + F["SIG"].all(), writes=F["CUM"].all())
        yield
        cum = rf(F["CUM"])[:, :]
        e_act(s, rf(F["EC"])[:, :], cum, AF.Exp, scale=NEG_EXP_HALF)
        e_tt(s, "dve", rf(F["EX"])[:, :], cum, rf(F["SIG"])[:, :], ALU.subtract)
        e_act(s, rf(F["EN"])[:, :], cum, AF.Exp, scale=-NEG_EXP_HALF)
        cum3 = b3(cum)
        cend = Ref(cum3.ap[:, :, 63:64].to_broadcast([128, 8, 64]), cum.bufs)
        e_tt(s, "dve", b3(rf(F["EH"])[:, :]), cend, cum3, ALU.subtract)
        yield
        e_act(s, rf(F["EX"])[:, :], rf(F["EX"])[:, :], AF.Exp, scale=NEG_EXP_HALF)
        e_act(s, rf(F["EH"])[:, :], rf(F["EH"])[:, :], AF.Exp, scale=NEG_EXP_HALF)
        e_tt(s, "dve", rf(Bf["Rt"])[:, :], r_, rf(F["EC"])[:, :], ALU.mult)
        e_tt(s, "dve", rf(Bf["RK"])[:, :], r_, rf(F["KH"])[:, :], ALU.mult)
        e_tt(s, "dve", rf(F["BA"])[:, :], rf(F["KKN"])[:, :], rf(F["A"])[:, :], ALU.mult)
        yield
        e_stt(s, "dve", rf(Bf["At"])[:, :], rf(F["KKN"])[:, :], -1.0, rf(F["EX"])[:, :], ALU.mult, ALU.mult)
        e_tt(s, "dve", rf(Bf["Bt"])[:, :], rf(F["BA"])[:, :], rf(F["EN"])[:, :], ALU.mult)
        e_tt(s, "dve", rf(Bf["Bh"])[:, :], rf(F["BA"])[:, :], rf(F["EH"])[:, :], ALU.mult)
        e_tt(s, "dve", rf(Bf["Kt"])[:, :], rf(F["KH"])[:, :], rf(F["EN"])[:, :], ALU.mult)
        e_tt(s, "dve", rf(Bf["Kh"])[:, :], rf(F["KH"])[:, :], rf(F["EH"])[:, :], ALU.mult)
        yield
        blk = lambda n, c8, hs: rf(Bf[n])[hs, c8 * 64:(c8 + 1) * 64]
        tb_ = lambda n, c8, hs: rf(T64[n])[hs, c8 * 64:(c8 + 1) * 64]
        idh = lambda hs: rf(ident)[hs, hs]
        for src, dst in (("VT", "V64"), ("Bh", "BH64"), ("Kh", "KH64")):
            pt = newps()
            for c8 in range(8):
                mm2(pt, c8 * 64, 64, lambda hs, c8=c8, src=src: blk(src, c8, hs), idh)
            e_copy(s, "act", rf(T64[dst])[:, :], rf(pt)[:, :])
            yield
        for lh, rh, mask, dst in (("Bt", "At", MUS, "N"), ("At", "Bt", MLS, "Q"), ("Kt", "At", MUS, "LAK"),
                                  ("Bt", "Rt", MUI, "ARB"), ("Kt", "Rt", MUI, "ARK")):
            pt = newps()
            for c8 in range(8):
                mm2(pt, c8 * 64, 64, lambda hs, c8=c8, lh=lh: blk(lh, c8, hs), lambda hs, c8=c8, rh=rh: blk(rh, c8, hs))
            e_tt(s, "dve", rf(T64[dst])[:, :], rf(pt)[:, :], rf(mask)[:, :], ALU.mult)
            yield
        e_tt(s, "dve", rf(T64["XA"])[:, :], rf(T64["N"])[:, :], rf(ID8)[:, :], ALU.add)
        Pn, Qn, Pn2, Qn2 = "N", "Q", "N2", "Q2"
        for lvl in range(1, 6):
            pq = newps()
            for c8 in range(8):
                mm2(pq, c8 * 64, 64, lambda hs, c8=c8, Pn=Pn: tb_(Pn, c8, hs), lambda hs, c8=c8, Qn=Qn: tb_(Qn, c8, hs))
            e_copy(s, "act", rf(T64[Qn2])[:, :], rf(pq)[:, :])
            if lvl < 5:
                pp_ = newps()
                for c8 in range(8):
                    mm2(pp_, c8 * 64, 64, lambda hs, c8=c8, Qn=Qn: tb_(Qn, c8, hs),
                        lambda hs, c8=c8, Pn=Pn: tb_(Pn, c8, hs))
                e_copy(s, "act", rf(T64[Pn2])[:, :], rf(pp_)[:, :])
            yield
            px = newps()
            for c8 in range(8):
                mm2(px, c8 * 64, 64, lambda hs, c8=c8, Qn2=Qn2: tb_(Qn2, c8, hs), lambda hs, c8=c8: tb_("XA", c8, hs))
            e_tt(s, "dve", rf(T64["XA"])[:, :], rf(T64["XA"])[:, :], rf(px)[:, :], ALU.add)
            yield
            Pn, Pn2 = Pn2, Pn
            Qn, Qn2 = Qn2, Qn
        pr0 = newps()
        for c8 in range(8):
            mm2(pr0, c8 * 64, 64, lambda hs, c8=c8: tb_("LAK", c8, hs), lambda hs, c8=c8: tb_("V64", c8, hs))
        e_copy(s, "act", rf(R0)[:, :], rf(pr0)[:, :])
        pg = newps()
        e_mm(s, rf(pg)[:, :], rf(G2)[:, hp * 128:(hp + 1) * 128], rf(SGL)[:, tok:tok + 512])
        e_copy(s, "act", rf(GT)[:, :], rf(pg)[:, :])
        yield
        PY, PT1 = B["PY"], B["PT1"]
        pbh = lambda hs: rf(Pb)[hs, :]
        ubh = lambda hs: rf(UB)[hs, :]
        for c8 in range(8):
            mm2(PT1, 0, 64, lambda hs: blk("At", c8, hs), pbh)
            mm2(PY, c8 * 64, 64, lambda hs: blk("Rt", c8, hs), pbh, True, False)
            e_tt(s, "dve", rf(RR)[:, :], rf(PT1)[:, 0:64], rf(R0)[:, c8 * 64:(c8 + 1) * 64], ALU.add)
            yield
            mm2(PT1, 64, 64, lambda hs: tb_("XA", c8, hs), lambda hs: rf(RR)[hs, :])
            e_copy(s, "act", rf(UB)[:, :], rf(PT1)[:, 64:128])
            yield
            mm2(PT1, 128, 64, lambda hs: tb_("KH64", c8, hs), lambda hs: tb_("V64", c8, hs), True, False)
            mm2(PT1, 128, 64, lambda hs: tb_("BH64", c8, hs), ubh, False, True)
            mm2(PY, c8 * 64, 64, lambda hs: tb_("ARB", c8, hs), ubh, False, False)
            mm2(PY, c8 * 64, 64, lambda hs: tb_("ARK", c8, hs), lambda hs: tb_("V64", c8, hs), False, True)
            e_stt(s, "dve", rf(Pf)[:, :], rf(Pf)[:, :], rf(F["EC"])[:, c8 * 64 + 63:c8 * 64 + 64], rf(PT1)[:, 128:192],
                  ALU.mult, ALU.add)
            e_copy(s, "act", rf(Pb)[:, :], rf(Pf)[:, :])
            yield
        e_copy(s, "act", rf(YF)[:, :], rf(PY)[:, :])
        yf3 = b3(rf(YF)[:, :])
        yq3 = b3(rf(YQ)[:, :])
        prk = newps()
        for c8 in range(8):
            mm2(prk, c8, 1, lambda hs: blk("RK", c8, hs), lambda hs: rf(RKb)[hs, hp:hp + 1])
        e_copy(s, "act", rf(RKS)[:, :], rf(prk)[:, 0:8])
        yield
        s.op("dve", lambda E: E.tensor_reduce(out=ST[:, :, 0], in_=YF[:, :].rearrange("p (a b) -> p a b", a=8),
                                              axis=AX.X, op=ALU.add), reads=YF.all(), writes=ST.all())
        e_act(s, rf(YQ)[:, :], rf(YF)[:, :], AF.Square)
        yield
        s.op("dve", lambda E: E.tensor_reduce(out=ST[:, :, 1], in_=YQ[:, :].rearrange("p (a b) -> p a b", a=8),
                                              axis=AX.X, op=ALU.add), reads=YQ.all(), writes=ST.all())
        e_ts(s, "dve", rf(ST)[:, :, 2], rf(ST)[:, :, 0], 1.0 / 64, None, ALU.mult)
        e_tt(s, "dve", rf(ST)[:, :, 0], rf(ST)[:, :, 2], rf(ST)[:, :, 2], ALU.mult)
        e_stt(s, "dve", rf(ST)[:, :, 1], rf(ST)[:, :, 1], 1.0 / 64, rf(ST)[:, :, 0], ALU.mult, ALU.subtract)
        e_ts(s, "dve", rf(ST)[:, :, 1], rf(ST)[:, :, 1], GN_EPS, None, ALU.add)
        e_act(s, rf(ST)[:, :, 3], rf(ST)[:, :, 1], AF.Sqrt)
        yield
        s.op("dve", lambda E: E.reciprocal(out=ST[:, :, 3], in_=ST[:, :, 3]), reads=ST.all(), writes=ST.all())
        mean_b = Ref(ST[:, :, 2:3].to_broadcast([128, 8, 64]), ST.all())
        rstd_b = Ref(ST[:, :, 3:4].to_broadcast([128, 8, 64]), ST.all())
        e_tt(s, "dve", yf3, yf3, mean_b, ALU.subtract)
        e_tt(s, "dve", yf3, yf3, rstd_b, ALU.mult)
        rks_b = Ref(RKS[:, :].rearrange("p (a b) -> p a b", b=1).to_broadcast([128, 8, 64]), RKS.all())
        e_tt(s, "dve", yq3, b3(rf(T64["V64"])[:, :]), rks_b, ALU.mult)
        lng = Ref(LNGB[:, 0:1, :].to_broadcast([128, 8, 64]), LNGB.all())
        lnb = Ref(LNGB[:, 1:2, :].to_broadcast([128, 8, 64]), LNGB.all())
        yield
        e_tt(s, "dve", yf3, yf3, lng, ALU.mult)
        e_tt(s, "dve", yf3, yf3, lnb, ALU.add)
        yield
        e_tt(s, "dve", rf(T64["YA"])[:, :], rf(YF)[:, :], rf(YQ)[:, :], ALU.add)
        yield
        pt = newps()
        for c8 in range(8):
            mm2(pt, c8 * 64, 64, lambda hs: tb_("YA", c8, hs), idh)
        s.op("dve", lambda E: E.tensor_tensor(
            out=OT[:, hp, tok:tok + 512], in0=pt[:, :], in1=GT[:, :], op=ALU.mult),
            reads=pt.all() + GT.all(), writes=OT.all())
        yield

    def stream(S, hps):
        B = sets[S]
        for hp in hps:
            w = B["WH"]
            s.dma("pool", w[:, :, :], D["l0_w_hp"][hp].rearrange("p (kc n) -> p kc n", kc=8), writes=w.all())
            LNGB = B["LNGB"]
            for hh in range(2):
                h = 2 * hp + hh
                s.dma("sp", LNGB[HS[hh], 0, :], D["l0_lnx_g"][h * 64:(h + 1) * 64].partition_broadcast(64),
                      writes=LNGB.all())
                s.dma("sp", LNGB[HS[hh], 1, :], D["l0_lnx_b"][h * 64:(h + 1) * 64].partition_broadcast(64),
                      writes=LNGB.all())
            e_memset(s, "dve", rf(B["Pf"])[:, :], 0.0)
            e_memset(s, "dve", rf(B["Pb"])[:, :], 0.0)
            yield
            for gq in range(4):
                yield from unit(hp, gq, B)

    gens = [stream(0, (0, 2)), stream(1, (1, 3))]
    alive = [True, True]
    first = True
    while any(alive):
        for S in range(2):
            if alive[S]:
                try:
                    next(gens[S])
                except StopIteration:
                    alive[S] = False
            if first and S == 0:
                for _ in range(STAGGER):
                    next(gens[0])
                first = False
    s.barrier()
    for kc in range(8):
        s.dma("sp", XT[:, kc, :], spill[:, kc, :], reads=spill.all(), writes=XT.all())


GAIN_NAMES = ["l0_ffn1_pre_g", "l0_ffn1_post_g", "l0_mix_pre_g", "l0_mix_post_g", "l0_ffn2_pre_g", "l0_ffn2_post_g",
              "l1_ffn1_pre_g", "l1_ffn1_post_g", "l1_mix_pre_g", "l1_mix_post_g", "l1_ffn2_pre_g", "l1_ffn2_post_g"]
HALF_GAINS = [1, 5, 7, 11]


def build_program(stages=("f01", "m0", "f02f11", "m1", "f12"), dbg=False):
    k = KB()
    nc = k.nc
    s = k.s
    xT_d = k.dram_in("xT", [DM, SEQ])
    gains_d = k.dram_in("gains", [128, 12 * 8])
    ffn_d = {}
    for nm in ("l0_ffn1", "l0_ffn2", "l1_ffn1", "l1_ffn2"):
        ffn_d[nm] = (k.dram_in(nm + "_w_in", [NJ, 128, 2048]), k.dram_in(nm + "_w_out", [2, 8, 128, 11 * 128]))
    D = {}
    for nm, shp in (("l0_pl", [128, 32]), ("l0_ph", [128, 32]), ("l0_mul", [128, 3]), ("l0_lnx_g", [512]),
                    ("l0_lnx_b", [512]), ("l0_w2", [64, 512]), ("l0_a2", [64, 512]), ("l0_g2", [128, 512]),
                    ("l0_gate_a_w", [8, 64, 64]), ("l0_gate_x_w", [8, 64, 64]), ("l0_w_lru", [4, 128, 8 * 256]),
                    ("l0_w_lora", [128, 8 * 256]), ("l0_w_hp", [4, 128, 8 * 384]), ("l0_w_out", [DM, DM])):
        D[nm] = k.dram_in(nm, shp)
    D["xt_spill"] = T("xt_spill", nc.dram_tensor("xt_spill", [128, 8, SEQ], F32, kind="Internal").ap())
    if dbg:
        D["dbg_OT"] = k.dram_out("dbg_OT", [128, 8, SEQ], BF16)
    wqkv_d = k.dram_in("l1_w_qkv", [8, 128, 8 * 384])
    l1_wo_d = k.dram_in("l1_w_out", [DM, DM])
    outT_d = k.dram_out("outT", [DM, SEQ])

    with k.es:
        XT = k.sb("XT", [128, 8, SEQ], F32, parts=4)
        C = {}
        C["gains"] = k.sb("gains", [128, 12, 8], F32)
        C["ones_m"] = k.sb("ones_m", [128, 128], BF16)
        PS = [k.ps(f"ps{i}", [128, 512]) for i in range(8)]
        C["HTG"] = k.sb("HTG", [128, 8, SEQ], BF16, parts=4)
        C["ht_ready"] = None

        s.op("dve", lambda e: e.memset(C["ones_m"][:, :], 1.0 / DM), writes=C["ones_m"].all())
        C["one_f"] = k.sb("one_f", [128, 1], F32)
        C["ones_col"] = k.sb("ones_col", [128, 1], BF16)
        C["ones_bf"] = k.sb("ones_bf", [128, 512], BF16)
        C["ident"] = k.sb("ident", [128, 128], BF16)
        C["ntri"] = k.sb("ntri", [128, 128], BF16)
        s.op("dve", lambda e: e.memset(C["one_f"][:, :], 1.0), writes=C["one_f"].all())
        s.op("dve", lambda e: e.memset(C["ones_col"][:, :], 1.0), writes=C["ones_col"].all())
        s.op("dve", lambda e: e.memset(C["ones_bf"][:, :], 1.0), writes=C["ones_bf"].all())
        s.op("pool", lambda e: e.affine_select(out=C["ident"][:, :], in_=C["ones_bf"][:, 0:128], pattern=[[-1, 128]],
                                               compare_op=ALU.is_equal, fill=0.0, base=0, channel_multiplier=1),
             reads=C["ones_bf"].all(), writes=C["ident"].all())
        s.op("pool", lambda e: e.affine_select(out=C["ntri"][:, :], in_=C["ones_bf"][:, 0:128], pattern=[[-1, 128]],
                                               compare_op=ALU.is_ge, fill=0.0, base=0, channel_multiplier=1),
             reads=C["ones_bf"].all(), writes=C["ntri"].all())
        s.op("dve", lambda e: e.tensor_scalar(out=C["ntri"][:, :], in0=C["ntri"][:, :], scalar1=-1.0, scalar2=None,
                                              op0=ALU.mult),
             reads=C["ntri"].all(), writes=C["ntri"].all())
        C["eps"] = k.sb("eps", [128, 1], F32)
        s.op("dve", lambda e: e.memset(C["eps"][:, :], NORM_EPS), writes=C["eps"].all())
        s.dma("sp", C["gains"][:, :, :], gains_d[:, :].rearrange("p (n c) -> p n c", n=12), writes=C["gains"].all())
        for gi in HALF_GAINS:
            s.op("dve", lambda e, gi=gi: e.tensor_scalar(out=C["gains"][:, gi, :], in0=C["gains"][:, gi, :],
                                                         scalar1=0.5, scalar2=None, op0=ALU.mult),
                 reads=C["gains"].all(), writes=C["gains"].all())
        for tb in range(4):
            for kc in range(8):
                s.dma("sp", XT[:, kc, tb * 512:(tb + 1) * 512], xT_d[kc * 128:(kc + 1) * 128, tb * 512:(tb + 1) * 512],
                      writes=XT.b(tb))

        PRE_GI = {"f01": 0, "m0": 2, "f02": 4, "f02f11": 4, "f11": 6, "m1": 8, "f12": 10}
        for si, st in enumerate(stages):
            nxt = PRE_GI[stages[si + 1]] if si + 1 < len(stages) else None
            if st == "f01":
                ffn_stage(k, C, XT, PS, [(*ffn_d["l0_ffn1"], 0, 1)], nxt)
            elif st == "f02":
                ffn_stage(k, C, XT, PS, [(*ffn_d["l0_ffn2"], 4, 5)], nxt)
            elif st == "f02f11":
                ffn_stage(k, C, XT, PS, [(*ffn_d["l0_ffn2"], 4, 5), (*ffn_d["l1_ffn1"], 6, 7)], nxt)
            elif st == "f11":
                ffn_stage(k, C, XT, PS, [(*ffn_d["l1_ffn1"], 6, 7)], nxt)
            elif st == "m0":
                mixer0_stage(k, C, XT, PS, D, 2, 3, nxt)
            elif st == "m1":
                if dbg:
                    C["dbg_OT"] = D["dbg_OT"]
                attn_stage(k, C, XT, PS, wqkv_d, l1_wo_d, 8, 9, nxt)
            elif st == "f12":
                ffn_stage(k, C, XT, PS, [(*ffn_d["l1_ffn2"], 10, 11)], nxt)

        for tb in range(4):
            for kc in range(8):
                s.dma("sp", outT_d[kc * 128:(kc + 1) * 128, tb * 512:(tb + 1) * 512], XT[:, kc, tb * 512:(tb + 1) * 512],
                      reads=XT.b(tb), writes=outT_d.all())
        s.barrier(engines=["sp"])
    return nc


def _col(v):
    return np.ascontiguousarray(np.asarray(v, np.float32).reshape(8, 128).T)


def prep_shared(inp):
    d = {}
    d["gains"] = np.ascontiguousarray(np.concatenate([_col(inp[n]) for n in GAIN_NAMES], axis=1))
    for nm in ("l0_ffn1", "l0_ffn2", "l1_ffn1", "l1_ffn2"):
        w_in = np.asarray(inp[nm + "_w_in"], np.float32)
        w_out = np.asarray(inp[nm + "_w_out"], np.float32)
        g = w_in[:, :DFF].reshape(8, 128, NJ, 128)
        u = w_in[:, DFF:].reshape(8, 128, NJ, 128)
        gu = np.concatenate([g, u], axis=3)
        d[nm + "_w_in"] = np.ascontiguousarray(gu.transpose(2, 1, 0, 3).reshape(NJ, 128, 2048))
        wo = w_out.reshape(2, 11, 128, 8, 128)
        d[nm + "_w_out"] = np.ascontiguousarray(wo.transpose(0, 3, 2, 1, 4).reshape(2, 8, 128, 11 * 128))
    f = lambda n: np.asarray(inp[n], np.float32)
    cw = f("l0_conv_w")
    pl = np.stack([cw[0], cw[1], cw[2], cw[3], f("l0_conv_b"), f("l0_gate_a_b"), f("l0_gate_x_b"), f("l0_lambda")], axis=1)
    d["l0_pl"] = np.ascontiguousarray(pl.reshape(4, 128, 8).transpose(1, 0, 2).reshape(128, 32))
    mu = f("l0_mu")
    ph = np.stack([mu[0:512], mu[512:1024], mu[1024:1536], f("l0_w0"), f("l0_a0"), f("l0_k_k"), f("l0_k_a"),
                   f("l0_r_k").reshape(512)], axis=1)
    d["l0_ph"] = np.ascontiguousarray(ph.reshape(4, 128, 8).transpose(1, 0, 2).reshape(128, 32))
    mul = np.zeros((128, 3), np.float32)
    mul[0:64, 0] = mu[1536:1600]
    mul[0:64, 1] = mu[1600:1664]
    mul[:, 2] = mu[1664:1792]
    d["l0_mul"] = mul
    for n in ("l0_lnx_g", "l0_lnx_b", "l0_w2", "l0_a2", "l0_g2", "l0_gate_a_w", "l0_gate_x_w", "l0_w_out"):
        d[n] = np.ascontiguousarray(f(n))
    wi = f("l0_w_in").reshape(8, 128, 2816)
    lru = np.concatenate([wi[:, :, 1792:2304].reshape(8, 128, 4, 128), wi[:, :, 2304:2816].reshape(8, 128, 4, 128)], axis=3)
    d["l0_w_lru"] = np.ascontiguousarray(lru.transpose(2, 1, 0, 3).reshape(4, 128, 8 * 256))
    d["l0_w_lora"] = np.ascontiguousarray(wi[:, :, 1536:1792].transpose(1, 0, 2).reshape(128, 8 * 256))
    hd = np.stack([wi[:, :, 0:512].reshape(8, 128, 4, 128), wi[:, :, 512:1024].reshape(8, 128, 4, 128),
                   wi[:, :, 1024:1536].reshape(8, 128, 4, 128)], axis=3)
    d["l0_w_hp"] = np.ascontiguousarray(hd.transpose(2, 1, 0, 3, 4).reshape(4, 128, 8 * 384))
    wq = np.asarray(inp["l1_w_qkv"], np.float32).reshape(8, 128, 3, 8, 128)
    d["l1_w_qkv"] = np.ascontiguousarray(wq.transpose(3, 1, 0, 2, 4).reshape(8, 128, 8 * 384))
    d["l1_w_out"] = np.ascontiguousarray(np.asarray(inp["l1_w_out"], np.float32))
    return d


_CACHE = {}


def kernel(**inputs):
    x = np.asarray(inputs["x"], np.float32)
    shared = prep_shared(inputs)
    if "nc" not in _CACHE:
        _CACHE["nc"] = build_program()
    nc = _CACHE["nc"]
    in_maps = []
    for c in range(N_CORES):
        m = dict(shared)
        m["xT"] = np.ascontiguousarray(x[c].T)
        in_maps.append(m)
    res = run_bass_kernel_spmd(nc, in_maps, core_ids=list(range(N_CORES)))
    out = np.stack([np.ascontiguousarray(res.results[c]["outT"].T) for c in range(N_CORES)], axis=0)
    return out.astype(np.float32)
```

```python
import math
from contextlib import ExitStack

import numpy as np
import concourse.bass as bass
import concourse.mybir as mybir
from concourse.bass_utils import run_bass_kernel_spmd

F32 = mybir.dt.float32
BF16 = mybir.dt.bfloat16
AF = mybir.ActivationFunctionType
ALU = mybir.AluOpType

SEQ = 2048
DM = 1024
DFF = 2816
NJ = 22
NORM_EPS = 1e-6
N_CORES = 8


class Buf:
    __slots__ = ("name", "w", "r")

    def __init__(self, name):
        self.name = name
        self.w = None
        self.r = {}


class T:
    def __init__(self, name, t, parts=1):
        self.name = name
        self.t = t
        self.bufs = [Buf(f"{name}.{i}") for i in range(parts)]

    def b(self, *idx):
        return [self.bufs[i] for i in idx]

    def all(self):
        return list(self.bufs)

    def __getitem__(self, key):
        return self.t[key]


class Sched:
    COMPUTE = ("pe", "act", "dve", "pool")

    def __init__(self, nc, es, n_dma_ch=20):
        self.nc = nc
        self.eng = {"pe": nc.tensor, "act": nc.scalar, "dve": nc.vector, "pool": nc.gpsimd, "sp": nc.sync}
        self.sems = {}
        self.cnt = {}
        for e in self.COMPUTE:
            self.sems[e] = es.enter_context(nc.semaphore(f"s_{e}"))
            self.cnt[e] = 0
        self.ch = {}
        self.ch_next = {}
        for q in ("sp", "pool", "act"):
            n = n_dma_ch if q != "act" else 4
            lst = []
            for i in range(n):
                key = f"d_{q}{i}"
                self.sems[key] = es.enter_context(nc.semaphore(key))
                self.cnt[key] = 0
                lst.append(key)
            self.ch[q] = lst
            self.ch_next[q] = 0
        self.seen = {e: {} for e in self.eng}
        self.n_wait = 0
        self.n_ins = 0

    def _wait(self, e, ev):
        key, val = ev
        if val <= 0:
            return
        if self.seen[e].get(key, 0) >= val:
            return
        self.seen[e][key] = val
        self.eng[e].wait_ge(self.sems[key], val)
        self.n_wait += 1

    def _deps(self, e, reads, writes):
        evs = {}

        def need(ev):
            if ev is None:
                return
            k_, v_ = ev
            if e == "pe" and k_ == "pe":
                return
            if evs.get(k_, 0) < v_:
                evs[k_] = v_

        for b in reads:
            need(b.w)
        for b in writes:
            need(b.w)
            for kv in b.r.items():
                need(kv)
        return evs

    def op(self, e, fn, reads=(), writes=()):
        evs = self._deps(e, reads, writes)
        for ev in evs.items():
            self._wait(e, ev)
        ins = fn(self.eng[e])
        self.cnt[e] += 1
        ev = (e, self.cnt[e])
        ins.then_inc(self.sems[e], 1)
        self.seen[e][e] = max(self.seen[e].get(e, 0), 0)
        for b in writes:
            b.w = ev
            b.r = {}
        for b in reads:
            if b.w is not ev:
                b.r[e] = self.cnt[e]
        self.n_ins += 1
        return ins

    def dma(self, q, out, in_, reads=(), writes=()):
        e = q
        evs = self._deps(e, reads, writes)
        key = self.ch[q][self.ch_next[q]]
        self.ch_next[q] = (self.ch_next[q] + 1) % len(self.ch[q])
        if evs.get(key, 0) < self.cnt[key]:
            evs[key] = self.cnt[key]
        for ev in evs.items():
            self._wait(e, ev)
        ins = self.eng[e].dma_start(out=out, in_=in_)
        self.cnt[key] += 16
        ins.then_inc(self.sems[key], 16)
        ev = (key, self.cnt[key])
        for b in writes:
            b.w = ev
            b.r = {}
        for b in reads:
            b.r[key] = self.cnt[key]
        self.n_ins += 1
        return ins

    def barrier(self, engines=None):
        evs = [(k_, v_) for k_, v_ in self.cnt.items() if v_ > 0]
        for e in (engines or self.eng):
            for ev in evs:
                if ev[0] == e:
                    continue
                self._wait(e, ev)


class Phase:
    def __init__(self, k):
        self.k = k
        self.es = ExitStack()

    def __enter__(self):
        self.es.__enter__()
        return self

    def __exit__(self, *a):
        self.k.s.barrier()
        return self.es.__exit__(*a)

    def sb(self, name, shape, dtype, parts=1):
        self.k.uid += 1
        t = self.es.enter_context(self.k.nc.sbuf_tensor(f"ph_{name}_{self.k.uid}", shape, dtype))
        return T(name, t, parts)


class KB:
    def __init__(self):
        self.nc = bass.Bass("TRN2", target_bir_lowering=False)
        self.es = ExitStack()
        self.s = Sched(self.nc, self.es)
        self.uid = 0

    def sb(self, name, shape, dtype, parts=1):
        t = self.es.enter_context(self.nc.sbuf_tensor("sb_" + name, shape, dtype))
        return T(name, t, parts)

    def ps(self, name, shape, dtype=F32, parts=1):
        t = self.es.enter_context(self.nc.psum_tensor("pp_" + name, shape, dtype))
        return T(name, t, parts)

    def dram_in(self, name, shape, dtype=F32):
        return T(name, self.nc.dram_tensor(name, list(shape), dtype, kind="ExternalInput").ap())

    def dram_out(self, name, shape, dtype=F32):
        return T(name, self.nc.dram_tensor(name, list(shape), dtype, kind="ExternalOutput").ap())

    def phase(self):
        return Phase(self)


class Bg:
    def __init__(self):
        self.q = []

    def add(self, gen, period=2):
        self.q.append([gen, period, period])

    def tick(self):
        for item in list(self.q):
            item[2] -= 1
            if item[2] <= 0:
                item[2] = item[1]
                try:
                    next(item[0])
                except StopIteration:
                    self.q.remove(item)

    def drain(self):
        while self.q:
            for item in list(self.q):
                try:
                    next(item[0])
                except StopIteration:
                    self.q.remove(item)


def rms_rstd_gen(k, C, src, src_bufs, SQ, PST, RSTD, ntok, fuse_sq=False):
    s = k.s
    s.op("act", lambda e: e.activation(out=SQ[:, :, 0:ntok], in_=src, func=AF.Square),
         reads=src_bufs, writes=SQ.all())
    if not fuse_sq:
        yield
    for kc in range(8):
        s.op("pe", lambda e, kc=kc: e.matmul(PST[:, 0:ntok], lhsT=C["ones_m"][:, :], rhs=SQ[:, kc, 0:ntok],
                                             start=(kc == 0), stop=(kc == 7)),
             reads=SQ.all() + C["ones_m"].all(), writes=PST.all())
    yield
    s.op("act", lambda e: e.activation(out=RSTD[:, 0:ntok], in_=PST[:, 0:ntok], func=AF.Ln, bias=C["eps"][:, 0:1]),
         reads=PST.all() + C["eps"].all(), writes=RSTD.all())
    yield
    s.op("act", lambda e: e.activation(out=RSTD[:, 0:ntok], in_=RSTD[:, 0:ntok], func=AF.Exp, scale=-0.5),
         reads=RSTD.all(), writes=RSTD.all())


def ffn_stage(k, C, XT, PS, ffns, next_gi=None):
    s = k.s
    G_ = C["gains"]
    with k.phase() as ph:
        HTG = C["HTG"]
        HTs = []
        for i in range(2):
            hv = T(f"HTv{i}", HTG.t[:, :, i * 1024:(i + 1) * 1024])
            hv.bufs = HTG.bufs[2 * i:2 * i + 2]
            HTs.append(hv)
        ACTT = ph.sb("ACTT", [128, 11, 1024], BF16, parts=22)
        YT = ph.sb("YT", [128, 8, 1024], F32, parts=16)
        SQ = [ph.sb(f"SQ{i}", [128, 8, 512], BF16) for i in range(2)]
        RSTD = [ph.sb(f"RSTD{i}", [128, 512], F32) for i in range(2)]
        WIN = [ph.sb(f"WIN{i}", [128, 8, 256], BF16) for i in range(3)]
        WOUT = [ph.sb(f"WOUT{i}", [128, 11, 128], BF16) for i in range(3)]
        SG = [ph.sb(f"SG{i}", [128, 512], F32) for i in range(2)]
        PG = [PS[0], PS[1]]
        PU = [PS[2], PS[3]]
        PY = [PS[4], PS[5]]
        PST = [PS[6], PS[7]]
        st = {"win": 0, "wout": 0, "pi": 0, "ni": 0}
        jobs = [(f, B) for f in range(len(ffns)) for B in range(2)]

        bg = Bg()

        def prenorm(ji):
            f, B = jobs[ji]
            HT = HTs[ji % 2]
            gi_pre = ffns[f][2]
            for sb_ in range(2):
                tok = B * 1024 + sb_ * 512
                xb = XT.b(B * 2 + sb_)
                n_ = st["ni"] % 2
                st["ni"] += 1
                yield from rms_rstd_gen(k, C, XT[:, :, tok:tok + 512], xb, SQ[n_], PST[n_], RSTD[n_], 512)
                for kc in range(8):
                    if kc == 4:
                        yield
                    s.op("dve", lambda e, kc=kc, tok=tok, sb_=sb_, n_=n_: e.scalar_tensor_tensor(
                        out=HT[:, kc, sb_ * 512:(sb_ + 1) * 512], in0=XT[:, kc, tok:tok + 512],
                        scalar=G_[:, gi_pre, kc:kc + 1], in1=RSTD[n_][:, :], op0=ALU.mult, op1=ALU.mult),
                        reads=xb + RSTD[n_].all() + G_.all(), writes=HT.b(sb_))

        def postnorm(ji):
            f, B = jobs[ji]
            gi_post = ffns[f][3]
            for sb_ in range(2):
                tok = B * 1024 + sb_ * 512
                rhs_sl = slice(sb_ * 512, (sb_ + 1) * 512)
                ybs = YT.b(*[dc * 2 + sb_ for dc in range(8)])
                xb = XT.b(B * 2 + sb_)
                n_ = st["ni"] % 2
                st["ni"] += 1
                yield from rms_rstd_gen(k, C, YT[:, :, rhs_sl], ybs, SQ[n_], PST[n_], RSTD[n_], 512)
                for dc in range(8):
                    if dc % 2 == 0 and dc > 0:
                        yield
                    s.op("dve", lambda e, dc=dc, rhs_sl=rhs_sl, n_=n_: e.scalar_tensor_tensor(
                        out=YT[:, dc, rhs_sl], in0=YT[:, dc, rhs_sl], scalar=G_[:, gi_post, dc:dc + 1],
                        in1=RSTD[n_][:, :], op0=ALU.mult, op1=ALU.mult),
                        reads=YT.b(dc * 2 + sb_) + RSTD[n_].all() + G_.all(), writes=YT.b(dc * 2 + sb_))
                    s.op("dve", lambda e, dc=dc, rhs_sl=rhs_sl, tok=tok: e.tensor_tensor(
                        out=XT[:, dc, tok:tok + 512], in0=XT[:, dc, tok:tok + 512], in1=YT[:, dc, rhs_sl], op=ALU.add),
                        reads=YT.b(dc * 2 + sb_) + xb, writes=xb)

        def up(ji, G, after_first=None):
            f, B = jobs[ji]
            HT = HTs[ji % 2]
            w_in_d = ffns[f][0]
            for jj in range(11):
                j = G * 11 + jj
                W = WIN[st["win"] % 3]
                st["win"] += 1
                s.dma("pool", W[:, :, :], w_in_d[j].rearrange("p (kc c) -> p kc c", kc=8), writes=W.all())
                for sb_ in range(2):
                    pg, pu, sg = PG[st["pi"] % 2], PU[st["pi"] % 2], SG[st["pi"] % 2]
                    st["pi"] += 1
                    rhs_sl = slice(sb_ * 512, (sb_ + 1) * 512)
                    for kc in range(8):
                        s.op("pe", lambda e, kc=kc, pg=pg, W=W, rhs_sl=rhs_sl: e.matmul(
                            pg[:, :], lhsT=W[:, kc, 0:128], rhs=HT[:, kc, rhs_sl], start=(kc == 0), stop=(kc == 7)),
                            reads=W.all() + HT.b(sb_), writes=pg.all())
                    for kc in range(8):
                        s.op("pe", lambda e, kc=kc, pu=pu, W=W, rhs_sl=rhs_sl: e.matmul(
                            pu[:, :], lhsT=W[:, kc, 128:256], rhs=HT[:, kc, rhs_sl], start=(kc == 0), stop=(kc == 7)),
                            reads=W.all() + HT.b(sb_), writes=pu.all())
                    s.op("act", lambda e, pg=pg, sg=sg: e.activation(out=sg[:, :], in_=pg[:, :], func=AF.Silu),
                         reads=pg.all(), writes=sg.all())
                    s.op("dve", lambda e, pu=pu, sg=sg, jj=jj, rhs_sl=rhs_sl: e.tensor_tensor(
                        out=ACTT[:, jj, rhs_sl], in0=sg[:, :], in1=pu[:, :], op=ALU.mult),
                        reads=sg.all() + pu.all(), writes=ACTT.b(jj * 2 + sb_))
                    bg.tick()
                if jj == 0 and after_first is not None:
                    after_first()

        def down(ji, G):
            f, B = jobs[ji]
            w_out_d = ffns[f][1]
            for dc in range(8):
                W = WOUT[st["wout"] % 3]
                st["wout"] += 1
                s.dma("pool", W[:, :, :], w_out_d[G, dc].rearrange("p (jj c) -> p jj c", jj=11), writes=W.all())
                for sb_ in range(2):
                    py = PY[st["pi"] % 2]
                    st["pi"] += 1
                    rhs_sl = slice(sb_ * 512, (sb_ + 1) * 512)
                    for jj in range(11):
                        s.op("pe", lambda e, jj=jj, py=py, W=W, rhs_sl=rhs_sl: e.matmul(
                            py[:, :], lhsT=W[:, jj, :], rhs=ACTT[:, jj, rhs_sl], start=(jj == 0), stop=(jj == 10)),
                            reads=W.all() + ACTT.b(jj * 2 + sb_), writes=py.all())
                    yb = YT.b(dc * 2 + sb_)
                    if G == 0:
                        s.op("act", lambda e, py=py, dc=dc, rhs_sl=rhs_sl: e.activation(
                            out=YT[:, dc, rhs_sl], in_=py[:, :], func=AF.Copy),
                            reads=py.all(), writes=yb)
                    else:
                        s.op("dve", lambda e, py=py, dc=dc, rhs_sl=rhs_sl: e.tensor_tensor(
                            out=YT[:, dc, rhs_sl], in0=YT[:, dc, rhs_sl], in1=py[:, :], op=ALU.add),
                            reads=py.all() + yb, writes=yb)
                    bg.tick()

        def next_prenorm():
            HT = HTs[0]
            for sb_ in range(2):
                tok = sb_ * 512
                xb = XT.b(sb_)
                n_ = st["ni"] % 2
                st["ni"] += 1
                yield from rms_rstd_gen(k, C, XT[:, :, tok:tok + 512], xb, SQ[n_], PST[n_], RSTD[n_], 512)
                for kc in range(8):
                    if kc == 4:
                        yield
                    s.op("dve", lambda e, kc=kc, tok=tok, sb_=sb_, n_=n_: e.scalar_tensor_tensor(
                        out=HT[:, kc, sb_ * 512:(sb_ + 1) * 512], in0=XT[:, kc, tok:tok + 512],
                        scalar=G_[:, next_gi, kc:kc + 1], in1=RSTD[n_][:, :], op0=ALU.mult, op1=ALU.mult),
                        reads=xb + RSTD[n_].all() + G_.all(), writes=HT.b(sb_))

        n = len(jobs)
        assert n % 2 == 0
        if C["ht_ready"] is not None and C["ht_ready"] == (ffns[0][2], (0, 1)):
            pass
        else:
            bg.add(prenorm(0))
            bg.drain()
        C["ht_ready"] = None
        for ji in range(n):
            up(ji, 0, after_first=(lambda ji=ji: bg.add(postnorm(ji - 1), 1)) if ji > 0 else None)
            bg.drain()
            down(ji, 0)
            if ji + 1 < n:
                bg.add(prenorm(ji + 1), 1)
            elif next_gi is not None:
                bg.add(next_prenorm(), 1)
                C["ht_ready"] = (next_gi, (0, 1))
            up(ji, 1)
            bg.drain()
            down(ji, 1)
        bg.add(postnorm(n - 1))
        bg.drain()


def prenorm_to_HT(k, C, ph, XT, HT, PS, gi_pre, col_off=0):
    s = k.s
    G_ = C["gains"]
    with k.phase() as p2:
        SQ = [p2.sb(f"SQ{i}", [128, 8, 512], BF16) for i in range(2)]
        RSTD = [p2.sb(f"RSTD{i}", [128, 512], F32) for i in range(2)]
        bg = Bg()

        def chain(tb):
            tok = tb * 512
            xb = XT.b(tb)
            yield from rms_rstd_gen(k, C, XT[:, :, tok:tok + 512], xb, SQ[tb % 2], PS[6 + tb % 2], RSTD[tb % 2], 512)
            for kc in range(8):
                if kc == 4:
                    yield
                s.op("dve", lambda e, kc=kc: e.scalar_tensor_tensor(
                    out=HT[:, kc, col_off + tok:col_off + tok + 512], in0=XT[:, kc, tok:tok + 512],
                    scalar=G_[:, gi_pre, kc:kc + 1], in1=RSTD[tb % 2][:, :], op0=ALU.mult, op1=ALU.mult),
                    reads=xb + RSTD[tb % 2].all() + G_.all(), writes=HT.b(tb))

        skip = ()
        if C["ht_ready"] is not None and C["ht_ready"][0] == gi_pre and col_off == 0:
            skip = C["ht_ready"][1]
        C["ht_ready"] = None
        for tb in range(4):
            if tb in skip:
                continue
            bg.add(chain(tb), 1)
            bg.tick()
            bg.tick()
        bg.drain()


def outproj_postnorm(k, C, XT, PS, OT, wo_d, gi_post, next_gi=None, WO=None):
    s = k.s
    G_ = C["gains"]
    with k.phase() as p3:
        if WO is None:
            WO = T("WOv", C["HTG"].t[:, :, 1024:2048])
            WO.bufs = C["HTG"].bufs[2:4]
            for kc in range(8):
                s.dma("pool", WO[:, kc, :], wo_d[kc * 128:(kc + 1) * 128, :], writes=WO.all())
        YTs = [p3.sb(f"YT{i}", [128, 8, 512], F32, parts=8) for i in range(2)]
        SQ1 = p3.sb("SQ", [128, 8, 512], BF16)
        RSTD = [p3.sb(f"RSTD{i}", [128, 512], F32) for i in range(2)]
        bg = Bg()

        def chain(tb):
            tok = tb * 512
            YT = YTs[tb % 2]
            yield from rms_rstd_gen(k, C, YT[:, :, :], YT.all(), SQ1, PS[6 + tb % 2], RSTD[tb % 2], 512, fuse_sq=True)
            xb = XT.b(tb)
            for dc in range(8):
                if dc % 2 == 0 and dc > 0:
                    yield
                s.op("dve", lambda e, dc=dc: e.scalar_tensor_tensor(
                    out=YT[:, dc, :], in0=YT[:, dc, :], scalar=G_[:, gi_post, dc:dc + 1],
                    in1=RSTD[tb % 2][:, :], op0=ALU.mult, op1=ALU.mult),
                    reads=YT.b(dc) + RSTD[tb % 2].all() + G_.all(), writes=YT.b(dc))
                s.op("dve", lambda e, dc=dc: e.tensor_tensor(
                    out=XT[:, dc, tok:tok + 512], in0=XT[:, dc, tok:tok + 512], in1=YT[:, dc, :], op=ALU.add),
                    reads=YT.b(dc) + xb, writes=xb)

        if next_gi is not None:
            RSTDn = p3.sb("RSTDn", [128, 512], F32)
        HTG = C["HTG"]

        def next_prenorm():
            for sb_ in range(2):
                tok = sb_ * 512
                xb = XT.b(sb_)
                yield from rms_rstd_gen(k, C, XT[:, :, tok:tok + 512], xb, SQ1, PS[0], RSTDn, 512, fuse_sq=True)
                for kc in range(8):
                    if kc == 4:
                        yield
                    s.op("dve", lambda e, kc=kc, tok=tok: e.scalar_tensor_tensor(
                        out=HTG[:, kc, tok:tok + 512], in0=XT[:, kc, tok:tok + 512],
                        scalar=G_[:, next_gi, kc:kc + 1], in1=RSTDn[:, :], op0=ALU.mult, op1=ALU.mult),
                        reads=xb + RSTDn.all() + G_.all(), writes=HTG.b(sb_))

        pi = 0
        for tb in range(4):
            tok = tb * 512
            YT = YTs[tb % 2]
            if tb == 3 and next_gi is not None:
                bg.add(next_prenorm(), 1)
                C["ht_ready"] = (next_gi, (0, 1))
            for dc in range(8):
                pp = PS[4 + pi % 2]
                pi += 1
                for kc in range(8):
                    s.op("pe", lambda e, kc=kc, dc=dc, pp=pp, tok=tok: e.matmul(
                        pp[:, :], lhsT=WO[:, kc, dc * 128:(dc + 1) * 128], rhs=OT[:, kc, tok:tok + 512],
                        start=(kc == 0), stop=(kc == 7)),
                        reads=WO.all() + OT.all(), writes=pp.all())
                s.op("act", lambda e, dc=dc, pp=pp, YT=YT: e.activation(out=YT[:, dc, :], in_=pp[:, :], func=AF.Copy),
                     reads=pp.all(), writes=YT.b(dc))
                bg.tick()
            bg.drain()
            bg.add(chain(tb), 1)
        bg.drain()


def attn_stage(k, C, XT, PS, wqkv_d, wo_d, gi_pre, gi_post, next_gi=None):
    s = k.s
    with k.phase() as ph:
        OT = ph.sb("OT", [128, 8, SEQ], BF16, parts=1)
        with k.phase() as pab:
            HT = C["HTG"]
            prenorm_to_HT(k, C, pab, XT, HT, PS, gi_pre)
            with k.phase() as pb:
                NEGM = pb.sb("negm", [128, 4, 512], BF16)
                ZB = pb.sb("zb", [128, 512], BF16)
                s.op("dve", lambda e: e.memset(ZB[:, :], 0.0), writes=ZB.all())
                for d in range(4):
                    s.op("pool", lambda e, d=d: e.affine_select(
                        out=NEGM[:, d, :], in_=ZB[:, :], pattern=[[1, 512]], compare_op=ALU.is_gt,
                        fill=-30000.0, base=-128 * d, channel_multiplier=-1),
                        reads=ZB.all(), writes=NEGM.all())
                QT = [pb.sb(f"QT{i}", [128, SEQ], BF16) for i in range(2)]
                KT = [pb.sb(f"KT{i}", [128, SEQ], BF16) for i in range(2)]
                V = [pb.sb(f"V{i}", [128, 16, 128], BF16) for i in range(2)]
                W = [pb.sb(f"WQKV{i}", [128, 8, 384], BF16) for i in range(1)]
                OTOK = [pb.sb(f"OTOK{i}", [128, 16, 128], BF16) for i in range(2)]
                E = [pb.sb(f"E{i}", [128, 512], F32) for i in range(3)]
                SP = [pb.sb(f"SP{i}", [128, 512], BF16) for i in range(5)]
                ATT = [pb.sb(f"ATT{i}", [128, 512], BF16) for i in range(3)]
                OACC = [pb.sb(f"OACC{i}", [128, 4, 64], F32) for i in range(2)]
                CACC2 = pb.sb("CACC2", [128, 2, 4], F32, parts=2)
                FS2 = [pb.sb(f"FS2_{i}", [128, 2, 4], F32) for i in range(4)]
                PZ = [PS[0], PS[1], PS[2], PS[3], PS[4]]
                PO = [PS[5], PS[6]]
                PP = [PS[7]]
                st = {"pi": 0}

                bg = Bg()

                def pre_hp(hp):
                    w = W[0]
                    qt, kt_, v = QT[hp % 2], KT[hp % 2], V[hp % 2]
                    s.dma("pool", w[:, :, :], wqkv_d[hp].rearrange("p (kc c) -> p kc c", kc=8), writes=w.all())
                    for which in range(2):
                        for tb in range(4):
                            pp = PP[st["pi"] % len(PP)]
                            st["pi"] += 1
                            for kc in range(8):
                                if kc in (2, 4, 6):
                                    yield
                                s.op("pe", lambda e, kc=kc, pp=pp, w=w, which=which, tb=tb: e.matmul(
                                    pp[:, :], lhsT=w[:, kc, which * 128:(which + 1) * 128],
                                    rhs=HT[:, kc, tb * 512:(tb + 1) * 512], start=(kc == 0), stop=(kc == 7)),
                                    reads=w.all() + HT.b(tb), writes=pp.all())
                            if which == 0:
                                s.op("dve", lambda e, pp=pp, qt=qt, tb=tb: e.tensor_scalar(
                                    out=qt[:, tb * 512:(tb + 1) * 512], in0=pp[:, :], scalar1=0.125, scalar2=None,
                                    op0=ALU.mult),
                                    reads=pp.all(), writes=qt.all())
                            else:
                                s.op("dve", lambda e, pp=pp, kt_=kt_, tb=tb: e.tensor_copy(
                                    out=kt_[:, tb * 512:(tb + 1) * 512], in_=pp[:, :]),
                                    reads=pp.all(), writes=kt_.all())
                            yield
                    for tg in range(4):
                        pp = PP[st["pi"] % len(PP)]
                        st["pi"] += 1
                        for tt in range(4):
                            tok = (tg * 4 + tt) * 128
                            if tt > 0:
                                yield
                            for kc in range(8):
                                s.op("pe", lambda e, kc=kc, pp=pp, w=w, tt=tt, tok=tok: e.matmul(
                                    pp[:, tt * 128:(tt + 1) * 128], lhsT=HT[:, kc, tok:tok + 128],
                                    rhs=w[:, kc, 256:384], start=(kc == 0), stop=(kc == 7)),
                                    reads=w.all() + HT.b(tg), writes=pp.all())
                        s.op("dve", lambda e, pp=pp, v=v, tg=tg: e.tensor_copy(
                            out=v[:, tg * 4:(tg + 1) * 4, :], in_=pp[:, :].rearrange("p (a b) -> p a b", a=4)),
                            reads=pp.all(), writes=v.all())
                        yield

                def post_hp(hp):
                    otok = OTOK[hp % 2]
                    for tg in range(4):
                        pp = PP[st["pi"] % len(PP)]
                        st["pi"] += 1
                        for tt in range(4):
                            s.op("pe", lambda e, pp=pp, tt=tt, tg=tg, otok=otok: e.matmul(
                                pp[:, tt * 128:(tt + 1) * 128], lhsT=otok[:, tg * 4 + tt, :], rhs=C["ident"][:, :],
                                start=True, stop=True),
                                reads=otok.all() + C["ident"].all(), writes=pp.all())
                        s.op("dve", lambda e, pp=pp, tg=tg, hp=hp: e.tensor_copy(
                            out=OT[:, hp, tg * 512:(tg + 1) * 512], in_=pp[:, :]),
                            reads=pp.all(), writes=OT.all())
                        yield

                units = []
                for hp in range(8):
                    for g in range(4):
                        for kt in range(4 * g + 3, -1, -1):
                            for hh in range(2):
                                units.append((hp, hh, g, kt))
                n = len(units)
                NPZ, NSP, NATT, NPO, NE = 5, 5, 3, 2, 3

                def u_(i):
                    hp, hh, g, kt = units[i]
                    d = kt - 4 * g
                    return hp, hh, g, kt, d, slice(hh * 64, (hh + 1) * 64)

                def c0_(i):
                    hp, hh, g, kt = units[i]
                    return max(kt - 4 * g, 0) * 128

                def s0_qk(i):
                    hp, hh, g, kt, d, hs = u_(i)
                    if hh == 0 and g == 0 and kt == 3:
                        if hp == 0:
                            bg.add(pre_hp(0))
                        bg.drain()
                    if hh == 0 and g == 1 and kt == 5 and hp + 1 < 8:
                        bg.add(pre_hp(hp + 1), 1)
                    pz, qt, kt_ = PZ[i % NPZ], QT[hp % 2], KT[hp % 2]
                    q0 = g * 512
                    c0 = c0_(i)
                    s.op("pe", lambda e: e.matmul(pz[:, c0:512], lhsT=kt_[hs, kt * 128:(kt + 1) * 128],
                                                  rhs=qt[hs, q0 + c0:q0 + 512], start=True, stop=(d < 0)),
                         reads=kt_.all() + qt.all(), writes=pz.all())
                    if d >= 0:
                        s.op("pe", lambda e: e.matmul(pz[:, c0:c0 + 128], lhsT=C["ident"][:, :], rhs=NEGM[:, d, c0:c0 + 128],
                                                      start=False, stop=True),
                             reads=C["ident"].all() + NEGM.all(), writes=pz.all())

                def s1_exp(i):
                    pz, e_ = PZ[i % NPZ], E[i % NE]
                    c0 = c0_(i)
                    s.op("act", lambda e: e.activation(out=e_[:, c0:512], in_=pz[:, c0:512], func=AF.Exp),
                         reads=pz.all(), writes=e_.all())

                def s2_ln(i):
                    e_, sp = E[i % NE], SP[i % NSP]
                    c0 = c0_(i)
                    s.op("act", lambda e: e.activation(out=sp[:, c0:512], in_=e_[:, c0:512], func=AF.Ln,
                                                       bias=C["one_f"][:, 0:1]),
                         reads=e_.all() + C["one_f"].all(), writes=sp.all())

                def s3_tri(i):
                    pz, sp = PZ[i % NPZ], SP[i % NSP]
                    c0 = c0_(i)
                    s.op("pe", lambda e: e.matmul(pz[:, c0:512], lhsT=C["ntri"][:, :], rhs=sp[:, c0:512], start=False,
                                                  stop=True, skip_group_check=True),
                         reads=sp.all() + C["ntri"].all(), writes=pz.all())

                def s4_att(i):
                    pz, att = PZ[i % NPZ], ATT[i % NATT]
                    c0 = c0_(i)
                    s.op("act", lambda e: e.activation(out=att[:, c0:512], in_=pz[:, c0:512], func=AF.Exp),
                         reads=pz.all(), writes=att.all())

                def s5_av(i):
                    hp, hh, g, kt, d, hs = u_(i)
                    qlo = max(d, 0)
                    sp, att, po, v = SP[i % NSP], ATT[i % NATT], PO[i % NPO], V[hp % 2]
                    for qi in range(qlo, 4):
                        s.op("pe", lambda e, qi=qi: e.matmul(
                            po[:, qi * 64:(qi + 1) * 64], lhsT=att[:, qi * 128:(qi + 1) * 128],
                            rhs=v[:, kt, hs], start=True, stop=True),
                            reads=att.all() + v.all(), writes=po.all())
                        s.op("pe", lambda e, qi=qi: e.matmul(
                            po[:, 256 + qi:257 + qi], lhsT=sp[:, qi * 128:(qi + 1) * 128],
                            rhs=C["ones_col"][:, 0:1], start=True, stop=True),
                            reads=sp.all() + C["ones_col"].all(), writes=po.all())

                def s6_acc(i):
                    hp, hh, g, kt, d, hs = u_(i)
                    qlo = max(d, 0)
                    span = hh
                    po = PO[i % NPO]
                    oacc, fs2 = OACC[span % 2], FS2[(i // 2) % 4]
                    cb = CACC2.b(hh)
                    otok = OTOK[hp % 2]
                    if kt != 4 * g + 3:
                        if hh == 0:
                            s.op("act", lambda e: e.activation(out=fs2[:, :, :], in_=CACC2[:, :, :], func=AF.Exp, scale=-1.0),
                                 reads=CACC2.all(), writes=fs2.all())
                        s.op("dve", lambda e: e.tensor_tensor(
                            out=CACC2[:, hh, qlo:4], in0=CACC2[:, hh, qlo:4], in1=po[:, 256 + qlo:260], op=ALU.add),
                            reads=po.all() + cb, writes=cb)
                        for qi in range(qlo, 4):
                            s.op("dve", lambda e, qi=qi: e.scalar_tensor_tensor(
                                out=oacc[:, qi, :], in0=po[:, qi * 64:(qi + 1) * 64], scalar=fs2[:, hh, qi:qi + 1],
                                in1=oacc[:, qi, :], op0=ALU.mult, op1=ALU.add),
                                reads=po.all() + fs2.all() + oacc.all(), writes=oacc.all())
                    else:
                        if qlo > 0:
                            s.op("dve", lambda e: e.memset(oacc[:, 0:qlo, :], 0.0), writes=oacc.all())
                            s.op("dve", lambda e: e.memset(CACC2[:, hh, 0:qlo], 0.0), writes=cb)
                        s.op("dve", lambda e: e.tensor_copy(
                            out=oacc[:, qlo:4, :], in_=po[:, qlo * 64:256].rearrange("p (a b) -> p a b", b=64)),
                            reads=po.all(), writes=oacc.all())
                        s.op("dve", lambda e: e.tensor_copy(out=CACC2[:, hh, qlo:4], in_=po[:, 256 + qlo:260]),
                             reads=po.all(), writes=cb)
                    if kt == 0:
                        s.op("dve", lambda e: e.tensor_copy(out=otok[:, 4 * g:4 * g + 4, hs], in_=oacc[:, :, :]),
                             reads=oacc.all(), writes=otok.all())
                        if hh == 1 and g == 3:
                            bg.add(post_hp(hp), 1)

                stages = ((0, s0_qk), (1, s1_exp), (2, s2_ln), (3, s3_tri), (4, s4_att), (5, s5_av), (6, s6_acc))
                for i in range(n + 6):
                    for lag, fn in stages:
                        if 0 <= i - lag < n:
                            fn(i - lag)
                    bg.tick()
                bg.drain()
        if "dbg_OT" in C:
            s.dma("sp", C["dbg_OT"][:, :, :], OT[:, :, :], reads=OT.all(), writes=C["dbg_OT"].all())
        outproj_postnorm(k, C, XT, PS, OT, wo_d, gi_post, next_gi)


AX = mybir.AxisListType


class Ref:
    __slots__ = ("ap", "bufs")

    def __init__(self, ap, bufs):
        self.ap = ap
        self.bufs = bufs


class _RefMaker:
    def __init__(self, t):
        self.t = t

    def __getitem__(self, key):
        return Ref(self.t.t[key], self.t.all())


def rf(t):
    return _RefMaker(t)


def _b(*refs):
    out = []
    for r in refs:
        if isinstance(r, Ref):
            out += r.bufs
    return out


def _a(x):
    return x.ap if isinstance(x, Ref) else x


def e_tt(s, eng, out, a, b, op):
    return s.op(eng, lambda E: E.tensor_tensor(out=out.ap, in0=a.ap, in1=b.ap, op=op), reads=_b(a, b), writes=out.bufs)


def e_ts(s, eng, out, a, s1, s2, op0, op1=None):
    if op1 is None:
        return s.op(eng, lambda E: E.tensor_scalar(out=out.ap, in0=a.ap, scalar1=_a(s1), scalar2=None, op0=op0),
                    reads=_b(a, s1), writes=out.bufs)
    return s.op(eng, lambda E: E.tensor_scalar(out=out.ap, in0=a.ap, scalar1=_a(s1), scalar2=_a(s2), op0=op0, op1=op1),
                reads=_b(a, s1, s2), writes=out.bufs)


def e_stt(s, eng, out, a, sc, b, op0, op1):
    return s.op(eng, lambda E: E.scalar_tensor_tensor(out=out.ap, in0=a.ap, scalar=_a(sc), in1=b.ap, op0=op0, op1=op1),
                reads=_b(a, sc, b), writes=out.bufs)


def e_act(s, out, a, func, bias=None, scale=None):
    kw = {}
    if bias is not None:
        kw["bias"] = _a(bias)
    if scale is not None:
        kw["scale"] = _a(scale)
    return s.op("act", lambda E: E.activation(out=out.ap, in_=a.ap, func=func, **kw), reads=_b(a, bias, scale),
                writes=out.bufs)


def e_mm(s, out, lhsT, rhs, start=True, stop=True):
    return s.op("pe", lambda E: E.matmul(out.ap, lhsT=lhsT.ap, rhs=rhs.ap, start=start, stop=stop),
                reads=_b(lhsT, rhs), writes=out.bufs)


def e_copy(s, eng, out, a):
    if eng == "act":
        return e_act(s, out, a, AF.Copy)
    return s.op(eng, lambda E: E.tensor_copy(out=out.ap, in_=a.ap), reads=_b(a), writes=out.bufs)


def e_memset(s, eng, out, val):
    return s.op(eng, lambda E: E.memset(out.ap, val), writes=out.bufs)


GN_EPS = 64e-5
STAGGER = 3
NEG_EXP_HALF = -0.6065306597126334


def mixer0_stage(k, C, XT, PS, D, gi_pre, gi_post, next_gi=None):
    s = k.s
    with k.phase() as ph:
        OT = ph.sb("OT", [128, 8, SEQ], BF16, parts=1)
        with k.phase() as pab:
            HT = C["HTG"]
            prenorm_to_HT(k, C, pab, XT, HT, PS, gi_pre)
            with k.phase() as pl:
                rglru_part(k, C, pl, HT, OT, PS, D)
            with k.phase() as pr:
                rwkv_part(k, C, pr, HT, OT, PS, D, XT)
        if "dbg_OT" in D:
            s.dma("sp", D["dbg_OT"][:, :, :], OT[:, :, :], reads=OT.all(), writes=D["dbg_OT"].all())
        outproj_postnorm(k, C, XT, PS, OT, D["l0_w_out"], gi_post, next_gi)


def rglru_part(k, C, p, HT, OT, PS, D):
    s = k.s
    PL = p.sb("PL", [128, 4, 8], F32)
    s.dma("sp", PL[:, :, :], D["l0_pl"][:, :].rearrange("p (c n) -> p c n", c=4), writes=PL.all())
    C1 = p.sb("C1", [128, 4], F32)
    e_act(s, rf(C1)[:, :], rf(PL)[:, :, 7], AF.Exp, scale=-1.0)
    e_act(s, rf(C1)[:, :], rf(C1)[:, :], AF.Ln, bias=rf(C["one_f"])[:, 0:1])
    e_ts(s, "dve", rf(C1)[:, :], rf(C1)[:, :], -8.0, None, ALU.mult)
    GAW = p.sb("GAW", [128, 4, 128], BF16)
    GXW = p.sb("GXW", [128, 4, 128], BF16)
    e_memset(s, "dve", rf(GAW)[:, :, :], 0.0)
    e_memset(s, "dve", rf(GXW)[:, :, :], 0.0)
    for n in range(8):
        ps_ = slice((n % 2) * 64, (n % 2) * 64 + 64)
        s.dma("pool", GAW[ps_, n // 2, ps_], D["l0_gate_a_w"][n], writes=GAW.all())
        s.dma("pool", GXW[ps_, n // 2, ps_], D["l0_gate_x_w"][n], writes=GXW.all())
    W = [p.sb(f"WL{i}", [128, 8, 256], BF16) for i in range(2)]
    XBs = [p.sb(f"XB{i}", [128, 515], F32) for i in range(2)]
    HHs = [[p.sb(f"HH{j}_{i}", [128, 512], F32) for i in range(2)] for j in range(2)]
    ts_ = [{n: p.sb(f"{n}{j}", [128, 512], F32) for n in ("GB", "XC", "R", "IG", "A", "U", "T1", "T2")} for j in range(2)]
    XCbs = [p.sb(f"XCb{j}", [128, 512], BF16) for j in range(2)]

    def unit(c, tb, j):
        w = W[j]
        XB, t_, XCb = XBs[j], ts_[j], XCbs[j]
        col = lambda n: rf(PL)[:, c, n:n + 1]
        tok = tb * 512
        px, pg = PS[2 * j], PS[2 * j + 1]
        for kc in range(8):
            e_mm(s, rf(px)[:, :], rf(w)[:, kc, 0:128], Ref(HT[:, kc, tok:tok + 512], HT.b(tb)), kc == 0, kc == 7)
        for kc in range(8):
            e_mm(s, rf(pg)[:, :], rf(w)[:, kc, 128:256], Ref(HT[:, kc, tok:tok + 512], HT.b(tb)), kc == 0, kc == 7)
        if tb == 0:
            e_memset(s, "dve", rf(XB)[:, 0:3], 0.0)
        else:
            e_copy(s, "dve", rf(XB)[:, 0:3], rf(XB)[:, 512:515])
        yield
        e_copy(s, "act", rf(XB)[:, 3:515], rf(px)[:, :])
        e_copy(s, "act", rf(t_["GB"])[:, :], rf(pg)[:, :])
        yield
        XC = t_["XC"]
        e_ts(s, "dve", rf(XC)[:, :], rf(XB)[:, 3:515], col(3), col(4), ALU.mult, ALU.add)
        for i in range(3):
            e_stt(s, "dve", rf(XC)[:, :], rf(XB)[:, i:i + 512], col(i), rf(XC)[:, :], ALU.mult, ALU.add)
        GB, T2 = t_["GB"], t_["T2"]
        e_act(s, rf(T2)[:, :], rf(GB)[:, :], AF.Gelu_apprx_tanh)
        yield
        e_copy(s, "act", rf(XCb)[:, :], rf(XC)[:, :])
        yield
        pr_, pig = PS[4 + 2 * j], PS[5 + 2 * j]
        e_mm(s, rf(pr_)[:, :], rf(GAW)[:, c, :], rf(XCb)[:, :])
        e_mm(s, rf(pig)[:, :], rf(GXW)[:, c, :], rf(XCb)[:, :])
        yield
        e_act(s, rf(t_["R"])[:, :], rf(pr_)[:, :], AF.Sigmoid, bias=col(5))
        e_act(s, rf(t_["IG"])[:, :], rf(pig)[:, :], AF.Sigmoid, bias=col(6))
        yield
        A = t_["A"]
        e_act(s, rf(A)[:, :], rf(t_["R"])[:, :], AF.Exp, scale=rf(C1)[:, c:c + 1])
        T1, U = t_["T1"], t_["U"]
        e_tt(s, "dve", rf(U)[:, :], rf(t_["IG"])[:, :], rf(XC)[:, :], ALU.mult)
        yield
        e_tt(s, "dve", rf(T1)[:, :], rf(A)[:, :], rf(A)[:, :], ALU.mult)
        yield
        e_ts(s, "dve", rf(T1)[:, :], rf(T1)[:, :], -1.0, 1.0, ALU.mult, ALU.add)
        yield
        e_act(s, rf(T1)[:, :], rf(T1)[:, :], AF.Sqrt)
        yield
        e_tt(s, "dve", rf(U)[:, :], rf(U)[:, :], rf(T1)[:, :], ALU.mult)
        yield
        H = HHs[j][tb % 2]
        Hp = HHs[j][(tb + 1) % 2]
        init = 0.0 if tb == 0 else Hp[:, 511:512]
        s.op("dve", lambda E: E.tensor_tensor_scan(
            out=H[:, :], data0=A[:, :], data1=U[:, :], initial=init, op0=ALU.mult, op1=ALU.add),
            reads=A.all() + U.all() + (Hp.all() if tb else []), writes=H.all())
        yield
        s.op("dve", lambda E: E.tensor_tensor(
            out=OT[:, 4 + c, tok:tok + 512], in0=H[:, :], in1=T2[:, :], op=ALU.mult),
            reads=H.all() + T2.all(), writes=OT.all())

    for cp in range(2):
        for j in range(2):
            c = 2 * cp + j
            s.dma("pool", W[j][:, :, :], D["l0_w_lru"][c].rearrange("p (kc n) -> p kc n", kc=8), writes=W[j].all())
        for tb in range(4):
            gens = [unit(2 * cp + j, tb, j) for j in range(2)]
            alive = [True, True]
            while any(alive):
                for j in range(2):
                    if alive[j]:
                        try:
                            next(gens[j])
                        except StopIteration:
                            alive[j] = False


def rwkv_part(k, C, p, HT, OT, PS, D, XT):
    s = k.s
    spill = D["xt_spill"]
    for kc in range(8):
        s.dma("sp", spill[:, kc, :], XT[:, kc, :], reads=XT.all(), writes=spill.all())
    PH = p.sb("PH", [128, 4, 8], F32)
    s.dma("sp", PH[:, :, :], D["l0_ph"][:, :].rearrange("p (h n) -> p h n", h=4), writes=PH.all())
    OM = p.sb("OM", [128, 4, 4], F32)
    e_ts(s, "dve", rf(OM)[:, :, 0:3], rf(PH)[:, :, 0:3], -1.0, 1.0, ALU.mult, ALU.add)
    e_ts(s, "dve", rf(OM)[:, :, 3:4], rf(PH)[:, :, 6:7], -1.0, 1.0, ALU.mult, ALU.add)
    RKb = p.sb("RKb", [128, 4], BF16)
    e_copy(s, "dve", rf(RKb)[:, :], rf(PH)[:, :, 7])
    MUL = p.sb("MUL", [128, 3], F32)
    s.dma("sp", MUL[:, :], D["l0_mul"][:, :], writes=MUL.all())
    OML = p.sb("OML", [128, 3], F32)
    e_ts(s, "dve", rf(OML)[:, :], rf(MUL)[:, :], -1.0, 1.0, ALU.mult, ALU.add)
    LNGBs = [p.sb(f"LNGB{i}", [128, 2, 64], F32) for i in range(2)]
    W2 = p.sb("W2", [64, 512], BF16)
    A2 = p.sb("A2", [64, 512], BF16)
    G2 = p.sb("G2", [128, 512], BF16)
    s.dma("pool", W2[:, :], D["l0_w2"][:, :], writes=W2.all())
    s.dma("pool", A2[:, :], D["l0_a2"][:, :], writes=A2.all())
    s.dma("pool", G2[:, :], D["l0_g2"][:, :], writes=G2.all())
    ob = C["ones_bf"]
    BLK = p.sb("BLK", [128, 128], BF16)
    e_memset(s, "dve", rf(BLK)[:, :], 0.0)
    e_memset(s, "dve", rf(BLK)[0:64, 0:64], 1.0)
    e_memset(s, "dve", rf(BLK)[64:128, 64:128], 1.0)
    M512 = p.sb("M512", [128, 512], BF16)
    MUS = p.sb("MUS", [128, 512], BF16)
    MUI = p.sb("MUI", [128, 512], BF16)
    MLS = p.sb("MLS", [128, 512], BF16)
    ID8 = p.sb("ID8", [128, 512], BF16)
    for hh in range(2):
        hs = slice(hh * 64, hh * 64 + 64)
        for dst, pat, cmp_, cm in ((M512, [[0, 8], [1, 64]], ALU.is_gt, 0), (MUS, [[0, 8], [1, 64]], ALU.is_gt, -1),
                                   (MUI, [[0, 8], [1, 64]], ALU.is_ge, -1), (MLS, [[0, 8], [-1, 64]], ALU.is_gt, 1),
                                   (ID8, [[0, 8], [-1, 64]], ALU.is_equal, 1)):
            s.op("pool", lambda E, dst=dst, pat=pat, cmp_=cmp_, cm=cm, hs=hs: E.affine_select(
                out=dst[hs, :], in_=ob[hs, :], pattern=pat, compare_op=cmp_, fill=0.0, base=0, channel_multiplier=cm),
                reads=ob.all(), writes=dst.all())
    ident = C["ident"]

    TW = p.sb("TW", [64, SEQ], BF16)
    AL = p.sb("AL", [64, SEQ], BF16)
    SGL = p.sb("SGL", [128, SEQ], BF16)
    with k.phase() as p0:
        WLo = p0.sb("WLo", [128, 8, 256], BF16)
        s.dma("pool", WLo[:, :, :], D["l0_w_lora"][:, :].rearrange("p (kc n) -> p kc n", kc=8), writes=WLo.all())
        PAl = [p0.sb(f"PAl{i}", [128, 513], F32) for i in range(3)]
        TMPl = [p0.sb(f"TMPl{i}", [128, 512], F32) for i in range(3)]

        def lora_chain(which, c0, c1, npart, dst):
            PA, tmpl = PAl[which], TMPl[which]
            for tb in range(4):
                tok = tb * 512
                pp = PS[which * 2 + tb % 2]
                for kc in range(8):
                    e_mm(s, rf(pp)[0:npart, :], rf(WLo)[:, kc, c0:c1], Ref(HT[:, kc, tok:tok + 512], HT.b(tb)), kc == 0, kc == 7)
                if tb == 0:
                    e_memset(s, "dve", rf(PA)[0:npart, 0:1], 0.0)
                else:
                    e_copy(s, "dve", rf(PA)[0:npart, 0:1], rf(PA)[0:npart, 512:513])
                yield
                e_copy(s, "act", rf(PA)[0:npart, 1:513], rf(pp)[0:npart, :])
                yield
                e_act(s, rf(tmpl)[0:npart, :], rf(PA)[0:npart, 0:512], AF.Copy, scale=rf(MUL)[0:npart, which:which + 1])
                yield
                e_stt(s, "dve", rf(tmpl)[0:npart, :], rf(PA)[0:npart, 1:513], rf(OML)[0:npart, which:which + 1],
                      rf(tmpl)[0:npart, :], ALU.mult, ALU.add)
                yield
                if which == 0:
                    e_act(s, rf(dst)[:, tok:tok + 512], rf(tmpl)[0:64, :], AF.Tanh)
                elif which == 1:
                    e_copy(s, "act", rf(dst)[:, tok:tok + 512], rf(tmpl)[0:64, :])
                else:
                    e_act(s, rf(dst)[:, tok:tok + 512], rf(tmpl)[:, :], AF.Sigmoid)
                yield

        lbg = Bg()
        for which, (c0, c1, npart, dst) in enumerate(((0, 64, 64, TW), (64, 128, 64, AL), (128, 256, 128, SGL))):
            lbg.add(lora_chain(which, c0, c1, npart, dst), 1)
        lbg.drain()

    s.barrier()
    XTf = XT.t
    XTb = XT.t.bitcast(BF16)
    f32n = ("r", "k", "SIG", "A", "KKN", "KH", "CUM", "EC", "EX", "EN", "TMP")
    b16n = ("Rt", "At", "Bt", "Kt", "Bh", "Kh", "RK", "VT", "KK2")
    t64n = ("V64", "BH64", "KH64", "N", "Q", "N2", "Q2", "XA", "LAK", "ARB", "ARK")
    sets = []
    for S in range(2):
        B = {}
        if S == 0:
            B["WH"] = p.sb("WH", [128, 8, 384], BF16)
            B["PA"] = [p.sb(f"PA{i}", [128, 513], F32) for i in range(3)]
            F = {n: p.sb("f_" + n, [128, 512], F32) for n in f32n}
            Bf = {n: p.sb("b_" + n, [128, 512], BF16) for n in b16n}
            T64 = {n: p.sb("t_" + n, [128, 512], BF16) for n in t64n}
        else:
            B["WH"] = T("WH1", XTb[:, 7, 0:3072].rearrange("p (kc n) -> p kc n", kc=8))
            B["PA"] = [T(f"PA1_{i}", XTf[:, 3, i * 513:(i + 1) * 513]) for i in range(3)]
            F = {n: T("f1_" + n, XTf[:, i // 4, (i % 4) * 512:(i % 4) * 512 + 512]) for i, n in enumerate(f32n)}
            bl = list(b16n) + list(t64n)
            vb = {n: T("b1_" + n, XTb[:, 4 + i // 8, (i % 8) * 512:(i % 8) * 512 + 512]) for i, n in enumerate(bl)}
            Bf = {n: vb[n] for n in b16n}
            T64 = {n: vb[n] for n in t64n}
        F["EH"] = F["TMP"]
        F["BA"] = F["SIG"]
        T64["YA"] = T64["LAK"]
        B["F"], B["Bf"], B["T64"] = F, Bf, T64
        B["R0"], B["YF"], B["YQ"], B["GT"] = F["SIG"], F["EX"], F["EN"], F["KH"]
        B["ST"] = p.sb(f"ST{S}", [128, 8, 4], F32)
        B["RKS"] = p.sb(f"RKS{S}", [128, 8], F32)
        B["Pf"] = p.sb(f"Pf{S}", [128, 64], F32)
        B["Pb"] = p.sb(f"Pb{S}", [128, 64], BF16)
        B["RR"] = p.sb(f"RR{S}", [128, 64], BF16)
        B["UB"] = p.sb(f"UB{S}", [128, 64], BF16)
        B["LNGB"] = LNGBs[S]
        B["PY"] = PS[4 + S]
        B["PT1"] = PS[6 + S]
        sets.append(B)
    b3 = lambda r_: Ref(r_.ap.rearrange("p (a b) -> p a b", a=8), r_.bufs)
    HS = (slice(0, 64), slice(64, 128))
    st = {"sci": 0}

    def newps():
        st["sci"] += 1
        return PS[st["sci"] % 4]

    def mm2(out_t, col0, ncol, lhs_fn, rhs_fn, start=True, stop=True):
        for hs in HS:
            e_mm(s, rf(out_t)[hs, col0:col0 + ncol], lhs_fn(hs), rhs_fn(hs), start, stop)

    def unit(hp, gq, B):
        F, Bf, T64, PA, w = B["F"], B["Bf"], B["T64"], B["PA"], B["WH"]
        R0, YF, YQ, GT, ST, RKS = B["R0"], B["YF"], B["YQ"], B["GT"], B["ST"], B["RKS"]
        Pf, Pb, RR, UB, LNGB = B["Pf"], B["Pb"], B["RR"], B["UB"], B["LNGB"]
        hc = lambda n: rf(PH)[:, hp, n:n + 1]
        tok = gq * 512
        for which, nm in enumerate(("r", "k", "v")):
            pp = newps()
            for kc in range(8):
                e_mm(s, rf(pp)[:, :], rf(w)[:, kc, which * 128:(which + 1) * 128],
                     Ref(HT[:, kc, tok:tok + 512], HT.b(gq)), kc == 0, kc == 7)
            pa = PA[which]
            if gq == 0:
                e_memset(s, "dve", rf(pa)[:, 0:1], 0.0)
            else:
                e_copy(s, "dve", rf(pa)[:, 0:1], rf(pa)[:, 512:513])
            e_copy(s, "act", rf(pa)[:, 1:513], rf(pp)[:, :])
            yield
            tmp_ = rf(F["TMP"])[:, :] if which != 1 else rf(F["CUM"])[:, :]
            e_act(s, tmp_, rf(pa)[:, 0:512], AF.Copy, scale=hc(which))
            dst_ = rf(Bf["VT"])[:, :] if nm == "v" else rf(F[nm])[:, :]
            e_stt(s, "dve", dst_, rf(pa)[:, 1:513], rf(OM)[:, hp, which:which + 1], tmp_, ALU.mult, ALU.add)
        r_, k_ = rf(F["r"])[:, :], rf(F["k"])[:, :]
        pz = newps()
        e_mm(s, rf(pz)[:, :], rf(W2)[:, hp * 128:(hp + 1) * 128], rf(TW)[:, tok:tok + 512])
        e_act(s, rf(F["SIG"])[:, :], rf(pz)[:, :], AF.Sigmoid, bias=hc(3))
        pz2 = newps()
        e_mm(s, rf(pz2)[:, :], rf(A2)[:, hp * 128:(hp + 1) * 128], rf(AL)[:, tok:tok + 512])
        e_act(s, rf(F["A"])[:, :], rf(pz2)[:, :], AF.Sigmoid, bias=hc(4))
        yield
        e_ts(s, "dve", rf(F["KKN"])[:, :], k_, hc(5), None, ALU.mult)
        e_act(s, rf(Bf["KK2"])[:, :], rf(F["KKN"])[:, :], AF.Square)
        yield
        pz = newps()
        e_mm(s, rf(pz)[:, :], rf(BLK)[:, :], rf(Bf["KK2"])[:, :])
        e_act(s, rf(F["TMP"])[:, :], rf(pz)[:, :], AF.Sqrt)
        yield
        e_ts(s, "dve", rf(F["TMP"])[:, :], rf(F["TMP"])[:, :], 1e-12, None, ALU.max)
        s.op("dve", lambda E: E.reciprocal(out=F["TMP"][:, :], in_=F["TMP"][:, :]), reads=F["TMP"].all(),
             writes=F["TMP"].all())
        e_tt(s, "dve", rf(F["KKN"])[:, :], rf(F["KKN"])[:, :], rf(F["TMP"])[:, :], ALU.mult)
        e_act(s, rf(F["KH"])[:, :], rf(F["A"])[:, :], AF.Identity, bias=rf(OM)[:, hp, 3:4], scale=hc(6))
        e_tt(s, "dve", rf(F["KH"])[:, :], rf(F["KH"])[:, :], k_, ALU.mult)
        yield
        s.op("dve", lambda E: E.tensor_tensor_scan(out=F["CUM"][:, :], data0=M512[:, :], data1=F["SIG"][:, :],
                                                   initial=0.0, op0=ALU.mult, op1=ALU.add),
             reads=M512.all() + F["SIG"].all(), writes=F["CUM"].all())
        yield
        cum = rf(F["CUM"])[:, :]
        e_act(s, rf(F["EC"])[:, :], cum, AF.Exp, scale=NEG_EXP_HALF)
        e_tt(s, "dve", rf(F["EX"])[:, :], cum, rf(F["SIG"])[:, :], ALU.subtract)
        e_act(s, rf(F["EN"])[:, :], cum, AF.Exp, scale=-NEG_EXP_HALF)
        cum3 = b3(cum)
        cend = Ref(cum3.ap[:, :, 63:64].to_broadcast([128, 8, 64]), cum.bufs)
        e_tt(s, "dve", b3(rf(F["EH"])[:, :]), cend, cum3, ALU.subtract)
        yield
        e_act(s, rf(F["EX"])[:, :], rf(F["EX"])[:, :], AF.Exp, scale=NEG_EXP_HALF)
        e_act(s, rf(F["EH"])[:, :], rf(F["EH"])[:, :], AF.Exp, scale=NEG_EXP_HALF)
        e_tt(s, "dve", rf(Bf["Rt"])[:, :], r_, rf(F["EC"])[:, :], ALU.mult)
        e_tt(s, "dve", rf(Bf["RK"])[:, :], r_, rf(F["KH"])[:, :], ALU.mult)
        e_tt(s, "dve", rf(F["BA"])[:, :], rf(F["KKN"])[:, :], rf(F["A"])[:, :], ALU.mult)
        yield
        e_stt(s, "dve", rf(Bf["At"])[:, :], rf(F["KKN"])[:, :], -1.0, rf(F["EX"])[:, :], ALU.mult, ALU.mult)
        e_tt(s, "dve", rf(Bf["Bt"])[:, :], rf(F["BA"])[:, :], rf(F["EN"])[:, :], ALU.mult)
        e_tt(s, "dve", rf(Bf["Bh"])[:, :], rf(F["BA"])[:, :], rf(F["EH"])[:, :], ALU.mult)
        e_tt(s, "dve", rf(Bf["Kt"])[:, :], rf(F["KH"])[:, :], rf(F["EN"])[:, :], ALU.mult)
        e_tt(s, "dve", rf(Bf["Kh"])[:, :], rf(F["KH"])[:, :], rf(F["EH"])[:, :], ALU.mult)
        yield
        blk = lambda n, c8, hs: rf(Bf[n])[hs, c8 * 64:(c8 + 1) * 64]
        tb_ = lambda n, c8, hs: rf(T64[n])[hs, c8 * 64:(c8 + 1) * 64]
        idh = lambda hs: rf(ident)[hs, hs]
        for src, dst in (("VT", "V64"), ("Bh", "BH64"), ("Kh", "KH64")):
            pt = newps()
            for c8 in range(8):
                mm2(pt, c8 * 64, 64, lambda hs, c8=c8, src=src: blk(src, c8, hs), idh)
            e_copy(s, "act", rf(T64[dst])[:, :], rf(pt)[:, :])
            yield
        for lh, rh, mask, dst in (("Bt", "At", MUS, "N"), ("At", "Bt", MLS, "Q"), ("Kt", "At", MUS, "LAK"),
                                  ("Bt", "Rt", MUI, "ARB"), ("Kt", "Rt", MUI, "ARK")):
            pt = newps()
            for c8 in range(8):
                mm2(pt, c8 * 64, 64, lambda hs, c8=c8, lh=lh: blk(lh, c8, hs), lambda hs, c8=c8, rh=rh: blk(rh, c8, hs))
            e_tt(s, "dve", rf(T64[dst])[:, :], rf(pt)[:, :], rf(mask)[:, :], ALU.mult)
            yield
        e_tt(s, "dve", rf(T64["XA"])[:, :], rf(T64["N"])[:, :], rf(ID8)[:, :], ALU.add)
        Pn, Qn, Pn2, Qn2 = "N", "Q", "N2", "Q2"
        for lvl in range(1, 6):
            pq = newps()
            for c8 in range(8):
                mm2(pq, c8 * 64, 64, lambda hs, c8=c8, Pn=Pn: tb_(Pn, c8, hs), lambda hs, c8=c8, Qn=Qn: tb_(Qn, c8, hs))
            e_copy(s, "act", rf(T64[Qn2])[:, :], rf(pq)[:, :])
            if lvl < 5:
                pp_ = newps()
                for c8 in range(8):
                    mm2(pp_, c8 * 64, 64, lambda hs, c8=c8, Qn=Qn: tb_(Qn, c8, hs),
                        lambda hs, c8=c8, Pn=Pn: tb_(Pn, c8, hs))
                e_copy(s, "act", rf(T64[Pn2])[:, :], rf(pp_)[:, :])
            yield
            px = newps()
            for c8 in range(8):
                mm2(px, c8 * 64, 64, lambda hs, c8=c8, Qn2=Qn2: tb_(Qn2, c8, hs), lambda hs, c8=c8: tb_("XA", c8, hs))
            e_tt(s, "dve", rf(T64["XA"])[:, :], rf(T64["XA"])[:, :], rf(px)[:, :], ALU.add)
            yield
            Pn, Pn2 = Pn2, Pn
            Qn, Qn2 = Qn2, Qn
        pr0 = newps()
        for c8 in range(8):
            mm2(pr0, c8 * 64, 64, lambda hs, c8=c8: tb_("LAK", c8, hs), lambda hs, c8=c8: tb_("V64", c8, hs))
        e_copy(s, "act", rf(R0)[:, :], rf(pr0)[:, :])
        pg = newps()
        e_mm(s, rf(pg)[:, :], rf(G2)[:, hp * 128:(hp + 1) * 128], rf(SGL)[:, tok:tok + 512])
        e_copy(s, "act", rf(GT)[:, :], rf(pg)[:, :])
        yield
        PY, PT1 = B["PY"], B["PT1"]
        pbh = lambda hs: rf(Pb)[hs, :]
        ubh = lambda hs: rf(UB)[hs, :]
        for c8 in range(8):
            mm2(PT1, 0, 64, lambda hs: blk("At", c8, hs), pbh)
            mm2(PY, c8 * 64, 64, lambda hs: blk("Rt", c8, hs), pbh, True, False)
            e_tt(s, "dve", rf(RR)[:, :], rf(PT1)[:, 0:64], rf(R0)[:, c8 * 64:(c8 + 1) * 64], ALU.add)
            yield
            mm2(PT1, 64, 64, lambda hs: tb_("XA", c8, hs), lambda hs: rf(RR)[hs, :])
            e_copy(s, "dve", rf(UB)[:, :], rf(PT1)[:, 64:128])
            yield
            mm2(PT1, 128, 64, lambda hs: tb_("KH64", c8, hs), lambda hs: tb_("V64", c8, hs), True, False)
            mm2(PT1, 128, 64, lambda hs: tb_("BH64", c8, hs), ubh, False, True)
            mm2(PY, c8 * 64, 64, lambda hs: tb_("ARB", c8, hs), ubh, False, False)
            mm2(PY, c8 * 64, 64, lambda hs: tb_("ARK", c8, hs), lambda hs: tb_("V64", c8, hs), False, True)
            e_stt(s, "dve", rf(Pf)[:, :], rf(Pf)[:, :], rf(F["EC"])[:, c8 * 64 + 63:c8 * 64 + 64], rf(PT1)[:, 128:192],
                  ALU.mult, ALU.add)
            e_copy(s, "dve", rf(Pb)[:, :], rf(Pf)[:, :])
            yield
        e_copy(s, "act", rf(YF)[:, :], rf(PY)[:, :])
        yf3 = b3(rf(YF)[:, :])
        yq3 = b3(rf(YQ)[:, :])
        prk = newps()
        for c8 in range(8):
            mm2(prk, c8, 1, lambda hs: blk("RK", c8, hs), lambda hs: rf(RKb)[hs, hp:hp + 1])
        e_copy(s, "act", rf(RKS)[:, :], rf(prk)[:, 0:8])
        yield
        s.op("dve", lambda E: E.tensor_reduce(out=ST[:, :, 0], in_=YF[:, :].rearrange("p (a b) -> p a b", a=8),
                                              axis=AX.X, op=ALU.add), reads=YF.all(), writes=ST.all())
        e_act(s, rf(YQ)[:, :], rf(YF)[:, :], AF.Square)
        yield
        s.op("dve", lambda E: E.tensor_reduce(out=ST[:, :, 1], in_=YQ[:, :].rearrange("p (a b) -> p a b", a=8),
                                              axis=AX.X, op=ALU.add), reads=YQ.all(), writes=ST.all())
        e_ts(s, "dve", rf(ST)[:, :, 2], rf(ST)[:, :, 0], 1.0 / 64, None, ALU.mult)
        e_tt(s, "dve", rf(ST)[:, :, 0], rf(ST)[:, :, 2], rf(ST)[:, :, 2], ALU.mult)
        e_stt(s, "dve", rf(ST)[:, :, 1], rf(ST)[:, :, 1], 1.0 / 64, rf(ST)[:, :, 0], ALU.mult, ALU.subtract)
        e_ts(s, "dve", rf(ST)[:, :, 1], rf(ST)[:, :, 1], GN_EPS, None, ALU.add)
        e_act(s, rf(ST)[:, :, 3], rf(ST)[:, :, 1], AF.Sqrt)
        yield
        s.op("dve", lambda E: E.reciprocal(out=ST[:, :, 3], in_=ST[:, :, 3]), reads=ST.all(), writes=ST.all())
        mean_b = Ref(ST[:, :, 2:3].to_broadcast([128, 8, 64]), ST.all())
        rstd_b = Ref(ST[:, :, 3:4].to_broadcast([128, 8, 64]), ST.all())
        e_tt(s, "dve", yf3, yf3, mean_b, ALU.subtract)
        e_tt(s, "dve", yf3, yf3, rstd_b, ALU.mult)
        rks_b = Ref(RKS[:, :].rearrange("p (a b) -> p a b", b=1).to_broadcast([128, 8, 64]), RKS.all())
        e_tt(s, "dve", yq3, b3(rf(T64["V64"])[:, :]), rks_b, ALU.mult)
        lng = Ref(LNGB[:, 0:1, :].to_broadcast([128, 8, 64]), LNGB.all())
        lnb = Ref(LNGB[:, 1:2, :].to_broadcast([128, 8, 64]), LNGB.all())
        yield
        e_tt(s, "dve", yf3, yf3, lng, ALU.mult)
        e_tt(s, "dve", yf3, yf3, lnb, ALU.add)
        yield
        e_tt(s, "dve", rf(T64["YA"])[:, :], rf(YF)[:, :], rf(YQ)[:, :], ALU.add)
        yield
        pt = newps()
        for c8 in range(8):
            mm2(pt, c8 * 64, 64, lambda hs: tb_("YA", c8, hs), idh)
        s.op("dve", lambda E: E.tensor_tensor(
            out=OT[:, hp, tok:tok + 512], in0=pt[:, :], in1=GT[:, :], op=ALU.mult),
            reads=pt.all() + GT.all(), writes=OT.all())
        yield

    def stream(S, hps):
        B = sets[S]
        for hp in hps:
            w = B["WH"]
            s.dma("pool", w[:, :, :], D["l0_w_hp"][hp].rearrange("p (kc n) -> p kc n", kc=8), writes=w.all())
            LNGB = B["LNGB"]
            for hh in range(2):
                h = 2 * hp + hh
                s.dma("sp", LNGB[HS[hh], 0, :], D["l0_lnx_g"][h * 64:(h + 1) * 64].partition_broadcast(64),
                      writes=LNGB.all())
                s.dma("sp", LNGB[HS[hh], 1, :], D["l0_lnx_b"][h * 64:(h + 1) * 64].partition_broadcast(64),
                      writes=LNGB.all())
            e_memset(s, "dve", rf(B["Pf"])[:, :], 0.0)
            e_memset(s, "dve", rf(B["Pb"])[:, :], 0.0)
            yield
            for gq in range(4):
                yield from unit(hp, gq, B)

    gens = [stream(0, (0, 2)), stream(1, (1, 3))]
    alive = [True, True]
    first = True
    while any(alive):
        for S in range(2):
            if alive[S]:
                try:
                    next(gens[S])
                except StopIteration:
                    alive[S] = False
            if first and S == 0:
                for _ in range(STAGGER):
                    next(gens[0])
                first = False
    s.barrier()
    for kc in range(8):
        s.dma("sp", XT[:, kc, :], spill[:, kc, :], reads=spill.all(), writes=XT.all())


GAIN_NAMES = ["l0_ffn1_pre_g", "l0_ffn1_post_g", "l0_mix_pre_g", "l0_mix_post_g", "l0_ffn2_pre_g", "l0_ffn2_post_g",
              "l1_ffn1_pre_g", "l1_ffn1_post_g", "l1_mix_pre_g", "l1_mix_post_g", "l1_ffn2_pre_g", "l1_ffn2_post_g"]
HALF_GAINS = [1, 5, 7, 11]


def build_program(stages=("f01", "m0", "f02f11", "m1", "f12"), dbg=False):
    k = KB()
    nc = k.nc
    s = k.s
    xT_d = k.dram_in("xT", [DM, SEQ])
    gains_d = k.dram_in("gains", [128, 12 * 8])
    ffn_d = {}
    for nm in ("l0_ffn1", "l0_ffn2", "l1_ffn1", "l1_ffn2"):
        ffn_d[nm] = (k.dram_in(nm + "_w_in", [NJ, 128, 2048]), k.dram_in(nm + "_w_out", [2, 8, 128, 11 * 128]))
    D = {}
    for nm, shp in (("l0_pl", [128, 32]), ("l0_ph", [128, 32]), ("l0_mul", [128, 3]), ("l0_lnx_g", [512]),
                    ("l0_lnx_b", [512]), ("l0_w2", [64, 512]), ("l0_a2", [64, 512]), ("l0_g2", [128, 512]),
                    ("l0_gate_a_w", [8, 64, 64]), ("l0_gate_x_w", [8, 64, 64]), ("l0_w_lru", [4, 128, 8 * 256]),
                    ("l0_w_lora", [128, 8 * 256]), ("l0_w_hp", [4, 128, 8 * 384]), ("l0_w_out", [DM, DM])):
        D[nm] = k.dram_in(nm, shp)
    D["xt_spill"] = T("xt_spill", nc.dram_tensor("xt_spill", [128, 8, SEQ], F32, kind="Internal").ap())
    if dbg:
        D["dbg_OT"] = k.dram_out("dbg_OT", [128, 8, SEQ], BF16)
    wqkv_d = k.dram_in("l1_w_qkv", [8, 128, 8 * 384])
    l1_wo_d = k.dram_in("l1_w_out", [DM, DM])
    outT_d = k.dram_out("outT", [DM, SEQ])

    with k.es:
        XT = k.sb("XT", [128, 8, SEQ], F32, parts=4)
        C = {}
        C["gains"] = k.sb("gains", [128, 12, 8], F32)
        C["ones_m"] = k.sb("ones_m", [128, 128], BF16)
        PS = [k.ps(f"ps{i}", [128, 512]) for i in range(8)]
        C["HTG"] = k.sb("HTG", [128, 8, SEQ], BF16, parts=4)
        C["ht_ready"] = None

        s.op("dve", lambda e: e.memset(C["ones_m"][:, :], 1.0 / DM), writes=C["ones_m"].all())
        C["one_f"] = k.sb("one_f", [128, 1], F32)
        C["ones_col"] = k.sb("ones_col", [128, 1], BF16)
        C["ones_bf"] = k.sb("ones_bf", [128, 512], BF16)
        C["ident"] = k.sb("ident", [128, 128], BF16)
        C["ntri"] = k.sb("ntri", [128, 128], BF16)
        s.op("dve", lambda e: e.memset(C["one_f"][:, :], 1.0), writes=C["one_f"].all())
        s.op("dve", lambda e: e.memset(C["ones_col"][:, :], 1.0), writes=C["ones_col"].all())
        s.op("dve", lambda e: e.memset(C["ones_bf"][:, :], 1.0), writes=C["ones_bf"].all())
        s.op("pool", lambda e: e.affine_select(out=C["ident"][:, :], in_=C["ones_bf"][:, 0:128], pattern=[[-1, 128]],
                                               compare_op=ALU.is_equal, fill=0.0, base=0, channel_multiplier=1),
             reads=C["ones_bf"].all(), writes=C["ident"].all())
        s.op("pool", lambda e: e.affine_select(out=C["ntri"][:, :], in_=C["ones_bf"][:, 0:128], pattern=[[-1, 128]],
                                               compare_op=ALU.is_ge, fill=0.0, base=0, channel_multiplier=1),
             reads=C["ones_bf"].all(), writes=C["ntri"].all())
        s.op("dve", lambda e: e.tensor_scalar(out=C["ntri"][:, :], in0=C["ntri"][:, :], scalar1=-1.0, scalar2=None,
                                              op0=ALU.mult),
             reads=C["ntri"].all(), writes=C["ntri"].all())
        C["eps"] = k.sb("eps", [128, 1], F32)
        s.op("dve", lambda e: e.memset(C["eps"][:, :], NORM_EPS), writes=C["eps"].all())
        s.dma("sp", C["gains"][:, :, :], gains_d[:, :].rearrange("p (n c) -> p n c", n=12), writes=C["gains"].all())
        for gi in HALF_GAINS:
            s.op("dve", lambda e, gi=gi: e.tensor_scalar(out=C["gains"][:, gi, :], in0=C["gains"][:, gi, :],
                                                         scalar1=0.5, scalar2=None, op0=ALU.mult),
                 reads=C["gains"].all(), writes=C["gains"].all())
        for tb in range(4):
            for kc in range(8):
                s.dma("sp", XT[:, kc, tb * 512:(tb + 1) * 512], xT_d[kc * 128:(kc + 1) * 128, tb * 512:(tb + 1) * 512],
                      writes=XT.b(tb))

        PRE_GI = {"f01": 0, "m0": 2, "f02": 4, "f02f11": 4, "f11": 6, "m1": 8, "f12": 10}
        for si, st in enumerate(stages):
            nxt = PRE_GI[stages[si + 1]] if si + 1 < len(stages) else None
            if st == "f01":
                ffn_stage(k, C, XT, PS, [(*ffn_d["l0_ffn1"], 0, 1)], nxt)
            elif st == "f02":
                ffn_stage(k, C, XT, PS, [(*ffn_d["l0_ffn2"], 4, 5)], nxt)
            elif st == "f02f11":
                ffn_stage(k, C, XT, PS, [(*ffn_d["l0_ffn2"], 4, 5), (*ffn_d["l1_ffn1"], 6, 7)], nxt)
            elif st == "f11":
                ffn_stage(k, C, XT, PS, [(*ffn_d["l1_ffn1"], 6, 7)], nxt)
            elif st == "m0":
                mixer0_stage(k, C, XT, PS, D, 2, 3, nxt)
            elif st == "m1":
                if dbg:
                    C["dbg_OT"] = D["dbg_OT"]
                attn_stage(k, C, XT, PS, wqkv_d, l1_wo_d, 8, 9, nxt)
            elif st == "f12":
                ffn_stage(k, C, XT, PS, [(*ffn_d["l1_ffn2"], 10, 11)], nxt)

        for tb in range(4):
            for kc in range(8):
                s.dma("sp", outT_d[kc * 128:(kc + 1) * 128, tb * 512:(tb + 1) * 512], XT[:, kc, tb * 512:(tb + 1) * 512],
                      reads=XT.b(tb), writes=outT_d.all())
        s.barrier(engines=["sp"])
    return nc


def _col(v):
    return np.ascontiguousarray(np.asarray(v, np.float32).reshape(8, 128).T)


def prep_shared(inp):
    d = {}
    d["gains"] = np.ascontiguousarray(np.concatenate([_col(inp[n]) for n in GAIN_NAMES], axis=1))
    for nm in ("l0_ffn1", "l0_ffn2", "l1_ffn1", "l1_ffn2"):
        w_in = np.asarray(inp[nm + "_w_in"], np.float32)
        w_out = np.asarray(inp[nm + "_w_out"], np.float32)
        g = w_in[:, :DFF].reshape(8, 128, NJ, 128)
        u = w_in[:, DFF:].reshape(8, 128, NJ, 128)
        gu = np.concatenate([g, u], axis=3)
        d[nm + "_w_in"] = np.ascontiguousarray(gu.transpose(2, 1, 0, 3).reshape(NJ, 128, 2048))
        wo = w_out.reshape(2, 11, 128, 8, 128)
        d[nm + "_w_out"] = np.ascontiguousarray(wo.transpose(0, 3, 2, 1, 4).reshape(2, 8, 128, 11 * 128))
    f = lambda n: np.asarray(inp[n], np.float32)
    cw = f("l0_conv_w")
    pl = np.stack([cw[0], cw[1], cw[2], cw[3], f("l0_conv_b"), f("l0_gate_a_b"), f("l0_gate_x_b"), f("l0_lambda")], axis=1)
    d["l0_pl"] = np.ascontiguousarray(pl.reshape(4, 128, 8).transpose(1, 0, 2).reshape(128, 32))
    mu = f("l0_mu")
    ph = np.stack([mu[0:512], mu[512:1024], mu[1024:1536], f("l0_w0"), f("l0_a0"), f("l0_k_k"), f("l0_k_a"),
                   f("l0_r_k").reshape(512)], axis=1)
    d["l0_ph"] = np.ascontiguousarray(ph.reshape(4, 128, 8).transpose(1, 0, 2).reshape(128, 32))
    mul = np.zeros((128, 3), np.float32)
    mul[0:64, 0] = mu[1536:1600]
    mul[0:64, 1] = mu[1600:1664]
    mul[:, 2] = mu[1664:1792]
    d["l0_mul"] = mul
    for n in ("l0_lnx_g", "l0_lnx_b", "l0_w2", "l0_a2", "l0_g2", "l0_gate_a_w", "l0_gate_x_w", "l0_w_out"):
        d[n] = np.ascontiguousarray(f(n))
    wi = f("l0_w_in").reshape(8, 128, 2816)
    lru = np.concatenate([wi[:, :, 1792:2304].reshape(8, 128, 4, 128), wi[:, :, 2304:2816].reshape(8, 128, 4, 128)], axis=3)
    d["l0_w_lru"] = np.ascontiguousarray(lru.transpose(2, 1, 0, 3).reshape(4, 128, 8 * 256))
    d["l0_w_lora"] = np.ascontiguousarray(wi[:, :, 1536:1792].transpose(1, 0, 2).reshape(128, 8 * 256))
    hd = np.stack([wi[:, :, 0:512].reshape(8, 128, 4, 128), wi[:, :, 512:1024].reshape(8, 128, 4, 128),
                   wi[:, :, 1024:1536].reshape(8, 128, 4, 128)], axis=3)
    d["l0_w_hp"] = np.ascontiguousarray(hd.transpose(2, 1, 0, 3, 4).reshape(4, 128, 8 * 384))
    wq = np.asarray(inp["l1_w_qkv"], np.float32).reshape(8, 128, 3, 8, 128)
    d["l1_w_qkv"] = np.ascontiguousarray(wq.transpose(3, 1, 0, 2, 4).reshape(8, 128, 8 * 384))
    d["l1_w_out"] = np.ascontiguousarray(np.asarray(inp["l1_w_out"], np.float32))
    return d


_CACHE = {}


def kernel(**inputs):
    x = np.asarray(inputs["x"], np.float32)
    shared = prep_shared(inputs)
    if "nc" not in _CACHE:
        _CACHE["nc"] = build_program()
    nc = _CACHE["nc"]
    in_maps = []
    for c in range(N_CORES):
        m = dict(shared)
        m["xT"] = np.ascontiguousarray(x[c].T)
        in_maps.append(m)
    res = run_bass_kernel_spmd(nc, in_maps, core_ids=list(range(N_CORES)))
    out = np.stack([np.ascontiguousarray(res.results[c]["outT"].T) for c in range(N_CORES)], axis=0)
    return out.astype(np.float32)
```
